# Optimizing a Trainium2 kernel written in Bass

```python
import jax
import jax.numpy as jnp
from jax import lax
import numpy as np

D_MODEL = 1024
BATCH = 2
SEQ = 16384
DEPTH = 2

GRID_W = 64
CTX_LEN = 256
N_SUB = 3
D_FF = 2816
NA_HEADS = 8
NA_HEAD_DIM = 64
NA_KH = 8
NA_KW = 16
ML_HEADS = 4
ML_HEAD_DIM = 128
ML_CHUNK = 128
ROPE_THETA = 10000.0
SG_CHUNK = 128
SG_WIDTH = 2048
SG_GROUPS = 8
NA_WIDTH = NA_HEADS * NA_HEAD_DIM
ML_WIDTH = ML_HEADS * ML_HEAD_DIM
MIX_WIDTH = NA_WIDTH + ML_WIDTH
N_GATES = 4 * ML_HEADS
P_EVEN = 3 * NA_WIDTH + 4 * ML_WIDTH + N_GATES
SPLITS = (NA_WIDTH, 2 * NA_WIDTH, 3 * NA_WIDTH, 3 * NA_WIDTH + ML_WIDTH, 3 * NA_WIDTH + 2 * ML_WIDTH,
          3 * NA_WIDTH + 3 * ML_WIDTH, 3 * NA_WIDTH + 4 * ML_WIDTH)
N_EVEN = (DEPTH + 1) // 2
N_ODD = DEPTH // 2
EPS = 1e-6

kernel_name = 'hybrid_natten_mlstm_sgmlp_dit_block'


def rms_norm(x, g):
    xf = x.astype(jnp.float32)
    y = xf * lax.rsqrt(jnp.mean(jnp.square(xf), axis=-1, keepdims=True) + EPS)
    return (y * g.astype(jnp.float32)).astype(x.dtype)


def layer_norm(x, g, b):
    xf = x.astype(jnp.float32)
    mu = jnp.mean(xf, axis=-1, keepdims=True)
    var = jnp.mean(jnp.square(xf - mu), axis=-1, keepdims=True)
    y = (xf - mu) * lax.rsqrt(var + EPS)
    return (y * g.astype(jnp.float32) + b.astype(jnp.float32)).astype(x.dtype)


def modulate(x, g, shift, scale):
    return rms_norm(x, g) * (1.0 + scale) + shift


def swiglu(h, w_in, w_out):
    a, b = jnp.split(h @ w_in, 2, axis=-1)
    return (jax.nn.silu(a) * b) @ w_out


def macaron_half_ffn(x, g, shift, scale, gate, w_in, w_out):
    return x + 0.5 * gate * swiglu(modulate(x, g, shift, scale), w_in, w_out)


def mod_terms(m, j):
    return m[:, 3 * j], m[:, 3 * j + 1], m[:, 3 * j + 2]


def axial_rope(x):
    n, dh = x.shape[1], x.shape[-1]
    n_pairs = dh // 4
    pos = jnp.arange(n)
    row = (pos // GRID_W).astype(jnp.float32)
    col = (pos % GRID_W).astype(jnp.float32)
    inv_freq = ROPE_THETA ** (-jnp.arange(n_pairs, dtype=jnp.float32) / n_pairs)
    ang = jnp.concatenate([row[:, None] * inv_freq, col[:, None] * inv_freq], axis=-1)
    cos = jnp.cos(ang)[None, :, None, :]
    sin = jnp.sin(ang)[None, :, None, :]
    xp = x.astype(jnp.float32).reshape(*x.shape[:-1], dh // 2, 2)
    x1, x2 = xp[..., 0], xp[..., 1]
    out = jnp.stack([x1 * cos - x2 * sin, x1 * sin + x2 * cos], axis=-1)
    return out.reshape(x.shape).astype(x.dtype)


def dense_attention(q, k, v):
    s = jnp.einsum('bqhd,bkhd->bhqk', q, k) * (q.shape[-1] ** -0.5)
    p = jax.nn.softmax(s.astype(jnp.float32), axis=-1).astype(v.dtype)
    return jnp.einsum('bhqk,bkhd->bqhd', p, v)


def neighbourhood_attention(q, k, v, kx, vx, rpb):
    B, N, H, Dh = q.shape
    rows = N // GRID_W
    kh = min(NA_KH, rows)
    kw = NA_KW
    scale = Dh ** -0.5
    qg = q.reshape(B, rows, GRID_W, H, Dh)
    kg = k.reshape(B, rows, GRID_W, H, Dh)
    vg = v.reshape(B, rows, GRID_W, H, Dh)
    cols = jnp.arange(GRID_W)
    col_idx = jnp.clip(cols - kw // 2, 0, GRID_W - kw)[:, None] + jnp.arange(kw)[None, :]
    col_rel = col_idx - cols[:, None] + (NA_KW - 1)

    def row_block(i):
        r0 = jnp.clip(i - kh // 2, 0, rows - kh)
        q_i = lax.dynamic_index_in_dim(qg, i, axis=1, keepdims=False)
        k_win = lax.dynamic_slice_in_dim(kg, r0, kh, axis=1)[:, :, col_idx]
        v_win = lax.dynamic_slice_in_dim(vg, r0, kh, axis=1)[:, :, col_idx]
        row_rel = r0 + jnp.arange(kh) - i + (NA_KH - 1)
        bias = rpb[:, row_rel[None, :, None], col_rel[:, None, :]]
        s_loc = jnp.einsum('bqhd,baqkhd->bhqak', q_i, k_win) * scale + bias
        s_ctx = jnp.einsum('bqhd,bchd->bhqc', q_i, kx) * scale
        s = jnp.concatenate([s_loc.reshape(B, H, GRID_W, kh * kw), s_ctx], axis=-1).astype(jnp.float32)
        p = jax.nn.softmax(s, axis=-1).astype(v.dtype)
        p_loc = p[..., :kh * kw].reshape(B, H, GRID_W, kh, kw)
        p_ctx = p[..., kh * kw:]
        return (jnp.einsum('bhqak,baqkhd->bqhd', p_loc, v_win)
                + jnp.einsum('bhqc,bchd->bqhd', p_ctx, vx))

    out = lax.map(row_block, jnp.arange(rows))
    return jnp.moveaxis(out, 0, 1).reshape(B, N, H, Dh)


def mlstm_scan(q, k, v, log_i, log_f, state, with_output):
    B, H, N, Dk = q.shape
    nc = N // ML_CHUNK

    def chunks(a):
        return jnp.moveaxis(a.reshape(B, H, nc, ML_CHUNK, *a.shape[3:]), 2, 0)

    tri = jnp.tril(jnp.ones((ML_CHUNK, ML_CHUNK), dtype=bool))

    def step(carry, inp):
        C, n, m = carry
        qc, kc, vc, ic, fc = inp
        b = jnp.cumsum(fc, axis=-1)
        b_end = b[..., -1]
        w_end = b_end[..., None] - b + ic
        m_new = jnp.maximum(b_end + m, jnp.max(w_end, axis=-1))
        a_prev = jnp.exp(b_end + m - m_new)
        a_tok = jnp.exp(w_end - m_new[..., None])
        C_new = a_prev[..., None, None] * C + jnp.einsum('bhs,bhsv,bhsk->bhvk', a_tok, vc, kc)
        n_new = a_prev[..., None] * n + jnp.einsum('bhs,bhsk->bhk', a_tok, kc)
        if not with_output:
            return (C_new, n_new, m_new), None
        log_w = jnp.where(tri, b[..., :, None] - b[..., None, :] + ic[..., None, :], -jnp.inf)
        log_inter = b + m[..., None]
        m_t = jnp.maximum(log_inter, jnp.max(log_w, axis=-1))
        w_intra = jnp.exp(log_w - m_t[..., None])
        w_inter = jnp.exp(log_inter - m_t)
        s = jnp.einsum('bhtk,bhsk->bhts', qc, kc) * w_intra
        num = (w_inter[..., None] * jnp.einsum('bhvk,bhtk->bhtv', C, qc)
               + jnp.einsum('bhts,bhsv->bhtv', s, vc))
        den = w_inter * jnp.einsum('bhk,bhtk->bht', n, qc) + jnp.sum(s, axis=-1)
        h = num / jnp.maximum(jnp.abs(den), jnp.exp(-m_t))[..., None]
        return (C_new, n_new, m_new), h

    state, hs = lax.scan(step, state, (chunks(q), chunks(k), chunks(v), chunks(log_i), chunks(log_f)))
    if not with_output:
        return state, None
    return state, jnp.moveaxis(hs, 0, 2).reshape(B, H, N, Dk)


def _flip(a, rev):
    return jnp.flip(a, axis=2) if rev else a


def mlstm_bidirectional(q, k, v, log_i, log_f, qx, kx, vx, log_ix, log_fx, ctx_out):
    B, _, H, Dk = q.shape
    bh = lambda a: jnp.moveaxis(a.astype(jnp.float32), 1, 2)
    lat = [bh(a) for a in (q, k, v)]
    cx = [bh(a) for a in (qx, kx, vx)]
    h_lat, h_ctx = [], []
    for d in range(2):
        rev = d == 1
        init = (jnp.zeros((B, H, Dk, Dk), jnp.float32), jnp.zeros((B, H, Dk), jnp.float32),
                jnp.zeros((B, H), jnp.float32))
        st, hc = mlstm_scan(*[_flip(a, rev) for a in cx], _flip(bh(log_ix[..., d]), rev),
                            _flip(bh(log_fx[..., d]), rev), init, ctx_out)
        _, hl = mlstm_scan(*[_flip(a, rev) for a in lat], _flip(bh(log_i[..., d]), rev),
                           _flip(bh(log_f[..., d]), rev), st, True)
        h_lat.append(_flip(hl, rev))
        if ctx_out:
            h_ctx.append(_flip(hc, rev))
    y = jnp.moveaxis(h_lat[0] + h_lat[1], 2, 1)
    yx = jnp.moveaxis(h_ctx[0] + h_ctx[1], 2, 1) if ctx_out else None
    return y, yx


def mlstm_readout(h, o, head_g):
    B, L, H, Dk = h.shape
    y = rms_norm(h, head_g) * jax.nn.sigmoid(o.astype(jnp.float32)).reshape(B, L, H, Dk)
    return y.reshape(B, L, H * Dk).astype(o.dtype)


def even_mixer(h, hx, w_in, rpb, gate_b, head_g, w_out, ctx_out):
    def project(t):
        p = t @ w_in
        Bt, L, _ = p.shape
        qa, ka, va, qb, kb, vb, ob, g = jnp.split(p, SPLITS, axis=-1)
        na = [a.reshape(Bt, L, NA_HEADS, NA_HEAD_DIM) for a in (qa, ka, va)]
        ml = [a.reshape(Bt, L, ML_HEADS, ML_HEAD_DIM) for a in (qb, kb, vb)]
        g = g.reshape(Bt, L, ML_HEADS, 2, 2).astype(jnp.float32) + gate_b
        return na, ml, ob, g[..., 0], jax.nn.log_sigmoid(g[..., 1])

    (qa, ka, va), (qb, kb, vb), ob, li, lf = project(h)
    (qax, kax, vax), (qbx, kbx, vbx), obx, lix, lfx = project(hx)
    B, N, _ = h.shape
    y_a = neighbourhood_attention(qa, ka, va, kax, vax, rpb).reshape(B, N, NA_WIDTH)
    k_scale = ML_HEAD_DIM ** -0.5
    h_b, h_bx = mlstm_bidirectional(axial_rope(qb), axial_rope(kb) * k_scale, vb, li, lf,
                                    qbx, kbx * k_scale, vbx, lix, lfx, ctx_out)
    y = jnp.concatenate([y_a, mlstm_readout(h_b, ob, head_g)], axis=-1) @ w_out
    if not ctx_out:
        return y, None
    Bx, Lc, _ = hx.shape
    y_ax = dense_attention(qax, kax, vax).reshape(Bx, Lc, NA_WIDTH)
    yx = jnp.concatenate([y_ax, mlstm_readout(h_bx, obx, head_g)], axis=-1) @ w_out
    return y, yx


def spatial_gating(h, w_in, ln_g, ln_b, w_s, b_s, w_out):
    B, L, _ = h.shape
    u, v = jnp.split(jax.nn.gelu(h @ w_in), 2, axis=-1)
    v = layer_norm(v, ln_g, ln_b)
    nc = L // SG_CHUNK
    vg = v.reshape(B, nc, SG_CHUNK, SG_GROUPS, SG_WIDTH // SG_GROUPS)
    mixed = jnp.einsum('gts,bcsgd->bctgd', w_s, vg) + b_s.T[None, None, :, :, None]
    return (u * mixed.reshape(B, L, SG_WIDTH)) @ w_out


def setup_inputs(seed: int = 0) -> dict:
    key = jax.random.key(seed)
    ks = jax.random.split(key, 24)
    nrm = lambda k, shape, std: jax.random.normal(k, shape, jnp.float32) * std
    D = D_MODEL
    forget_base = jnp.linspace(3.0, 6.0, ML_HEADS, dtype=jnp.float32)[:, None]
    gate_b = jnp.stack([nrm(ks[11], (N_EVEN, ML_HEADS, 2), 0.1),
                        forget_base + nrm(ks[12], (N_EVEN, ML_HEADS, 2), 0.1)], axis=-1)
    return {
        'x': nrm(ks[0], (BATCH, SEQ, D), 1.0),
        'c': nrm(ks[1], (BATCH, D), 1.0),
        'ctx': nrm(ks[2], (BATCH, CTX_LEN, D), 1.0),
        'c_ctx': nrm(ks[3], (D,), 1.0),
        'w_mod': nrm(ks[4], (DEPTH, D, 3 * N_SUB * D), 0.5 * D ** -0.5),
        'b_mod': nrm(ks[5], (DEPTH, 3 * N_SUB * D), 0.02),
        'norm_g': 1.0 + nrm(ks[6], (DEPTH, N_SUB, D), 0.05),
        'ffn_w_in': nrm(ks[7], (DEPTH, 2, D, 2 * D_FF), D ** -0.5),
        'ffn_w_out': nrm(ks[8], (DEPTH, 2, D_FF, D), D_FF ** -0.5),
        'mix_w_in': nrm(ks[9], (N_EVEN, D, P_EVEN), D ** -0.5),
        'na_rpb': nrm(ks[10], (N_EVEN, NA_HEADS, 2 * NA_KH - 1, 2 * NA_KW - 1), 0.1),
        'ml_gate_b': gate_b,
        'ml_head_g': 1.0 + nrm(ks[13], (N_EVEN, ML_HEADS, ML_HEAD_DIM), 0.05),
        'mix_w_out': nrm(ks[14], (N_EVEN, MIX_WIDTH, D), MIX_WIDTH ** -0.5),
        'sg_w_in': nrm(ks[15], (N_ODD, D, 2 * SG_WIDTH), D ** -0.5),
        'sg_ln_g': 1.0 + nrm(ks[16], (N_ODD, SG_WIDTH), 0.05),
        'sg_ln_b': nrm(ks[17], (N_ODD, SG_WIDTH), 0.02),
        'sg_w_s': nrm(ks[18], (N_ODD, SG_GROUPS, SG_CHUNK, SG_CHUNK), 0.5 * SG_CHUNK ** -0.5),
        'sg_b_s': 1.0 + nrm(ks[19], (N_ODD, SG_GROUPS, SG_CHUNK), 0.1),
        'sg_w_out': nrm(ks[20], (N_ODD, SG_WIDTH, D), SG_WIDTH ** -0.5),
        'final_g': 1.0 + nrm(ks[21], (D,), 0.05),
    }


def reference(x, c, ctx, c_ctx, w_mod, b_mod, norm_g, ffn_w_in, ffn_w_out, mix_w_in, na_rpb, ml_gate_b,
              ml_head_g, mix_w_out, sg_w_in, sg_ln_g, sg_ln_b, sg_w_s, sg_b_s, sg_w_out, final_g):
    B, N, D = x.shape
    last_ctx_layer = ((DEPTH - 1) // 2) * 2
    silu_c = jax.nn.silu(c)
    silu_cx = jax.nn.silu(c_ctx)[None]
    xc = ctx
    for l in range(DEPTH):
        ctx_in = l <= last_ctx_layer
        ctx_out = l < last_ctx_layer
        mod = (silu_c @ w_mod[l] + b_mod[l]).reshape(B, 3 * N_SUB, 1, D)
        x = macaron_half_ffn(x, norm_g[l, 0], *mod_terms(mod, 0), ffn_w_in[l, 0], ffn_w_out[l, 0])
        sh, sc, gt = mod_terms(mod, 1)
        h = modulate(x, norm_g[l, 1], sh, sc)
        hx = None
        if ctx_in:
            modx = (silu_cx @ w_mod[l] + b_mod[l]).reshape(1, 3 * N_SUB, 1, D)
            xc = macaron_half_ffn(xc, norm_g[l, 0], *mod_terms(modx, 0), ffn_w_in[l, 0], ffn_w_out[l, 0])
            shx, scx, gtx = mod_terms(modx, 1)
            hx = modulate(xc, norm_g[l, 1], shx, scx)
        if l % 2 == 0:
            e = l // 2
            y, yx = even_mixer(h, hx, mix_w_in[e], na_rpb[e], ml_gate_b[e], ml_head_g[e], mix_w_out[e], ctx_out)
        else:
            o = l // 2
            y = spatial_gating(h, sg_w_in[o], sg_ln_g[o], sg_ln_b[o], sg_w_s[o], sg_b_s[o], sg_w_out[o])
            yx = (spatial_gating(hx, sg_w_in[o], sg_ln_g[o], sg_ln_b[o], sg_w_s[o], sg_b_s[o], sg_w_out[o])
                  if ctx_out else None)
        x = x + gt * y
        x = macaron_half_ffn(x, norm_g[l, 2], *mod_terms(mod, 2), ffn_w_in[l, 1], ffn_w_out[l, 1])
        if ctx_out:
            xc = xc + gtx * yx
            xc = macaron_half_ffn(xc, norm_g[l, 2], *mod_terms(modx, 2), ffn_w_in[l, 1], ffn_w_out[l, 1])
    return rms_norm(x, final_g)
```

```python
import contextlib
import numpy as np
import concourse.bass as bass
import concourse.mybir as mybir
from concourse.bass_utils import run_bass_kernel_spmd

F32 = mybir.dt.float32
BF16 = mybir.dt.bfloat16
AF = mybir.ActivationFunctionType
ALU = mybir.AluOpType
AX = mybir.AxisListType

D = 1024
DFF = 2816
NE = 4608
NU = 4864
OWN0 = 256
NOWN = 4096
CTX0 = 4608
TW = 256
EPS = 1e-6
NEG = -30000.0

ENGS = ("tensor", "vector", "scalar", "gpsimd", "sync")
NDMASEM = 32
NHW = 24


class Buf:
    __slots__ = ("name", "writers", "readers")

    def __init__(self, name):
        self.name = name
        self.writers = []
        self.readers = []


class Prog:
    def __init__(self, nc, st):
        self.nc = nc
        self.q = {e: [] for e in ENGS}
        self.cnt = {e: 0 for e in ENGS}
        self.seen = {e: {} for e in ENGS}
        self.dcnt = [0] * NDMASEM
        self.dnext = 0
        self.dnext_sw = 0
        self.bufs = {}
        self.esem = {e: st.enter_context(nc.semaphore("s_" + e)) for e in ENGS}
        self.dsem = [st.enter_context(nc.semaphore("d%d" % i)) for i in range(NDMASEM)]
        self.lastinc = {e: True for e in ENGS}

    def buf(self, name):
        b = self.bufs.get(name)
        if b is None:
            b = self.bufs[name] = Buf(name)
        return b

    def _bl(self, lst):
        return [self.buf(b) if isinstance(b, str) else b for b in lst]

    def _deps(self, reads, writes):
        deps = {}
        for b in reads:
            for k, v in b.writers:
                if deps.get(k, 0) < v:
                    deps[k] = v
        for b in writes:
            for k, v in b.writers:
                if deps.get(k, 0) < v:
                    deps[k] = v
            for k, v in b.readers:
                if deps.get(k, 0) < v:
                    deps[k] = v
        return deps

    def _waits(self, eng, deps):
        seen = self.seen[eng]
        waits = []
        for k, v in deps.items():
            if k == "tensor" and eng == "tensor":
                continue
            if seen.get(k, 0) >= v:
                continue
            seen[k] = v
            waits.append((k, v))
        return waits

    def _record(self, ev, reads, writes):
        k = ev[0]
        for b in writes:
            b.writers = [ev]
            b.readers = []
        for b in reads:
            if b in writes:
                continue
            b.readers = [e for e in b.readers if e[0] != k] + [ev]

    def op(self, eng, fn, reads=(), writes=(), inc=True):
        reads = self._bl(reads)
        writes = self._bl(writes)
        waits = self._waits(eng, self._deps(reads, writes))
        if inc:
            self.cnt[eng] += 1
            ev = (eng, self.cnt[eng])
        else:
            ev = (eng, self.cnt[eng] + 1)
        self.lastinc[eng] = inc
        self.q[eng].append(("op", fn, waits, inc))
        self._record(ev, reads, writes)

    def dma(self, eng, fn, reads=(), writes=()):
        reads = self._bl(reads)
        writes = self._bl(writes)
        deps = self._deps(reads, writes)
        if eng == "gpsimd":
            i = NHW + self.dnext_sw
            self.dnext_sw = (self.dnext_sw + 1) % (NDMASEM - NHW)
        else:
            i = self.dnext
            self.dnext = (self.dnext + 1) % NHW
        key = ("d", i)
        if self.dcnt[i] > 0 and deps.get(key, 0) < self.dcnt[i]:
            deps[key] = self.dcnt[i]
        waits = self._waits(eng, deps)
        self.dcnt[i] += 16
        ev = (key, self.dcnt[i])
        self.q[eng].append(("dma", fn, waits, i))
        self._record(ev, reads, writes)

    def cc(self, fn, reads=(), writes=()):
        self.op("gpsimd", fn, reads=reads, writes=writes, inc=True)

    def barrier(self):
        for e in ENGS:
            assert self.lastinc[e], e
        deps = {e: self.cnt[e] for e in ENGS if self.cnt[e] > 0}
        for i in range(NDMASEM):
            if self.dcnt[i] > 0:
                deps[("d", i)] = self.dcnt[i]
        for e in ENGS:
            d = {k: v for k, v in deps.items() if k != e}
            waits = self._waits(e, d)
            self.q[e].append(("wait", None, waits, None))

    def emit(self):
        nc = self.nc
        esem, dsem = self.esem, self.dsem

        def semof(k):
            return esem[k] if isinstance(k, str) else dsem[k[1]]

        def run(engname):
            items = self.q[engname]

            def body(e):
                for kind, fn, waits, x in items:
                    for k, v in waits:
                        e.wait_ge(semof(k), v)
                    if kind == "op":
                        ins = fn(e)
                        if x:
                            ins.then_inc(esem[engname], 1)
                    elif kind == "dma":
                        fn(e).then_inc(dsem[x], 16)
            return body

        with nc.Block() as block:
            block.tensor(run("tensor"))
            block.vector(run("vector"))
            block.scalar(run("scalar"))
            block.gpsimd(run("gpsimd"))
            block.sync(run("sync"))
        self.q = {e: [] for e in ENGS}


def _rowmap(s):
    rm = np.zeros(72, np.int64)
    rm[4:68] = s * 64 + np.arange(64)
    rm[0:4] = (s * 64 - 4 + np.arange(4)) if s > 0 else np.array([5, 6, 7, 8])
    rm[68:72] = (s * 64 + 64 + np.arange(4)) if s < 3 else np.array([248, 249, 250, 251])
    return rm


def _na_bias_tables(rpb, s):
    rm = _rowmap(s)
    out = np.full((5, 8, 128, 576), NEG, np.float32)
    reps = [0, 1, 2, 30, 31]
    cols = np.arange(64)
    c0 = np.clip(cols - 8, 0, 64 - 16)
    for ci, p in enumerate(reps):
        for r in range(2):
            i = s * 64 + 2 * p + r
            r0 = min(max(i - 4, 0), 256 - 8)
            seen_rows = set()
            for j in range(9):
                krow = int(rm[2 * p + j])
                if krow < r0 or krow >= r0 + 8 or krow in seen_rows:
                    continue
                seen_rows.add(krow)
                rr = krow - i + 7
                for qc in range(64):
                    kc = np.arange(c0[qc], c0[qc] + 16)
                    out[ci][:, r * 64 + qc, j * 64 + kc] = rpb[:, rr, kc - qc + 15]
    return out


def _rope_tables(s):
    rm = _rowmap(s)
    row = np.repeat(rm, 64).astype(np.float32)
    col = np.tile(np.arange(64), 72).astype(np.float32)
    inv = (10000.0 ** (-np.arange(32, dtype=np.float32) / 32)).astype(np.float32)
    ang = np.concatenate([row[:, None] * inv, col[:, None] * inv], axis=-1).astype(np.float32)
    cos = np.cos(ang).astype(np.float32)
    sin = np.sin(ang).astype(np.float32)
    cos2 = np.repeat(cos, 2, axis=1)
    sin2 = np.repeat(sin, 2, axis=1)
    sin2[:, 0::2] *= -1.0
    cos_u = np.ones((NU, 128), np.float32)
    sin_u = np.zeros((NU, 128), np.float32)
    cos_u[:NE] = cos2
    sin_u[:NE] = sin2
    return np.tile(cos_u, (1, 4)), np.tile(sin_u, (1, 4))


def _host_prep(inp):
    x = np.asarray(inp["x"], np.float32)
    shared = {}
    shared["w_mod"] = np.ascontiguousarray(inp["w_mod"], np.float32)
    shared["b_modT"] = np.ascontiguousarray(np.asarray(inp["b_mod"], np.float32).reshape(2, 72, 128).transpose(2, 0, 1))
    shared["norm_gT"] = np.ascontiguousarray(np.asarray(inp["norm_g"], np.float32).reshape(2, 3, 8, 128).transpose(3, 0, 1, 2))
    shared["ffn_w_in"] = np.ascontiguousarray(inp["ffn_w_in"], np.float32)
    shared["ffn_w_out"] = np.ascontiguousarray(inp["ffn_w_out"], np.float32)
    mw = np.asarray(inp["mix_w_in"], np.float32)[0]
    shared["mix_w_in"] = np.ascontiguousarray(mw[:, :3584])
    wg = np.zeros((1024, 128), np.float32)
    gb = np.zeros((128, 1), np.float32)
    gate_b = np.asarray(inp["ml_gate_b"], np.float32)[0]
    for h in range(4):
        for d in range(2):
            for q in range(2):
                wg[:, q * 64 + d * 32 + h] = mw[:, 3584 + h * 4 + d * 2 + q]
                gb[q * 64 + d * 32 + h, 0] = gate_b[h, d, q]
    shared["w_gate"] = wg
    shared["gate_b"] = gb
    shared["headg_bc"] = np.ascontiguousarray(np.broadcast_to(np.asarray(inp["ml_head_g"], np.float32)[0].reshape(1, 512), (128, 512)))
    shared["mix_w_out"] = np.ascontiguousarray(np.asarray(inp["mix_w_out"], np.float32)[0])
    shared["sg_w_in"] = np.ascontiguousarray(np.asarray(inp["sg_w_in"], np.float32)[0])
    shared["sg_w_out"] = np.ascontiguousarray(np.asarray(inp["sg_w_out"], np.float32)[0])
    shared["sg_lng_bc"] = np.ascontiguousarray(np.broadcast_to(np.asarray(inp["sg_ln_g"], np.float32)[0].reshape(1, 2048), (128, 2048)))
    shared["sg_lnb_bc"] = np.ascontiguousarray(np.broadcast_to(np.asarray(inp["sg_ln_b"], np.float32)[0].reshape(1, 2048), (128, 2048)))
    shared["sg_w_sT"] = np.ascontiguousarray(np.asarray(inp["sg_w_s"], np.float32)[0].transpose(2, 0, 1))
    shared["sg_b_s"] = np.ascontiguousarray(np.asarray(inp["sg_b_s"], np.float32)[0].reshape(1, 1024))
    shared["final_g_bc"] = np.ascontiguousarray(np.broadcast_to(np.asarray(inp["final_g"], np.float32).reshape(1, 1024), (128, 1024)))
    shared["ident"] = np.eye(128, dtype=np.float32)
    tri = np.zeros((128, 2, 128), np.float32)
    ss, tt = np.meshgrid(np.arange(128), np.arange(128), indexing="ij")
    tri[:, 0, :] = (tt >= ss)
    tri[:, 1, :] = (tt <= ss)
    shared["tri"] = tri
    sel = np.zeros((64, 8, 128), np.float32)
    for d in range(2):
        for h in range(4):
            sel[d * 32 + h, d * 4 + h, :] = 1.0
    shared["selm"] = sel
    rpb = np.asarray(inp["na_rpb"], np.float32)[0]
    c = np.asarray(inp["c"], np.float32)
    cctx = np.asarray(inp["c_ctx"], np.float32)
    ctx = np.asarray(inp["ctx"], np.float32)
    maps = []
    for core in range(8):
        b, s = core // 4, core % 4
        rm = _rowmap(s)
        tok = (rm[:, None] * 64 + np.arange(64)[None, :]).reshape(-1)
        xin = np.concatenate([x[b][tok], ctx[b]], axis=0)
        cT = np.stack([c[b].reshape(8, 128).T, cctx.reshape(8, 128).T], axis=-1)
        cos4, sin4 = _rope_tables(s)
        selv = np.zeros((128, 16), np.float32)
        for j in range(4):
            selv[:, j] = 1.0 if j < s else 0.0
            selv[:, 4 + j] = 1.0 - selv[:, j]
            selv[:, 8 + j] = 1.0 if j > s else 0.0
            selv[:, 12 + j] = 1.0 - selv[:, 8 + j]
        m = dict(shared)
        m["xin"] = np.ascontiguousarray(xin)
        m["cT"] = np.ascontiguousarray(cT.astype(np.float32))
        m["ropecos"] = cos4
        m["ropesin"] = sin4
        nb = _na_bias_tables(rpb, s)
        nbp = np.full((5, 8, 128, 640), NEG, np.float32)
        nbp[..., :576] = nb
        m["nabiasT"] = np.ascontiguousarray(nbp.reshape(5, 8, 128, 5, 128).transpose(0, 1, 4, 3, 2))
        m["selv"] = selv
        maps.append(m)
    return maps


INPUT_SHAPES = {
    "xin": [NU, 1024], "cT": [128, 8, 2], "w_mod": [2, 1024, 9216], "b_modT": [128, 2, 72], "norm_gT": [128, 2, 3, 8],
    "ffn_w_in": [2, 2, 1024, 5632], "ffn_w_out": [2, 2, 2816, 1024], "mix_w_in": [1024, 3584], "w_gate": [1024, 128],
    "gate_b": [128, 1], "headg_bc": [128, 512], "mix_w_out": [1024, 1024], "sg_w_in": [1024, 4096], "sg_w_out": [2048, 1024],
    "sg_lng_bc": [128, 2048], "sg_lnb_bc": [128, 2048], "sg_w_sT": [128, 8, 128], "sg_b_s": [1, 1024], "final_g_bc": [128, 1024],
    "ident": [128, 128], "tri": [128, 2, 128], "selm": [64, 8, 128], "ropecos": [NU, 512], "ropesin": [NU, 512],
    "nabiasT": [5, 8, 128, 5, 128], "selv": [128, 16],
}


class Ctx:
    pass


_UC = [0]


def _u():
    _UC[0] += 1
    return "u%d" % _UC[0]


def build(stages="all", dbg=()):
    nc = bass.Bass("TRN2", target_bir_lowering=False)
    I = {k: nc.dram_tensor(k, shp, F32, kind="ExternalInput").ap() for k, shp in INPUT_SHAPES.items()}
    OUT = nc.dram_tensor("out", [NOWN, 1024], F32, kind="ExternalOutput").ap()

    def scratch(name, shape, dt):
        if name in dbg:
            return nc.dram_tensor(name, shape, dt, kind="ExternalOutput").ap()
        return nc.dram_tensor(name, shape, dt).ap()

    G = Ctx()
    G.nc, G.I, G.OUT = nc, I, OUT
    G.dbg = dbg
    if dbg:
        G.DBG = {k: nc.dram_tensor(k, shp, dt, kind="ExternalOutput").ap() for k, (shp, dt) in {
            "D_MOD": ([128, 3, 2, 3, 2, 8], F32), "D_X0": ([128, 8, TW], F32), "D_X1": ([128, 8, TW], F32),
            "D_H": ([128, 8, TW], BF16), "D_HID": ([128, 22, TW], BF16), "D_RSTD": ([128, TW], F32)}.items()}
    G.XT = scratch("XT", [128, 8, NU], F32)
    G.NAQT = scratch("NAQT", [128, 4, NU], BF16)
    G.NAKT = scratch("NAKT", [128, 4, NU], BF16)
    G.NAV = scratch("NAV", [NU, 520], BF16)
    G.MLQT = scratch("MLQT", [128, 4, NU], BF16)
    G.MLKT = scratch("MLKT", [128, 4, NU], BF16)
    G.MLK = scratch("MLK", [NU, 512], BF16)
    G.MLV = scratch("MLV", [NU, 516], BF16)
    G.MLG2 = scratch("MLG2", [NU, 512], F32)
    G.GIF = scratch("GIF", [128, NU], F32)
    G.YT = scratch("YT", [128, 8, NOWN], BF16)
    G.HF = scratch("HF", [NOWN, 512], F32)
    G.HB = scratch("HB", [NOWN, 512], F32)
    G.CCI = scratch("CCI", [128, 1040], F32)
    G.CCO = scratch("CCO", [512, 1040], F32)

    with contextlib.ExitStack() as gst:
        P = Prog(nc, gst)
        G.P = P

        def gsb(name, shape, dt):
            return gst.enter_context(nc.sbuf_tensor(name, shape, dt))
        G.ident_f = gsb("ident_f", [128, 128], F32)
        G.ident_b = gsb("ident_b", [128, 128], BF16)
        G.ident8 = gsb("ident8", [128, 128], BF16)
        G.ones_b = gsb("ones_b", [128, 128], BF16)
        G.ones_f = gsb("ones_f", [128, 128], F32)
        G.c_eps = gsb("c_eps", [128, 1], F32)
        G.c_one = gsb("c_one", [128, 1], F32)
        G.SH = gsb("SH", [128, 2, 3, 2, 8], F32)
        G.GM = gsb("GM", [128, 2, 3, 2, 8], F32)
        G.GT = gsb("GT", [128, 2, 3, 2, 8], F32)

        P.dma("sync", lambda e: e.dma_start(out=G.ident_f[:], in_=I["ident"][:, :]), writes=["ident_f"])
        P.op("vector", lambda e: e.tensor_copy(G.ident_b[:], G.ident_f[:]), reads=["ident_f"], writes=["ident_b"])
        P.op("vector", lambda e: e.tensor_scalar(G.ident8[:], G.ident_f[:], 8.0, None, ALU.mult), reads=["ident_f"], writes=["ident8"])
        P.op("vector", lambda e: e.memset(G.ones_b[:], 1.0), writes=["ones_b"])
        P.op("vector", lambda e: e.memset(G.ones_f[:], 1.0), writes=["ones_f"])
        P.op("vector", lambda e: e.memset(G.c_eps[:], EPS), writes=["c_eps"])
        P.op("vector", lambda e: e.memset(G.c_one[:], 1.0), writes=["c_one"])

        ext_tiles = [(ti, 0) for ti in range(18)] + [(18, 1)]
        own_tiles = [(ti, 0) for ti in range(1, 17)]
        S = stages
        stage_mod(G)
        stage_t0(G)
        if S in ("ffn_only",):
            stage_ffn(G, 0, 0, ext_tiles)
            stage_final(G)
        elif S == "ffn_dbg":
            stage_ffn(G, 0, 0, [(1, 0)])
        else:
            stage_ffn(G, 0, 0, ext_tiles)
            stage_inproj(G)
            stage_na(G)
            stage_ml(G)
            stage_outproj(G)
            stage_ffn(G, 0, 1, own_tiles)
            stage_ffn(G, 1, 0, own_tiles)
            stage_sg(G)
            stage_ffn(G, 1, 1, own_tiles)
            stage_final(G)
    return nc


_AC = [0]


def _alloc(G, st):
    nc = G.nc
    _AC[0] += 1
    sfx = "_%d" % _AC[0]

    def sb(name, shape, dt):
        return st.enter_context(nc.sbuf_tensor(name + sfx, shape, dt))

    def ps(name, shape, dt=F32):
        return st.enter_context(nc.psum_tensor(name + sfx, shape, dt))
    return sb, ps


def stage_mod(G):
    P, I, nc = G.P, G.I, G.nc
    with contextlib.ExitStack() as st:
        sb, ps = _alloc(G, st)
        cT = sb("m_cT", [128, 8, 2], F32)
        sil = sb("m_sil", [128, 8, 2], BF16)
        wm = [sb("m_wm%d" % i, [128, 8, 1024], BF16) for i in range(3)]
        modv = sb("m_modv", [128, 2, 72, 2], F32)
        bmod = sb("m_bmod", [128, 2, 72], F32)
        ng = sb("m_ng", [128, 2, 3, 8], F32)
        pp = [ps("m_ps%d" % i, [128, 8, 2]) for i in range(2)]
        xs = [sb("t_xs%d" % i, [128, 1024], F32) for i in range(3)]
        xo = [sb("t_xo%d" % i, [128, 8, 128], F32) for i in range(2)]
        pt = [ps("t_pt%d" % i, [128, 8, 128]) for i in range(2)]
        P.dma("sync", lambda e: e.dma_start(out=cT[:], in_=I["cT"][:, :, :]), writes=["m_cT"])
        P.dma("sync", lambda e: e.dma_start(out=bmod[:], in_=I["b_modT"][:, :, :]), writes=["m_bmod"])
        P.dma("sync", lambda e: e.dma_start(out=ng[:], in_=I["norm_gT"][:, :, :, :]), writes=["m_ng"])
        P.op("scalar", lambda e: e.activation(out=sil[:], in_=cT[:], func=AF.Silu), reads=["m_cT"], writes=["m_sil"])
        NSUB = NU // 128

        def t0_load(i):
            if i < NSUB:
                b3 = i % 3
                P.dma("sync", lambda e, b3=b3, i=i: e.dma_start(out=xs[b3][:], in_=I["xin"][i * 128:(i + 1) * 128, :]), writes=["t_xs%d" % b3])

        def t0_sub(i):
            t0 = i * 128
            b, b3 = i % 2, i % 3
            for k in range(8):
                P.op("tensor", lambda e, b=b, b3=b3, k=k: e.transpose(pt[b][:, k, :], xs[b3][:, k * 128:(k + 1) * 128], G.ident_f[:]),
                     reads=["t_xs%d" % b3, "ident_f"], writes=["t_pt%d" % b], inc=(k == 7))
            if b == 0:
                P.op("vector", lambda e, b=b: e.tensor_copy(xo[b][:], pt[b][:]), reads=["t_pt%d" % b], writes=["t_xo%d" % b])
            else:
                P.op("scalar", lambda e, b=b: e.activation(out=xo[b][:], in_=pt[b][:], func=AF.Identity), reads=["t_pt%d" % b], writes=["t_xo%d" % b])
            t0_load(i + 3)
            P.dma("sync", lambda e, b=b, t0=t0: e.dma_start(out=G.XT[:, :, t0:t0 + 128], in_=xo[b][:]),
                  reads=["t_xo%d" % b], writes=["XT%d" % (t0 // TW)])

        for i in range(3):
            t0_load(i)
        n = 0
        ti = 0
        for l in range(2):
            wsrc = I["w_mod"][l].rearrange("(k p) n -> p k n", p=128)
            for blk in range(9):
                w = wm[n % 3]
                wn = "m_wm%d" % (n % 3)
                pt_ = pp[n % 2]
                pn = "m_ps%d" % (n % 2)
                P.dma("gpsimd", lambda e, w=w, blk=blk, wsrc=wsrc: e.dma_start(out=w[:], in_=wsrc[:, :, blk * 1024:(blk + 1) * 1024]),
                      writes=[wn])
                for jj in range(8):
                    for k in range(8):
                        P.op("tensor", lambda e, w=w, pt_=pt_, jj=jj, k=k: e.matmul(
                            pt_[:, jj, :], lhsT=w[:, k, jj * 128:(jj + 1) * 128], rhs=sil[:, k, :], start=(k == 0), stop=(k == 7)),
                            reads=[wn, "m_sil"], writes=[pn], inc=(jj == 7 and k == 7))
                for m in range(2):
                    P.op("vector", lambda e, pt_=pt_, l=l, blk=blk, m=m: e.tensor_tensor(
                        out=modv[:, l, blk * 8:(blk + 1) * 8, m], in0=pt_[:, :, m], in1=bmod[:, l, blk * 8:(blk + 1) * 8], op=ALU.add),
                        reads=[pn, "m_bmod"], writes=["m_modv"])
                n += 1
                for _ in range(2):
                    if ti < NSUB:
                        t0_sub(ti)
                        ti += 1
        while ti < NSUB:
            t0_sub(ti)
            ti += 1
        for l in range(2):
            for j in range(3):
                for m in range(2):
                    sh = modv[:, l, (3 * j) * 8:(3 * j) * 8 + 8, m]
                    sc = modv[:, l, (3 * j + 1) * 8:(3 * j + 1) * 8 + 8, m]
                    gt = modv[:, l, (3 * j + 2) * 8:(3 * j + 2) * 8 + 8, m]
                    P.op("vector", lambda e, sh=sh, l=l, j=j, m=m: e.tensor_copy(G.SH[:, l, j, m, :], sh), reads=["m_modv"], writes=["modc"])
                    P.op("vector", lambda e, sc=sc, l=l, j=j, m=m: e.scalar_tensor_tensor(
                        out=G.GM[:, l, j, m, :], in0=sc, scalar=1.0, in1=ng[:, l, j, :], op0=ALU.add, op1=ALU.mult),
                        reads=["m_modv", "m_ng"], writes=["modc"])
                    P.op("vector", lambda e, gt=gt, l=l, j=j, m=m: e.tensor_scalar(
                        G.GT[:, l, j, m, :], gt, (1.0 if j == 1 else 0.5), None, ALU.mult), reads=["m_modv"], writes=["modc"])
        P.barrier()
        P.emit()


def stage_t0(G):
    return


class NormScratch:
    def __init__(self, G, sb, ps, pfx, W=TW):
        self.sq = [sb(pfx + "sq%d" % i, [128, W], BF16) for i in range(2)]
        self.rs = sb(pfx + "rs", [128, W], F32)
        self.rstd = sb(pfx + "rstd", [128, W], F32)
        self.tmp = [sb(pfx + "tmp%d" % i, [128, W], F32) for i in range(2)]
        self.pn = ps(pfx + "pn", [128, W])
        self.pfx = pfx


def norm_mod(G, NS, x, xname, gm, sh, h, hname, W=TW, phase=None):
    P = G.P
    pfx = NS.pfx
    for k in range(8):
        if phase is None:
            sq, sqn = NS.sq[k % 2], pfx + "sq%d" % (k % 2)
        else:
            sq, sqn = NS.sq8[k], pfx + "sq8_%d" % k
        if phase in (None, "A"):
            P.op("scalar", lambda e, sq=sq, k=k: e.activation(out=sq[:, :W], in_=x[:, k, :], func=AF.Square), reads=[xname], writes=[sqn])
        if phase in (None, "B"):
            P.op("tensor", lambda e, sq=sq, k=k: e.matmul(NS.pn[:, :W], lhsT=G.ones_b[:], rhs=sq[:, :W], start=(k == 0), stop=(k == 7)),
                 reads=[sqn, "ones_b"], writes=[pfx + "pn"], inc=True)
    if phase == "A":
        return
    P.op("scalar", lambda e: e.activation(out=NS.rs[:, :W], in_=NS.pn[:, :W], func=AF.Sqrt, bias=G.c_eps[:], scale=1.0 / 1024.0),
         reads=[pfx + "pn", "c_eps"], writes=[pfx + "rs"])
    P.op("vector", lambda e: e.reciprocal(NS.rstd[:, :W], NS.rs[:, :W]), reads=[pfx + "rs"], writes=[pfx + "rstd"])
    for k in range(8):
        tmp = NS.tmp[k % 2]
        tn = pfx + "tmp%d" % (k % 2)
        P.op("vector", lambda e, tmp=tmp, k=k: e.tensor_tensor(out=tmp[:, :W], in0=x[:, k, :], in1=NS.rstd[:, :W], op=ALU.mult),
             reads=[xname, pfx + "rstd"], writes=[tn])
        P.op("scalar", lambda e, tmp=tmp, k=k: e.activation(out=h[:, k, :], in_=tmp[:, :W], func=AF.Identity, bias=sh[:, k:k + 1], scale=gm[:, k:k + 1]),
             reads=[tn, "modc"], writes=[hname])


def load_w_cast(G, dst, dname, src, nk, ncols, step=1024):
    P = G.P
    v = src.rearrange("(k p) n -> p k n", p=128)
    for c0 in range(0, ncols, step):
        c1 = min(ncols, c0 + step)
        P.dma("gpsimd", lambda e, c0=c0, c1=c1: e.dma_start(out=dst[:, :, c0:c1], in_=v[:, :, c0:c1]), writes=[dname])


def stage_ffn(G, l, i, tiles):
    P, I, nc = G.P, G.I, G.nc
    j = 0 if i == 0 else 2
    with contextlib.ExitStack() as st:
        sb, ps = _alloc(G, st)
        wi = sb("f_wi", [128, 8, 2 * DFF], BF16)
        wo = sb("f_wo", [128, 22, 1024], BF16)
        xt = [sb("f_xt%d" % b, [128, 8, TW], F32) for b in range(2)]
        hh = [sb("f_h%d" % b, [128, 8, TW], BF16) for b in range(2)]
        hid = sb("f_hid", [128, 22, TW], BF16)
        sa = [sb("f_sa%d" % b, [128, TW], F32) for b in range(2)]
        NS = NormScratch(G, sb, ps, "f_")
        NS.sq8 = [sb("f_sq8_%d" % k, [128, TW], BF16) for k in range(8)]
        pa = [ps("f_pa%d" % b, [128, TW]) for b in range(2)]
        pb = [ps("f_pb%d" % b, [128, TW]) for b in range(2)]
        po = [ps("f_po%d" % b, [128, TW]) for b in range(2)]
        load_w_cast(G, wi, "f_wi", I["ffn_w_in"][l, i], 8, 2 * DFF)
        load_w_cast(G, wo, "f_wo", I["ffn_w_out"][l, i], 22, 1024)
        def ld(n):
            ti_ = tiles[n][0]
            xb = xt[n % 2]
            P.dma("sync", lambda e, xb=xb, ti_=ti_: e.dma_start(out=xb[:], in_=G.XT[:, :, ti_ * TW:(ti_ + 1) * TW]),
                  reads=["XT%d" % ti_], writes=["f_xt%d" % (n % 2)])
        def do_norm(n, phase=None):
            ti_, m_ = tiles[n]
            bb = n % 2
            norm_mod(G, NS, xt[bb], "f_xt%d" % bb, G.GM[:, l, j, m_, :], G.SH[:, l, j, m_, :], hh[bb], "f_h%d" % bb, phase=phase)
        ld(0)
        if len(tiles) > 1:
            ld(1)
        do_norm(0)
        for n, (ti, m) in enumerate(tiles):
            t0 = ti * TW
            b = n % 2
            x, xn = xt[b], "f_xt%d" % b
            h, hn = hh[b], "f_h%d" % b
            for jj in range(22):
                q = jj % 2
                for half, pp, pn in ((0, pa[q], "f_pa%d" % q), (1, pb[q], "f_pb%d" % q)):
                    c0 = half * DFF + jj * 128
                    for k in range(8):
                        P.op("tensor", lambda e, pp=pp, c0=c0, k=k, h=h: e.matmul(
                            pp[:], lhsT=wi[:, k, c0:c0 + 128], rhs=h[:, k, :], start=(k == 0), stop=(k == 7)),
                            reads=["f_wi", hn], writes=[pn], inc=(k == 7))
                P.op("scalar", lambda e, q=q: e.activation(out=sa[q][:], in_=pa[q][:], func=AF.Silu), reads=["f_pa%d" % q], writes=["f_sa%d" % q])
                P.op("vector", lambda e, q=q, jj=jj: e.tensor_tensor(out=hid[:, jj, :], in0=sa[q][:], in1=pb[q][:], op=ALU.mult),
                     reads=["f_sa%d" % q, "f_pb%d" % q], writes=["f_hid%d" % jj])
            if n + 1 < len(tiles):
                do_norm(n + 1, "A")
            for f in range(8):
                q = f % 2
                if f == 3 and n + 1 < len(tiles):
                    do_norm(n + 1, "B")
                for jj in range(22):
                    P.op("tensor", lambda e, q=q, f=f, jj=jj: e.matmul(
                        po[q][:], lhsT=wo[:, jj, f * 128:(f + 1) * 128], rhs=hid[:, jj, :], start=(jj == 0), stop=(jj == 21)),
                        reads=["f_wo", "f_hid%d" % jj], writes=["f_po%d" % q], inc=(jj == 21))
                gsc = G.GT[:, l, j, m, f:f + 1]
                P.op("vector", lambda e, q=q, f=f, x=x, gsc=gsc: e.scalar_tensor_tensor(
                    out=x[:, f, :], in0=po[q][:], scalar=gsc, in1=x[:, f, :], op0=ALU.mult, op1=ALU.add),
                    reads=["f_po%d" % q, "modc", xn], writes=[xn])
            if G.dbg and ti == 1:
                P.dma("sync", lambda e: e.dma_start(out=G.DBG["D_HID"][:, :, :], in_=hid[:]), reads=["f_hid%d" % q for q in range(22)], writes=["dbg3"])
                P.dma("sync", lambda e, x=x: e.dma_start(out=G.DBG["D_X1"][:, :, :], in_=x[:]), reads=[xn], writes=["dbg4"])
            P.dma("sync", lambda e, x=x, t0=t0: e.dma_start(out=G.XT[:, :, t0:t0 + TW], in_=x[:]), reads=[xn], writes=["XT%d" % ti])
            if n + 2 < len(tiles):
                ld(n + 2)
        P.barrier()
        P.emit()


def stage_final(G):
    P, I, nc = G.P, G.I, G.nc
    with contextlib.ExitStack() as st:
        sb, ps = _alloc(G, st)
        fg = sb("k_fg", [128, 1024], F32)
        xs = [sb("k_xs%d" % b, [128, 8, 128], F32) for b in range(2)]
        junk = sb("k_junk", [128, 1024], F32)
        yo = [sb("k_yo%d" % b, [128, 1024], F32) for b in range(2)]
        ssq = sb("k_ssq", [128, 2], F32)
        rs = sb("k_rs", [128, 2], F32)
        pt = [ps("k_pt%d" % b, [128, 8, 128]) for b in range(2)]
        P.dma("sync", lambda e: e.dma_start(out=fg[:], in_=I["final_g_bc"][:, :]), writes=["k_fg"])
        for i in range(NOWN // 128):
            t0 = OWN0 + i * 128
            b = i % 2
            P.dma("gpsimd", lambda e, b=b, t0=t0: e.dma_start(out=xs[b][:], in_=G.XT[:, :, t0:t0 + 128]),
                  reads=["XT%d" % (t0 // TW)], writes=["k_xs%d" % b])
            for k in range(8):
                P.op("tensor", lambda e, b=b, k=k: e.transpose(pt[b][:, k, :], xs[b][:, k, :], G.ident_f[:]),
                     reads=["k_xs%d" % b, "ident_f"], writes=["k_pt%d" % b], inc=(k == 7))
            ptf = pt[b][:].rearrange("p k n -> p (k n)")
            P.op("scalar", lambda e, b=b, ptf=ptf: e.activation(out=junk[:], in_=ptf, func=AF.Square, accum_out=ssq[:, b:b + 1]),
                 reads=["k_pt%d" % b], writes=["k_junk", "k_ssq%d" % b])
            P.op("scalar", lambda e, b=b: e.activation(out=rs[:, b:b + 1], in_=ssq[:, b:b + 1], func=AF.Sqrt, bias=G.c_eps[:], scale=1.0 / 1024.0),
                 reads=["k_ssq%d" % b, "c_eps"], writes=["k_rs%d" % b])
            P.op("vector", lambda e, b=b: e.reciprocal(rs[:, b:b + 1], rs[:, b:b + 1]), reads=["k_rs%d" % b], writes=["k_rs%d" % b])
            P.op("vector", lambda e, b=b, ptf=ptf: e.scalar_tensor_tensor(
                out=yo[b][:], in0=ptf, scalar=rs[:, b:b + 1], in1=fg[:], op0=ALU.mult, op1=ALU.mult),
                reads=["k_pt%d" % b, "k_rs%d" % b, "k_fg"], writes=["k_yo%d" % b])
            P.dma("sync", lambda e, b=b, i=i: e.dma_start(out=G.OUT[i * 128:(i + 1) * 128, :], in_=yo[b][:]),
                  reads=["k_yo%d" % b], writes=["OUT"])
        P.barrier()
        P.emit()


def stage_inproj(G):
    P, I, nc = G.P, G.I, G.nc
    l, j = 0, 1
    tiles = [(ti, 0) for ti in range(18)] + [(18, 1)]
    with contextlib.ExitStack() as st:
        sb, ps = _alloc(G, st)
        wfm = sb("i_wfm", [128, 8, 1024], BF16)
        wg = sb("i_wg", [128, 8, 128], BF16)
        wtm = sb("i_wtm", [128, 8, 2560], BF16)
        xt = [sb("i_xt%d" % b, [128, 8, TW], F32) for b in range(2)]
        hh = [sb("i_h%d" % b, [128, 8, TW], BF16) for b in range(2)]
        NS = NormScratch(G, sb, ps, "i_")
        fm = [sb("i_fm%d" % b, [128, 8, TW], BF16) for b in range(2)]
        gts = [sb("i_gt%d" % b, [128, TW], F32) for b in range(2)]
        cosT = [sb("i_cos%d" % b, [128, 512], F32) for b in range(2)]
        sinT = [sb("i_sin%d" % b, [128, 512], F32) for b in range(2)]
        hg = sb("i_hg", [128, 512], F32)
        vt = [sb("i_vt%d" % b, [128, 8, 65], BF16) for b in range(2)]
        mv = [sb("i_mv%d" % b, [128, 4, 129], BF16) for b in range(2)]
        xs = [sb("i_xs%d" % b, [128, 512], F32) for b in range(2)]
        r1 = [sb("i_r1%d" % b, [128, 512], F32) for b in range(2)]
        r2 = [sb("i_r2%d" % b, [128, 512], F32) for b in range(2)]
        qr = [sb("i_qr%d" % b, [128, 512], BF16) for b in range(2)]
        sg = sb("i_sg", [128, 512], F32)
        g2 = [sb("i_g2%d" % b, [128, 512], F32) for b in range(2)]
        tq = [sb("i_tq%d" % b, [128, 4, 128], BF16) for b in range(2)]
        pfm = [ps("i_pfm%d" % b, [128, TW]) for b in range(2)]
        ptm = [ps("i_ptm%d" % b, [128, 512]) for b in range(2)]
        ptr = [ps("i_ptr%d" % b, [128, 4, 128], BF16) for b in range(2)]
        load_w_cast(G, wfm, "i_wfm", I["mix_w_in"][:, 0:1024], 8, 1024)
        load_w_cast(G, wg, "i_wg", I["w_gate"], 8, 128)
        load_w_cast(G, wtm, "i_wtm", I["mix_w_in"][:, 1024:3584], 8, 2560)
        P.dma("sync", lambda e: e.dma_start(out=hg[:], in_=I["headg_bc"][:, :]), writes=["i_hg"])
        for b in range(2):
            P.op("vector", lambda e, b=b: e.memset(vt[b][:], 1.0), writes=["i_vt%d" % b])
            P.op("vector", lambda e, b=b: e.memset(mv[b][:], 1.0), writes=["i_mv%d" % b])

        def ld(n):
            ti_ = tiles[n][0]
            xb = xt[n % 2]
            P.dma("gpsimd", lambda e, xb=xb, ti_=ti_: e.dma_start(out=xb[:], in_=G.XT[:, :, ti_ * TW:(ti_ + 1) * TW]),
                  reads=["XT%d" % ti_], writes=["i_xt%d" % (n % 2)])
        ld(0)
        cnt = {"s": 0, "r": 0, "t": 0}
        deferred = []

        def rope_and_T(pt_, ptn, scale, dstT, tmaj_dst, ts0):
            a = cnt["r"] % 2
            cnt["r"] += 1
            sb_ = cnt["s"] % 2
            X, R1, R2, QR, TQ, PT = xs[a], r1[a], r2[a], qr[a], tq[a], ptr[a]
            xn_, r1n, r2n, qrn, tqn, ptn2 = "i_xs%d" % a, "i_r1%d" % a, "i_r2%d" % a, "i_qr%d" % a, "i_tq%d" % a, "i_ptr%d" % a
            P.op("scalar", lambda e: e.activation(out=X[:], in_=pt_[:], func=AF.Copy, scale=scale), reads=[ptn], writes=[xn_])
            P.op("vector", lambda e: e.tensor_tensor(out=R1[:], in0=X[:], in1=cosT[sb_][:], op=ALU.mult), reads=[xn_, "i_cos%d" % sb_], writes=[r1n])
            Xv = X[:].rearrange("p (i t) -> p i t", t=2)
            Sv = sinT[sb_][:].rearrange("p (i t) -> p i t", t=2)
            Rv = R2[:].rearrange("p (i t) -> p i t", t=2)
            P.op("vector", lambda e: e.tensor_tensor(out=Rv[:, :, 0], in0=Xv[:, :, 1], in1=Sv[:, :, 0], op=ALU.mult),
                 reads=[xn_, "i_sin%d" % sb_], writes=[r2n])
            P.op("vector", lambda e: e.tensor_tensor(out=Rv[:, :, 1], in0=Xv[:, :, 0], in1=Sv[:, :, 1], op=ALU.mult),
                 reads=[xn_, "i_sin%d" % sb_], writes=[r2n])
            P.op("vector", lambda e: e.tensor_tensor(out=QR[:], in0=R1[:], in1=R2[:], op=ALU.add), reads=[r1n, r2n], writes=[qrn])
            if tmaj_dst is not None:
                P.dma("sync", lambda e: e.dma_start(out=tmaj_dst[ts0:ts0 + 128, :], in_=QR[:]), reads=[qrn], writes=[_u()])
            def later():
                for hd in range(4):
                    P.op("tensor", lambda e, hd=hd: e.transpose(PT[:, hd, :], QR[:, hd * 128:(hd + 1) * 128], G.ident_b[:]),
                         reads=[qrn, "ident_b"], writes=[ptn2], inc=(hd == 3))
                P.op("scalar", lambda e: e.activation(out=TQ[:], in_=PT[:], func=AF.Copy), reads=[ptn2], writes=[tqn])
                P.dma("sync", lambda e: e.dma_start(out=dstT[:, :, ts0:ts0 + 128], in_=TQ[:]), reads=[tqn], writes=[_u()])
            deferred.append(later)

        for n, (ti, m) in enumerate(tiles):
            t0 = ti * TW
            b = n % 2
            x, xn = xt[b], "i_xt%d" % b
            h, hn = hh[b], "i_h%d" % b
            if n + 1 < len(tiles):
                ld(n + 1)
            if n == 0:
                norm_mod(G, NS, x, xn, G.GM[:, l, j, m, :], G.SH[:, l, j, m, :], h, hn)
            FM, fmn = fm[b], "i_fm%d" % b
            for fc in range(8):
                q = fc % 2
                for k in range(8):
                    P.op("tensor", lambda e, q=q, fc=fc, k=k, h=h: e.matmul(
                        pfm[q][:], lhsT=wfm[:, k, fc * 128:(fc + 1) * 128], rhs=h[:, k, :], start=(k == 0), stop=(k == 7)),
                        reads=["i_wfm", hn], writes=["i_pfm%d" % q], inc=(k == 7))
                P.op("scalar", lambda e, q=q, fc=fc, FM=FM: e.activation(out=FM[:, fc, :], in_=pfm[q][:], func=AF.Copy),
                     reads=["i_pfm%d" % q], writes=[fmn])
            P.dma("sync", lambda e, FM=FM, t0=t0: e.dma_start(out=G.NAQT[:, :, t0:t0 + TW], in_=FM[:, 0:4, :]), reads=[fmn], writes=[_u()])
            P.dma("sync", lambda e, FM=FM, t0=t0: e.dma_start(out=G.NAKT[:, :, t0:t0 + TW], in_=FM[:, 4:8, :]), reads=[fmn], writes=[_u()])
            GTS, gtn = gts[b], "i_gt%d" % b
            for k in range(8):
                P.op("tensor", lambda e, k=k, h=h: e.matmul(pfm[0][:], lhsT=wg[:, k, :], rhs=h[:, k, :], start=(k == 0), stop=(k == 7)),
                     reads=["i_wg", hn], writes=["i_pfm0"], inc=(k == 7))
            P.op("vector", lambda e, GTS=GTS: e.tensor_copy(GTS[:], pfm[0][:]), reads=["i_pfm0"], writes=[gtn])
            P.dma("sync", lambda e, GTS=GTS, t0=t0: e.dma_start(out=G.GIF[:, t0:t0 + TW], in_=GTS[:]), reads=[gtn], writes=[_u()])
            if n + 1 < len(tiles):
                ti2, m2 = tiles[n + 1]
                b2 = (n + 1) % 2
                norm_mod(G, NS, xt[b2], "i_xt%d" % b2, G.GM[:, l, j, m2, :], G.SH[:, l, j, m2, :], hh[b2], "i_h%d" % b2)
            for s_ in range(TW // 128):
                ts0 = t0 + s_ * 128
                sbi = cnt["s"] % 2
                P.dma("gpsimd", lambda e, sbi=sbi, ts0=ts0: e.dma_start(out=cosT[sbi][:], in_=I["ropecos"][ts0:ts0 + 128, :]), writes=["i_cos%d" % sbi])
                P.dma("gpsimd", lambda e, sbi=sbi, ts0=ts0: e.dma_start(out=sinT[sbi][:], in_=I["ropesin"][ts0:ts0 + 128, :]), writes=["i_sin%d" % sbi])
                for blk in range(5):
                    a = cnt["t"] % 2
                    cnt["t"] += 1
                    PT_, ptn = ptm[a], "i_ptm%d" % a
                    for k in range(8):
                        P.op("tensor", lambda e, PT_=PT_, k=k, h=h, s_=s_, blk=blk: e.matmul(
                            PT_[:], lhsT=h[:, k, s_ * 128:(s_ + 1) * 128], rhs=wtm[:, k, blk * 512:(blk + 1) * 512], start=(k == 0), stop=(k == 7)),
                            reads=["i_wtm", hn], writes=[ptn], inc=(k == 7))
                    while len(deferred) > (1 if blk == 3 else 0):
                        deferred.pop(0)()
                    if blk == 0:
                        VT = vt[sbi]
                        P.op("scalar", lambda e, VT=VT, PT_=PT_: e.activation(out=VT[:, :, 0:64], in_=PT_[:].rearrange("p (h d) -> p h d", d=64), func=AF.Copy),
                             reads=[ptn], writes=["i_vt%d" % sbi])
                        P.dma("sync", lambda e, VT=VT, ts0=ts0: e.dma_start(out=G.NAV[ts0:ts0 + 128, :], in_=VT[:].rearrange("p h d -> p (h d)")),
                              reads=["i_vt%d" % sbi], writes=[_u()])
                    elif blk == 1:
                        rope_and_T(PT_, ptn, 1.0, G.MLQT, None, ts0)
                    elif blk == 2:
                        rope_and_T(PT_, ptn, 128.0 ** -0.5, G.MLKT, G.MLK, ts0)
                    elif blk == 3:
                        MV = mv[sbi]
                        P.op("scalar", lambda e, MV=MV, PT_=PT_: e.activation(out=MV[:, :, 0:128], in_=PT_[:].rearrange("p (h d) -> p h d", d=128), func=AF.Copy),
                             reads=[ptn], writes=["i_mv%d" % sbi])
                        P.dma("sync", lambda e, MV=MV, ts0=ts0: e.dma_start(out=G.MLV[ts0:ts0 + 128, :], in_=MV[:].rearrange("p h d -> p (h d)")),
                              reads=["i_mv%d" % sbi], writes=[_u()])
                    else:
                        G2 = g2[sbi]
                        P.op("scalar", lambda e, PT_=PT_: e.activation(out=sg[:], in_=PT_[:], func=AF.Sigmoid), reads=[ptn], writes=["i_sg"])
                        P.op("vector", lambda e, G2=G2: e.tensor_tensor(out=G2[:], in0=sg[:], in1=hg[:], op=ALU.mult), reads=["i_sg", "i_hg"], writes=["i_g2%d" % sbi])
                        P.dma("sync", lambda e, G2=G2, ts0=ts0: e.dma_start(out=G.MLG2[ts0:ts0 + 128, :], in_=G2[:]), reads=["i_g2%d" % sbi], writes=[_u()])
                while deferred:
                    deferred.pop(0)()
                cnt["s"] += 1
        P.barrier()
        P.emit()


def stage_na(G):
    P, I, nc = G.P, G.I, G.nc
    with contextlib.ExitStack() as st:
        sb, ps = _alloc(G, st)
        KT = sb("n_KT", [128, 4, NU], BF16)
        V = sb("n_V", [128, 38, 520], BF16)
        QT = sb("n_QT", [128, 4, NOWN], BF16)
        BI = sb("n_BI", [128, 5, 8, 5, 128], BF16)
        sAb = [sb("n_sAb%d" % b, [128, 4, 128], F32) for b in range(2)]
        sBb = [sb("n_sBb%d" % b, [128, 128], F32) for b in range(2)]
        pt = [sb("n_pt%d" % b, [128, 7, 128], BF16) for b in range(2)]
        ya = [sb("n_ya%d" % b, [128, 512], BF16) for b in range(2)]
        rec = [sb("n_rec%d" % b, [128, 8], F32) for b in range(2)]
        yt = [sb("n_yt%d" % b, [128, 4, 128], BF16) for b in range(2)]
        sA = [ps("n_sA%d" % b, [128, 4, 128]) for b in range(2)]
        sB = [ps("n_sB%d" % b, [128, 4, 128]) for b in range(2)]
        O = ps("n_O", [128, 8, 128])
        ptr = ps("n_ptr", [128, 4, 128], BF16)
        P.dma("sync", lambda e: e.dma_start(out=KT[:], in_=G.NAKT[:, :, :]), reads=["dramNA"], writes=["n_KT"])
        P.dma("sync", lambda e: e.dma_start(out=V[:], in_=G.NAV.rearrange("(c p) n -> p c n", p=128)), reads=["dramNA"], writes=["n_V"])
        P.dma("sync", lambda e: e.dma_start(out=QT[:], in_=G.NAQT[:, :, OWN0:OWN0 + NOWN]), reads=["dramNA"], writes=["n_QT"])
        for c5 in range(5):
            P.dma("gpsimd", lambda e, c5=c5: e.dma_start(out=BI[:, c5], in_=I["nabiasT"][c5].rearrange("h k j q -> k h j q")), writes=["n_BI"])
        hb = 0
        for p in range(32):
            cls = 0 if p == 0 else 1 if p == 1 else 3 if p == 30 else 4 if p == 31 else 2
            pb2 = p % 2
            for h in range(8):
                hc, b0 = h // 2, (h % 2) * 64
                a = hb % 2
                hb += 1
                q_ap = QT[b0:b0 + 64, hc, p * 128:(p + 1) * 128]
                SA, SB, PT = sA[a], sB[a], pt[a]
                san, sbn, ptn = "n_sA%d" % a, "n_sB%d" % a, "n_pt%d" % a
                for jj in range(4):
                    k0 = (p + jj) * 128
                    P.op("tensor", lambda e, SA=SA, jj=jj, k0=k0, q_ap=q_ap, hc=hc, b0=b0: e.matmul(
                        SA[:, jj, :], lhsT=KT[b0:b0 + 64, hc, k0:k0 + 128], rhs=q_ap, start=True, stop=True),
                        reads=["n_KT", "n_QT"], writes=[san], inc=(jj == 3))
                k0 = (p + 4) * 128
                P.op("tensor", lambda e, SB=SB, k0=k0, q_ap=q_ap, hc=hc, b0=b0: e.matmul(
                    SB[0:64, 0, :], lhsT=KT[b0:b0 + 64, hc, k0:k0 + 64], rhs=q_ap, start=True, stop=True),
                    reads=["n_KT", "n_QT"], writes=[sbn], inc=False)
                for c in range(2):
                    k0 = CTX0 + c * 128
                    P.op("tensor", lambda e, SB=SB, c=c, k0=k0, q_ap=q_ap, hc=hc, b0=b0: e.matmul(
                        SB[:, 1 + c, :], lhsT=KT[b0:b0 + 64, hc, k0:k0 + 128], rhs=q_ap, start=True, stop=True),
                        reads=["n_KT", "n_QT"], writes=[sbn], inc=(c == 1))
                AB, BB = sAb[a], sBb[a]
                abn, bbn = "n_sAb%d" % a, "n_sBb%d" % a
                P.op("vector", lambda e, SA=SA, AB=AB, cls=cls, h=h: e.scalar_tensor_tensor(
                    out=AB[:], in0=SA[:], scalar=0.125, in1=BI[:, cls, h, 0:4, :], op0=ALU.mult, op1=ALU.add),
                    reads=[san, "n_BI"], writes=[abn])
                P.op("vector", lambda e, SB=SB, BB=BB, cls=cls, h=h: e.scalar_tensor_tensor(
                    out=BB[0:64, :], in0=SB[0:64, 0, :], scalar=0.125, in1=BI[0:64, cls, h, 4, :], op0=ALU.mult, op1=ALU.add),
                    reads=[sbn, "n_BI"], writes=[bbn])
                P.op("scalar", lambda e, AB=AB, PT=PT: e.activation(out=PT[:, 0:4, :], in_=AB[:], func=AF.Exp), reads=[abn], writes=[ptn])
                P.op("scalar", lambda e, SB=SB, PT=PT: e.activation(out=PT[:, 5:7, :], in_=SB[:, 1:3, :], func=AF.Exp, scale=0.125), reads=[sbn, bbn], writes=[ptn])
                P.op("scalar", lambda e, BB=BB, PT=PT: e.activation(out=PT[0:64, 4, :], in_=BB[0:64, :], func=AF.Exp), reads=[bbn], writes=[ptn])
                specs = [(jj, 128, p + jj) for jj in range(4)] + [(4, 64, p + 4), (5, 128, 36), (6, 128, 37)]
                for si, (slot, nk, vc) in enumerate(specs):
                    P.op("tensor", lambda e, PT=PT, slot=slot, nk=nk, vc=vc, h=h, si=si: e.matmul(
                        O[:, h, 0:65], lhsT=PT[0:nk, slot, :], rhs=V[0:nk, vc, h * 65:(h + 1) * 65], start=(si == 0), stop=(si == 6)),
                        reads=[ptn, "n_V"], writes=["n_O"], inc=(si == 6))
            R, YA, YT_ = rec[pb2], ya[pb2], yt[pb2]
            rn, yan, ytn = "n_rec%d" % pb2, "n_ya%d" % pb2, "n_yt%d" % pb2
            P.op("vector", lambda e, R=R: e.reciprocal(R[:], O[:, :, 64]), reads=["n_O"], writes=[rn])
            for h in range(8):
                P.op("scalar", lambda e, R=R, YA=YA, h=h: e.activation(out=YA[:, h * 64:(h + 1) * 64], in_=O[:, h, 0:64], func=AF.Copy, scale=R[:, h:h + 1]),
                     reads=["n_O", rn], writes=[yan])
            for c in range(4):
                P.op("tensor", lambda e, YA=YA, c=c: e.transpose(ptr[:, c, :], YA[:, c * 128:(c + 1) * 128], G.ident_b[:]),
                     reads=[yan, "ident_b"], writes=["n_ptr"], inc=(c == 3))
            P.op("vector", lambda e, YT_=YT_: e.tensor_copy(YT_[:], ptr[:]), reads=["n_ptr"], writes=[ytn])
            P.dma("sync", lambda e, YT_=YT_, p=p: e.dma_start(out=G.YT[:, 0:4, p * 128:(p + 1) * 128], in_=YT_[:]), reads=[ytn], writes=[_u()])
        P.barrier()
        P.emit()


def stage_ml(G):
    P, I, nc = G.P, G.I, G.nc
    NCH = 32
    with contextlib.ExitStack() as st:
        sb, ps = _alloc(G, st)
        TOK = sb("l_TOK", [128, 34, 5, 8], F32)
        EBEND = sb("l_EBEND", [128, 8, 32], F32)
        ATOT = sb("l_ATOT", [128, 8], F32)
        with contextlib.ExitStack() as st1:
            sb1, ps1 = _alloc(G, st1)
            LI = sb1("l_LI", [64, NU], F32)
            SP = sb1("l_SP", [64, NU], F32)
            CL = sb1("l_CL", [64, NU], F32)
            CG = sb1("l_CG", [64, NU], F32)
            TM = sb1("l_TM", [64, NU], F32)
            OQ = [sb1("l_OQ%d" % b, [64, NU], F32) for b in range(2)]
            gbI = sb1("l_gbI", [64, 1], F32)
            gbF = sb1("l_gbF", [64, 1], F32)
            CE = sb1("l_CE", [64, 32], F32)
            ntot = sb1("l_ntot", [64, 2], F32)
            tot = sb1("l_tot", [64, 2], F32)
            SELM = sb1("l_SELM", [64, 8, 128], F32)
            ptr = [ps1("l_ptr%d" % b, [128, 8, 64]) for b in range(2)]
            pe = ps1("l_pe", [128, 8, 32])
            pa = ps1("l_pa", [128, 8, 2])
            own = slice(OWN0, OWN0 + NOWN)
            cxs = slice(CTX0, CTX0 + 256)
            P.dma("sync", lambda e: e.dma_start(out=LI[:], in_=G.GIF[0:64, :]), reads=["dramML"], writes=["l_LI"])
            P.dma("sync", lambda e: e.dma_start(out=SP[:], in_=G.GIF[64:128, :]), reads=["dramML"], writes=["l_SP"])
            P.dma("sync", lambda e: e.dma_start(out=gbI[:], in_=I["gate_b"][0:64, :]), writes=["l_gbI"])
            P.dma("sync", lambda e: e.dma_start(out=gbF[:], in_=I["gate_b"][64:128, :]), writes=["l_gbF"])
            P.dma("sync", lambda e: e.dma_start(out=SELM[:], in_=I["selm"][:, :, :]), writes=["l_SELM"])
            P.op("vector", lambda e: e.tensor_scalar(gbF[:], gbF[:], -1.0, None, ALU.mult), reads=["l_gbF"], writes=["l_gbF"])
            P.op("scalar", lambda e: e.activation(out=LI[:], in_=LI[:], func=AF.Identity, bias=gbI[:]), reads=["l_LI", "l_gbI"], writes=["l_LI"])
            P.op("scalar", lambda e: e.activation(out=SP[:], in_=SP[:], func=AF.Exp, bias=gbF[:], scale=-1.0), reads=["l_SP", "l_gbF"], writes=["l_SP"])
            P.op("scalar", lambda e: e.activation(out=SP[:], in_=SP[:], func=AF.Ln, bias=G.c_one[0:64, :]), reads=["l_SP", "c_one"], writes=["l_SP"])
            P.op("vector", lambda e: e.memset(TM[:], 1.0), writes=["l_TM"])
            P.op("vector", lambda e: e.tensor_tensor_scan(out=CG[:, own], data0=TM[:, own], data1=SP[:, own], initial=0.0, op0=ALU.mult, op1=ALU.add),
                 reads=["l_TM", "l_SP"], writes=["l_CG"])
            P.op("vector", lambda e: e.tensor_tensor_scan(out=CG[:, cxs], data0=TM[:, cxs], data1=SP[:, cxs], initial=0.0, op0=ALU.mult, op1=ALU.add),
                 reads=["l_TM", "l_SP"], writes=["l_CG"])
            TMo = TM[:, own].rearrange("p (c t) -> p c t", t=128)
            P.op("vector", lambda e: e.memset(TMo[:, :, 0:1], 0.0), reads=["l_CG"], writes=["l_TM"])
            P.op("vector", lambda e: e.tensor_tensor_scan(out=CL[:, own], data0=TM[:, own], data1=SP[:, own], initial=0.0, op0=ALU.mult, op1=ALU.add),
                 reads=["l_TM", "l_SP"], writes=["l_CL"])
            CLo = CL[:, own].rearrange("p (c t) -> p c t", t=128)
            SPo = SP[:, own].rearrange("p (c t) -> p c t", t=128)
            P.op("vector", lambda e: e.tensor_copy(CE[:], CLo[:, :, 127]), reads=["l_CL"], writes=["l_CE"])
            P.op("vector", lambda e: e.tensor_copy(tot[:, 0:1], CG[:, OWN0 + NOWN - 1:OWN0 + NOWN]), reads=["l_CG"], writes=["l_tot"])
            P.op("vector", lambda e: e.tensor_copy(tot[:, 1:2], CG[:, CTX0 + 255:CTX0 + 256]), reads=["l_CG"], writes=["l_tot"])
            P.op("vector", lambda e: e.tensor_scalar(ntot[:], tot[:], -1.0, None, ALU.mult), reads=["l_tot"], writes=["l_ntot"])
            for c in range(NCH):
                P.op("vector", lambda e, c=c: e.tensor_scalar(CLo[32:64, c, :], CLo[32:64, c, :], CE[32:64, c:c + 1], -1.0, ALU.subtract, ALU.mult),
                     reads=["l_CL", "l_CE"], writes=["l_CL"])
            P.op("vector", lambda e: e.tensor_tensor(out=CL[32:64, own], in0=CL[32:64, own], in1=SP[32:64, own], op=ALU.add),
                 reads=["l_CL", "l_SP"], writes=["l_CL"])
            for r in range(8):
                P.op("tensor", lambda e, r=r: e.matmul(pe[:, r, :], lhsT=SELM[:, r, :], rhs=CE[:], start=True, stop=True),
                     reads=["l_SELM", "l_CE"], writes=["l_pe"], inc=(r == 7))
            P.op("scalar", lambda e: e.activation(out=EBEND[:], in_=pe[:], func=AF.Exp, scale=-1.0), reads=["l_pe"], writes=["l_EBEND"])
            for r in range(8):
                P.op("tensor", lambda e, r=r: e.matmul(pa[:, r, :], lhsT=SELM[:, r, :], rhs=tot[:], start=True, stop=True),
                     reads=["l_SELM", "l_tot"], writes=["l_pa"], inc=(r == 7))
            P.op("scalar", lambda e: e.activation(out=ATOT[:], in_=pa[:, :, 0], func=AF.Exp, scale=-1.0), reads=["l_pa"], writes=["l_ATOT"])

            tcnt = {"n": 0}

            def transpose_out(Q, qn, qty, chunks):
                for g0 in range(0, len(chunks), 8):
                    grp = chunks[g0:g0 + 8]
                    a = tcnt["n"] % 2
                    tcnt["n"] += 1
                    for gi, (ci, col0) in enumerate(grp):
                        P.op("tensor", lambda e, a=a, gi=gi, col0=col0: e.transpose(ptr[a][:, gi, :], Q[0:64, col0:col0 + 128], G.ident_f[0:64, 0:64]),
                             reads=[qn, "ident_f"], writes=["l_ptr%d" % a], inc=(gi == len(grp) - 1))
                    c_first = grp[0][0]
                    ng_ = len(grp)
                    src = ptr[a][:, 0:ng_, :].rearrange("p g (d x) -> p g d x", d=2)[:, :, :, 0:4]
                    dst = TOK[:, c_first:c_first + ng_, qty, :].rearrange("p g (d x) -> p g d x", d=2)
                    P.op("vector", lambda e, src=src, dst=dst: e.tensor_copy(dst, src), reads=["l_ptr%d" % a], writes=["l_TOK"])

            own_chunks = [(c, OWN0 + c * 128) for c in range(NCH)]
            ctx_chunks = [(32 + c, CTX0 + c * 128) for c in range(2)]
            P.op("scalar", lambda e: e.activation(out=OQ[0][:, own], in_=CL[:, own], func=AF.Exp, scale=-1.0), reads=["l_CL"], writes=["l_OQ0"])
            transpose_out(OQ[0], "l_OQ0", 0, own_chunks)
            P.op("scalar", lambda e: e.activation(out=OQ[1][:, own], in_=CL[:, own], func=AF.Exp), reads=["l_CL"], writes=["l_OQ1"])
            transpose_out(OQ[1], "l_OQ1", 4, own_chunks)
            P.op("vector", lambda e: e.tensor_tensor(out=TM[:, own], in0=LI[:, own], in1=CL[:, own], op=ALU.add), reads=["l_LI", "l_CL"], writes=["l_TM"])
            P.op("scalar", lambda e: e.activation(out=OQ[1][:, own], in_=TM[:, own], func=AF.Exp), reads=["l_TM"], writes=["l_OQ1"])
            transpose_out(OQ[1], "l_OQ1", 1, own_chunks)
            TMo2 = TM[:, own].rearrange("p (c t) -> p c t", t=128)
            for c in range(NCH):
                P.op("vector", lambda e, c=c: e.tensor_scalar(TMo2[:, c, :], TMo2[:, c, :], CE[:, c:c + 1], None, ALU.subtract),
                     reads=["l_TM", "l_CE", "l_OQ1"], writes=["l_TM"])
            P.op("scalar", lambda e: e.activation(out=OQ[0][:, own], in_=TM[:, own], func=AF.Exp), reads=["l_TM"], writes=["l_OQ0"])
            transpose_out(OQ[0], "l_OQ0", 2, own_chunks)
            for (sl, ti_) in ((own, 0), (cxs, 1)):
                P.op("vector", lambda e, sl=sl: e.tensor_tensor(out=TM[0:32, sl], in0=LI[0:32, sl], in1=CG[0:32, sl], op=ALU.add),
                     reads=["l_LI", "l_CG"], writes=["l_TM"])
                P.op("vector", lambda e, sl=sl: e.tensor_tensor(out=TM[32:64, sl], in0=LI[32:64, sl], in1=CG[32:64, sl], op=ALU.subtract),
                     reads=["l_LI", "l_CG"], writes=["l_TM"])
                P.op("vector", lambda e, sl=sl: e.tensor_tensor(out=TM[32:64, sl], in0=TM[32:64, sl], in1=SP[32:64, sl], op=ALU.add),
                     reads=["l_TM", "l_SP"], writes=["l_TM"])
                P.op("scalar", lambda e, sl=sl, ti_=ti_: e.activation(out=OQ[1][0:32, sl], in_=TM[0:32, sl], func=AF.Exp, bias=ntot[0:32, ti_:ti_ + 1]),
                     reads=["l_TM", "l_ntot"], writes=["l_OQ1"])
                P.op("scalar", lambda e, sl=sl: e.activation(out=OQ[1][32:64, sl], in_=TM[32:64, sl], func=AF.Exp), reads=["l_TM"], writes=["l_OQ1"])
            transpose_out(OQ[1], "l_OQ1", 3, own_chunks + ctx_chunks)
            P.barrier()
            P.emit()

        KTOK = sb("l_KTOK", [128, 34, 512], BF16)
        VTOK = sb("l_VTOK", [128, 34, 516], BF16)
        TRI = sb("l_TRI", [128, 2, 128], F32)
        SELV = sb("l_SELV", [128, 16], F32)
        PAY = sb("l_PAY", [128, 8, 130], F32)
        GATH = sb("l_GATH", [128, 4, 1040], F32)
        STATE = sb("l_STATE", [128, 8, 129], F32)
        STB = sb("l_STB", [128, 8, 129], BF16)
        KA = [sb("l_KA%d" % b, [128, 128], BF16) for b in range(3)]
        alpha = sb("l_alpha", [128, 1], F32)
        tmpL = sb("l_tmpL", [128, 129], F32)
        QTc = [[sb("l_QTc%d%d" % (d_, b), [128, 4, 128], BF16) for b in range(2)] for d_ in range(2)]
        KTc = [[sb("l_KTc%d%d" % (d_, b), [128, 4, 128], BF16) for b in range(2)] for d_ in range(2)]
        HS = [[sb("l_HS%d%d" % (d_, b), [128, 512], F32) for b in range(2)] for d_ in range(2)]
        PTt = [sb("l_PT%d" % b, [128, 128], BF16) for b in range(2)]
        pS = [ps("l_pS%d" % b, [128, 128]) for b in range(2)]
        pU = [ps("l_pU%d" % b, [128, 132]) for b in range(4)]
        pN = [ps("l_pN%d" % b, [128, 132]) for b in range(2)]
        pL = [pU[0], pU[1]]
        den = [sb("l_den%d" % b, [128, 8], F32) for b in range(2)]
        P.dma("sync", lambda e: e.dma_start(out=KTOK[:, 0:32, :], in_=G.MLK[OWN0:OWN0 + NOWN, :].rearrange("(c p) n -> p c n", p=128)), reads=["dramML"], writes=["l_KTOK"])
        P.dma("sync", lambda e: e.dma_start(out=KTOK[:, 32:34, :], in_=G.MLK[CTX0:CTX0 + 256, :].rearrange("(c p) n -> p c n", p=128)), reads=["dramML"], writes=["l_KTOK"])
        P.dma("sync", lambda e: e.dma_start(out=VTOK[:, 0:32, :], in_=G.MLV[OWN0:OWN0 + NOWN, :].rearrange("(c p) n -> p c n", p=128)), reads=["dramML"], writes=["l_VTOK"])
        P.dma("sync", lambda e: e.dma_start(out=VTOK[:, 32:34, :], in_=G.MLV[CTX0:CTX0 + 256, :].rearrange("(c p) n -> p c n", p=128)), reads=["dramML"], writes=["l_VTOK"])
        P.dma("sync", lambda e: e.dma_start(out=TRI[:], in_=I["tri"][:, :, :]), writes=["l_TRI"])
        P.dma("sync", lambda e: e.dma_start(out=SELV[:], in_=I["selv"][:, :]), writes=["l_SELV"])
        kacnt = {"n": 0}

        def scaled_k(c, h, qty, r):
            a = kacnt["n"] % 3
            kacnt["n"] += 1
            eng = ("vector", "scalar", "scalar")[a]
            src = KTOK[:, c, h * 128:(h + 1) * 128]
            sc = TOK[:, c, qty, r:r + 1]
            if eng == "scalar":
                P.op("scalar", lambda e: e.activation(out=KA[a][:], in_=src, func=AF.Copy, scale=sc), reads=["l_KTOK", "l_TOK"], writes=["l_KA%d" % a])
            else:
                P.op(eng, lambda e: e.tensor_scalar(KA[a][:], src, sc, None, ALU.mult), reads=["l_KTOK", "l_TOK"], writes=["l_KA%d" % a])
            return KA[a], "l_KA%d" % a

        n2 = 0
        for r in range(8):
            h = r % 4
            for (chs, dstname) in ((list(range(32)), "own"), ([32, 33], "ctx")):
                pp, ppn = pL[n2 % 2], "l_pU%d" % (n2 % 2)
                n2 += 1
                for i_, c in enumerate(chs):
                    ka, kan = scaled_k(c, h, 3, r)
                    P.op("tensor", lambda e, pp=pp, ka=ka, c=c, h=h, i_=i_, L=len(chs): e.matmul(
                        pp[:, 0:129], lhsT=ka[:], rhs=VTOK[:, c, h * 129:(h + 1) * 129], start=(i_ == 0), stop=(i_ == L - 1)),
                        reads=[kan, "l_VTOK"], writes=[ppn], inc=True)
                if dstname == "own":
                    P.op("vector", lambda e, pp=pp, r=r: e.tensor_copy(PAY[:, r, 0:129], pp[:, 0:129]), reads=[ppn], writes=["l_PAY"])
                else:
                    P.op("vector", lambda e, pp=pp, r=r: e.tensor_copy(STATE[:, r, :], pp[:, 0:129]), reads=[ppn], writes=["l_STATE"])
        P.op("vector", lambda e: e.tensor_copy(PAY[:, :, 129], ATOT[:]), reads=["l_ATOT", "l_PAY"], writes=["l_PAY"])
        P.dma("sync", lambda e: e.dma_start(out=G.CCI[:, :], in_=PAY[:].rearrange("p r n -> p (r n)")), reads=["l_PAY"], writes=["CCI"])
        for _rep in range(3):
            P.cc(lambda e: e.collective_compute("AllGather", ALU.bypass, replica_groups=[[0, 1, 2, 3], [4, 5, 6, 7]],
                                                ins=[G.CCI.opt()], outs=[G.CCO.opt()]), reads=["CCI"], writes=["CCO"])
        P.dma("sync", lambda e: e.dma_start(out=GATH[:], in_=G.CCO.rearrange("(j p) n -> p j n", p=128)), reads=["CCO"], writes=["l_GATH"])
        for r in range(8):
            d = r // 4
            order = range(4) if d == 0 else range(3, -1, -1)
            for jseg in order:
                so = 0 if d == 0 else 8
                A_j = GATH[:, jseg, r * 130 + 129:r * 130 + 130]
                L_j = GATH[:, jseg, r * 130:r * 130 + 129]
                P.op("vector", lambda e, A_j=A_j, so=so, jseg=jseg: e.tensor_scalar(
                    alpha[:], A_j, SELV[:, so + jseg:so + jseg + 1], SELV[:, so + 4 + jseg:so + 5 + jseg], ALU.mult, ALU.add),
                    reads=["l_GATH", "l_SELV"], writes=["l_alpha"])
                P.op("vector", lambda e, L_j=L_j, so=so, jseg=jseg: e.tensor_scalar(tmpL[:], L_j, SELV[:, so + jseg:so + jseg + 1], None, ALU.mult),
                     reads=["l_GATH", "l_SELV"], writes=["l_tmpL"])
                P.op("vector", lambda e, r=r: e.scalar_tensor_tensor(out=STATE[:, r, :], in0=STATE[:, r, :], scalar=alpha[:], in1=tmpL[:], op0=ALU.mult, op1=ALU.add),
                     reads=["l_STATE", "l_alpha", "l_tmpL"], writes=["l_STATE"])
        P.op("scalar", lambda e: e.activation(out=STB[:], in_=STATE[:], func=AF.Copy), reads=["l_STATE"], writes=["l_STB"])

        def ld4(i):
            if i >= NCH:
                return
            bb = i % 2
            for d_ in range(2):
                c_ = i if d_ == 0 else NCH - 1 - i
                tk0 = OWN0 + c_ * 128
                P.dma("sync", lambda e, bb=bb, d_=d_, tk0=tk0: e.dma_start(out=QTc[d_][bb][:], in_=G.MLQT[:, :, tk0:tk0 + 128]), writes=["l_QTc%d%d" % (d_, bb)])
                P.dma("sync", lambda e, bb=bb, d_=d_, tk0=tk0: e.dma_start(out=KTc[d_][bb][:], in_=G.MLKT[:, :, tk0:tk0 + 128]), writes=["l_KTc%d%d" % (d_, bb)])
        ld4(0)
        for i in range(NCH):
            b = i % 2
            ld4(i + 1)
            cs = (i, NCH - 1 - i)
            for hh_ in range(2):
                items = [(h, d) for h in (2 * hh_, 2 * hh_ + 1) for d in range(2)]
                for (h, d) in items:
                    c = cs[d]
                    r = d * 4 + h
                    u = (h % 2) * 2 + d
                    qn, kn = "l_QTc%d%d" % (d, b), "l_KTc%d%d" % (d, b)
                    P.op("tensor", lambda e, b=b, h=h, d=d: e.matmul(pS[d][:], lhsT=KTc[d][b][:, h, :], rhs=QTc[d][b][:, h, :], start=True, stop=True),
                         reads=[kn, qn], writes=["l_pS%d" % d], inc=True)
                    P.op("vector", lambda e, c=c, r=r, d=d: e.scalar_tensor_tensor(
                        out=PTt[d][:], in0=pS[d][:], scalar=TOK[:, c, 1, r:r + 1], in1=TRI[:, d, :], op0=ALU.mult, op1=ALU.mult),
                        reads=["l_pS%d" % d, "l_TOK", "l_TRI"], writes=["l_PT%d" % d])
                    P.op("tensor", lambda e, c=c, h=h, d=d, u=u: e.matmul(pU[u][:, 0:129], lhsT=PTt[d][:], rhs=VTOK[:, c, h * 129:(h + 1) * 129], start=True, stop=False),
                         reads=["l_PT%d" % d, "l_VTOK"], writes=["l_pU%d" % u], inc=False)
                    P.op("tensor", lambda e, b=b, h=h, d=d, r=r, u=u: e.matmul(pU[u][:, 0:129], lhsT=QTc[d][b][:, h, :], rhs=STB[:, r, :], start=False, stop=True),
                         reads=[qn, "l_STB%d" % r, "l_STB"], writes=["l_pU%d" % u], inc=True)
                for (h, d) in items:
                    c = cs[d]
                    r = d * 4 + h
                    a = r % 2
                    ka, kan = scaled_k(c, h, 2, r)
                    P.op("tensor", lambda e, a=a, ka=ka, c=c, h=h: e.matmul(pN[a][:, 0:129], lhsT=ka[:], rhs=VTOK[:, c, h * 129:(h + 1) * 129], start=True, stop=True),
                         reads=[kan, "l_VTOK"], writes=["l_pN%d" % a], inc=True)
                    P.op("vector", lambda e, a=a, r=r, c=c: e.scalar_tensor_tensor(
                        out=STATE[:, r, :], in0=STATE[:, r, :], scalar=EBEND[:, r, c:c + 1], in1=pN[a][:, 0:129], op0=ALU.mult, op1=ALU.add),
                        reads=["l_STATE%d" % r, "l_EBEND", "l_pN%d" % a, "l_STATE"], writes=["l_STATE%d" % r])
                    P.op("scalar", lambda e, r=r: e.activation(out=STB[:, r, :], in_=STATE[:, r, :], func=AF.Copy),
                         reads=["l_STATE%d" % r], writes=["l_STB%d" % r])
                for (h, d) in items:
                    u = (h % 2) * 2 + d
                    dn, dnn = den[d], "l_den%d" % d
                    P.op("scalar", lambda e, dn=dn, u=u, h=h: e.activation(out=dn[:, h:h + 1], in_=pU[u][:, 128:129], func=AF.Abs),
                         reads=["l_pU%d" % u], writes=[dnn])
                for d in range(2):
                    c = cs[d]
                    dn, dnn = den[d], "l_den%d" % d
                    h0 = 2 * hh_
                    REB2 = TOK[:, c, 4, d * 4 + h0:d * 4 + h0 + 2]
                    P.op("vector", lambda e, dn=dn, REB2=REB2, h0=h0: e.tensor_tensor(out=dn[:, h0:h0 + 2], in0=dn[:, h0:h0 + 2], in1=REB2, op=ALU.max),
                         reads=[dnn, "l_TOK"], writes=[dnn])
                    P.op("vector", lambda e, dn=dn, h0=h0: e.reciprocal(dn[:, h0:h0 + 2], dn[:, h0:h0 + 2]), reads=[dnn], writes=[dnn])
                for (h, d) in items:
                    u = (h % 2) * 2 + d
                    dn, dnn = den[d], "l_den%d" % d
                    hsn = "l_HS%d%d" % (d, b)
                    if d == 0:
                        P.op("scalar", lambda e, b=b, h=h, dn=dn, u=u: e.activation(out=HS[0][b][:, h * 128:(h + 1) * 128], in_=pU[u][:, 0:128], func=AF.Copy, scale=dn[:, h:h + 1]),
                             reads=["l_pU%d" % u, dnn], writes=[hsn])
                    else:
                        P.op("vector", lambda e, b=b, h=h, dn=dn, u=u: e.tensor_scalar(HS[1][b][:, h * 128:(h + 1) * 128], pU[u][:, 0:128], dn[:, h:h + 1], None, ALU.mult),
                             reads=["l_pU%d" % u, dnn], writes=[hsn])
            P.dma("sync", lambda e, b=b, c=cs[0]: e.dma_start(out=G.HF[c * 128:(c + 1) * 128, :], in_=HS[0][b][:]), reads=["l_HS0%d" % b], writes=[_u()])
            P.dma("sync", lambda e, b=b, c=cs[1]: e.dma_start(out=G.HB[c * 128:(c + 1) * 128, :], in_=HS[1][b][:]), reads=["l_HS1%d" % b], writes=[_u()])
        P.barrier()
        P.emit()

    with contextlib.ExitStack() as st:
        sb, ps = _alloc(G, st)
        NB = 3
        hf = [sb("r_hf%d" % b, [128, 512], F32) for b in range(NB)]
        hb = [sb("r_hb%d" % b, [128, 512], F32) for b in range(NB)]
        g2 = [sb("r_g2%d" % b, [128, 512], F32) for b in range(NB)]
        junk = sb("r_junk", [128, 128], F32)
        ssq = [sb("r_ssq%d" % b, [128, 4], F32) for b in range(2)]
        Yb = [sb("r_Y%d" % b, [128, 512], BF16) for b in range(2)]
        ytb = [sb("r_yt%d" % b, [128, 4, 128], BF16) for b in range(2)]
        ptr2 = [ps("r_ptr%d" % b, [128, 4, 128], BF16) for b in range(2)]

        def ld5(c):
            if c >= NCH:
                return
            b3 = c % NB
            tk0 = OWN0 + c * 128
            P.dma("sync", lambda e, b3=b3, c=c: e.dma_start(out=hf[b3][:], in_=G.HF[c * 128:(c + 1) * 128, :]), writes=["r_hf%d" % b3])
            P.dma("sync", lambda e, b3=b3, c=c: e.dma_start(out=hb[b3][:], in_=G.HB[c * 128:(c + 1) * 128, :]), writes=["r_hb%d" % b3])
            P.dma("sync", lambda e, b3=b3, tk0=tk0: e.dma_start(out=g2[b3][:], in_=G.MLG2[tk0:tk0 + 128, :]), writes=["r_g2%d" % b3])
        ld5(0)
        ld5(1)
        for c in range(NCH):
            ld5(c + 2)
            b3, b = c % NB, c % 2
            P.op("vector", lambda e, b3=b3: e.tensor_tensor(out=hf[b3][:], in0=hf[b3][:], in1=hb[b3][:], op=ALU.add),
                 reads=["r_hf%d" % b3, "r_hb%d" % b3], writes=["r_hf%d" % b3])
            for h in range(4):
                P.op("scalar", lambda e, b3=b3, b=b, h=h: e.activation(out=junk[:], in_=hf[b3][:, h * 128:(h + 1) * 128], func=AF.Square, accum_out=ssq[b][:, h:h + 1]),
                     reads=["r_hf%d" % b3], writes=["r_junk", "r_ssq%d" % b])
            P.op("scalar", lambda e, b=b: e.activation(out=ssq[b][:], in_=ssq[b][:], func=AF.Sqrt, bias=G.c_eps[:], scale=1.0 / 128.0),
                 reads=["r_ssq%d" % b, "c_eps"], writes=["r_ssq%d" % b])
            P.op("vector", lambda e, b=b: e.reciprocal(ssq[b][:], ssq[b][:]), reads=["r_ssq%d" % b], writes=["r_ssq%d" % b])
            for h in range(4):
                P.op("vector", lambda e, b3=b3, b=b, h=h: e.scalar_tensor_tensor(
                    out=Yb[b][:, h * 128:(h + 1) * 128], in0=hf[b3][:, h * 128:(h + 1) * 128], scalar=ssq[b][:, h:h + 1],
                    in1=g2[b3][:, h * 128:(h + 1) * 128], op0=ALU.mult, op1=ALU.mult),
                    reads=["r_hf%d" % b3, "r_ssq%d" % b, "r_g2%d" % b3], writes=["r_Y%d" % b])
            for h in range(4):
                P.op("tensor", lambda e, b=b, h=h: e.transpose(ptr2[b][:, h, :], Yb[b][:, h * 128:(h + 1) * 128], G.ident_b[:]),
                     reads=["r_Y%d" % b, "ident_b"], writes=["r_ptr%d" % b], inc=(h == 3))
            P.op("scalar", lambda e, b=b: e.activation(out=ytb[b][:], in_=ptr2[b][:], func=AF.Copy), reads=["r_ptr%d" % b], writes=["r_yt%d" % b])
            P.dma("sync", lambda e, b=b, c=c: e.dma_start(out=G.YT[:, 4:8, c * 128:(c + 1) * 128], in_=ytb[b][:]), reads=["r_yt%d" % b], writes=[_u()])
        P.barrier()
        P.emit()


def stage_outproj(G):
    P, I, nc = G.P, G.I, G.nc
    l, j, m = 0, 1, 0
    with contextlib.ExitStack() as st:
        sb, ps = _alloc(G, st)
        wo = sb("o_wo", [128, 8, 1024], BF16)
        xt = [sb("o_xt%d" % b, [128, 8, TW], F32) for b in range(2)]
        yt = [sb("o_yt%d" % b, [128, 8, TW], BF16) for b in range(2)]
        po = [ps("o_po%d" % b, [128, TW]) for b in range(2)]
        load_w_cast(G, wo, "o_wo", I["mix_w_out"], 8, 1024)
        def ld(n):
            if n >= 16:
                return
            ti_, b_ = n + 1, n % 2
            P.dma("sync", lambda e, b_=b_, ti_=ti_: e.dma_start(out=xt[b_][:], in_=G.XT[:, :, ti_ * TW:(ti_ + 1) * TW]), reads=["XT%d" % ti_], writes=["o_xt%d" % b_])
            P.dma("sync", lambda e, b_=b_, n=n: e.dma_start(out=yt[b_][:], in_=G.YT[:, :, n * TW:(n + 1) * TW]), reads=["dramYT"], writes=["o_yt%d" % b_])
        ld(0)
        ld(1)
        for n in range(16):
            ti = n + 1
            t0 = ti * TW
            b = n % 2
            for f in range(8):
                q = f % 2
                for k in range(8):
                    P.op("tensor", lambda e, q=q, f=f, k=k, b=b: e.matmul(po[q][:], lhsT=wo[:, k, f * 128:(f + 1) * 128], rhs=yt[b][:, k, :], start=(k == 0), stop=(k == 7)),
                         reads=["o_wo", "o_yt%d" % b], writes=["o_po%d" % q], inc=(k == 7))
                gsc = G.GT[:, l, j, m, f:f + 1]
                P.op("vector", lambda e, q=q, f=f, b=b, gsc=gsc: e.scalar_tensor_tensor(
                    out=xt[b][:, f, :], in0=po[q][:], scalar=gsc, in1=xt[b][:, f, :], op0=ALU.mult, op1=ALU.add),
                    reads=["o_po%d" % q, "modc", "o_xt%d" % b], writes=["o_xt%d" % b])
            P.dma("sync", lambda e, b=b, t0=t0: e.dma_start(out=G.XT[:, :, t0:t0 + TW], in_=xt[b][:]), reads=["o_xt%d" % b], writes=["XT%d" % ti])
            ld(n + 2)
        P.barrier()
        P.emit()


def stage_sg(G):
    P, I, nc = G.P, G.I, G.nc
    l, j, m = 1, 1, 0
    with contextlib.ExitStack() as st:
        sb, ps = _alloc(G, st)
        wu = sb("g_wu", [128, 8, 2048], BF16)
        wv = sb("g_wv", [128, 8, 2048], BF16)
        wo = sb("g_wo", [128, 16, 1024], BF16)
        wsT = sb("g_wsT", [128, 8, 128], BF16)
        bs = sb("g_bs", [1, 1024], BF16)
        ones1 = sb("g_ones1", [1, 256], BF16)
        lng = sb("g_lng", [128, 2048], F32)
        lnb = sb("g_lnb", [128, 2048], F32)
        xt = [sb("g_xt%d" % b, [128, 8, TW], F32) for b in range(2)]
        hh = [sb("g_h%d" % b, [128, 8, TW], BF16) for b in range(2)]
        NS = NormScratch(G, sb, ps, "g_")
        uT = sb("g_uT", [128, 16, TW], BF16)
        vraw = [sb("g_vraw%d" % b, [128, 2048], F32) for b in range(2)]
        vn = [sb("g_vn%d" % b, [128, 2048], BF16) for b in range(2)]
        gated = sb("g_gated", [128, 16, TW], BF16)
        stats = [sb("g_stats%d" % b, [128, 4, 6], F32) for b in range(2)]
        mv = [sb("g_mv%d" % b, [128, 4], F32) for b in range(2)]
        pu = [ps("g_pu%d" % b, [128, TW]) for b in range(2)]
        pm = [ps("g_pm%d" % b, [128, TW]) for b in range(2)]
        po1 = ps("g_po", [128, TW])
        po = [po1, po1]
        pv = [ps("g_pv%d" % b, [128, 512]) for b in range(2)]
        load_w_cast(G, wu, "g_wu", I["sg_w_in"][:, 0:2048], 8, 2048)
        load_w_cast(G, wv, "g_wv", I["sg_w_in"][:, 2048:4096], 8, 2048)
        load_w_cast(G, wo, "g_wo", I["sg_w_out"], 16, 1024)
        P.dma("gpsimd", lambda e: e.dma_start(out=wsT[:], in_=I["sg_w_sT"][:, :, :]), writes=["g_wsT"])
        P.dma("gpsimd", lambda e: e.dma_start(out=bs[:], in_=I["sg_b_s"][:, :]), writes=["g_bs"])
        P.op("vector", lambda e: e.memset(ones1[:], 1.0), writes=["g_ones1"])
        P.dma("sync", lambda e: e.dma_start(out=lng[:], in_=I["sg_lng_bc"][:, :]), writes=["g_lng"])
        P.dma("sync", lambda e: e.dma_start(out=lnb[:], in_=I["sg_lnb_bc"][:, :]), writes=["g_lnb"])
        tiles = list(range(1, 17))

        def ld(n):
            ti_ = tiles[n]
            xb = xt[n % 2]
            P.dma("sync", lambda e, xb=xb, ti_=ti_: e.dma_start(out=xb[:], in_=G.XT[:, :, ti_ * TW:(ti_ + 1) * TW]),
                  reads=["XT%d" % ti_], writes=["g_xt%d" % (n % 2)])
        def do_norm(n):
            bb = n % 2
            norm_mod(G, NS, xt[bb], "g_xt%d" % bb, G.GM[:, l, j, m, :], G.SH[:, l, j, m, :], hh[bb], "g_h%d" % bb)
        ld(0)
        ld(1)
        do_norm(0)
        vcnt = 0
        for n, ti in enumerate(tiles):
            t0 = ti * TW
            b = n % 2
            x, xn = xt[b], "g_xt%d" % b
            h, hn = hh[b], "g_h%d" % b
            NSB = TW // 128
            for s_ in range(NSB):
                VR = vraw[s_]
                for blk in range(4):
                    q = blk % 2
                    for k in range(8):
                        P.op("tensor", lambda e, q=q, blk=blk, k=k, h=h, s_=s_: e.matmul(
                            pv[q][:], lhsT=h[:, k, s_ * 128:(s_ + 1) * 128], rhs=wv[:, k, blk * 512:(blk + 1) * 512], start=(k == 0), stop=(k == 7)),
                            reads=["g_wv", hn], writes=["g_pv%d" % q], inc=(k == 7))
                    P.op("scalar", lambda e, q=q, blk=blk, VR=VR: e.activation(out=VR[:, blk * 512:(blk + 1) * 512], in_=pv[q][:], func=AF.Gelu_apprx_tanh),
                         reads=["g_pv%d" % q], writes=["g_vraw%d" % s_])
                    P.op("vector", lambda e, blk=blk, VR=VR, s_=s_: e.bn_stats(stats[s_][:, blk, :], VR[:, blk * 512:(blk + 1) * 512]),
                         reads=["g_vraw%d" % s_], writes=["g_stats%d" % s_])
                P.op("vector", lambda e, s_=s_: e.bn_aggr(mv[s_][:, 0:2], stats[s_][:].rearrange("p a b -> p (a b)")), reads=["g_stats%d" % s_], writes=["g_mv%d" % s_])
                P.op("scalar", lambda e, s_=s_: e.activation(out=mv[s_][:, 2:3], in_=mv[s_][:, 1:2], func=AF.Sqrt, bias=G.c_eps[:], scale=1.0),
                     reads=["g_mv%d" % s_, "c_eps"], writes=["g_mv%d" % s_])
                P.op("vector", lambda e, s_=s_: e.reciprocal(mv[s_][:, 2:3], mv[s_][:, 2:3]), reads=["g_mv%d" % s_], writes=["g_mv%d" % s_])
                P.op("vector", lambda e, s_=s_: e.tensor_scalar(mv[s_][:, 3:4], mv[s_][:, 0:1], mv[s_][:, 2:3], -1.0, ALU.mult, ALU.mult),
                     reads=["g_mv%d" % s_], writes=["g_mv%d" % s_])
            for fc in range(16):
                q = fc % 2
                if fc == 6:
                    vns = []
                    for s_ in range(NSB):
                        VR = vraw[s_]
                        VN, vnn = vn[vcnt % 2], "g_vn%d" % (vcnt % 2)
                        vcnt += 1
                        vns.append((VN, vnn))
                        P.op("scalar", lambda e, VR=VR, s_=s_: e.activation(out=VR[:], in_=VR[:], func=AF.Identity, bias=mv[s_][:, 3:4], scale=mv[s_][:, 2:3]),
                             reads=["g_vraw%d" % s_, "g_mv%d" % s_], writes=["g_vraw%d" % s_])
                        P.op("vector", lambda e, VR=VR: e.tensor_tensor(out=VR[:], in0=VR[:], in1=lng[:], op=ALU.mult),
                             reads=["g_vraw%d" % s_, "g_lng"], writes=["g_vraw%d" % s_])
                        P.op("vector", lambda e, VR=VR, VN=VN: e.tensor_tensor(out=VN[:], in0=VR[:], in1=lnb[:], op=ALU.add),
                             reads=["g_vraw%d" % s_, "g_lnb"], writes=[vnn])
                for k in range(8):
                    P.op("tensor", lambda e, q=q, fc=fc, k=k, h=h: e.matmul(pu[q][:], lhsT=wu[:, k, fc * 128:(fc + 1) * 128], rhs=h[:, k, :], start=(k == 0), stop=(k == 7)),
                         reads=["g_wu", hn], writes=["g_pu%d" % q], inc=(k == 7))
                P.op("scalar", lambda e, q=q, fc=fc: e.activation(out=uT[:, fc, :], in_=pu[q][:], func=AF.Gelu_apprx_tanh), reads=["g_pu%d" % q], writes=["g_uT%d" % fc])
            for s_ in range(NSB):
                VN, vnn = vns[s_]
                for fc in range(16):
                    g_ = fc // 2
                    q = fc % 2
                    P.op("tensor", lambda e, q=q, fc=fc, g_=g_, VN=VN, s_=s_: e.matmul(
                        pm[q][:, s_ * 128:(s_ + 1) * 128], lhsT=VN[:, fc * 128:(fc + 1) * 128], rhs=wsT[:, g_, :], start=True, stop=False),
                        reads=[vnn, "g_wsT"], writes=["g_pm%d" % q], inc=False)
                    P.op("tensor", lambda e, q=q, g_=g_, s_=s_: e.matmul(
                        pm[q][:, s_ * 128:(s_ + 1) * 128], lhsT=ones1[0:1, 0:128], rhs=bs[0:1, g_ * 128:(g_ + 1) * 128], start=False, stop=True),
                        reads=["g_ones1", "g_bs"], writes=["g_pm%d" % q], inc=True)
                    P.op("vector", lambda e, q=q, fc=fc, s_=s_: e.tensor_tensor(
                        out=gated[:, fc, s_ * 128:(s_ + 1) * 128], in0=uT[:, fc, s_ * 128:(s_ + 1) * 128], in1=pm[q][:, s_ * 128:(s_ + 1) * 128], op=ALU.mult),
                        reads=["g_uT%d" % fc, "g_pm%d" % q], writes=["g_gated%d" % fc])
            if n + 1 < len(tiles):
                do_norm(n + 1)
            for f in range(8):
                q = f % 2
                for fc in range(16):
                    P.op("tensor", lambda e, q=q, f=f, fc=fc: e.matmul(po[q][:], lhsT=wo[:, fc, f * 128:(f + 1) * 128], rhs=gated[:, fc, :], start=(fc == 0), stop=(fc == 15)),
                         reads=["g_wo", "g_gated%d" % fc], writes=["g_po"], inc=(fc == 15))
                gsc = G.GT[:, l, j, m, f:f + 1]
                P.op("vector", lambda e, q=q, f=f, x=x, gsc=gsc: e.scalar_tensor_tensor(
                    out=x[:, f, :], in0=po[q][:], scalar=gsc, in1=x[:, f, :], op0=ALU.mult, op1=ALU.add),
                    reads=["g_po", "modc", xn], writes=[xn])
            P.dma("sync", lambda e, x=x, t0=t0: e.dma_start(out=G.XT[:, :, t0:t0 + TW], in_=x[:]), reads=[xn], writes=["XT%d" % ti])
            if n + 2 < len(tiles):
                ld(n + 2)
        P.barrier()
        P.emit()


_NC_CACHE = {}


def kernel(**inputs):
    maps = _host_prep(inputs)
    if "nc" not in _NC_CACHE:
        _NC_CACHE["nc"] = build()
    nc = _NC_CACHE["nc"]
    res = run_bass_kernel_spmd(nc, maps, core_ids=list(range(8)))
    out = np.zeros((2, 16384, 1024), np.float32)
    for core in range(8):
        b, s = core // 4, core % 4
        out[b, s * 4096:(s + 1) * 4096, :] = res.results[core]["out"]
    return out
```

```python
import contextlib
import numpy as np
import concourse.bass as bass
import concourse.mybir as mybir
from concourse.bass_utils import run_bass_kernel_spmd

F32 = mybir.dt.float32
BF16 = mybir.dt.bfloat16
AF = mybir.ActivationFunctionType
ALU = mybir.AluOpType
AX = mybir.AxisListType

D = 1024
DFF = 2816
NE = 4608
NU = 4864
OWN0 = 256
NOWN = 4096
CTX0 = 4608
TW = 256
EPS = 1e-6
NEG = -30000.0

ENGS = ("tensor", "vector", "scalar", "gpsimd", "sync")
NDMASEM = 32
NHW = 24


class Buf:
    __slots__ = ("name", "writers", "readers")

    def __init__(self, name):
        self.name = name
        self.writers = []
        self.readers = []


class Prog:
    def __init__(self, nc, st):
        self.nc = nc
        self.q = {e: [] for e in ENGS}
        self.cnt = {e: 0 for e in ENGS}
        self.seen = {e: {} for e in ENGS}
        self.dcnt = [0] * NDMASEM
        self.dnext = 0
        self.dnext_sw = 0
        self.bufs = {}
        self.esem = {e: st.enter_context(nc.semaphore("s_" + e)) for e in ENGS}
        self.dsem = [st.enter_context(nc.semaphore("d%d" % i)) for i in range(NDMASEM)]
        self.lastinc = {e: True for e in ENGS}

    def buf(self, name):
        b = self.bufs.get(name)
        if b is None:
            b = self.bufs[name] = Buf(name)
        return b

    def _bl(self, lst):
        return [self.buf(b) if isinstance(b, str) else b for b in lst]

    def _deps(self, reads, writes):
        deps = {}
        for b in reads:
            for k, v in b.writers:
                if deps.get(k, 0) < v:
                    deps[k] = v
        for b in writes:
            for k, v in b.writers:
                if deps.get(k, 0) < v:
                    deps[k] = v
            for k, v in b.readers:
                if deps.get(k, 0) < v:
                    deps[k] = v
        return deps

    def _waits(self, eng, deps):
        seen = self.seen[eng]
        waits = []
        for k, v in deps.items():
            if k == "tensor" and eng == "tensor":
                continue
            if seen.get(k, 0) >= v:
                continue
            seen[k] = v
            waits.append((k, v))
        return waits

    def _record(self, ev, reads, writes):
        k = ev[0]
        for b in writes:
            b.writers = [ev]
            b.readers = []
        for b in reads:
            if b in writes:
                continue
            b.readers = [e for e in b.readers if e[0] != k] + [ev]

    def op(self, eng, fn, reads=(), writes=(), inc=True):
        reads = self._bl(reads)
        writes = self._bl(writes)
        waits = self._waits(eng, self._deps(reads, writes))
        if inc:
            self.cnt[eng] += 1
            ev = (eng, self.cnt[eng])
        else:
            ev = (eng, self.cnt[eng] + 1)
        self.lastinc[eng] = inc
        self.q[eng].append(("op", fn, waits, inc))
        self._record(ev, reads, writes)

    def dma(self, eng, fn, reads=(), writes=()):
        reads = self._bl(reads)
        writes = self._bl(writes)
        deps = self._deps(reads, writes)
        if eng == "gpsimd":
            i = NHW + self.dnext_sw
            self.dnext_sw = (self.dnext_sw + 1) % (NDMASEM - NHW)
        else:
            i = self.dnext
            self.dnext = (self.dnext + 1) % NHW
        key = ("d", i)
        if self.dcnt[i] > 0 and deps.get(key, 0) < self.dcnt[i]:
            deps[key] = self.dcnt[i]
        waits = self._waits(eng, deps)
        self.dcnt[i] += 16
        ev = (key, self.dcnt[i])
        self.q[eng].append(("dma", fn, waits, i))
        self._record(ev, reads, writes)

    def cc(self, fn, reads=(), writes=()):
        self.op("gpsimd", fn, reads=reads, writes=writes, inc=True)

    def barrier(self):
        for e in ENGS:
            assert self.lastinc[e], e
        deps = {e: self.cnt[e] for e in ENGS if self.cnt[e] > 0}
        for i in range(NDMASEM):
            if self.dcnt[i] > 0:
                deps[("d", i)] = self.dcnt[i]
        for e in ENGS:
            d = {k: v for k, v in deps.items() if k != e}
            waits = self._waits(e, d)
            self.q[e].append(("wait", None, waits, None))

    def emit(self):
        nc = self.nc
        esem, dsem = self.esem, self.dsem

        def semof(k):
            return esem[k] if isinstance(k, str) else dsem[k[1]]

        def run(engname):
            items = self.q[engname]

            def body(e):
                for kind, fn, waits, x in items:
                    for k, v in waits:
                        e.wait_ge(semof(k), v)
                    if kind == "op":
                        ins = fn(e)
                        if x:
                            ins.then_inc(esem[engname], 1)
                    elif kind == "dma":
                        fn(e).then_inc(dsem[x], 16)
            return body

        with nc.Block() as block:
            block.tensor(run("tensor"))
            block.vector(run("vector"))
            block.scalar(run("scalar"))
            block.gpsimd(run("gpsimd"))
            block.sync(run("sync"))
        self.q = {e: [] for e in ENGS}


def _rowmap(s):
    rm = np.zeros(72, np.int64)
    rm[4:68] = s * 64 + np.arange(64)
    rm[0:4] = (s * 64 - 4 + np.arange(4)) if s > 0 else np.array([5, 6, 7, 8])
    rm[68:72] = (s * 64 + 64 + np.arange(4)) if s < 3 else np.array([248, 249, 250, 251])
    return rm


def _na_bias_tables(rpb, s):
    rm = _rowmap(s)
    out = np.full((5, 8, 128, 576), NEG, np.float32)
    reps = [0, 1, 2, 30, 31]
    cols = np.arange(64)
    c0 = np.clip(cols - 8, 0, 64 - 16)
    for ci, p in enumerate(reps):
        for r in range(2):
            i = s * 64 + 2 * p + r
            r0 = min(max(i - 4, 0), 256 - 8)
            seen_rows = set()
            for j in range(9):
                krow = int(rm[2 * p + j])
                if krow < r0 or krow >= r0 + 8 or krow in seen_rows:
                    continue
                seen_rows.add(krow)
                rr = krow - i + 7
                for qc in range(64):
                    kc = np.arange(c0[qc], c0[qc] + 16)
                    out[ci][:, r * 64 + qc, j * 64 + kc] = rpb[:, rr, kc - qc + 15]
    return out


def _rope_tables(s):
    rm = _rowmap(s)
    row = np.repeat(rm, 64).astype(np.float32)
    col = np.tile(np.arange(64), 72).astype(np.float32)
    inv = (10000.0 ** (-np.arange(32, dtype=np.float32) / 32)).astype(np.float32)
    ang = np.concatenate([row[:, None] * inv, col[:, None] * inv], axis=-1).astype(np.float32)
    cos = np.cos(ang).astype(np.float32)
    sin = np.sin(ang).astype(np.float32)
    cos2 = np.repeat(cos, 2, axis=1)
    sin2 = np.repeat(sin, 2, axis=1)
    sin2[:, 0::2] *= -1.0
    cos_u = np.ones((NU, 128), np.float32)
    sin_u = np.zeros((NU, 128), np.float32)
    cos_u[:NE] = cos2
    sin_u[:NE] = sin2
    return np.tile(cos_u, (1, 4)), np.tile(sin_u, (1, 4))


def _host_prep(inp):
    x = np.asarray(inp["x"], np.float32)
    shared = {}
    shared["w_mod"] = np.ascontiguousarray(inp["w_mod"], np.float32)
    shared["b_modT"] = np.ascontiguousarray(np.asarray(inp["b_mod"], np.float32).reshape(2, 72, 128).transpose(2, 0, 1))
    shared["norm_gT"] = np.ascontiguousarray(np.asarray(inp["norm_g"], np.float32).reshape(2, 3, 8, 128).transpose(3, 0, 1, 2))
    shared["ffn_w_in"] = np.ascontiguousarray(inp["ffn_w_in"], np.float32)
    shared["ffn_w_out"] = np.ascontiguousarray(inp["ffn_w_out"], np.float32)
    mw = np.asarray(inp["mix_w_in"], np.float32)[0]
    shared["mix_w_in"] = np.ascontiguousarray(mw[:, :3584])
    wg = np.zeros((1024, 128), np.float32)
    gb = np.zeros((128, 1), np.float32)
    gate_b = np.asarray(inp["ml_gate_b"], np.float32)[0]
    for h in range(4):
        for d in range(2):
            for q in range(2):
                wg[:, q * 64 + d * 32 + h] = mw[:, 3584 + h * 4 + d * 2 + q]
                gb[q * 64 + d * 32 + h, 0] = gate_b[h, d, q]
    shared["w_gate"] = wg
    shared["gate_b"] = gb
    shared["headg_bc"] = np.ascontiguousarray(np.broadcast_to(np.asarray(inp["ml_head_g"], np.float32)[0].reshape(1, 512), (128, 512)))
    shared["mix_w_out"] = np.ascontiguousarray(np.asarray(inp["mix_w_out"], np.float32)[0])
    shared["sg_w_in"] = np.ascontiguousarray(np.asarray(inp["sg_w_in"], np.float32)[0])
    shared["sg_w_out"] = np.ascontiguousarray(np.asarray(inp["sg_w_out"], np.float32)[0])
    shared["sg_lng_bc"] = np.ascontiguousarray(np.broadcast_to(np.asarray(inp["sg_ln_g"], np.float32)[0].reshape(1, 2048), (128, 2048)))
    shared["sg_lnb_bc"] = np.ascontiguousarray(np.broadcast_to(np.asarray(inp["sg_ln_b"], np.float32)[0].reshape(1, 2048), (128, 2048)))
    shared["sg_w_sT"] = np.ascontiguousarray(np.asarray(inp["sg_w_s"], np.float32)[0].transpose(2, 0, 1))
    shared["sg_b_s"] = np.ascontiguousarray(np.asarray(inp["sg_b_s"], np.float32)[0].reshape(1, 1024))
    shared["final_g_bc"] = np.ascontiguousarray(np.broadcast_to(np.asarray(inp["final_g"], np.float32).reshape(1, 1024), (128, 1024)))
    shared["ident"] = np.eye(128, dtype=np.float32)
    tri = np.zeros((128, 2, 128), np.float32)
    ss, tt = np.meshgrid(np.arange(128), np.arange(128), indexing="ij")
    tri[:, 0, :] = (tt >= ss)
    tri[:, 1, :] = (tt <= ss)
    shared["tri"] = tri
    sel = np.zeros((64, 8, 128), np.float32)
    for d in range(2):
        for h in range(4):
            sel[d * 32 + h, d * 4 + h, :] = 1.0
    shared["selm"] = sel
    rpb = np.asarray(inp["na_rpb"], np.float32)[0]
    c = np.asarray(inp["c"], np.float32)
    cctx = np.asarray(inp["c_ctx"], np.float32)
    ctx = np.asarray(inp["ctx"], np.float32)
    maps = []
    for core in range(8):
        b, s = core // 4, core % 4
        rm = _rowmap(s)
        tok = (rm[:, None] * 64 + np.arange(64)[None, :]).reshape(-1)
        xin = np.concatenate([x[b][tok], ctx[b]], axis=0)
        cT = np.stack([c[b].reshape(8, 128).T, cctx.reshape(8, 128).T], axis=-1)
        cos4, sin4 = _rope_tables(s)
        selv = np.zeros((128, 16), np.float32)
        for j in range(4):
            selv[:, j] = 1.0 if j < s else 0.0
            selv[:, 4 + j] = 1.0 - selv[:, j]
            selv[:, 8 + j] = 1.0 if j > s else 0.0
            selv[:, 12 + j] = 1.0 - selv[:, 8 + j]
        m = dict(shared)
        m["xin"] = np.ascontiguousarray(xin)
        m["cT"] = np.ascontiguousarray(cT.astype(np.float32))
        m["ropecos"] = cos4
        m["ropesin"] = sin4
        nb = _na_bias_tables(rpb, s)
        nbp = np.full((5, 8, 128, 640), NEG, np.float32)
        nbp[..., :576] = nb
        m["nabiasT"] = np.ascontiguousarray(nbp.reshape(5, 8, 128, 5, 128).transpose(0, 1, 4, 3, 2))
        m["selv"] = selv
        maps.append(m)
    return maps


INPUT_SHAPES = {
    "xin": [NU, 1024], "cT": [128, 8, 2], "w_mod": [2, 1024, 9216], "b_modT": [128, 2, 72], "norm_gT": [128, 2, 3, 8],
    "ffn_w_in": [2, 2, 1024, 5632], "ffn_w_out": [2, 2, 2816, 1024], "mix_w_in": [1024, 3584], "w_gate": [1024, 128],
    "gate_b": [128, 1], "headg_bc": [128, 512], "mix_w_out": [1024, 1024], "sg_w_in": [1024, 4096], "sg_w_out": [2048, 1024],
    "sg_lng_bc": [128, 2048], "sg_lnb_bc": [128, 2048], "sg_w_sT": [128, 8, 128], "sg_b_s": [1, 1024], "final_g_bc": [128, 1024],
    "ident": [128, 128], "tri": [128, 2, 128], "selm": [64, 8, 128], "ropecos": [NU, 512], "ropesin": [NU, 512],
    "nabiasT": [5, 8, 128, 5, 128], "selv": [128, 16],
}


class Ctx:
    pass


_UC = [0]


def _u():
    _UC[0] += 1
    return "u%d" % _UC[0]


def build(stages="all", dbg=()):
    nc = bass.Bass("TRN2", target_bir_lowering=False)
    I = {k: nc.dram_tensor(k, shp, F32, kind="ExternalInput").ap() for k, shp in INPUT_SHAPES.items()}
    OUT = nc.dram_tensor("out", [NOWN, 1024], F32, kind="ExternalOutput").ap()

    def scratch(name, shape, dt):
        if name in dbg:
            return nc.dram_tensor(name, shape, dt, kind="ExternalOutput").ap()
        return nc.dram_tensor(name, shape, dt).ap()

    G = Ctx()
    G.nc, G.I, G.OUT = nc, I, OUT
    G.dbg = dbg
    if dbg:
        G.DBG = {k: nc.dram_tensor(k, shp, dt, kind="ExternalOutput").ap() for k, (shp, dt) in {
            "D_MOD": ([128, 3, 2, 3, 2, 8], F32), "D_X0": ([128, 8, TW], F32), "D_X1": ([128, 8, TW], F32),
            "D_H": ([128, 8, TW], BF16), "D_HID": ([128, 22, TW], BF16), "D_RSTD": ([128, TW], F32)}.items()}
    G.XT = scratch("XT", [128, 8, NU], F32)
    G.NAQT = scratch("NAQT", [128, 4, NU], BF16)
    G.NAKT = scratch("NAKT", [128, 4, NU], BF16)
    G.NAV = scratch("NAV", [NU, 520], BF16)
    G.MLQT = scratch("MLQT", [128, 4, NU], BF16)
    G.MLKT = scratch("MLKT", [128, 4, NU], BF16)
    G.MLK = scratch("MLK", [NU, 512], BF16)
    G.MLV = scratch("MLV", [NU, 516], BF16)
    G.MLG2 = scratch("MLG2", [NU, 512], F32)
    G.GIF = scratch("GIF", [128, NU], F32)
    G.YT = scratch("YT", [128, 8, NOWN], BF16)
    G.HF = scratch("HF", [NOWN, 512], F32)
    G.HB = scratch("HB", [NOWN, 512], F32)
    G.CCI = scratch("CCI", [128, 1040], F32)
    G.CCO = scratch("CCO", [512, 1040], F32)

    with contextlib.ExitStack() as gst:
        P = Prog(nc, gst)
        G.P = P

        def gsb(name, shape, dt):
            return gst.enter_context(nc.sbuf_tensor(name, shape, dt))
        G.ident_f = gsb("ident_f", [128, 128], F32)
        G.ident_b = gsb("ident_b", [128, 128], BF16)
        G.ident8 = gsb("ident8", [128, 128], BF16)
        G.ones_b = gsb("ones_b", [128, 128], BF16)
        G.ones_f = gsb("ones_f", [128, 128], F32)
        G.c_eps = gsb("c_eps", [128, 1], F32)
        G.c_one = gsb("c_one", [128, 1], F32)
        G.SH = gsb("SH", [128, 2, 3, 2, 8], F32)
        G.GM = gsb("GM", [128, 2, 3, 2, 8], F32)
        G.GT = gsb("GT", [128, 2, 3, 2, 8], F32)

        P.dma("sync", lambda e: e.dma_start(out=G.ident_f[:], in_=I["ident"][:, :]), writes=["ident_f"])
        P.op("vector", lambda e: e.tensor_copy(G.ident_b[:], G.ident_f[:]), reads=["ident_f"], writes=["ident_b"])
        P.op("vector", lambda e: e.tensor_scalar(G.ident8[:], G.ident_f[:], 8.0, None, ALU.mult), reads=["ident_f"], writes=["ident8"])
        P.op("vector", lambda e: e.memset(G.ones_b[:], 1.0), writes=["ones_b"])
        P.op("vector", lambda e: e.memset(G.ones_f[:], 1.0), writes=["ones_f"])
        P.op("vector", lambda e: e.memset(G.c_eps[:], EPS), writes=["c_eps"])
        P.op("vector", lambda e: e.memset(G.c_one[:], 1.0), writes=["c_one"])

        ext_tiles = [(ti, 0) for ti in range(18)] + [(18, 1)]
        own_tiles = [(ti, 0) for ti in range(1, 17)]
        S = stages
        stage_mod(G)
        stage_t0(G)
        if S in ("ffn_only",):
            stage_ffn(G, 0, 0, ext_tiles)
            stage_final(G)
        elif S == "ffn_dbg":
            stage_ffn(G, 0, 0, [(1, 0)])
        else:
            stage_ffn(G, 0, 0, ext_tiles)
            stage_inproj(G)
            stage_na(G)
            stage_ml(G)
            stage_outproj(G)
            stage_ffn(G, 0, 1, own_tiles)
            stage_ffn(G, 1, 0, own_tiles)
            stage_sg(G)
            stage_ffn(G, 1, 1, own_tiles)
            stage_final(G)
    return nc


_AC = [0]


def _alloc(G, st):
    nc = G.nc
    _AC[0] += 1
    sfx = "_%d" % _AC[0]

    def sb(name, shape, dt):
        return st.enter_context(nc.sbuf_tensor(name + sfx, shape, dt))

    def ps(name, shape, dt=F32):
        return st.enter_context(nc.psum_tensor(name + sfx, shape, dt))
    return sb, ps


def stage_mod(G):
    P, I, nc = G.P, G.I, G.nc
    with contextlib.ExitStack() as st:
        sb, ps = _alloc(G, st)
        cT = sb("m_cT", [128, 8, 2], F32)
        sil = sb("m_sil", [128, 8, 2], BF16)
        wm = [sb("m_wm%d" % i, [128, 8, 1024], BF16) for i in range(3)]
        modv = sb("m_modv", [128, 2, 72, 2], F32)
        bmod = sb("m_bmod", [128, 2, 72], F32)
        ng = sb("m_ng", [128, 2, 3, 8], F32)
        pp = [ps("m_ps%d" % i, [128, 8, 2]) for i in range(2)]
        xs = [sb("t_xs%d" % i, [128, 1024], F32) for i in range(3)]
        xo = [sb("t_xo%d" % i, [128, 8, 128], F32) for i in range(2)]
        pt = [ps("t_pt%d" % i, [128, 8, 128]) for i in range(2)]
        P.dma("sync", lambda e: e.dma_start(out=cT[:], in_=I["cT"][:, :, :]), writes=["m_cT"])
        P.dma("sync", lambda e: e.dma_start(out=bmod[:], in_=I["b_modT"][:, :, :]), writes=["m_bmod"])
        P.dma("sync", lambda e: e.dma_start(out=ng[:], in_=I["norm_gT"][:, :, :, :]), writes=["m_ng"])
        P.op("scalar", lambda e: e.activation(out=sil[:], in_=cT[:], func=AF.Silu), reads=["m_cT"], writes=["m_sil"])
        NSUB = NU // 128

        def t0_load(i):
            if i < NSUB:
                b3 = i % 3
                P.dma("sync", lambda e, b3=b3, i=i: e.dma_start(out=xs[b3][:], in_=I["xin"][i * 128:(i + 1) * 128, :]), writes=["t_xs%d" % b3])

        def t0_sub(i):
            t0 = i * 128
            b, b3 = i % 2, i % 3
            for k in range(8):
                P.op("tensor", lambda e, b=b, b3=b3, k=k: e.transpose(pt[b][:, k, :], xs[b3][:, k * 128:(k + 1) * 128], G.ident_f[:]),
                     reads=["t_xs%d" % b3, "ident_f"], writes=["t_pt%d" % b], inc=(k == 7))
            if b == 0:
                P.op("vector", lambda e, b=b: e.tensor_copy(xo[b][:], pt[b][:]), reads=["t_pt%d" % b], writes=["t_xo%d" % b])
            else:
                P.op("scalar", lambda e, b=b: e.activation(out=xo[b][:], in_=pt[b][:], func=AF.Identity), reads=["t_pt%d" % b], writes=["t_xo%d" % b])
            t0_load(i + 3)
            P.dma("sync", lambda e, b=b, t0=t0: e.dma_start(out=G.XT[:, :, t0:t0 + 128], in_=xo[b][:]),
                  reads=["t_xo%d" % b], writes=["XT%d" % (t0 // TW)])

        for i in range(3):
            t0_load(i)
        n = 0
        ti = 0
        for l in range(2):
            wsrc = I["w_mod"][l].rearrange("(k p) n -> p k n", p=128)
            for blk in range(9):
                w = wm[n % 3]
                wn = "m_wm%d" % (n % 3)
                pt_ = pp[n % 2]
                pn = "m_ps%d" % (n % 2)
                P.dma("gpsimd", lambda e, w=w, blk=blk, wsrc=wsrc: e.dma_start(out=w[:], in_=wsrc[:, :, blk * 1024:(blk + 1) * 1024]),
                      writes=[wn])
                for jj in range(8):
                    for k in range(8):
                        P.op("tensor", lambda e, w=w, pt_=pt_, jj=jj, k=k: e.matmul(
                            pt_[:, jj, :], lhsT=w[:, k, jj * 128:(jj + 1) * 128], rhs=sil[:, k, :], start=(k == 0), stop=(k == 7)),
                            reads=[wn, "m_sil"], writes=[pn], inc=(jj == 7 and k == 7))
                for m in range(2):
                    P.op("vector", lambda e, pt_=pt_, l=l, blk=blk, m=m: e.tensor_tensor(
                        out=modv[:, l, blk * 8:(blk + 1) * 8, m], in0=pt_[:, :, m], in1=bmod[:, l, blk * 8:(blk + 1) * 8], op=ALU.add),
                        reads=[pn, "m_bmod"], writes=["m_modv"])
                n += 1
                for _ in range(2):
                    if ti < NSUB:
                        t0_sub(ti)
                        ti += 1
        while ti < NSUB:
            t0_sub(ti)
            ti += 1
        for l in range(2):
            for j in range(3):
                for m in range(2):
                    sh = modv[:, l, (3 * j) * 8:(3 * j) * 8 + 8, m]
                    sc = modv[:, l, (3 * j + 1) * 8:(3 * j + 1) * 8 + 8, m]
                    gt = modv[:, l, (3 * j + 2) * 8:(3 * j + 2) * 8 + 8, m]
                    P.op("vector", lambda e, sh=sh, l=l, j=j, m=m: e.tensor_copy(G.SH[:, l, j, m, :], sh), reads=["m_modv"], writes=["modc"])
                    P.op("vector", lambda e, sc=sc, l=l, j=j, m=m: e.scalar_tensor_tensor(
                        out=G.GM[:, l, j, m, :], in0=sc, scalar=1.0, in1=ng[:, l, j, :], op0=ALU.add, op1=ALU.mult),
                        reads=["m_modv", "m_ng"], writes=["modc"])
                    P.op("vector", lambda e, gt=gt, l=l, j=j, m=m: e.tensor_scalar(
                        G.GT[:, l, j, m, :], gt, (1.0 if j == 1 else 0.5), None, ALU.mult), reads=["m_modv"], writes=["modc"])
        P.barrier()
        P.emit()


def stage_t0(G):
    return


class NormScratch:
    def __init__(self, G, sb, ps, pfx, W=TW):
        self.sq = [sb(pfx + "sq%d" % i, [128, W], BF16) for i in range(2)]
        self.rs = sb(pfx + "rs", [128, W], F32)
        self.rstd = sb(pfx + "rstd", [128, W], F32)
        self.tmp = [sb(pfx + "tmp%d" % i, [128, W], F32) for i in range(2)]
        self.pn = ps(pfx + "pn", [128, W])
        self.pfx = pfx


def norm_mod(G, NS, x, xname, gm, sh, h, hname, W=TW, phase=None):
    P = G.P
    pfx = NS.pfx
    for k in range(8):
        if phase is None:
            sq, sqn = NS.sq[k % 2], pfx + "sq%d" % (k % 2)
        else:
            sq, sqn = NS.sq8[k], pfx + "sq8_%d" % k
        if phase in (None, "A"):
            P.op("scalar", lambda e, sq=sq, k=k: e.activation(out=sq[:, :W], in_=x[:, k, :], func=AF.Square), reads=[xname], writes=[sqn])
        if phase in (None, "B"):
            P.op("tensor", lambda e, sq=sq, k=k: e.matmul(NS.pn[:, :W], lhsT=G.ones_b[:], rhs=sq[:, :W], start=(k == 0), stop=(k == 7)),
                 reads=[sqn, "ones_b"], writes=[pfx + "pn"], inc=True)
    if phase == "A":
        return
    P.op("scalar", lambda e: e.activation(out=NS.rs[:, :W], in_=NS.pn[:, :W], func=AF.Sqrt, bias=G.c_eps[:], scale=1.0 / 1024.0),
         reads=[pfx + "pn", "c_eps"], writes=[pfx + "rs"])
    P.op("vector", lambda e: e.reciprocal(NS.rstd[:, :W], NS.rs[:, :W]), reads=[pfx + "rs"], writes=[pfx + "rstd"])
    for k in range(8):
        tmp = NS.tmp[k % 2]
        tn = pfx + "tmp%d" % (k % 2)
        P.op("vector", lambda e, tmp=tmp, k=k: e.tensor_tensor(out=tmp[:, :W], in0=x[:, k, :], in1=NS.rstd[:, :W], op=ALU.mult),
             reads=[xname, pfx + "rstd"], writes=[tn])
        P.op("scalar", lambda e, tmp=tmp, k=k: e.activation(out=h[:, k, :], in_=tmp[:, :W], func=AF.Identity, bias=sh[:, k:k + 1], scale=gm[:, k:k + 1]),
             reads=[tn, "modc"], writes=[hname])


def load_w_cast(G, dst, dname, src, nk, ncols, step=1024):
    P = G.P
    v = src.rearrange("(k p) n -> p k n", p=128)
    for c0 in range(0, ncols, step):
        c1 = min(ncols, c0 + step)
        P.dma("gpsimd", lambda e, c0=c0, c1=c1: e.dma_start(out=dst[:, :, c0:c1], in_=v[:, :, c0:c1]), writes=[dname])


def stage_ffn(G, l, i, tiles):
    P, I, nc = G.P, G.I, G.nc
    j = 0 if i == 0 else 2
    with contextlib.ExitStack() as st:
        sb, ps = _alloc(G, st)
        wi = sb("f_wi", [128, 8, 2 * DFF], BF16)
        wo = sb("f_wo", [128, 22, 1024], BF16)
        xt = [sb("f_xt%d" % b, [128, 8, TW], F32) for b in range(2)]
        hh = [sb("f_h%d" % b, [128, 8, TW], BF16) for b in range(2)]
        hid = sb("f_hid", [128, 22, TW], BF16)
        sa = [sb("f_sa%d" % b, [128, TW], F32) for b in range(2)]
        NS = NormScratch(G, sb, ps, "f_")
        NS.sq8 = [sb("f_sq8_%d" % k, [128, TW], BF16) for k in range(8)]
        pa = [ps("f_pa%d" % b, [128, TW]) for b in range(2)]
        pb = [ps("f_pb%d" % b, [128, TW]) for b in range(2)]
        po = [ps("f_po%d" % b, [128, TW]) for b in range(2)]
        wv_ = I["ffn_w_in"][l, i].rearrange("(k p) n -> p k n", p=128)
        for pc in (0, 2, 3, 1, 4, 5):
            c0, c1 = pc * 1024, min(2 * DFF, (pc + 1) * 1024)
            P.dma("gpsimd", lambda e, c0=c0, c1=c1: e.dma_start(out=wi[:, :, c0:c1], in_=wv_[:, :, c0:c1]), writes=["f_wi%d" % pc])
        load_w_cast(G, wo, "f_wo", I["ffn_w_out"][l, i], 22, 1024)
        def ld(n):
            ti_ = tiles[n][0]
            xb = xt[n % 2]
            P.dma("sync", lambda e, xb=xb, ti_=ti_: e.dma_start(out=xb[:], in_=G.XT[:, :, ti_ * TW:(ti_ + 1) * TW]),
                  reads=["XT%d" % ti_], writes=["f_xt%d" % (n % 2)])
        def do_norm(n, phase=None):
            ti_, m_ = tiles[n]
            bb = n % 2
            norm_mod(G, NS, xt[bb], "f_xt%d" % bb, G.GM[:, l, j, m_, :], G.SH[:, l, j, m_, :], hh[bb], "f_h%d" % bb, phase=phase)
        ld(0)
        if len(tiles) > 1:
            ld(1)
        do_norm(0)
        for n, (ti, m) in enumerate(tiles):
            t0 = ti * TW
            b = n % 2
            x, xn = xt[b], "f_xt%d" % b
            h, hn = hh[b], "f_h%d" % b
            for jj in range(22):
                q = jj % 2
                for half, pp, pn in ((0, pa[q], "f_pa%d" % q), (1, pb[q], "f_pb%d" % q)):
                    c0 = half * DFF + jj * 128
                    for k in range(8):
                        P.op("tensor", lambda e, pp=pp, c0=c0, k=k, h=h: e.matmul(
                            pp[:], lhsT=wi[:, k, c0:c0 + 128], rhs=h[:, k, :], start=(k == 0), stop=(k == 7)),
                            reads=["f_wi%d" % (c0 // 1024), "f_wi%d" % ((c0 + 127) // 1024), hn], writes=[pn], inc=(k == 7))
                P.op("scalar", lambda e, q=q: e.activation(out=sa[q][:], in_=pa[q][:], func=AF.Silu), reads=["f_pa%d" % q], writes=["f_sa%d" % q])
                P.op("vector", lambda e, q=q, jj=jj: e.tensor_tensor(out=hid[:, jj, :], in0=sa[q][:], in1=pb[q][:], op=ALU.mult),
                     reads=["f_sa%d" % q, "f_pb%d" % q], writes=["f_hid%d" % jj])
            if n + 1 < len(tiles):
                do_norm(n + 1, "A")
            for f in range(8):
                q = f % 2
                if f == 3 and n + 1 < len(tiles):
                    do_norm(n + 1, "B")
                for jj in range(22):
                    P.op("tensor", lambda e, q=q, f=f, jj=jj: e.matmul(
                        po[q][:], lhsT=wo[:, jj, f * 128:(f + 1) * 128], rhs=hid[:, jj, :], start=(jj == 0), stop=(jj == 21)),
                        reads=["f_wo", "f_hid%d" % jj], writes=["f_po%d" % q], inc=(jj == 21))
                gsc = G.GT[:, l, j, m, f:f + 1]
                P.op("vector", lambda e, q=q, f=f, x=x, gsc=gsc: e.scalar_tensor_tensor(
                    out=x[:, f, :], in0=po[q][:], scalar=gsc, in1=x[:, f, :], op0=ALU.mult, op1=ALU.add),
                    reads=["f_po%d" % q, "modc", xn], writes=[xn])
            if G.dbg and ti == 1:
                P.dma("sync", lambda e: e.dma_start(out=G.DBG["D_HID"][:, :, :], in_=hid[:]), reads=["f_hid%d" % q for q in range(22)], writes=["dbg3"])
                P.dma("sync", lambda e, x=x: e.dma_start(out=G.DBG["D_X1"][:, :, :], in_=x[:]), reads=[xn], writes=["dbg4"])
            P.dma("sync", lambda e, x=x, t0=t0: e.dma_start(out=G.XT[:, :, t0:t0 + TW], in_=x[:]), reads=[xn], writes=["XT%d" % ti])
            if n + 2 < len(tiles):
                ld(n + 2)
        P.barrier()
        P.emit()


def stage_final(G):
    P, I, nc = G.P, G.I, G.nc
    with contextlib.ExitStack() as st:
        sb, ps = _alloc(G, st)
        fg = sb("k_fg", [128, 1024], F32)
        xs = [sb("k_xs%d" % b, [128, 8, 128], F32) for b in range(2)]
        junk = sb("k_junk", [128, 1024], F32)
        yo = [sb("k_yo%d" % b, [128, 1024], F32) for b in range(2)]
        ssq = sb("k_ssq", [128, 2], F32)
        rs = sb("k_rs", [128, 2], F32)
        pt = [ps("k_pt%d" % b, [128, 8, 128]) for b in range(2)]
        P.dma("sync", lambda e: e.dma_start(out=fg[:], in_=I["final_g_bc"][:, :]), writes=["k_fg"])
        for i in range(NOWN // 128):
            t0 = OWN0 + i * 128
            b = i % 2
            P.dma("gpsimd", lambda e, b=b, t0=t0: e.dma_start(out=xs[b][:], in_=G.XT[:, :, t0:t0 + 128]),
                  reads=["XT%d" % (t0 // TW)], writes=["k_xs%d" % b])
            for k in range(8):
                P.op("tensor", lambda e, b=b, k=k: e.transpose(pt[b][:, k, :], xs[b][:, k, :], G.ident_f[:]),
                     reads=["k_xs%d" % b, "ident_f"], writes=["k_pt%d" % b], inc=(k == 7))
            ptf = pt[b][:].rearrange("p k n -> p (k n)")
            P.op("scalar", lambda e, b=b, ptf=ptf: e.activation(out=junk[:], in_=ptf, func=AF.Square, accum_out=ssq[:, b:b + 1]),
                 reads=["k_pt%d" % b], writes=["k_junk", "k_ssq%d" % b])
            P.op("scalar", lambda e, b=b: e.activation(out=rs[:, b:b + 1], in_=ssq[:, b:b + 1], func=AF.Sqrt, bias=G.c_eps[:], scale=1.0 / 1024.0),
                 reads=["k_ssq%d" % b, "c_eps"], writes=["k_rs%d" % b])
            P.op("vector", lambda e, b=b: e.reciprocal(rs[:, b:b + 1], rs[:, b:b + 1]), reads=["k_rs%d" % b], writes=["k_rs%d" % b])
            P.op("vector", lambda e, b=b, ptf=ptf: e.scalar_tensor_tensor(
                out=yo[b][:], in0=ptf, scalar=rs[:, b:b + 1], in1=fg[:], op0=ALU.mult, op1=ALU.mult),
                reads=["k_pt%d" % b, "k_rs%d" % b, "k_fg"], writes=["k_yo%d" % b])
            P.dma("sync", lambda e, b=b, i=i: e.dma_start(out=G.OUT[i * 128:(i + 1) * 128, :], in_=yo[b][:]),
                  reads=["k_yo%d" % b], writes=["OUT"])
        P.barrier()
        P.emit()


def stage_inproj(G):
    P, I, nc = G.P, G.I, G.nc
    l, j = 0, 1
    tiles = [(ti, 0) for ti in range(18)] + [(18, 1)]
    with contextlib.ExitStack() as st:
        sb, ps = _alloc(G, st)
        wfm = sb("i_wfm", [128, 8, 1024], BF16)
        wg = sb("i_wg", [128, 8, 128], BF16)
        wtm = sb("i_wtm", [128, 8, 2560], BF16)
        xt = [sb("i_xt%d" % b, [128, 8, TW], F32) for b in range(2)]
        hh = [sb("i_h%d" % b, [128, 8, TW], BF16) for b in range(2)]
        NS = NormScratch(G, sb, ps, "i_")
        fm = [sb("i_fm%d" % b, [128, 8, TW], BF16) for b in range(2)]
        gts = [sb("i_gt%d" % b, [128, TW], F32) for b in range(2)]
        cosT = [sb("i_cos%d" % b, [128, 512], F32) for b in range(2)]
        sinT = [sb("i_sin%d" % b, [128, 512], F32) for b in range(2)]
        hg = sb("i_hg", [128, 512], F32)
        vt = [sb("i_vt%d" % b, [128, 8, 65], BF16) for b in range(2)]
        mv = [sb("i_mv%d" % b, [128, 4, 129], BF16) for b in range(2)]
        xs = [sb("i_xs%d" % b, [128, 512], F32) for b in range(2)]
        r1 = [sb("i_r1%d" % b, [128, 512], F32) for b in range(2)]
        r2 = [sb("i_r2%d" % b, [128, 512], F32) for b in range(2)]
        qr = [sb("i_qr%d" % b, [128, 512], BF16) for b in range(2)]
        sg = sb("i_sg", [128, 512], F32)
        g2 = [sb("i_g2%d" % b, [128, 512], F32) for b in range(2)]
        tq = [sb("i_tq%d" % b, [128, 4, 128], BF16) for b in range(2)]
        pfm = [ps("i_pfm%d" % b, [128, TW]) for b in range(2)]
        ptm = [ps("i_ptm%d" % b, [128, 512]) for b in range(2)]
        ptr = [ps("i_ptr%d" % b, [128, 4, 128], BF16) for b in range(2)]
        load_w_cast(G, wfm, "i_wfm", I["mix_w_in"][:, 0:1024], 8, 1024)
        load_w_cast(G, wg, "i_wg", I["w_gate"], 8, 128)
        load_w_cast(G, wtm, "i_wtm", I["mix_w_in"][:, 1024:3584], 8, 2560)
        P.dma("sync", lambda e: e.dma_start(out=hg[:], in_=I["headg_bc"][:, :]), writes=["i_hg"])
        for b in range(2):
            P.op("vector", lambda e, b=b: e.memset(vt[b][:], 1.0), writes=["i_vt%d" % b])
            P.op("vector", lambda e, b=b: e.memset(mv[b][:], 1.0), writes=["i_mv%d" % b])

        def ld(n):
            ti_ = tiles[n][0]
            xb = xt[n % 2]
            P.dma("gpsimd", lambda e, xb=xb, ti_=ti_: e.dma_start(out=xb[:], in_=G.XT[:, :, ti_ * TW:(ti_ + 1) * TW]),
                  reads=["XT%d" % ti_], writes=["i_xt%d" % (n % 2)])
        ld(0)
        cnt = {"s": 0, "r": 0, "t": 0}
        deferred = []

        def rope_and_T(pt_, ptn, scale, dstT, tmaj_dst, ts0):
            a = cnt["r"] % 2
            cnt["r"] += 1
            sb_ = cnt["s"] % 2
            X, R1, R2, QR, TQ, PT = xs[a], r1[a], r2[a], qr[a], tq[a], ptr[a]
            xn_, r1n, r2n, qrn, tqn, ptn2 = "i_xs%d" % a, "i_r1%d" % a, "i_r2%d" % a, "i_qr%d" % a, "i_tq%d" % a, "i_ptr%d" % a
            P.op("scalar", lambda e: e.activation(out=X[:], in_=pt_[:], func=AF.Copy, scale=scale), reads=[ptn], writes=[xn_])
            P.op("vector", lambda e: e.tensor_tensor(out=R1[:], in0=X[:], in1=cosT[sb_][:], op=ALU.mult), reads=[xn_, "i_cos%d" % sb_], writes=[r1n])
            Xv = X[:].rearrange("p (i t) -> p i t", t=2)
            Sv = sinT[sb_][:].rearrange("p (i t) -> p i t", t=2)
            Rv = R2[:].rearrange("p (i t) -> p i t", t=2)
            P.op("vector", lambda e: e.tensor_tensor(out=Rv[:, :, 0], in0=Xv[:, :, 1], in1=Sv[:, :, 0], op=ALU.mult),
                 reads=[xn_, "i_sin%d" % sb_], writes=[r2n])
            P.op("vector", lambda e: e.tensor_tensor(out=Rv[:, :, 1], in0=Xv[:, :, 0], in1=Sv[:, :, 1], op=ALU.mult),
                 reads=[xn_, "i_sin%d" % sb_], writes=[r2n])
            P.op("vector", lambda e: e.tensor_tensor(out=QR[:], in0=R1[:], in1=R2[:], op=ALU.add), reads=[r1n, r2n], writes=[qrn])
            if tmaj_dst is not None:
                P.dma("sync", lambda e: e.dma_start(out=tmaj_dst[ts0:ts0 + 128, :], in_=QR[:]), reads=[qrn], writes=[_u()])
            def later():
                for hd in range(4):
                    P.op("tensor", lambda e, hd=hd: e.transpose(PT[:, hd, :], QR[:, hd * 128:(hd + 1) * 128], G.ident_b[:]),
                         reads=[qrn, "ident_b"], writes=[ptn2], inc=(hd == 3))
                P.op("scalar", lambda e: e.activation(out=TQ[:], in_=PT[:], func=AF.Copy), reads=[ptn2], writes=[tqn])
                P.dma("sync", lambda e: e.dma_start(out=dstT[:, :, ts0:ts0 + 128], in_=TQ[:]), reads=[tqn], writes=[_u()])
            deferred.append(later)

        for n, (ti, m) in enumerate(tiles):
            t0 = ti * TW
            b = n % 2
            x, xn = xt[b], "i_xt%d" % b
            h, hn = hh[b], "i_h%d" % b
            if n + 1 < len(tiles):
                ld(n + 1)
            if n == 0:
                norm_mod(G, NS, x, xn, G.GM[:, l, j, m, :], G.SH[:, l, j, m, :], h, hn)
            FM, fmn = fm[b], "i_fm%d" % b
            for fc in range(8):
                q = fc % 2
                for k in range(8):
                    P.op("tensor", lambda e, q=q, fc=fc, k=k, h=h: e.matmul(
                        pfm[q][:], lhsT=wfm[:, k, fc * 128:(fc + 1) * 128], rhs=h[:, k, :], start=(k == 0), stop=(k == 7)),
                        reads=["i_wfm", hn], writes=["i_pfm%d" % q], inc=(k == 7))
                P.op("scalar", lambda e, q=q, fc=fc, FM=FM: e.activation(out=FM[:, fc, :], in_=pfm[q][:], func=AF.Copy),
                     reads=["i_pfm%d" % q], writes=[fmn])
            P.dma("sync", lambda e, FM=FM, t0=t0: e.dma_start(out=G.NAQT[:, :, t0:t0 + TW], in_=FM[:, 0:4, :]), reads=[fmn], writes=[_u()])
            P.dma("sync", lambda e, FM=FM, t0=t0: e.dma_start(out=G.NAKT[:, :, t0:t0 + TW], in_=FM[:, 4:8, :]), reads=[fmn], writes=[_u()])
            GTS, gtn = gts[b], "i_gt%d" % b
            for k in range(8):
                P.op("tensor", lambda e, k=k, h=h: e.matmul(pfm[0][:], lhsT=wg[:, k, :], rhs=h[:, k, :], start=(k == 0), stop=(k == 7)),
                     reads=["i_wg", hn], writes=["i_pfm0"], inc=(k == 7))
            P.op("vector", lambda e, GTS=GTS: e.tensor_copy(GTS[:], pfm[0][:]), reads=["i_pfm0"], writes=[gtn])
            P.dma("sync", lambda e, GTS=GTS, t0=t0: e.dma_start(out=G.GIF[:, t0:t0 + TW], in_=GTS[:]), reads=[gtn], writes=[_u()])
            if n + 1 < len(tiles):
                ti2, m2 = tiles[n + 1]
                b2 = (n + 1) % 2
                norm_mod(G, NS, xt[b2], "i_xt%d" % b2, G.GM[:, l, j, m2, :], G.SH[:, l, j, m2, :], hh[b2], "i_h%d" % b2)
            for s_ in range(TW // 128):
                ts0 = t0 + s_ * 128
                sbi = cnt["s"] % 2
                P.dma("gpsimd", lambda e, sbi=sbi, ts0=ts0: e.dma_start(out=cosT[sbi][:], in_=I["ropecos"][ts0:ts0 + 128, :]), writes=["i_cos%d" % sbi])
                P.dma("gpsimd", lambda e, sbi=sbi, ts0=ts0: e.dma_start(out=sinT[sbi][:], in_=I["ropesin"][ts0:ts0 + 128, :]), writes=["i_sin%d" % sbi])
                for blk in range(5):
                    a = cnt["t"] % 2
                    cnt["t"] += 1
                    PT_, ptn = ptm[a], "i_ptm%d" % a
                    for k in range(8):
                        P.op("tensor", lambda e, PT_=PT_, k=k, h=h, s_=s_, blk=blk: e.matmul(
                            PT_[:], lhsT=h[:, k, s_ * 128:(s_ + 1) * 128], rhs=wtm[:, k, blk * 512:(blk + 1) * 512], start=(k == 0), stop=(k == 7)),
                            reads=["i_wtm", hn], writes=[ptn], inc=(k == 7))
                    while len(deferred) > (1 if blk == 3 else 0):
                        deferred.pop(0)()
                    if blk == 0:
                        VT = vt[sbi]
                        P.op("scalar", lambda e, VT=VT, PT_=PT_: e.activation(out=VT[:, :, 0:64], in_=PT_[:].rearrange("p (h d) -> p h d", d=64), func=AF.Copy),
                             reads=[ptn], writes=["i_vt%d" % sbi])
                        P.dma("sync", lambda e, VT=VT, ts0=ts0: e.dma_start(out=G.NAV[ts0:ts0 + 128, :], in_=VT[:].rearrange("p h d -> p (h d)")),
                              reads=["i_vt%d" % sbi], writes=[_u()])
                    elif blk == 1:
                        rope_and_T(PT_, ptn, 1.0, G.MLQT, None, ts0)
                    elif blk == 2:
                        rope_and_T(PT_, ptn, 128.0 ** -0.5, G.MLKT, G.MLK, ts0)
                    elif blk == 3:
                        MV = mv[sbi]
                        P.op("scalar", lambda e, MV=MV, PT_=PT_: e.activation(out=MV[:, :, 0:128], in_=PT_[:].rearrange("p (h d) -> p h d", d=128), func=AF.Copy),
                             reads=[ptn], writes=["i_mv%d" % sbi])
                        P.dma("sync", lambda e, MV=MV, ts0=ts0: e.dma_start(out=G.MLV[ts0:ts0 + 128, :], in_=MV[:].rearrange("p h d -> p (h d)")),
                              reads=["i_mv%d" % sbi], writes=[_u()])
                    else:
                        G2 = g2[sbi]
                        P.op("scalar", lambda e, PT_=PT_: e.activation(out=sg[:], in_=PT_[:], func=AF.Sigmoid), reads=[ptn], writes=["i_sg"])
                        P.op("vector", lambda e, G2=G2: e.tensor_tensor(out=G2[:], in0=sg[:], in1=hg[:], op=ALU.mult), reads=["i_sg", "i_hg"], writes=["i_g2%d" % sbi])
                        P.dma("sync", lambda e, G2=G2, ts0=ts0: e.dma_start(out=G.MLG2[ts0:ts0 + 128, :], in_=G2[:]), reads=["i_g2%d" % sbi], writes=[_u()])
                while deferred:
                    deferred.pop(0)()
                cnt["s"] += 1
        P.barrier()
        P.emit()


def stage_na(G):
    P, I, nc = G.P, G.I, G.nc
    with contextlib.ExitStack() as st:
        sb, ps = _alloc(G, st)
        KT = sb("n_KT", [128, 4, NU], BF16)
        V = sb("n_V", [128, 38, 520], BF16)
        QT = sb("n_QT", [128, 4, NOWN], BF16)
        BI = sb("n_BI", [128, 5, 8, 5, 128], BF16)
        sAb = [sb("n_sAb%d" % b, [128, 4, 128], F32) for b in range(2)]
        sBb = [sb("n_sBb%d" % b, [128, 128], F32) for b in range(2)]
        pt = [sb("n_pt%d" % b, [128, 7, 128], BF16) for b in range(2)]
        ya = [sb("n_ya%d" % b, [128, 512], BF16) for b in range(2)]
        rec = [sb("n_rec%d" % b, [128, 8], F32) for b in range(2)]
        yt = [sb("n_yt%d" % b, [128, 4, 128], BF16) for b in range(2)]
        sA = [ps("n_sA%d" % b, [128, 4, 128]) for b in range(2)]
        sB = [ps("n_sB%d" % b, [128, 4, 128]) for b in range(2)]
        O = ps("n_O", [128, 8, 128])
        ptr = ps("n_ptr", [128, 4, 128], BF16)
        P.dma("sync", lambda e: e.dma_start(out=KT[:], in_=G.NAKT[:, :, :]), reads=["dramNA"], writes=["n_KT"])
        P.dma("sync", lambda e: e.dma_start(out=V[:], in_=G.NAV.rearrange("(c p) n -> p c n", p=128)), reads=["dramNA"], writes=["n_V"])
        P.dma("sync", lambda e: e.dma_start(out=QT[:], in_=G.NAQT[:, :, OWN0:OWN0 + NOWN]), reads=["dramNA"], writes=["n_QT"])
        for c5 in range(5):
            P.dma("gpsimd", lambda e, c5=c5: e.dma_start(out=BI[:, c5], in_=I["nabiasT"][c5].rearrange("h k j q -> k h j q")), writes=["n_BI"])
        hb = 0
        for p in range(32):
            cls = 0 if p == 0 else 1 if p == 1 else 3 if p == 30 else 4 if p == 31 else 2
            pb2 = p % 2
            for h in range(8):
                hc, b0 = h // 2, (h % 2) * 64
                a = hb % 2
                hb += 1
                q_ap = QT[b0:b0 + 64, hc, p * 128:(p + 1) * 128]
                SA, SB, PT = sA[a], sB[a], pt[a]
                san, sbn, ptn = "n_sA%d" % a, "n_sB%d" % a, "n_pt%d" % a
                for jj in range(4):
                    k0 = (p + jj) * 128
                    P.op("tensor", lambda e, SA=SA, jj=jj, k0=k0, q_ap=q_ap, hc=hc, b0=b0: e.matmul(
                        SA[:, jj, :], lhsT=KT[b0:b0 + 64, hc, k0:k0 + 128], rhs=q_ap, start=True, stop=True),
                        reads=["n_KT", "n_QT"], writes=[san], inc=(jj == 3))
                k0 = (p + 4) * 128
                P.op("tensor", lambda e, SB=SB, k0=k0, q_ap=q_ap, hc=hc, b0=b0: e.matmul(
                    SB[0:64, 0, :], lhsT=KT[b0:b0 + 64, hc, k0:k0 + 64], rhs=q_ap, start=True, stop=True),
                    reads=["n_KT", "n_QT"], writes=[sbn], inc=False)
                for c in range(2):
                    k0 = CTX0 + c * 128
                    P.op("tensor", lambda e, SB=SB, c=c, k0=k0, q_ap=q_ap, hc=hc, b0=b0: e.matmul(
                        SB[:, 1 + c, :], lhsT=KT[b0:b0 + 64, hc, k0:k0 + 128], rhs=q_ap, start=True, stop=True),
                        reads=["n_KT", "n_QT"], writes=[sbn], inc=(c == 1))
                AB, BB = sAb[a], sBb[a]
                abn, bbn = "n_sAb%d" % a, "n_sBb%d" % a
                P.op("vector", lambda e, SA=SA, AB=AB, cls=cls, h=h: e.scalar_tensor_tensor(
                    out=AB[:], in0=SA[:], scalar=0.125, in1=BI[:, cls, h, 0:4, :], op0=ALU.mult, op1=ALU.add),
                    reads=[san, "n_BI"], writes=[abn])
                P.op("vector", lambda e, SB=SB, BB=BB, cls=cls, h=h: e.scalar_tensor_tensor(
                    out=BB[0:64, :], in0=SB[0:64, 0, :], scalar=0.125, in1=BI[0:64, cls, h, 4, :], op0=ALU.mult, op1=ALU.add),
                    reads=[sbn, "n_BI"], writes=[bbn])
                P.op("scalar", lambda e, AB=AB, PT=PT: e.activation(out=PT[:, 0:4, :], in_=AB[:], func=AF.Exp), reads=[abn], writes=[ptn])
                P.op("scalar", lambda e, SB=SB, PT=PT: e.activation(out=PT[:, 5:7, :], in_=SB[:, 1:3, :], func=AF.Exp, scale=0.125), reads=[sbn, bbn], writes=[ptn])
                P.op("scalar", lambda e, BB=BB, PT=PT: e.activation(out=PT[0:64, 4, :], in_=BB[0:64, :], func=AF.Exp), reads=[bbn], writes=[ptn])
                specs = [(jj, 128, p + jj) for jj in range(4)] + [(4, 64, p + 4), (5, 128, 36), (6, 128, 37)]
                for si, (slot, nk, vc) in enumerate(specs):
                    P.op("tensor", lambda e, PT=PT, slot=slot, nk=nk, vc=vc, h=h, si=si: e.matmul(
                        O[:, h, 0:65], lhsT=PT[0:nk, slot, :], rhs=V[0:nk, vc, h * 65:(h + 1) * 65], start=(si == 0), stop=(si == 6)),
                        reads=[ptn, "n_V"], writes=["n_O"], inc=(si == 6))
            R, YA, YT_ = rec[pb2], ya[pb2], yt[pb2]
            rn, yan, ytn = "n_rec%d" % pb2, "n_ya%d" % pb2, "n_yt%d" % pb2
            P.op("vector", lambda e, R=R: e.reciprocal(R[:], O[:, :, 64]), reads=["n_O"], writes=[rn])
            for h in range(8):
                P.op("scalar", lambda e, R=R, YA=YA, h=h: e.activation(out=YA[:, h * 64:(h + 1) * 64], in_=O[:, h, 0:64], func=AF.Copy, scale=R[:, h:h + 1]),
                     reads=["n_O", rn], writes=[yan])
            for c in range(4):
                P.op("tensor", lambda e, YA=YA, c=c: e.transpose(ptr[:, c, :], YA[:, c * 128:(c + 1) * 128], G.ident_b[:]),
                     reads=[yan, "ident_b"], writes=["n_ptr"], inc=(c == 3))
            P.op("vector", lambda e, YT_=YT_: e.tensor_copy(YT_[:], ptr[:]), reads=["n_ptr"], writes=[ytn])
            P.dma("sync", lambda e, YT_=YT_, p=p: e.dma_start(out=G.YT[:, 0:4, p * 128:(p + 1) * 128], in_=YT_[:]), reads=[ytn], writes=[_u()])
        P.barrier()
        P.emit()


def stage_ml(G):
    P, I, nc = G.P, G.I, G.nc
    NCH = 32
    with contextlib.ExitStack() as st:
        sb, ps = _alloc(G, st)
        TOK = sb("l_TOK", [128, 34, 5, 8], F32)
        EBEND = sb("l_EBEND", [128, 8, 32], F32)
        ATOT = sb("l_ATOT", [128, 8], F32)
        with contextlib.ExitStack() as st1:
            sb1, ps1 = _alloc(G, st1)
            LI = sb1("l_LI", [64, NU], F32)
            SP = sb1("l_SP", [64, NU], F32)
            CL = sb1("l_CL", [64, NU], F32)
            CG = sb1("l_CG", [64, NU], F32)
            TM = sb1("l_TM", [64, NU], F32)
            OQ = [sb1("l_OQ%d" % b, [64, NU], F32) for b in range(2)]
            gbI = sb1("l_gbI", [64, 1], F32)
            gbF = sb1("l_gbF", [64, 1], F32)
            CE = sb1("l_CE", [64, 32], F32)
            ntot = sb1("l_ntot", [64, 2], F32)
            tot = sb1("l_tot", [64, 2], F32)
            SELM = sb1("l_SELM", [64, 8, 128], F32)
            ptr = [ps1("l_ptr%d" % b, [128, 8, 64]) for b in range(2)]
            pe = ps1("l_pe", [128, 8, 32])
            pa = ps1("l_pa", [128, 8, 2])
            own = slice(OWN0, OWN0 + NOWN)
            cxs = slice(CTX0, CTX0 + 256)
            P.dma("sync", lambda e: e.dma_start(out=LI[:], in_=G.GIF[0:64, :]), reads=["dramML"], writes=["l_LI"])
            P.dma("sync", lambda e: e.dma_start(out=SP[:], in_=G.GIF[64:128, :]), reads=["dramML"], writes=["l_SP"])
            P.dma("sync", lambda e: e.dma_start(out=gbI[:], in_=I["gate_b"][0:64, :]), writes=["l_gbI"])
            P.dma("sync", lambda e: e.dma_start(out=gbF[:], in_=I["gate_b"][64:128, :]), writes=["l_gbF"])
            P.dma("sync", lambda e: e.dma_start(out=SELM[:], in_=I["selm"][:, :, :]), writes=["l_SELM"])
            P.op("vector", lambda e: e.tensor_scalar(gbF[:], gbF[:], -1.0, None, ALU.mult), reads=["l_gbF"], writes=["l_gbF"])
            P.op("scalar", lambda e: e.activation(out=LI[:], in_=LI[:], func=AF.Identity, bias=gbI[:]), reads=["l_LI", "l_gbI"], writes=["l_LI"])
            P.op("scalar", lambda e: e.activation(out=SP[:], in_=SP[:], func=AF.Exp, bias=gbF[:], scale=-1.0), reads=["l_SP", "l_gbF"], writes=["l_SP"])
            P.op("scalar", lambda e: e.activation(out=SP[:], in_=SP[:], func=AF.Ln, bias=G.c_one[0:64, :]), reads=["l_SP", "c_one"], writes=["l_SP"])
            P.op("vector", lambda e: e.memset(TM[:], 1.0), writes=["l_TM"])
            P.op("vector", lambda e: e.tensor_tensor_scan(out=CG[:, own], data0=TM[:, own], data1=SP[:, own], initial=0.0, op0=ALU.mult, op1=ALU.add),
                 reads=["l_TM", "l_SP"], writes=["l_CG"])
            P.op("vector", lambda e: e.tensor_tensor_scan(out=CG[:, cxs], data0=TM[:, cxs], data1=SP[:, cxs], initial=0.0, op0=ALU.mult, op1=ALU.add),
                 reads=["l_TM", "l_SP"], writes=["l_CG"])
            TMo = TM[:, own].rearrange("p (c t) -> p c t", t=128)
            P.op("vector", lambda e: e.memset(TMo[:, :, 0:1], 0.0), reads=["l_CG"], writes=["l_TM"])
            P.op("vector", lambda e: e.tensor_tensor_scan(out=CL[:, own], data0=TM[:, own], data1=SP[:, own], initial=0.0, op0=ALU.mult, op1=ALU.add),
                 reads=["l_TM", "l_SP"], writes=["l_CL"])
            CLo = CL[:, own].rearrange("p (c t) -> p c t", t=128)
            SPo = SP[:, own].rearrange("p (c t) -> p c t", t=128)
            P.op("vector", lambda e: e.tensor_copy(CE[:], CLo[:, :, 127]), reads=["l_CL"], writes=["l_CE"])
            P.op("vector", lambda e: e.tensor_copy(tot[:, 0:1], CG[:, OWN0 + NOWN - 1:OWN0 + NOWN]), reads=["l_CG"], writes=["l_tot"])
            P.op("vector", lambda e: e.tensor_copy(tot[:, 1:2], CG[:, CTX0 + 255:CTX0 + 256]), reads=["l_CG"], writes=["l_tot"])
            P.op("vector", lambda e: e.tensor_scalar(ntot[:], tot[:], -1.0, None, ALU.mult), reads=["l_tot"], writes=["l_ntot"])
            for c in range(NCH):
                P.op("vector", lambda e, c=c: e.tensor_scalar(CLo[32:64, c, :], CLo[32:64, c, :], CE[32:64, c:c + 1], -1.0, ALU.subtract, ALU.mult),
                     reads=["l_CL", "l_CE"], writes=["l_CL"])
            P.op("vector", lambda e: e.tensor_tensor(out=CL[32:64, own], in0=CL[32:64, own], in1=SP[32:64, own], op=ALU.add),
                 reads=["l_CL", "l_SP"], writes=["l_CL"])
            for r in range(8):
                P.op("tensor", lambda e, r=r: e.matmul(pe[:, r, :], lhsT=SELM[:, r, :], rhs=CE[:], start=True, stop=True),
                     reads=["l_SELM", "l_CE"], writes=["l_pe"], inc=(r == 7))
            P.op("scalar", lambda e: e.activation(out=EBEND[:], in_=pe[:], func=AF.Exp, scale=-1.0), reads=["l_pe"], writes=["l_EBEND"])
            for r in range(8):
                P.op("tensor", lambda e, r=r: e.matmul(pa[:, r, :], lhsT=SELM[:, r, :], rhs=tot[:], start=True, stop=True),
                     reads=["l_SELM", "l_tot"], writes=["l_pa"], inc=(r == 7))
            P.op("scalar", lambda e: e.activation(out=ATOT[:], in_=pa[:, :, 0], func=AF.Exp, scale=-1.0), reads=["l_pa"], writes=["l_ATOT"])

            tcnt = {"n": 0}

            def transpose_out(Q, qn, qty, chunks):
                for g0 in range(0, len(chunks), 8):
                    grp = chunks[g0:g0 + 8]
                    a = tcnt["n"] % 2
                    tcnt["n"] += 1
                    for gi, (ci, col0) in enumerate(grp):
                        P.op("tensor", lambda e, a=a, gi=gi, col0=col0: e.transpose(ptr[a][:, gi, :], Q[0:64, col0:col0 + 128], G.ident_f[0:64, 0:64]),
                             reads=[qn, "ident_f"], writes=["l_ptr%d" % a], inc=(gi == len(grp) - 1))
                    c_first = grp[0][0]
                    ng_ = len(grp)
                    src = ptr[a][:, 0:ng_, :].rearrange("p g (d x) -> p g d x", d=2)[:, :, :, 0:4]
                    dst = TOK[:, c_first:c_first + ng_, qty, :].rearrange("p g (d x) -> p g d x", d=2)
                    P.op("vector", lambda e, src=src, dst=dst: e.tensor_copy(dst, src), reads=["l_ptr%d" % a], writes=["l_TOK"])

            own_chunks = [(c, OWN0 + c * 128) for c in range(NCH)]
            ctx_chunks = [(32 + c, CTX0 + c * 128) for c in range(2)]
            P.op("scalar", lambda e: e.activation(out=OQ[0][:, own], in_=CL[:, own], func=AF.Exp, scale=-1.0), reads=["l_CL"], writes=["l_OQ0"])
            transpose_out(OQ[0], "l_OQ0", 0, own_chunks)
            P.op("scalar", lambda e: e.activation(out=OQ[1][:, own], in_=CL[:, own], func=AF.Exp), reads=["l_CL"], writes=["l_OQ1"])
            transpose_out(OQ[1], "l_OQ1", 4, own_chunks)
            P.op("vector", lambda e: e.tensor_tensor(out=TM[:, own], in0=LI[:, own], in1=CL[:, own], op=ALU.add), reads=["l_LI", "l_CL"], writes=["l_TM"])
            P.op("scalar", lambda e: e.activation(out=OQ[1][:, own], in_=TM[:, own], func=AF.Exp), reads=["l_TM"], writes=["l_OQ1"])
            transpose_out(OQ[1], "l_OQ1", 1, own_chunks)
            TMo2 = TM[:, own].rearrange("p (c t) -> p c t", t=128)
            for c in range(NCH):
                P.op("vector", lambda e, c=c: e.tensor_scalar(TMo2[:, c, :], TMo2[:, c, :], CE[:, c:c + 1], None, ALU.subtract),
                     reads=["l_TM", "l_CE", "l_OQ1"], writes=["l_TM"])
            P.op("scalar", lambda e: e.activation(out=OQ[0][:, own], in_=TM[:, own], func=AF.Exp), reads=["l_TM"], writes=["l_OQ0"])
            transpose_out(OQ[0], "l_OQ0", 2, own_chunks)
            for (sl, ti_) in ((own, 0), (cxs, 1)):
                P.op("vector", lambda e, sl=sl: e.tensor_tensor(out=TM[0:32, sl], in0=LI[0:32, sl], in1=CG[0:32, sl], op=ALU.add),
                     reads=["l_LI", "l_CG"], writes=["l_TM"])
                P.op("vector", lambda e, sl=sl: e.tensor_tensor(out=TM[32:64, sl], in0=LI[32:64, sl], in1=CG[32:64, sl], op=ALU.subtract),
                     reads=["l_LI", "l_CG"], writes=["l_TM"])
                P.op("vector", lambda e, sl=sl: e.tensor_tensor(out=TM[32:64, sl], in0=TM[32:64, sl], in1=SP[32:64, sl], op=ALU.add),
                     reads=["l_TM", "l_SP"], writes=["l_TM"])
                P.op("scalar", lambda e, sl=sl, ti_=ti_: e.activation(out=OQ[1][0:32, sl], in_=TM[0:32, sl], func=AF.Exp, bias=ntot[0:32, ti_:ti_ + 1]),
                     reads=["l_TM", "l_ntot"], writes=["l_OQ1"])
                P.op("scalar", lambda e, sl=sl: e.activation(out=OQ[1][32:64, sl], in_=TM[32:64, sl], func=AF.Exp), reads=["l_TM"], writes=["l_OQ1"])
            transpose_out(OQ[1], "l_OQ1", 3, own_chunks + ctx_chunks)
            P.barrier()
            P.emit()

        KTOK = sb("l_KTOK", [128, 34, 512], BF16)
        VTOK = sb("l_VTOK", [128, 34, 516], BF16)
        TRI = sb("l_TRI", [128, 2, 128], F32)
        SELV = sb("l_SELV", [128, 16], F32)
        PAY = sb("l_PAY", [128, 8, 130], F32)
        GATH = sb("l_GATH", [128, 4, 1040], F32)
        STATE = sb("l_STATE", [128, 8, 129], F32)
        STB = sb("l_STB", [128, 8, 129], BF16)
        KA = [sb("l_KA%d" % b, [128, 128], BF16) for b in range(3)]
        alpha = sb("l_alpha", [128, 1], F32)
        tmpL = sb("l_tmpL", [128, 129], F32)
        QTc = [[sb("l_QTc%d%d" % (d_, b), [128, 4, 128], BF16) for b in range(2)] for d_ in range(2)]
        KTc = [[sb("l_KTc%d%d" % (d_, b), [128, 4, 128], BF16) for b in range(2)] for d_ in range(2)]
        HS = [[sb("l_HS%d%d" % (d_, b), [128, 512], F32) for b in range(2)] for d_ in range(2)]
        PTt = [sb("l_PT%d" % b, [128, 128], BF16) for b in range(2)]
        pS = [ps("l_pS%d" % b, [128, 128]) for b in range(2)]
        pU = [ps("l_pU%d" % b, [128, 132]) for b in range(4)]
        pN = [ps("l_pN%d" % b, [128, 132]) for b in range(2)]
        pL = [pU[0], pU[1]]
        den = [sb("l_den%d" % b, [128, 8], F32) for b in range(2)]
        P.dma("sync", lambda e: e.dma_start(out=KTOK[:, 0:32, :], in_=G.MLK[OWN0:OWN0 + NOWN, :].rearrange("(c p) n -> p c n", p=128)), reads=["dramML"], writes=["l_KTOK"])
        P.dma("sync", lambda e: e.dma_start(out=KTOK[:, 32:34, :], in_=G.MLK[CTX0:CTX0 + 256, :].rearrange("(c p) n -> p c n", p=128)), reads=["dramML"], writes=["l_KTOK"])
        P.dma("sync", lambda e: e.dma_start(out=VTOK[:, 0:32, :], in_=G.MLV[OWN0:OWN0 + NOWN, :].rearrange("(c p) n -> p c n", p=128)), reads=["dramML"], writes=["l_VTOK"])
        P.dma("sync", lambda e: e.dma_start(out=VTOK[:, 32:34, :], in_=G.MLV[CTX0:CTX0 + 256, :].rearrange("(c p) n -> p c n", p=128)), reads=["dramML"], writes=["l_VTOK"])
        P.dma("sync", lambda e: e.dma_start(out=TRI[:], in_=I["tri"][:, :, :]), writes=["l_TRI"])
        P.dma("sync", lambda e: e.dma_start(out=SELV[:], in_=I["selv"][:, :]), writes=["l_SELV"])
        kacnt = {"n": 0}

        def scaled_k(c, h, qty, r):
            a = kacnt["n"] % 3
            kacnt["n"] += 1
            eng = ("vector", "scalar", "scalar")[a]
            src = KTOK[:, c, h * 128:(h + 1) * 128]
            sc = TOK[:, c, qty, r:r + 1]
            if eng == "scalar":
                P.op("scalar", lambda e: e.activation(out=KA[a][:], in_=src, func=AF.Copy, scale=sc), reads=["l_KTOK", "l_TOK"], writes=["l_KA%d" % a])
            else:
                P.op(eng, lambda e: e.tensor_scalar(KA[a][:], src, sc, None, ALU.mult), reads=["l_KTOK", "l_TOK"], writes=["l_KA%d" % a])
            return KA[a], "l_KA%d" % a

        n2 = 0
        for r in range(8):
            h = r % 4
            for (chs, dstname) in ((list(range(32)), "own"), ([32, 33], "ctx")):
                pp, ppn = pL[n2 % 2], "l_pU%d" % (n2 % 2)
                n2 += 1
                for i_, c in enumerate(chs):
                    ka, kan = scaled_k(c, h, 3, r)
                    P.op("tensor", lambda e, pp=pp, ka=ka, c=c, h=h, i_=i_, L=len(chs): e.matmul(
                        pp[:, 0:129], lhsT=ka[:], rhs=VTOK[:, c, h * 129:(h + 1) * 129], start=(i_ == 0), stop=(i_ == L - 1)),
                        reads=[kan, "l_VTOK"], writes=[ppn], inc=True)
                if dstname == "own":
                    P.op("vector", lambda e, pp=pp, r=r: e.tensor_copy(PAY[:, r, 0:129], pp[:, 0:129]), reads=[ppn], writes=["l_PAY"])
                else:
                    P.op("vector", lambda e, pp=pp, r=r: e.tensor_copy(STATE[:, r, :], pp[:, 0:129]), reads=[ppn], writes=["l_STATE"])
        P.op("vector", lambda e: e.tensor_copy(PAY[:, :, 129], ATOT[:]), reads=["l_ATOT", "l_PAY"], writes=["l_PAY"])
        P.dma("sync", lambda e: e.dma_start(out=G.CCI[:, :], in_=PAY[:].rearrange("p r n -> p (r n)")), reads=["l_PAY"], writes=["CCI"])
        for _rep in range(3):
            P.cc(lambda e: e.collective_compute("AllGather", ALU.bypass, replica_groups=[[0, 1, 2, 3], [4, 5, 6, 7]],
                                                ins=[G.CCI.opt()], outs=[G.CCO.opt()]), reads=["CCI"], writes=["CCO"])
        P.dma("sync", lambda e: e.dma_start(out=GATH[:], in_=G.CCO.rearrange("(j p) n -> p j n", p=128)), reads=["CCO"], writes=["l_GATH"])
        for r in range(8):
            d = r // 4
            order = range(4) if d == 0 else range(3, -1, -1)
            for jseg in order:
                so = 0 if d == 0 else 8
                A_j = GATH[:, jseg, r * 130 + 129:r * 130 + 130]
                L_j = GATH[:, jseg, r * 130:r * 130 + 129]
                P.op("vector", lambda e, A_j=A_j, so=so, jseg=jseg: e.tensor_scalar(
                    alpha[:], A_j, SELV[:, so + jseg:so + jseg + 1], SELV[:, so + 4 + jseg:so + 5 + jseg], ALU.mult, ALU.add),
                    reads=["l_GATH", "l_SELV"], writes=["l_alpha"])
                P.op("vector", lambda e, L_j=L_j, so=so, jseg=jseg: e.tensor_scalar(tmpL[:], L_j, SELV[:, so + jseg:so + jseg + 1], None, ALU.mult),
                     reads=["l_GATH", "l_SELV"], writes=["l_tmpL"])
                P.op("vector", lambda e, r=r: e.scalar_tensor_tensor(out=STATE[:, r, :], in0=STATE[:, r, :], scalar=alpha[:], in1=tmpL[:], op0=ALU.mult, op1=ALU.add),
                     reads=["l_STATE", "l_alpha", "l_tmpL"], writes=["l_STATE"])
        P.op("scalar", lambda e: e.activation(out=STB[:], in_=STATE[:], func=AF.Copy), reads=["l_STATE"], writes=["l_STB"])

        def ld4(i):
            if i >= NCH:
                return
            bb = i % 2
            for d_ in range(2):
                c_ = i if d_ == 0 else NCH - 1 - i
                tk0 = OWN0 + c_ * 128
                P.dma("sync", lambda e, bb=bb, d_=d_, tk0=tk0: e.dma_start(out=QTc[d_][bb][:], in_=G.MLQT[:, :, tk0:tk0 + 128]), writes=["l_QTc%d%d" % (d_, bb)])
                P.dma("sync", lambda e, bb=bb, d_=d_, tk0=tk0: e.dma_start(out=KTc[d_][bb][:], in_=G.MLKT[:, :, tk0:tk0 + 128]), writes=["l_KTc%d%d" % (d_, bb)])
        ld4(0)
        for i in range(NCH):
            b = i % 2
            ld4(i + 1)
            cs = (i, NCH - 1 - i)
            for hh_ in range(2):
                items = [(h, d) for h in (2 * hh_, 2 * hh_ + 1) for d in range(2)]
                for (h, d) in items:
                    c = cs[d]
                    r = d * 4 + h
                    u = (h % 2) * 2 + d
                    qn, kn = "l_QTc%d%d" % (d, b), "l_KTc%d%d" % (d, b)
                    P.op("tensor", lambda e, b=b, h=h, d=d: e.matmul(pS[d][:], lhsT=KTc[d][b][:, h, :], rhs=QTc[d][b][:, h, :], start=True, stop=True),
                         reads=[kn, qn], writes=["l_pS%d" % d], inc=True)
                    P.op("vector", lambda e, c=c, r=r, d=d: e.scalar_tensor_tensor(
                        out=PTt[d][:], in0=pS[d][:], scalar=TOK[:, c, 1, r:r + 1], in1=TRI[:, d, :], op0=ALU.mult, op1=ALU.mult),
                        reads=["l_pS%d" % d, "l_TOK", "l_TRI"], writes=["l_PT%d" % d])
                    P.op("tensor", lambda e, c=c, h=h, d=d, u=u: e.matmul(pU[u][:, 0:129], lhsT=PTt[d][:], rhs=VTOK[:, c, h * 129:(h + 1) * 129], start=True, stop=False),
                         reads=["l_PT%d" % d, "l_VTOK"], writes=["l_pU%d" % u], inc=False)
                    P.op("tensor", lambda e, b=b, h=h, d=d, r=r, u=u: e.matmul(pU[u][:, 0:129], lhsT=QTc[d][b][:, h, :], rhs=STB[:, r, :], start=False, stop=True),
                         reads=[qn, "l_STB%d" % r, "l_STB"], writes=["l_pU%d" % u], inc=True)
                for (h, d) in items:
                    c = cs[d]
                    r = d * 4 + h
                    a = r % 2
                    ka, kan = scaled_k(c, h, 2, r)
                    P.op("tensor", lambda e, a=a, ka=ka, c=c, h=h: e.matmul(pN[a][:, 0:129], lhsT=ka[:], rhs=VTOK[:, c, h * 129:(h + 1) * 129], start=True, stop=True),
                         reads=[kan, "l_VTOK"], writes=["l_pN%d" % a], inc=True)
                    P.op("vector", lambda e, a=a, r=r, c=c: e.scalar_tensor_tensor(
                        out=STATE[:, r, :], in0=STATE[:, r, :], scalar=EBEND[:, r, c:c + 1], in1=pN[a][:, 0:129], op0=ALU.mult, op1=ALU.add),
                        reads=["l_STATE%d" % r, "l_EBEND", "l_pN%d" % a, "l_STATE"], writes=["l_STATE%d" % r])
                    P.op("scalar", lambda e, r=r: e.activation(out=STB[:, r, :], in_=STATE[:, r, :], func=AF.Copy),
                         reads=["l_STATE%d" % r], writes=["l_STB%d" % r])
                for (h, d) in items:
                    u = (h % 2) * 2 + d
                    dn, dnn = den[d], "l_den%d" % d
                    P.op("scalar", lambda e, dn=dn, u=u, h=h: e.activation(out=dn[:, h:h + 1], in_=pU[u][:, 128:129], func=AF.Abs),
                         reads=["l_pU%d" % u], writes=[dnn])
                for d in range(2):
                    c = cs[d]
                    dn, dnn = den[d], "l_den%d" % d
                    h0 = 2 * hh_
                    REB2 = TOK[:, c, 4, d * 4 + h0:d * 4 + h0 + 2]
                    P.op("vector", lambda e, dn=dn, REB2=REB2, h0=h0: e.tensor_tensor(out=dn[:, h0:h0 + 2], in0=dn[:, h0:h0 + 2], in1=REB2, op=ALU.max),
                         reads=[dnn, "l_TOK"], writes=[dnn])
                    P.op("vector", lambda e, dn=dn, h0=h0: e.reciprocal(dn[:, h0:h0 + 2], dn[:, h0:h0 + 2]), reads=[dnn], writes=[dnn])
                for (h, d) in items:
                    u = (h % 2) * 2 + d
                    dn, dnn = den[d], "l_den%d" % d
                    hsn = "l_HS%d%d" % (d, b)
                    if d == 0:
                        P.op("scalar", lambda e, b=b, h=h, dn=dn, u=u: e.activation(out=HS[0][b][:, h * 128:(h + 1) * 128], in_=pU[u][:, 0:128], func=AF.Copy, scale=dn[:, h:h + 1]),
                             reads=["l_pU%d" % u, dnn], writes=[hsn])
                    else:
                        P.op("vector", lambda e, b=b, h=h, dn=dn, u=u: e.tensor_scalar(HS[1][b][:, h * 128:(h + 1) * 128], pU[u][:, 0:128], dn[:, h:h + 1], None, ALU.mult),
                             reads=["l_pU%d" % u, dnn], writes=[hsn])
            P.dma("sync", lambda e, b=b, c=cs[0]: e.dma_start(out=G.HF[c * 128:(c + 1) * 128, :], in_=HS[0][b][:]), reads=["l_HS0%d" % b], writes=[_u()])
            P.dma("sync", lambda e, b=b, c=cs[1]: e.dma_start(out=G.HB[c * 128:(c + 1) * 128, :], in_=HS[1][b][:]), reads=["l_HS1%d" % b], writes=[_u()])
        P.barrier()
        P.emit()

    with contextlib.ExitStack() as st:
        sb, ps = _alloc(G, st)
        NB = 3
        hf = [sb("r_hf%d" % b, [128, 512], F32) for b in range(NB)]
        hb = [sb("r_hb%d" % b, [128, 512], F32) for b in range(NB)]
        g2 = [sb("r_g2%d" % b, [128, 512], F32) for b in range(NB)]
        junk = sb("r_junk", [128, 128], F32)
        ssq = [sb("r_ssq%d" % b, [128, 4], F32) for b in range(2)]
        Yb = [sb("r_Y%d" % b, [128, 512], BF16) for b in range(2)]
        ytb = [sb("r_yt%d" % b, [128, 4, 128], BF16) for b in range(2)]
        ptr2 = [ps("r_ptr%d" % b, [128, 4, 128], BF16) for b in range(2)]

        def ld5(c):
            if c >= NCH:
                return
            b3 = c % NB
            tk0 = OWN0 + c * 128
            P.dma("sync", lambda e, b3=b3, c=c: e.dma_start(out=hf[b3][:], in_=G.HF[c * 128:(c + 1) * 128, :]), writes=["r_hf%d" % b3])
            P.dma("sync", lambda e, b3=b3, c=c: e.dma_start(out=hb[b3][:], in_=G.HB[c * 128:(c + 1) * 128, :]), writes=["r_hb%d" % b3])
            P.dma("sync", lambda e, b3=b3, tk0=tk0: e.dma_start(out=g2[b3][:], in_=G.MLG2[tk0:tk0 + 128, :]), writes=["r_g2%d" % b3])
        ld5(0)
        ld5(1)
        for c in range(NCH):
            ld5(c + 2)
            b3, b = c % NB, c % 2
            P.op("vector", lambda e, b3=b3: e.tensor_tensor(out=hf[b3][:], in0=hf[b3][:], in1=hb[b3][:], op=ALU.add),
                 reads=["r_hf%d" % b3, "r_hb%d" % b3], writes=["r_hf%d" % b3])
            for h in range(4):
                P.op("scalar", lambda e, b3=b3, b=b, h=h: e.activation(out=junk[:], in_=hf[b3][:, h * 128:(h + 1) * 128], func=AF.Square, accum_out=ssq[b][:, h:h + 1]),
                     reads=["r_hf%d" % b3], writes=["r_junk", "r_ssq%d" % b])
            P.op("scalar", lambda e, b=b: e.activation(out=ssq[b][:], in_=ssq[b][:], func=AF.Sqrt, bias=G.c_eps[:], scale=1.0 / 128.0),
                 reads=["r_ssq%d" % b, "c_eps"], writes=["r_ssq%d" % b])
            P.op("vector", lambda e, b=b: e.reciprocal(ssq[b][:], ssq[b][:]), reads=["r_ssq%d" % b], writes=["r_ssq%d" % b])
            for h in range(4):
                P.op("vector", lambda e, b3=b3, b=b, h=h: e.scalar_tensor_tensor(
                    out=Yb[b][:, h * 128:(h + 1) * 128], in0=hf[b3][:, h * 128:(h + 1) * 128], scalar=ssq[b][:, h:h + 1],
                    in1=g2[b3][:, h * 128:(h + 1) * 128], op0=ALU.mult, op1=ALU.mult),
                    reads=["r_hf%d" % b3, "r_ssq%d" % b, "r_g2%d" % b3], writes=["r_Y%d" % b])
            for h in range(4):
                P.op("tensor", lambda e, b=b, h=h: e.transpose(ptr2[b][:, h, :], Yb[b][:, h * 128:(h + 1) * 128], G.ident_b[:]),
                     reads=["r_Y%d" % b, "ident_b"], writes=["r_ptr%d" % b], inc=(h == 3))
            P.op("scalar", lambda e, b=b: e.activation(out=ytb[b][:], in_=ptr2[b][:], func=AF.Copy), reads=["r_ptr%d" % b], writes=["r_yt%d" % b])
            P.dma("sync", lambda e, b=b, c=c: e.dma_start(out=G.YT[:, 4:8, c * 128:(c + 1) * 128], in_=ytb[b][:]), reads=["r_yt%d" % b], writes=[_u()])
        P.barrier()
        P.emit()


def stage_outproj(G):
    P, I, nc = G.P, G.I, G.nc
    l, j, m = 0, 1, 0
    with contextlib.ExitStack() as st:
        sb, ps = _alloc(G, st)
        wo = sb("o_wo", [128, 8, 1024], BF16)
        xt = [sb("o_xt%d" % b, [128, 8, TW], F32) for b in range(2)]
        yt = [sb("o_yt%d" % b, [128, 8, TW], BF16) for b in range(2)]
        po = [ps("o_po%d" % b, [128, TW]) for b in range(2)]
        load_w_cast(G, wo, "o_wo", I["mix_w_out"], 8, 1024)
        def ld(n):
            if n >= 16:
                return
            ti_, b_ = n + 1, n % 2
            P.dma("sync", lambda e, b_=b_, ti_=ti_: e.dma_start(out=xt[b_][:], in_=G.XT[:, :, ti_ * TW:(ti_ + 1) * TW]), reads=["XT%d" % ti_], writes=["o_xt%d" % b_])
            P.dma("sync", lambda e, b_=b_, n=n: e.dma_start(out=yt[b_][:], in_=G.YT[:, :, n * TW:(n + 1) * TW]), reads=["dramYT"], writes=["o_yt%d" % b_])
        ld(0)
        ld(1)
        for n in range(16):
            ti = n + 1
            t0 = ti * TW
            b = n % 2
            for f in range(8):
                q = f % 2
                for k in range(8):
                    P.op("tensor", lambda e, q=q, f=f, k=k, b=b: e.matmul(po[q][:], lhsT=wo[:, k, f * 128:(f + 1) * 128], rhs=yt[b][:, k, :], start=(k == 0), stop=(k == 7)),
                         reads=["o_wo", "o_yt%d" % b], writes=["o_po%d" % q], inc=(k == 7))
                gsc = G.GT[:, l, j, m, f:f + 1]
                P.op("vector", lambda e, q=q, f=f, b=b, gsc=gsc: e.scalar_tensor_tensor(
                    out=xt[b][:, f, :], in0=po[q][:], scalar=gsc, in1=xt[b][:, f, :], op0=ALU.mult, op1=ALU.add),
                    reads=["o_po%d" % q, "modc", "o_xt%d" % b], writes=["o_xt%d" % b])
            P.dma("sync", lambda e, b=b, t0=t0: e.dma_start(out=G.XT[:, :, t0:t0 + TW], in_=xt[b][:]), reads=["o_xt%d" % b], writes=["XT%d" % ti])
            ld(n + 2)
        P.barrier()
        P.emit()


def stage_sg(G):
    P, I, nc = G.P, G.I, G.nc
    l, j, m = 1, 1, 0
    with contextlib.ExitStack() as st:
        sb, ps = _alloc(G, st)
        wu = sb("g_wu", [128, 8, 2048], BF16)
        wv = sb("g_wv", [128, 8, 2048], BF16)
        wo = sb("g_wo", [128, 16, 1024], BF16)
        wsT = sb("g_wsT", [128, 8, 128], BF16)
        bs = sb("g_bs", [1, 1024], BF16)
        ones1 = sb("g_ones1", [1, 256], BF16)
        lng = sb("g_lng", [128, 2048], F32)
        lnb = sb("g_lnb", [128, 2048], F32)
        xt = [sb("g_xt%d" % b, [128, 8, TW], F32) for b in range(2)]
        hh = [sb("g_h%d" % b, [128, 8, TW], BF16) for b in range(2)]
        NS = NormScratch(G, sb, ps, "g_")
        uT = sb("g_uT", [128, 16, TW], BF16)
        vraw = [sb("g_vraw%d" % b, [128, 2048], F32) for b in range(2)]
        vn = [sb("g_vn%d" % b, [128, 2048], BF16) for b in range(2)]
        gated = sb("g_gated", [128, 16, TW], BF16)
        stats = [sb("g_stats%d" % b, [128, 4, 6], F32) for b in range(2)]
        mv = [sb("g_mv%d" % b, [128, 4], F32) for b in range(2)]
        pu = [ps("g_pu%d" % b, [128, TW]) for b in range(2)]
        pm = [ps("g_pm%d" % b, [128, TW]) for b in range(2)]
        po1 = ps("g_po", [128, TW])
        po = [po1, po1]
        pv = [ps("g_pv%d" % b, [128, 512]) for b in range(2)]
        load_w_cast(G, wu, "g_wu", I["sg_w_in"][:, 0:2048], 8, 2048)
        load_w_cast(G, wv, "g_wv", I["sg_w_in"][:, 2048:4096], 8, 2048)
        load_w_cast(G, wo, "g_wo", I["sg_w_out"], 16, 1024)
        P.dma("gpsimd", lambda e: e.dma_start(out=wsT[:], in_=I["sg_w_sT"][:, :, :]), writes=["g_wsT"])
        P.dma("gpsimd", lambda e: e.dma_start(out=bs[:], in_=I["sg_b_s"][:, :]), writes=["g_bs"])
        P.op("vector", lambda e: e.memset(ones1[:], 1.0), writes=["g_ones1"])
        P.dma("sync", lambda e: e.dma_start(out=lng[:], in_=I["sg_lng_bc"][:, :]), writes=["g_lng"])
        P.dma("sync", lambda e: e.dma_start(out=lnb[:], in_=I["sg_lnb_bc"][:, :]), writes=["g_lnb"])
        tiles = list(range(1, 17))

        def ld(n):
            ti_ = tiles[n]
            xb = xt[n % 2]
            P.dma("sync", lambda e, xb=xb, ti_=ti_: e.dma_start(out=xb[:], in_=G.XT[:, :, ti_ * TW:(ti_ + 1) * TW]),
                  reads=["XT%d" % ti_], writes=["g_xt%d" % (n % 2)])
        def do_norm(n):
            bb = n % 2
            norm_mod(G, NS, xt[bb], "g_xt%d" % bb, G.GM[:, l, j, m, :], G.SH[:, l, j, m, :], hh[bb], "g_h%d" % bb)
        ld(0)
        ld(1)
        do_norm(0)
        vcnt = 0
        for n, ti in enumerate(tiles):
            t0 = ti * TW
            b = n % 2
            x, xn = xt[b], "g_xt%d" % b
            h, hn = hh[b], "g_h%d" % b
            NSB = TW // 128
            for s_ in range(NSB):
                VR = vraw[s_]
                for blk in range(4):
                    q = blk % 2
                    for k in range(8):
                        P.op("tensor", lambda e, q=q, blk=blk, k=k, h=h, s_=s_: e.matmul(
                            pv[q][:], lhsT=h[:, k, s_ * 128:(s_ + 1) * 128], rhs=wv[:, k, blk * 512:(blk + 1) * 512], start=(k == 0), stop=(k == 7)),
                            reads=["g_wv", hn], writes=["g_pv%d" % q], inc=(k == 7))
                    P.op("scalar", lambda e, q=q, blk=blk, VR=VR: e.activation(out=VR[:, blk * 512:(blk + 1) * 512], in_=pv[q][:], func=AF.Gelu_apprx_tanh),
                         reads=["g_pv%d" % q], writes=["g_vraw%d" % s_])
                    P.op("vector", lambda e, blk=blk, VR=VR, s_=s_: e.bn_stats(stats[s_][:, blk, :], VR[:, blk * 512:(blk + 1) * 512]),
                         reads=["g_vraw%d" % s_], writes=["g_stats%d" % s_])
                P.op("vector", lambda e, s_=s_: e.bn_aggr(mv[s_][:, 0:2], stats[s_][:].rearrange("p a b -> p (a b)")), reads=["g_stats%d" % s_], writes=["g_mv%d" % s_])
                P.op("scalar", lambda e, s_=s_: e.activation(out=mv[s_][:, 2:3], in_=mv[s_][:, 1:2], func=AF.Sqrt, bias=G.c_eps[:], scale=1.0),
                     reads=["g_mv%d" % s_, "c_eps"], writes=["g_mv%d" % s_])
                P.op("vector", lambda e, s_=s_: e.reciprocal(mv[s_][:, 2:3], mv[s_][:, 2:3]), reads=["g_mv%d" % s_], writes=["g_mv%d" % s_])
                P.op("vector", lambda e, s_=s_: e.tensor_scalar(mv[s_][:, 3:4], mv[s_][:, 0:1], mv[s_][:, 2:3], -1.0, ALU.mult, ALU.mult),
                     reads=["g_mv%d" % s_], writes=["g_mv%d" % s_])
            for fc in range(16):
                q = fc % 2
                if fc == 6:
                    vns = []
                    for s_ in range(NSB):
                        VR = vraw[s_]
                        VN, vnn = vn[vcnt % 2], "g_vn%d" % (vcnt % 2)
                        vcnt += 1
                        vns.append((VN, vnn))
                        P.op("scalar", lambda e, VR=VR, s_=s_: e.activation(out=VR[:], in_=VR[:], func=AF.Identity, bias=mv[s_][:, 3:4], scale=mv[s_][:, 2:3]),
                             reads=["g_vraw%d" % s_, "g_mv%d" % s_], writes=["g_vraw%d" % s_])
                        P.op("vector", lambda e, VR=VR: e.tensor_tensor(out=VR[:], in0=VR[:], in1=lng[:], op=ALU.mult),
                             reads=["g_vraw%d" % s_, "g_lng"], writes=["g_vraw%d" % s_])
                        P.op("vector", lambda e, VR=VR, VN=VN: e.tensor_tensor(out=VN[:], in0=VR[:], in1=lnb[:], op=ALU.add),
                             reads=["g_vraw%d" % s_, "g_lnb"], writes=[vnn])
                for k in range(8):
                    P.op("tensor", lambda e, q=q, fc=fc, k=k, h=h: e.matmul(pu[q][:], lhsT=wu[:, k, fc * 128:(fc + 1) * 128], rhs=h[:, k, :], start=(k == 0), stop=(k == 7)),
                         reads=["g_wu", hn], writes=["g_pu%d" % q], inc=(k == 7))
                P.op("scalar", lambda e, q=q, fc=fc: e.activation(out=uT[:, fc, :], in_=pu[q][:], func=AF.Gelu_apprx_tanh), reads=["g_pu%d" % q], writes=["g_uT%d" % fc])
            for s_ in range(NSB):
                VN, vnn = vns[s_]
                for fc in range(16):
                    g_ = fc // 2
                    q = fc % 2
                    P.op("tensor", lambda e, q=q, fc=fc, g_=g_, VN=VN, s_=s_: e.matmul(
                        pm[q][:, s_ * 128:(s_ + 1) * 128], lhsT=VN[:, fc * 128:(fc + 1) * 128], rhs=wsT[:, g_, :], start=True, stop=False),
                        reads=[vnn, "g_wsT"], writes=["g_pm%d" % q], inc=False)
                    P.op("tensor", lambda e, q=q, g_=g_, s_=s_: e.matmul(
                        pm[q][:, s_ * 128:(s_ + 1) * 128], lhsT=ones1[0:1, 0:128], rhs=bs[0:1, g_ * 128:(g_ + 1) * 128], start=False, stop=True),
                        reads=["g_ones1", "g_bs"], writes=["g_pm%d" % q], inc=True)
                    P.op("vector", lambda e, q=q, fc=fc, s_=s_: e.tensor_tensor(
                        out=gated[:, fc, s_ * 128:(s_ + 1) * 128], in0=uT[:, fc, s_ * 128:(s_ + 1) * 128], in1=pm[q][:, s_ * 128:(s_ + 1) * 128], op=ALU.mult),
                        reads=["g_uT%d" % fc, "g_pm%d" % q], writes=["g_gated%d" % fc])
            if n + 1 < len(tiles):
                do_norm(n + 1)
            for f in range(8):
                q = f % 2
                for fc in range(16):
                    P.op("tensor", lambda e, q=q, f=f, fc=fc: e.matmul(po[q][:], lhsT=wo[:, fc, f * 128:(f + 1) * 128], rhs=gated[:, fc, :], start=(fc == 0), stop=(fc == 15)),
                         reads=["g_wo", "g_gated%d" % fc], writes=["g_po"], inc=(fc == 15))
                gsc = G.GT[:, l, j, m, f:f + 1]
                P.op("vector", lambda e, q=q, f=f, x=x, gsc=gsc: e.scalar_tensor_tensor(
                    out=x[:, f, :], in0=po[q][:], scalar=gsc, in1=x[:, f, :], op0=ALU.mult, op1=ALU.add),
                    reads=["g_po", "modc", xn], writes=[xn])
            P.dma("sync", lambda e, x=x, t0=t0: e.dma_start(out=G.XT[:, :, t0:t0 + TW], in_=x[:]), reads=[xn], writes=["XT%d" % ti])
            if n + 2 < len(tiles):
                ld(n + 2)
        P.barrier()
        P.emit()


_NC_CACHE = {}


def kernel(**inputs):
    maps = _host_prep(inputs)
    if "nc" not in _NC_CACHE:
        _NC_CACHE["nc"] = build()
    nc = _NC_CACHE["nc"]
    res = run_bass_kernel_spmd(nc, maps, core_ids=list(range(8)))
    out = np.zeros((2, 16384, 1024), np.float32)
    for core in range(8):
        b, s = core // 4, core % 4
        out[b, s * 4096:(s + 1) * 4096, :] = res.results[core]["out"]
    return out
```

```python
import contextlib
import numpy as np
import concourse.bass as bass
import concourse.mybir as mybir
from concourse.bass_utils import run_bass_kernel_spmd

F32 = mybir.dt.float32
BF16 = mybir.dt.bfloat16
AF = mybir.ActivationFunctionType
ALU = mybir.AluOpType
AX = mybir.AxisListType

D = 1024
DFF = 2816
NE = 4608
NU = 4864
OWN0 = 256
NOWN = 4096
CTX0 = 4608
TW = 256
EPS = 1e-6
NEG = -30000.0

ENGS = ("tensor", "vector", "scalar", "gpsimd", "sync")
NDMASEM = 32
NHW = 24


class Buf:
    __slots__ = ("name", "writers", "readers")

    def __init__(self, name):
        self.name = name
        self.writers = []
        self.readers = []


class Prog:
    def __init__(self, nc, st):
        self.nc = nc
        self.q = {e: [] for e in ENGS}
        self.cnt = {e: 0 for e in ENGS}
        self.seen = {e: {} for e in ENGS}
        self.dcnt = [0] * NDMASEM
        self.dnext = 0
        self.dnext_sw = 0
        self.bufs = {}
        self.esem = {e: st.enter_context(nc.semaphore("s_" + e)) for e in ENGS}
        self.dsem = [st.enter_context(nc.semaphore("d%d" % i)) for i in range(NDMASEM)]
        self.lastinc = {e: True for e in ENGS}

    def buf(self, name):
        b = self.bufs.get(name)
        if b is None:
            b = self.bufs[name] = Buf(name)
        return b

    def _bl(self, lst):
        return [self.buf(b) if isinstance(b, str) else b for b in lst]

    def _deps(self, reads, writes):
        deps = {}
        for b in reads:
            for k, v in b.writers:
                if deps.get(k, 0) < v:
                    deps[k] = v
        for b in writes:
            for k, v in b.writers:
                if deps.get(k, 0) < v:
                    deps[k] = v
            for k, v in b.readers:
                if deps.get(k, 0) < v:
                    deps[k] = v
        return deps

    def _waits(self, eng, deps):
        seen = self.seen[eng]
        waits = []
        for k, v in deps.items():
            if k == "tensor" and eng == "tensor":
                continue
            if seen.get(k, 0) >= v:
                continue
            seen[k] = v
            waits.append((k, v))
        return waits

    def _record(self, ev, reads, writes):
        k = ev[0]
        for b in writes:
            b.writers = [ev]
            b.readers = []
        for b in reads:
            if b in writes:
                continue
            b.readers = [e for e in b.readers if e[0] != k] + [ev]

    def op(self, eng, fn, reads=(), writes=(), inc=True):
        reads = self._bl(reads)
        writes = self._bl(writes)
        waits = self._waits(eng, self._deps(reads, writes))
        if inc:
            self.cnt[eng] += 1
            ev = (eng, self.cnt[eng])
        else:
            ev = (eng, self.cnt[eng] + 1)
        self.lastinc[eng] = inc
        self.q[eng].append(("op", fn, waits, inc))
        self._record(ev, reads, writes)

    def dma(self, eng, fn, reads=(), writes=()):
        reads = self._bl(reads)
        writes = self._bl(writes)
        deps = self._deps(reads, writes)
        if eng == "gpsimd":
            i = NHW + self.dnext_sw
            self.dnext_sw = (self.dnext_sw + 1) % (NDMASEM - NHW)
        else:
            i = self.dnext
            self.dnext = (self.dnext + 1) % NHW
        key = ("d", i)
        if self.dcnt[i] > 0 and deps.get(key, 0) < self.dcnt[i]:
            deps[key] = self.dcnt[i]
        waits = self._waits(eng, deps)
        self.dcnt[i] += 16
        ev = (key, self.dcnt[i])
        self.q[eng].append(("dma", fn, waits, i))
        self._record(ev, reads, writes)

    def cc(self, fn, reads=(), writes=()):
        self.op("gpsimd", fn, reads=reads, writes=writes, inc=True)

    def barrier(self):
        for e in ENGS:
            assert self.lastinc[e], e
        deps = {e: self.cnt[e] for e in ENGS if self.cnt[e] > 0}
        for i in range(NDMASEM):
            if self.dcnt[i] > 0:
                deps[("d", i)] = self.dcnt[i]
        for e in ENGS:
            d = {k: v for k, v in deps.items() if k != e}
            waits = self._waits(e, d)
            self.q[e].append(("wait", None, waits, None))

    def emit(self):
        nc = self.nc
        esem, dsem = self.esem, self.dsem

        def semof(k):
            return esem[k] if isinstance(k, str) else dsem[k[1]]

        def run(engname):
            items = self.q[engname]

            def body(e):
                for kind, fn, waits, x in items:
                    for k, v in waits:
                        e.wait_ge(semof(k), v)
                    if kind == "op":
                        ins = fn(e)
                        if x:
                            ins.then_inc(esem[engname], 1)
                    elif kind == "dma":
                        fn(e).then_inc(dsem[x], 16)
            return body

        with nc.Block() as block:
            block.tensor(run("tensor"))
            block.vector(run("vector"))
            block.scalar(run("scalar"))
            block.gpsimd(run("gpsimd"))
            block.sync(run("sync"))
        self.q = {e: [] for e in ENGS}


def _rowmap(s):
    rm = np.zeros(72, np.int64)
    rm[4:68] = s * 64 + np.arange(64)
    rm[0:4] = (s * 64 - 4 + np.arange(4)) if s > 0 else np.array([5, 6, 7, 8])
    rm[68:72] = (s * 64 + 64 + np.arange(4)) if s < 3 else np.array([248, 249, 250, 251])
    return rm


def _na_bias_tables(rpb, s):
    rm = _rowmap(s)
    out = np.full((5, 8, 128, 576), NEG, np.float32)
    reps = [0, 1, 2, 30, 31]
    cols = np.arange(64)
    c0 = np.clip(cols - 8, 0, 64 - 16)
    for ci, p in enumerate(reps):
        for r in range(2):
            i = s * 64 + 2 * p + r
            r0 = min(max(i - 4, 0), 256 - 8)
            seen_rows = set()
            for j in range(9):
                krow = int(rm[2 * p + j])
                if krow < r0 or krow >= r0 + 8 or krow in seen_rows:
                    continue
                seen_rows.add(krow)
                rr = krow - i + 7
                for qc in range(64):
                    kc = np.arange(c0[qc], c0[qc] + 16)
                    out[ci][:, r * 64 + qc, j * 64 + kc] = rpb[:, rr, kc - qc + 15]
    return out


def _rope_tables(s):
    rm = _rowmap(s)
    row = np.repeat(rm, 64).astype(np.float32)
    col = np.tile(np.arange(64), 72).astype(np.float32)
    inv = (10000.0 ** (-np.arange(32, dtype=np.float32) / 32)).astype(np.float32)
    ang = np.concatenate([row[:, None] * inv, col[:, None] * inv], axis=-1).astype(np.float32)
    cos = np.cos(ang).astype(np.float32)
    sin = np.sin(ang).astype(np.float32)
    cos2 = np.repeat(cos, 2, axis=1)
    sin2 = np.repeat(sin, 2, axis=1)
    sin2[:, 0::2] *= -1.0
    cos_u = np.ones((NU, 128), np.float32)
    sin_u = np.zeros((NU, 128), np.float32)
    cos_u[:NE] = cos2
    sin_u[:NE] = sin2
    return np.tile(cos_u, (1, 4)), np.tile(sin_u, (1, 4))


def _host_prep(inp):
    x = np.asarray(inp["x"], np.float32)
    shared = {}
    shared["w_mod"] = np.ascontiguousarray(inp["w_mod"], np.float32)
    shared["b_modT"] = np.ascontiguousarray(np.asarray(inp["b_mod"], np.float32).reshape(2, 72, 128).transpose(2, 0, 1))
    shared["norm_gT"] = np.ascontiguousarray(np.asarray(inp["norm_g"], np.float32).reshape(2, 3, 8, 128).transpose(3, 0, 1, 2))
    shared["ffn_w_in"] = np.ascontiguousarray(inp["ffn_w_in"], np.float32)
    shared["ffn_w_out"] = np.ascontiguousarray(inp["ffn_w_out"], np.float32)
    mw = np.asarray(inp["mix_w_in"], np.float32)[0]
    shared["mix_w_in"] = np.ascontiguousarray(mw[:, :3584])
    wg = np.zeros((1024, 128), np.float32)
    gb = np.zeros((128, 1), np.float32)
    gate_b = np.asarray(inp["ml_gate_b"], np.float32)[0]
    for h in range(4):
        for d in range(2):
            for q in range(2):
                wg[:, q * 64 + d * 32 + h] = mw[:, 3584 + h * 4 + d * 2 + q]
                gb[q * 64 + d * 32 + h, 0] = gate_b[h, d, q]
    shared["w_gate"] = wg
    shared["gate_b"] = gb
    shared["headg_bc"] = np.ascontiguousarray(np.broadcast_to(np.asarray(inp["ml_head_g"], np.float32)[0].reshape(1, 512), (128, 512)))
    shared["mix_w_out"] = np.ascontiguousarray(np.asarray(inp["mix_w_out"], np.float32)[0])
    shared["sg_w_in"] = np.ascontiguousarray(np.asarray(inp["sg_w_in"], np.float32)[0])
    shared["sg_w_out"] = np.ascontiguousarray(np.asarray(inp["sg_w_out"], np.float32)[0])
    shared["sg_lng_bc"] = np.ascontiguousarray(np.broadcast_to(np.asarray(inp["sg_ln_g"], np.float32)[0].reshape(1, 2048), (128, 2048)))
    shared["sg_lnb_bc"] = np.ascontiguousarray(np.broadcast_to(np.asarray(inp["sg_ln_b"], np.float32)[0].reshape(1, 2048), (128, 2048)))
    shared["sg_w_sT"] = np.ascontiguousarray(np.asarray(inp["sg_w_s"], np.float32)[0].transpose(2, 0, 1))
    shared["sg_b_s"] = np.ascontiguousarray(np.asarray(inp["sg_b_s"], np.float32)[0].reshape(1, 1024))
    shared["final_g_bc"] = np.ascontiguousarray(np.broadcast_to(np.asarray(inp["final_g"], np.float32).reshape(1, 1024), (128, 1024)))
    shared["ident"] = np.eye(128, dtype=np.float32)
    tri = np.zeros((128, 2, 128), np.float32)
    ss, tt = np.meshgrid(np.arange(128), np.arange(128), indexing="ij")
    tri[:, 0, :] = (tt >= ss)
    tri[:, 1, :] = (tt <= ss)
    shared["tri"] = tri
    sel = np.zeros((64, 8, 128), np.float32)
    for d in range(2):
        for h in range(4):
            sel[d * 32 + h, d * 4 + h, :] = 1.0
    shared["selm"] = sel
    rpb = np.asarray(inp["na_rpb"], np.float32)[0]
    c = np.asarray(inp["c"], np.float32)
    cctx = np.asarray(inp["c_ctx"], np.float32)
    ctx = np.asarray(inp["ctx"], np.float32)
    maps = []
    for core in range(8):
        b, s = core // 4, core % 4
        rm = _rowmap(s)
        tok = (rm[:, None] * 64 + np.arange(64)[None, :]).reshape(-1)
        xin = np.concatenate([x[b][tok], ctx[b]], axis=0)
        cT = np.stack([c[b].reshape(8, 128).T, cctx.reshape(8, 128).T], axis=-1)
        cos4, sin4 = _rope_tables(s)
        selv = np.zeros((128, 16), np.float32)
        for j in range(4):
            selv[:, j] = 1.0 if j < s else 0.0
            selv[:, 4 + j] = 1.0 - selv[:, j]
            selv[:, 8 + j] = 1.0 if j > s else 0.0
            selv[:, 12 + j] = 1.0 - selv[:, 8 + j]
        m = dict(shared)
        m["xin"] = np.ascontiguousarray(xin)
        m["cT"] = np.ascontiguousarray(cT.astype(np.float32))
        m["ropecos"] = cos4
        m["ropesin"] = sin4
        nb = _na_bias_tables(rpb, s)
        nbp = np.full((5, 8, 128, 640), NEG, np.float32)
        nbp[..., :576] = nb
        m["nabiasT"] = np.ascontiguousarray(nbp.reshape(5, 8, 128, 5, 128).transpose(0, 1, 4, 3, 2))
        m["selv"] = selv
        maps.append(m)
    return maps


INPUT_SHAPES = {
    "xin": [NU, 1024], "cT": [128, 8, 2], "w_mod": [2, 1024, 9216], "b_modT": [128, 2, 72], "norm_gT": [128, 2, 3, 8],
    "ffn_w_in": [2, 2, 1024, 5632], "ffn_w_out": [2, 2, 2816, 1024], "mix_w_in": [1024, 3584], "w_gate": [1024, 128],
    "gate_b": [128, 1], "headg_bc": [128, 512], "mix_w_out": [1024, 1024], "sg_w_in": [1024, 4096], "sg_w_out": [2048, 1024],
    "sg_lng_bc": [128, 2048], "sg_lnb_bc": [128, 2048], "sg_w_sT": [128, 8, 128], "sg_b_s": [1, 1024], "final_g_bc": [128, 1024],
    "ident": [128, 128], "tri": [128, 2, 128], "selm": [64, 8, 128], "ropecos": [NU, 512], "ropesin": [NU, 512],
    "nabiasT": [5, 8, 128, 5, 128], "selv": [128, 16],
}


class Ctx:
    pass


_UC = [0]


def _u():
    _UC[0] += 1
    return "u%d" % _UC[0]


def build(stages="all", dbg=()):
    nc = bass.Bass("TRN2", target_bir_lowering=False)
    I = {k: nc.dram_tensor(k, shp, F32, kind="ExternalInput").ap() for k, shp in INPUT_SHAPES.items()}
    OUT = nc.dram_tensor("out", [NOWN, 1024], F32, kind="ExternalOutput").ap()

    def scratch(name, shape, dt):
        if name in dbg:
            return nc.dram_tensor(name, shape, dt, kind="ExternalOutput").ap()
        return nc.dram_tensor(name, shape, dt).ap()

    G = Ctx()
    G.nc, G.I, G.OUT = nc, I, OUT
    G.dbg = dbg
    if dbg:
        G.DBG = {k: nc.dram_tensor(k, shp, dt, kind="ExternalOutput").ap() for k, (shp, dt) in {
            "D_MOD": ([128, 3, 2, 3, 2, 8], F32), "D_X0": ([128, 8, TW], F32), "D_X1": ([128, 8, TW], F32),
            "D_H": ([128, 8, TW], BF16), "D_HID": ([128, 22, TW], BF16), "D_RSTD": ([128, TW], F32)}.items()}
    G.XT = scratch("XT", [128, 8, NU], F32)
    G.NAQT = scratch("NAQT", [128, 4, NU], BF16)
    G.NAKT = scratch("NAKT", [128, 4, NU], BF16)
    G.NAV = scratch("NAV", [NU, 520], BF16)
    G.MLQT = scratch("MLQT", [128, 4, NU], BF16)
    G.MLKT = scratch("MLKT", [128, 4, NU], BF16)
    G.MLK = scratch("MLK", [NU, 512], BF16)
    G.MLV = scratch("MLV", [NU, 516], BF16)
    G.MLG2 = scratch("MLG2", [NU, 512], F32)
    G.GIF = scratch("GIF", [128, NU], F32)
    G.YT = scratch("YT", [128, 8, NOWN], BF16)
    G.HF = scratch("HF", [NOWN, 512], F32)
    G.HB = scratch("HB", [NOWN, 512], F32)
    G.CCI = scratch("CCI", [128, 1040], F32)
    G.CCO = scratch("CCO", [512, 1040], F32)

    with contextlib.ExitStack() as gst:
        P = Prog(nc, gst)
        G.P = P

        def gsb(name, shape, dt):
            return gst.enter_context(nc.sbuf_tensor(name, shape, dt))
        G.ident_f = gsb("ident_f", [128, 128], F32)
        G.ident_b = gsb("ident_b", [128, 128], BF16)
        G.ident8 = gsb("ident8", [128, 128], BF16)
        G.ones_b = gsb("ones_b", [128, 128], BF16)
        G.ones_f = gsb("ones_f", [128, 128], F32)
        G.c_eps = gsb("c_eps", [128, 1], F32)
        G.c_one = gsb("c_one", [128, 1], F32)
        G.SH = gsb("SH", [128, 2, 3, 2, 8], F32)
        G.GM = gsb("GM", [128, 2, 3, 2, 8], F32)
        G.GT = gsb("GT", [128, 2, 3, 2, 8], F32)

        P.dma("sync", lambda e: e.dma_start(out=G.ident_f[:], in_=I["ident"][:, :]), writes=["ident_f"])
        P.op("vector", lambda e: e.tensor_copy(G.ident_b[:], G.ident_f[:]), reads=["ident_f"], writes=["ident_b"])
        P.op("vector", lambda e: e.tensor_scalar(G.ident8[:], G.ident_f[:], 8.0, None, ALU.mult), reads=["ident_f"], writes=["ident8"])
        P.op("vector", lambda e: e.memset(G.ones_b[:], 1.0), writes=["ones_b"])
        P.op("vector", lambda e: e.memset(G.ones_f[:], 1.0), writes=["ones_f"])
        P.op("vector", lambda e: e.memset(G.c_eps[:], EPS), writes=["c_eps"])
        P.op("vector", lambda e: e.memset(G.c_one[:], 1.0), writes=["c_one"])

        ext_tiles = [(ti, 0) for ti in range(18)] + [(18, 1)]
        own_tiles = [(ti, 0) for ti in range(1, 17)]
        S = stages
        stage_mod(G)
        stage_t0(G)
        if S in ("ffn_only",):
            stage_ffn(G, 0, 0, ext_tiles)
            stage_final(G)
        elif S == "ffn_dbg":
            stage_ffn(G, 0, 0, [(1, 0)])
        else:
            stage_ffn(G, 0, 0, ext_tiles)
            stage_inproj(G)
            stage_na(G)
            stage_ml(G)
            stage_outproj(G)
            stage_ffn(G, 0, 1, own_tiles)
            stage_ffn(G, 1, 0, own_tiles)
            stage_sg(G)
            stage_ffn(G, 1, 1, own_tiles)
            stage_final(G)
    return nc


_AC = [0]


def _alloc(G, st):
    nc = G.nc
    _AC[0] += 1
    sfx = "_%d" % _AC[0]

    def sb(name, shape, dt):
        return st.enter_context(nc.sbuf_tensor(name + sfx, shape, dt))

    def ps(name, shape, dt=F32):
        return st.enter_context(nc.psum_tensor(name + sfx, shape, dt))
    return sb, ps


def stage_mod(G):
    P, I, nc = G.P, G.I, G.nc
    with contextlib.ExitStack() as st:
        sb, ps = _alloc(G, st)
        cT = sb("m_cT", [128, 8, 2], F32)
        sil = sb("m_sil", [128, 8, 2], BF16)
        wm = [sb("m_wm%d" % i, [128, 8, 1024], BF16) for i in range(3)]
        modv = sb("m_modv", [128, 2, 72, 2], F32)
        bmod = sb("m_bmod", [128, 2, 72], F32)
        ng = sb("m_ng", [128, 2, 3, 8], F32)
        pp = [ps("m_ps%d" % i, [128, 8, 2]) for i in range(2)]
        xs = [sb("t_xs%d" % i, [128, 1024], F32) for i in range(3)]
        xo = [sb("t_xo%d" % i, [128, 8, 128], F32) for i in range(2)]
        pt = [ps("t_pt%d" % i, [128, 8, 128]) for i in range(2)]
        P.dma("sync", lambda e: e.dma_start(out=cT[:], in_=I["cT"][:, :, :]), writes=["m_cT"])
        P.dma("sync", lambda e: e.dma_start(out=bmod[:], in_=I["b_modT"][:, :, :]), writes=["m_bmod"])
        P.dma("sync", lambda e: e.dma_start(out=ng[:], in_=I["norm_gT"][:, :, :, :]), writes=["m_ng"])
        P.op("scalar", lambda e: e.activation(out=sil[:], in_=cT[:], func=AF.Silu), reads=["m_cT"], writes=["m_sil"])
        NSUB = NU // 128

        def t0_load(i):
            if i < NSUB:
                b3 = i % 3
                P.dma("sync", lambda e, b3=b3, i=i: e.dma_start(out=xs[b3][:], in_=I["xin"][i * 128:(i + 1) * 128, :]), writes=["t_xs%d" % b3])

        def t0_sub(i):
            t0 = i * 128
            b, b3 = i % 2, i % 3
            for k in range(8):
                P.op("tensor", lambda e, b=b, b3=b3, k=k: e.transpose(pt[b][:, k, :], xs[b3][:, k * 128:(k + 1) * 128], G.ident_f[:]),
                     reads=["t_xs%d" % b3, "ident_f"], writes=["t_pt%d" % b], inc=(k == 7))
            if b == 0:
                P.op("vector", lambda e, b=b: e.tensor_copy(xo[b][:], pt[b][:]), reads=["t_pt%d" % b], writes=["t_xo%d" % b])
            else:
                P.op("scalar", lambda e, b=b: e.activation(out=xo[b][:], in_=pt[b][:], func=AF.Identity), reads=["t_pt%d" % b], writes=["t_xo%d" % b])
            t0_load(i + 3)
            P.dma("sync", lambda e, b=b, t0=t0: e.dma_start(out=G.XT[:, :, t0:t0 + 128], in_=xo[b][:]),
                  reads=["t_xo%d" % b], writes=["XT%d" % (t0 // TW)])

        for i in range(3):
            t0_load(i)
        n = 0
        ti = 0
        for l in range(2):
            wsrc = I["w_mod"][l].rearrange("(k p) n -> p k n", p=128)
            for blk in range(9):
                w = wm[n % 3]
                wn = "m_wm%d" % (n % 3)
                pt_ = pp[n % 2]
                pn = "m_ps%d" % (n % 2)
                P.dma("gpsimd", lambda e, w=w, blk=blk, wsrc=wsrc: e.dma_start(out=w[:], in_=wsrc[:, :, blk * 1024:(blk + 1) * 1024]),
                      writes=[wn])
                for jj in range(8):
                    for k in range(8):
                        P.op("tensor", lambda e, w=w, pt_=pt_, jj=jj, k=k: e.matmul(
                            pt_[:, jj, :], lhsT=w[:, k, jj * 128:(jj + 1) * 128], rhs=sil[:, k, :], start=(k == 0), stop=(k == 7)),
                            reads=[wn, "m_sil"], writes=[pn], inc=(jj == 7 and k == 7))
                for m in range(2):
                    P.op("vector", lambda e, pt_=pt_, l=l, blk=blk, m=m: e.tensor_tensor(
                        out=modv[:, l, blk * 8:(blk + 1) * 8, m], in0=pt_[:, :, m], in1=bmod[:, l, blk * 8:(blk + 1) * 8], op=ALU.add),
                        reads=[pn, "m_bmod"], writes=["m_modv"])
                n += 1
                for _ in range(2):
                    if ti < NSUB:
                        t0_sub(ti)
                        ti += 1
        while ti < NSUB:
            t0_sub(ti)
            ti += 1
        for l in range(2):
            for j in range(3):
                for m in range(2):
                    sh = modv[:, l, (3 * j) * 8:(3 * j) * 8 + 8, m]
                    sc = modv[:, l, (3 * j + 1) * 8:(3 * j + 1) * 8 + 8, m]
                    gt = modv[:, l, (3 * j + 2) * 8:(3 * j + 2) * 8 + 8, m]
                    P.op("vector", lambda e, sh=sh, l=l, j=j, m=m: e.tensor_copy(G.SH[:, l, j, m, :], sh), reads=["m_modv"], writes=["modc"])
                    P.op("vector", lambda e, sc=sc, l=l, j=j, m=m: e.scalar_tensor_tensor(
                        out=G.GM[:, l, j, m, :], in0=sc, scalar=1.0, in1=ng[:, l, j, :], op0=ALU.add, op1=ALU.mult),
                        reads=["m_modv", "m_ng"], writes=["modc"])
                    P.op("vector", lambda e, gt=gt, l=l, j=j, m=m: e.tensor_scalar(
                        G.GT[:, l, j, m, :], gt, (1.0 if j == 1 else 0.5), None, ALU.mult), reads=["m_modv"], writes=["modc"])
        P.barrier()
        P.emit()


def stage_t0(G):
    return


class NormScratch:
    def __init__(self, G, sb, ps, pfx, W=TW):
        self.sq = [sb(pfx + "sq%d" % i, [128, W], BF16) for i in range(2)]
        self.rs = sb(pfx + "rs", [128, W], F32)
        self.rstd = sb(pfx + "rstd", [128, W], F32)
        self.tmp = [sb(pfx + "tmp%d" % i, [128, W], F32) for i in range(2)]
        self.pn = ps(pfx + "pn", [128, W])
        self.pfx = pfx


def norm_mod(G, NS, x, xname, gm, sh, h, hname, W=TW, phase=None):
    P = G.P
    pfx = NS.pfx
    for k in range(8):
        if phase is None:
            sq, sqn = NS.sq[k % 2], pfx + "sq%d" % (k % 2)
        else:
            sq, sqn = NS.sq8[k], pfx + "sq8_%d" % k
        if phase in (None, "A"):
            P.op("scalar", lambda e, sq=sq, k=k: e.activation(out=sq[:, :W], in_=x[:, k, :], func=AF.Square), reads=[xname], writes=[sqn])
        if phase in (None, "B"):
            P.op("tensor", lambda e, sq=sq, k=k: e.matmul(NS.pn[:, :W], lhsT=G.ones_b[:], rhs=sq[:, :W], start=(k == 0), stop=(k == 7)),
                 reads=[sqn, "ones_b"], writes=[pfx + "pn"], inc=True)
    if phase == "A":
        return
    P.op("scalar", lambda e: e.activation(out=NS.rs[:, :W], in_=NS.pn[:, :W], func=AF.Sqrt, bias=G.c_eps[:], scale=1.0 / 1024.0),
         reads=[pfx + "pn", "c_eps"], writes=[pfx + "rs"])
    P.op("vector", lambda e: e.reciprocal(NS.rstd[:, :W], NS.rs[:, :W]), reads=[pfx + "rs"], writes=[pfx + "rstd"])
    for k in range(8):
        tmp = NS.tmp[k % 2]
        tn = pfx + "tmp%d" % (k % 2)
        P.op("vector", lambda e, tmp=tmp, k=k: e.tensor_tensor(out=tmp[:, :W], in0=x[:, k, :], in1=NS.rstd[:, :W], op=ALU.mult),
             reads=[xname, pfx + "rstd"], writes=[tn])
        P.op("scalar", lambda e, tmp=tmp, k=k: e.activation(out=h[:, k, :], in_=tmp[:, :W], func=AF.Identity, bias=sh[:, k:k + 1], scale=gm[:, k:k + 1]),
             reads=[tn, "modc"], writes=[hname])


def load_w_cast(G, dst, dname, src, nk, ncols, step=1024):
    P = G.P
    v = src.rearrange("(k p) n -> p k n", p=128)
    for c0 in range(0, ncols, step):
        c1 = min(ncols, c0 + step)
        P.dma("gpsimd", lambda e, c0=c0, c1=c1: e.dma_start(out=dst[:, :, c0:c1], in_=v[:, :, c0:c1]), writes=[dname])


def stage_ffn(G, l, i, tiles):
    P, I, nc = G.P, G.I, G.nc
    j = 0 if i == 0 else 2
    with contextlib.ExitStack() as st:
        sb, ps = _alloc(G, st)
        wi = sb("f_wi", [128, 8, 2 * DFF], BF16)
        wo = sb("f_wo", [128, 22, 1024], BF16)
        xt = [sb("f_xt%d" % b, [128, 8, TW], F32) for b in range(2)]
        hh = [sb("f_h%d" % b, [128, 8, TW], BF16) for b in range(2)]
        hid = sb("f_hid", [128, 22, TW], BF16)
        sa = [sb("f_sa%d" % b, [128, TW], F32) for b in range(2)]
        NS = NormScratch(G, sb, ps, "f_")
        NS.sq8 = [sb("f_sq8_%d" % k, [128, TW], BF16) for k in range(8)]
        pa = [ps("f_pa%d" % b, [128, TW]) for b in range(2)]
        pb = [ps("f_pb%d" % b, [128, TW]) for b in range(2)]
        po = [ps("f_po%d" % b, [128, TW]) for b in range(2)]
        wv_ = I["ffn_w_in"][l, i].rearrange("(k p) n -> p k n", p=128)
        for pc in (0, 2, 3, 1, 4, 5):
            c0, c1 = pc * 1024, min(2 * DFF, (pc + 1) * 1024)
            P.dma("gpsimd", lambda e, c0=c0, c1=c1: e.dma_start(out=wi[:, :, c0:c1], in_=wv_[:, :, c0:c1]), writes=["f_wi%d" % pc])
        load_w_cast(G, wo, "f_wo", I["ffn_w_out"][l, i], 22, 1024)
        def ld(n):
            ti_ = tiles[n][0]
            xb = xt[n % 2]
            P.dma("sync", lambda e, xb=xb, ti_=ti_: e.dma_start(out=xb[:], in_=G.XT[:, :, ti_ * TW:(ti_ + 1) * TW]),
                  reads=["XT%d" % ti_], writes=["f_xt%d" % (n % 2)])
        def do_norm(n, phase=None):
            ti_, m_ = tiles[n]
            bb = n % 2
            norm_mod(G, NS, xt[bb], "f_xt%d" % bb, G.GM[:, l, j, m_, :], G.SH[:, l, j, m_, :], hh[bb], "f_h%d" % bb, phase=phase)
        ld(0)
        if len(tiles) > 1:
            ld(1)
        do_norm(0)
        for n, (ti, m) in enumerate(tiles):
            t0 = ti * TW
            b = n % 2
            x, xn = xt[b], "f_xt%d" % b
            h, hn = hh[b], "f_h%d" % b
            for jj in range(22):
                q = jj % 2
                for half, pp, pn in ((0, pa[q], "f_pa%d" % q), (1, pb[q], "f_pb%d" % q)):
                    c0 = half * DFF + jj * 128
                    for k in range(8):
                        P.op("tensor", lambda e, pp=pp, c0=c0, k=k, h=h: e.matmul(
                            pp[:], lhsT=wi[:, k, c0:c0 + 128], rhs=h[:, k, :], start=(k == 0), stop=(k == 7)),
                            reads=["f_wi%d" % (c0 // 1024), "f_wi%d" % ((c0 + 127) // 1024), hn], writes=[pn], inc=(k == 7))
                P.op("scalar", lambda e, q=q: e.activation(out=sa[q][:], in_=pa[q][:], func=AF.Silu), reads=["f_pa%d" % q], writes=["f_sa%d" % q])
                P.op("vector", lambda e, q=q, jj=jj: e.tensor_tensor(out=hid[:, jj, :], in0=sa[q][:], in1=pb[q][:], op=ALU.mult),
                     reads=["f_sa%d" % q, "f_pb%d" % q], writes=["f_hid%d" % jj])
            if n + 1 < len(tiles):
                do_norm(n + 1, "A")
            for f in range(8):
                q = f % 2
                if f == 3 and n + 1 < len(tiles):
                    do_norm(n + 1, "B")
                for jj in range(22):
                    P.op("tensor", lambda e, q=q, f=f, jj=jj: e.matmul(
                        po[q][:], lhsT=wo[:, jj, f * 128:(f + 1) * 128], rhs=hid[:, jj, :], start=(jj == 0), stop=(jj == 21)),
                        reads=["f_wo", "f_hid%d" % jj], writes=["f_po%d" % q], inc=(jj == 21))
                gsc = G.GT[:, l, j, m, f:f + 1]
                P.op("vector", lambda e, q=q, f=f, x=x, gsc=gsc: e.scalar_tensor_tensor(
                    out=x[:, f, :], in0=po[q][:], scalar=gsc, in1=x[:, f, :], op0=ALU.mult, op1=ALU.add),
                    reads=["f_po%d" % q, "modc", xn], writes=[xn])
            if G.dbg and ti == 1:
                P.dma("sync", lambda e: e.dma_start(out=G.DBG["D_HID"][:, :, :], in_=hid[:]), reads=["f_hid%d" % q for q in range(22)], writes=["dbg3"])
                P.dma("sync", lambda e, x=x: e.dma_start(out=G.DBG["D_X1"][:, :, :], in_=x[:]), reads=[xn], writes=["dbg4"])
            P.dma("sync", lambda e, x=x, t0=t0: e.dma_start(out=G.XT[:, :, t0:t0 + TW], in_=x[:]), reads=[xn], writes=["XT%d" % ti])
            if n + 2 < len(tiles):
                ld(n + 2)
        P.barrier()
        P.emit()


def stage_final(G):
    P, I, nc = G.P, G.I, G.nc
    with contextlib.ExitStack() as st:
        sb, ps = _alloc(G, st)
        fg = sb("k_fg", [128, 1024], F32)
        xs = [sb("k_xs%d" % b, [128, 8, 128], F32) for b in range(2)]
        junk = sb("k_junk", [128, 1024], F32)
        yo = [sb("k_yo%d" % b, [128, 1024], F32) for b in range(2)]
        ssq = sb("k_ssq", [128, 2], F32)
        rs = sb("k_rs", [128, 2], F32)
        pt = [ps("k_pt%d" % b, [128, 8, 128]) for b in range(2)]
        P.dma("sync", lambda e: e.dma_start(out=fg[:], in_=I["final_g_bc"][:, :]), writes=["k_fg"])
        for i in range(NOWN // 128):
            t0 = OWN0 + i * 128
            b = i % 2
            P.dma("gpsimd", lambda e, b=b, t0=t0: e.dma_start(out=xs[b][:], in_=G.XT[:, :, t0:t0 + 128]),
                  reads=["XT%d" % (t0 // TW)], writes=["k_xs%d" % b])
            for k in range(8):
                P.op("tensor", lambda e, b=b, k=k: e.transpose(pt[b][:, k, :], xs[b][:, k, :], G.ident_f[:]),
                     reads=["k_xs%d" % b, "ident_f"], writes=["k_pt%d" % b], inc=(k == 7))
            ptf = pt[b][:].rearrange("p k n -> p (k n)")
            P.op("scalar", lambda e, b=b, ptf=ptf: e.activation(out=junk[:], in_=ptf, func=AF.Square, accum_out=ssq[:, b:b + 1]),
                 reads=["k_pt%d" % b], writes=["k_junk", "k_ssq%d" % b])
            P.op("scalar", lambda e, b=b: e.activation(out=rs[:, b:b + 1], in_=ssq[:, b:b + 1], func=AF.Sqrt, bias=G.c_eps[:], scale=1.0 / 1024.0),
                 reads=["k_ssq%d" % b, "c_eps"], writes=["k_rs%d" % b])
            P.op("vector", lambda e, b=b: e.reciprocal(rs[:, b:b + 1], rs[:, b:b + 1]), reads=["k_rs%d" % b], writes=["k_rs%d" % b])
            P.op("vector", lambda e, b=b, ptf=ptf: e.scalar_tensor_tensor(
                out=yo[b][:], in0=ptf, scalar=rs[:, b:b + 1], in1=fg[:], op0=ALU.mult, op1=ALU.mult),
                reads=["k_pt%d" % b, "k_rs%d" % b, "k_fg"], writes=["k_yo%d" % b])
            P.dma("sync", lambda e, b=b, i=i: e.dma_start(out=G.OUT[i * 128:(i + 1) * 128, :], in_=yo[b][:]),
                  reads=["k_yo%d" % b], writes=["OUT"])
        P.barrier()
        P.emit()


def stage_inproj(G):
    P, I, nc = G.P, G.I, G.nc
    l, j = 0, 1
    tiles = [(ti, 0) for ti in range(18)] + [(18, 1)]
    with contextlib.ExitStack() as st:
        sb, ps = _alloc(G, st)
        wfm = sb("i_wfm", [128, 8, 1024], BF16)
        wg = sb("i_wg", [128, 8, 128], BF16)
        wtm = sb("i_wtm", [128, 8, 2560], BF16)
        xt = [sb("i_xt%d" % b, [128, 8, TW], F32) for b in range(2)]
        hh = [sb("i_h%d" % b, [128, 8, TW], BF16) for b in range(2)]
        NS = NormScratch(G, sb, ps, "i_")
        fm = [sb("i_fm%d" % b, [128, 8, TW], BF16) for b in range(2)]
        gts = [sb("i_gt%d" % b, [128, TW], F32) for b in range(2)]
        cosT = [sb("i_cos%d" % b, [128, 512], F32) for b in range(2)]
        sinT = [sb("i_sin%d" % b, [128, 512], F32) for b in range(2)]
        hg = sb("i_hg", [128, 512], F32)
        vt = [sb("i_vt%d" % b, [128, 8, 65], BF16) for b in range(2)]
        mv = [sb("i_mv%d" % b, [128, 4, 129], BF16) for b in range(2)]
        xs = [sb("i_xs%d" % b, [128, 512], F32) for b in range(2)]
        r1 = [sb("i_r1%d" % b, [128, 512], F32) for b in range(2)]
        r2 = [sb("i_r2%d" % b, [128, 512], F32) for b in range(2)]
        qr = [sb("i_qr%d" % b, [128, 512], BF16) for b in range(2)]
        sg = sb("i_sg", [128, 512], F32)
        g2 = [sb("i_g2%d" % b, [128, 512], F32) for b in range(2)]
        tq = [sb("i_tq%d" % b, [128, 4, 128], BF16) for b in range(2)]
        pfm = [ps("i_pfm%d" % b, [128, TW]) for b in range(2)]
        ptm = [ps("i_ptm%d" % b, [128, 512]) for b in range(2)]
        ptr = [ps("i_ptr%d" % b, [128, 4, 128], BF16) for b in range(2)]
        load_w_cast(G, wfm, "i_wfm", I["mix_w_in"][:, 0:1024], 8, 1024)
        load_w_cast(G, wg, "i_wg", I["w_gate"], 8, 128)
        load_w_cast(G, wtm, "i_wtm", I["mix_w_in"][:, 1024:3584], 8, 2560)
        P.dma("sync", lambda e: e.dma_start(out=hg[:], in_=I["headg_bc"][:, :]), writes=["i_hg"])
        for b in range(2):
            P.op("vector", lambda e, b=b: e.memset(vt[b][:], 1.0), writes=["i_vt%d" % b])
            P.op("vector", lambda e, b=b: e.memset(mv[b][:], 1.0), writes=["i_mv%d" % b])

        def ld(n):
            ti_ = tiles[n][0]
            xb = xt[n % 2]
            P.dma("gpsimd", lambda e, xb=xb, ti_=ti_: e.dma_start(out=xb[:], in_=G.XT[:, :, ti_ * TW:(ti_ + 1) * TW]),
                  reads=["XT%d" % ti_], writes=["i_xt%d" % (n % 2)])
        ld(0)
        cnt = {"s": 0, "r": 0, "t": 0}
        deferred = []

        def rope_and_T(pt_, ptn, scale, dstT, tmaj_dst, ts0):
            a = cnt["r"] % 2
            cnt["r"] += 1
            sb_ = cnt["s"] % 2
            X, R1, R2, QR, TQ, PT = xs[a], r1[a], r2[a], qr[a], tq[a], ptr[a]
            xn_, r1n, r2n, qrn, tqn, ptn2 = "i_xs%d" % a, "i_r1%d" % a, "i_r2%d" % a, "i_qr%d" % a, "i_tq%d" % a, "i_ptr%d" % a
            P.op("scalar", lambda e: e.activation(out=X[:], in_=pt_[:], func=AF.Copy, scale=scale), reads=[ptn], writes=[xn_])
            P.op("vector", lambda e: e.tensor_tensor(out=R1[:], in0=X[:], in1=cosT[sb_][:], op=ALU.mult), reads=[xn_, "i_cos%d" % sb_], writes=[r1n])
            Xv = X[:].rearrange("p (i t) -> p i t", t=2)
            Sv = sinT[sb_][:].rearrange("p (i t) -> p i t", t=2)
            Rv = R2[:].rearrange("p (i t) -> p i t", t=2)
            P.op("vector", lambda e: e.tensor_tensor(out=Rv[:, :, 0], in0=Xv[:, :, 1], in1=Sv[:, :, 0], op=ALU.mult),
                 reads=[xn_, "i_sin%d" % sb_], writes=[r2n])
            P.op("vector", lambda e: e.tensor_tensor(out=Rv[:, :, 1], in0=Xv[:, :, 0], in1=Sv[:, :, 1], op=ALU.mult),
                 reads=[xn_, "i_sin%d" % sb_], writes=[r2n])
            P.op("vector", lambda e: e.tensor_tensor(out=QR[:], in0=R1[:], in1=R2[:], op=ALU.add), reads=[r1n, r2n], writes=[qrn])
            if tmaj_dst is not None:
                P.dma("sync", lambda e: e.dma_start(out=tmaj_dst[ts0:ts0 + 128, :], in_=QR[:]), reads=[qrn], writes=[_u()])
            def later():
                for hd in range(4):
                    P.op("tensor", lambda e, hd=hd: e.transpose(PT[:, hd, :], QR[:, hd * 128:(hd + 1) * 128], G.ident_b[:]),
                         reads=[qrn, "ident_b"], writes=[ptn2], inc=(hd == 3))
                P.op("scalar", lambda e: e.activation(out=TQ[:], in_=PT[:], func=AF.Copy), reads=[ptn2], writes=[tqn])
                P.dma("sync", lambda e: e.dma_start(out=dstT[:, :, ts0:ts0 + 128], in_=TQ[:]), reads=[tqn], writes=[_u()])
            deferred.append(later)

        for n, (ti, m) in enumerate(tiles):
            t0 = ti * TW
            b = n % 2
            x, xn = xt[b], "i_xt%d" % b
            h, hn = hh[b], "i_h%d" % b
            if n + 1 < len(tiles):
                ld(n + 1)
            if n == 0:
                norm_mod(G, NS, x, xn, G.GM[:, l, j, m, :], G.SH[:, l, j, m, :], h, hn)
            FM, fmn = fm[b], "i_fm%d" % b
            for fc in range(8):
                q = fc % 2
                for k in range(8):
                    P.op("tensor", lambda e, q=q, fc=fc, k=k, h=h: e.matmul(
                        pfm[q][:], lhsT=wfm[:, k, fc * 128:(fc + 1) * 128], rhs=h[:, k, :], start=(k == 0), stop=(k == 7)),
                        reads=["i_wfm", hn], writes=["i_pfm%d" % q], inc=(k == 7))
                P.op("scalar", lambda e, q=q, fc=fc, FM=FM: e.activation(out=FM[:, fc, :], in_=pfm[q][:], func=AF.Copy),
                     reads=["i_pfm%d" % q], writes=[fmn])
            P.dma("sync", lambda e, FM=FM, t0=t0: e.dma_start(out=G.NAQT[:, :, t0:t0 + TW], in_=FM[:, 0:4, :]), reads=[fmn], writes=[_u()])
            P.dma("sync", lambda e, FM=FM, t0=t0: e.dma_start(out=G.NAKT[:, :, t0:t0 + TW], in_=FM[:, 4:8, :]), reads=[fmn], writes=[_u()])
            GTS, gtn = gts[b], "i_gt%d" % b
            for k in range(8):
                P.op("tensor", lambda e, k=k, h=h: e.matmul(pfm[0][:], lhsT=wg[:, k, :], rhs=h[:, k, :], start=(k == 0), stop=(k == 7)),
                     reads=["i_wg", hn], writes=["i_pfm0"], inc=(k == 7))
            P.op("vector", lambda e, GTS=GTS: e.tensor_copy(GTS[:], pfm[0][:]), reads=["i_pfm0"], writes=[gtn])
            P.dma("sync", lambda e, GTS=GTS, t0=t0: e.dma_start(out=G.GIF[:, t0:t0 + TW], in_=GTS[:]), reads=[gtn], writes=[_u()])
            if n + 1 < len(tiles):
                ti2, m2 = tiles[n + 1]
                b2 = (n + 1) % 2
                norm_mod(G, NS, xt[b2], "i_xt%d" % b2, G.GM[:, l, j, m2, :], G.SH[:, l, j, m2, :], hh[b2], "i_h%d" % b2)
            for s_ in range(TW // 128):
                ts0 = t0 + s_ * 128
                sbi = cnt["s"] % 2
                P.dma("gpsimd", lambda e, sbi=sbi, ts0=ts0: e.dma_start(out=cosT[sbi][:], in_=I["ropecos"][ts0:ts0 + 128, :]), writes=["i_cos%d" % sbi])
                P.dma("gpsimd", lambda e, sbi=sbi, ts0=ts0: e.dma_start(out=sinT[sbi][:], in_=I["ropesin"][ts0:ts0 + 128, :]), writes=["i_sin%d" % sbi])
                for blk in range(5):
                    a = cnt["t"] % 2
                    cnt["t"] += 1
                    PT_, ptn = ptm[a], "i_ptm%d" % a
                    for k in range(8):
                        P.op("tensor", lambda e, PT_=PT_, k=k, h=h, s_=s_, blk=blk: e.matmul(
                            PT_[:], lhsT=h[:, k, s_ * 128:(s_ + 1) * 128], rhs=wtm[:, k, blk * 512:(blk + 1) * 512], start=(k == 0), stop=(k == 7)),
                            reads=["i_wtm", hn], writes=[ptn], inc=(k == 7))
                    while len(deferred) > (1 if blk == 3 else 0):
                        deferred.pop(0)()
                    if blk == 0:
                        VT = vt[sbi]
                        P.op("scalar", lambda e, VT=VT, PT_=PT_: e.activation(out=VT[:, :, 0:64], in_=PT_[:].rearrange("p (h d) -> p h d", d=64), func=AF.Copy),
                             reads=[ptn], writes=["i_vt%d" % sbi])
                        P.dma("sync", lambda e, VT=VT, ts0=ts0: e.dma_start(out=G.NAV[ts0:ts0 + 128, :], in_=VT[:].rearrange("p h d -> p (h d)")),
                              reads=["i_vt%d" % sbi], writes=[_u()])
                    elif blk == 1:
                        rope_and_T(PT_, ptn, 1.0, G.MLQT, None, ts0)
                    elif blk == 2:
                        rope_and_T(PT_, ptn, 128.0 ** -0.5, G.MLKT, G.MLK, ts0)
                    elif blk == 3:
                        MV = mv[sbi]
                        P.op("scalar", lambda e, MV=MV, PT_=PT_: e.activation(out=MV[:, :, 0:128], in_=PT_[:].rearrange("p (h d) -> p h d", d=128), func=AF.Copy),
                             reads=[ptn], writes=["i_mv%d" % sbi])
                        P.dma("sync", lambda e, MV=MV, ts0=ts0: e.dma_start(out=G.MLV[ts0:ts0 + 128, :], in_=MV[:].rearrange("p h d -> p (h d)")),
                              reads=["i_mv%d" % sbi], writes=[_u()])
                    else:
                        G2 = g2[sbi]
                        P.op("scalar", lambda e, PT_=PT_: e.activation(out=sg[:], in_=PT_[:], func=AF.Sigmoid), reads=[ptn], writes=["i_sg"])
                        P.op("vector", lambda e, G2=G2: e.tensor_tensor(out=G2[:], in0=sg[:], in1=hg[:], op=ALU.mult), reads=["i_sg", "i_hg"], writes=["i_g2%d" % sbi])
                        P.dma("sync", lambda e, G2=G2, ts0=ts0: e.dma_start(out=G.MLG2[ts0:ts0 + 128, :], in_=G2[:]), reads=["i_g2%d" % sbi], writes=[_u()])
                while deferred:
                    deferred.pop(0)()
                cnt["s"] += 1
        P.barrier()
        P.emit()


def stage_na(G):
    P, I, nc = G.P, G.I, G.nc
    with contextlib.ExitStack() as st:
        sb, ps = _alloc(G, st)
        KT = sb("n_KT", [128, 4, NU], BF16)
        V = sb("n_V", [128, 38, 520], BF16)
        QT = sb("n_QT", [128, 4, NOWN], BF16)
        BI = sb("n_BI", [128, 5, 8, 5, 128], BF16)
        sAb = [sb("n_sAb%d" % b, [128, 4, 128], F32) for b in range(2)]
        sBb = [sb("n_sBb%d" % b, [128, 128], F32) for b in range(2)]
        pt = [sb("n_pt%d" % b, [128, 7, 128], BF16) for b in range(2)]
        ya = [sb("n_ya%d" % b, [128, 512], BF16) for b in range(2)]
        rec = [sb("n_rec%d" % b, [128, 8], F32) for b in range(2)]
        yt = [sb("n_yt%d" % b, [128, 4, 128], BF16) for b in range(2)]
        sA = [ps("n_sA%d" % b, [128, 4, 128]) for b in range(2)]
        sB = [ps("n_sB%d" % b, [128, 4, 128]) for b in range(2)]
        O = ps("n_O", [128, 8, 128])
        ptr = ps("n_ptr", [128, 4, 128], BF16)
        P.dma("sync", lambda e: e.dma_start(out=KT[:], in_=G.NAKT[:, :, :]), reads=["dramNA"], writes=["n_KT"])
        P.dma("sync", lambda e: e.dma_start(out=V[:], in_=G.NAV.rearrange("(c p) n -> p c n", p=128)), reads=["dramNA"], writes=["n_V"])
        P.dma("sync", lambda e: e.dma_start(out=QT[:], in_=G.NAQT[:, :, OWN0:OWN0 + NOWN]), reads=["dramNA"], writes=["n_QT"])
        for c5 in range(5):
            P.dma("gpsimd", lambda e, c5=c5: e.dma_start(out=BI[:, c5], in_=I["nabiasT"][c5].rearrange("h k j q -> k h j q")), writes=["n_BI"])
        hb = 0
        for p in range(32):
            cls = 0 if p == 0 else 1 if p == 1 else 3 if p == 30 else 4 if p == 31 else 2
            pb2 = p % 2
            for h in range(8):
                hc, b0 = h // 2, (h % 2) * 64
                a = hb % 2
                hb += 1
                q_ap = QT[b0:b0 + 64, hc, p * 128:(p + 1) * 128]
                SA, SB, PT = sA[a], sB[a], pt[a]
                san, sbn, ptn = "n_sA%d" % a, "n_sB%d" % a, "n_pt%d" % a
                for jj in range(4):
                    k0 = (p + jj) * 128
                    P.op("tensor", lambda e, SA=SA, jj=jj, k0=k0, q_ap=q_ap, hc=hc, b0=b0: e.matmul(
                        SA[:, jj, :], lhsT=KT[b0:b0 + 64, hc, k0:k0 + 128], rhs=q_ap, start=True, stop=True),
                        reads=["n_KT", "n_QT"], writes=[san], inc=(jj == 3))
                k0 = (p + 4) * 128
                P.op("tensor", lambda e, SB=SB, k0=k0, q_ap=q_ap, hc=hc, b0=b0: e.matmul(
                    SB[0:64, 0, :], lhsT=KT[b0:b0 + 64, hc, k0:k0 + 64], rhs=q_ap, start=True, stop=True),
                    reads=["n_KT", "n_QT"], writes=[sbn], inc=False)
                for c in range(2):
                    k0 = CTX0 + c * 128
                    P.op("tensor", lambda e, SB=SB, c=c, k0=k0, q_ap=q_ap, hc=hc, b0=b0: e.matmul(
                        SB[:, 1 + c, :], lhsT=KT[b0:b0 + 64, hc, k0:k0 + 128], rhs=q_ap, start=True, stop=True),
                        reads=["n_KT", "n_QT"], writes=[sbn], inc=(c == 1))
                AB, BB = sAb[a], sBb[a]
                abn, bbn = "n_sAb%d" % a, "n_sBb%d" % a
                P.op("vector", lambda e, SA=SA, AB=AB, cls=cls, h=h: e.scalar_tensor_tensor(
                    out=AB[:], in0=SA[:], scalar=0.125, in1=BI[:, cls, h, 0:4, :], op0=ALU.mult, op1=ALU.add),
                    reads=[san, "n_BI"], writes=[abn])
                P.op("vector", lambda e, SB=SB, BB=BB, cls=cls, h=h: e.scalar_tensor_tensor(
                    out=BB[0:64, :], in0=SB[0:64, 0, :], scalar=0.125, in1=BI[0:64, cls, h, 4, :], op0=ALU.mult, op1=ALU.add),
                    reads=[sbn, "n_BI"], writes=[bbn])
                P.op("scalar", lambda e, AB=AB, PT=PT: e.activation(out=PT[:, 0:4, :], in_=AB[:], func=AF.Exp), reads=[abn], writes=[ptn])
                P.op("scalar", lambda e, SB=SB, PT=PT: e.activation(out=PT[:, 5:7, :], in_=SB[:, 1:3, :], func=AF.Exp, scale=0.125), reads=[sbn, bbn], writes=[ptn])
                P.op("scalar", lambda e, BB=BB, PT=PT: e.activation(out=PT[0:64, 4, :], in_=BB[0:64, :], func=AF.Exp), reads=[bbn], writes=[ptn])
                specs = [(jj, 128, p + jj) for jj in range(4)] + [(4, 64, p + 4), (5, 128, 36), (6, 128, 37)]
                for si, (slot, nk, vc) in enumerate(specs):
                    P.op("tensor", lambda e, PT=PT, slot=slot, nk=nk, vc=vc, h=h, si=si: e.matmul(
                        O[:, h, 0:65], lhsT=PT[0:nk, slot, :], rhs=V[0:nk, vc, h * 65:(h + 1) * 65], start=(si == 0), stop=(si == 6)),
                        reads=[ptn, "n_V"], writes=["n_O%d" % (h // 4)], inc=(si == 6))
                if h in (3, 7):
                    hf_ = h // 4
                    R, YA = rec[pb2], ya[pb2]
                    rn, yan = "n_rec%d_%d" % (pb2, hf_), "n_ya%d" % pb2
                    P.op("vector", lambda e, R=R, hf_=hf_: e.reciprocal(R[:, hf_ * 4:hf_ * 4 + 4], O[:, hf_ * 4:hf_ * 4 + 4, 64]), reads=["n_O%d" % hf_], writes=[rn])
                    for h2 in range(hf_ * 4, hf_ * 4 + 4):
                        P.op("scalar", lambda e, R=R, YA=YA, h2=h2: e.activation(out=YA[:, h2 * 64:(h2 + 1) * 64], in_=O[:, h2, 0:64], func=AF.Copy, scale=R[:, h2:h2 + 1]),
                             reads=["n_O%d" % hf_, rn], writes=[yan])
            YA, YT_ = ya[pb2], yt[pb2]
            yan, ytn = "n_ya%d" % pb2, "n_yt%d" % pb2
            for c in range(4):
                P.op("tensor", lambda e, YA=YA, c=c: e.transpose(ptr[:, c, :], YA[:, c * 128:(c + 1) * 128], G.ident_b[:]),
                     reads=[yan, "ident_b"], writes=["n_ptr"], inc=(c == 3))
            P.op("vector", lambda e, YT_=YT_: e.tensor_copy(YT_[:], ptr[:]), reads=["n_ptr"], writes=[ytn])
            P.dma("sync", lambda e, YT_=YT_, p=p: e.dma_start(out=G.YT[:, 0:4, p * 128:(p + 1) * 128], in_=YT_[:]), reads=[ytn], writes=[_u()])
        P.barrier()
        P.emit()


def stage_ml(G):
    P, I, nc = G.P, G.I, G.nc
    NCH = 32
    with contextlib.ExitStack() as st:
        sb, ps = _alloc(G, st)
        TOK = sb("l_TOK", [128, 34, 5, 8], F32)
        EBEND = sb("l_EBEND", [128, 8, 32], F32)
        ATOT = sb("l_ATOT", [128, 8], F32)
        with contextlib.ExitStack() as st1:
            sb1, ps1 = _alloc(G, st1)
            LI = sb1("l_LI", [64, NU], F32)
            SP = sb1("l_SP", [64, NU], F32)
            CL = sb1("l_CL", [64, NU], F32)
            CG = sb1("l_CG", [64, NU], F32)
            TM = sb1("l_TM", [64, NU], F32)
            OQ = [sb1("l_OQ%d" % b, [64, NU], F32) for b in range(2)]
            gbI = sb1("l_gbI", [64, 1], F32)
            gbF = sb1("l_gbF", [64, 1], F32)
            CE = sb1("l_CE", [64, 32], F32)
            ntot = sb1("l_ntot", [64, 2], F32)
            tot = sb1("l_tot", [64, 2], F32)
            SELM = sb1("l_SELM", [64, 8, 128], F32)
            ptr = [ps1("l_ptr%d" % b, [128, 8, 64]) for b in range(2)]
            pe = ps1("l_pe", [128, 8, 32])
            pa = ps1("l_pa", [128, 8, 2])
            own = slice(OWN0, OWN0 + NOWN)
            cxs = slice(CTX0, CTX0 + 256)
            P.dma("sync", lambda e: e.dma_start(out=LI[:], in_=G.GIF[0:64, :]), reads=["dramML"], writes=["l_LI"])
            P.dma("sync", lambda e: e.dma_start(out=SP[:], in_=G.GIF[64:128, :]), reads=["dramML"], writes=["l_SP"])
            P.dma("sync", lambda e: e.dma_start(out=gbI[:], in_=I["gate_b"][0:64, :]), writes=["l_gbI"])
            P.dma("sync", lambda e: e.dma_start(out=gbF[:], in_=I["gate_b"][64:128, :]), writes=["l_gbF"])
            P.dma("sync", lambda e: e.dma_start(out=SELM[:], in_=I["selm"][:, :, :]), writes=["l_SELM"])
            P.op("vector", lambda e: e.tensor_scalar(gbF[:], gbF[:], -1.0, None, ALU.mult), reads=["l_gbF"], writes=["l_gbF"])
            P.op("scalar", lambda e: e.activation(out=LI[:], in_=LI[:], func=AF.Identity, bias=gbI[:]), reads=["l_LI", "l_gbI"], writes=["l_LI"])
            P.op("scalar", lambda e: e.activation(out=SP[:], in_=SP[:], func=AF.Exp, bias=gbF[:], scale=-1.0), reads=["l_SP", "l_gbF"], writes=["l_SP"])
            P.op("scalar", lambda e: e.activation(out=SP[:], in_=SP[:], func=AF.Ln, bias=G.c_one[0:64, :]), reads=["l_SP", "c_one"], writes=["l_SP"])
            P.op("vector", lambda e: e.memset(TM[:], 1.0), writes=["l_TM"])
            P.op("vector", lambda e: e.tensor_tensor_scan(out=CG[:, own], data0=TM[:, own], data1=SP[:, own], initial=0.0, op0=ALU.mult, op1=ALU.add),
                 reads=["l_TM", "l_SP"], writes=["l_CG"])
            P.op("vector", lambda e: e.tensor_tensor_scan(out=CG[:, cxs], data0=TM[:, cxs], data1=SP[:, cxs], initial=0.0, op0=ALU.mult, op1=ALU.add),
                 reads=["l_TM", "l_SP"], writes=["l_CG"])
            TMo = TM[:, own].rearrange("p (c t) -> p c t", t=128)
            P.op("vector", lambda e: e.memset(TMo[:, :, 0:1], 0.0), reads=["l_CG"], writes=["l_TM"])
            P.op("vector", lambda e: e.tensor_tensor_scan(out=CL[:, own], data0=TM[:, own], data1=SP[:, own], initial=0.0, op0=ALU.mult, op1=ALU.add),
                 reads=["l_TM", "l_SP"], writes=["l_CL"])
            CLo = CL[:, own].rearrange("p (c t) -> p c t", t=128)
            SPo = SP[:, own].rearrange("p (c t) -> p c t", t=128)
            P.op("vector", lambda e: e.tensor_copy(CE[:], CLo[:, :, 127]), reads=["l_CL"], writes=["l_CE"])
            P.op("vector", lambda e: e.tensor_copy(tot[:, 0:1], CG[:, OWN0 + NOWN - 1:OWN0 + NOWN]), reads=["l_CG"], writes=["l_tot"])
            P.op("vector", lambda e: e.tensor_copy(tot[:, 1:2], CG[:, CTX0 + 255:CTX0 + 256]), reads=["l_CG"], writes=["l_tot"])
            P.op("vector", lambda e: e.tensor_scalar(ntot[:], tot[:], -1.0, None, ALU.mult), reads=["l_tot"], writes=["l_ntot"])
            for c in range(NCH):
                P.op("vector", lambda e, c=c: e.tensor_scalar(CLo[32:64, c, :], CLo[32:64, c, :], CE[32:64, c:c + 1], -1.0, ALU.subtract, ALU.mult),
                     reads=["l_CL", "l_CE"], writes=["l_CL"])
            P.op("vector", lambda e: e.tensor_tensor(out=CL[32:64, own], in0=CL[32:64, own], in1=SP[32:64, own], op=ALU.add),
                 reads=["l_CL", "l_SP"], writes=["l_CL"])
            for r in range(8):
                P.op("tensor", lambda e, r=r: e.matmul(pe[:, r, :], lhsT=SELM[:, r, :], rhs=CE[:], start=True, stop=True),
                     reads=["l_SELM", "l_CE"], writes=["l_pe"], inc=(r == 7))
            P.op("scalar", lambda e: e.activation(out=EBEND[:], in_=pe[:], func=AF.Exp, scale=-1.0), reads=["l_pe"], writes=["l_EBEND"])
            for r in range(8):
                P.op("tensor", lambda e, r=r: e.matmul(pa[:, r, :], lhsT=SELM[:, r, :], rhs=tot[:], start=True, stop=True),
                     reads=["l_SELM", "l_tot"], writes=["l_pa"], inc=(r == 7))
            P.op("scalar", lambda e: e.activation(out=ATOT[:], in_=pa[:, :, 0], func=AF.Exp, scale=-1.0), reads=["l_pa"], writes=["l_ATOT"])

            tcnt = {"n": 0}

            def transpose_out(Q, qn, qty, chunks):
                for g0 in range(0, len(chunks), 8):
                    grp = chunks[g0:g0 + 8]
                    a = tcnt["n"] % 2
                    tcnt["n"] += 1
                    for gi, (ci, col0) in enumerate(grp):
                        P.op("tensor", lambda e, a=a, gi=gi, col0=col0: e.transpose(ptr[a][:, gi, :], Q[0:64, col0:col0 + 128], G.ident_f[0:64, 0:64]),
                             reads=[qn, "ident_f"], writes=["l_ptr%d" % a], inc=(gi == len(grp) - 1))
                    c_first = grp[0][0]
                    ng_ = len(grp)
                    src = ptr[a][:, 0:ng_, :].rearrange("p g (d x) -> p g d x", d=2)[:, :, :, 0:4]
                    dst = TOK[:, c_first:c_first + ng_, qty, :].rearrange("p g (d x) -> p g d x", d=2)
                    P.op("vector", lambda e, src=src, dst=dst: e.tensor_copy(dst, src), reads=["l_ptr%d" % a], writes=["l_TOK"])

            own_chunks = [(c, OWN0 + c * 128) for c in range(NCH)]
            ctx_chunks = [(32 + c, CTX0 + c * 128) for c in range(2)]
            P.op("scalar", lambda e: e.activation(out=OQ[0][:, own], in_=CL[:, own], func=AF.Exp, scale=-1.0), reads=["l_CL"], writes=["l_OQ0"])
            transpose_out(OQ[0], "l_OQ0", 0, own_chunks)
            P.op("scalar", lambda e: e.activation(out=OQ[1][:, own], in_=CL[:, own], func=AF.Exp), reads=["l_CL"], writes=["l_OQ1"])
            transpose_out(OQ[1], "l_OQ1", 4, own_chunks)
            P.op("vector", lambda e: e.tensor_tensor(out=TM[:, own], in0=LI[:, own], in1=CL[:, own], op=ALU.add), reads=["l_LI", "l_CL"], writes=["l_TM"])
            P.op("scalar", lambda e: e.activation(out=OQ[1][:, own], in_=TM[:, own], func=AF.Exp), reads=["l_TM"], writes=["l_OQ1"])
            transpose_out(OQ[1], "l_OQ1", 1, own_chunks)
            TMo2 = TM[:, own].rearrange("p (c t) -> p c t", t=128)
            for c in range(NCH):
                P.op("vector", lambda e, c=c: e.tensor_scalar(TMo2[:, c, :], TMo2[:, c, :], CE[:, c:c + 1], None, ALU.subtract),
                     reads=["l_TM", "l_CE", "l_OQ1"], writes=["l_TM"])
            P.op("scalar", lambda e: e.activation(out=OQ[0][:, own], in_=TM[:, own], func=AF.Exp), reads=["l_TM"], writes=["l_OQ0"])
            transpose_out(OQ[0], "l_OQ0", 2, own_chunks)
            for (sl, ti_) in ((own, 0), (cxs, 1)):
                P.op("vector", lambda e, sl=sl: e.tensor_tensor(out=TM[0:32, sl], in0=LI[0:32, sl], in1=CG[0:32, sl], op=ALU.add),
                     reads=["l_LI", "l_CG"], writes=["l_TM"])
                P.op("vector", lambda e, sl=sl: e.tensor_tensor(out=TM[32:64, sl], in0=LI[32:64, sl], in1=CG[32:64, sl], op=ALU.subtract),
                     reads=["l_LI", "l_CG"], writes=["l_TM"])
                P.op("vector", lambda e, sl=sl: e.tensor_tensor(out=TM[32:64, sl], in0=TM[32:64, sl], in1=SP[32:64, sl], op=ALU.add),
                     reads=["l_TM", "l_SP"], writes=["l_TM"])
                P.op("scalar", lambda e, sl=sl, ti_=ti_: e.activation(out=OQ[1][0:32, sl], in_=TM[0:32, sl], func=AF.Exp, bias=ntot[0:32, ti_:ti_ + 1]),
                     reads=["l_TM", "l_ntot"], writes=["l_OQ1"])
                P.op("scalar", lambda e, sl=sl: e.activation(out=OQ[1][32:64, sl], in_=TM[32:64, sl], func=AF.Exp), reads=["l_TM"], writes=["l_OQ1"])
            transpose_out(OQ[1], "l_OQ1", 3, own_chunks + ctx_chunks)
            P.barrier()
            P.emit()

        KTOK = sb("l_KTOK", [128, 34, 512], BF16)
        VTOK = sb("l_VTOK", [128, 34, 516], BF16)
        TRI = sb("l_TRI", [128, 2, 128], F32)
        SELV = sb("l_SELV", [128, 16], F32)
        PAY = sb("l_PAY", [128, 8, 130], F32)
        GATH = sb("l_GATH", [128, 4, 1040], F32)
        STATE = sb("l_STATE", [128, 8, 129], F32)
        STB = sb("l_STB", [128, 8, 129], BF16)
        KA = [sb("l_KA%d" % b, [128, 128], BF16) for b in range(3)]
        alpha = sb("l_alpha", [128, 1], F32)
        tmpL = sb("l_tmpL", [128, 129], F32)
        QTc = [[sb("l_QTc%d%d" % (d_, b), [128, 4, 128], BF16) for b in range(2)] for d_ in range(2)]
        KTc = [[sb("l_KTc%d%d" % (d_, b), [128, 4, 128], BF16) for b in range(2)] for d_ in range(2)]
        HS = [[sb("l_HS%d%d" % (d_, b), [128, 512], F32) for b in range(2)] for d_ in range(2)]
        PTt = [sb("l_PT%d" % b, [128, 128], BF16) for b in range(2)]
        pS = [ps("l_pS%d" % b, [128, 128]) for b in range(2)]
        pU = [ps("l_pU%d" % b, [128, 132]) for b in range(4)]
        pN = [ps("l_pN%d" % b, [128, 132]) for b in range(2)]
        pL = [pU[0], pU[1]]
        den = [sb("l_den%d" % b, [128, 8], F32) for b in range(2)]
        P.dma("sync", lambda e: e.dma_start(out=KTOK[:, 0:32, :], in_=G.MLK[OWN0:OWN0 + NOWN, :].rearrange("(c p) n -> p c n", p=128)), reads=["dramML"], writes=["l_KTOK"])
        P.dma("sync", lambda e: e.dma_start(out=KTOK[:, 32:34, :], in_=G.MLK[CTX0:CTX0 + 256, :].rearrange("(c p) n -> p c n", p=128)), reads=["dramML"], writes=["l_KTOK"])
        P.dma("sync", lambda e: e.dma_start(out=VTOK[:, 0:32, :], in_=G.MLV[OWN0:OWN0 + NOWN, :].rearrange("(c p) n -> p c n", p=128)), reads=["dramML"], writes=["l_VTOK"])
        P.dma("sync", lambda e: e.dma_start(out=VTOK[:, 32:34, :], in_=G.MLV[CTX0:CTX0 + 256, :].rearrange("(c p) n -> p c n", p=128)), reads=["dramML"], writes=["l_VTOK"])
        P.dma("sync", lambda e: e.dma_start(out=TRI[:], in_=I["tri"][:, :, :]), writes=["l_TRI"])
        P.dma("sync", lambda e: e.dma_start(out=SELV[:], in_=I["selv"][:, :]), writes=["l_SELV"])
        kacnt = {"n": 0}

        def scaled_k(c, h, qty, r):
            a = kacnt["n"] % 3
            kacnt["n"] += 1
            eng = ("vector", "scalar", "scalar")[a]
            src = KTOK[:, c, h * 128:(h + 1) * 128]
            sc = TOK[:, c, qty, r:r + 1]
            if eng == "scalar":
                P.op("scalar", lambda e: e.activation(out=KA[a][:], in_=src, func=AF.Copy, scale=sc), reads=["l_KTOK", "l_TOK"], writes=["l_KA%d" % a])
            else:
                P.op(eng, lambda e: e.tensor_scalar(KA[a][:], src, sc, None, ALU.mult), reads=["l_KTOK", "l_TOK"], writes=["l_KA%d" % a])
            return KA[a], "l_KA%d" % a

        n2 = 0
        for r in range(8):
            h = r % 4
            for (chs, dstname) in ((list(range(32)), "own"), ([32, 33], "ctx")):
                pp, ppn = pL[n2 % 2], "l_pU%d" % (n2 % 2)
                n2 += 1
                for i_, c in enumerate(chs):
                    ka, kan = scaled_k(c, h, 3, r)
                    P.op("tensor", lambda e, pp=pp, ka=ka, c=c, h=h, i_=i_, L=len(chs): e.matmul(
                        pp[:, 0:129], lhsT=ka[:], rhs=VTOK[:, c, h * 129:(h + 1) * 129], start=(i_ == 0), stop=(i_ == L - 1)),
                        reads=[kan, "l_VTOK"], writes=[ppn], inc=True)
                if dstname == "own":
                    P.op("vector", lambda e, pp=pp, r=r: e.tensor_copy(PAY[:, r, 0:129], pp[:, 0:129]), reads=[ppn], writes=["l_PAY"])
                else:
                    P.op("vector", lambda e, pp=pp, r=r: e.tensor_copy(STATE[:, r, :], pp[:, 0:129]), reads=[ppn], writes=["l_STATE"])
        P.op("vector", lambda e: e.tensor_copy(PAY[:, :, 129], ATOT[:]), reads=["l_ATOT", "l_PAY"], writes=["l_PAY"])
        P.dma("sync", lambda e: e.dma_start(out=G.CCI[:, :], in_=PAY[:].rearrange("p r n -> p (r n)")), reads=["l_PAY"], writes=["CCI"])
        for _rep in range(3):
            P.cc(lambda e: e.collective_compute("AllGather", ALU.bypass, replica_groups=[[0, 1, 2, 3], [4, 5, 6, 7]],
                                                ins=[G.CCI.opt()], outs=[G.CCO.opt()]), reads=["CCI"], writes=["CCO"])
        P.dma("sync", lambda e: e.dma_start(out=GATH[:], in_=G.CCO.rearrange("(j p) n -> p j n", p=128)), reads=["CCO"], writes=["l_GATH"])
        for r in range(8):
            d = r // 4
            order = range(4) if d == 0 else range(3, -1, -1)
            for jseg in order:
                so = 0 if d == 0 else 8
                A_j = GATH[:, jseg, r * 130 + 129:r * 130 + 130]
                L_j = GATH[:, jseg, r * 130:r * 130 + 129]
                P.op("vector", lambda e, A_j=A_j, so=so, jseg=jseg: e.tensor_scalar(
                    alpha[:], A_j, SELV[:, so + jseg:so + jseg + 1], SELV[:, so + 4 + jseg:so + 5 + jseg], ALU.mult, ALU.add),
                    reads=["l_GATH", "l_SELV"], writes=["l_alpha"])
                P.op("vector", lambda e, L_j=L_j, so=so, jseg=jseg: e.tensor_scalar(tmpL[:], L_j, SELV[:, so + jseg:so + jseg + 1], None, ALU.mult),
                     reads=["l_GATH", "l_SELV"], writes=["l_tmpL"])
                P.op("vector", lambda e, r=r: e.scalar_tensor_tensor(out=STATE[:, r, :], in0=STATE[:, r, :], scalar=alpha[:], in1=tmpL[:], op0=ALU.mult, op1=ALU.add),
                     reads=["l_STATE", "l_alpha", "l_tmpL"], writes=["l_STATE"])
        P.op("scalar", lambda e: e.activation(out=STB[:], in_=STATE[:], func=AF.Copy), reads=["l_STATE"], writes=["l_STB"])

        def ld4(i):
            if i >= NCH:
                return
            bb = i % 2
            for d_ in range(2):
                c_ = i if d_ == 0 else NCH - 1 - i
                tk0 = OWN0 + c_ * 128
                P.dma("sync", lambda e, bb=bb, d_=d_, tk0=tk0: e.dma_start(out=QTc[d_][bb][:], in_=G.MLQT[:, :, tk0:tk0 + 128]), writes=["l_QTc%d%d" % (d_, bb)])
                P.dma("sync", lambda e, bb=bb, d_=d_, tk0=tk0: e.dma_start(out=KTc[d_][bb][:], in_=G.MLKT[:, :, tk0:tk0 + 128]), writes=["l_KTc%d%d" % (d_, bb)])
        ld4(0)
        for i in range(NCH):
            b = i % 2
            ld4(i + 1)
            cs = (i, NCH - 1 - i)
            for hh_ in range(2):
                items = [(h, d) for h in (2 * hh_, 2 * hh_ + 1) for d in range(2)]
                for (h, d) in items:
                    c = cs[d]
                    r = d * 4 + h
                    u = (h % 2) * 2 + d
                    qn, kn = "l_QTc%d%d" % (d, b), "l_KTc%d%d" % (d, b)
                    P.op("tensor", lambda e, b=b, h=h, d=d: e.matmul(pS[d][:], lhsT=KTc[d][b][:, h, :], rhs=QTc[d][b][:, h, :], start=True, stop=True),
                         reads=[kn, qn], writes=["l_pS%d" % d], inc=True)
                    P.op("vector", lambda e, c=c, r=r, d=d: e.scalar_tensor_tensor(
                        out=PTt[d][:], in0=pS[d][:], scalar=TOK[:, c, 1, r:r + 1], in1=TRI[:, d, :], op0=ALU.mult, op1=ALU.mult),
                        reads=["l_pS%d" % d, "l_TOK", "l_TRI"], writes=["l_PT%d" % d])
                    P.op("tensor", lambda e, c=c, h=h, d=d, u=u: e.matmul(pU[u][:, 0:129], lhsT=PTt[d][:], rhs=VTOK[:, c, h * 129:(h + 1) * 129], start=True, stop=False),
                         reads=["l_PT%d" % d, "l_VTOK"], writes=["l_pU%d" % u], inc=False)
                    P.op("tensor", lambda e, b=b, h=h, d=d, r=r, u=u: e.matmul(pU[u][:, 0:129], lhsT=QTc[d][b][:, h, :], rhs=STB[:, r, :], start=False, stop=True),
                         reads=[qn, "l_STB%d" % r, "l_STB"], writes=["l_pU%d" % u], inc=True)
                for (h, d) in items:
                    c = cs[d]
                    r = d * 4 + h
                    a = r % 2
                    ka, kan = scaled_k(c, h, 2, r)
                    P.op("tensor", lambda e, a=a, ka=ka, c=c, h=h: e.matmul(pN[a][:, 0:129], lhsT=ka[:], rhs=VTOK[:, c, h * 129:(h + 1) * 129], start=True, stop=True),
                         reads=[kan, "l_VTOK"], writes=["l_pN%d" % a], inc=True)
                    P.op("vector", lambda e, a=a, r=r, c=c: e.scalar_tensor_tensor(
                        out=STATE[:, r, :], in0=STATE[:, r, :], scalar=EBEND[:, r, c:c + 1], in1=pN[a][:, 0:129], op0=ALU.mult, op1=ALU.add),
                        reads=["l_STATE%d" % r, "l_EBEND", "l_pN%d" % a, "l_STATE"], writes=["l_STATE%d" % r])
                    P.op("scalar", lambda e, r=r: e.activation(out=STB[:, r, :], in_=STATE[:, r, :], func=AF.Copy),
                         reads=["l_STATE%d" % r], writes=["l_STB%d" % r])
                for (h, d) in items:
                    u = (h % 2) * 2 + d
                    dn, dnn = den[d], "l_den%d" % d
                    P.op("scalar", lambda e, dn=dn, u=u, h=h: e.activation(out=dn[:, h:h + 1], in_=pU[u][:, 128:129], func=AF.Abs),
                         reads=["l_pU%d" % u], writes=[dnn])
                for d in range(2):
                    c = cs[d]
                    dn, dnn = den[d], "l_den%d" % d
                    h0 = 2 * hh_
                    REB2 = TOK[:, c, 4, d * 4 + h0:d * 4 + h0 + 2]
                    P.op("vector", lambda e, dn=dn, REB2=REB2, h0=h0: e.tensor_tensor(out=dn[:, h0:h0 + 2], in0=dn[:, h0:h0 + 2], in1=REB2, op=ALU.max),
                         reads=[dnn, "l_TOK"], writes=[dnn])
                    P.op("vector", lambda e, dn=dn, h0=h0: e.reciprocal(dn[:, h0:h0 + 2], dn[:, h0:h0 + 2]), reads=[dnn], writes=[dnn])
                for (h, d) in items:
                    u = (h % 2) * 2 + d
                    dn, dnn = den[d], "l_den%d" % d
                    hsn = "l_HS%d%d" % (d, b)
                    if d == 0:
                        P.op("scalar", lambda e, b=b, h=h, dn=dn, u=u: e.activation(out=HS[0][b][:, h * 128:(h + 1) * 128], in_=pU[u][:, 0:128], func=AF.Copy, scale=dn[:, h:h + 1]),
                             reads=["l_pU%d" % u, dnn], writes=[hsn])
                    else:
                        P.op("vector", lambda e, b=b, h=h, dn=dn, u=u: e.tensor_scalar(HS[1][b][:, h * 128:(h + 1) * 128], pU[u][:, 0:128], dn[:, h:h + 1], None, ALU.mult),
                             reads=["l_pU%d" % u, dnn], writes=[hsn])
            P.dma("sync", lambda e, b=b, c=cs[0]: e.dma_start(out=G.HF[c * 128:(c + 1) * 128, :], in_=HS[0][b][:]), reads=["l_HS0%d" % b], writes=[_u()])
            P.dma("sync", lambda e, b=b, c=cs[1]: e.dma_start(out=G.HB[c * 128:(c + 1) * 128, :], in_=HS[1][b][:]), reads=["l_HS1%d" % b], writes=[_u()])
        P.barrier()
        P.emit()

    with contextlib.ExitStack() as st:
        sb, ps = _alloc(G, st)
        NB = 3
        hf = [sb("r_hf%d" % b, [128, 512], F32) for b in range(NB)]
        hb = [sb("r_hb%d" % b, [128, 512], F32) for b in range(NB)]
        g2 = [sb("r_g2%d" % b, [128, 512], F32) for b in range(NB)]
        junk = sb("r_junk", [128, 128], F32)
        ssq = [sb("r_ssq%d" % b, [128, 4], F32) for b in range(2)]
        Yb = [sb("r_Y%d" % b, [128, 512], BF16) for b in range(2)]
        ytb = [sb("r_yt%d" % b, [128, 4, 128], BF16) for b in range(2)]
        ptr2 = [ps("r_ptr%d" % b, [128, 4, 128], BF16) for b in range(2)]

        def ld5(c):
            if c >= NCH:
                return
            b3 = c % NB
            tk0 = OWN0 + c * 128
            P.dma("sync", lambda e, b3=b3, c=c: e.dma_start(out=hf[b3][:], in_=G.HF[c * 128:(c + 1) * 128, :]), writes=["r_hf%d" % b3])
            P.dma("sync", lambda e, b3=b3, c=c: e.dma_start(out=hb[b3][:], in_=G.HB[c * 128:(c + 1) * 128, :]), writes=["r_hb%d" % b3])
            P.dma("sync", lambda e, b3=b3, tk0=tk0: e.dma_start(out=g2[b3][:], in_=G.MLG2[tk0:tk0 + 128, :]), writes=["r_g2%d" % b3])
        ld5(0)
        ld5(1)
        for c in range(NCH):
            ld5(c + 2)
            b3, b = c % NB, c % 2
            P.op("vector", lambda e, b3=b3: e.tensor_tensor(out=hf[b3][:], in0=hf[b3][:], in1=hb[b3][:], op=ALU.add),
                 reads=["r_hf%d" % b3, "r_hb%d" % b3], writes=["r_hf%d" % b3])
            for h in range(4):
                P.op("scalar", lambda e, b3=b3, b=b, h=h: e.activation(out=junk[:], in_=hf[b3][:, h * 128:(h + 1) * 128], func=AF.Square, accum_out=ssq[b][:, h:h + 1]),
                     reads=["r_hf%d" % b3], writes=["r_junk", "r_ssq%d" % b])
            P.op("scalar", lambda e, b=b: e.activation(out=ssq[b][:], in_=ssq[b][:], func=AF.Sqrt, bias=G.c_eps[:], scale=1.0 / 128.0),
                 reads=["r_ssq%d" % b, "c_eps"], writes=["r_ssq%d" % b])
            P.op("vector", lambda e, b=b: e.reciprocal(ssq[b][:], ssq[b][:]), reads=["r_ssq%d" % b], writes=["r_ssq%d" % b])
            for h in range(4):
                P.op("vector", lambda e, b3=b3, b=b, h=h: e.scalar_tensor_tensor(
                    out=Yb[b][:, h * 128:(h + 1) * 128], in0=hf[b3][:, h * 128:(h + 1) * 128], scalar=ssq[b][:, h:h + 1],
                    in1=g2[b3][:, h * 128:(h + 1) * 128], op0=ALU.mult, op1=ALU.mult),
                    reads=["r_hf%d" % b3, "r_ssq%d" % b, "r_g2%d" % b3], writes=["r_Y%d" % b])
            for h in range(4):
                P.op("tensor", lambda e, b=b, h=h: e.transpose(ptr2[b][:, h, :], Yb[b][:, h * 128:(h + 1) * 128], G.ident_b[:]),
                     reads=["r_Y%d" % b, "ident_b"], writes=["r_ptr%d" % b], inc=(h == 3))
            P.op("scalar", lambda e, b=b: e.activation(out=ytb[b][:], in_=ptr2[b][:], func=AF.Copy), reads=["r_ptr%d" % b], writes=["r_yt%d" % b])
            P.dma("sync", lambda e, b=b, c=c: e.dma_start(out=G.YT[:, 4:8, c * 128:(c + 1) * 128], in_=ytb[b][:]), reads=["r_yt%d" % b], writes=[_u()])
        P.barrier()
        P.emit()


def stage_outproj(G):
    P, I, nc = G.P, G.I, G.nc
    l, j, m = 0, 1, 0
    with contextlib.ExitStack() as st:
        sb, ps = _alloc(G, st)
        wo = sb("o_wo", [128, 8, 1024], BF16)
        xt = [sb("o_xt%d" % b, [128, 8, TW], F32) for b in range(2)]
        yt = [sb("o_yt%d" % b, [128, 8, TW], BF16) for b in range(2)]
        po = [ps("o_po%d" % b, [128, TW]) for b in range(2)]
        load_w_cast(G, wo, "o_wo", I["mix_w_out"], 8, 1024)
        def ld(n):
            if n >= 16:
                return
            ti_, b_ = n + 1, n % 2
            P.dma("sync", lambda e, b_=b_, ti_=ti_: e.dma_start(out=xt[b_][:], in_=G.XT[:, :, ti_ * TW:(ti_ + 1) * TW]), reads=["XT%d" % ti_], writes=["o_xt%d" % b_])
            P.dma("sync", lambda e, b_=b_, n=n: e.dma_start(out=yt[b_][:], in_=G.YT[:, :, n * TW:(n + 1) * TW]), reads=["dramYT"], writes=["o_yt%d" % b_])
        ld(0)
        ld(1)
        for n in range(16):
            ti = n + 1
            t0 = ti * TW
            b = n % 2
            for f in range(8):
                q = f % 2
                for k in range(8):
                    P.op("tensor", lambda e, q=q, f=f, k=k, b=b: e.matmul(po[q][:], lhsT=wo[:, k, f * 128:(f + 1) * 128], rhs=yt[b][:, k, :], start=(k == 0), stop=(k == 7)),
                         reads=["o_wo", "o_yt%d" % b], writes=["o_po%d" % q], inc=(k == 7))
                gsc = G.GT[:, l, j, m, f:f + 1]
                P.op("vector", lambda e, q=q, f=f, b=b, gsc=gsc: e.scalar_tensor_tensor(
                    out=xt[b][:, f, :], in0=po[q][:], scalar=gsc, in1=xt[b][:, f, :], op0=ALU.mult, op1=ALU.add),
                    reads=["o_po%d" % q, "modc", "o_xt%d" % b], writes=["o_xt%d" % b])
            P.dma("sync", lambda e, b=b, t0=t0: e.dma_start(out=G.XT[:, :, t0:t0 + TW], in_=xt[b][:]), reads=["o_xt%d" % b], writes=["XT%d" % ti])
            ld(n + 2)
        P.barrier()
        P.emit()


def stage_sg(G):
    P, I, nc = G.P, G.I, G.nc
    l, j, m = 1, 1, 0
    with contextlib.ExitStack() as st:
        sb, ps = _alloc(G, st)
        wu = sb("g_wu", [128, 8, 2048], BF16)
        wv = sb("g_wv", [128, 8, 2048], BF16)
        wo = sb("g_wo", [128, 16, 1024], BF16)
        wsT = sb("g_wsT", [128, 8, 128], BF16)
        bs = sb("g_bs", [1, 1024], BF16)
        ones1 = sb("g_ones1", [1, 256], BF16)
        lng = sb("g_lng", [128, 2048], F32)
        lnb = sb("g_lnb", [128, 2048], F32)
        xt = [sb("g_xt%d" % b, [128, 8, TW], F32) for b in range(2)]
        hh = [sb("g_h%d" % b, [128, 8, TW], BF16) for b in range(2)]
        NS = NormScratch(G, sb, ps, "g_")
        uT = sb("g_uT", [128, 16, TW], BF16)
        vraw = [sb("g_vraw%d" % b, [128, 2048], F32) for b in range(2)]
        vn = [sb("g_vn%d" % b, [128, 2048], BF16) for b in range(2)]
        gated = sb("g_gated", [128, 16, TW], BF16)
        stats = [sb("g_stats%d" % b, [128, 4, 6], F32) for b in range(2)]
        mv = [sb("g_mv%d" % b, [128, 4], F32) for b in range(2)]
        pu = [ps("g_pu%d" % b, [128, TW]) for b in range(2)]
        pm = [ps("g_pm%d" % b, [128, 4, 128]) for b in range(2)]
        po1 = ps("g_po", [128, TW])
        po = [po1, po1]
        pv = [ps("g_pv%d" % b, [128, 512]) for b in range(2)]
        load_w_cast(G, wu, "g_wu", I["sg_w_in"][:, 0:2048], 8, 2048)
        load_w_cast(G, wv, "g_wv", I["sg_w_in"][:, 2048:4096], 8, 2048)
        load_w_cast(G, wo, "g_wo", I["sg_w_out"], 16, 1024)
        P.dma("gpsimd", lambda e: e.dma_start(out=wsT[:], in_=I["sg_w_sT"][:, :, :]), writes=["g_wsT"])
        P.dma("gpsimd", lambda e: e.dma_start(out=bs[:], in_=I["sg_b_s"][:, :]), writes=["g_bs"])
        P.op("vector", lambda e: e.memset(ones1[:], 1.0), writes=["g_ones1"])
        P.dma("sync", lambda e: e.dma_start(out=lng[:], in_=I["sg_lng_bc"][:, :]), writes=["g_lng"])
        P.dma("sync", lambda e: e.dma_start(out=lnb[:], in_=I["sg_lnb_bc"][:, :]), writes=["g_lnb"])
        tiles = list(range(1, 17))

        def ld(n):
            ti_ = tiles[n]
            xb = xt[n % 2]
            P.dma("sync", lambda e, xb=xb, ti_=ti_: e.dma_start(out=xb[:], in_=G.XT[:, :, ti_ * TW:(ti_ + 1) * TW]),
                  reads=["XT%d" % ti_], writes=["g_xt%d" % (n % 2)])
        def do_norm(n):
            bb = n % 2
            norm_mod(G, NS, xt[bb], "g_xt%d" % bb, G.GM[:, l, j, m, :], G.SH[:, l, j, m, :], hh[bb], "g_h%d" % bb)
        ld(0)
        ld(1)
        do_norm(0)
        vcnt = 0
        for n, ti in enumerate(tiles):
            t0 = ti * TW
            b = n % 2
            x, xn = xt[b], "g_xt%d" % b
            h, hn = hh[b], "g_h%d" % b
            NSB = TW // 128
            for s_ in range(NSB):
                VR = vraw[s_]
                for blk in range(4):
                    q = blk % 2
                    for k in range(8):
                        P.op("tensor", lambda e, q=q, blk=blk, k=k, h=h, s_=s_: e.matmul(
                            pv[q][:], lhsT=h[:, k, s_ * 128:(s_ + 1) * 128], rhs=wv[:, k, blk * 512:(blk + 1) * 512], start=(k == 0), stop=(k == 7)),
                            reads=["g_wv", hn], writes=["g_pv%d" % q], inc=(k == 7))
                    P.op("scalar", lambda e, q=q, blk=blk, VR=VR: e.activation(out=VR[:, blk * 512:(blk + 1) * 512], in_=pv[q][:], func=AF.Gelu_apprx_tanh),
                         reads=["g_pv%d" % q], writes=["g_vraw%d" % s_])
                    P.op("vector", lambda e, blk=blk, VR=VR, s_=s_: e.bn_stats(stats[s_][:, blk, :], VR[:, blk * 512:(blk + 1) * 512]),
                         reads=["g_vraw%d" % s_], writes=["g_stats%d" % s_])
                P.op("vector", lambda e, s_=s_: e.bn_aggr(mv[s_][:, 0:2], stats[s_][:].rearrange("p a b -> p (a b)")), reads=["g_stats%d" % s_], writes=["g_mv%d" % s_])
                P.op("scalar", lambda e, s_=s_: e.activation(out=mv[s_][:, 2:3], in_=mv[s_][:, 1:2], func=AF.Sqrt, bias=G.c_eps[:], scale=1.0),
                     reads=["g_mv%d" % s_, "c_eps"], writes=["g_mv%d" % s_])
                P.op("vector", lambda e, s_=s_: e.reciprocal(mv[s_][:, 2:3], mv[s_][:, 2:3]), reads=["g_mv%d" % s_], writes=["g_mv%d" % s_])
                P.op("vector", lambda e, s_=s_: e.tensor_scalar(mv[s_][:, 3:4], mv[s_][:, 0:1], mv[s_][:, 2:3], -1.0, ALU.mult, ALU.mult),
                     reads=["g_mv%d" % s_], writes=["g_mv%d" % s_])
            for fc in range(16):
                q = fc % 2
                if fc == 6:
                    vns = []
                    for s_ in range(NSB):
                        VR = vraw[s_]
                        VN, vnn = vn[vcnt % 2], "g_vn%d" % (vcnt % 2)
                        vcnt += 1
                        vns.append((VN, vnn))
                        P.op("scalar", lambda e, VR=VR, s_=s_: e.activation(out=VR[:], in_=VR[:], func=AF.Identity, bias=mv[s_][:, 3:4], scale=mv[s_][:, 2:3]),
                             reads=["g_vraw%d" % s_, "g_mv%d" % s_], writes=["g_vraw%d" % s_])
                        P.op("vector", lambda e, VR=VR: e.tensor_tensor(out=VR[:], in0=VR[:], in1=lng[:], op=ALU.mult),
                             reads=["g_vraw%d" % s_, "g_lng"], writes=["g_vraw%d" % s_])
                        P.op("vector", lambda e, VR=VR, VN=VN: e.tensor_tensor(out=VN[:], in0=VR[:], in1=lnb[:], op=ALU.add),
                             reads=["g_vraw%d" % s_, "g_lnb"], writes=[vnn])
                for k in range(8):
                    P.op("tensor", lambda e, q=q, fc=fc, k=k, h=h: e.matmul(pu[q][:], lhsT=wu[:, k, fc * 128:(fc + 1) * 128], rhs=h[:, k, :], start=(k == 0), stop=(k == 7)),
                         reads=["g_wu", hn], writes=["g_pu%d" % q], inc=(k == 7))
                P.op("scalar", lambda e, q=q, fc=fc: e.activation(out=uT[:, fc, :], in_=pu[q][:], func=AF.Gelu_apprx_tanh), reads=["g_pu%d" % q], writes=["g_uT%d" % fc])
            for s_ in range(NSB):
                VN, vnn = vns[s_]
                for g4 in range(4):
                    q = g4 % 2
                    for f4 in range(4):
                        fc = g4 * 4 + f4
                        g_ = fc // 2
                        P.op("tensor", lambda e, q=q, fc=fc, f4=f4, g_=g_, VN=VN: e.matmul(
                            pm[q][:, f4, :], lhsT=VN[:, fc * 128:(fc + 1) * 128], rhs=wsT[:, g_, :], start=True, stop=False),
                            reads=[vnn, "g_wsT"], writes=["g_pm%d" % q], inc=False)
                        P.op("tensor", lambda e, q=q, f4=f4, g_=g_: e.matmul(
                            pm[q][:, f4, :], lhsT=ones1[0:1, 0:128], rhs=bs[0:1, g_ * 128:(g_ + 1) * 128], start=False, stop=True),
                            reads=["g_ones1", "g_bs"], writes=["g_pm%d" % q], inc=(f4 == 3))
                    P.op("vector", lambda e, q=q, g4=g4, s_=s_: e.tensor_tensor(
                        out=gated[:, g4 * 4:(g4 + 1) * 4, s_ * 128:(s_ + 1) * 128], in0=uT[:, g4 * 4:(g4 + 1) * 4, s_ * 128:(s_ + 1) * 128], in1=pm[q][:], op=ALU.mult),
                        reads=["g_uT%d" % fc_ for fc_ in range(g4 * 4, g4 * 4 + 4)] + ["g_pm%d" % q],
                        writes=["g_gated%d" % fc_ for fc_ in range(g4 * 4, g4 * 4 + 4)])
            if n + 1 < len(tiles):
                do_norm(n + 1)
            for f in range(8):
                q = f % 2
                for fc in range(16):
                    P.op("tensor", lambda e, q=q, f=f, fc=fc: e.matmul(po[q][:], lhsT=wo[:, fc, f * 128:(f + 1) * 128], rhs=gated[:, fc, :], start=(fc == 0), stop=(fc == 15)),
                         reads=["g_wo", "g_gated%d" % fc], writes=["g_po"], inc=(fc == 15))
                gsc = G.GT[:, l, j, m, f:f + 1]
                P.op("vector", lambda e, q=q, f=f, x=x, gsc=gsc: e.scalar_tensor_tensor(
                    out=x[:, f, :], in0=po[q][:], scalar=gsc, in1=x[:, f, :], op0=ALU.mult, op1=ALU.add),
                    reads=["g_po", "modc", xn], writes=[xn])
            P.dma("sync", lambda e, x=x, t0=t0: e.dma_start(out=G.XT[:, :, t0:t0 + TW], in_=x[:]), reads=[xn], writes=["XT%d" % ti])
            if n + 2 < len(tiles):
                ld(n + 2)
        P.barrier()
        P.emit()


_NC_CACHE = {}


def kernel(**inputs):
    maps = _host_prep(inputs)
    if "nc" not in _NC_CACHE:
        _NC_CACHE["nc"] = build()
    nc = _NC_CACHE["nc"]
    res = run_bass_kernel_spmd(nc, maps, core_ids=list(range(8)))
    out = np.zeros((2, 16384, 1024), np.float32)
    for core in range(8):
        b, s = core // 4, core % 4
        out[b, s * 4096:(s + 1) * 4096, :] = res.results[core]["out"]
    return out
```

```python
import contextlib
import numpy as np
import concourse.bass as bass
import concourse.mybir as mybir
from concourse.bass_utils import run_bass_kernel_spmd

F32 = mybir.dt.float32
BF16 = mybir.dt.bfloat16
AF = mybir.ActivationFunctionType
ALU = mybir.AluOpType
AX = mybir.AxisListType

D = 1024
DFF = 2816
NE = 4608
NU = 4864
OWN0 = 256
NOWN = 4096
CTX0 = 4608
TW = 256
EPS = 1e-6
NEG = -30000.0

ENGS = ("tensor", "vector", "scalar", "gpsimd", "sync")
NDMASEM = 32
NHW = 24


class Buf:
    __slots__ = ("name", "writers", "readers")

    def __init__(self, name):
        self.name = name
        self.writers = []
        self.readers = []


class Prog:
    def __init__(self, nc, st):
        self.nc = nc
        self.q = {e: [] for e in ENGS}
        self.cnt = {e: 0 for e in ENGS}
        self.seen = {e: {} for e in ENGS}
        self.dcnt = [0] * NDMASEM
        self.dnext = 0
        self.dnext_sw = 0
        self.bufs = {}
        self.esem = {e: st.enter_context(nc.semaphore("s_" + e)) for e in ENGS}
        self.dsem = [st.enter_context(nc.semaphore("d%d" % i)) for i in range(NDMASEM)]
        self.lastinc = {e: True for e in ENGS}

    def buf(self, name):
        b = self.bufs.get(name)
        if b is None:
            b = self.bufs[name] = Buf(name)
        return b

    def _bl(self, lst):
        return [self.buf(b) if isinstance(b, str) else b for b in lst]

    def _deps(self, reads, writes):
        deps = {}
        for b in reads:
            for k, v in b.writers:
                if deps.get(k, 0) < v:
                    deps[k] = v
        for b in writes:
            for k, v in b.writers:
                if deps.get(k, 0) < v:
                    deps[k] = v
            for k, v in b.readers:
                if deps.get(k, 0) < v:
                    deps[k] = v
        return deps

    def _waits(self, eng, deps):
        seen = self.seen[eng]
        waits = []
        for k, v in deps.items():
            if k == "tensor" and eng == "tensor":
                continue
            if seen.get(k, 0) >= v:
                continue
            seen[k] = v
            waits.append((k, v))
        return waits

    def _record(self, ev, reads, writes):
        k = ev[0]
        for b in writes:
            b.writers = [ev]
            b.readers = []
        for b in reads:
            if b in writes:
                continue
            b.readers = [e for e in b.readers if e[0] != k] + [ev]

    def op(self, eng, fn, reads=(), writes=(), inc=True):
        reads = self._bl(reads)
        writes = self._bl(writes)
        waits = self._waits(eng, self._deps(reads, writes))
        if inc:
            self.cnt[eng] += 1
            ev = (eng, self.cnt[eng])
        else:
            ev = (eng, self.cnt[eng] + 1)
        self.lastinc[eng] = inc
        self.q[eng].append(("op", fn, waits, inc))
        self._record(ev, reads, writes)

    def dma(self, eng, fn, reads=(), writes=()):
        reads = self._bl(reads)
        writes = self._bl(writes)
        deps = self._deps(reads, writes)
        if eng == "gpsimd":
            i = NHW + self.dnext_sw
            self.dnext_sw = (self.dnext_sw + 1) % (NDMASEM - NHW)
        else:
            i = self.dnext
            self.dnext = (self.dnext + 1) % NHW
        key = ("d", i)
        if self.dcnt[i] > 0 and deps.get(key, 0) < self.dcnt[i]:
            deps[key] = self.dcnt[i]
        waits = self._waits(eng, deps)
        self.dcnt[i] += 16
        ev = (key, self.dcnt[i])
        self.q[eng].append(("dma", fn, waits, i))
        self._record(ev, reads, writes)

    def cc(self, fn, reads=(), writes=()):
        self.op("gpsimd", fn, reads=reads, writes=writes, inc=True)

    def barrier(self):
        for e in ENGS:
            assert self.lastinc[e], e
        deps = {e: self.cnt[e] for e in ENGS if self.cnt[e] > 0}
        for i in range(NDMASEM):
            if self.dcnt[i] > 0:
                deps[("d", i)] = self.dcnt[i]
        for e in ENGS:
            d = {k: v for k, v in deps.items() if k != e}
            waits = self._waits(e, d)
            self.q[e].append(("wait", None, waits, None))

    def emit(self):
        nc = self.nc
        esem, dsem = self.esem, self.dsem

        def semof(k):
            return esem[k] if isinstance(k, str) else dsem[k[1]]

        def run(engname):
            items = self.q[engname]

            def body(e):
                for kind, fn, waits, x in items:
                    for k, v in waits:
                        e.wait_ge(semof(k), v)
                    if kind == "op":
                        ins = fn(e)
                        if x:
                            ins.then_inc(esem[engname], 1)
                    elif kind == "dma":
                        fn(e).then_inc(dsem[x], 16)
            return body

        with nc.Block() as block:
            block.tensor(run("tensor"))
            block.vector(run("vector"))
            block.scalar(run("scalar"))
            block.gpsimd(run("gpsimd"))
            block.sync(run("sync"))
        self.q = {e: [] for e in ENGS}


def _rowmap(s):
    rm = np.zeros(72, np.int64)
    rm[4:68] = s * 64 + np.arange(64)
    rm[0:4] = (s * 64 - 4 + np.arange(4)) if s > 0 else np.array([5, 6, 7, 8])
    rm[68:72] = (s * 64 + 64 + np.arange(4)) if s < 3 else np.array([248, 249, 250, 251])
    return rm


def _na_bias_tables(rpb, s):
    rm = _rowmap(s)
    out = np.full((5, 8, 128, 576), NEG, np.float32)
    reps = [0, 1, 2, 30, 31]
    cols = np.arange(64)
    c0 = np.clip(cols - 8, 0, 64 - 16)
    for ci, p in enumerate(reps):
        for r in range(2):
            i = s * 64 + 2 * p + r
            r0 = min(max(i - 4, 0), 256 - 8)
            seen_rows = set()
            for j in range(9):
                krow = int(rm[2 * p + j])
                if krow < r0 or krow >= r0 + 8 or krow in seen_rows:
                    continue
                seen_rows.add(krow)
                rr = krow - i + 7
                for qc in range(64):
                    kc = np.arange(c0[qc], c0[qc] + 16)
                    out[ci][:, r * 64 + qc, j * 64 + kc] = rpb[:, rr, kc - qc + 15]
    return out


def _rope_tables(s):
    rm = _rowmap(s)
    row = np.repeat(rm, 64).astype(np.float32)
    col = np.tile(np.arange(64), 72).astype(np.float32)
    inv = (10000.0 ** (-np.arange(32, dtype=np.float32) / 32)).astype(np.float32)
    ang = np.concatenate([row[:, None] * inv, col[:, None] * inv], axis=-1).astype(np.float32)
    cos = np.cos(ang).astype(np.float32)
    sin = np.sin(ang).astype(np.float32)
    cos2 = np.repeat(cos, 2, axis=1)
    sin2 = np.repeat(sin, 2, axis=1)
    sin2[:, 0::2] *= -1.0
    cos_u = np.ones((NU, 128), np.float32)
    sin_u = np.zeros((NU, 128), np.float32)
    cos_u[:NE] = cos2
    sin_u[:NE] = sin2
    return np.tile(cos_u, (1, 4)), np.tile(sin_u, (1, 4))


def _host_prep(inp):
    x = np.asarray(inp["x"], np.float32)
    shared = {}
    shared["w_mod"] = np.ascontiguousarray(inp["w_mod"], np.float32)
    shared["b_modT"] = np.ascontiguousarray(np.asarray(inp["b_mod"], np.float32).reshape(2, 72, 128).transpose(2, 0, 1))
    shared["norm_gT"] = np.ascontiguousarray(np.asarray(inp["norm_g"], np.float32).reshape(2, 3, 8, 128).transpose(3, 0, 1, 2))
    shared["ffn_w_in"] = np.ascontiguousarray(inp["ffn_w_in"], np.float32)
    shared["ffn_w_out"] = np.ascontiguousarray(inp["ffn_w_out"], np.float32)
    mw = np.asarray(inp["mix_w_in"], np.float32)[0]
    shared["mix_w_in"] = np.ascontiguousarray(mw[:, :3584])
    wg = np.zeros((1024, 128), np.float32)
    gb = np.zeros((128, 1), np.float32)
    gate_b = np.asarray(inp["ml_gate_b"], np.float32)[0]
    for h in range(4):
        for d in range(2):
            for q in range(2):
                wg[:, q * 64 + d * 32 + h] = mw[:, 3584 + h * 4 + d * 2 + q]
                gb[q * 64 + d * 32 + h, 0] = gate_b[h, d, q]
    shared["w_gate"] = wg
    shared["gate_b"] = gb
    shared["headg_bc"] = np.ascontiguousarray(np.broadcast_to(np.asarray(inp["ml_head_g"], np.float32)[0].reshape(1, 512), (128, 512)))
    shared["mix_w_out"] = np.ascontiguousarray(np.asarray(inp["mix_w_out"], np.float32)[0])
    shared["sg_w_in"] = np.ascontiguousarray(np.asarray(inp["sg_w_in"], np.float32)[0])
    shared["sg_w_out"] = np.ascontiguousarray(np.asarray(inp["sg_w_out"], np.float32)[0])
    shared["sg_lng_bc"] = np.ascontiguousarray(np.broadcast_to(np.asarray(inp["sg_ln_g"], np.float32)[0].reshape(1, 2048), (128, 2048)))
    shared["sg_lnb_bc"] = np.ascontiguousarray(np.broadcast_to(np.asarray(inp["sg_ln_b"], np.float32)[0].reshape(1, 2048), (128, 2048)))
    shared["sg_w_sT"] = np.ascontiguousarray(np.asarray(inp["sg_w_s"], np.float32)[0].transpose(2, 0, 1))
    shared["sg_b_s"] = np.ascontiguousarray(np.asarray(inp["sg_b_s"], np.float32)[0].reshape(1, 1024))
    shared["final_g_bc"] = np.ascontiguousarray(np.broadcast_to(np.asarray(inp["final_g"], np.float32).reshape(1, 1024), (128, 1024)))
    shared["ident"] = np.eye(128, dtype=np.float32)
    tri = np.zeros((128, 2, 128), np.float32)
    ss, tt = np.meshgrid(np.arange(128), np.arange(128), indexing="ij")
    tri[:, 0, :] = (tt >= ss)
    tri[:, 1, :] = (tt <= ss)
    shared["tri"] = tri
    sel = np.zeros((64, 8, 128), np.float32)
    for d in range(2):
        for h in range(4):
            sel[d * 32 + h, d * 4 + h, :] = 1.0
    shared["selm"] = sel
    rpb = np.asarray(inp["na_rpb"], np.float32)[0]
    c = np.asarray(inp["c"], np.float32)
    cctx = np.asarray(inp["c_ctx"], np.float32)
    ctx = np.asarray(inp["ctx"], np.float32)
    maps = []
    for core in range(8):
        b, s = core // 4, core % 4
        rm = _rowmap(s)
        tok = (rm[:, None] * 64 + np.arange(64)[None, :]).reshape(-1)
        xin = np.concatenate([x[b][tok], ctx[b]], axis=0)
        cT = np.stack([c[b].reshape(8, 128).T, cctx.reshape(8, 128).T], axis=-1)
        cos4, sin4 = _rope_tables(s)
        selv = np.zeros((128, 16), np.float32)
        for j in range(4):
            selv[:, j] = 1.0 if j < s else 0.0
            selv[:, 4 + j] = 1.0 - selv[:, j]
            selv[:, 8 + j] = 1.0 if j > s else 0.0
            selv[:, 12 + j] = 1.0 - selv[:, 8 + j]
        m = dict(shared)
        m["xin"] = np.ascontiguousarray(xin)
        m["cT"] = np.ascontiguousarray(cT.astype(np.float32))
        m["ropecos"] = cos4
        m["ropesin"] = sin4
        nb = _na_bias_tables(rpb, s)
        nbp = np.full((5, 8, 128, 640), NEG, np.float32)
        nbp[..., :576] = nb
        m["nabiasT"] = np.ascontiguousarray(nbp.reshape(5, 8, 128, 5, 128).transpose(0, 1, 4, 3, 2))
        m["selv"] = selv
        maps.append(m)
    return maps


INPUT_SHAPES = {
    "xin": [NU, 1024], "cT": [128, 8, 2], "w_mod": [2, 1024, 9216], "b_modT": [128, 2, 72], "norm_gT": [128, 2, 3, 8],
    "ffn_w_in": [2, 2, 1024, 5632], "ffn_w_out": [2, 2, 2816, 1024], "mix_w_in": [1024, 3584], "w_gate": [1024, 128],
    "gate_b": [128, 1], "headg_bc": [128, 512], "mix_w_out": [1024, 1024], "sg_w_in": [1024, 4096], "sg_w_out": [2048, 1024],
    "sg_lng_bc": [128, 2048], "sg_lnb_bc": [128, 2048], "sg_w_sT": [128, 8, 128], "sg_b_s": [1, 1024], "final_g_bc": [128, 1024],
    "ident": [128, 128], "tri": [128, 2, 128], "selm": [64, 8, 128], "ropecos": [NU, 512], "ropesin": [NU, 512],
    "nabiasT": [5, 8, 128, 5, 128], "selv": [128, 16],
}


class Ctx:
    pass


_UC = [0]


def _u():
    _UC[0] += 1
    return "u%d" % _UC[0]


def build(stages="all", dbg=()):
    nc = bass.Bass("TRN2", target_bir_lowering=False)
    I = {k: nc.dram_tensor(k, shp, F32, kind="ExternalInput").ap() for k, shp in INPUT_SHAPES.items()}
    OUT = nc.dram_tensor("out", [NOWN, 1024], F32, kind="ExternalOutput").ap()

    def scratch(name, shape, dt):
        if name in dbg:
            return nc.dram_tensor(name, shape, dt, kind="ExternalOutput").ap()
        return nc.dram_tensor(name, shape, dt).ap()

    G = Ctx()
    G.nc, G.I, G.OUT = nc, I, OUT
    G.dbg = dbg
    if dbg:
        G.DBG = {k: nc.dram_tensor(k, shp, dt, kind="ExternalOutput").ap() for k, (shp, dt) in {
            "D_MOD": ([128, 3, 2, 3, 2, 8], F32), "D_X0": ([128, 8, TW], F32), "D_X1": ([128, 8, TW], F32),
            "D_H": ([128, 8, TW], BF16), "D_HID": ([128, 22, TW], BF16), "D_RSTD": ([128, TW], F32)}.items()}
    G.XT = scratch("XT", [128, 8, NU], F32)
    G.NAQT = scratch("NAQT", [128, 4, NU], BF16)
    G.NAKT = scratch("NAKT", [128, 4, NU], BF16)
    G.NAV = scratch("NAV", [NU, 520], BF16)
    G.MLQT = scratch("MLQT", [128, 4, NU], BF16)
    G.MLKT = scratch("MLKT", [128, 4, NU], BF16)
    G.MLK = scratch("MLK", [NU, 512], BF16)
    G.MLV = scratch("MLV", [NU, 516], BF16)
    G.MLG2 = scratch("MLG2", [NU, 512], F32)
    G.GIF = scratch("GIF", [128, NU], F32)
    G.YT = scratch("YT", [128, 8, NOWN], BF16)
    G.HF = scratch("HF", [NOWN, 512], F32)
    G.HB = scratch("HB", [NOWN, 512], F32)
    G.CCI = scratch("CCI", [128, 1040], F32)
    G.CCO = scratch("CCO", [512, 1040], F32)

    with contextlib.ExitStack() as gst:
        P = Prog(nc, gst)
        G.P = P

        def gsb(name, shape, dt):
            return gst.enter_context(nc.sbuf_tensor(name, shape, dt))
        G.ident_f = gsb("ident_f", [128, 128], F32)
        G.ident_b = gsb("ident_b", [128, 128], BF16)
        G.ident8 = gsb("ident8", [128, 128], BF16)
        G.ones_b = gsb("ones_b", [128, 128], BF16)
        G.ones_f = gsb("ones_f", [128, 128], F32)
        G.c_eps = gsb("c_eps", [128, 1], F32)
        G.c_one = gsb("c_one", [128, 1], F32)
        G.SH = gsb("SH", [128, 2, 3, 2, 8], F32)
        G.GM = gsb("GM", [128, 2, 3, 2, 8], F32)
        G.GT = gsb("GT", [128, 2, 3, 2, 8], F32)

        P.dma("sync", lambda e: e.dma_start(out=G.ident_f[:], in_=I["ident"][:, :]), writes=["ident_f"])
        P.op("vector", lambda e: e.tensor_copy(G.ident_b[:], G.ident_f[:]), reads=["ident_f"], writes=["ident_b"])
        P.op("vector", lambda e: e.tensor_scalar(G.ident8[:], G.ident_f[:], 8.0, None, ALU.mult), reads=["ident_f"], writes=["ident8"])
        P.op("vector", lambda e: e.memset(G.ones_b[:], 1.0), writes=["ones_b"])
        P.op("vector", lambda e: e.memset(G.ones_f[:], 1.0), writes=["ones_f"])
        P.op("vector", lambda e: e.memset(G.c_eps[:], EPS), writes=["c_eps"])
        P.op("vector", lambda e: e.memset(G.c_one[:], 1.0), writes=["c_one"])

        ext_tiles = [(ti, 0) for ti in range(18)] + [(18, 1)]
        own_tiles = [(ti, 0) for ti in range(1, 17)]
        S = stages
        stage_mod(G)
        stage_t0(G)
        if S in ("ffn_only",):
            stage_ffn(G, 0, 0, ext_tiles)
            stage_final(G)
        elif S == "ffn_dbg":
            stage_ffn(G, 0, 0, [(1, 0)])
        else:
            stage_ffn(G, 0, 0, ext_tiles)
            stage_inproj(G)
            stage_na(G)
            stage_ml(G)
            stage_outproj(G)
            stage_ffn(G, 0, 1, own_tiles)
            stage_ffn(G, 1, 0, own_tiles)
            stage_sg(G)
            stage_ffn(G, 1, 1, own_tiles)
            stage_final(G)
    return nc


_AC = [0]


def _alloc(G, st):
    nc = G.nc
    _AC[0] += 1
    sfx = "_%d" % _AC[0]

    def sb(name, shape, dt):
        return st.enter_context(nc.sbuf_tensor(name + sfx, shape, dt))

    def ps(name, shape, dt=F32):
        return st.enter_context(nc.psum_tensor(name + sfx, shape, dt))
    return sb, ps


def stage_mod(G):
    P, I, nc = G.P, G.I, G.nc
    with contextlib.ExitStack() as st:
        sb, ps = _alloc(G, st)
        cT = sb("m_cT", [128, 8, 2], F32)
        sil = sb("m_sil", [128, 8, 2], BF16)
        wm = [sb("m_wm%d" % i, [128, 8, 1024], BF16) for i in range(3)]
        modv = sb("m_modv", [128, 2, 72, 2], F32)
        bmod = sb("m_bmod", [128, 2, 72], F32)
        ng = sb("m_ng", [128, 2, 3, 8], F32)
        pp = [ps("m_ps%d" % i, [128, 8, 2]) for i in range(2)]
        xs = [sb("t_xs%d" % i, [128, 1024], F32) for i in range(3)]
        xo = [sb("t_xo%d" % i, [128, 8, 128], F32) for i in range(2)]
        pt = [ps("t_pt%d" % i, [128, 8, 128]) for i in range(2)]
        P.dma("sync", lambda e: e.dma_start(out=cT[:], in_=I["cT"][:, :, :]), writes=["m_cT"])
        P.dma("sync", lambda e: e.dma_start(out=bmod[:], in_=I["b_modT"][:, :, :]), writes=["m_bmod"])
        P.dma("sync", lambda e: e.dma_start(out=ng[:], in_=I["norm_gT"][:, :, :, :]), writes=["m_ng"])
        P.op("scalar", lambda e: e.activation(out=sil[:], in_=cT[:], func=AF.Silu), reads=["m_cT"], writes=["m_sil"])
        NSUB = NU // 128

        def t0_load(i):
            if i < NSUB:
                b3 = i % 3
                P.dma("sync", lambda e, b3=b3, i=i: e.dma_start(out=xs[b3][:], in_=I["xin"][i * 128:(i + 1) * 128, :]), writes=["t_xs%d" % b3])

        def t0_sub(i):
            t0 = i * 128
            b, b3 = i % 2, i % 3
            for k in range(8):
                P.op("tensor", lambda e, b=b, b3=b3, k=k: e.transpose(pt[b][:, k, :], xs[b3][:, k * 128:(k + 1) * 128], G.ident_f[:]),
                     reads=["t_xs%d" % b3, "ident_f"], writes=["t_pt%d" % b], inc=(k == 7))
            if b == 0:
                P.op("vector", lambda e, b=b: e.tensor_copy(xo[b][:], pt[b][:]), reads=["t_pt%d" % b], writes=["t_xo%d" % b])
            else:
                P.op("scalar", lambda e, b=b: e.activation(out=xo[b][:], in_=pt[b][:], func=AF.Identity), reads=["t_pt%d" % b], writes=["t_xo%d" % b])
            t0_load(i + 3)
            P.dma("sync", lambda e, b=b, t0=t0: e.dma_start(out=G.XT[:, :, t0:t0 + 128], in_=xo[b][:]),
                  reads=["t_xo%d" % b], writes=["XT%d" % (t0 // TW)])

        for i in range(3):
            t0_load(i)
        n = 0
        ti = 0
        for l in range(2):
            wsrc = I["w_mod"][l].rearrange("(k p) n -> p k n", p=128)
            for blk in range(9):
                w = wm[n % 3]
                wn = "m_wm%d" % (n % 3)
                pt_ = pp[n % 2]
                pn = "m_ps%d" % (n % 2)
                P.dma("gpsimd", lambda e, w=w, blk=blk, wsrc=wsrc: e.dma_start(out=w[:], in_=wsrc[:, :, blk * 1024:(blk + 1) * 1024]),
                      writes=[wn])
                for jj in range(8):
                    for k in range(8):
                        P.op("tensor", lambda e, w=w, pt_=pt_, jj=jj, k=k: e.matmul(
                            pt_[:, jj, :], lhsT=w[:, k, jj * 128:(jj + 1) * 128], rhs=sil[:, k, :], start=(k == 0), stop=(k == 7)),
                            reads=[wn, "m_sil"], writes=[pn], inc=(jj == 7 and k == 7))
                for m in range(2):
                    P.op("vector", lambda e, pt_=pt_, l=l, blk=blk, m=m: e.tensor_tensor(
                        out=modv[:, l, blk * 8:(blk + 1) * 8, m], in0=pt_[:, :, m], in1=bmod[:, l, blk * 8:(blk + 1) * 8], op=ALU.add),
                        reads=[pn, "m_bmod"], writes=["m_modv"])
                n += 1
                for _ in range(2):
                    if ti < NSUB:
                        t0_sub(ti)
                        ti += 1
        while ti < NSUB:
            t0_sub(ti)
            ti += 1
        for l in range(2):
            for j in range(3):
                for m in range(2):
                    sh = modv[:, l, (3 * j) * 8:(3 * j) * 8 + 8, m]
                    sc = modv[:, l, (3 * j + 1) * 8:(3 * j + 1) * 8 + 8, m]
                    gt = modv[:, l, (3 * j + 2) * 8:(3 * j + 2) * 8 + 8, m]
                    P.op("vector", lambda e, sh=sh, l=l, j=j, m=m: e.tensor_copy(G.SH[:, l, j, m, :], sh), reads=["m_modv"], writes=["modc"])
                    P.op("vector", lambda e, sc=sc, l=l, j=j, m=m: e.scalar_tensor_tensor(
                        out=G.GM[:, l, j, m, :], in0=sc, scalar=1.0, in1=ng[:, l, j, :], op0=ALU.add, op1=ALU.mult),
                        reads=["m_modv", "m_ng"], writes=["modc"])
                    P.op("vector", lambda e, gt=gt, l=l, j=j, m=m: e.tensor_scalar(
                        G.GT[:, l, j, m, :], gt, (1.0 if j == 1 else 0.5), None, ALU.mult), reads=["m_modv"], writes=["modc"])
        P.barrier()
        P.emit()


def stage_t0(G):
    return


class NormScratch:
    def __init__(self, G, sb, ps, pfx, W=TW):
        self.sq = [sb(pfx + "sq%d" % i, [128, W], BF16) for i in range(2)]
        self.rs = sb(pfx + "rs", [128, W], F32)
        self.rstd = sb(pfx + "rstd", [128, W], F32)
        self.tmp = [sb(pfx + "tmp%d" % i, [128, W], F32) for i in range(2)]
        self.pn = ps(pfx + "pn", [128, W])
        self.pfx = pfx


def norm_mod(G, NS, x, xname, gm, sh, h, hname, W=TW, phase=None):
    P = G.P
    pfx = NS.pfx
    for k in range(8):
        if phase is None:
            sq, sqn = NS.sq[k % 2], pfx + "sq%d" % (k % 2)
        else:
            sq, sqn = NS.sq8[k], pfx + "sq8_%d" % k
        if phase in (None, "A"):
            P.op("scalar", lambda e, sq=sq, k=k: e.activation(out=sq[:, :W], in_=x[:, k, :], func=AF.Square), reads=[xname], writes=[sqn])
        if phase in (None, "B"):
            P.op("tensor", lambda e, sq=sq, k=k: e.matmul(NS.pn[:, :W], lhsT=G.ones_b[:], rhs=sq[:, :W], start=(k == 0), stop=(k == 7)),
                 reads=[sqn, "ones_b"], writes=[pfx + "pn"], inc=True)
    if phase == "A":
        return
    P.op("scalar", lambda e: e.activation(out=NS.rs[:, :W], in_=NS.pn[:, :W], func=AF.Sqrt, bias=G.c_eps[:], scale=1.0 / 1024.0),
         reads=[pfx + "pn", "c_eps"], writes=[pfx + "rs"])
    P.op("vector", lambda e: e.reciprocal(NS.rstd[:, :W], NS.rs[:, :W]), reads=[pfx + "rs"], writes=[pfx + "rstd"])
    for k in range(8):
        tmp = NS.tmp[k % 2]
        tn = pfx + "tmp%d" % (k % 2)
        P.op("vector", lambda e, tmp=tmp, k=k: e.tensor_tensor(out=tmp[:, :W], in0=x[:, k, :], in1=NS.rstd[:, :W], op=ALU.mult),
             reads=[xname, pfx + "rstd"], writes=[tn])
        P.op("scalar", lambda e, tmp=tmp, k=k: e.activation(out=h[:, k, :], in_=tmp[:, :W], func=AF.Identity, bias=sh[:, k:k + 1], scale=gm[:, k:k + 1]),
             reads=[tn, "modc"], writes=[hname])


def load_w_cast(G, dst, dname, src, nk, ncols, step=1024):
    P = G.P
    v = src.rearrange("(k p) n -> p k n", p=128)
    for c0 in range(0, ncols, step):
        c1 = min(ncols, c0 + step)
        P.dma("gpsimd", lambda e, c0=c0, c1=c1: e.dma_start(out=dst[:, :, c0:c1], in_=v[:, :, c0:c1]), writes=[dname])


def stage_ffn(G, l, i, tiles):
    P, I, nc = G.P, G.I, G.nc
    j = 0 if i == 0 else 2
    with contextlib.ExitStack() as st:
        sb, ps = _alloc(G, st)
        wi = sb("f_wi", [128, 8, 2 * DFF], BF16)
        wo = sb("f_wo", [128, 22, 1024], BF16)
        xt = [sb("f_xt%d" % b, [128, 8, TW], F32) for b in range(2)]
        hh = [sb("f_h%d" % b, [128, 8, TW], BF16) for b in range(2)]
        hid = sb("f_hid", [128, 22, TW], BF16)
        sa = [sb("f_sa%d" % b, [128, TW], F32) for b in range(2)]
        NS = NormScratch(G, sb, ps, "f_")
        NS.sq8 = [sb("f_sq8_%d" % k, [128, TW], BF16) for k in range(8)]
        pa = [ps("f_pa%d" % b, [128, TW]) for b in range(2)]
        pb = [ps("f_pb%d" % b, [128, TW]) for b in range(2)]
        po = [ps("f_po%d" % b, [128, TW]) for b in range(2)]
        wv_ = I["ffn_w_in"][l, i].rearrange("(k p) n -> p k n", p=128)
        for pc in (0, 2, 3, 1, 4, 5):
            c0, c1 = pc * 1024, min(2 * DFF, (pc + 1) * 1024)
            P.dma("gpsimd", lambda e, c0=c0, c1=c1: e.dma_start(out=wi[:, :, c0:c1], in_=wv_[:, :, c0:c1]), writes=["f_wi%d" % pc])
        load_w_cast(G, wo, "f_wo", I["ffn_w_out"][l, i], 22, 1024)
        def ld(n):
            ti_ = tiles[n][0]
            xb = xt[n % 2]
            P.dma("sync", lambda e, xb=xb, ti_=ti_: e.dma_start(out=xb[:], in_=G.XT[:, :, ti_ * TW:(ti_ + 1) * TW]),
                  reads=["XT%d" % ti_], writes=["f_xt%d" % (n % 2)])
        def do_norm(n, phase=None):
            ti_, m_ = tiles[n]
            bb = n % 2
            norm_mod(G, NS, xt[bb], "f_xt%d" % bb, G.GM[:, l, j, m_, :], G.SH[:, l, j, m_, :], hh[bb], "f_h%d" % bb, phase=phase)
        ld(0)
        if len(tiles) > 1:
            ld(1)
        do_norm(0)
        for n, (ti, m) in enumerate(tiles):
            t0 = ti * TW
            b = n % 2
            x, xn = xt[b], "f_xt%d" % b
            h, hn = hh[b], "f_h%d" % b
            for jj in range(22):
                q = jj % 2
                for half, pp, pn in ((0, pa[q], "f_pa%d" % q), (1, pb[q], "f_pb%d" % q)):
                    c0 = half * DFF + jj * 128
                    for k in range(8):
                        P.op("tensor", lambda e, pp=pp, c0=c0, k=k, h=h: e.matmul(
                            pp[:], lhsT=wi[:, k, c0:c0 + 128], rhs=h[:, k, :], start=(k == 0), stop=(k == 7)),
                            reads=["f_wi%d" % (c0 // 1024), "f_wi%d" % ((c0 + 127) // 1024), hn], writes=[pn], inc=(k == 7))
                P.op("scalar", lambda e, q=q: e.activation(out=sa[q][:], in_=pa[q][:], func=AF.Silu), reads=["f_pa%d" % q], writes=["f_sa%d" % q])
                P.op("vector", lambda e, q=q, jj=jj: e.tensor_tensor(out=hid[:, jj, :], in0=sa[q][:], in1=pb[q][:], op=ALU.mult),
                     reads=["f_sa%d" % q, "f_pb%d" % q], writes=["f_hid%d" % jj])
            if n + 1 < len(tiles):
                do_norm(n + 1, "A")
            for f in range(8):
                q = f % 2
                if f == 3 and n + 1 < len(tiles):
                    do_norm(n + 1, "B")
                for jj in range(22):
                    P.op("tensor", lambda e, q=q, f=f, jj=jj: e.matmul(
                        po[q][:], lhsT=wo[:, jj, f * 128:(f + 1) * 128], rhs=hid[:, jj, :], start=(jj == 0), stop=(jj == 21)),
                        reads=["f_wo", "f_hid%d" % jj], writes=["f_po%d" % q], inc=(jj == 21))
                gsc = G.GT[:, l, j, m, f:f + 1]
                P.op("vector", lambda e, q=q, f=f, x=x, gsc=gsc: e.scalar_tensor_tensor(
                    out=x[:, f, :], in0=po[q][:], scalar=gsc, in1=x[:, f, :], op0=ALU.mult, op1=ALU.add),
                    reads=["f_po%d" % q, "modc", xn], writes=[xn])
            if G.dbg and ti == 1:
                P.dma("sync", lambda e: e.dma_start(out=G.DBG["D_HID"][:, :, :], in_=hid[:]), reads=["f_hid%d" % q for q in range(22)], writes=["dbg3"])
                P.dma("sync", lambda e, x=x: e.dma_start(out=G.DBG["D_X1"][:, :, :], in_=x[:]), reads=[xn], writes=["dbg4"])
            P.dma("sync", lambda e, x=x, t0=t0: e.dma_start(out=G.XT[:, :, t0:t0 + TW], in_=x[:]), reads=[xn], writes=["XT%d" % ti])
            if n + 2 < len(tiles):
                ld(n + 2)
        P.barrier()
        P.emit()


def stage_final(G):
    P, I, nc = G.P, G.I, G.nc
    with contextlib.ExitStack() as st:
        sb, ps = _alloc(G, st)
        fg = sb("k_fg", [128, 1024], F32)
        xs = [sb("k_xs%d" % b, [128, 8, 128], F32) for b in range(2)]
        junk = sb("k_junk", [128, 1024], F32)
        yo = [sb("k_yo%d" % b, [128, 1024], F32) for b in range(2)]
        ssq = sb("k_ssq", [128, 2], F32)
        rs = sb("k_rs", [128, 2], F32)
        pt = [ps("k_pt%d" % b, [128, 8, 128]) for b in range(2)]
        P.dma("sync", lambda e: e.dma_start(out=fg[:], in_=I["final_g_bc"][:, :]), writes=["k_fg"])
        for i in range(NOWN // 128):
            t0 = OWN0 + i * 128
            b = i % 2
            P.dma("gpsimd", lambda e, b=b, t0=t0: e.dma_start(out=xs[b][:], in_=G.XT[:, :, t0:t0 + 128]),
                  reads=["XT%d" % (t0 // TW)], writes=["k_xs%d" % b])
            for k in range(8):
                P.op("tensor", lambda e, b=b, k=k: e.transpose(pt[b][:, k, :], xs[b][:, k, :], G.ident_f[:]),
                     reads=["k_xs%d" % b, "ident_f"], writes=["k_pt%d" % b], inc=(k == 7))
            ptf = pt[b][:].rearrange("p k n -> p (k n)")
            P.op("scalar", lambda e, b=b, ptf=ptf: e.activation(out=junk[:], in_=ptf, func=AF.Square, accum_out=ssq[:, b:b + 1]),
                 reads=["k_pt%d" % b], writes=["k_junk", "k_ssq%d" % b])
            P.op("scalar", lambda e, b=b: e.activation(out=rs[:, b:b + 1], in_=ssq[:, b:b + 1], func=AF.Sqrt, bias=G.c_eps[:], scale=1.0 / 1024.0),
                 reads=["k_ssq%d" % b, "c_eps"], writes=["k_rs%d" % b])
            P.op("vector", lambda e, b=b: e.reciprocal(rs[:, b:b + 1], rs[:, b:b + 1]), reads=["k_rs%d" % b], writes=["k_rs%d" % b])
            P.op("vector", lambda e, b=b, ptf=ptf: e.scalar_tensor_tensor(
                out=yo[b][:], in0=ptf, scalar=rs[:, b:b + 1], in1=fg[:], op0=ALU.mult, op1=ALU.mult),
                reads=["k_pt%d" % b, "k_rs%d" % b, "k_fg"], writes=["k_yo%d" % b])
            P.dma("sync", lambda e, b=b, i=i: e.dma_start(out=G.OUT[i * 128:(i + 1) * 128, :], in_=yo[b][:]),
                  reads=["k_yo%d" % b], writes=["OUT"])
        P.barrier()
        P.emit()


def stage_inproj(G):
    P, I, nc = G.P, G.I, G.nc
    l, j = 0, 1
    tiles = [(ti, 0) for ti in range(18)] + [(18, 1)]
    with contextlib.ExitStack() as st:
        sb, ps = _alloc(G, st)
        wfm = sb("i_wfm", [128, 8, 1024], BF16)
        wg = sb("i_wg", [128, 8, 128], BF16)
        wtm = sb("i_wtm", [128, 8, 2560], BF16)
        xt = [sb("i_xt%d" % b, [128, 8, TW], F32) for b in range(2)]
        hh = [sb("i_h%d" % b, [128, 8, TW], BF16) for b in range(2)]
        NS = NormScratch(G, sb, ps, "i_")
        NS.sq8 = [sb("i_sq8_%d" % k, [128, TW], BF16) for k in range(8)]
        fm = [sb("i_fm%d" % b, [128, 8, TW], BF16) for b in range(2)]
        gts = [sb("i_gt%d" % b, [128, TW], F32) for b in range(2)]
        cosT = [sb("i_cos%d" % b, [128, 512], F32) for b in range(2)]
        sinT = [sb("i_sin%d" % b, [128, 512], F32) for b in range(2)]
        hg = sb("i_hg", [128, 512], F32)
        vt = [sb("i_vt%d" % b, [128, 8, 65], BF16) for b in range(2)]
        mv = [sb("i_mv%d" % b, [128, 4, 129], BF16) for b in range(2)]
        xs = [sb("i_xs%d" % b, [128, 512], F32) for b in range(2)]
        r1 = [sb("i_r1%d" % b, [128, 512], F32) for b in range(2)]
        r2 = [sb("i_r2%d" % b, [128, 512], F32) for b in range(2)]
        qr = [sb("i_qr%d" % b, [128, 512], BF16) for b in range(2)]
        sg = sb("i_sg", [128, 512], F32)
        g2 = [sb("i_g2%d" % b, [128, 512], F32) for b in range(2)]
        tq = [sb("i_tq%d" % b, [128, 4, 128], BF16) for b in range(2)]
        pfm = [ps("i_pfm%d" % b, [128, TW]) for b in range(2)]
        ptm = [ps("i_ptm%d" % b, [128, 512]) for b in range(2)]
        ptr = [ps("i_ptr%d" % b, [128, 4, 128], BF16) for b in range(2)]
        load_w_cast(G, wfm, "i_wfm", I["mix_w_in"][:, 0:1024], 8, 1024)
        load_w_cast(G, wg, "i_wg", I["w_gate"], 8, 128)
        load_w_cast(G, wtm, "i_wtm", I["mix_w_in"][:, 1024:3584], 8, 2560)
        P.dma("sync", lambda e: e.dma_start(out=hg[:], in_=I["headg_bc"][:, :]), writes=["i_hg"])
        for b in range(2):
            P.op("vector", lambda e, b=b: e.memset(vt[b][:], 1.0), writes=["i_vt%d" % b])
            P.op("vector", lambda e, b=b: e.memset(mv[b][:], 1.0), writes=["i_mv%d" % b])

        def ld(n):
            ti_ = tiles[n][0]
            xb = xt[n % 2]
            P.dma("gpsimd", lambda e, xb=xb, ti_=ti_: e.dma_start(out=xb[:], in_=G.XT[:, :, ti_ * TW:(ti_ + 1) * TW]),
                  reads=["XT%d" % ti_], writes=["i_xt%d" % (n % 2)])
        ld(0)
        cnt = {"s": 0, "r": 0, "t": 0}
        deferred = []

        def rope_and_T(pt_, ptn, scale, dstT, tmaj_dst, ts0):
            a = cnt["r"] % 2
            cnt["r"] += 1
            sb_ = cnt["s"] % 2
            X, R1, R2, QR, TQ, PT = xs[a], r1[a], r2[a], qr[a], tq[a], ptr[a]
            xn_, r1n, r2n, qrn, tqn, ptn2 = "i_xs%d" % a, "i_r1%d" % a, "i_r2%d" % a, "i_qr%d" % a, "i_tq%d" % a, "i_ptr%d" % a
            P.op("scalar", lambda e: e.activation(out=X[:], in_=pt_[:], func=AF.Copy, scale=scale), reads=[ptn], writes=[xn_])
            P.op("vector", lambda e: e.tensor_tensor(out=R1[:], in0=X[:], in1=cosT[sb_][:], op=ALU.mult), reads=[xn_, "i_cos%d" % sb_], writes=[r1n])
            Xv = X[:].rearrange("p (i t) -> p i t", t=2)
            Sv = sinT[sb_][:].rearrange("p (i t) -> p i t", t=2)
            Rv = R2[:].rearrange("p (i t) -> p i t", t=2)
            P.op("vector", lambda e: e.tensor_tensor(out=Rv[:, :, 0], in0=Xv[:, :, 1], in1=Sv[:, :, 0], op=ALU.mult),
                 reads=[xn_, "i_sin%d" % sb_], writes=[r2n])
            P.op("vector", lambda e: e.tensor_tensor(out=Rv[:, :, 1], in0=Xv[:, :, 0], in1=Sv[:, :, 1], op=ALU.mult),
                 reads=[xn_, "i_sin%d" % sb_], writes=[r2n])
            P.op("vector", lambda e: e.tensor_tensor(out=QR[:], in0=R1[:], in1=R2[:], op=ALU.add), reads=[r1n, r2n], writes=[qrn])
            if tmaj_dst is not None:
                P.dma("sync", lambda e: e.dma_start(out=tmaj_dst[ts0:ts0 + 128, :], in_=QR[:]), reads=[qrn], writes=[_u()])
            def later():
                for hd in range(4):
                    P.op("tensor", lambda e, hd=hd: e.transpose(PT[:, hd, :], QR[:, hd * 128:(hd + 1) * 128], G.ident_b[:]),
                         reads=[qrn, "ident_b"], writes=[ptn2], inc=(hd == 3))
                P.op("scalar", lambda e: e.activation(out=TQ[:], in_=PT[:], func=AF.Copy), reads=[ptn2], writes=[tqn])
                P.dma("sync", lambda e: e.dma_start(out=dstT[:, :, ts0:ts0 + 128], in_=TQ[:]), reads=[tqn], writes=[_u()])
            deferred.append(later)

        for n, (ti, m) in enumerate(tiles):
            t0 = ti * TW
            b = n % 2
            x, xn = xt[b], "i_xt%d" % b
            h, hn = hh[b], "i_h%d" % b
            if n + 1 < len(tiles):
                ld(n + 1)
            if n == 0:
                norm_mod(G, NS, x, xn, G.GM[:, l, j, m, :], G.SH[:, l, j, m, :], h, hn)
            FM, fmn = fm[b], "i_fm%d" % b
            no_q = ti in (0, 17, 18)
            blks = [0] if ti in (0, 17) else ([0, 2, 3] if ti == 18 else [0, 1, 2, 3, 4])
            for fc in (range(4, 8) if no_q else range(8)):
                q = fc % 2
                for k in range(8):
                    P.op("tensor", lambda e, q=q, fc=fc, k=k, h=h: e.matmul(
                        pfm[q][:], lhsT=wfm[:, k, fc * 128:(fc + 1) * 128], rhs=h[:, k, :], start=(k == 0), stop=(k == 7)),
                        reads=["i_wfm", hn], writes=["i_pfm%d" % q], inc=(k == 7))
                P.op("scalar", lambda e, q=q, fc=fc, FM=FM: e.activation(out=FM[:, fc, :], in_=pfm[q][:], func=AF.Copy),
                     reads=["i_pfm%d" % q], writes=[fmn])
            if not no_q:
                P.dma("sync", lambda e, FM=FM, t0=t0: e.dma_start(out=G.NAQT[:, :, t0:t0 + TW], in_=FM[:, 0:4, :]), reads=[fmn], writes=[_u()])
            P.dma("sync", lambda e, FM=FM, t0=t0: e.dma_start(out=G.NAKT[:, :, t0:t0 + TW], in_=FM[:, 4:8, :]), reads=[fmn], writes=[_u()])
            GTS, gtn = gts[b], "i_gt%d" % b
            for k in range(8):
                P.op("tensor", lambda e, k=k, h=h: e.matmul(pfm[0][:], lhsT=wg[:, k, :], rhs=h[:, k, :], start=(k == 0), stop=(k == 7)),
                     reads=["i_wg", hn], writes=["i_pfm0"], inc=(k == 7))
            P.op("vector", lambda e, GTS=GTS: e.tensor_copy(GTS[:], pfm[0][:]), reads=["i_pfm0"], writes=[gtn])
            P.dma("sync", lambda e, GTS=GTS, t0=t0: e.dma_start(out=G.GIF[:, t0:t0 + TW], in_=GTS[:]), reads=[gtn], writes=[_u()])
            if n + 1 < len(tiles):
                ti2, m2 = tiles[n + 1]
                b2 = (n + 1) % 2
                norm_mod(G, NS, xt[b2], "i_xt%d" % b2, G.GM[:, l, j, m2, :], G.SH[:, l, j, m2, :], hh[b2], "i_h%d" % b2, phase="A")
            for s_ in range(TW // 128):
                if s_ == 1 and n + 1 < len(tiles):
                    norm_mod(G, NS, xt[b2], "i_xt%d" % b2, G.GM[:, l, j, m2, :], G.SH[:, l, j, m2, :], hh[b2], "i_h%d" % b2, phase="B")
                ts0 = t0 + s_ * 128
                sbi = cnt["s"] % 2
                P.dma("gpsimd", lambda e, sbi=sbi, ts0=ts0: e.dma_start(out=cosT[sbi][:], in_=I["ropecos"][ts0:ts0 + 128, :]), writes=["i_cos%d" % sbi])
                P.dma("gpsimd", lambda e, sbi=sbi, ts0=ts0: e.dma_start(out=sinT[sbi][:], in_=I["ropesin"][ts0:ts0 + 128, :]), writes=["i_sin%d" % sbi])
                for blk in blks:
                    a = cnt["t"] % 2
                    cnt["t"] += 1
                    PT_, ptn = ptm[a], "i_ptm%d" % a
                    for k in range(8):
                        P.op("tensor", lambda e, PT_=PT_, k=k, h=h, s_=s_, blk=blk: e.matmul(
                            PT_[:], lhsT=h[:, k, s_ * 128:(s_ + 1) * 128], rhs=wtm[:, k, blk * 512:(blk + 1) * 512], start=(k == 0), stop=(k == 7)),
                            reads=["i_wtm", hn], writes=[ptn], inc=(k == 7))
                    while len(deferred) > (1 if blk == 3 else 0):
                        deferred.pop(0)()
                    if blk == 0:
                        VT = vt[sbi]
                        P.op("scalar", lambda e, VT=VT, PT_=PT_: e.activation(out=VT[:, :, 0:64], in_=PT_[:].rearrange("p (h d) -> p h d", d=64), func=AF.Copy),
                             reads=[ptn], writes=["i_vt%d" % sbi])
                        P.dma("sync", lambda e, VT=VT, ts0=ts0: e.dma_start(out=G.NAV[ts0:ts0 + 128, :], in_=VT[:].rearrange("p h d -> p (h d)")),
                              reads=["i_vt%d" % sbi], writes=[_u()])
                    elif blk == 1:
                        rope_and_T(PT_, ptn, 1.0, G.MLQT, None, ts0)
                    elif blk == 2:
                        rope_and_T(PT_, ptn, 128.0 ** -0.5, G.MLKT, G.MLK, ts0)
                    elif blk == 3:
                        MV = mv[sbi]
                        P.op("scalar", lambda e, MV=MV, PT_=PT_: e.activation(out=MV[:, :, 0:128], in_=PT_[:].rearrange("p (h d) -> p h d", d=128), func=AF.Copy),
                             reads=[ptn], writes=["i_mv%d" % sbi])
                        P.dma("sync", lambda e, MV=MV, ts0=ts0: e.dma_start(out=G.MLV[ts0:ts0 + 128, :], in_=MV[:].rearrange("p h d -> p (h d)")),
                              reads=["i_mv%d" % sbi], writes=[_u()])
                    else:
                        G2 = g2[sbi]
                        P.op("scalar", lambda e, PT_=PT_: e.activation(out=sg[:], in_=PT_[:], func=AF.Sigmoid), reads=[ptn], writes=["i_sg"])
                        P.op("vector", lambda e, G2=G2: e.tensor_tensor(out=G2[:], in0=sg[:], in1=hg[:], op=ALU.mult), reads=["i_sg", "i_hg"], writes=["i_g2%d" % sbi])
                        P.dma("sync", lambda e, G2=G2, ts0=ts0: e.dma_start(out=G.MLG2[ts0:ts0 + 128, :], in_=G2[:]), reads=["i_g2%d" % sbi], writes=[_u()])
                while deferred:
                    deferred.pop(0)()
                cnt["s"] += 1
        P.barrier()
        P.emit()


def stage_na(G):
    P, I, nc = G.P, G.I, G.nc
    with contextlib.ExitStack() as st:
        sb, ps = _alloc(G, st)
        KT = sb("n_KT", [128, 4, NU], BF16)
        V = sb("n_V", [128, 38, 520], BF16)
        QT = sb("n_QT", [128, 4, NOWN], BF16)
        BI = sb("n_BI", [128, 5, 8, 5, 128], BF16)
        sAb = [sb("n_sAb%d" % b, [128, 4, 128], F32) for b in range(2)]
        sBb = [sb("n_sBb%d" % b, [128, 128], F32) for b in range(2)]
        pt = [sb("n_pt%d" % b, [128, 7, 128], BF16) for b in range(2)]
        ya = [sb("n_ya%d" % b, [128, 512], BF16) for b in range(2)]
        rec = [sb("n_rec%d" % b, [128, 8], F32) for b in range(2)]
        yt = [sb("n_yt%d" % b, [128, 4, 128], BF16) for b in range(2)]
        sA = [ps("n_sA%d" % b, [128, 4, 128]) for b in range(2)]
        sB = [ps("n_sB%d" % b, [128, 4, 128]) for b in range(2)]
        O = ps("n_O", [128, 8, 128])
        ptr = ps("n_ptr", [128, 4, 128], BF16)
        P.dma("sync", lambda e: e.dma_start(out=KT[:], in_=G.NAKT[:, :, :]), reads=["dramNA"], writes=["n_KT"])
        P.dma("sync", lambda e: e.dma_start(out=V[:], in_=G.NAV.rearrange("(c p) n -> p c n", p=128)), reads=["dramNA"], writes=["n_V"])
        P.dma("sync", lambda e: e.dma_start(out=QT[:], in_=G.NAQT[:, :, OWN0:OWN0 + NOWN]), reads=["dramNA"], writes=["n_QT"])
        for c5 in range(5):
            P.dma("gpsimd", lambda e, c5=c5: e.dma_start(out=BI[:, c5], in_=I["nabiasT"][c5].rearrange("h k j q -> k h j q")), writes=["n_BI"])
        hb = 0
        for p in range(32):
            cls = 0 if p == 0 else 1 if p == 1 else 3 if p == 30 else 4 if p == 31 else 2
            pb2 = p % 2
            for h in range(8):
                hc, b0 = h // 2, (h % 2) * 64
                a = hb % 2
                hb += 1
                q_ap = QT[b0:b0 + 64, hc, p * 128:(p + 1) * 128]
                SA, SB, PT = sA[a], sB[a], pt[a]
                san, sbn, ptn = "n_sA%d" % a, "n_sB%d" % a, "n_pt%d" % a
                for jj in range(4):
                    k0 = (p + jj) * 128
                    P.op("tensor", lambda e, SA=SA, jj=jj, k0=k0, q_ap=q_ap, hc=hc, b0=b0: e.matmul(
                        SA[:, jj, :], lhsT=KT[b0:b0 + 64, hc, k0:k0 + 128], rhs=q_ap, start=True, stop=True),
                        reads=["n_KT", "n_QT"], writes=[san], inc=(jj == 3))
                k0 = (p + 4) * 128
                P.op("tensor", lambda e, SB=SB, k0=k0, q_ap=q_ap, hc=hc, b0=b0: e.matmul(
                    SB[0:64, 0, :], lhsT=KT[b0:b0 + 64, hc, k0:k0 + 64], rhs=q_ap, start=True, stop=True),
                    reads=["n_KT", "n_QT"], writes=[sbn], inc=False)
                for c in range(2):
                    k0 = CTX0 + c * 128
                    P.op("tensor", lambda e, SB=SB, c=c, k0=k0, q_ap=q_ap, hc=hc, b0=b0: e.matmul(
                        SB[:, 1 + c, :], lhsT=KT[b0:b0 + 64, hc, k0:k0 + 128], rhs=q_ap, start=True, stop=True),
                        reads=["n_KT", "n_QT"], writes=[sbn], inc=(c == 1))
                AB, BB = sAb[a], sBb[a]
                abn, bbn = "n_sAb%d" % a, "n_sBb%d" % a
                P.op("vector", lambda e, SA=SA, AB=AB, cls=cls, h=h: e.scalar_tensor_tensor(
                    out=AB[:], in0=SA[:], scalar=0.125, in1=BI[:, cls, h, 0:4, :], op0=ALU.mult, op1=ALU.add),
                    reads=[san, "n_BI"], writes=[abn])
                P.op("vector", lambda e, SB=SB, BB=BB, cls=cls, h=h: e.scalar_tensor_tensor(
                    out=BB[0:64, :], in0=SB[0:64, 0, :], scalar=0.125, in1=BI[0:64, cls, h, 4, :], op0=ALU.mult, op1=ALU.add),
                    reads=[sbn, "n_BI"], writes=[bbn])
                P.op("scalar", lambda e, AB=AB, PT=PT: e.activation(out=PT[:, 0:4, :], in_=AB[:], func=AF.Exp), reads=[abn], writes=[ptn])
                P.op("scalar", lambda e, SB=SB, PT=PT: e.activation(out=PT[:, 5:7, :], in_=SB[:, 1:3, :], func=AF.Exp, scale=0.125), reads=[sbn, bbn], writes=[ptn])
                P.op("scalar", lambda e, BB=BB, PT=PT: e.activation(out=PT[0:64, 4, :], in_=BB[0:64, :], func=AF.Exp), reads=[bbn], writes=[ptn])
                specs = [(jj, 128, p + jj) for jj in range(4)] + [(4, 64, p + 4), (5, 128, 36), (6, 128, 37)]
                for si, (slot, nk, vc) in enumerate(specs):
                    P.op("tensor", lambda e, PT=PT, slot=slot, nk=nk, vc=vc, h=h, si=si: e.matmul(
                        O[:, h, 0:65], lhsT=PT[0:nk, slot, :], rhs=V[0:nk, vc, h * 65:(h + 1) * 65], start=(si == 0), stop=(si == 6)),
                        reads=[ptn, "n_V"], writes=["n_O%d" % (h // 4)], inc=(si == 6))
                if h in (3, 7):
                    hf_ = h // 4
                    R, YA = rec[pb2], ya[pb2]
                    rn, yan = "n_rec%d_%d" % (pb2, hf_), "n_ya%d" % pb2
                    P.op("vector", lambda e, R=R, hf_=hf_: e.reciprocal(R[:, hf_ * 4:hf_ * 4 + 4], O[:, hf_ * 4:hf_ * 4 + 4, 64]), reads=["n_O%d" % hf_], writes=[rn])
                    for h2 in range(hf_ * 4, hf_ * 4 + 4):
                        P.op("scalar", lambda e, R=R, YA=YA, h2=h2: e.activation(out=YA[:, h2 * 64:(h2 + 1) * 64], in_=O[:, h2, 0:64], func=AF.Copy, scale=R[:, h2:h2 + 1]),
                             reads=["n_O%d" % hf_, rn], writes=[yan])
            YA, YT_ = ya[pb2], yt[pb2]
            yan, ytn = "n_ya%d" % pb2, "n_yt%d" % pb2
            for c in range(4):
                P.op("tensor", lambda e, YA=YA, c=c: e.transpose(ptr[:, c, :], YA[:, c * 128:(c + 1) * 128], G.ident_b[:]),
                     reads=[yan, "ident_b"], writes=["n_ptr"], inc=(c == 3))
            P.op("vector", lambda e, YT_=YT_: e.tensor_copy(YT_[:], ptr[:]), reads=["n_ptr"], writes=[ytn])
            P.dma("sync", lambda e, YT_=YT_, p=p: e.dma_start(out=G.YT[:, 0:4, p * 128:(p + 1) * 128], in_=YT_[:]), reads=[ytn], writes=[_u()])
        P.barrier()
        P.emit()


def stage_ml(G):
    P, I, nc = G.P, G.I, G.nc
    NCH = 32
    with contextlib.ExitStack() as st:
        sb, ps = _alloc(G, st)
        TOK = sb("l_TOK", [128, 34, 5, 8], F32)
        EBEND = sb("l_EBEND", [128, 8, 32], F32)
        ATOT = sb("l_ATOT", [128, 8], F32)
        with contextlib.ExitStack() as st1:
            sb1, ps1 = _alloc(G, st1)
            LI = sb1("l_LI", [64, NU], F32)
            SP = sb1("l_SP", [64, NU], F32)
            CL = sb1("l_CL", [64, NU], F32)
            CG = sb1("l_CG", [64, NU], F32)
            TM = sb1("l_TM", [64, NU], F32)
            OQ = [sb1("l_OQ%d" % b, [64, NU], F32) for b in range(2)]
            gbI = sb1("l_gbI", [64, 1], F32)
            gbF = sb1("l_gbF", [64, 1], F32)
            CE = sb1("l_CE", [64, 32], F32)
            ntot = sb1("l_ntot", [64, 2], F32)
            tot = sb1("l_tot", [64, 2], F32)
            SELM = sb1("l_SELM", [64, 8, 128], F32)
            ptr = [ps1("l_ptr%d" % b, [128, 8, 64]) for b in range(2)]
            pe = ps1("l_pe", [128, 8, 32])
            pa = ps1("l_pa", [128, 8, 2])
            own = slice(OWN0, OWN0 + NOWN)
            cxs = slice(CTX0, CTX0 + 256)
            P.dma("sync", lambda e: e.dma_start(out=LI[:], in_=G.GIF[0:64, :]), reads=["dramML"], writes=["l_LI"])
            P.dma("sync", lambda e: e.dma_start(out=SP[:], in_=G.GIF[64:128, :]), reads=["dramML"], writes=["l_SP"])
            P.dma("sync", lambda e: e.dma_start(out=gbI[:], in_=I["gate_b"][0:64, :]), writes=["l_gbI"])
            P.dma("sync", lambda e: e.dma_start(out=gbF[:], in_=I["gate_b"][64:128, :]), writes=["l_gbF"])
            P.dma("sync", lambda e: e.dma_start(out=SELM[:], in_=I["selm"][:, :, :]), writes=["l_SELM"])
            P.op("vector", lambda e: e.tensor_scalar(gbF[:], gbF[:], -1.0, None, ALU.mult), reads=["l_gbF"], writes=["l_gbF"])
            P.op("scalar", lambda e: e.activation(out=LI[:], in_=LI[:], func=AF.Identity, bias=gbI[:]), reads=["l_LI", "l_gbI"], writes=["l_LI"])
            P.op("scalar", lambda e: e.activation(out=SP[:], in_=SP[:], func=AF.Exp, bias=gbF[:], scale=-1.0), reads=["l_SP", "l_gbF"], writes=["l_SP"])
            P.op("scalar", lambda e: e.activation(out=SP[:], in_=SP[:], func=AF.Ln, bias=G.c_one[0:64, :]), reads=["l_SP", "c_one"], writes=["l_SP"])
            P.op("vector", lambda e: e.memset(TM[:], 1.0), writes=["l_TM"])
            P.op("vector", lambda e: e.tensor_tensor_scan(out=CG[:, own], data0=TM[:, own], data1=SP[:, own], initial=0.0, op0=ALU.mult, op1=ALU.add),
                 reads=["l_TM", "l_SP"], writes=["l_CG"])
            P.op("vector", lambda e: e.tensor_tensor_scan(out=CG[:, cxs], data0=TM[:, cxs], data1=SP[:, cxs], initial=0.0, op0=ALU.mult, op1=ALU.add),
                 reads=["l_TM", "l_SP"], writes=["l_CG"])
            TMo = TM[:, own].rearrange("p (c t) -> p c t", t=128)
            P.op("vector", lambda e: e.memset(TMo[:, :, 0:1], 0.0), reads=["l_CG"], writes=["l_TM"])
            P.op("vector", lambda e: e.tensor_tensor_scan(out=CL[:, own], data0=TM[:, own], data1=SP[:, own], initial=0.0, op0=ALU.mult, op1=ALU.add),
                 reads=["l_TM", "l_SP"], writes=["l_CL"])
            CLo = CL[:, own].rearrange("p (c t) -> p c t", t=128)
            SPo = SP[:, own].rearrange("p (c t) -> p c t", t=128)
            P.op("vector", lambda e: e.tensor_copy(CE[:], CLo[:, :, 127]), reads=["l_CL"], writes=["l_CE"])
            P.op("vector", lambda e: e.tensor_copy(tot[:, 0:1], CG[:, OWN0 + NOWN - 1:OWN0 + NOWN]), reads=["l_CG"], writes=["l_tot"])
            P.op("vector", lambda e: e.tensor_copy(tot[:, 1:2], CG[:, CTX0 + 255:CTX0 + 256]), reads=["l_CG"], writes=["l_tot"])
            P.op("vector", lambda e: e.tensor_scalar(ntot[:], tot[:], -1.0, None, ALU.mult), reads=["l_tot"], writes=["l_ntot"])
            for c in range(NCH):
                P.op("vector", lambda e, c=c: e.tensor_scalar(CLo[32:64, c, :], CLo[32:64, c, :], CE[32:64, c:c + 1], -1.0, ALU.subtract, ALU.mult),
                     reads=["l_CL", "l_CE"], writes=["l_CL"])
            P.op("vector", lambda e: e.tensor_tensor(out=CL[32:64, own], in0=CL[32:64, own], in1=SP[32:64, own], op=ALU.add),
                 reads=["l_CL", "l_SP"], writes=["l_CL"])
            for r in range(8):
                P.op("tensor", lambda e, r=r: e.matmul(pe[:, r, :], lhsT=SELM[:, r, :], rhs=CE[:], start=True, stop=True),
                     reads=["l_SELM", "l_CE"], writes=["l_pe"], inc=(r == 7))
            P.op("scalar", lambda e: e.activation(out=EBEND[:], in_=pe[:], func=AF.Exp, scale=-1.0), reads=["l_pe"], writes=["l_EBEND"])
            for r in range(8):
                P.op("tensor", lambda e, r=r: e.matmul(pa[:, r, :], lhsT=SELM[:, r, :], rhs=tot[:], start=True, stop=True),
                     reads=["l_SELM", "l_tot"], writes=["l_pa"], inc=(r == 7))
            P.op("scalar", lambda e: e.activation(out=ATOT[:], in_=pa[:, :, 0], func=AF.Exp, scale=-1.0), reads=["l_pa"], writes=["l_ATOT"])

            tcnt = {"n": 0}

            def transpose_out(Q, qn, qty, chunks):
                for g0 in range(0, len(chunks), 8):
                    grp = chunks[g0:g0 + 8]
                    a = tcnt["n"] % 2
                    tcnt["n"] += 1
                    for gi, (ci, col0) in enumerate(grp):
                        P.op("tensor", lambda e, a=a, gi=gi, col0=col0: e.transpose(ptr[a][:, gi, :], Q[0:64, col0:col0 + 128], G.ident_f[0:64, 0:64]),
                             reads=[qn, "ident_f"], writes=["l_ptr%d" % a], inc=(gi == len(grp) - 1))
                    c_first = grp[0][0]
                    ng_ = len(grp)
                    src = ptr[a][:, 0:ng_, :].rearrange("p g (d x) -> p g d x", d=2)[:, :, :, 0:4]
                    dst = TOK[:, c_first:c_first + ng_, qty, :].rearrange("p g (d x) -> p g d x", d=2)
                    P.op("vector", lambda e, src=src, dst=dst: e.tensor_copy(dst, src), reads=["l_ptr%d" % a], writes=["l_TOK"])

            own_chunks = [(c, OWN0 + c * 128) for c in range(NCH)]
            ctx_chunks = [(32 + c, CTX0 + c * 128) for c in range(2)]
            P.op("scalar", lambda e: e.activation(out=OQ[0][:, own], in_=CL[:, own], func=AF.Exp, scale=-1.0), reads=["l_CL"], writes=["l_OQ0"])
            transpose_out(OQ[0], "l_OQ0", 0, own_chunks)
            P.op("scalar", lambda e: e.activation(out=OQ[1][:, own], in_=CL[:, own], func=AF.Exp), reads=["l_CL"], writes=["l_OQ1"])
            transpose_out(OQ[1], "l_OQ1", 4, own_chunks)
            P.op("vector", lambda e: e.tensor_tensor(out=TM[:, own], in0=LI[:, own], in1=CL[:, own], op=ALU.add), reads=["l_LI", "l_CL"], writes=["l_TM"])
            P.op("scalar", lambda e: e.activation(out=OQ[1][:, own], in_=TM[:, own], func=AF.Exp), reads=["l_TM"], writes=["l_OQ1"])
            transpose_out(OQ[1], "l_OQ1", 1, own_chunks)
            TMo2 = TM[:, own].rearrange("p (c t) -> p c t", t=128)
            for c in range(NCH):
                P.op("vector", lambda e, c=c: e.tensor_scalar(TMo2[:, c, :], TMo2[:, c, :], CE[:, c:c + 1], None, ALU.subtract),
                     reads=["l_TM", "l_CE", "l_OQ1"], writes=["l_TM"])
            P.op("scalar", lambda e: e.activation(out=OQ[0][:, own], in_=TM[:, own], func=AF.Exp), reads=["l_TM"], writes=["l_OQ0"])
            transpose_out(OQ[0], "l_OQ0", 2, own_chunks)
            for (sl, ti_) in ((own, 0), (cxs, 1)):
                P.op("vector", lambda e, sl=sl: e.tensor_tensor(out=TM[0:32, sl], in0=LI[0:32, sl], in1=CG[0:32, sl], op=ALU.add),
                     reads=["l_LI", "l_CG"], writes=["l_TM"])
                P.op("vector", lambda e, sl=sl: e.tensor_tensor(out=TM[32:64, sl], in0=LI[32:64, sl], in1=CG[32:64, sl], op=ALU.subtract),
                     reads=["l_LI", "l_CG"], writes=["l_TM"])
                P.op("vector", lambda e, sl=sl: e.tensor_tensor(out=TM[32:64, sl], in0=TM[32:64, sl], in1=SP[32:64, sl], op=ALU.add),
                     reads=["l_TM", "l_SP"], writes=["l_TM"])
                P.op("scalar", lambda e, sl=sl, ti_=ti_: e.activation(out=OQ[1][0:32, sl], in_=TM[0:32, sl], func=AF.Exp, bias=ntot[0:32, ti_:ti_ + 1]),
                     reads=["l_TM", "l_ntot"], writes=["l_OQ1"])
                P.op("scalar", lambda e, sl=sl: e.activation(out=OQ[1][32:64, sl], in_=TM[32:64, sl], func=AF.Exp), reads=["l_TM"], writes=["l_OQ1"])
            transpose_out(OQ[1], "l_OQ1", 3, own_chunks + ctx_chunks)
            P.barrier()
            P.emit()

        KTOK = sb("l_KTOK", [128, 34, 512], BF16)
        VTOK = sb("l_VTOK", [128, 34, 516], BF16)
        TRI = sb("l_TRI", [128, 2, 128], F32)
        SELV = sb("l_SELV", [128, 16], F32)
        PAY = sb("l_PAY", [128, 8, 130], F32)
        GATH = sb("l_GATH", [128, 4, 1040], F32)
        STATE = sb("l_STATE", [128, 8, 129], F32)
        STB = sb("l_STB", [128, 8, 129], BF16)
        KA = [sb("l_KA%d" % b, [128, 128], BF16) for b in range(3)]
        alpha = sb("l_alpha", [128, 1], F32)
        tmpL = sb("l_tmpL", [128, 129], F32)
        QTc = [[sb("l_QTc%d%d" % (d_, b), [128, 4, 128], BF16) for b in range(2)] for d_ in range(2)]
        KTc = [[sb("l_KTc%d%d" % (d_, b), [128, 4, 128], BF16) for b in range(2)] for d_ in range(2)]
        HS = [[sb("l_HS%d%d" % (d_, b), [128, 512], F32) for b in range(2)] for d_ in range(2)]
        PTt = [sb("l_PT%d" % b, [128, 128], BF16) for b in range(2)]
        pS = [ps("l_pS%d" % b, [128, 128]) for b in range(2)]
        pU = [ps("l_pU%d" % b, [128, 132]) for b in range(4)]
        pN = [ps("l_pN%d" % b, [128, 132]) for b in range(2)]
        pL = [pU[0], pU[1]]
        den = [sb("l_den%d" % b, [128, 8], F32) for b in range(2)]
        P.dma("sync", lambda e: e.dma_start(out=KTOK[:, 0:32, :], in_=G.MLK[OWN0:OWN0 + NOWN, :].rearrange("(c p) n -> p c n", p=128)), reads=["dramML"], writes=["l_KTOK"])
        P.dma("sync", lambda e: e.dma_start(out=KTOK[:, 32:34, :], in_=G.MLK[CTX0:CTX0 + 256, :].rearrange("(c p) n -> p c n", p=128)), reads=["dramML"], writes=["l_KTOK"])
        P.dma("sync", lambda e: e.dma_start(out=VTOK[:, 0:32, :], in_=G.MLV[OWN0:OWN0 + NOWN, :].rearrange("(c p) n -> p c n", p=128)), reads=["dramML"], writes=["l_VTOK"])
        P.dma("sync", lambda e: e.dma_start(out=VTOK[:, 32:34, :], in_=G.MLV[CTX0:CTX0 + 256, :].rearrange("(c p) n -> p c n", p=128)), reads=["dramML"], writes=["l_VTOK"])
        P.dma("sync", lambda e: e.dma_start(out=TRI[:], in_=I["tri"][:, :, :]), writes=["l_TRI"])
        P.dma("sync", lambda e: e.dma_start(out=SELV[:], in_=I["selv"][:, :]), writes=["l_SELV"])
        kacnt = {"n": 0}

        def scaled_k(c, h, qty, r):
            a = kacnt["n"] % 3
            kacnt["n"] += 1
            eng = ("vector", "scalar", "scalar")[a]
            src = KTOK[:, c, h * 128:(h + 1) * 128]
            sc = TOK[:, c, qty, r:r + 1]
            if eng == "scalar":
                P.op("scalar", lambda e: e.activation(out=KA[a][:], in_=src, func=AF.Copy, scale=sc), reads=["l_KTOK", "l_TOK"], writes=["l_KA%d" % a])
            else:
                P.op(eng, lambda e: e.tensor_scalar(KA[a][:], src, sc, None, ALU.mult), reads=["l_KTOK", "l_TOK"], writes=["l_KA%d" % a])
            return KA[a], "l_KA%d" % a

        n2 = 0
        for r in range(8):
            h = r % 4
            for (chs, dstname) in ((list(range(32)), "own"), ([32, 33], "ctx")):
                pp, ppn = pL[n2 % 2], "l_pU%d" % (n2 % 2)
                n2 += 1
                for i_, c in enumerate(chs):
                    ka, kan = scaled_k(c, h, 3, r)
                    P.op("tensor", lambda e, pp=pp, ka=ka, c=c, h=h, i_=i_, L=len(chs): e.matmul(
                        pp[:, 0:129], lhsT=ka[:], rhs=VTOK[:, c, h * 129:(h + 1) * 129], start=(i_ == 0), stop=(i_ == L - 1)),
                        reads=[kan, "l_VTOK"], writes=[ppn], inc=True)
                if dstname == "own":
                    P.op("vector", lambda e, pp=pp, r=r: e.tensor_copy(PAY[:, r, 0:129], pp[:, 0:129]), reads=[ppn], writes=["l_PAY"])
                else:
                    P.op("vector", lambda e, pp=pp, r=r: e.tensor_copy(STATE[:, r, :], pp[:, 0:129]), reads=[ppn], writes=["l_STATE"])
        P.op("vector", lambda e: e.tensor_copy(PAY[:, :, 129], ATOT[:]), reads=["l_ATOT", "l_PAY"], writes=["l_PAY"])
        P.dma("sync", lambda e: e.dma_start(out=G.CCI[:, :], in_=PAY[:].rearrange("p r n -> p (r n)")), reads=["l_PAY"], writes=["CCI"])
        for _rep in range(3):
            P.cc(lambda e: e.collective_compute("AllGather", ALU.bypass, replica_groups=[[0, 1, 2, 3], [4, 5, 6, 7]],
                                                ins=[G.CCI.opt()], outs=[G.CCO.opt()]), reads=["CCI"], writes=["CCO"])
        P.dma("sync", lambda e: e.dma_start(out=GATH[:], in_=G.CCO.rearrange("(j p) n -> p j n", p=128)), reads=["CCO"], writes=["l_GATH"])
        for r in range(8):
            d = r // 4
            order = range(4) if d == 0 else range(3, -1, -1)
            for jseg in order:
                so = 0 if d == 0 else 8
                A_j = GATH[:, jseg, r * 130 + 129:r * 130 + 130]
                L_j = GATH[:, jseg, r * 130:r * 130 + 129]
                P.op("vector", lambda e, A_j=A_j, so=so, jseg=jseg: e.tensor_scalar(
                    alpha[:], A_j, SELV[:, so + jseg:so + jseg + 1], SELV[:, so + 4 + jseg:so + 5 + jseg], ALU.mult, ALU.add),
                    reads=["l_GATH", "l_SELV"], writes=["l_alpha"])
                P.op("vector", lambda e, L_j=L_j, so=so, jseg=jseg: e.tensor_scalar(tmpL[:], L_j, SELV[:, so + jseg:so + jseg + 1], None, ALU.mult),
                     reads=["l_GATH", "l_SELV"], writes=["l_tmpL"])
                P.op("vector", lambda e, r=r: e.scalar_tensor_tensor(out=STATE[:, r, :], in0=STATE[:, r, :], scalar=alpha[:], in1=tmpL[:], op0=ALU.mult, op1=ALU.add),
                     reads=["l_STATE", "l_alpha", "l_tmpL"], writes=["l_STATE"])
        P.op("scalar", lambda e: e.activation(out=STB[:], in_=STATE[:], func=AF.Copy), reads=["l_STATE"], writes=["l_STB"])

        def ld4(i):
            if i >= NCH:
                return
            bb = i % 2
            for d_ in range(2):
                c_ = i if d_ == 0 else NCH - 1 - i
                tk0 = OWN0 + c_ * 128
                P.dma("sync", lambda e, bb=bb, d_=d_, tk0=tk0: e.dma_start(out=QTc[d_][bb][:], in_=G.MLQT[:, :, tk0:tk0 + 128]), writes=["l_QTc%d%d" % (d_, bb)])
                P.dma("sync", lambda e, bb=bb, d_=d_, tk0=tk0: e.dma_start(out=KTc[d_][bb][:], in_=G.MLKT[:, :, tk0:tk0 + 128]), writes=["l_KTc%d%d" % (d_, bb)])
        ld4(0)
        for i in range(NCH):
            b = i % 2
            ld4(i + 1)
            cs = (i, NCH - 1 - i)
            for hh_ in range(2):
                items = [(h, d) for h in (2 * hh_, 2 * hh_ + 1) for d in range(2)]
                for (h, d) in items:
                    c = cs[d]
                    r = d * 4 + h
                    u = (h % 2) * 2 + d
                    qn, kn = "l_QTc%d%d" % (d, b), "l_KTc%d%d" % (d, b)
                    P.op("tensor", lambda e, b=b, h=h, d=d: e.matmul(pS[d][:], lhsT=KTc[d][b][:, h, :], rhs=QTc[d][b][:, h, :], start=True, stop=True),
                         reads=[kn, qn], writes=["l_pS%d" % d], inc=True)
                    P.op("vector", lambda e, c=c, r=r, d=d: e.scalar_tensor_tensor(
                        out=PTt[d][:], in0=pS[d][:], scalar=TOK[:, c, 1, r:r + 1], in1=TRI[:, d, :], op0=ALU.mult, op1=ALU.mult),
                        reads=["l_pS%d" % d, "l_TOK", "l_TRI"], writes=["l_PT%d" % d])
                    P.op("tensor", lambda e, c=c, h=h, d=d, u=u: e.matmul(pU[u][:, 0:129], lhsT=PTt[d][:], rhs=VTOK[:, c, h * 129:(h + 1) * 129], start=True, stop=False),
                         reads=["l_PT%d" % d, "l_VTOK"], writes=["l_pU%d" % u], inc=False)
                    P.op("tensor", lambda e, b=b, h=h, d=d, r=r, u=u: e.matmul(pU[u][:, 0:129], lhsT=QTc[d][b][:, h, :], rhs=STB[:, r, :], start=False, stop=True),
                         reads=[qn, "l_STB%d" % r, "l_STB"], writes=["l_pU%d" % u], inc=True)
                for (h, d) in items:
                    c = cs[d]
                    r = d * 4 + h
                    a = r % 2
                    ka, kan = scaled_k(c, h, 2, r)
                    P.op("tensor", lambda e, a=a, ka=ka, c=c, h=h: e.matmul(pN[a][:, 0:129], lhsT=ka[:], rhs=VTOK[:, c, h * 129:(h + 1) * 129], start=True, stop=True),
                         reads=[kan, "l_VTOK"], writes=["l_pN%d" % a], inc=True)
                    P.op("vector", lambda e, a=a, r=r, c=c: e.scalar_tensor_tensor(
                        out=STATE[:, r, :], in0=STATE[:, r, :], scalar=EBEND[:, r, c:c + 1], in1=pN[a][:, 0:129], op0=ALU.mult, op1=ALU.add),
                        reads=["l_STATE%d" % r, "l_EBEND", "l_pN%d" % a, "l_STATE"], writes=["l_STATE%d" % r])
                    P.op("scalar", lambda e, r=r: e.activation(out=STB[:, r, :], in_=STATE[:, r, :], func=AF.Copy),
                         reads=["l_STATE%d" % r], writes=["l_STB%d" % r])
                for (h, d) in items:
                    u = (h % 2) * 2 + d
                    dn, dnn = den[d], "l_den%d" % d
                    P.op("scalar", lambda e, dn=dn, u=u, h=h: e.activation(out=dn[:, h:h + 1], in_=pU[u][:, 128:129], func=AF.Abs),
                         reads=["l_pU%d" % u], writes=[dnn])
                for d in range(2):
                    c = cs[d]
                    dn, dnn = den[d], "l_den%d" % d
                    h0 = 2 * hh_
                    REB2 = TOK[:, c, 4, d * 4 + h0:d * 4 + h0 + 2]
                    P.op("vector", lambda e, dn=dn, REB2=REB2, h0=h0: e.tensor_tensor(out=dn[:, h0:h0 + 2], in0=dn[:, h0:h0 + 2], in1=REB2, op=ALU.max),
                         reads=[dnn, "l_TOK"], writes=[dnn])
                    P.op("vector", lambda e, dn=dn, h0=h0: e.reciprocal(dn[:, h0:h0 + 2], dn[:, h0:h0 + 2]), reads=[dnn], writes=[dnn])
                for (h, d) in items:
                    u = (h % 2) * 2 + d
                    dn, dnn = den[d], "l_den%d" % d
                    hsn = "l_HS%d%d" % (d, b)
                    if d == 0:
                        P.op("scalar", lambda e, b=b, h=h, dn=dn, u=u: e.activation(out=HS[0][b][:, h * 128:(h + 1) * 128], in_=pU[u][:, 0:128], func=AF.Copy, scale=dn[:, h:h + 1]),
                             reads=["l_pU%d" % u, dnn], writes=[hsn])
                    else:
                        P.op("vector", lambda e, b=b, h=h, dn=dn, u=u: e.tensor_scalar(HS[1][b][:, h * 128:(h + 1) * 128], pU[u][:, 0:128], dn[:, h:h + 1], None, ALU.mult),
                             reads=["l_pU%d" % u, dnn], writes=[hsn])
            P.dma("sync", lambda e, b=b, c=cs[0]: e.dma_start(out=G.HF[c * 128:(c + 1) * 128, :], in_=HS[0][b][:]), reads=["l_HS0%d" % b], writes=[_u()])
            P.dma("sync", lambda e, b=b, c=cs[1]: e.dma_start(out=G.HB[c * 128:(c + 1) * 128, :], in_=HS[1][b][:]), reads=["l_HS1%d" % b], writes=[_u()])
        P.barrier()
        P.emit()

    with contextlib.ExitStack() as st:
        sb, ps = _alloc(G, st)
        NB = 3
        hf = [sb("r_hf%d" % b, [128, 512], F32) for b in range(NB)]
        hb = [sb("r_hb%d" % b, [128, 512], F32) for b in range(NB)]
        g2 = [sb("r_g2%d" % b, [128, 512], F32) for b in range(NB)]
        junk = sb("r_junk", [128, 128], F32)
        ssq = [sb("r_ssq%d" % b, [128, 4], F32) for b in range(2)]
        Yb = [sb("r_Y%d" % b, [128, 512], BF16) for b in range(2)]
        ytb = [sb("r_yt%d" % b, [128, 4, 128], BF16) for b in range(2)]
        ptr2 = [ps("r_ptr%d" % b, [128, 4, 128], BF16) for b in range(2)]

        def ld5(c):
            if c >= NCH:
                return
            b3 = c % NB
            tk0 = OWN0 + c * 128
            P.dma("sync", lambda e, b3=b3, c=c: e.dma_start(out=hf[b3][:], in_=G.HF[c * 128:(c + 1) * 128, :]), writes=["r_hf%d" % b3])
            P.dma("sync", lambda e, b3=b3, c=c: e.dma_start(out=hb[b3][:], in_=G.HB[c * 128:(c + 1) * 128, :]), writes=["r_hb%d" % b3])
            P.dma("sync", lambda e, b3=b3, tk0=tk0: e.dma_start(out=g2[b3][:], in_=G.MLG2[tk0:tk0 + 128, :]), writes=["r_g2%d" % b3])
        ld5(0)
        ld5(1)
        for c in range(NCH):
            ld5(c + 2)
            b3, b = c % NB, c % 2
            P.op("vector", lambda e, b3=b3: e.tensor_tensor(out=hf[b3][:], in0=hf[b3][:], in1=hb[b3][:], op=ALU.add),
                 reads=["r_hf%d" % b3, "r_hb%d" % b3], writes=["r_hf%d" % b3])
            for h in range(4):
                P.op("scalar", lambda e, b3=b3, b=b, h=h: e.activation(out=junk[:], in_=hf[b3][:, h * 128:(h + 1) * 128], func=AF.Square, accum_out=ssq[b][:, h:h + 1]),
                     reads=["r_hf%d" % b3], writes=["r_junk", "r_ssq%d" % b])
            P.op("scalar", lambda e, b=b: e.activation(out=ssq[b][:], in_=ssq[b][:], func=AF.Sqrt, bias=G.c_eps[:], scale=1.0 / 128.0),
                 reads=["r_ssq%d" % b, "c_eps"], writes=["r_ssq%d" % b])
            P.op("vector", lambda e, b=b: e.reciprocal(ssq[b][:], ssq[b][:]), reads=["r_ssq%d" % b], writes=["r_ssq%d" % b])
            for h in range(4):
                P.op("vector", lambda e, b3=b3, b=b, h=h: e.scalar_tensor_tensor(
                    out=Yb[b][:, h * 128:(h + 1) * 128], in0=hf[b3][:, h * 128:(h + 1) * 128], scalar=ssq[b][:, h:h + 1],
                    in1=g2[b3][:, h * 128:(h + 1) * 128], op0=ALU.mult, op1=ALU.mult),
                    reads=["r_hf%d" % b3, "r_ssq%d" % b, "r_g2%d" % b3], writes=["r_Y%d" % b])
            for h in range(4):
                P.op("tensor", lambda e, b=b, h=h: e.transpose(ptr2[b][:, h, :], Yb[b][:, h * 128:(h + 1) * 128], G.ident_b[:]),
                     reads=["r_Y%d" % b, "ident_b"], writes=["r_ptr%d" % b], inc=(h == 3))
            P.op("scalar", lambda e, b=b: e.activation(out=ytb[b][:], in_=ptr2[b][:], func=AF.Copy), reads=["r_ptr%d" % b], writes=["r_yt%d" % b])
            P.dma("sync", lambda e, b=b, c=c: e.dma_start(out=G.YT[:, 4:8, c * 128:(c + 1) * 128], in_=ytb[b][:]), reads=["r_yt%d" % b], writes=[_u()])
        P.barrier()
        P.emit()


def stage_outproj(G):
    P, I, nc = G.P, G.I, G.nc
    l, j, m = 0, 1, 0
    with contextlib.ExitStack() as st:
        sb, ps = _alloc(G, st)
        wo = sb("o_wo", [128, 8, 1024], BF16)
        xt = [sb("o_xt%d" % b, [128, 8, TW], F32) for b in range(2)]
        yt = [sb("o_yt%d" % b, [128, 8, TW], BF16) for b in range(2)]
        po = [ps("o_po%d" % b, [128, TW]) for b in range(2)]
        load_w_cast(G, wo, "o_wo", I["mix_w_out"], 8, 1024)
        def ld(n):
            if n >= 16:
                return
            ti_, b_ = n + 1, n % 2
            P.dma("sync", lambda e, b_=b_, ti_=ti_: e.dma_start(out=xt[b_][:], in_=G.XT[:, :, ti_ * TW:(ti_ + 1) * TW]), reads=["XT%d" % ti_], writes=["o_xt%d" % b_])
            P.dma("sync", lambda e, b_=b_, n=n: e.dma_start(out=yt[b_][:], in_=G.YT[:, :, n * TW:(n + 1) * TW]), reads=["dramYT"], writes=["o_yt%d" % b_])
        ld(0)
        ld(1)
        for n in range(16):
            ti = n + 1
            t0 = ti * TW
            b = n % 2
            for f in range(8):
                q = f % 2
                for k in range(8):
                    P.op("tensor", lambda e, q=q, f=f, k=k, b=b: e.matmul(po[q][:], lhsT=wo[:, k, f * 128:(f + 1) * 128], rhs=yt[b][:, k, :], start=(k == 0), stop=(k == 7)),
                         reads=["o_wo", "o_yt%d" % b], writes=["o_po%d" % q], inc=(k == 7))
                gsc = G.GT[:, l, j, m, f:f + 1]
                P.op("vector", lambda e, q=q, f=f, b=b, gsc=gsc: e.scalar_tensor_tensor(
                    out=xt[b][:, f, :], in0=po[q][:], scalar=gsc, in1=xt[b][:, f, :], op0=ALU.mult, op1=ALU.add),
                    reads=["o_po%d" % q, "modc", "o_xt%d" % b], writes=["o_xt%d" % b])
            P.dma("sync", lambda e, b=b, t0=t0: e.dma_start(out=G.XT[:, :, t0:t0 + TW], in_=xt[b][:]), reads=["o_xt%d" % b], writes=["XT%d" % ti])
            ld(n + 2)
        P.barrier()
        P.emit()


def stage_sg(G):
    P, I, nc = G.P, G.I, G.nc
    l, j, m = 1, 1, 0
    with contextlib.ExitStack() as st:
        sb, ps = _alloc(G, st)
        wu = sb("g_wu", [128, 8, 2048], BF16)
        wv = sb("g_wv", [128, 8, 2048], BF16)
        wo = sb("g_wo", [128, 16, 1024], BF16)
        wsT = sb("g_wsT", [128, 8, 128], BF16)
        bs = sb("g_bs", [1, 1024], BF16)
        ones1 = sb("g_ones1", [1, 256], BF16)
        lng = sb("g_lng", [128, 2048], F32)
        lnb = sb("g_lnb", [128, 2048], F32)
        xt = [sb("g_xt%d" % b, [128, 8, TW], F32) for b in range(2)]
        hh = [sb("g_h%d" % b, [128, 8, TW], BF16) for b in range(2)]
        NS = NormScratch(G, sb, ps, "g_")
        NS.sq8 = [sb("g_sq8_%d" % k, [128, TW], BF16) for k in range(8)]
        uT = sb("g_uT", [128, 16, TW], BF16)
        vraw = [sb("g_vraw%d" % b, [128, 2048], F32) for b in range(2)]
        vn = [sb("g_vn%d" % b, [128, 2048], BF16) for b in range(2)]
        gated = sb("g_gated", [128, 16, TW], BF16)
        stats = [sb("g_stats%d" % b, [128, 4, 6], F32) for b in range(2)]
        mv = [sb("g_mv%d" % b, [128, 4], F32) for b in range(2)]
        pu = [ps("g_pu%d" % b, [128, TW]) for b in range(2)]
        pm = [ps("g_pm%d" % b, [128, 4, 128]) for b in range(2)]
        po1 = ps("g_po", [128, TW])
        po = [po1, po1]
        pv = [ps("g_pv%d" % b, [128, 512]) for b in range(2)]
        load_w_cast(G, wu, "g_wu", I["sg_w_in"][:, 0:2048], 8, 2048)
        load_w_cast(G, wv, "g_wv", I["sg_w_in"][:, 2048:4096], 8, 2048)
        load_w_cast(G, wo, "g_wo", I["sg_w_out"], 16, 1024)
        P.dma("gpsimd", lambda e: e.dma_start(out=wsT[:], in_=I["sg_w_sT"][:, :, :]), writes=["g_wsT"])
        P.dma("gpsimd", lambda e: e.dma_start(out=bs[:], in_=I["sg_b_s"][:, :]), writes=["g_bs"])
        P.op("vector", lambda e: e.memset(ones1[:], 1.0), writes=["g_ones1"])
        P.dma("sync", lambda e: e.dma_start(out=lng[:], in_=I["sg_lng_bc"][:, :]), writes=["g_lng"])
        P.dma("sync", lambda e: e.dma_start(out=lnb[:], in_=I["sg_lnb_bc"][:, :]), writes=["g_lnb"])
        tiles = list(range(1, 17))

        def ld(n):
            ti_ = tiles[n]
            xb = xt[n % 2]
            P.dma("sync", lambda e, xb=xb, ti_=ti_: e.dma_start(out=xb[:], in_=G.XT[:, :, ti_ * TW:(ti_ + 1) * TW]),
                  reads=["XT%d" % ti_], writes=["g_xt%d" % (n % 2)])
        def do_norm(n, phase=None):
            bb = n % 2
            norm_mod(G, NS, xt[bb], "g_xt%d" % bb, G.GM[:, l, j, m, :], G.SH[:, l, j, m, :], hh[bb], "g_h%d" % bb, phase=phase)
        ld(0)
        ld(1)
        do_norm(0)
        vcnt = 0
        for n, ti in enumerate(tiles):
            t0 = ti * TW
            b = n % 2
            x, xn = xt[b], "g_xt%d" % b
            h, hn = hh[b], "g_h%d" % b
            NSB = TW // 128
            for s_ in range(NSB):
                VR = vraw[s_]
                for blk in range(4):
                    q = blk % 2
                    for k in range(8):
                        P.op("tensor", lambda e, q=q, blk=blk, k=k, h=h, s_=s_: e.matmul(
                            pv[q][:], lhsT=h[:, k, s_ * 128:(s_ + 1) * 128], rhs=wv[:, k, blk * 512:(blk + 1) * 512], start=(k == 0), stop=(k == 7)),
                            reads=["g_wv", hn], writes=["g_pv%d" % q], inc=(k == 7))
                    P.op("scalar", lambda e, q=q, blk=blk, VR=VR: e.activation(out=VR[:, blk * 512:(blk + 1) * 512], in_=pv[q][:], func=AF.Gelu_apprx_tanh),
                         reads=["g_pv%d" % q], writes=["g_vraw%d" % s_])
                    P.op("vector", lambda e, blk=blk, VR=VR, s_=s_: e.bn_stats(stats[s_][:, blk, :], VR[:, blk * 512:(blk + 1) * 512]),
                         reads=["g_vraw%d" % s_], writes=["g_stats%d" % s_])
                P.op("vector", lambda e, s_=s_: e.bn_aggr(mv[s_][:, 0:2], stats[s_][:].rearrange("p a b -> p (a b)")), reads=["g_stats%d" % s_], writes=["g_mv%d" % s_])
                P.op("scalar", lambda e, s_=s_: e.activation(out=mv[s_][:, 2:3], in_=mv[s_][:, 1:2], func=AF.Sqrt, bias=G.c_eps[:], scale=1.0),
                     reads=["g_mv%d" % s_, "c_eps"], writes=["g_mv%d" % s_])
                P.op("vector", lambda e, s_=s_: e.reciprocal(mv[s_][:, 2:3], mv[s_][:, 2:3]), reads=["g_mv%d" % s_], writes=["g_mv%d" % s_])
                P.op("vector", lambda e, s_=s_: e.tensor_scalar(mv[s_][:, 3:4], mv[s_][:, 0:1], mv[s_][:, 2:3], -1.0, ALU.mult, ALU.mult),
                     reads=["g_mv%d" % s_], writes=["g_mv%d" % s_])
            for fc in range(16):
                q = fc % 2
                if fc == 6:
                    vns = []
                    for s_ in range(NSB):
                        VR = vraw[s_]
                        VN, vnn = vn[vcnt % 2], "g_vn%d" % (vcnt % 2)
                        vcnt += 1
                        vns.append((VN, vnn))
                        P.op("scalar", lambda e, VR=VR, s_=s_: e.activation(out=VR[:], in_=VR[:], func=AF.Identity, bias=mv[s_][:, 3:4], scale=mv[s_][:, 2:3]),
                             reads=["g_vraw%d" % s_, "g_mv%d" % s_], writes=["g_vraw%d" % s_])
                        P.op("vector", lambda e, VR=VR: e.tensor_tensor(out=VR[:], in0=VR[:], in1=lng[:], op=ALU.mult),
                             reads=["g_vraw%d" % s_, "g_lng"], writes=["g_vraw%d" % s_])
                        P.op("vector", lambda e, VR=VR, VN=VN: e.tensor_tensor(out=VN[:], in0=VR[:], in1=lnb[:], op=ALU.add),
                             reads=["g_vraw%d" % s_, "g_lnb"], writes=[vnn])
                for k in range(8):
                    P.op("tensor", lambda e, q=q, fc=fc, k=k, h=h: e.matmul(pu[q][:], lhsT=wu[:, k, fc * 128:(fc + 1) * 128], rhs=h[:, k, :], start=(k == 0), stop=(k == 7)),
                         reads=["g_wu", hn], writes=["g_pu%d" % q], inc=(k == 7))
                P.op("scalar", lambda e, q=q, fc=fc: e.activation(out=uT[:, fc, :], in_=pu[q][:], func=AF.Gelu_apprx_tanh), reads=["g_pu%d" % q], writes=["g_uT%d" % fc])
            for s_ in range(NSB):
                VN, vnn = vns[s_]
                for g4 in range(4):
                    q = g4 % 2
                    for f4 in range(4):
                        fc = g4 * 4 + f4
                        g_ = fc // 2
                        P.op("tensor", lambda e, q=q, fc=fc, f4=f4, g_=g_, VN=VN: e.matmul(
                            pm[q][:, f4, :], lhsT=VN[:, fc * 128:(fc + 1) * 128], rhs=wsT[:, g_, :], start=True, stop=False),
                            reads=[vnn, "g_wsT"], writes=["g_pm%d" % q], inc=False)
                        P.op("tensor", lambda e, q=q, f4=f4, g_=g_: e.matmul(
                            pm[q][:, f4, :], lhsT=ones1[0:1, 0:128], rhs=bs[0:1, g_ * 128:(g_ + 1) * 128], start=False, stop=True),
                            reads=["g_ones1", "g_bs"], writes=["g_pm%d" % q], inc=(f4 == 3))
                    P.op("vector", lambda e, q=q, g4=g4, s_=s_: e.tensor_tensor(
                        out=gated[:, g4 * 4:(g4 + 1) * 4, s_ * 128:(s_ + 1) * 128], in0=uT[:, g4 * 4:(g4 + 1) * 4, s_ * 128:(s_ + 1) * 128], in1=pm[q][:], op=ALU.mult),
                        reads=["g_uT%d" % fc_ for fc_ in range(g4 * 4, g4 * 4 + 4)] + ["g_pm%d" % q],
                        writes=["g_gated%d" % fc_ for fc_ in range(g4 * 4, g4 * 4 + 4)])
            if n + 1 < len(tiles):
                do_norm(n + 1, "A")
            for f in range(8):
                q = f % 2
                if f == 3 and n + 1 < len(tiles):
                    do_norm(n + 1, "B")
                for fc in range(16):
                    P.op("tensor", lambda e, q=q, f=f, fc=fc: e.matmul(po[q][:], lhsT=wo[:, fc, f * 128:(f + 1) * 128], rhs=gated[:, fc, :], start=(fc == 0), stop=(fc == 15)),
                         reads=["g_wo", "g_gated%d" % fc], writes=["g_po"], inc=(fc == 15))
                gsc = G.GT[:, l, j, m, f:f + 1]
                P.op("vector", lambda e, q=q, f=f, x=x, gsc=gsc: e.scalar_tensor_tensor(
                    out=x[:, f, :], in0=po[q][:], scalar=gsc, in1=x[:, f, :], op0=ALU.mult, op1=ALU.add),
                    reads=["g_po", "modc", xn], writes=[xn])
            P.dma("sync", lambda e, x=x, t0=t0: e.dma_start(out=G.XT[:, :, t0:t0 + TW], in_=x[:]), reads=[xn], writes=["XT%d" % ti])
            if n + 2 < len(tiles):
                ld(n + 2)
        P.barrier()
        P.emit()


_NC_CACHE = {}


def kernel(**inputs):
    maps = _host_prep(inputs)
    if "nc" not in _NC_CACHE:
        _NC_CACHE["nc"] = build()
    nc = _NC_CACHE["nc"]
    res = run_bass_kernel_spmd(nc, maps, core_ids=list(range(8)))
    out = np.zeros((2, 16384, 1024), np.float32)
    for core in range(8):
        b, s = core // 4, core % 4
        out[b, s * 4096:(s + 1) * 4096, :] = res.results[core]["out"]
    return out
```

```python
import contextlib
import numpy as np
import concourse.bass as bass
import concourse.mybir as mybir
from concourse.bass_utils import run_bass_kernel_spmd

F32 = mybir.dt.float32
BF16 = mybir.dt.bfloat16
AF = mybir.ActivationFunctionType
ALU = mybir.AluOpType
AX = mybir.AxisListType

D = 1024
DFF = 2816
NE = 4608
NU = 4864
OWN0 = 256
NOWN = 4096
CTX0 = 4608
TW = 256
EPS = 1e-6
NEG = -30000.0

ENGS = ("tensor", "vector", "scalar", "gpsimd", "sync")
NDMASEM = 32
NHW = 24


class Buf:
    __slots__ = ("name", "writers", "readers")

    def __init__(self, name):
        self.name = name
        self.writers = []
        self.readers = []


class Prog:
    def __init__(self, nc, st):
        self.nc = nc
        self.q = {e: [] for e in ENGS}
        self.cnt = {e: 0 for e in ENGS}
        self.seen = {e: {} for e in ENGS}
        self.dcnt = [0] * NDMASEM
        self.dnext = 0
        self.dnext_sw = 0
        self.bufs = {}
        self.esem = {e: st.enter_context(nc.semaphore("s_" + e)) for e in ENGS}
        self.dsem = [st.enter_context(nc.semaphore("d%d" % i)) for i in range(NDMASEM)]
        self.lastinc = {e: True for e in ENGS}

    def buf(self, name):
        b = self.bufs.get(name)
        if b is None:
            b = self.bufs[name] = Buf(name)
        return b

    def _bl(self, lst):
        return [self.buf(b) if isinstance(b, str) else b for b in lst]

    def _deps(self, reads, writes):
        deps = {}
        for b in reads:
            for k, v in b.writers:
                if deps.get(k, 0) < v:
                    deps[k] = v
        for b in writes:
            for k, v in b.writers:
                if deps.get(k, 0) < v:
                    deps[k] = v
            for k, v in b.readers:
                if deps.get(k, 0) < v:
                    deps[k] = v
        return deps

    def _waits(self, eng, deps):
        seen = self.seen[eng]
        waits = []
        for k, v in deps.items():
            if k == "tensor" and eng == "tensor":
                continue
            if seen.get(k, 0) >= v:
                continue
            seen[k] = v
            waits.append((k, v))
        return waits

    def _record(self, ev, reads, writes):
        k = ev[0]
        for b in writes:
            b.writers = [ev]
            b.readers = []
        for b in reads:
            if b in writes:
                continue
            b.readers = [e for e in b.readers if e[0] != k] + [ev]

    def op(self, eng, fn, reads=(), writes=(), inc=True):
        reads = self._bl(reads)
        writes = self._bl(writes)
        waits = self._waits(eng, self._deps(reads, writes))
        if inc:
            self.cnt[eng] += 1
            ev = (eng, self.cnt[eng])
        else:
            ev = (eng, self.cnt[eng] + 1)
        self.lastinc[eng] = inc
        self.q[eng].append(("op", fn, waits, inc))
        self._record(ev, reads, writes)

    def dma(self, eng, fn, reads=(), writes=()):
        reads = self._bl(reads)
        writes = self._bl(writes)
        deps = self._deps(reads, writes)
        if eng == "gpsimd":
            i = NHW + self.dnext_sw
            self.dnext_sw = (self.dnext_sw + 1) % (NDMASEM - NHW)
        else:
            i = self.dnext
            self.dnext = (self.dnext + 1) % NHW
        key = ("d", i)
        if self.dcnt[i] > 0 and deps.get(key, 0) < self.dcnt[i]:
            deps[key] = self.dcnt[i]
        waits = self._waits(eng, deps)
        self.dcnt[i] += 16
        ev = (key, self.dcnt[i])
        self.q[eng].append(("dma", fn, waits, i))
        self._record(ev, reads, writes)

    def cc(self, fn, reads=(), writes=()):
        self.op("gpsimd", fn, reads=reads, writes=writes, inc=True)

    def barrier(self):
        for e in ENGS:
            assert self.lastinc[e], e
        deps = {e: self.cnt[e] for e in ENGS if self.cnt[e] > 0}
        for i in range(NDMASEM):
            if self.dcnt[i] > 0:
                deps[("d", i)] = self.dcnt[i]
        for e in ENGS:
            d = {k: v for k, v in deps.items() if k != e}
            waits = self._waits(e, d)
            self.q[e].append(("wait", None, waits, None))

    def emit(self):
        nc = self.nc
        esem, dsem = self.esem, self.dsem

        def semof(k):
            return esem[k] if isinstance(k, str) else dsem[k[1]]

        def run(engname):
            items = self.q[engname]

            def body(e):
                for kind, fn, waits, x in items:
                    for k, v in waits:
                        e.wait_ge(semof(k), v)
                    if kind == "op":
                        ins = fn(e)
                        if x:
                            ins.then_inc(esem[engname], 1)
                    elif kind == "dma":
                        fn(e).then_inc(dsem[x], 16)
            return body

        with nc.Block() as block:
            block.tensor(run("tensor"))
            block.vector(run("vector"))
            block.scalar(run("scalar"))
            block.gpsimd(run("gpsimd"))
            block.sync(run("sync"))
        self.q = {e: [] for e in ENGS}


def _rowmap(s):
    rm = np.zeros(72, np.int64)
    rm[4:68] = s * 64 + np.arange(64)
    rm[0:4] = (s * 64 - 4 + np.arange(4)) if s > 0 else np.array([5, 6, 7, 8])
    rm[68:72] = (s * 64 + 64 + np.arange(4)) if s < 3 else np.array([248, 249, 250, 251])
    return rm


def _na_bias_tables(rpb, s):
    rm = _rowmap(s)
    out = np.full((5, 8, 128, 576), NEG, np.float32)
    reps = [0, 1, 2, 30, 31]
    cols = np.arange(64)
    c0 = np.clip(cols - 8, 0, 64 - 16)
    for ci, p in enumerate(reps):
        for r in range(2):
            i = s * 64 + 2 * p + r
            r0 = min(max(i - 4, 0), 256 - 8)
            seen_rows = set()
            for j in range(9):
                krow = int(rm[2 * p + j])
                if krow < r0 or krow >= r0 + 8 or krow in seen_rows:
                    continue
                seen_rows.add(krow)
                rr = krow - i + 7
                for qc in range(64):
                    kc = np.arange(c0[qc], c0[qc] + 16)
                    out[ci][:, r * 64 + qc, j * 64 + kc] = rpb[:, rr, kc - qc + 15]
    return out


def _rope_tables(s):
    rm = _rowmap(s)
    row = np.repeat(rm, 64).astype(np.float32)
    col = np.tile(np.arange(64), 72).astype(np.float32)
    inv = (10000.0 ** (-np.arange(32, dtype=np.float32) / 32)).astype(np.float32)
    ang = np.concatenate([row[:, None] * inv, col[:, None] * inv], axis=-1).astype(np.float32)
    cos = np.cos(ang).astype(np.float32)
    sin = np.sin(ang).astype(np.float32)
    cos2 = np.repeat(cos, 2, axis=1)
    sin2 = np.repeat(sin, 2, axis=1)
    sin2[:, 0::2] *= -1.0
    cos_u = np.ones((NU, 128), np.float32)
    sin_u = np.zeros((NU, 128), np.float32)
    cos_u[:NE] = cos2
    sin_u[:NE] = sin2
    return np.tile(cos_u, (1, 4)), np.tile(sin_u, (1, 4))


def _host_prep(inp):
    x = np.asarray(inp["x"], np.float32)
    shared = {}
    shared["w_mod"] = np.ascontiguousarray(inp["w_mod"], np.float32)
    shared["b_modT"] = np.ascontiguousarray(np.asarray(inp["b_mod"], np.float32).reshape(2, 72, 128).transpose(2, 0, 1))
    shared["norm_gT"] = np.ascontiguousarray(np.asarray(inp["norm_g"], np.float32).reshape(2, 3, 8, 128).transpose(3, 0, 1, 2))
    shared["ffn_w_in"] = np.ascontiguousarray(inp["ffn_w_in"], np.float32)
    shared["ffn_w_out"] = np.ascontiguousarray(inp["ffn_w_out"], np.float32)
    mw = np.asarray(inp["mix_w_in"], np.float32)[0]
    shared["mix_w_in"] = np.ascontiguousarray(mw[:, :3584])
    wg = np.zeros((1024, 128), np.float32)
    gb = np.zeros((128, 1), np.float32)
    gate_b = np.asarray(inp["ml_gate_b"], np.float32)[0]
    for h in range(4):
        for d in range(2):
            for q in range(2):
                wg[:, q * 64 + d * 32 + h] = mw[:, 3584 + h * 4 + d * 2 + q]
                gb[q * 64 + d * 32 + h, 0] = gate_b[h, d, q]
    shared["w_gate"] = wg
    shared["gate_b"] = gb
    shared["headg_bc"] = np.ascontiguousarray(np.broadcast_to(np.asarray(inp["ml_head_g"], np.float32)[0].reshape(1, 512), (128, 512)))
    shared["mix_w_out"] = np.ascontiguousarray(np.asarray(inp["mix_w_out"], np.float32)[0])
    shared["sg_w_in"] = np.ascontiguousarray(np.asarray(inp["sg_w_in"], np.float32)[0])
    shared["sg_w_out"] = np.ascontiguousarray(np.asarray(inp["sg_w_out"], np.float32)[0])
    shared["sg_lng_bc"] = np.ascontiguousarray(np.broadcast_to(np.asarray(inp["sg_ln_g"], np.float32)[0].reshape(1, 2048), (128, 2048)))
    shared["sg_lnb_bc"] = np.ascontiguousarray(np.broadcast_to(np.asarray(inp["sg_ln_b"], np.float32)[0].reshape(1, 2048), (128, 2048)))
    shared["sg_w_sT"] = np.ascontiguousarray(np.asarray(inp["sg_w_s"], np.float32)[0].transpose(2, 0, 1))
    shared["sg_b_s"] = np.ascontiguousarray(np.asarray(inp["sg_b_s"], np.float32)[0].reshape(1, 1024))
    shared["final_g_bc"] = np.ascontiguousarray(np.broadcast_to(np.asarray(inp["final_g"], np.float32).reshape(1, 1024), (128, 1024)))
    shared["ident"] = np.eye(128, dtype=np.float32)
    tri = np.zeros((128, 2, 128), np.float32)
    ss, tt = np.meshgrid(np.arange(128), np.arange(128), indexing="ij")
    tri[:, 0, :] = (tt >= ss)
    tri[:, 1, :] = (tt <= ss)
    shared["tri"] = tri
    sel = np.zeros((64, 8, 128), np.float32)
    for d in range(2):
        for h in range(4):
            sel[d * 32 + h, d * 4 + h, :] = 1.0
    shared["selm"] = sel
    rpb = np.asarray(inp["na_rpb"], np.float32)[0]
    c = np.asarray(inp["c"], np.float32)
    cctx = np.asarray(inp["c_ctx"], np.float32)
    ctx = np.asarray(inp["ctx"], np.float32)
    maps = []
    for core in range(8):
        b, s = core // 4, core % 4
        rm = _rowmap(s)
        tok = (rm[:, None] * 64 + np.arange(64)[None, :]).reshape(-1)
        xin = np.concatenate([x[b][tok], ctx[b]], axis=0)
        cT = np.stack([c[b].reshape(8, 128).T, cctx.reshape(8, 128).T], axis=-1)
        cos4, sin4 = _rope_tables(s)
        selv = np.zeros((128, 16), np.float32)
        for j in range(4):
            selv[:, j] = 1.0 if j < s else 0.0
            selv[:, 4 + j] = 1.0 - selv[:, j]
            selv[:, 8 + j] = 1.0 if j > s else 0.0
            selv[:, 12 + j] = 1.0 - selv[:, 8 + j]
        m = dict(shared)
        m["xin"] = np.ascontiguousarray(xin)
        m["cT"] = np.ascontiguousarray(cT.astype(np.float32))
        m["ropecos"] = cos4
        m["ropesin"] = sin4
        nb = _na_bias_tables(rpb, s)
        nbp = np.full((5, 8, 128, 640), NEG, np.float32)
        nbp[..., :576] = nb
        m["nabiasT"] = np.ascontiguousarray(nbp.reshape(5, 8, 128, 5, 128).transpose(0, 1, 4, 3, 2))
        m["selv"] = selv
        maps.append(m)
    return maps


INPUT_SHAPES = {
    "xin": [NU, 1024], "cT": [128, 8, 2], "w_mod": [2, 1024, 9216], "b_modT": [128, 2, 72], "norm_gT": [128, 2, 3, 8],
    "ffn_w_in": [2, 2, 1024, 5632], "ffn_w_out": [2, 2, 2816, 1024], "mix_w_in": [1024, 3584], "w_gate": [1024, 128],
    "gate_b": [128, 1], "headg_bc": [128, 512], "mix_w_out": [1024, 1024], "sg_w_in": [1024, 4096], "sg_w_out": [2048, 1024],
    "sg_lng_bc": [128, 2048], "sg_lnb_bc": [128, 2048], "sg_w_sT": [128, 8, 128], "sg_b_s": [1, 1024], "final_g_bc": [128, 1024],
    "ident": [128, 128], "tri": [128, 2, 128], "selm": [64, 8, 128], "ropecos": [NU, 512], "ropesin": [NU, 512],
    "nabiasT": [5, 8, 128, 5, 128], "selv": [128, 16],
}


class Ctx:
    pass


_UC = [0]


def _u():
    _UC[0] += 1
    return "u%d" % _UC[0]


def build(stages="all", dbg=()):
    nc = bass.Bass("TRN2", target_bir_lowering=False)
    I = {k: nc.dram_tensor(k, shp, F32, kind="ExternalInput").ap() for k, shp in INPUT_SHAPES.items()}
    OUT = nc.dram_tensor("out", [NOWN, 1024], F32, kind="ExternalOutput").ap()

    def scratch(name, shape, dt):
        if name in dbg:
            return nc.dram_tensor(name, shape, dt, kind="ExternalOutput").ap()
        return nc.dram_tensor(name, shape, dt).ap()

    G = Ctx()
    G.nc, G.I, G.OUT = nc, I, OUT
    G.dbg = dbg
    if dbg:
        G.DBG = {k: nc.dram_tensor(k, shp, dt, kind="ExternalOutput").ap() for k, (shp, dt) in {
            "D_MOD": ([128, 3, 2, 3, 2, 8], F32), "D_X0": ([128, 8, TW], F32), "D_X1": ([128, 8, TW], F32),
            "D_H": ([128, 8, TW], BF16), "D_HID": ([128, 22, TW], BF16), "D_RSTD": ([128, TW], F32)}.items()}
    G.XT = scratch("XT", [128, 8, NU], F32)
    G.NAQT = scratch("NAQT", [128, 4, NU], BF16)
    G.NAKT = scratch("NAKT", [128, 4, NU], BF16)
    G.NAV = scratch("NAV", [NU, 520], BF16)
    G.MLQT = scratch("MLQT", [128, 4, NU], BF16)
    G.MLKT = scratch("MLKT", [128, 4, NU], BF16)
    G.MLK = scratch("MLK", [NU, 512], BF16)
    G.MLV = scratch("MLV", [NU, 516], BF16)
    G.MLG2 = scratch("MLG2", [NU, 512], F32)
    G.GIF = scratch("GIF", [128, NU], F32)
    G.YT = scratch("YT", [128, 8, NOWN], BF16)
    G.HF = scratch("HF", [NOWN, 512], F32)
    G.HB = scratch("HB", [NOWN, 512], F32)
    G.CCI = scratch("CCI", [128, 1040], F32)
    G.CCO = scratch("CCO", [512, 1040], F32)

    with contextlib.ExitStack() as gst:
        P = Prog(nc, gst)
        G.P = P

        def gsb(name, shape, dt):
            return gst.enter_context(nc.sbuf_tensor(name, shape, dt))
        G.ident_f = gsb("ident_f", [128, 128], F32)
        G.ident_b = gsb("ident_b", [128, 128], BF16)
        G.ident8 = gsb("ident8", [128, 128], BF16)
        G.ones_b = gsb("ones_b", [128, 128], BF16)
        G.ones_f = gsb("ones_f", [128, 128], F32)
        G.c_eps = gsb("c_eps", [128, 1], F32)
        G.c_one = gsb("c_one", [128, 1], F32)
        G.SH = gsb("SH", [128, 2, 3, 2, 8], F32)
        G.GM = gsb("GM", [128, 2, 3, 2, 8], F32)
        G.GT = gsb("GT", [128, 2, 3, 2, 8], F32)

        P.dma("sync", lambda e: e.dma_start(out=G.ident_f[:], in_=I["ident"][:, :]), writes=["ident_f"])
        P.op("vector", lambda e: e.tensor_copy(G.ident_b[:], G.ident_f[:]), reads=["ident_f"], writes=["ident_b"])
        P.op("vector", lambda e: e.tensor_scalar(G.ident8[:], G.ident_f[:], 8.0, None, ALU.mult), reads=["ident_f"], writes=["ident8"])
        P.op("vector", lambda e: e.memset(G.ones_b[:], 1.0), writes=["ones_b"])
        P.op("vector", lambda e: e.memset(G.ones_f[:], 1.0), writes=["ones_f"])
        P.op("vector", lambda e: e.memset(G.c_eps[:], EPS), writes=["c_eps"])
        P.op("vector", lambda e: e.memset(G.c_one[:], 1.0), writes=["c_one"])

        ext_tiles = [(ti, 0) for ti in range(18)] + [(18, 1)]
        own_tiles = [(ti, 0) for ti in range(1, 17)]
        S = stages
        stage_mod(G)
        stage_t0(G)
        if S in ("ffn_only",):
            stage_ffn(G, 0, 0, ext_tiles)
            stage_final(G)
        elif S == "ffn_dbg":
            stage_ffn(G, 0, 0, [(1, 0)])
        else:
            stage_ffn(G, 0, 0, ext_tiles)
            stage_inproj(G)
            stage_na(G)
            stage_ml(G)
            stage_outproj(G)
            stage_ffn(G, 0, 1, own_tiles)
            stage_ffn(G, 1, 0, own_tiles)
            stage_sg(G)
            stage_ffn(G, 1, 1, own_tiles)
            stage_final(G)
    return nc


_AC = [0]


def _alloc(G, st):
    nc = G.nc
    _AC[0] += 1
    sfx = "_%d" % _AC[0]

    def sb(name, shape, dt):
        return st.enter_context(nc.sbuf_tensor(name + sfx, shape, dt))

    def ps(name, shape, dt=F32):
        return st.enter_context(nc.psum_tensor(name + sfx, shape, dt))
    return sb, ps


def stage_mod(G):
    P, I, nc = G.P, G.I, G.nc
    with contextlib.ExitStack() as st:
        sb, ps = _alloc(G, st)
        cT = sb("m_cT", [128, 8, 2], F32)
        sil = sb("m_sil", [128, 8, 2], BF16)
        wm = [sb("m_wm%d" % i, [128, 8, 1024], BF16) for i in range(3)]
        modv = sb("m_modv", [128, 2, 72, 2], F32)
        bmod = sb("m_bmod", [128, 2, 72], F32)
        ng = sb("m_ng", [128, 2, 3, 8], F32)
        pp = [ps("m_ps%d" % i, [128, 8, 2]) for i in range(2)]
        xs = [sb("t_xs%d" % i, [128, 1024], F32) for i in range(3)]
        xo = [sb("t_xo%d" % i, [128, 8, 128], F32) for i in range(2)]
        pt = [ps("t_pt%d" % i, [128, 8, 128]) for i in range(2)]
        P.dma("sync", lambda e: e.dma_start(out=cT[:], in_=I["cT"][:, :, :]), writes=["m_cT"])
        P.dma("sync", lambda e: e.dma_start(out=bmod[:], in_=I["b_modT"][:, :, :]), writes=["m_bmod"])
        P.dma("sync", lambda e: e.dma_start(out=ng[:], in_=I["norm_gT"][:, :, :, :]), writes=["m_ng"])
        P.op("scalar", lambda e: e.activation(out=sil[:], in_=cT[:], func=AF.Silu), reads=["m_cT"], writes=["m_sil"])
        NSUB = NU // 128

        def t0_load(i):
            if i < NSUB:
                b3 = i % 3
                P.dma("sync", lambda e, b3=b3, i=i: e.dma_start(out=xs[b3][:], in_=I["xin"][i * 128:(i + 1) * 128, :]), writes=["t_xs%d" % b3])

        def t0_sub(i):
            t0 = i * 128
            b, b3 = i % 2, i % 3
            for k in range(8):
                P.op("tensor", lambda e, b=b, b3=b3, k=k: e.transpose(pt[b][:, k, :], xs[b3][:, k * 128:(k + 1) * 128], G.ident_f[:]),
                     reads=["t_xs%d" % b3, "ident_f"], writes=["t_pt%d" % b], inc=(k == 7))
            if b == 0:
                P.op("vector", lambda e, b=b: e.tensor_copy(xo[b][:], pt[b][:]), reads=["t_pt%d" % b], writes=["t_xo%d" % b])
            else:
                P.op("scalar", lambda e, b=b: e.activation(out=xo[b][:], in_=pt[b][:], func=AF.Identity), reads=["t_pt%d" % b], writes=["t_xo%d" % b])
            t0_load(i + 3)
            P.dma("sync", lambda e, b=b, t0=t0: e.dma_start(out=G.XT[:, :, t0:t0 + 128], in_=xo[b][:]),
                  reads=["t_xo%d" % b], writes=["XT%d" % (t0 // TW)])

        for i in range(3):
            t0_load(i)
        n = 0
        ti = 0
        for l in range(2):
            wsrc = I["w_mod"][l].rearrange("(k p) n -> p k n", p=128)
            for blk in range(9):
                w = wm[n % 3]
                wn = "m_wm%d" % (n % 3)
                pt_ = pp[n % 2]
                pn = "m_ps%d" % (n % 2)
                P.dma("gpsimd", lambda e, w=w, blk=blk, wsrc=wsrc: e.dma_start(out=w[:], in_=wsrc[:, :, blk * 1024:(blk + 1) * 1024]),
                      writes=[wn])
                for jj in range(8):
                    for k in range(8):
                        P.op("tensor", lambda e, w=w, pt_=pt_, jj=jj, k=k: e.matmul(
                            pt_[:, jj, :], lhsT=w[:, k, jj * 128:(jj + 1) * 128], rhs=sil[:, k, :], start=(k == 0), stop=(k == 7)),
                            reads=[wn, "m_sil"], writes=[pn], inc=(jj == 7 and k == 7))
                for m in range(2):
                    P.op("vector", lambda e, pt_=pt_, l=l, blk=blk, m=m: e.tensor_tensor(
                        out=modv[:, l, blk * 8:(blk + 1) * 8, m], in0=pt_[:, :, m], in1=bmod[:, l, blk * 8:(blk + 1) * 8], op=ALU.add),
                        reads=[pn, "m_bmod"], writes=["m_modv"])
                n += 1
                for _ in range(2):
                    if ti < NSUB:
                        t0_sub(ti)
                        ti += 1
        while ti < NSUB:
            t0_sub(ti)
            ti += 1
        for l in range(2):
            for j in range(3):
                for m in range(2):
                    sh = modv[:, l, (3 * j) * 8:(3 * j) * 8 + 8, m]
                    sc = modv[:, l, (3 * j + 1) * 8:(3 * j + 1) * 8 + 8, m]
                    gt = modv[:, l, (3 * j + 2) * 8:(3 * j + 2) * 8 + 8, m]
                    P.op("vector", lambda e, sh=sh, l=l, j=j, m=m: e.tensor_copy(G.SH[:, l, j, m, :], sh), reads=["m_modv"], writes=["modc"])
                    P.op("vector", lambda e, sc=sc, l=l, j=j, m=m: e.scalar_tensor_tensor(
                        out=G.GM[:, l, j, m, :], in0=sc, scalar=1.0, in1=ng[:, l, j, :], op0=ALU.add, op1=ALU.mult),
                        reads=["m_modv", "m_ng"], writes=["modc"])
                    P.op("vector", lambda e, gt=gt, l=l, j=j, m=m: e.tensor_scalar(
                        G.GT[:, l, j, m, :], gt, (1.0 if j == 1 else 0.5), None, ALU.mult), reads=["m_modv"], writes=["modc"])
        P.barrier()
        P.emit()


def stage_t0(G):
    return


class NormScratch:
    def __init__(self, G, sb, ps, pfx, W=TW):
        self.sq = [sb(pfx + "sq%d" % i, [128, W], BF16) for i in range(2)]
        self.rs = sb(pfx + "rs", [128, W], F32)
        self.rstd = sb(pfx + "rstd", [128, W], F32)
        self.tmp = [sb(pfx + "tmp%d" % i, [128, W], F32) for i in range(2)]
        self.pn = ps(pfx + "pn", [128, W])
        self.pfx = pfx


def norm_mod(G, NS, x, xname, gm, sh, h, hname, W=TW, phase=None):
    P = G.P
    pfx = NS.pfx
    for k in range(8):
        if phase is None:
            sq, sqn = NS.sq[k % 2], pfx + "sq%d" % (k % 2)
        else:
            sq, sqn = NS.sq8[k], pfx + "sq8_%d" % k
        if phase in (None, "A"):
            P.op("scalar", lambda e, sq=sq, k=k: e.activation(out=sq[:, :W], in_=x[:, k, :], func=AF.Square), reads=[xname], writes=[sqn])
        if phase in (None, "B"):
            P.op("tensor", lambda e, sq=sq, k=k: e.matmul(NS.pn[:, :W], lhsT=G.ones_b[:], rhs=sq[:, :W], start=(k == 0), stop=(k == 7)),
                 reads=[sqn, "ones_b"], writes=[pfx + "pn"], inc=True)
    if phase == "A":
        return
    P.op("scalar", lambda e: e.activation(out=NS.rs[:, :W], in_=NS.pn[:, :W], func=AF.Sqrt, bias=G.c_eps[:], scale=1.0 / 1024.0),
         reads=[pfx + "pn", "c_eps"], writes=[pfx + "rs"])
    P.op("vector", lambda e: e.reciprocal(NS.rstd[:, :W], NS.rs[:, :W]), reads=[pfx + "rs"], writes=[pfx + "rstd"])
    for k in range(8):
        tmp = NS.tmp[k % 2]
        tn = pfx + "tmp%d" % (k % 2)
        P.op("vector", lambda e, tmp=tmp, k=k: e.tensor_tensor(out=tmp[:, :W], in0=x[:, k, :], in1=NS.rstd[:, :W], op=ALU.mult),
             reads=[xname, pfx + "rstd"], writes=[tn])
        P.op("scalar", lambda e, tmp=tmp, k=k: e.activation(out=h[:, k, :], in_=tmp[:, :W], func=AF.Identity, bias=sh[:, k:k + 1], scale=gm[:, k:k + 1]),
             reads=[tn, "modc"], writes=[hname])


def load_w_cast(G, dst, dname, src, nk, ncols, step=1024):
    P = G.P
    v = src.rearrange("(k p) n -> p k n", p=128)
    for c0 in range(0, ncols, step):
        c1 = min(ncols, c0 + step)
        P.dma("gpsimd", lambda e, c0=c0, c1=c1: e.dma_start(out=dst[:, :, c0:c1], in_=v[:, :, c0:c1]), writes=[dname])


def stage_ffn(G, l, i, tiles):
    P, I, nc = G.P, G.I, G.nc
    j = 0 if i == 0 else 2
    with contextlib.ExitStack() as st:
        sb, ps = _alloc(G, st)
        wi = sb("f_wi", [128, 8, 2 * DFF], BF16)
        wo = sb("f_wo", [128, 22, 1024], BF16)
        xt = [sb("f_xt%d" % b, [128, 8, TW], F32) for b in range(2)]
        hh = [sb("f_h%d" % b, [128, 8, TW], BF16) for b in range(2)]
        hid = sb("f_hid", [128, 22, TW], BF16)
        sa = [sb("f_sa%d" % b, [128, TW], F32) for b in range(2)]
        NS = NormScratch(G, sb, ps, "f_")
        NS.sq8 = [sb("f_sq8_%d" % k, [128, TW], BF16) for k in range(8)]
        pa = [ps("f_pa%d" % b, [128, TW]) for b in range(2)]
        pb = [ps("f_pb%d" % b, [128, TW]) for b in range(2)]
        po = [ps("f_po%d" % b, [128, TW]) for b in range(2)]
        wv_ = I["ffn_w_in"][l, i].rearrange("(k p) n -> p k n", p=128)
        for pc in (0, 2, 3, 1, 4, 5):
            c0, c1 = pc * 1024, min(2 * DFF, (pc + 1) * 1024)
            P.dma("gpsimd", lambda e, c0=c0, c1=c1: e.dma_start(out=wi[:, :, c0:c1], in_=wv_[:, :, c0:c1]), writes=["f_wi%d" % pc])
        load_w_cast(G, wo, "f_wo", I["ffn_w_out"][l, i], 22, 1024)
        def ld(n):
            ti_ = tiles[n][0]
            xb = xt[n % 2]
            P.dma("sync", lambda e, xb=xb, ti_=ti_: e.dma_start(out=xb[:], in_=G.XT[:, :, ti_ * TW:(ti_ + 1) * TW]),
                  reads=["XT%d" % ti_], writes=["f_xt%d" % (n % 2)])
        def do_norm(n, phase=None):
            ti_, m_ = tiles[n]
            bb = n % 2
            norm_mod(G, NS, xt[bb], "f_xt%d" % bb, G.GM[:, l, j, m_, :], G.SH[:, l, j, m_, :], hh[bb], "f_h%d" % bb, phase=phase)
        ld(0)
        if len(tiles) > 1:
            ld(1)
        do_norm(0)
        for n, (ti, m) in enumerate(tiles):
            t0 = ti * TW
            b = n % 2
            x, xn = xt[b], "f_xt%d" % b
            h, hn = hh[b], "f_h%d" % b
            for jj in range(22):
                q = jj % 2
                for half, pp, pn in ((0, pa[q], "f_pa%d" % q), (1, pb[q], "f_pb%d" % q)):
                    c0 = half * DFF + jj * 128
                    for k in range(8):
                        P.op("tensor", lambda e, pp=pp, c0=c0, k=k, h=h: e.matmul(
                            pp[:], lhsT=wi[:, k, c0:c0 + 128], rhs=h[:, k, :], start=(k == 0), stop=(k == 7)),
                            reads=["f_wi%d" % (c0 // 1024), "f_wi%d" % ((c0 + 127) // 1024), hn], writes=[pn], inc=(k == 7))
                P.op("scalar", lambda e, q=q: e.activation(out=sa[q][:], in_=pa[q][:], func=AF.Silu), reads=["f_pa%d" % q], writes=["f_sa%d" % q])
                P.op("vector", lambda e, q=q, jj=jj: e.tensor_tensor(out=hid[:, jj, :], in0=sa[q][:], in1=pb[q][:], op=ALU.mult),
                     reads=["f_sa%d" % q, "f_pb%d" % q], writes=["f_hid%d" % jj])
            if n + 1 < len(tiles):
                do_norm(n + 1, "A")
            for f in range(8):
                q = f % 2
                if f == 3 and n + 1 < len(tiles):
                    do_norm(n + 1, "B")
                for jj in range(22):
                    P.op("tensor", lambda e, q=q, f=f, jj=jj: e.matmul(
                        po[q][:], lhsT=wo[:, jj, f * 128:(f + 1) * 128], rhs=hid[:, jj, :], start=(jj == 0), stop=(jj == 21)),
                        reads=["f_wo", "f_hid%d" % jj], writes=["f_po%d" % q], inc=(jj == 21))
                gsc = G.GT[:, l, j, m, f:f + 1]
                P.op("vector", lambda e, q=q, f=f, x=x, gsc=gsc: e.scalar_tensor_tensor(
                    out=x[:, f, :], in0=po[q][:], scalar=gsc, in1=x[:, f, :], op0=ALU.mult, op1=ALU.add),
                    reads=["f_po%d" % q, "modc", xn], writes=[xn])
            if G.dbg and ti == 1:
                P.dma("sync", lambda e: e.dma_start(out=G.DBG["D_HID"][:, :, :], in_=hid[:]), reads=["f_hid%d" % q for q in range(22)], writes=["dbg3"])
                P.dma("sync", lambda e, x=x: e.dma_start(out=G.DBG["D_X1"][:, :, :], in_=x[:]), reads=[xn], writes=["dbg4"])
            P.dma("sync", lambda e, x=x, t0=t0: e.dma_start(out=G.XT[:, :, t0:t0 + TW], in_=x[:]), reads=[xn], writes=["XT%d" % ti])
            if n + 2 < len(tiles):
                ld(n + 2)
        P.barrier()
        P.emit()


def stage_final(G):
    P, I, nc = G.P, G.I, G.nc
    with contextlib.ExitStack() as st:
        sb, ps = _alloc(G, st)
        fg = sb("k_fg", [128, 1024], F32)
        xs = [sb("k_xs%d" % b, [128, 8, 128], F32) for b in range(2)]
        junk = sb("k_junk", [128, 1024], F32)
        yo = [sb("k_yo%d" % b, [128, 1024], F32) for b in range(2)]
        ssq = sb("k_ssq", [128, 2], F32)
        rs = sb("k_rs", [128, 2], F32)
        pt = [ps("k_pt%d" % b, [128, 8, 128]) for b in range(2)]
        P.dma("sync", lambda e: e.dma_start(out=fg[:], in_=I["final_g_bc"][:, :]), writes=["k_fg"])
        for i in range(NOWN // 128):
            t0 = OWN0 + i * 128
            b = i % 2
            P.dma("gpsimd", lambda e, b=b, t0=t0: e.dma_start(out=xs[b][:], in_=G.XT[:, :, t0:t0 + 128]),
                  reads=["XT%d" % (t0 // TW)], writes=["k_xs%d" % b])
            for k in range(8):
                P.op("tensor", lambda e, b=b, k=k: e.transpose(pt[b][:, k, :], xs[b][:, k, :], G.ident_f[:]),
                     reads=["k_xs%d" % b, "ident_f"], writes=["k_pt%d" % b], inc=(k == 7))
            ptf = pt[b][:].rearrange("p k n -> p (k n)")
            P.op("scalar", lambda e, b=b, ptf=ptf: e.activation(out=junk[:], in_=ptf, func=AF.Square, accum_out=ssq[:, b:b + 1]),
                 reads=["k_pt%d" % b], writes=["k_junk", "k_ssq%d" % b])
            P.op("scalar", lambda e, b=b: e.activation(out=rs[:, b:b + 1], in_=ssq[:, b:b + 1], func=AF.Sqrt, bias=G.c_eps[:], scale=1.0 / 1024.0),
                 reads=["k_ssq%d" % b, "c_eps"], writes=["k_rs%d" % b])
            P.op("vector", lambda e, b=b: e.reciprocal(rs[:, b:b + 1], rs[:, b:b + 1]), reads=["k_rs%d" % b], writes=["k_rs%d" % b])
            P.op("vector", lambda e, b=b, ptf=ptf: e.scalar_tensor_tensor(
                out=yo[b][:], in0=ptf, scalar=rs[:, b:b + 1], in1=fg[:], op0=ALU.mult, op1=ALU.mult),
                reads=["k_pt%d" % b, "k_rs%d" % b, "k_fg"], writes=["k_yo%d" % b])
            P.dma("sync", lambda e, b=b, i=i: e.dma_start(out=G.OUT[i * 128:(i + 1) * 128, :], in_=yo[b][:]),
                  reads=["k_yo%d" % b], writes=["OUT"])
        P.barrier()
        P.emit()


def stage_inproj(G):
    P, I, nc = G.P, G.I, G.nc
    l, j = 0, 1
    tiles = [(ti, 0) for ti in range(18)] + [(18, 1)]
    with contextlib.ExitStack() as st:
        sb, ps = _alloc(G, st)
        wfm = sb("i_wfm", [128, 8, 1024], BF16)
        wg = sb("i_wg", [128, 8, 128], BF16)
        wtm = sb("i_wtm", [128, 8, 2560], BF16)
        xt = [sb("i_xt%d" % b, [128, 8, TW], F32) for b in range(2)]
        hh = [sb("i_h%d" % b, [128, 8, TW], BF16) for b in range(2)]
        NS = NormScratch(G, sb, ps, "i_")
        NS.sq8 = [sb("i_sq8_%d" % k, [128, TW], BF16) for k in range(8)]
        fm = [sb("i_fm%d" % b, [128, 8, TW], BF16) for b in range(2)]
        gts = [sb("i_gt%d" % b, [128, TW], F32) for b in range(2)]
        cosT = [sb("i_cos%d" % b, [128, 512], F32) for b in range(2)]
        sinT = [sb("i_sin%d" % b, [128, 512], F32) for b in range(2)]
        hg = sb("i_hg", [128, 512], F32)
        vt = [sb("i_vt%d" % b, [128, 8, 65], BF16) for b in range(2)]
        mv = [sb("i_mv%d" % b, [128, 4, 129], BF16) for b in range(2)]
        xs = [sb("i_xs%d" % b, [128, 512], F32) for b in range(2)]
        r1 = [sb("i_r1%d" % b, [128, 512], F32) for b in range(2)]
        r2 = [sb("i_r2%d" % b, [128, 512], F32) for b in range(2)]
        qr = [sb("i_qr%d" % b, [128, 512], BF16) for b in range(2)]
        sg = sb("i_sg", [128, 512], F32)
        g2 = [sb("i_g2%d" % b, [128, 512], F32) for b in range(2)]
        tq = [sb("i_tq%d" % b, [128, 4, 128], BF16) for b in range(2)]
        pfm = [ps("i_pfm%d" % b, [128, TW]) for b in range(2)]
        ptm = [ps("i_ptm%d" % b, [128, 512]) for b in range(2)]
        ptr = [ps("i_ptr%d" % b, [128, 4, 128], BF16) for b in range(2)]
        load_w_cast(G, wfm, "i_wfm", I["mix_w_in"][:, 0:1024], 8, 1024)
        load_w_cast(G, wg, "i_wg", I["w_gate"], 8, 128)
        load_w_cast(G, wtm, "i_wtm", I["mix_w_in"][:, 1024:3584], 8, 2560)
        P.dma("sync", lambda e: e.dma_start(out=hg[:], in_=I["headg_bc"][:, :]), writes=["i_hg"])
        for b in range(2):
            P.op("vector", lambda e, b=b: e.memset(vt[b][:], 1.0), writes=["i_vt%d" % b])
            P.op("vector", lambda e, b=b: e.memset(mv[b][:], 1.0), writes=["i_mv%d" % b])

        def ld(n):
            ti_ = tiles[n][0]
            xb = xt[n % 2]
            P.dma("gpsimd", lambda e, xb=xb, ti_=ti_: e.dma_start(out=xb[:], in_=G.XT[:, :, ti_ * TW:(ti_ + 1) * TW]),
                  reads=["XT%d" % ti_], writes=["i_xt%d" % (n % 2)])
        ld(0)
        cnt = {"s": 0, "r": 0, "t": 0}
        deferred = []

        def rope_and_T(pt_, ptn, scale, dstT, tmaj_dst, ts0):
            a = cnt["r"] % 2
            cnt["r"] += 1
            sb_ = cnt["s"] % 2
            X, R1, R2, QR, TQ, PT = xs[a], r1[a], r2[a], qr[a], tq[a], ptr[a]
            xn_, r1n, r2n, qrn, tqn, ptn2 = "i_xs%d" % a, "i_r1%d" % a, "i_r2%d" % a, "i_qr%d" % a, "i_tq%d" % a, "i_ptr%d" % a
            P.op("scalar", lambda e: e.activation(out=X[:], in_=pt_[:], func=AF.Copy, scale=scale), reads=[ptn], writes=[xn_])
            P.op("vector", lambda e: e.tensor_tensor(out=R1[:], in0=X[:], in1=cosT[sb_][:], op=ALU.mult), reads=[xn_, "i_cos%d" % sb_], writes=[r1n])
            Xv = X[:].rearrange("p (i t) -> p i t", t=2)
            Sv = sinT[sb_][:].rearrange("p (i t) -> p i t", t=2)
            Rv = R2[:].rearrange("p (i t) -> p i t", t=2)
            P.op("vector", lambda e: e.tensor_tensor(out=Rv[:, :, 0], in0=Xv[:, :, 1], in1=Sv[:, :, 0], op=ALU.mult),
                 reads=[xn_, "i_sin%d" % sb_], writes=[r2n])
            P.op("vector", lambda e: e.tensor_tensor(out=Rv[:, :, 1], in0=Xv[:, :, 0], in1=Sv[:, :, 1], op=ALU.mult),
                 reads=[xn_, "i_sin%d" % sb_], writes=[r2n])
            P.op("vector", lambda e: e.tensor_tensor(out=QR[:], in0=R1[:], in1=R2[:], op=ALU.add), reads=[r1n, r2n], writes=[qrn])
            if tmaj_dst is not None:
                P.dma("sync", lambda e: e.dma_start(out=tmaj_dst[ts0:ts0 + 128, :], in_=QR[:]), reads=[qrn], writes=[_u()])
            def later():
                for hd in range(4):
                    P.op("tensor", lambda e, hd=hd: e.transpose(PT[:, hd, :], QR[:, hd * 128:(hd + 1) * 128], G.ident_b[:]),
                         reads=[qrn, "ident_b"], writes=[ptn2], inc=(hd == 3))
                P.op("scalar", lambda e: e.activation(out=TQ[:], in_=PT[:], func=AF.Copy), reads=[ptn2], writes=[tqn])
                P.dma("sync", lambda e: e.dma_start(out=dstT[:, :, ts0:ts0 + 128], in_=TQ[:]), reads=[tqn], writes=[_u()])
            deferred.append(later)

        for n, (ti, m) in enumerate(tiles):
            t0 = ti * TW
            b = n % 2
            x, xn = xt[b], "i_xt%d" % b
            h, hn = hh[b], "i_h%d" % b
            if n + 1 < len(tiles):
                ld(n + 1)
            if n == 0:
                norm_mod(G, NS, x, xn, G.GM[:, l, j, m, :], G.SH[:, l, j, m, :], h, hn)
            FM, fmn = fm[b], "i_fm%d" % b
            no_q = ti in (0, 17, 18)
            blks = [0] if ti in (0, 17) else ([0, 2, 3] if ti == 18 else [0, 1, 2, 3, 4])
            for fc in (range(4, 8) if no_q else range(8)):
                q = fc % 2
                for k in range(8):
                    P.op("tensor", lambda e, q=q, fc=fc, k=k, h=h: e.matmul(
                        pfm[q][:], lhsT=wfm[:, k, fc * 128:(fc + 1) * 128], rhs=h[:, k, :], start=(k == 0), stop=(k == 7)),
                        reads=["i_wfm", hn], writes=["i_pfm%d" % q], inc=(k == 7))
                P.op("scalar", lambda e, q=q, fc=fc, FM=FM: e.activation(out=FM[:, fc, :], in_=pfm[q][:], func=AF.Copy),
                     reads=["i_pfm%d" % q], writes=[fmn])
            if not no_q:
                P.dma("sync", lambda e, FM=FM, t0=t0: e.dma_start(out=G.NAQT[:, :, t0:t0 + TW], in_=FM[:, 0:4, :]), reads=[fmn], writes=[_u()])
            P.dma("sync", lambda e, FM=FM, t0=t0: e.dma_start(out=G.NAKT[:, :, t0:t0 + TW], in_=FM[:, 4:8, :]), reads=[fmn], writes=[_u()])
            GTS, gtn = gts[b], "i_gt%d" % b
            for k in range(8):
                P.op("tensor", lambda e, k=k, h=h: e.matmul(pfm[0][:], lhsT=wg[:, k, :], rhs=h[:, k, :], start=(k == 0), stop=(k == 7)),
                     reads=["i_wg", hn], writes=["i_pfm0"], inc=(k == 7))
            P.op("vector", lambda e, GTS=GTS: e.tensor_copy(GTS[:], pfm[0][:]), reads=["i_pfm0"], writes=[gtn])
            P.dma("sync", lambda e, GTS=GTS, t0=t0: e.dma_start(out=G.GIF[:, t0:t0 + TW], in_=GTS[:]), reads=[gtn], writes=[_u()])
            if n + 1 < len(tiles):
                ti2, m2 = tiles[n + 1]
                b2 = (n + 1) % 2
                norm_mod(G, NS, xt[b2], "i_xt%d" % b2, G.GM[:, l, j, m2, :], G.SH[:, l, j, m2, :], hh[b2], "i_h%d" % b2, phase="A")
            for s_ in range(TW // 128):
                if s_ == 1 and n + 1 < len(tiles):
                    norm_mod(G, NS, xt[b2], "i_xt%d" % b2, G.GM[:, l, j, m2, :], G.SH[:, l, j, m2, :], hh[b2], "i_h%d" % b2, phase="B")
                ts0 = t0 + s_ * 128
                sbi = cnt["s"] % 2
                P.dma("gpsimd", lambda e, sbi=sbi, ts0=ts0: e.dma_start(out=cosT[sbi][:], in_=I["ropecos"][ts0:ts0 + 128, :]), writes=["i_cos%d" % sbi])
                P.dma("gpsimd", lambda e, sbi=sbi, ts0=ts0: e.dma_start(out=sinT[sbi][:], in_=I["ropesin"][ts0:ts0 + 128, :]), writes=["i_sin%d" % sbi])
                for blk in blks:
                    a = cnt["t"] % 2
                    cnt["t"] += 1
                    PT_, ptn = ptm[a], "i_ptm%d" % a
                    for k in range(8):
                        P.op("tensor", lambda e, PT_=PT_, k=k, h=h, s_=s_, blk=blk: e.matmul(
                            PT_[:], lhsT=h[:, k, s_ * 128:(s_ + 1) * 128], rhs=wtm[:, k, blk * 512:(blk + 1) * 512], start=(k == 0), stop=(k == 7)),
                            reads=["i_wtm", hn], writes=[ptn], inc=(k == 7))
                    while len(deferred) > (1 if blk == 3 else 0):
                        deferred.pop(0)()
                    if blk == 0:
                        VT = vt[sbi]
                        P.op("scalar", lambda e, VT=VT, PT_=PT_: e.activation(out=VT[:, :, 0:64], in_=PT_[:].rearrange("p (h d) -> p h d", d=64), func=AF.Copy),
                             reads=[ptn], writes=["i_vt%d" % sbi])
                        P.dma("sync", lambda e, VT=VT, ts0=ts0: e.dma_start(out=G.NAV[ts0:ts0 + 128, :], in_=VT[:].rearrange("p h d -> p (h d)")),
                              reads=["i_vt%d" % sbi], writes=[_u()])
                    elif blk == 1:
                        rope_and_T(PT_, ptn, 1.0, G.MLQT, None, ts0)
                    elif blk == 2:
                        rope_and_T(PT_, ptn, 128.0 ** -0.5, G.MLKT, G.MLK, ts0)
                    elif blk == 3:
                        MV = mv[sbi]
                        P.op("scalar", lambda e, MV=MV, PT_=PT_: e.activation(out=MV[:, :, 0:128], in_=PT_[:].rearrange("p (h d) -> p h d", d=128), func=AF.Copy),
                             reads=[ptn], writes=["i_mv%d" % sbi])
                        P.dma("sync", lambda e, MV=MV, ts0=ts0: e.dma_start(out=G.MLV[ts0:ts0 + 128, :], in_=MV[:].rearrange("p h d -> p (h d)")),
                              reads=["i_mv%d" % sbi], writes=[_u()])
                    else:
                        G2 = g2[sbi]
                        P.op("scalar", lambda e, PT_=PT_: e.activation(out=sg[:], in_=PT_[:], func=AF.Sigmoid), reads=[ptn], writes=["i_sg"])
                        P.op("vector", lambda e, G2=G2: e.tensor_tensor(out=G2[:], in0=sg[:], in1=hg[:], op=ALU.mult), reads=["i_sg", "i_hg"], writes=["i_g2%d" % sbi])
                        P.dma("sync", lambda e, G2=G2, ts0=ts0: e.dma_start(out=G.MLG2[ts0:ts0 + 128, :], in_=G2[:]), reads=["i_g2%d" % sbi], writes=[_u()])
                while deferred:
                    deferred.pop(0)()
                cnt["s"] += 1
        P.barrier()
        P.emit()


def stage_na(G):
    P, I, nc = G.P, G.I, G.nc
    with contextlib.ExitStack() as st:
        sb, ps = _alloc(G, st)
        KT = sb("n_KT", [128, 4, NU], BF16)
        V = sb("n_V", [128, 38, 520], BF16)
        QT = sb("n_QT", [128, 4, NOWN], BF16)
        BI = sb("n_BI", [128, 5, 8, 5, 128], BF16)
        sAb = [sb("n_sAb%d" % b, [128, 4, 128], F32) for b in range(2)]
        sBb = [sb("n_sBb%d" % b, [128, 128], F32) for b in range(2)]
        pt = [sb("n_pt%d" % b, [128, 7, 128], BF16) for b in range(2)]
        ya = [sb("n_ya%d" % b, [128, 512], BF16) for b in range(2)]
        rec = [sb("n_rec%d" % b, [128, 8], F32) for b in range(2)]
        yt = [sb("n_yt%d" % b, [128, 4, 128], BF16) for b in range(2)]
        sA = [ps("n_sA%d" % b, [128, 4, 128]) for b in range(2)]
        sB = [ps("n_sB%d" % b, [128, 4, 128]) for b in range(2)]
        O = ps("n_O", [128, 8, 128])
        ptr = ps("n_ptr", [128, 4, 128], BF16)
        P.dma("sync", lambda e: e.dma_start(out=KT[:], in_=G.NAKT[:, :, :]), reads=["dramNA"], writes=["n_KT"])
        P.dma("sync", lambda e: e.dma_start(out=V[:], in_=G.NAV.rearrange("(c p) n -> p c n", p=128)), reads=["dramNA"], writes=["n_V"])
        P.dma("sync", lambda e: e.dma_start(out=QT[:], in_=G.NAQT[:, :, OWN0:OWN0 + NOWN]), reads=["dramNA"], writes=["n_QT"])
        for c5 in range(5):
            P.dma("gpsimd", lambda e, c5=c5: e.dma_start(out=BI[:, c5], in_=I["nabiasT"][c5].rearrange("h k j q -> k h j q")), writes=["n_BI"])
        def scores(p, h, a):
            cls = 0 if p == 0 else 1 if p == 1 else 3 if p == 30 else 4 if p == 31 else 2
            hc, b0 = h // 2, (h % 2) * 64
            q_ap = QT[b0:b0 + 64, hc, p * 128:(p + 1) * 128]
            SA, SB, PT = sA[a], sB[a], pt[a]
            san, sbn, ptn = "n_sA%d" % a, "n_sB%d" % a, "n_pt%d" % a
            for jj in range(4):
                k0 = (p + jj) * 128
                P.op("tensor", lambda e, jj=jj, k0=k0: e.matmul(
                    SA[:, jj, :], lhsT=KT[b0:b0 + 64, hc, k0:k0 + 128], rhs=q_ap, start=True, stop=True),
                    reads=["n_KT", "n_QT"], writes=[san], inc=(jj == 3))
            k0h = (p + 4) * 128
            P.op("tensor", lambda e: e.matmul(
                SB[0:64, 0, :], lhsT=KT[b0:b0 + 64, hc, k0h:k0h + 64], rhs=q_ap, start=True, stop=True),
                reads=["n_KT", "n_QT"], writes=[sbn], inc=False)
            for c in range(2):
                k0 = CTX0 + c * 128
                P.op("tensor", lambda e, c=c, k0=k0: e.matmul(
                    SB[:, 1 + c, :], lhsT=KT[b0:b0 + 64, hc, k0:k0 + 128], rhs=q_ap, start=True, stop=True),
                    reads=["n_KT", "n_QT"], writes=[sbn], inc=(c == 1))
            AB, BB = sAb[a], sBb[a]
            abn, bbn = "n_sAb%d" % a, "n_sBb%d" % a
            P.op("vector", lambda e: e.scalar_tensor_tensor(
                out=AB[:], in0=SA[:], scalar=0.125, in1=BI[:, cls, h, 0:4, :], op0=ALU.mult, op1=ALU.add),
                reads=[san, "n_BI"], writes=[abn])
            P.op("vector", lambda e: e.scalar_tensor_tensor(
                out=BB[0:64, :], in0=SB[0:64, 0, :], scalar=0.125, in1=BI[0:64, cls, h, 4, :], op0=ALU.mult, op1=ALU.add),
                reads=[sbn, "n_BI"], writes=[bbn])
            P.op("scalar", lambda e: e.activation(out=PT[:, 0:4, :], in_=AB[:], func=AF.Exp), reads=[abn], writes=[ptn])
            P.op("scalar", lambda e: e.activation(out=PT[:, 5:7, :], in_=SB[:, 1:3, :], func=AF.Exp, scale=0.125), reads=[sbn, bbn], writes=[ptn])
            P.op("scalar", lambda e: e.activation(out=PT[0:64, 4, :], in_=BB[0:64, :], func=AF.Exp), reads=[bbn], writes=[ptn])

        def pv(p, h, a):
            pb2 = p % 2
            PT, ptn = pt[a], "n_pt%d" % a
            specs = [(jj, 128, p + jj) for jj in range(4)] + [(4, 64, p + 4), (5, 128, 36), (6, 128, 37)]
            for si, (slot, nk, vc) in enumerate(specs):
                P.op("tensor", lambda e, slot=slot, nk=nk, vc=vc, si=si: e.matmul(
                    O[:, h, 0:65], lhsT=PT[0:nk, slot, :], rhs=V[0:nk, vc, h * 65:(h + 1) * 65], start=(si == 0), stop=(si == 6)),
                    reads=[ptn, "n_V"], writes=["n_O%d" % (h // 4)], inc=(si == 6))
            if h in (3, 7):
                hf_ = h // 4
                R, YA = rec[pb2], ya[pb2]
                rn, yan = "n_rec%d_%d" % (pb2, hf_), "n_ya%d" % pb2
                P.op("vector", lambda e: e.reciprocal(R[:, hf_ * 4:hf_ * 4 + 4], O[:, hf_ * 4:hf_ * 4 + 4, 64]), reads=["n_O%d" % hf_], writes=[rn])
                for h2 in range(hf_ * 4, hf_ * 4 + 4):
                    P.op("scalar", lambda e, h2=h2: e.activation(out=YA[:, h2 * 64:(h2 + 1) * 64], in_=O[:, h2, 0:64], func=AF.Copy, scale=R[:, h2:h2 + 1]),
                         reads=["n_O%d" % hf_, rn], writes=[yan])
            if h == 7:
                YA, YT_ = ya[pb2], yt[pb2]
                yan, ytn = "n_ya%d" % pb2, "n_yt%d" % pb2
                for c in range(4):
                    P.op("tensor", lambda e, c=c: e.transpose(ptr[:, c, :], YA[:, c * 128:(c + 1) * 128], G.ident_b[:]),
                         reads=[yan, "ident_b"], writes=["n_ptr"], inc=(c == 3))
                P.op("vector", lambda e: e.tensor_copy(YT_[:], ptr[:]), reads=["n_ptr"], writes=[ytn])
                P.dma("sync", lambda e: e.dma_start(out=G.YT[:, 0:4, p * 128:(p + 1) * 128], in_=YT_[:]), reads=[ytn], writes=[_u()])

        items = [(p, h) for p in range(32) for h in range(8)]
        scores(items[0][0], items[0][1], 0)
        for i_, (p, h) in enumerate(items):
            if i_ + 1 < len(items):
                scores(items[i_ + 1][0], items[i_ + 1][1], (i_ + 1) % 2)
            pv(p, h, i_ % 2)
        P.barrier()
        P.emit()


def stage_ml(G):
    P, I, nc = G.P, G.I, G.nc
    NCH = 32
    with contextlib.ExitStack() as st:
        sb, ps = _alloc(G, st)
        TOK = sb("l_TOK", [128, 34, 5, 8], F32)
        EBEND = sb("l_EBEND", [128, 8, 32], F32)
        ATOT = sb("l_ATOT", [128, 8], F32)
        with contextlib.ExitStack() as st1:
            sb1, ps1 = _alloc(G, st1)
            LI = sb1("l_LI", [64, NU], F32)
            SP = sb1("l_SP", [64, NU], F32)
            CL = sb1("l_CL", [64, NU], F32)
            CG = sb1("l_CG", [64, NU], F32)
            TM = sb1("l_TM", [64, NU], F32)
            OQ = [sb1("l_OQ%d" % b, [64, NU], F32) for b in range(2)]
            gbI = sb1("l_gbI", [64, 1], F32)
            gbF = sb1("l_gbF", [64, 1], F32)
            CE = sb1("l_CE", [64, 32], F32)
            ntot = sb1("l_ntot", [64, 2], F32)
            tot = sb1("l_tot", [64, 2], F32)
            SELM = sb1("l_SELM", [64, 8, 128], F32)
            ptr = [ps1("l_ptr%d" % b, [128, 8, 64]) for b in range(2)]
            pe = ps1("l_pe", [128, 8, 32])
            pa = ps1("l_pa", [128, 8, 2])
            own = slice(OWN0, OWN0 + NOWN)
            cxs = slice(CTX0, CTX0 + 256)
            P.dma("sync", lambda e: e.dma_start(out=LI[:], in_=G.GIF[0:64, :]), reads=["dramML"], writes=["l_LI"])
            P.dma("sync", lambda e: e.dma_start(out=SP[:], in_=G.GIF[64:128, :]), reads=["dramML"], writes=["l_SP"])
            P.dma("sync", lambda e: e.dma_start(out=gbI[:], in_=I["gate_b"][0:64, :]), writes=["l_gbI"])
            P.dma("sync", lambda e: e.dma_start(out=gbF[:], in_=I["gate_b"][64:128, :]), writes=["l_gbF"])
            P.dma("sync", lambda e: e.dma_start(out=SELM[:], in_=I["selm"][:, :, :]), writes=["l_SELM"])
            P.op("vector", lambda e: e.tensor_scalar(gbF[:], gbF[:], -1.0, None, ALU.mult), reads=["l_gbF"], writes=["l_gbF"])
            P.op("scalar", lambda e: e.activation(out=LI[:], in_=LI[:], func=AF.Identity, bias=gbI[:]), reads=["l_LI", "l_gbI"], writes=["l_LI"])
            P.op("scalar", lambda e: e.activation(out=SP[:], in_=SP[:], func=AF.Exp, bias=gbF[:], scale=-1.0), reads=["l_SP", "l_gbF"], writes=["l_SP"])
            P.op("scalar", lambda e: e.activation(out=SP[:], in_=SP[:], func=AF.Ln, bias=G.c_one[0:64, :]), reads=["l_SP", "c_one"], writes=["l_SP"])
            P.op("vector", lambda e: e.memset(TM[:], 1.0), writes=["l_TM"])
            P.op("vector", lambda e: e.tensor_tensor_scan(out=CG[:, own], data0=TM[:, own], data1=SP[:, own], initial=0.0, op0=ALU.mult, op1=ALU.add),
                 reads=["l_TM", "l_SP"], writes=["l_CG"])
            P.op("vector", lambda e: e.tensor_tensor_scan(out=CG[:, cxs], data0=TM[:, cxs], data1=SP[:, cxs], initial=0.0, op0=ALU.mult, op1=ALU.add),
                 reads=["l_TM", "l_SP"], writes=["l_CG"])
            TMo = TM[:, own].rearrange("p (c t) -> p c t", t=128)
            P.op("vector", lambda e: e.memset(TMo[:, :, 0:1], 0.0), reads=["l_CG"], writes=["l_TM"])
            P.op("vector", lambda e: e.tensor_tensor_scan(out=CL[:, own], data0=TM[:, own], data1=SP[:, own], initial=0.0, op0=ALU.mult, op1=ALU.add),
                 reads=["l_TM", "l_SP"], writes=["l_CL"])
            CLo = CL[:, own].rearrange("p (c t) -> p c t", t=128)
            SPo = SP[:, own].rearrange("p (c t) -> p c t", t=128)
            P.op("vector", lambda e: e.tensor_copy(CE[:], CLo[:, :, 127]), reads=["l_CL"], writes=["l_CE"])
            P.op("vector", lambda e: e.tensor_copy(tot[:, 0:1], CG[:, OWN0 + NOWN - 1:OWN0 + NOWN]), reads=["l_CG"], writes=["l_tot"])
            P.op("vector", lambda e: e.tensor_copy(tot[:, 1:2], CG[:, CTX0 + 255:CTX0 + 256]), reads=["l_CG"], writes=["l_tot"])
            P.op("vector", lambda e: e.tensor_scalar(ntot[:], tot[:], -1.0, None, ALU.mult), reads=["l_tot"], writes=["l_ntot"])
            for c in range(NCH):
                P.op("vector", lambda e, c=c: e.tensor_scalar(CLo[32:64, c, :], CLo[32:64, c, :], CE[32:64, c:c + 1], -1.0, ALU.subtract, ALU.mult),
                     reads=["l_CL", "l_CE"], writes=["l_CL"])
            P.op("vector", lambda e: e.tensor_tensor(out=CL[32:64, own], in0=CL[32:64, own], in1=SP[32:64, own], op=ALU.add),
                 reads=["l_CL", "l_SP"], writes=["l_CL"])
            for r in range(8):
                P.op("tensor", lambda e, r=r: e.matmul(pe[:, r, :], lhsT=SELM[:, r, :], rhs=CE[:], start=True, stop=True),
                     reads=["l_SELM", "l_CE"], writes=["l_pe"], inc=(r == 7))
            P.op("scalar", lambda e: e.activation(out=EBEND[:], in_=pe[:], func=AF.Exp, scale=-1.0), reads=["l_pe"], writes=["l_EBEND"])
            for r in range(8):
                P.op("tensor", lambda e, r=r: e.matmul(pa[:, r, :], lhsT=SELM[:, r, :], rhs=tot[:], start=True, stop=True),
                     reads=["l_SELM", "l_tot"], writes=["l_pa"], inc=(r == 7))
            P.op("scalar", lambda e: e.activation(out=ATOT[:], in_=pa[:, :, 0], func=AF.Exp, scale=-1.0), reads=["l_pa"], writes=["l_ATOT"])

            tcnt = {"n": 0}

            def transpose_out(Q, qn, qty, chunks):
                for g0 in range(0, len(chunks), 8):
                    grp = chunks[g0:g0 + 8]
                    a = tcnt["n"] % 2
                    tcnt["n"] += 1
                    for gi, (ci, col0) in enumerate(grp):
                        P.op("tensor", lambda e, a=a, gi=gi, col0=col0: e.transpose(ptr[a][:, gi, :], Q[0:64, col0:col0 + 128], G.ident_f[0:64, 0:64]),
                             reads=[qn, "ident_f"], writes=["l_ptr%d" % a], inc=(gi == len(grp) - 1))
                    c_first = grp[0][0]
                    ng_ = len(grp)
                    src = ptr[a][:, 0:ng_, :].rearrange("p g (d x) -> p g d x", d=2)[:, :, :, 0:4]
                    dst = TOK[:, c_first:c_first + ng_, qty, :].rearrange("p g (d x) -> p g d x", d=2)
                    P.op("vector", lambda e, src=src, dst=dst: e.tensor_copy(dst, src), reads=["l_ptr%d" % a], writes=["l_TOK"])

            own_chunks = [(c, OWN0 + c * 128) for c in range(NCH)]
            ctx_chunks = [(32 + c, CTX0 + c * 128) for c in range(2)]
            P.op("scalar", lambda e: e.activation(out=OQ[0][:, own], in_=CL[:, own], func=AF.Exp, scale=-1.0), reads=["l_CL"], writes=["l_OQ0"])
            transpose_out(OQ[0], "l_OQ0", 0, own_chunks)
            P.op("scalar", lambda e: e.activation(out=OQ[1][:, own], in_=CL[:, own], func=AF.Exp), reads=["l_CL"], writes=["l_OQ1"])
            transpose_out(OQ[1], "l_OQ1", 4, own_chunks)
            P.op("vector", lambda e: e.tensor_tensor(out=TM[:, own], in0=LI[:, own], in1=CL[:, own], op=ALU.add), reads=["l_LI", "l_CL"], writes=["l_TM"])
            P.op("scalar", lambda e: e.activation(out=OQ[1][:, own], in_=TM[:, own], func=AF.Exp), reads=["l_TM"], writes=["l_OQ1"])
            transpose_out(OQ[1], "l_OQ1", 1, own_chunks)
            TMo2 = TM[:, own].rearrange("p (c t) -> p c t", t=128)
            for c in range(NCH):
                P.op("vector", lambda e, c=c: e.tensor_scalar(TMo2[:, c, :], TMo2[:, c, :], CE[:, c:c + 1], None, ALU.subtract),
                     reads=["l_TM", "l_CE", "l_OQ1"], writes=["l_TM"])
            P.op("scalar", lambda e: e.activation(out=OQ[0][:, own], in_=TM[:, own], func=AF.Exp), reads=["l_TM"], writes=["l_OQ0"])
            transpose_out(OQ[0], "l_OQ0", 2, own_chunks)
            for (sl, ti_) in ((own, 0), (cxs, 1)):
                P.op("vector", lambda e, sl=sl: e.tensor_tensor(out=TM[0:32, sl], in0=LI[0:32, sl], in1=CG[0:32, sl], op=ALU.add),
                     reads=["l_LI", "l_CG"], writes=["l_TM"])
                P.op("vector", lambda e, sl=sl: e.tensor_tensor(out=TM[32:64, sl], in0=LI[32:64, sl], in1=CG[32:64, sl], op=ALU.subtract),
                     reads=["l_LI", "l_CG"], writes=["l_TM"])
                P.op("vector", lambda e, sl=sl: e.tensor_tensor(out=TM[32:64, sl], in0=TM[32:64, sl], in1=SP[32:64, sl], op=ALU.add),
                     reads=["l_TM", "l_SP"], writes=["l_TM"])
                P.op("scalar", lambda e, sl=sl, ti_=ti_: e.activation(out=OQ[1][0:32, sl], in_=TM[0:32, sl], func=AF.Exp, bias=ntot[0:32, ti_:ti_ + 1]),
                     reads=["l_TM", "l_ntot"], writes=["l_OQ1"])
                P.op("scalar", lambda e, sl=sl: e.activation(out=OQ[1][32:64, sl], in_=TM[32:64, sl], func=AF.Exp), reads=["l_TM"], writes=["l_OQ1"])
            transpose_out(OQ[1], "l_OQ1", 3, own_chunks + ctx_chunks)
            P.barrier()
            P.emit()

        KTOK = sb("l_KTOK", [128, 34, 512], BF16)
        VTOK = sb("l_VTOK", [128, 34, 516], BF16)
        TRI = sb("l_TRI", [128, 2, 128], F32)
        SELV = sb("l_SELV", [128, 16], F32)
        PAY = sb("l_PAY", [128, 8, 130], F32)
        GATH = sb("l_GATH", [128, 4, 1040], F32)
        STATE = sb("l_STATE", [128, 8, 129], F32)
        STB = sb("l_STB", [128, 8, 129], BF16)
        KA = [sb("l_KA%d" % b, [128, 128], BF16) for b in range(3)]
        alpha = sb("l_alpha", [128, 1], F32)
        tmpL = sb("l_tmpL", [128, 129], F32)
        QTc = [[sb("l_QTc%d%d" % (d_, b), [128, 4, 128], BF16) for b in range(2)] for d_ in range(2)]
        KTc = [[sb("l_KTc%d%d" % (d_, b), [128, 4, 128], BF16) for b in range(2)] for d_ in range(2)]
        HS = [[sb("l_HS%d%d" % (d_, b), [128, 512], F32) for b in range(2)] for d_ in range(2)]
        PTt = [sb("l_PT%d" % b, [128, 128], BF16) for b in range(2)]
        pS = [ps("l_pS%d" % b, [128, 128]) for b in range(2)]
        pU = [ps("l_pU%d" % b, [128, 132]) for b in range(4)]
        pN = [ps("l_pN%d" % b, [128, 132]) for b in range(2)]
        pL = [pU[0], pU[1]]
        den = [sb("l_den%d" % b, [128, 8], F32) for b in range(2)]
        P.dma("sync", lambda e: e.dma_start(out=KTOK[:, 0:32, :], in_=G.MLK[OWN0:OWN0 + NOWN, :].rearrange("(c p) n -> p c n", p=128)), reads=["dramML"], writes=["l_KTOK"])
        P.dma("sync", lambda e: e.dma_start(out=KTOK[:, 32:34, :], in_=G.MLK[CTX0:CTX0 + 256, :].rearrange("(c p) n -> p c n", p=128)), reads=["dramML"], writes=["l_KTOK"])
        P.dma("sync", lambda e: e.dma_start(out=VTOK[:, 0:32, :], in_=G.MLV[OWN0:OWN0 + NOWN, :].rearrange("(c p) n -> p c n", p=128)), reads=["dramML"], writes=["l_VTOK"])
        P.dma("sync", lambda e: e.dma_start(out=VTOK[:, 32:34, :], in_=G.MLV[CTX0:CTX0 + 256, :].rearrange("(c p) n -> p c n", p=128)), reads=["dramML"], writes=["l_VTOK"])
        P.dma("sync", lambda e: e.dma_start(out=TRI[:], in_=I["tri"][:, :, :]), writes=["l_TRI"])
        P.dma("sync", lambda e: e.dma_start(out=SELV[:], in_=I["selv"][:, :]), writes=["l_SELV"])
        kacnt = {"n": 0}

        def scaled_k(c, h, qty, r):
            a = kacnt["n"] % 3
            kacnt["n"] += 1
            eng = ("vector", "scalar", "scalar")[a]
            src = KTOK[:, c, h * 128:(h + 1) * 128]
            sc = TOK[:, c, qty, r:r + 1]
            if eng == "scalar":
                P.op("scalar", lambda e: e.activation(out=KA[a][:], in_=src, func=AF.Copy, scale=sc), reads=["l_KTOK", "l_TOK"], writes=["l_KA%d" % a])
            else:
                P.op(eng, lambda e: e.tensor_scalar(KA[a][:], src, sc, None, ALU.mult), reads=["l_KTOK", "l_TOK"], writes=["l_KA%d" % a])
            return KA[a], "l_KA%d" % a

        n2 = 0
        for r in range(8):
            h = r % 4
            for (chs, dstname) in ((list(range(32)), "own"), ([32, 33], "ctx")):
                pp, ppn = pL[n2 % 2], "l_pU%d" % (n2 % 2)
                n2 += 1
                for i_, c in enumerate(chs):
                    ka, kan = scaled_k(c, h, 3, r)
                    P.op("tensor", lambda e, pp=pp, ka=ka, c=c, h=h, i_=i_, L=len(chs): e.matmul(
                        pp[:, 0:129], lhsT=ka[:], rhs=VTOK[:, c, h * 129:(h + 1) * 129], start=(i_ == 0), stop=(i_ == L - 1)),
                        reads=[kan, "l_VTOK"], writes=[ppn], inc=True)
                if dstname == "own":
                    P.op("vector", lambda e, pp=pp, r=r: e.tensor_copy(PAY[:, r, 0:129], pp[:, 0:129]), reads=[ppn], writes=["l_PAY"])
                else:
                    P.op("vector", lambda e, pp=pp, r=r: e.tensor_copy(STATE[:, r, :], pp[:, 0:129]), reads=[ppn], writes=["l_STATE"])
        P.op("vector", lambda e: e.tensor_copy(PAY[:, :, 129], ATOT[:]), reads=["l_ATOT", "l_PAY"], writes=["l_PAY"])
        P.dma("sync", lambda e: e.dma_start(out=G.CCI[:, :], in_=PAY[:].rearrange("p r n -> p (r n)")), reads=["l_PAY"], writes=["CCI"])
        for _rep in range(3):
            P.cc(lambda e: e.collective_compute("AllGather", ALU.bypass, replica_groups=[[0, 1, 2, 3], [4, 5, 6, 7]],
                                                ins=[G.CCI.opt()], outs=[G.CCO.opt()]), reads=["CCI"], writes=["CCO"])
        P.dma("sync", lambda e: e.dma_start(out=GATH[:], in_=G.CCO.rearrange("(j p) n -> p j n", p=128)), reads=["CCO"], writes=["l_GATH"])
        for r in range(8):
            d = r // 4
            order = range(4) if d == 0 else range(3, -1, -1)
            for jseg in order:
                so = 0 if d == 0 else 8
                A_j = GATH[:, jseg, r * 130 + 129:r * 130 + 130]
                L_j = GATH[:, jseg, r * 130:r * 130 + 129]
                P.op("vector", lambda e, A_j=A_j, so=so, jseg=jseg: e.tensor_scalar(
                    alpha[:], A_j, SELV[:, so + jseg:so + jseg + 1], SELV[:, so + 4 + jseg:so + 5 + jseg], ALU.mult, ALU.add),
                    reads=["l_GATH", "l_SELV"], writes=["l_alpha"])
                P.op("vector", lambda e, L_j=L_j, so=so, jseg=jseg: e.tensor_scalar(tmpL[:], L_j, SELV[:, so + jseg:so + jseg + 1], None, ALU.mult),
                     reads=["l_GATH", "l_SELV"], writes=["l_tmpL"])
                P.op("vector", lambda e, r=r: e.scalar_tensor_tensor(out=STATE[:, r, :], in0=STATE[:, r, :], scalar=alpha[:], in1=tmpL[:], op0=ALU.mult, op1=ALU.add),
                     reads=["l_STATE", "l_alpha", "l_tmpL"], writes=["l_STATE"])
        P.op("scalar", lambda e: e.activation(out=STB[:], in_=STATE[:], func=AF.Copy), reads=["l_STATE"], writes=["l_STB"])

        def ld4(i):
            if i >= NCH:
                return
            bb = i % 2
            for d_ in range(2):
                c_ = i if d_ == 0 else NCH - 1 - i
                tk0 = OWN0 + c_ * 128
                P.dma("sync", lambda e, bb=bb, d_=d_, tk0=tk0: e.dma_start(out=QTc[d_][bb][:], in_=G.MLQT[:, :, tk0:tk0 + 128]), writes=["l_QTc%d%d" % (d_, bb)])
                P.dma("sync", lambda e, bb=bb, d_=d_, tk0=tk0: e.dma_start(out=KTc[d_][bb][:], in_=G.MLKT[:, :, tk0:tk0 + 128]), writes=["l_KTc%d%d" % (d_, bb)])
        ld4(0)
        for i in range(NCH):
            b = i % 2
            ld4(i + 1)
            cs = (i, NCH - 1 - i)
            for hh_ in range(2):
                items = [(h, d) for h in (2 * hh_, 2 * hh_ + 1) for d in range(2)]
                for (h, d) in items:
                    c = cs[d]
                    r = d * 4 + h
                    u = (h % 2) * 2 + d
                    qn, kn = "l_QTc%d%d" % (d, b), "l_KTc%d%d" % (d, b)
                    P.op("tensor", lambda e, b=b, h=h, d=d: e.matmul(pS[d][:], lhsT=KTc[d][b][:, h, :], rhs=QTc[d][b][:, h, :], start=True, stop=True),
                         reads=[kn, qn], writes=["l_pS%d" % d], inc=True)
                    P.op("vector", lambda e, c=c, r=r, d=d: e.scalar_tensor_tensor(
                        out=PTt[d][:], in0=pS[d][:], scalar=TOK[:, c, 1, r:r + 1], in1=TRI[:, d, :], op0=ALU.mult, op1=ALU.mult),
                        reads=["l_pS%d" % d, "l_TOK", "l_TRI"], writes=["l_PT%d" % d])
                    P.op("tensor", lambda e, c=c, h=h, d=d, u=u: e.matmul(pU[u][:, 0:129], lhsT=PTt[d][:], rhs=VTOK[:, c, h * 129:(h + 1) * 129], start=True, stop=False),
                         reads=["l_PT%d" % d, "l_VTOK"], writes=["l_pU%d" % u], inc=False)
                    P.op("tensor", lambda e, b=b, h=h, d=d, r=r, u=u: e.matmul(pU[u][:, 0:129], lhsT=QTc[d][b][:, h, :], rhs=STB[:, r, :], start=False, stop=True),
                         reads=[qn, "l_STB%d" % r, "l_STB"], writes=["l_pU%d" % u], inc=True)
                for (h, d) in items:
                    c = cs[d]
                    r = d * 4 + h
                    a = r % 2
                    ka, kan = scaled_k(c, h, 2, r)
                    P.op("tensor", lambda e, a=a, ka=ka, c=c, h=h: e.matmul(pN[a][:, 0:129], lhsT=ka[:], rhs=VTOK[:, c, h * 129:(h + 1) * 129], start=True, stop=True),
                         reads=[kan, "l_VTOK"], writes=["l_pN%d" % a], inc=True)
                    P.op("vector", lambda e, a=a, r=r, c=c: e.scalar_tensor_tensor(
                        out=STATE[:, r, :], in0=STATE[:, r, :], scalar=EBEND[:, r, c:c + 1], in1=pN[a][:, 0:129], op0=ALU.mult, op1=ALU.add),
                        reads=["l_STATE%d" % r, "l_EBEND", "l_pN%d" % a, "l_STATE"], writes=["l_STATE%d" % r])
                    P.op("scalar", lambda e, r=r: e.activation(out=STB[:, r, :], in_=STATE[:, r, :], func=AF.Copy),
                         reads=["l_STATE%d" % r], writes=["l_STB%d" % r])
                for (h, d) in items:
                    u = (h % 2) * 2 + d
                    dn, dnn = den[d], "l_den%d" % d
                    P.op("scalar", lambda e, dn=dn, u=u, h=h: e.activation(out=dn[:, h:h + 1], in_=pU[u][:, 128:129], func=AF.Abs),
                         reads=["l_pU%d" % u], writes=[dnn])
                for d in range(2):
                    c = cs[d]
                    dn, dnn = den[d], "l_den%d" % d
                    h0 = 2 * hh_
                    REB2 = TOK[:, c, 4, d * 4 + h0:d * 4 + h0 + 2]
                    P.op("vector", lambda e, dn=dn, REB2=REB2, h0=h0: e.tensor_tensor(out=dn[:, h0:h0 + 2], in0=dn[:, h0:h0 + 2], in1=REB2, op=ALU.max),
                         reads=[dnn, "l_TOK"], writes=[dnn])
                    P.op("vector", lambda e, dn=dn, h0=h0: e.reciprocal(dn[:, h0:h0 + 2], dn[:, h0:h0 + 2]), reads=[dnn], writes=[dnn])
                for (h, d) in items:
                    u = (h % 2) * 2 + d
                    dn, dnn = den[d], "l_den%d" % d
                    hsn = "l_HS%d%d" % (d, b)
                    if d == 0:
                        P.op("scalar", lambda e, b=b, h=h, dn=dn, u=u: e.activation(out=HS[0][b][:, h * 128:(h + 1) * 128], in_=pU[u][:, 0:128], func=AF.Copy, scale=dn[:, h:h + 1]),
                             reads=["l_pU%d" % u, dnn], writes=[hsn])
                    else:
                        P.op("vector", lambda e, b=b, h=h, dn=dn, u=u: e.tensor_scalar(HS[1][b][:, h * 128:(h + 1) * 128], pU[u][:, 0:128], dn[:, h:h + 1], None, ALU.mult),
                             reads=["l_pU%d" % u, dnn], writes=[hsn])
            P.dma("sync", lambda e, b=b, c=cs[0]: e.dma_start(out=G.HF[c * 128:(c + 1) * 128, :], in_=HS[0][b][:]), reads=["l_HS0%d" % b], writes=[_u()])
            P.dma("sync", lambda e, b=b, c=cs[1]: e.dma_start(out=G.HB[c * 128:(c + 1) * 128, :], in_=HS[1][b][:]), reads=["l_HS1%d" % b], writes=[_u()])
        P.barrier()
        P.emit()

    with contextlib.ExitStack() as st:
        sb, ps = _alloc(G, st)
        NB = 3
        hf = [sb("r_hf%d" % b, [128, 512], F32) for b in range(NB)]
        hb = [sb("r_hb%d" % b, [128, 512], F32) for b in range(NB)]
        g2 = [sb("r_g2%d" % b, [128, 512], F32) for b in range(NB)]
        junk = sb("r_junk", [128, 128], F32)
        ssq = [sb("r_ssq%d" % b, [128, 4], F32) for b in range(2)]
        Yb = [sb("r_Y%d" % b, [128, 512], BF16) for b in range(2)]
        ytb = [sb("r_yt%d" % b, [128, 4, 128], BF16) for b in range(2)]
        ptr2 = [ps("r_ptr%d" % b, [128, 4, 128], BF16) for b in range(2)]

        def ld5(c):
            if c >= NCH:
                return
            b3 = c % NB
            tk0 = OWN0 + c * 128
            P.dma("sync", lambda e, b3=b3, c=c: e.dma_start(out=hf[b3][:], in_=G.HF[c * 128:(c + 1) * 128, :]), writes=["r_hf%d" % b3])
            P.dma("sync", lambda e, b3=b3, c=c: e.dma_start(out=hb[b3][:], in_=G.HB[c * 128:(c + 1) * 128, :]), writes=["r_hb%d" % b3])
            P.dma("sync", lambda e, b3=b3, tk0=tk0: e.dma_start(out=g2[b3][:], in_=G.MLG2[tk0:tk0 + 128, :]), writes=["r_g2%d" % b3])
        ld5(0)
        ld5(1)
        for c in range(NCH):
            ld5(c + 2)
            b3, b = c % NB, c % 2
            P.op("vector", lambda e, b3=b3: e.tensor_tensor(out=hf[b3][:], in0=hf[b3][:], in1=hb[b3][:], op=ALU.add),
                 reads=["r_hf%d" % b3, "r_hb%d" % b3], writes=["r_hf%d" % b3])
            for h in range(4):
                P.op("scalar", lambda e, b3=b3, b=b, h=h: e.activation(out=junk[:], in_=hf[b3][:, h * 128:(h + 1) * 128], func=AF.Square, accum_out=ssq[b][:, h:h + 1]),
                     reads=["r_hf%d" % b3], writes=["r_junk", "r_ssq%d" % b])
            P.op("scalar", lambda e, b=b: e.activation(out=ssq[b][:], in_=ssq[b][:], func=AF.Sqrt, bias=G.c_eps[:], scale=1.0 / 128.0),
                 reads=["r_ssq%d" % b, "c_eps"], writes=["r_ssq%d" % b])
            P.op("vector", lambda e, b=b: e.reciprocal(ssq[b][:], ssq[b][:]), reads=["r_ssq%d" % b], writes=["r_ssq%d" % b])
            for h in range(4):
                P.op("vector", lambda e, b3=b3, b=b, h=h: e.scalar_tensor_tensor(
                    out=Yb[b][:, h * 128:(h + 1) * 128], in0=hf[b3][:, h * 128:(h + 1) * 128], scalar=ssq[b][:, h:h + 1],
                    in1=g2[b3][:, h * 128:(h + 1) * 128], op0=ALU.mult, op1=ALU.mult),
                    reads=["r_hf%d" % b3, "r_ssq%d" % b, "r_g2%d" % b3], writes=["r_Y%d" % b])
            for h in range(4):
                P.op("tensor", lambda e, b=b, h=h: e.transpose(ptr2[b][:, h, :], Yb[b][:, h * 128:(h + 1) * 128], G.ident_b[:]),
                     reads=["r_Y%d" % b, "ident_b"], writes=["r_ptr%d" % b], inc=(h == 3))
            P.op("scalar", lambda e, b=b: e.activation(out=ytb[b][:], in_=ptr2[b][:], func=AF.Copy), reads=["r_ptr%d" % b], writes=["r_yt%d" % b])
            P.dma("sync", lambda e, b=b, c=c: e.dma_start(out=G.YT[:, 4:8, c * 128:(c + 1) * 128], in_=ytb[b][:]), reads=["r_yt%d" % b], writes=[_u()])
        P.barrier()
        P.emit()


def stage_outproj(G):
    P, I, nc = G.P, G.I, G.nc
    l, j, m = 0, 1, 0
    with contextlib.ExitStack() as st:
        sb, ps = _alloc(G, st)
        wo = sb("o_wo", [128, 8, 1024], BF16)
        xt = [sb("o_xt%d" % b, [128, 8, TW], F32) for b in range(2)]
        yt = [sb("o_yt%d" % b, [128, 8, TW], BF16) for b in range(2)]
        po = [ps("o_po%d" % b, [128, TW]) for b in range(2)]
        load_w_cast(G, wo, "o_wo", I["mix_w_out"], 8, 1024)
        def ld(n):
            if n >= 16:
                return
            ti_, b_ = n + 1, n % 2
            P.dma("sync", lambda e, b_=b_, ti_=ti_: e.dma_start(out=xt[b_][:], in_=G.XT[:, :, ti_ * TW:(ti_ + 1) * TW]), reads=["XT%d" % ti_], writes=["o_xt%d" % b_])
            P.dma("sync", lambda e, b_=b_, n=n: e.dma_start(out=yt[b_][:], in_=G.YT[:, :, n * TW:(n + 1) * TW]), reads=["dramYT"], writes=["o_yt%d" % b_])
        ld(0)
        ld(1)
        for n in range(16):
            ti = n + 1
            t0 = ti * TW
            b = n % 2
            for f in range(8):
                q = f % 2
                for k in range(8):
                    P.op("tensor", lambda e, q=q, f=f, k=k, b=b: e.matmul(po[q][:], lhsT=wo[:, k, f * 128:(f + 1) * 128], rhs=yt[b][:, k, :], start=(k == 0), stop=(k == 7)),
                         reads=["o_wo", "o_yt%d" % b], writes=["o_po%d" % q], inc=(k == 7))
                gsc = G.GT[:, l, j, m, f:f + 1]
                P.op("vector", lambda e, q=q, f=f, b=b, gsc=gsc: e.scalar_tensor_tensor(
                    out=xt[b][:, f, :], in0=po[q][:], scalar=gsc, in1=xt[b][:, f, :], op0=ALU.mult, op1=ALU.add),
                    reads=["o_po%d" % q, "modc", "o_xt%d" % b], writes=["o_xt%d" % b])
            P.dma("sync", lambda e, b=b, t0=t0: e.dma_start(out=G.XT[:, :, t0:t0 + TW], in_=xt[b][:]), reads=["o_xt%d" % b], writes=["XT%d" % ti])
            ld(n + 2)
        P.barrier()
        P.emit()


def stage_sg(G):
    P, I, nc = G.P, G.I, G.nc
    l, j, m = 1, 1, 0
    with contextlib.ExitStack() as st:
        sb, ps = _alloc(G, st)
        wu = sb("g_wu", [128, 8, 2048], BF16)
        wv = sb("g_wv", [128, 8, 2048], BF16)
        wo = sb("g_wo", [128, 16, 1024], BF16)
        wsT = sb("g_wsT", [128, 8, 128], BF16)
        bs = sb("g_bs", [1, 1024], BF16)
        ones1 = sb("g_ones1", [1, 256], BF16)
        lng = sb("g_lng", [128, 2048], F32)
        lnb = sb("g_lnb", [128, 2048], F32)
        xt = [sb("g_xt%d" % b, [128, 8, TW], F32) for b in range(2)]
        hh = [sb("g_h%d" % b, [128, 8, TW], BF16) for b in range(2)]
        NS = NormScratch(G, sb, ps, "g_")
        NS.sq8 = [sb("g_sq8_%d" % k, [128, TW], BF16) for k in range(8)]
        uT = sb("g_uT", [128, 16, TW], BF16)
        vraw = [sb("g_vraw%d" % b, [128, 2048], F32) for b in range(2)]
        vn = [sb("g_vn%d" % b, [128, 2048], BF16) for b in range(2)]
        gated = sb("g_gated", [128, 16, TW], BF16)
        stats = [sb("g_stats%d" % b, [128, 4, 6], F32) for b in range(2)]
        mv = [sb("g_mv%d" % b, [128, 4], F32) for b in range(2)]
        pu = [ps("g_pu%d" % b, [128, TW]) for b in range(2)]
        pm = [ps("g_pm%d" % b, [128, 4, 128]) for b in range(2)]
        po1 = ps("g_po", [128, TW])
        po = [po1, po1]
        pv = [ps("g_pv%d" % b, [128, 512]) for b in range(2)]
        load_w_cast(G, wu, "g_wu", I["sg_w_in"][:, 0:2048], 8, 2048)
        load_w_cast(G, wv, "g_wv", I["sg_w_in"][:, 2048:4096], 8, 2048)
        load_w_cast(G, wo, "g_wo", I["sg_w_out"], 16, 1024)
        P.dma("gpsimd", lambda e: e.dma_start(out=wsT[:], in_=I["sg_w_sT"][:, :, :]), writes=["g_wsT"])
        P.dma("gpsimd", lambda e: e.dma_start(out=bs[:], in_=I["sg_b_s"][:, :]), writes=["g_bs"])
        P.op("vector", lambda e: e.memset(ones1[:], 1.0), writes=["g_ones1"])
        P.dma("sync", lambda e: e.dma_start(out=lng[:], in_=I["sg_lng_bc"][:, :]), writes=["g_lng"])
        P.dma("sync", lambda e: e.dma_start(out=lnb[:], in_=I["sg_lnb_bc"][:, :]), writes=["g_lnb"])
        tiles = list(range(1, 17))

        def ld(n):
            ti_ = tiles[n]
            xb = xt[n % 2]
            P.dma("sync", lambda e, xb=xb, ti_=ti_: e.dma_start(out=xb[:], in_=G.XT[:, :, ti_ * TW:(ti_ + 1) * TW]),
                  reads=["XT%d" % ti_], writes=["g_xt%d" % (n % 2)])
        def do_norm(n, phase=None):
            bb = n % 2
            norm_mod(G, NS, xt[bb], "g_xt%d" % bb, G.GM[:, l, j, m, :], G.SH[:, l, j, m, :], hh[bb], "g_h%d" % bb, phase=phase)
        ld(0)
        ld(1)
        do_norm(0)
        vcnt = 0
        for n, ti in enumerate(tiles):
            t0 = ti * TW
            b = n % 2
            x, xn = xt[b], "g_xt%d" % b
            h, hn = hh[b], "g_h%d" % b
            NSB = TW // 128
            for s_ in range(NSB):
                VR = vraw[s_]
                for blk in range(4):
                    q = blk % 2
                    for k in range(8):
                        P.op("tensor", lambda e, q=q, blk=blk, k=k, h=h, s_=s_: e.matmul(
                            pv[q][:], lhsT=h[:, k, s_ * 128:(s_ + 1) * 128], rhs=wv[:, k, blk * 512:(blk + 1) * 512], start=(k == 0), stop=(k == 7)),
                            reads=["g_wv", hn], writes=["g_pv%d" % q], inc=(k == 7))
                    P.op("scalar", lambda e, q=q, blk=blk, VR=VR: e.activation(out=VR[:, blk * 512:(blk + 1) * 512], in_=pv[q][:], func=AF.Gelu_apprx_tanh),
                         reads=["g_pv%d" % q], writes=["g_vraw%d" % s_])
                    P.op("vector", lambda e, blk=blk, VR=VR, s_=s_: e.bn_stats(stats[s_][:, blk, :], VR[:, blk * 512:(blk + 1) * 512]),
                         reads=["g_vraw%d" % s_], writes=["g_stats%d" % s_])
                P.op("vector", lambda e, s_=s_: e.bn_aggr(mv[s_][:, 0:2], stats[s_][:].rearrange("p a b -> p (a b)")), reads=["g_stats%d" % s_], writes=["g_mv%d" % s_])
                P.op("scalar", lambda e, s_=s_: e.activation(out=mv[s_][:, 2:3], in_=mv[s_][:, 1:2], func=AF.Sqrt, bias=G.c_eps[:], scale=1.0),
                     reads=["g_mv%d" % s_, "c_eps"], writes=["g_mv%d" % s_])
                P.op("vector", lambda e, s_=s_: e.reciprocal(mv[s_][:, 2:3], mv[s_][:, 2:3]), reads=["g_mv%d" % s_], writes=["g_mv%d" % s_])
                P.op("vector", lambda e, s_=s_: e.tensor_scalar(mv[s_][:, 3:4], mv[s_][:, 0:1], mv[s_][:, 2:3], -1.0, ALU.mult, ALU.mult),
                     reads=["g_mv%d" % s_], writes=["g_mv%d" % s_])
            for fc in range(16):
                q = fc % 2
                if fc == 6:
                    vns = []
                    for s_ in range(NSB):
                        VR = vraw[s_]
                        VN, vnn = vn[vcnt % 2], "g_vn%d" % (vcnt % 2)
                        vcnt += 1
                        vns.append((VN, vnn))
                        P.op("scalar", lambda e, VR=VR, s_=s_: e.activation(out=VR[:], in_=VR[:], func=AF.Identity, bias=mv[s_][:, 3:4], scale=mv[s_][:, 2:3]),
                             reads=["g_vraw%d" % s_, "g_mv%d" % s_], writes=["g_vraw%d" % s_])
                        P.op("vector", lambda e, VR=VR: e.tensor_tensor(out=VR[:], in0=VR[:], in1=lng[:], op=ALU.mult),
                             reads=["g_vraw%d" % s_, "g_lng"], writes=["g_vraw%d" % s_])
                        P.op("vector", lambda e, VR=VR, VN=VN: e.tensor_tensor(out=VN[:], in0=VR[:], in1=lnb[:], op=ALU.add),
                             reads=["g_vraw%d" % s_, "g_lnb"], writes=[vnn])
                for k in range(8):
                    P.op("tensor", lambda e, q=q, fc=fc, k=k, h=h: e.matmul(pu[q][:], lhsT=wu[:, k, fc * 128:(fc + 1) * 128], rhs=h[:, k, :], start=(k == 0), stop=(k == 7)),
                         reads=["g_wu", hn], writes=["g_pu%d" % q], inc=(k == 7))
                P.op("scalar", lambda e, q=q, fc=fc: e.activation(out=uT[:, fc, :], in_=pu[q][:], func=AF.Gelu_apprx_tanh), reads=["g_pu%d" % q], writes=["g_uT%d" % fc])
            for s_ in range(NSB):
                VN, vnn = vns[s_]
                for g4 in range(4):
                    q = g4 % 2
                    for f4 in range(4):
                        fc = g4 * 4 + f4
                        g_ = fc // 2
                        P.op("tensor", lambda e, q=q, fc=fc, f4=f4, g_=g_, VN=VN: e.matmul(
                            pm[q][:, f4, :], lhsT=VN[:, fc * 128:(fc + 1) * 128], rhs=wsT[:, g_, :], start=True, stop=False),
                            reads=[vnn, "g_wsT"], writes=["g_pm%d" % q], inc=False)
                        P.op("tensor", lambda e, q=q, f4=f4, g_=g_: e.matmul(
                            pm[q][:, f4, :], lhsT=ones1[0:1, 0:128], rhs=bs[0:1, g_ * 128:(g_ + 1) * 128], start=False, stop=True),
                            reads=["g_ones1", "g_bs"], writes=["g_pm%d" % q], inc=(f4 == 3))
                    P.op("vector", lambda e, q=q, g4=g4, s_=s_: e.tensor_tensor(
                        out=gated[:, g4 * 4:(g4 + 1) * 4, s_ * 128:(s_ + 1) * 128], in0=uT[:, g4 * 4:(g4 + 1) * 4, s_ * 128:(s_ + 1) * 128], in1=pm[q][:], op=ALU.mult),
                        reads=["g_uT%d" % fc_ for fc_ in range(g4 * 4, g4 * 4 + 4)] + ["g_pm%d" % q],
                        writes=["g_gated%d" % fc_ for fc_ in range(g4 * 4, g4 * 4 + 4)])
            if n + 1 < len(tiles):
                do_norm(n + 1, "A")
            for f in range(8):
                q = f % 2
                if f == 3 and n + 1 < len(tiles):
                    do_norm(n + 1, "B")
                for fc in range(16):
                    P.op("tensor", lambda e, q=q, f=f, fc=fc: e.matmul(po[q][:], lhsT=wo[:, fc, f * 128:(f + 1) * 128], rhs=gated[:, fc, :], start=(fc == 0), stop=(fc == 15)),
                         reads=["g_wo", "g_gated%d" % fc], writes=["g_po"], inc=(fc == 15))
                gsc = G.GT[:, l, j, m, f:f + 1]
                P.op("vector", lambda e, q=q, f=f, x=x, gsc=gsc: e.scalar_tensor_tensor(
                    out=x[:, f, :], in0=po[q][:], scalar=gsc, in1=x[:, f, :], op0=ALU.mult, op1=ALU.add),
                    reads=["g_po", "modc", xn], writes=[xn])
            P.dma("sync", lambda e, x=x, t0=t0: e.dma_start(out=G.XT[:, :, t0:t0 + TW], in_=x[:]), reads=[xn], writes=["XT%d" % ti])
            if n + 2 < len(tiles):
                ld(n + 2)
        P.barrier()
        P.emit()


_NC_CACHE = {}


def kernel(**inputs):
    maps = _host_prep(inputs)
    if "nc" not in _NC_CACHE:
        _NC_CACHE["nc"] = build()
    nc = _NC_CACHE["nc"]
    res = run_bass_kernel_spmd(nc, maps, core_ids=list(range(8)))
    out = np.zeros((2, 16384, 1024), np.float32)
    for core in range(8):
        b, s = core // 4, core % 4
        out[b, s * 4096:(s + 1) * 4096, :] = res.results[core]["out"]
    return out
```

```python
import contextlib
import numpy as np
import concourse.bass as bass
import concourse.mybir as mybir
from concourse.bass_utils import run_bass_kernel_spmd

F32 = mybir.dt.float32
BF16 = mybir.dt.bfloat16
AF = mybir.ActivationFunctionType
ALU = mybir.AluOpType
AX = mybir.AxisListType

D = 1024
DFF = 2816
NE = 4608
NU = 4864
OWN0 = 256
NOWN = 4096
CTX0 = 4608
TW = 256
EPS = 1e-6
NEG = -30000.0

ENGS = ("tensor", "vector", "scalar", "gpsimd", "sync")
NDMASEM = 32
NHW = 24


class Buf:
    __slots__ = ("name", "writers", "readers")

    def __init__(self, name):
        self.name = name
        self.writers = []
        self.readers = []


class Prog:
    def __init__(self, nc, st):
        self.nc = nc
        self.q = {e: [] for e in ENGS}
        self.cnt = {e: 0 for e in ENGS}
        self.seen = {e: {} for e in ENGS}
        self.dcnt = [0] * NDMASEM
        self.dnext = 0
        self.dnext_sw = 0
        self.bufs = {}
        self.esem = {e: st.enter_context(nc.semaphore("s_" + e)) for e in ENGS}
        self.dsem = [st.enter_context(nc.semaphore("d%d" % i)) for i in range(NDMASEM)]
        self.lastinc = {e: True for e in ENGS}

    def buf(self, name):
        b = self.bufs.get(name)
        if b is None:
            b = self.bufs[name] = Buf(name)
        return b

    def _bl(self, lst):
        return [self.buf(b) if isinstance(b, str) else b for b in lst]

    def _deps(self, reads, writes):
        deps = {}
        for b in reads:
            for k, v in b.writers:
                if deps.get(k, 0) < v:
                    deps[k] = v
        for b in writes:
            for k, v in b.writers:
                if deps.get(k, 0) < v:
                    deps[k] = v
            for k, v in b.readers:
                if deps.get(k, 0) < v:
                    deps[k] = v
        return deps

    def _waits(self, eng, deps):
        seen = self.seen[eng]
        waits = []
        for k, v in deps.items():
            if k == "tensor" and eng == "tensor":
                continue
            if seen.get(k, 0) >= v:
                continue
            seen[k] = v
            waits.append((k, v))
        return waits

    def _record(self, ev, reads, writes):
        k = ev[0]
        for b in writes:
            b.writers = [ev]
            b.readers = []
        for b in reads:
            if b in writes:
                continue
            b.readers = [e for e in b.readers if e[0] != k] + [ev]

    def op(self, eng, fn, reads=(), writes=(), inc=True):
        reads = self._bl(reads)
        writes = self._bl(writes)
        waits = self._waits(eng, self._deps(reads, writes))
        if inc:
            self.cnt[eng] += 1
            ev = (eng, self.cnt[eng])
        else:
            ev = (eng, self.cnt[eng] + 1)
        self.lastinc[eng] = inc
        self.q[eng].append(("op", fn, waits, inc))
        self._record(ev, reads, writes)

    def dma(self, eng, fn, reads=(), writes=()):
        reads = self._bl(reads)
        writes = self._bl(writes)
        deps = self._deps(reads, writes)
        if eng == "gpsimd":
            i = NHW + self.dnext_sw
            self.dnext_sw = (self.dnext_sw + 1) % (NDMASEM - NHW)
        else:
            i = self.dnext
            self.dnext = (self.dnext + 1) % NHW
        key = ("d", i)
        if self.dcnt[i] > 0 and deps.get(key, 0) < self.dcnt[i]:
            deps[key] = self.dcnt[i]
        waits = self._waits(eng, deps)
        self.dcnt[i] += 16
        ev = (key, self.dcnt[i])
        self.q[eng].append(("dma", fn, waits, i))
        self._record(ev, reads, writes)

    def cc(self, fn, reads=(), writes=()):
        self.op("gpsimd", fn, reads=reads, writes=writes, inc=True)

    def barrier(self):
        for e in ENGS:
            assert self.lastinc[e], e
        deps = {e: self.cnt[e] for e in ENGS if self.cnt[e] > 0}
        for i in range(NDMASEM):
            if self.dcnt[i] > 0:
                deps[("d", i)] = self.dcnt[i]
        for e in ENGS:
            d = {k: v for k, v in deps.items() if k != e}
            waits = self._waits(e, d)
            self.q[e].append(("wait", None, waits, None))

    def emit(self):
        nc = self.nc
        esem, dsem = self.esem, self.dsem

        def semof(k):
            return esem[k] if isinstance(k, str) else dsem[k[1]]

        def run(engname):
            items = self.q[engname]

            def body(e):
                for kind, fn, waits, x in items:
                    for k, v in waits:
                        e.wait_ge(semof(k), v)
                    if kind == "op":
                        ins = fn(e)
                        if x:
                            ins.then_inc(esem[engname], 1)
                    elif kind == "dma":
                        fn(e).then_inc(dsem[x], 16)
            return body

        with nc.Block() as block:
            block.tensor(run("tensor"))
            block.vector(run("vector"))
            block.scalar(run("scalar"))
            block.gpsimd(run("gpsimd"))
            block.sync(run("sync"))
        self.q = {e: [] for e in ENGS}


def _rowmap(s):
    rm = np.zeros(72, np.int64)
    rm[4:68] = s * 64 + np.arange(64)
    rm[0:4] = (s * 64 - 4 + np.arange(4)) if s > 0 else np.array([5, 6, 7, 8])
    rm[68:72] = (s * 64 + 64 + np.arange(4)) if s < 3 else np.array([248, 249, 250, 251])
    return rm


def _na_bias_tables(rpb, s):
    rm = _rowmap(s)
    out = np.full((5, 8, 128, 576), NEG, np.float32)
    reps = [0, 1, 2, 30, 31]
    cols = np.arange(64)
    c0 = np.clip(cols - 8, 0, 64 - 16)
    for ci, p in enumerate(reps):
        for r in range(2):
            i = s * 64 + 2 * p + r
            r0 = min(max(i - 4, 0), 256 - 8)
            seen_rows = set()
            for j in range(9):
                krow = int(rm[2 * p + j])
                if krow < r0 or krow >= r0 + 8 or krow in seen_rows:
                    continue
                seen_rows.add(krow)
                rr = krow - i + 7
                for qc in range(64):
                    kc = np.arange(c0[qc], c0[qc] + 16)
                    out[ci][:, r * 64 + qc, j * 64 + kc] = rpb[:, rr, kc - qc + 15]
    return out


def _rope_tables(s):
    rm = _rowmap(s)
    row = np.repeat(rm, 64).astype(np.float32)
    col = np.tile(np.arange(64), 72).astype(np.float32)
    inv = (10000.0 ** (-np.arange(32, dtype=np.float32) / 32)).astype(np.float32)
    ang = np.concatenate([row[:, None] * inv, col[:, None] * inv], axis=-1).astype(np.float32)
    cos = np.cos(ang).astype(np.float32)
    sin = np.sin(ang).astype(np.float32)
    cos2 = np.repeat(cos, 2, axis=1)
    sin2 = np.repeat(sin, 2, axis=1)
    sin2[:, 0::2] *= -1.0
    cos_u = np.ones((NU, 128), np.float32)
    sin_u = np.zeros((NU, 128), np.float32)
    cos_u[:NE] = cos2
    sin_u[:NE] = sin2
    return np.tile(cos_u, (1, 4)), np.tile(sin_u, (1, 4))


def _host_prep(inp):
    x = np.asarray(inp["x"], np.float32)
    shared = {}
    shared["w_mod"] = np.ascontiguousarray(inp["w_mod"], np.float32)
    shared["b_modT"] = np.ascontiguousarray(np.asarray(inp["b_mod"], np.float32).reshape(2, 72, 128).transpose(2, 0, 1))
    shared["norm_gT"] = np.ascontiguousarray(np.asarray(inp["norm_g"], np.float32).reshape(2, 3, 8, 128).transpose(3, 0, 1, 2))
    shared["ffn_w_in"] = np.ascontiguousarray(inp["ffn_w_in"], np.float32)
    shared["ffn_w_out"] = np.ascontiguousarray(inp["ffn_w_out"], np.float32)
    mw = np.asarray(inp["mix_w_in"], np.float32)[0]
    shared["mix_w_in"] = np.ascontiguousarray(mw[:, :3584])
    wg = np.zeros((1024, 128), np.float32)
    gb = np.zeros((128, 1), np.float32)
    gate_b = np.asarray(inp["ml_gate_b"], np.float32)[0]
    for h in range(4):
        for d in range(2):
            for q in range(2):
                wg[:, q * 64 + d * 32 + h] = mw[:, 3584 + h * 4 + d * 2 + q]
                gb[q * 64 + d * 32 + h, 0] = gate_b[h, d, q]
    shared["w_gate"] = wg
    shared["gate_b"] = gb
    shared["headg_bc"] = np.ascontiguousarray(np.broadcast_to(np.asarray(inp["ml_head_g"], np.float32)[0].reshape(1, 512), (128, 512)))
    shared["mix_w_out"] = np.ascontiguousarray(np.asarray(inp["mix_w_out"], np.float32)[0])
    shared["sg_w_in"] = np.ascontiguousarray(np.asarray(inp["sg_w_in"], np.float32)[0])
    shared["sg_w_out"] = np.ascontiguousarray(np.asarray(inp["sg_w_out"], np.float32)[0])
    shared["sg_lng_bc"] = np.ascontiguousarray(np.broadcast_to(np.asarray(inp["sg_ln_g"], np.float32)[0].reshape(1, 2048), (128, 2048)))
    shared["sg_lnb_bc"] = np.ascontiguousarray(np.broadcast_to(np.asarray(inp["sg_ln_b"], np.float32)[0].reshape(1, 2048), (128, 2048)))
    shared["sg_w_sT"] = np.ascontiguousarray(np.asarray(inp["sg_w_s"], np.float32)[0].transpose(2, 0, 1))
    shared["sg_b_s"] = np.ascontiguousarray(np.asarray(inp["sg_b_s"], np.float32)[0].reshape(1, 1024))
    shared["final_g_bc"] = np.ascontiguousarray(np.broadcast_to(np.asarray(inp["final_g"], np.float32).reshape(1, 1024), (128, 1024)))
    shared["ident"] = np.eye(128, dtype=np.float32)
    tri = np.zeros((128, 2, 128), np.float32)
    ss, tt = np.meshgrid(np.arange(128), np.arange(128), indexing="ij")
    tri[:, 0, :] = (tt >= ss)
    tri[:, 1, :] = (tt <= ss)
    shared["tri"] = tri
    sel = np.zeros((64, 8, 128), np.float32)
    for d in range(2):
        for h in range(4):
            sel[d * 32 + h, d * 4 + h, :] = 1.0
    shared["selm"] = sel
    rpb = np.asarray(inp["na_rpb"], np.float32)[0]
    c = np.asarray(inp["c"], np.float32)
    cctx = np.asarray(inp["c_ctx"], np.float32)
    ctx = np.asarray(inp["ctx"], np.float32)
    maps = []
    for core in range(8):
        b, s = core // 4, core % 4
        rm = _rowmap(s)
        tok = (rm[:, None] * 64 + np.arange(64)[None, :]).reshape(-1)
        xin = np.concatenate([x[b][tok], ctx[b]], axis=0)
        cT = np.stack([c[b].reshape(8, 128).T, cctx.reshape(8, 128).T], axis=-1)
        cos4, sin4 = _rope_tables(s)
        selv = np.zeros((128, 16), np.float32)
        for j in range(4):
            selv[:, j] = 1.0 if j < s else 0.0
            selv[:, 4 + j] = 1.0 - selv[:, j]
            selv[:, 8 + j] = 1.0 if j > s else 0.0
            selv[:, 12 + j] = 1.0 - selv[:, 8 + j]
        m = dict(shared)
        m["xin"] = np.ascontiguousarray(xin)
        m["cT"] = np.ascontiguousarray(cT.astype(np.float32))
        m["ropecos"] = cos4
        m["ropesin"] = sin4
        nb = _na_bias_tables(rpb, s)
        nbp = np.full((5, 8, 128, 640), NEG, np.float32)
        nbp[..., :576] = nb
        m["nabiasT"] = np.ascontiguousarray(nbp.reshape(5, 8, 128, 5, 128).transpose(0, 1, 4, 3, 2))
        m["selv"] = selv
        maps.append(m)
    return maps


INPUT_SHAPES = {
    "xin": [NU, 1024], "cT": [128, 8, 2], "w_mod": [2, 1024, 9216], "b_modT": [128, 2, 72], "norm_gT": [128, 2, 3, 8],
    "ffn_w_in": [2, 2, 1024, 5632], "ffn_w_out": [2, 2, 2816, 1024], "mix_w_in": [1024, 3584], "w_gate": [1024, 128],
    "gate_b": [128, 1], "headg_bc": [128, 512], "mix_w_out": [1024, 1024], "sg_w_in": [1024, 4096], "sg_w_out": [2048, 1024],
    "sg_lng_bc": [128, 2048], "sg_lnb_bc": [128, 2048], "sg_w_sT": [128, 8, 128], "sg_b_s": [1, 1024], "final_g_bc": [128, 1024],
    "ident": [128, 128], "tri": [128, 2, 128], "selm": [64, 8, 128], "ropecos": [NU, 512], "ropesin": [NU, 512],
    "nabiasT": [5, 8, 128, 5, 128], "selv": [128, 16],
}


class Ctx:
    pass


_UC = [0]


def _u():
    _UC[0] += 1
    return "u%d" % _UC[0]


def build(stages="all", dbg=()):
    nc = bass.Bass("TRN2", target_bir_lowering=False)
    I = {k: nc.dram_tensor(k, shp, F32, kind="ExternalInput").ap() for k, shp in INPUT_SHAPES.items()}
    OUT = nc.dram_tensor("out", [NOWN, 1024], F32, kind="ExternalOutput").ap()

    def scratch(name, shape, dt):
        if name in dbg:
            return nc.dram_tensor(name, shape, dt, kind="ExternalOutput").ap()
        return nc.dram_tensor(name, shape, dt).ap()

    G = Ctx()
    G.nc, G.I, G.OUT = nc, I, OUT
    G.dbg = dbg
    if dbg:
        G.DBG = {k: nc.dram_tensor(k, shp, dt, kind="ExternalOutput").ap() for k, (shp, dt) in {
            "D_MOD": ([128, 3, 2, 3, 2, 8], F32), "D_X0": ([128, 8, TW], F32), "D_X1": ([128, 8, TW], F32),
            "D_H": ([128, 8, TW], BF16), "D_HID": ([128, 22, TW], BF16), "D_RSTD": ([128, TW], F32)}.items()}
    G.XT = scratch("XT", [128, 8, NU], F32)
    G.NAQT = scratch("NAQT", [128, 4, NU], BF16)
    G.NAKT = scratch("NAKT", [128, 4, NU], BF16)
    G.NAV = scratch("NAV", [NU, 520], BF16)
    G.MLQT = scratch("MLQT", [128, 4, NU], BF16)
    G.MLKT = scratch("MLKT", [128, 4, NU], BF16)
    G.MLK = scratch("MLK", [NU, 512], BF16)
    G.MLV = scratch("MLV", [NU, 516], BF16)
    G.MLG2 = scratch("MLG2", [NU, 512], F32)
    G.GIF = scratch("GIF", [128, NU], F32)
    G.YT = scratch("YT", [128, 8, NOWN], BF16)
    G.HF = scratch("HF", [NOWN, 512], F32)
    G.HB = scratch("HB", [NOWN, 512], F32)
    G.CCI = scratch("CCI", [128, 1040], F32)
    G.CCO = scratch("CCO", [512, 1040], F32)

    with contextlib.ExitStack() as gst:
        P = Prog(nc, gst)
        G.P = P

        def gsb(name, shape, dt):
            return gst.enter_context(nc.sbuf_tensor(name, shape, dt))
        G.ident_f = gsb("ident_f", [128, 128], F32)
        G.ident_b = gsb("ident_b", [128, 128], BF16)
        G.ident8 = gsb("ident8", [128, 128], BF16)
        G.ones_b = gsb("ones_b", [128, 128], BF16)
        G.ones_f = gsb("ones_f", [128, 128], F32)
        G.c_eps = gsb("c_eps", [128, 1], F32)
        G.c_one = gsb("c_one", [128, 1], F32)
        G.SH = gsb("SH", [128, 2, 3, 2, 8], F32)
        G.GM = gsb("GM", [128, 2, 3, 2, 8], F32)
        G.GT = gsb("GT", [128, 2, 3, 2, 8], F32)

        P.dma("sync", lambda e: e.dma_start(out=G.ident_f[:], in_=I["ident"][:, :]), writes=["ident_f"])
        P.op("vector", lambda e: e.tensor_copy(G.ident_b[:], G.ident_f[:]), reads=["ident_f"], writes=["ident_b"])
        P.op("vector", lambda e: e.tensor_scalar(G.ident8[:], G.ident_f[:], 8.0, None, ALU.mult), reads=["ident_f"], writes=["ident8"])
        P.op("vector", lambda e: e.memset(G.ones_b[:], 1.0), writes=["ones_b"])
        P.op("vector", lambda e: e.memset(G.ones_f[:], 1.0), writes=["ones_f"])
        P.op("vector", lambda e: e.memset(G.c_eps[:], EPS), writes=["c_eps"])
        P.op("vector", lambda e: e.memset(G.c_one[:], 1.0), writes=["c_one"])

        ext_tiles = [(ti, 0) for ti in range(18)] + [(18, 1)]
        own_tiles = [(ti, 0) for ti in range(1, 17)]
        S = stages
        stage_mod(G)
        stage_t0(G)
        if S in ("ffn_only",):
            stage_ffn(G, 0, 0, ext_tiles)
            stage_final(G)
        elif S == "ffn_dbg":
            stage_ffn(G, 0, 0, [(1, 0)])
        else:
            stage_ffn(G, 0, 0, ext_tiles)
            stage_inproj(G)
            stage_na(G)
            stage_ml(G)
            stage_outproj(G)
            stage_ffn(G, 0, 1, own_tiles)
            stage_ffn(G, 1, 0, own_tiles)
            stage_sg(G)
            stage_ffn(G, 1, 1, own_tiles)
            stage_final(G)
    return nc


_AC = [0]


def _alloc(G, st):
    nc = G.nc
    _AC[0] += 1
    sfx = "_%d" % _AC[0]

    def sb(name, shape, dt):
        return st.enter_context(nc.sbuf_tensor(name + sfx, shape, dt))

    def ps(name, shape, dt=F32):
        return st.enter_context(nc.psum_tensor(name + sfx, shape, dt))
    return sb, ps


def stage_mod(G):
    P, I, nc = G.P, G.I, G.nc
    with contextlib.ExitStack() as st:
        sb, ps = _alloc(G, st)
        cT = sb("m_cT", [128, 8, 2], F32)
        sil = sb("m_sil", [128, 8, 2], BF16)
        wm = [sb("m_wm%d" % i, [128, 8, 1024], BF16) for i in range(3)]
        modv = sb("m_modv", [128, 2, 72, 2], F32)
        bmod = sb("m_bmod", [128, 2, 72], F32)
        ng = sb("m_ng", [128, 2, 3, 8], F32)
        pp = [ps("m_ps%d" % i, [128, 8, 2]) for i in range(2)]
        xs = [sb("t_xs%d" % i, [128, 1024], F32) for i in range(3)]
        xo = [sb("t_xo%d" % i, [128, 8, 128], F32) for i in range(2)]
        pt = [ps("t_pt%d" % i, [128, 8, 128]) for i in range(2)]
        P.dma("sync", lambda e: e.dma_start(out=cT[:], in_=I["cT"][:, :, :]), writes=["m_cT"])
        P.dma("sync", lambda e: e.dma_start(out=bmod[:], in_=I["b_modT"][:, :, :]), writes=["m_bmod"])
        P.dma("sync", lambda e: e.dma_start(out=ng[:], in_=I["norm_gT"][:, :, :, :]), writes=["m_ng"])
        P.op("scalar", lambda e: e.activation(out=sil[:], in_=cT[:], func=AF.Silu), reads=["m_cT"], writes=["m_sil"])
        NSUB = NU // 128

        def t0_load(i):
            if i < NSUB:
                b3 = i % 3
                P.dma("sync", lambda e, b3=b3, i=i: e.dma_start(out=xs[b3][:], in_=I["xin"][i * 128:(i + 1) * 128, :]), writes=["t_xs%d" % b3])

        def t0_sub(i):
            t0 = i * 128
            b, b3 = i % 2, i % 3
            for k in range(8):
                P.op("tensor", lambda e, b=b, b3=b3, k=k: e.transpose(pt[b][:, k, :], xs[b3][:, k * 128:(k + 1) * 128], G.ident_f[:]),
                     reads=["t_xs%d" % b3, "ident_f"], writes=["t_pt%d" % b], inc=(k == 7))
            if b == 0:
                P.op("vector", lambda e, b=b: e.tensor_copy(xo[b][:], pt[b][:]), reads=["t_pt%d" % b], writes=["t_xo%d" % b])
            else:
                P.op("scalar", lambda e, b=b: e.activation(out=xo[b][:], in_=pt[b][:], func=AF.Identity), reads=["t_pt%d" % b], writes=["t_xo%d" % b])
            t0_load(i + 3)
            P.dma("sync", lambda e, b=b, t0=t0: e.dma_start(out=G.XT[:, :, t0:t0 + 128], in_=xo[b][:]),
                  reads=["t_xo%d" % b], writes=["XT%d" % (t0 // TW)])

        for i in range(3):
            t0_load(i)
        n = 0
        ti = 0
        for l in range(2):
            wsrc = I["w_mod"][l].rearrange("(k p) n -> p k n", p=128)
            for blk in range(9):
                w = wm[n % 3]
                wn = "m_wm%d" % (n % 3)
                pt_ = pp[n % 2]
                pn = "m_ps%d" % (n % 2)
                P.dma("gpsimd", lambda e, w=w, blk=blk, wsrc=wsrc: e.dma_start(out=w[:], in_=wsrc[:, :, blk * 1024:(blk + 1) * 1024]),
                      writes=[wn])
                for jj in range(8):
                    for k in range(8):
                        P.op("tensor", lambda e, w=w, pt_=pt_, jj=jj, k=k: e.matmul(
                            pt_[:, jj, :], lhsT=w[:, k, jj * 128:(jj + 1) * 128], rhs=sil[:, k, :], start=(k == 0), stop=(k == 7)),
                            reads=[wn, "m_sil"], writes=[pn], inc=(jj == 7 and k == 7))
                for m in range(2):
                    P.op("vector", lambda e, pt_=pt_, l=l, blk=blk, m=m: e.tensor_tensor(
                        out=modv[:, l, blk * 8:(blk + 1) * 8, m], in0=pt_[:, :, m], in1=bmod[:, l, blk * 8:(blk + 1) * 8], op=ALU.add),
                        reads=[pn, "m_bmod"], writes=["m_modv"])
                n += 1
                for _ in range(2):
                    if ti < NSUB:
                        t0_sub(ti)
                        ti += 1
        while ti < NSUB:
            t0_sub(ti)
            ti += 1
        for l in range(2):
            for j in range(3):
                for m in range(2):
                    sh = modv[:, l, (3 * j) * 8:(3 * j) * 8 + 8, m]
                    sc = modv[:, l, (3 * j + 1) * 8:(3 * j + 1) * 8 + 8, m]
                    gt = modv[:, l, (3 * j + 2) * 8:(3 * j + 2) * 8 + 8, m]
                    P.op("vector", lambda e, sh=sh, l=l, j=j, m=m: e.tensor_copy(G.SH[:, l, j, m, :], sh), reads=["m_modv"], writes=["modc"])
                    P.op("vector", lambda e, sc=sc, l=l, j=j, m=m: e.scalar_tensor_tensor(
                        out=G.GM[:, l, j, m, :], in0=sc, scalar=1.0, in1=ng[:, l, j, :], op0=ALU.add, op1=ALU.mult),
                        reads=["m_modv", "m_ng"], writes=["modc"])
                    P.op("vector", lambda e, gt=gt, l=l, j=j, m=m: e.tensor_scalar(
                        G.GT[:, l, j, m, :], gt, (1.0 if j == 1 else 0.5), None, ALU.mult), reads=["m_modv"], writes=["modc"])
        P.barrier()
        P.emit()


def stage_t0(G):
    return


class NormScratch:
    def __init__(self, G, sb, ps, pfx, W=TW):
        self.sq = [sb(pfx + "sq%d" % i, [128, W], BF16) for i in range(2)]
        self.rs = sb(pfx + "rs", [128, W], F32)
        self.rstd = sb(pfx + "rstd", [128, W], F32)
        self.tmp = [sb(pfx + "tmp%d" % i, [128, W], F32) for i in range(2)]
        self.pn = ps(pfx + "pn", [128, W])
        self.pfx = pfx


def norm_mod(G, NS, x, xname, gm, sh, h, hname, W=TW, phase=None):
    P = G.P
    pfx = NS.pfx
    for k in range(8):
        if phase is None:
            sq, sqn = NS.sq[k % 2], pfx + "sq%d" % (k % 2)
        else:
            sq, sqn = NS.sq8[k], pfx + "sq8_%d" % k
        if phase in (None, "A"):
            P.op("scalar", lambda e, sq=sq, k=k: e.activation(out=sq[:, :W], in_=x[:, k, :], func=AF.Square), reads=[xname], writes=[sqn])
        if phase in (None, "B"):
            P.op("tensor", lambda e, sq=sq, k=k: e.matmul(NS.pn[:, :W], lhsT=G.ones_b[:], rhs=sq[:, :W], start=(k == 0), stop=(k == 7)),
                 reads=[sqn, "ones_b"], writes=[pfx + "pn"], inc=True)
    if phase == "A":
        return
    P.op("scalar", lambda e: e.activation(out=NS.rs[:, :W], in_=NS.pn[:, :W], func=AF.Sqrt, bias=G.c_eps[:], scale=1.0 / 1024.0),
         reads=[pfx + "pn", "c_eps"], writes=[pfx + "rs"])
    P.op("vector", lambda e: e.reciprocal(NS.rstd[:, :W], NS.rs[:, :W]), reads=[pfx + "rs"], writes=[pfx + "rstd"])
    for k in range(8):
        tmp = NS.tmp[k % 2]
        tn = pfx + "tmp%d" % (k % 2)
        P.op("vector", lambda e, tmp=tmp, k=k: e.tensor_tensor(out=tmp[:, :W], in0=x[:, k, :], in1=NS.rstd[:, :W], op=ALU.mult),
             reads=[xname, pfx + "rstd"], writes=[tn])
        P.op("scalar", lambda e, tmp=tmp, k=k: e.activation(out=h[:, k, :], in_=tmp[:, :W], func=AF.Identity, bias=sh[:, k:k + 1], scale=gm[:, k:k + 1]),
             reads=[tn, "modc"], writes=[hname])


def load_w_cast(G, dst, dname, src, nk, ncols, step=1024):
    P = G.P
    v = src.rearrange("(k p) n -> p k n", p=128)
    for c0 in range(0, ncols, step):
        c1 = min(ncols, c0 + step)
        P.dma("gpsimd", lambda e, c0=c0, c1=c1: e.dma_start(out=dst[:, :, c0:c1], in_=v[:, :, c0:c1]), writes=[dname])


def stage_ffn(G, l, i, tiles):
    P, I, nc = G.P, G.I, G.nc
    j = 0 if i == 0 else 2
    with contextlib.ExitStack() as st:
        sb, ps = _alloc(G, st)
        wi = sb("f_wi", [128, 8, 2 * DFF], BF16)
        wo = sb("f_wo", [128, 22, 1024], BF16)
        xt = [sb("f_xt%d" % b, [128, 8, TW], F32) for b in range(2)]
        hh = [sb("f_h%d" % b, [128, 8, TW], BF16) for b in range(2)]
        hid = sb("f_hid", [128, 22, TW], BF16)
        sa = [sb("f_sa%d" % b, [128, TW], F32) for b in range(2)]
        NS = NormScratch(G, sb, ps, "f_")
        NS.sq8 = [sb("f_sq8_%d" % k, [128, TW], BF16) for k in range(8)]
        pa = [ps("f_pa%d" % b, [128, TW]) for b in range(2)]
        pb = [ps("f_pb%d" % b, [128, TW]) for b in range(2)]
        po = [ps("f_po%d" % b, [128, TW]) for b in range(2)]
        wv_ = I["ffn_w_in"][l, i].rearrange("(k p) n -> p k n", p=128)
        for pc in (0, 2, 3, 1, 4, 5):
            c0, c1 = pc * 1024, min(2 * DFF, (pc + 1) * 1024)
            P.dma("gpsimd", lambda e, c0=c0, c1=c1: e.dma_start(out=wi[:, :, c0:c1], in_=wv_[:, :, c0:c1]), writes=["f_wi%d" % pc])
        load_w_cast(G, wo, "f_wo", I["ffn_w_out"][l, i], 22, 1024)
        def ld(n):
            ti_ = tiles[n][0]
            xb = xt[n % 2]
            P.dma("sync", lambda e, xb=xb, ti_=ti_: e.dma_start(out=xb[:], in_=G.XT[:, :, ti_ * TW:(ti_ + 1) * TW]),
                  reads=["XT%d" % ti_], writes=["f_xt%d" % (n % 2)])
        def do_norm(n, phase=None):
            ti_, m_ = tiles[n]
            bb = n % 2
            norm_mod(G, NS, xt[bb], "f_xt%d" % bb, G.GM[:, l, j, m_, :], G.SH[:, l, j, m_, :], hh[bb], "f_h%d" % bb, phase=phase)
        ld(0)
        if len(tiles) > 1:
            ld(1)
        do_norm(0)
        for n, (ti, m) in enumerate(tiles):
            t0 = ti * TW
            b = n % 2
            x, xn = xt[b], "f_xt%d" % b
            h, hn = hh[b], "f_h%d" % b
            for jj in range(22):
                q = jj % 2
                for half, pp, pn in ((0, pa[q], "f_pa%d" % q), (1, pb[q], "f_pb%d" % q)):
                    c0 = half * DFF + jj * 128
                    for k in range(8):
                        P.op("tensor", lambda e, pp=pp, c0=c0, k=k, h=h: e.matmul(
                            pp[:], lhsT=wi[:, k, c0:c0 + 128], rhs=h[:, k, :], start=(k == 0), stop=(k == 7)),
                            reads=["f_wi%d" % (c0 // 1024), "f_wi%d" % ((c0 + 127) // 1024), hn], writes=[pn], inc=(k == 7))
                P.op("scalar", lambda e, q=q: e.activation(out=sa[q][:], in_=pa[q][:], func=AF.Silu), reads=["f_pa%d" % q], writes=["f_sa%d" % q])
                P.op("vector", lambda e, q=q, jj=jj: e.tensor_tensor(out=hid[:, jj, :], in0=sa[q][:], in1=pb[q][:], op=ALU.mult),
                     reads=["f_sa%d" % q, "f_pb%d" % q], writes=["f_hid%d" % jj])
            if n + 1 < len(tiles):
                do_norm(n + 1, "A")
            for f in range(8):
                q = f % 2
                if f == 3 and n + 1 < len(tiles):
                    do_norm(n + 1, "B")
                for jj in range(22):
                    P.op("tensor", lambda e, q=q, f=f, jj=jj: e.matmul(
                        po[q][:], lhsT=wo[:, jj, f * 128:(f + 1) * 128], rhs=hid[:, jj, :], start=(jj == 0), stop=(jj == 21)),
                        reads=["f_wo", "f_hid%d" % jj], writes=["f_po%d" % q], inc=(jj == 21))
                gsc = G.GT[:, l, j, m, f:f + 1]
                P.op("vector", lambda e, q=q, f=f, x=x, gsc=gsc: e.scalar_tensor_tensor(
                    out=x[:, f, :], in0=po[q][:], scalar=gsc, in1=x[:, f, :], op0=ALU.mult, op1=ALU.add),
                    reads=["f_po%d" % q, "modc", xn], writes=[xn])
            if G.dbg and ti == 1:
                P.dma("sync", lambda e: e.dma_start(out=G.DBG["D_HID"][:, :, :], in_=hid[:]), reads=["f_hid%d" % q for q in range(22)], writes=["dbg3"])
                P.dma("sync", lambda e, x=x: e.dma_start(out=G.DBG["D_X1"][:, :, :], in_=x[:]), reads=[xn], writes=["dbg4"])
            P.dma("sync", lambda e, x=x, t0=t0: e.dma_start(out=G.XT[:, :, t0:t0 + TW], in_=x[:]), reads=[xn], writes=["XT%d" % ti])
            if n + 2 < len(tiles):
                ld(n + 2)
        P.barrier()
        P.emit()


def stage_final(G):
    P, I, nc = G.P, G.I, G.nc
    with contextlib.ExitStack() as st:
        sb, ps = _alloc(G, st)
        fg = sb("k_fg", [128, 1024], F32)
        xs = [sb("k_xs%d" % b, [128, 8, 128], F32) for b in range(2)]
        junk = sb("k_junk", [128, 1024], F32)
        yo = [sb("k_yo%d" % b, [128, 1024], F32) for b in range(2)]
        ssq = sb("k_ssq", [128, 2], F32)
        rs = sb("k_rs", [128, 2], F32)
        pt = [ps("k_pt%d" % b, [128, 8, 128]) for b in range(2)]
        P.dma("sync", lambda e: e.dma_start(out=fg[:], in_=I["final_g_bc"][:, :]), writes=["k_fg"])
        for i in range(NOWN // 128):
            t0 = OWN0 + i * 128
            b = i % 2
            P.dma("gpsimd", lambda e, b=b, t0=t0: e.dma_start(out=xs[b][:], in_=G.XT[:, :, t0:t0 + 128]),
                  reads=["XT%d" % (t0 // TW)], writes=["k_xs%d" % b])
            for k in range(8):
                P.op("tensor", lambda e, b=b, k=k: e.transpose(pt[b][:, k, :], xs[b][:, k, :], G.ident_f[:]),
                     reads=["k_xs%d" % b, "ident_f"], writes=["k_pt%d" % b], inc=(k == 7))
            ptf = pt[b][:].rearrange("p k n -> p (k n)")
            P.op("scalar", lambda e, b=b, ptf=ptf: e.activation(out=junk[:], in_=ptf, func=AF.Square, accum_out=ssq[:, b:b + 1]),
                 reads=["k_pt%d" % b], writes=["k_junk", "k_ssq%d" % b])
            P.op("scalar", lambda e, b=b: e.activation(out=rs[:, b:b + 1], in_=ssq[:, b:b + 1], func=AF.Sqrt, bias=G.c_eps[:], scale=1.0 / 1024.0),
                 reads=["k_ssq%d" % b, "c_eps"], writes=["k_rs%d" % b])
            P.op("vector", lambda e, b=b: e.reciprocal(rs[:, b:b + 1], rs[:, b:b + 1]), reads=["k_rs%d" % b], writes=["k_rs%d" % b])
            P.op("vector", lambda e, b=b, ptf=ptf: e.scalar_tensor_tensor(
                out=yo[b][:], in0=ptf, scalar=rs[:, b:b + 1], in1=fg[:], op0=ALU.mult, op1=ALU.mult),
                reads=["k_pt%d" % b, "k_rs%d" % b, "k_fg"], writes=["k_yo%d" % b])
            P.dma("sync", lambda e, b=b, i=i: e.dma_start(out=G.OUT[i * 128:(i + 1) * 128, :], in_=yo[b][:]),
                  reads=["k_yo%d" % b], writes=["OUT"])
        P.barrier()
        P.emit()


def stage_inproj(G):
    P, I, nc = G.P, G.I, G.nc
    l, j = 0, 1
    tiles = [(ti, 0) for ti in range(18)] + [(18, 1)]
    with contextlib.ExitStack() as st:
        sb, ps = _alloc(G, st)
        wfm = sb("i_wfm", [128, 8, 1024], BF16)
        wg = sb("i_wg", [128, 8, 128], BF16)
        wtm = sb("i_wtm", [128, 8, 2560], BF16)
        xt = [sb("i_xt%d" % b, [128, 8, TW], F32) for b in range(2)]
        hh = [sb("i_h%d" % b, [128, 8, TW], BF16) for b in range(2)]
        NS = NormScratch(G, sb, ps, "i_")
        NS.sq8 = [sb("i_sq8_%d" % k, [128, TW], BF16) for k in range(8)]
        fm = [sb("i_fm%d" % b, [128, 8, TW], BF16) for b in range(2)]
        gts = [sb("i_gt%d" % b, [128, TW], F32) for b in range(2)]
        cosT = [sb("i_cos%d" % b, [128, 512], F32) for b in range(2)]
        sinT = [sb("i_sin%d" % b, [128, 512], F32) for b in range(2)]
        hg = sb("i_hg", [128, 512], F32)
        vt = [sb("i_vt%d" % b, [128, 8, 65], BF16) for b in range(2)]
        mv = [sb("i_mv%d" % b, [128, 4, 129], BF16) for b in range(2)]
        xs = [sb("i_xs%d" % b, [128, 512], F32) for b in range(2)]
        r1 = [sb("i_r1%d" % b, [128, 512], F32) for b in range(2)]
        r2 = [sb("i_r2%d" % b, [128, 512], F32) for b in range(2)]
        qr = [sb("i_qr%d" % b, [128, 512], BF16) for b in range(2)]
        sg = sb("i_sg", [128, 512], F32)
        g2 = [sb("i_g2%d" % b, [128, 512], F32) for b in range(2)]
        tq = [sb("i_tq%d" % b, [128, 4, 128], BF16) for b in range(2)]
        pfm = [ps("i_pfm%d" % b, [128, TW]) for b in range(2)]
        ptm = [ps("i_ptm%d" % b, [128, 512]) for b in range(2)]
        ptr = [ps("i_ptr%d" % b, [128, 4, 128], BF16) for b in range(2)]
        load_w_cast(G, wfm, "i_wfm", I["mix_w_in"][:, 0:1024], 8, 1024)
        load_w_cast(G, wg, "i_wg", I["w_gate"], 8, 128)
        load_w_cast(G, wtm, "i_wtm", I["mix_w_in"][:, 1024:3584], 8, 2560)
        P.dma("sync", lambda e: e.dma_start(out=hg[:], in_=I["headg_bc"][:, :]), writes=["i_hg"])
        for b in range(2):
            P.op("vector", lambda e, b=b: e.memset(vt[b][:], 1.0), writes=["i_vt%d" % b])
            P.op("vector", lambda e, b=b: e.memset(mv[b][:], 1.0), writes=["i_mv%d" % b])

        def ld(n):
            ti_ = tiles[n][0]
            xb = xt[n % 2]
            P.dma("gpsimd", lambda e, xb=xb, ti_=ti_: e.dma_start(out=xb[:], in_=G.XT[:, :, ti_ * TW:(ti_ + 1) * TW]),
                  reads=["XT%d" % ti_], writes=["i_xt%d" % (n % 2)])
        ld(0)
        cnt = {"s": 0, "r": 0, "t": 0}
        deferred = []

        def rope_and_T(pt_, ptn, scale, dstT, tmaj_dst, ts0):
            a = cnt["r"] % 2
            cnt["r"] += 1
            sb_ = cnt["s"] % 2
            X, R1, R2, QR, TQ, PT = xs[a], r1[a], r2[a], qr[a], tq[a], ptr[a]
            xn_, r1n, r2n, qrn, tqn, ptn2 = "i_xs%d" % a, "i_r1%d" % a, "i_r2%d" % a, "i_qr%d" % a, "i_tq%d" % a, "i_ptr%d" % a
            P.op("scalar", lambda e: e.activation(out=X[:], in_=pt_[:], func=AF.Copy, scale=scale), reads=[ptn], writes=[xn_])
            P.op("vector", lambda e: e.tensor_tensor(out=R1[:], in0=X[:], in1=cosT[sb_][:], op=ALU.mult), reads=[xn_, "i_cos%d" % sb_], writes=[r1n])
            Xv = X[:].rearrange("p (i t) -> p i t", t=2)
            Sv = sinT[sb_][:].rearrange("p (i t) -> p i t", t=2)
            Rv = R2[:].rearrange("p (i t) -> p i t", t=2)
            P.op("vector", lambda e: e.tensor_tensor(out=Rv[:, :, 0], in0=Xv[:, :, 1], in1=Sv[:, :, 0], op=ALU.mult),
                 reads=[xn_, "i_sin%d" % sb_], writes=[r2n])
            P.op("vector", lambda e: e.tensor_tensor(out=Rv[:, :, 1], in0=Xv[:, :, 0], in1=Sv[:, :, 1], op=ALU.mult),
                 reads=[xn_, "i_sin%d" % sb_], writes=[r2n])
            P.op("vector", lambda e: e.tensor_tensor(out=QR[:], in0=R1[:], in1=R2[:], op=ALU.add), reads=[r1n, r2n], writes=[qrn])
            if tmaj_dst is not None:
                P.dma("sync", lambda e: e.dma_start(out=tmaj_dst[ts0:ts0 + 128, :], in_=QR[:]), reads=[qrn], writes=[_u()])
            def later():
                for hd in range(4):
                    P.op("tensor", lambda e, hd=hd: e.transpose(PT[:, hd, :], QR[:, hd * 128:(hd + 1) * 128], G.ident_b[:]),
                         reads=[qrn, "ident_b"], writes=[ptn2], inc=(hd == 3))
                P.op("scalar", lambda e: e.activation(out=TQ[:], in_=PT[:], func=AF.Copy), reads=[ptn2], writes=[tqn])
                P.dma("sync", lambda e: e.dma_start(out=dstT[:, :, ts0:ts0 + 128], in_=TQ[:]), reads=[tqn], writes=[_u()])
            deferred.append(later)

        for n, (ti, m) in enumerate(tiles):
            t0 = ti * TW
            b = n % 2
            x, xn = xt[b], "i_xt%d" % b
            h, hn = hh[b], "i_h%d" % b
            if n + 1 < len(tiles):
                ld(n + 1)
            if n == 0:
                norm_mod(G, NS, x, xn, G.GM[:, l, j, m, :], G.SH[:, l, j, m, :], h, hn)
            FM, fmn = fm[b], "i_fm%d" % b
            no_q = ti in (0, 17, 18)
            blks = [0] if ti in (0, 17) else ([0, 2, 3] if ti == 18 else [0, 1, 2, 3, 4])
            for fc in (range(4, 8) if no_q else range(8)):
                q = fc % 2
                for k in range(8):
                    P.op("tensor", lambda e, q=q, fc=fc, k=k, h=h: e.matmul(
                        pfm[q][:], lhsT=wfm[:, k, fc * 128:(fc + 1) * 128], rhs=h[:, k, :], start=(k == 0), stop=(k == 7)),
                        reads=["i_wfm", hn], writes=["i_pfm%d" % q], inc=(k == 7))
                P.op("scalar", lambda e, q=q, fc=fc, FM=FM: e.activation(out=FM[:, fc, :], in_=pfm[q][:], func=AF.Copy),
                     reads=["i_pfm%d" % q], writes=[fmn])
            if not no_q:
                P.dma("sync", lambda e, FM=FM, t0=t0: e.dma_start(out=G.NAQT[:, :, t0:t0 + TW], in_=FM[:, 0:4, :]), reads=[fmn], writes=[_u()])
            P.dma("sync", lambda e, FM=FM, t0=t0: e.dma_start(out=G.NAKT[:, :, t0:t0 + TW], in_=FM[:, 4:8, :]), reads=[fmn], writes=[_u()])
            GTS, gtn = gts[b], "i_gt%d" % b
            for k in range(8):
                P.op("tensor", lambda e, k=k, h=h: e.matmul(pfm[0][:], lhsT=wg[:, k, :], rhs=h[:, k, :], start=(k == 0), stop=(k == 7)),
                     reads=["i_wg", hn], writes=["i_pfm0"], inc=(k == 7))
            P.op("vector", lambda e, GTS=GTS: e.tensor_copy(GTS[:], pfm[0][:]), reads=["i_pfm0"], writes=[gtn])
            P.dma("sync", lambda e, GTS=GTS, t0=t0: e.dma_start(out=G.GIF[:, t0:t0 + TW], in_=GTS[:]), reads=[gtn], writes=[_u()])
            if n + 1 < len(tiles):
                ti2, m2 = tiles[n + 1]
                b2 = (n + 1) % 2
                norm_mod(G, NS, xt[b2], "i_xt%d" % b2, G.GM[:, l, j, m2, :], G.SH[:, l, j, m2, :], hh[b2], "i_h%d" % b2, phase="A")
            for s_ in range(TW // 128):
                if s_ == 1 and n + 1 < len(tiles):
                    norm_mod(G, NS, xt[b2], "i_xt%d" % b2, G.GM[:, l, j, m2, :], G.SH[:, l, j, m2, :], hh[b2], "i_h%d" % b2, phase="B")
                ts0 = t0 + s_ * 128
                sbi = cnt["s"] % 2
                P.dma("gpsimd", lambda e, sbi=sbi, ts0=ts0: e.dma_start(out=cosT[sbi][:], in_=I["ropecos"][ts0:ts0 + 128, :]), writes=["i_cos%d" % sbi])
                P.dma("gpsimd", lambda e, sbi=sbi, ts0=ts0: e.dma_start(out=sinT[sbi][:], in_=I["ropesin"][ts0:ts0 + 128, :]), writes=["i_sin%d" % sbi])
                for blk in blks:
                    a = cnt["t"] % 2
                    cnt["t"] += 1
                    PT_, ptn = ptm[a], "i_ptm%d" % a
                    for k in range(8):
                        P.op("tensor", lambda e, PT_=PT_, k=k, h=h, s_=s_, blk=blk: e.matmul(
                            PT_[:], lhsT=h[:, k, s_ * 128:(s_ + 1) * 128], rhs=wtm[:, k, blk * 512:(blk + 1) * 512], start=(k == 0), stop=(k == 7)),
                            reads=["i_wtm", hn], writes=[ptn], inc=(k == 7))
                    while len(deferred) > (1 if blk == 3 else 0):
                        deferred.pop(0)()
                    if blk == 0:
                        VT = vt[sbi]
                        P.op("scalar", lambda e, VT=VT, PT_=PT_: e.activation(out=VT[:, :, 0:64], in_=PT_[:].rearrange("p (h d) -> p h d", d=64), func=AF.Copy),
                             reads=[ptn], writes=["i_vt%d" % sbi])
                        P.dma("sync", lambda e, VT=VT, ts0=ts0: e.dma_start(out=G.NAV[ts0:ts0 + 128, :], in_=VT[:].rearrange("p h d -> p (h d)")),
                              reads=["i_vt%d" % sbi], writes=[_u()])
                    elif blk == 1:
                        rope_and_T(PT_, ptn, 1.0, G.MLQT, None, ts0)
                    elif blk == 2:
                        rope_and_T(PT_, ptn, 128.0 ** -0.5, G.MLKT, G.MLK, ts0)
                    elif blk == 3:
                        MV = mv[sbi]
                        P.op("scalar", lambda e, MV=MV, PT_=PT_: e.activation(out=MV[:, :, 0:128], in_=PT_[:].rearrange("p (h d) -> p h d", d=128), func=AF.Copy),
                             reads=[ptn], writes=["i_mv%d" % sbi])
                        P.dma("sync", lambda e, MV=MV, ts0=ts0: e.dma_start(out=G.MLV[ts0:ts0 + 128, :], in_=MV[:].rearrange("p h d -> p (h d)")),
                              reads=["i_mv%d" % sbi], writes=[_u()])
                    else:
                        G2 = g2[sbi]
                        P.op("scalar", lambda e, PT_=PT_: e.activation(out=sg[:], in_=PT_[:], func=AF.Sigmoid), reads=[ptn], writes=["i_sg"])
                        P.op("vector", lambda e, G2=G2: e.tensor_tensor(out=G2[:], in0=sg[:], in1=hg[:], op=ALU.mult), reads=["i_sg", "i_hg"], writes=["i_g2%d" % sbi])
                        P.dma("sync", lambda e, G2=G2, ts0=ts0: e.dma_start(out=G.MLG2[ts0:ts0 + 128, :], in_=G2[:]), reads=["i_g2%d" % sbi], writes=[_u()])
                while deferred:
                    deferred.pop(0)()
                cnt["s"] += 1
        P.barrier()
        P.emit()


def stage_na(G):
    P, I, nc = G.P, G.I, G.nc
    with contextlib.ExitStack() as st:
        sb, ps = _alloc(G, st)
        KT = sb("n_KT", [128, 4, NU], BF16)
        V = sb("n_V", [128, 38, 520], BF16)
        QT = sb("n_QT", [128, 4, NOWN], BF16)
        BI = sb("n_BI", [128, 5, 8, 5, 128], BF16)
        sAb = [sb("n_sAb%d" % b, [128, 4, 128], F32) for b in range(2)]
        sBb = [sb("n_sBb%d" % b, [128, 128], F32) for b in range(2)]
        pt = [sb("n_pt%d" % b, [128, 7, 128], BF16) for b in range(2)]
        ya = [sb("n_ya%d" % b, [128, 512], BF16) for b in range(2)]
        rec = [sb("n_rec%d" % b, [128, 8], F32) for b in range(2)]
        yt = [sb("n_yt%d" % b, [128, 4, 128], BF16) for b in range(2)]
        sA = [ps("n_sA%d" % b, [128, 4, 128]) for b in range(2)]
        sB = [ps("n_sB%d" % b, [128, 4, 128]) for b in range(2)]
        O = ps("n_O", [128, 8, 128])
        ptr = ps("n_ptr", [128, 4, 128], BF16)
        P.dma("sync", lambda e: e.dma_start(out=KT[:], in_=G.NAKT[:, :, :]), reads=["dramNA"], writes=["n_KT"])
        P.dma("sync", lambda e: e.dma_start(out=V[:], in_=G.NAV.rearrange("(c p) n -> p c n", p=128)), reads=["dramNA"], writes=["n_V"])
        P.dma("sync", lambda e: e.dma_start(out=QT[:], in_=G.NAQT[:, :, OWN0:OWN0 + NOWN]), reads=["dramNA"], writes=["n_QT"])
        for c5 in range(5):
            P.dma("gpsimd", lambda e, c5=c5: e.dma_start(out=BI[:, c5], in_=I["nabiasT"][c5].rearrange("h k j q -> k h j q")), writes=["n_BI"])
        def scores(p, h, a):
            cls = 0 if p == 0 else 1 if p == 1 else 3 if p == 30 else 4 if p == 31 else 2
            hc, b0 = h // 2, (h % 2) * 64
            q_ap = QT[b0:b0 + 64, hc, p * 128:(p + 1) * 128]
            SA, SB, PT = sA[a], sB[a], pt[a]
            san, sbn, ptn = "n_sA%d" % a, "n_sB%d" % a, "n_pt%d" % a
            for jj in range(4):
                k0 = (p + jj) * 128
                P.op("tensor", lambda e, jj=jj, k0=k0: e.matmul(
                    SA[:, jj, :], lhsT=KT[b0:b0 + 64, hc, k0:k0 + 128], rhs=q_ap, start=True, stop=True),
                    reads=["n_KT", "n_QT"], writes=[san], inc=(jj == 3))
            k0h = (p + 4) * 128
            P.op("tensor", lambda e: e.matmul(
                SB[0:64, 0, :], lhsT=KT[b0:b0 + 64, hc, k0h:k0h + 64], rhs=q_ap, start=True, stop=True),
                reads=["n_KT", "n_QT"], writes=[sbn], inc=False)
            for c in range(2):
                k0 = CTX0 + c * 128
                P.op("tensor", lambda e, c=c, k0=k0: e.matmul(
                    SB[:, 1 + c, :], lhsT=KT[b0:b0 + 64, hc, k0:k0 + 128], rhs=q_ap, start=True, stop=True),
                    reads=["n_KT", "n_QT"], writes=[sbn], inc=(c == 1))
            AB, BB = sAb[a], sBb[a]
            abn, bbn = "n_sAb%d" % a, "n_sBb%d" % a
            P.op("vector", lambda e: e.scalar_tensor_tensor(
                out=AB[:], in0=SA[:], scalar=0.125, in1=BI[:, cls, h, 0:4, :], op0=ALU.mult, op1=ALU.add),
                reads=[san, "n_BI"], writes=[abn])
            P.op("vector", lambda e: e.scalar_tensor_tensor(
                out=BB[0:64, :], in0=SB[0:64, 0, :], scalar=0.125, in1=BI[0:64, cls, h, 4, :], op0=ALU.mult, op1=ALU.add),
                reads=[sbn, "n_BI"], writes=[bbn])
            P.op("scalar", lambda e: e.activation(out=PT[:, 0:4, :], in_=AB[:], func=AF.Exp), reads=[abn], writes=[ptn])
            P.op("scalar", lambda e: e.activation(out=PT[:, 5:7, :], in_=SB[:, 1:3, :], func=AF.Exp, scale=0.125), reads=[sbn, bbn], writes=[ptn])
            P.op("scalar", lambda e: e.activation(out=PT[0:64, 4, :], in_=BB[0:64, :], func=AF.Exp), reads=[bbn], writes=[ptn])

        def pv(p, h, a):
            pb2 = p % 2
            PT, ptn = pt[a], "n_pt%d" % a
            specs = [(jj, 128, p + jj) for jj in range(4)] + [(4, 64, p + 4), (5, 128, 36), (6, 128, 37)]
            for si, (slot, nk, vc) in enumerate(specs):
                P.op("tensor", lambda e, slot=slot, nk=nk, vc=vc, si=si: e.matmul(
                    O[:, h, 0:65], lhsT=PT[0:nk, slot, :], rhs=V[0:nk, vc, h * 65:(h + 1) * 65], start=(si == 0), stop=(si == 6)),
                    reads=[ptn, "n_V"], writes=["n_O%d" % (h // 4)], inc=(si == 6))
            if h in (3, 7):
                hf_ = h // 4
                R, YA = rec[pb2], ya[pb2]
                rn, yan = "n_rec%d_%d" % (pb2, hf_), "n_ya%d" % pb2
                P.op("vector", lambda e: e.reciprocal(R[:, hf_ * 4:hf_ * 4 + 4], O[:, hf_ * 4:hf_ * 4 + 4, 64]), reads=["n_O%d" % hf_], writes=[rn])
                for h2 in range(hf_ * 4, hf_ * 4 + 4):
                    P.op("scalar", lambda e, h2=h2: e.activation(out=YA[:, h2 * 64:(h2 + 1) * 64], in_=O[:, h2, 0:64], func=AF.Copy, scale=R[:, h2:h2 + 1]),
                         reads=["n_O%d" % hf_, rn], writes=[yan])
            if h == 7:
                YA, YT_ = ya[pb2], yt[pb2]
                yan, ytn = "n_ya%d" % pb2, "n_yt%d" % pb2
                for c in range(4):
                    P.op("tensor", lambda e, c=c: e.transpose(ptr[:, c, :], YA[:, c * 128:(c + 1) * 128], G.ident_b[:]),
                         reads=[yan, "ident_b"], writes=["n_ptr"], inc=(c == 3))
                P.op("vector", lambda e: e.tensor_copy(YT_[:], ptr[:]), reads=["n_ptr"], writes=[ytn])
                P.dma("sync", lambda e: e.dma_start(out=G.YT[:, 0:4, p * 128:(p + 1) * 128], in_=YT_[:]), reads=[ytn], writes=[_u()])

        items = [(p, h) for p in range(32) for h in range(8)]
        scores(items[0][0], items[0][1], 0)
        for i_, (p, h) in enumerate(items):
            if i_ + 1 < len(items):
                scores(items[i_ + 1][0], items[i_ + 1][1], (i_ + 1) % 2)
            pv(p, h, i_ % 2)
        P.barrier()
        P.emit()


def stage_ml(G):
    P, I, nc = G.P, G.I, G.nc
    NCH = 32
    with contextlib.ExitStack() as st:
        sb, ps = _alloc(G, st)
        TOK = sb("l_TOK", [128, 34, 5, 8], F32)
        EBEND = sb("l_EBEND", [128, 8, 32], F32)
        ATOT = sb("l_ATOT", [128, 8], F32)
        with contextlib.ExitStack() as st1:
            sb1, ps1 = _alloc(G, st1)
            LI = sb1("l_LI", [64, NU], F32)
            SP = sb1("l_SP", [64, NU], F32)
            CL = sb1("l_CL", [64, NU], F32)
            CG = sb1("l_CG", [64, NU], F32)
            TM = sb1("l_TM", [64, NU], F32)
            OQ = [sb1("l_OQ%d" % b, [64, NU], F32) for b in range(2)]
            gbI = sb1("l_gbI", [64, 1], F32)
            gbF = sb1("l_gbF", [64, 1], F32)
            CE = sb1("l_CE", [64, 32], F32)
            ntot = sb1("l_ntot", [64, 2], F32)
            tot = sb1("l_tot", [64, 2], F32)
            SELM = sb1("l_SELM", [64, 8, 128], F32)
            ptr = [ps1("l_ptr%d" % b, [128, 8, 64]) for b in range(2)]
            pe = ps1("l_pe", [128, 8, 32])
            pa = ps1("l_pa", [128, 8, 2])
            own = slice(OWN0, OWN0 + NOWN)
            cxs = slice(CTX0, CTX0 + 256)
            P.dma("sync", lambda e: e.dma_start(out=LI[:], in_=G.GIF[0:64, :]), reads=["dramML"], writes=["l_LI"])
            P.dma("sync", lambda e: e.dma_start(out=SP[:], in_=G.GIF[64:128, :]), reads=["dramML"], writes=["l_SP"])
            P.dma("sync", lambda e: e.dma_start(out=gbI[:], in_=I["gate_b"][0:64, :]), writes=["l_gbI"])
            P.dma("sync", lambda e: e.dma_start(out=gbF[:], in_=I["gate_b"][64:128, :]), writes=["l_gbF"])
            P.dma("sync", lambda e: e.dma_start(out=SELM[:], in_=I["selm"][:, :, :]), writes=["l_SELM"])
            P.op("vector", lambda e: e.tensor_scalar(gbF[:], gbF[:], -1.0, None, ALU.mult), reads=["l_gbF"], writes=["l_gbF"])
            P.op("scalar", lambda e: e.activation(out=LI[:], in_=LI[:], func=AF.Identity, bias=gbI[:]), reads=["l_LI", "l_gbI"], writes=["l_LI"])
            P.op("scalar", lambda e: e.activation(out=SP[:], in_=SP[:], func=AF.Exp, bias=gbF[:], scale=-1.0), reads=["l_SP", "l_gbF"], writes=["l_SP"])
            P.op("scalar", lambda e: e.activation(out=SP[:], in_=SP[:], func=AF.Ln, bias=G.c_one[0:64, :]), reads=["l_SP", "c_one"], writes=["l_SP"])
            P.op("vector", lambda e: e.memset(TM[:], 1.0), writes=["l_TM"])
            P.op("vector", lambda e: e.tensor_tensor_scan(out=CG[:, own], data0=TM[:, own], data1=SP[:, own], initial=0.0, op0=ALU.mult, op1=ALU.add),
                 reads=["l_TM", "l_SP"], writes=["l_CG"])
            P.op("vector", lambda e: e.tensor_tensor_scan(out=CG[:, cxs], data0=TM[:, cxs], data1=SP[:, cxs], initial=0.0, op0=ALU.mult, op1=ALU.add),
                 reads=["l_TM", "l_SP"], writes=["l_CG"])
            TMo = TM[:, own].rearrange("p (c t) -> p c t", t=128)
            P.op("vector", lambda e: e.memset(TMo[:, :, 0:1], 0.0), reads=["l_CG"], writes=["l_TM"])
            P.op("vector", lambda e: e.tensor_tensor_scan(out=CL[:, own], data0=TM[:, own], data1=SP[:, own], initial=0.0, op0=ALU.mult, op1=ALU.add),
                 reads=["l_TM", "l_SP"], writes=["l_CL"])
            CLo = CL[:, own].rearrange("p (c t) -> p c t", t=128)
            SPo = SP[:, own].rearrange("p (c t) -> p c t", t=128)
            P.op("vector", lambda e: e.tensor_copy(CE[:], CLo[:, :, 127]), reads=["l_CL"], writes=["l_CE"])
            P.op("vector", lambda e: e.tensor_copy(tot[:, 0:1], CG[:, OWN0 + NOWN - 1:OWN0 + NOWN]), reads=["l_CG"], writes=["l_tot"])
            P.op("vector", lambda e: e.tensor_copy(tot[:, 1:2], CG[:, CTX0 + 255:CTX0 + 256]), reads=["l_CG"], writes=["l_tot"])
            P.op("vector", lambda e: e.tensor_scalar(ntot[:], tot[:], -1.0, None, ALU.mult), reads=["l_tot"], writes=["l_ntot"])
            for c in range(NCH):
                P.op("vector", lambda e, c=c: e.tensor_scalar(CLo[32:64, c, :], CLo[32:64, c, :], CE[32:64, c:c + 1], -1.0, ALU.subtract, ALU.mult),
                     reads=["l_CL", "l_CE"], writes=["l_CL"])
            P.op("vector", lambda e: e.tensor_tensor(out=CL[32:64, own], in0=CL[32:64, own], in1=SP[32:64, own], op=ALU.add),
                 reads=["l_CL", "l_SP"], writes=["l_CL"])
            for r in range(8):
                P.op("tensor", lambda e, r=r: e.matmul(pe[:, r, :], lhsT=SELM[:, r, :], rhs=CE[:], start=True, stop=True),
                     reads=["l_SELM", "l_CE"], writes=["l_pe"], inc=(r == 7))
            P.op("scalar", lambda e: e.activation(out=EBEND[:], in_=pe[:], func=AF.Exp, scale=-1.0), reads=["l_pe"], writes=["l_EBEND"])
            for r in range(8):
                P.op("tensor", lambda e, r=r: e.matmul(pa[:, r, :], lhsT=SELM[:, r, :], rhs=tot[:], start=True, stop=True),
                     reads=["l_SELM", "l_tot"], writes=["l_pa"], inc=(r == 7))
            P.op("scalar", lambda e: e.activation(out=ATOT[:], in_=pa[:, :, 0], func=AF.Exp, scale=-1.0), reads=["l_pa"], writes=["l_ATOT"])

            tcnt = {"n": 0}

            def transpose_out(Q, qn, qty, chunks):
                for g0 in range(0, len(chunks), 8):
                    grp = chunks[g0:g0 + 8]
                    a = tcnt["n"] % 2
                    tcnt["n"] += 1
                    for gi, (ci, col0) in enumerate(grp):
                        P.op("tensor", lambda e, a=a, gi=gi, col0=col0: e.transpose(ptr[a][:, gi, :], Q[0:64, col0:col0 + 128], G.ident_f[0:64, 0:64]),
                             reads=[qn, "ident_f"], writes=["l_ptr%d" % a], inc=(gi == len(grp) - 1))
                    c_first = grp[0][0]
                    ng_ = len(grp)
                    src = ptr[a][:, 0:ng_, :].rearrange("p g (d x) -> p g d x", d=2)[:, :, :, 0:4]
                    dst = TOK[:, c_first:c_first + ng_, qty, :].rearrange("p g (d x) -> p g d x", d=2)
                    P.op("vector", lambda e, src=src, dst=dst: e.tensor_copy(dst, src), reads=["l_ptr%d" % a], writes=["l_TOK"])

            own_chunks = [(c, OWN0 + c * 128) for c in range(NCH)]
            ctx_chunks = [(32 + c, CTX0 + c * 128) for c in range(2)]
            P.op("scalar", lambda e: e.activation(out=OQ[0][:, own], in_=CL[:, own], func=AF.Exp, scale=-1.0), reads=["l_CL"], writes=["l_OQ0"])
            transpose_out(OQ[0], "l_OQ0", 0, own_chunks)
            P.op("scalar", lambda e: e.activation(out=OQ[1][:, own], in_=CL[:, own], func=AF.Exp), reads=["l_CL"], writes=["l_OQ1"])
            transpose_out(OQ[1], "l_OQ1", 4, own_chunks)
            P.op("vector", lambda e: e.tensor_tensor(out=TM[:, own], in0=LI[:, own], in1=CL[:, own], op=ALU.add), reads=["l_LI", "l_CL"], writes=["l_TM"])
            P.op("scalar", lambda e: e.activation(out=OQ[1][:, own], in_=TM[:, own], func=AF.Exp), reads=["l_TM"], writes=["l_OQ1"])
            transpose_out(OQ[1], "l_OQ1", 1, own_chunks)
            TMo2 = TM[:, own].rearrange("p (c t) -> p c t", t=128)
            for c in range(NCH):
                P.op("vector", lambda e, c=c: e.tensor_scalar(TMo2[:, c, :], TMo2[:, c, :], CE[:, c:c + 1], None, ALU.subtract),
                     reads=["l_TM", "l_CE", "l_OQ1"], writes=["l_TM"])
            P.op("scalar", lambda e: e.activation(out=OQ[0][:, own], in_=TM[:, own], func=AF.Exp), reads=["l_TM"], writes=["l_OQ0"])
            transpose_out(OQ[0], "l_OQ0", 2, own_chunks)
            for (sl, ti_) in ((own, 0), (cxs, 1)):
                P.op("vector", lambda e, sl=sl: e.tensor_tensor(out=TM[0:32, sl], in0=LI[0:32, sl], in1=CG[0:32, sl], op=ALU.add),
                     reads=["l_LI", "l_CG"], writes=["l_TM"])
                P.op("vector", lambda e, sl=sl: e.tensor_tensor(out=TM[32:64, sl], in0=LI[32:64, sl], in1=CG[32:64, sl], op=ALU.subtract),
                     reads=["l_LI", "l_CG"], writes=["l_TM"])
                P.op("vector", lambda e, sl=sl: e.tensor_tensor(out=TM[32:64, sl], in0=TM[32:64, sl], in1=SP[32:64, sl], op=ALU.add),
                     reads=["l_TM", "l_SP"], writes=["l_TM"])
                P.op("scalar", lambda e, sl=sl, ti_=ti_: e.activation(out=OQ[1][0:32, sl], in_=TM[0:32, sl], func=AF.Exp, bias=ntot[0:32, ti_:ti_ + 1]),
                     reads=["l_TM", "l_ntot"], writes=["l_OQ1"])
                P.op("scalar", lambda e, sl=sl: e.activation(out=OQ[1][32:64, sl], in_=TM[32:64, sl], func=AF.Exp), reads=["l_TM"], writes=["l_OQ1"])
            transpose_out(OQ[1], "l_OQ1", 3, own_chunks + ctx_chunks)
            P.barrier()
            P.emit()

        KTOK = sb("l_KTOK", [128, 34, 512], BF16)
        VTOK = sb("l_VTOK", [128, 34, 516], BF16)
        TRI = sb("l_TRI", [128, 2, 128], F32)
        SELV = sb("l_SELV", [128, 16], F32)
        PAY = sb("l_PAY", [128, 8, 130], F32)
        GATH = sb("l_GATH", [128, 4, 1040], F32)
        STATE = sb("l_STATE", [128, 8, 129], F32)
        STB = sb("l_STB", [128, 8, 129], BF16)
        KA = [sb("l_KA%d" % b, [128, 128], BF16) for b in range(3)]
        alpha = sb("l_alpha", [128, 1], F32)
        tmpL = sb("l_tmpL", [128, 129], F32)
        QTc = [[sb("l_QTc%d%d" % (d_, b), [128, 4, 128], BF16) for b in range(2)] for d_ in range(2)]
        KTc = [[sb("l_KTc%d%d" % (d_, b), [128, 4, 128], BF16) for b in range(2)] for d_ in range(2)]
        HS = [[sb("l_HS%d%d" % (d_, b), [128, 512], F32) for b in range(2)] for d_ in range(2)]
        PTt = [sb("l_PT%d" % b, [128, 128], BF16) for b in range(2)]
        pS = [ps("l_pS%d" % b, [128, 128]) for b in range(2)]
        pU = [ps("l_pU%d" % b, [128, 132]) for b in range(4)]
        pN = [ps("l_pN%d" % b, [128, 132]) for b in range(2)]
        pL = [pU[0], pU[1]]
        den = [sb("l_den%d" % b, [128, 8], F32) for b in range(2)]
        P.dma("sync", lambda e: e.dma_start(out=KTOK[:, 0:32, :], in_=G.MLK[OWN0:OWN0 + NOWN, :].rearrange("(c p) n -> p c n", p=128)), reads=["dramML"], writes=["l_KTOK"])
        P.dma("sync", lambda e: e.dma_start(out=KTOK[:, 32:34, :], in_=G.MLK[CTX0:CTX0 + 256, :].rearrange("(c p) n -> p c n", p=128)), reads=["dramML"], writes=["l_KTOK"])
        P.dma("sync", lambda e: e.dma_start(out=VTOK[:, 0:32, :], in_=G.MLV[OWN0:OWN0 + NOWN, :].rearrange("(c p) n -> p c n", p=128)), reads=["dramML"], writes=["l_VTOK"])
        P.dma("sync", lambda e: e.dma_start(out=VTOK[:, 32:34, :], in_=G.MLV[CTX0:CTX0 + 256, :].rearrange("(c p) n -> p c n", p=128)), reads=["dramML"], writes=["l_VTOK"])
        P.dma("sync", lambda e: e.dma_start(out=TRI[:], in_=I["tri"][:, :, :]), writes=["l_TRI"])
        P.dma("sync", lambda e: e.dma_start(out=SELV[:], in_=I["selv"][:, :]), writes=["l_SELV"])
        kacnt = {"n": 0}

        def scaled_k(c, h, qty, r):
            a = kacnt["n"] % 3
            kacnt["n"] += 1
            eng = ("vector", "scalar", "scalar")[a]
            src = KTOK[:, c, h * 128:(h + 1) * 128]
            sc = TOK[:, c, qty, r:r + 1]
            if eng == "scalar":
                P.op("scalar", lambda e: e.activation(out=KA[a][:], in_=src, func=AF.Copy, scale=sc), reads=["l_KTOK", "l_TOK"], writes=["l_KA%d" % a])
            else:
                P.op(eng, lambda e: e.tensor_scalar(KA[a][:], src, sc, None, ALU.mult), reads=["l_KTOK", "l_TOK"], writes=["l_KA%d" % a])
            return KA[a], "l_KA%d" % a

        n2 = 0
        for r in range(8):
            h = r % 4
            for (chs, dstname) in ((list(range(32)), "own"), ([32, 33], "ctx")):
                pp, ppn = pL[n2 % 2], "l_pU%d" % (n2 % 2)
                n2 += 1
                for i_, c in enumerate(chs):
                    ka, kan = scaled_k(c, h, 3, r)
                    P.op("tensor", lambda e, pp=pp, ka=ka, c=c, h=h, i_=i_, L=len(chs): e.matmul(
                        pp[:, 0:129], lhsT=ka[:], rhs=VTOK[:, c, h * 129:(h + 1) * 129], start=(i_ == 0), stop=(i_ == L - 1)),
                        reads=[kan, "l_VTOK"], writes=[ppn], inc=True)
                if dstname == "own":
                    P.op("vector", lambda e, pp=pp, r=r: e.tensor_copy(PAY[:, r, 0:129], pp[:, 0:129]), reads=[ppn], writes=["l_PAY"])
                else:
                    P.op("vector", lambda e, pp=pp, r=r: e.tensor_copy(STATE[:, r, :], pp[:, 0:129]), reads=[ppn], writes=["l_STATE"])
        P.op("vector", lambda e: e.tensor_copy(PAY[:, :, 129], ATOT[:]), reads=["l_ATOT", "l_PAY"], writes=["l_PAY"])
        P.dma("sync", lambda e: e.dma_start(out=G.CCI[:, :], in_=PAY[:].rearrange("p r n -> p (r n)")), reads=["l_PAY"], writes=["CCI"])
        for _rep in range(3):
            P.cc(lambda e: e.collective_compute("AllGather", ALU.bypass, replica_groups=[[0, 1, 2, 3], [4, 5, 6, 7]],
                                                ins=[G.CCI.opt()], outs=[G.CCO.opt()]), reads=["CCI"], writes=["CCO"])
        P.dma("sync", lambda e: e.dma_start(out=GATH[:], in_=G.CCO.rearrange("(j p) n -> p j n", p=128)), reads=["CCO"], writes=["l_GATH"])
        for r in range(8):
            d = r // 4
            order = range(4) if d == 0 else range(3, -1, -1)
            for jseg in order:
                so = 0 if d == 0 else 8
                A_j = GATH[:, jseg, r * 130 + 129:r * 130 + 130]
                L_j = GATH[:, jseg, r * 130:r * 130 + 129]
                P.op("vector", lambda e, A_j=A_j, so=so, jseg=jseg: e.tensor_scalar(
                    alpha[:], A_j, SELV[:, so + jseg:so + jseg + 1], SELV[:, so + 4 + jseg:so + 5 + jseg], ALU.mult, ALU.add),
                    reads=["l_GATH", "l_SELV"], writes=["l_alpha"])
                P.op("vector", lambda e, L_j=L_j, so=so, jseg=jseg: e.tensor_scalar(tmpL[:], L_j, SELV[:, so + jseg:so + jseg + 1], None, ALU.mult),
                     reads=["l_GATH", "l_SELV"], writes=["l_tmpL"])
                P.op("vector", lambda e, r=r: e.scalar_tensor_tensor(out=STATE[:, r, :], in0=STATE[:, r, :], scalar=alpha[:], in1=tmpL[:], op0=ALU.mult, op1=ALU.add),
                     reads=["l_STATE", "l_alpha", "l_tmpL"], writes=["l_STATE"])
        P.op("scalar", lambda e: e.activation(out=STB[:], in_=STATE[:], func=AF.Copy), reads=["l_STATE"], writes=["l_STB"])

        def ld4(i):
            if i >= NCH:
                return
            bb = i % 2
            for d_ in range(2):
                c_ = i if d_ == 0 else NCH - 1 - i
                tk0 = OWN0 + c_ * 128
                P.dma("sync", lambda e, bb=bb, d_=d_, tk0=tk0: e.dma_start(out=QTc[d_][bb][:], in_=G.MLQT[:, :, tk0:tk0 + 128]), writes=["l_QTc%d%d" % (d_, bb)])
                P.dma("sync", lambda e, bb=bb, d_=d_, tk0=tk0: e.dma_start(out=KTc[d_][bb][:], in_=G.MLKT[:, :, tk0:tk0 + 128]), writes=["l_KTc%d%d" % (d_, bb)])
        ld4(0)
        for i in range(NCH):
            b = i % 2
            ld4(i + 1)
            cs = (i, NCH - 1 - i)
            for hh_ in range(2):
                items = [(h, d) for h in (2 * hh_, 2 * hh_ + 1) for d in range(2)]
                def front(h, d):
                    c = cs[d]
                    r = d * 4 + h
                    qn, kn = "l_QTc%d%d" % (d, b), "l_KTc%d%d" % (d, b)
                    P.op("tensor", lambda e, h=h, d=d, b=b: e.matmul(pS[d][:], lhsT=KTc[d][b][:, h, :], rhs=QTc[d][b][:, h, :], start=True, stop=True),
                         reads=[kn, qn], writes=["l_pS%d" % d], inc=True)
                    P.op("vector", lambda e, c=c, r=r, d=d: e.scalar_tensor_tensor(
                        out=PTt[d][:], in0=pS[d][:], scalar=TOK[:, c, 1, r:r + 1], in1=TRI[:, d, :], op0=ALU.mult, op1=ALU.mult),
                        reads=["l_pS%d" % d, "l_TOK", "l_TRI"], writes=["l_PT%d" % d])

                def back(h, d):
                    c = cs[d]
                    r = d * 4 + h
                    u = (h % 2) * 2 + d
                    qn = "l_QTc%d%d" % (d, b)
                    P.op("tensor", lambda e, c=c, h=h, d=d, u=u: e.matmul(pU[u][:, 0:129], lhsT=PTt[d][:], rhs=VTOK[:, c, h * 129:(h + 1) * 129], start=True, stop=False),
                         reads=["l_PT%d" % d, "l_VTOK"], writes=["l_pU%d" % u], inc=False)
                    P.op("tensor", lambda e, h=h, d=d, r=r, u=u, b=b: e.matmul(pU[u][:, 0:129], lhsT=QTc[d][b][:, h, :], rhs=STB[:, r, :], start=False, stop=True),
                         reads=[qn, "l_STB%d" % r, "l_STB"], writes=["l_pU%d" % u], inc=True)
                front(*items[0])
                for k_ in range(4):
                    if k_ + 1 < 4:
                        front(*items[k_ + 1])
                    back(*items[k_])
                for (h, d) in items:
                    c = cs[d]
                    r = d * 4 + h
                    a = r % 2
                    ka, kan = scaled_k(c, h, 2, r)
                    P.op("tensor", lambda e, a=a, ka=ka, c=c, h=h: e.matmul(pN[a][:, 0:129], lhsT=ka[:], rhs=VTOK[:, c, h * 129:(h + 1) * 129], start=True, stop=True),
                         reads=[kan, "l_VTOK"], writes=["l_pN%d" % a], inc=True)
                    P.op("vector", lambda e, a=a, r=r, c=c: e.scalar_tensor_tensor(
                        out=STATE[:, r, :], in0=STATE[:, r, :], scalar=EBEND[:, r, c:c + 1], in1=pN[a][:, 0:129], op0=ALU.mult, op1=ALU.add),
                        reads=["l_STATE%d" % r, "l_EBEND", "l_pN%d" % a, "l_STATE"], writes=["l_STATE%d" % r])
                    P.op("scalar", lambda e, r=r: e.activation(out=STB[:, r, :], in_=STATE[:, r, :], func=AF.Copy),
                         reads=["l_STATE%d" % r], writes=["l_STB%d" % r])
                for (h, d) in items:
                    u = (h % 2) * 2 + d
                    dn, dnn = den[d], "l_den%d" % d
                    P.op("scalar", lambda e, dn=dn, u=u, h=h: e.activation(out=dn[:, h:h + 1], in_=pU[u][:, 128:129], func=AF.Abs),
                         reads=["l_pU%d" % u], writes=[dnn])
                for d in range(2):
                    c = cs[d]
                    dn, dnn = den[d], "l_den%d" % d
                    h0 = 2 * hh_
                    REB2 = TOK[:, c, 4, d * 4 + h0:d * 4 + h0 + 2]
                    P.op("vector", lambda e, dn=dn, REB2=REB2, h0=h0: e.tensor_tensor(out=dn[:, h0:h0 + 2], in0=dn[:, h0:h0 + 2], in1=REB2, op=ALU.max),
                         reads=[dnn, "l_TOK"], writes=[dnn])
                    P.op("vector", lambda e, dn=dn, h0=h0: e.reciprocal(dn[:, h0:h0 + 2], dn[:, h0:h0 + 2]), reads=[dnn], writes=[dnn])
                for (h, d) in items:
                    u = (h % 2) * 2 + d
                    dn, dnn = den[d], "l_den%d" % d
                    hsn = "l_HS%d%d" % (d, b)
                    if d == 0:
                        P.op("scalar", lambda e, b=b, h=h, dn=dn, u=u: e.activation(out=HS[0][b][:, h * 128:(h + 1) * 128], in_=pU[u][:, 0:128], func=AF.Copy, scale=dn[:, h:h + 1]),
                             reads=["l_pU%d" % u, dnn], writes=[hsn])
                    else:
                        P.op("vector", lambda e, b=b, h=h, dn=dn, u=u: e.tensor_scalar(HS[1][b][:, h * 128:(h + 1) * 128], pU[u][:, 0:128], dn[:, h:h + 1], None, ALU.mult),
                             reads=["l_pU%d" % u, dnn], writes=[hsn])
            P.dma("sync", lambda e, b=b, c=cs[0]: e.dma_start(out=G.HF[c * 128:(c + 1) * 128, :], in_=HS[0][b][:]), reads=["l_HS0%d" % b], writes=[_u()])
            P.dma("sync", lambda e, b=b, c=cs[1]: e.dma_start(out=G.HB[c * 128:(c + 1) * 128, :], in_=HS[1][b][:]), reads=["l_HS1%d" % b], writes=[_u()])
        P.barrier()
        P.emit()

    with contextlib.ExitStack() as st:
        sb, ps = _alloc(G, st)
        NB = 3
        hf = [sb("r_hf%d" % b, [128, 512], F32) for b in range(NB)]
        hb = [sb("r_hb%d" % b, [128, 512], F32) for b in range(NB)]
        g2 = [sb("r_g2%d" % b, [128, 512], F32) for b in range(NB)]
        junk = sb("r_junk", [128, 128], F32)
        ssq = [sb("r_ssq%d" % b, [128, 4], F32) for b in range(2)]
        Yb = [sb("r_Y%d" % b, [128, 512], BF16) for b in range(2)]
        ytb = [sb("r_yt%d" % b, [128, 4, 128], BF16) for b in range(2)]
        ptr2 = [ps("r_ptr%d" % b, [128, 4, 128], BF16) for b in range(2)]

        def ld5(c):
            if c >= NCH:
                return
            b3 = c % NB
            tk0 = OWN0 + c * 128
            P.dma("sync", lambda e, b3=b3, c=c: e.dma_start(out=hf[b3][:], in_=G.HF[c * 128:(c + 1) * 128, :]), writes=["r_hf%d" % b3])
            P.dma("sync", lambda e, b3=b3, c=c: e.dma_start(out=hb[b3][:], in_=G.HB[c * 128:(c + 1) * 128, :]), writes=["r_hb%d" % b3])
            P.dma("sync", lambda e, b3=b3, tk0=tk0: e.dma_start(out=g2[b3][:], in_=G.MLG2[tk0:tk0 + 128, :]), writes=["r_g2%d" % b3])
        ld5(0)
        ld5(1)
        for c in range(NCH):
            ld5(c + 2)
            b3, b = c % NB, c % 2
            P.op("vector", lambda e, b3=b3: e.tensor_tensor(out=hf[b3][:], in0=hf[b3][:], in1=hb[b3][:], op=ALU.add),
                 reads=["r_hf%d" % b3, "r_hb%d" % b3], writes=["r_hf%d" % b3])
            for h in range(4):
                P.op("scalar", lambda e, b3=b3, b=b, h=h: e.activation(out=junk[:], in_=hf[b3][:, h * 128:(h + 1) * 128], func=AF.Square, accum_out=ssq[b][:, h:h + 1]),
                     reads=["r_hf%d" % b3], writes=["r_junk", "r_ssq%d" % b])
            P.op("scalar", lambda e, b=b: e.activation(out=ssq[b][:], in_=ssq[b][:], func=AF.Sqrt, bias=G.c_eps[:], scale=1.0 / 128.0),
                 reads=["r_ssq%d" % b, "c_eps"], writes=["r_ssq%d" % b])
            P.op("vector", lambda e, b=b: e.reciprocal(ssq[b][:], ssq[b][:]), reads=["r_ssq%d" % b], writes=["r_ssq%d" % b])
            for h in range(4):
                P.op("vector", lambda e, b3=b3, b=b, h=h: e.scalar_tensor_tensor(
                    out=Yb[b][:, h * 128:(h + 1) * 128], in0=hf[b3][:, h * 128:(h + 1) * 128], scalar=ssq[b][:, h:h + 1],
                    in1=g2[b3][:, h * 128:(h + 1) * 128], op0=ALU.mult, op1=ALU.mult),
                    reads=["r_hf%d" % b3, "r_ssq%d" % b, "r_g2%d" % b3], writes=["r_Y%d" % b])
            for h in range(4):
                P.op("tensor", lambda e, b=b, h=h: e.transpose(ptr2[b][:, h, :], Yb[b][:, h * 128:(h + 1) * 128], G.ident_b[:]),
                     reads=["r_Y%d" % b, "ident_b"], writes=["r_ptr%d" % b], inc=(h == 3))
            P.op("scalar", lambda e, b=b: e.activation(out=ytb[b][:], in_=ptr2[b][:], func=AF.Copy), reads=["r_ptr%d" % b], writes=["r_yt%d" % b])
            P.dma("sync", lambda e, b=b, c=c: e.dma_start(out=G.YT[:, 4:8, c * 128:(c + 1) * 128], in_=ytb[b][:]), reads=["r_yt%d" % b], writes=[_u()])
        P.barrier()
        P.emit()


def stage_outproj(G):
    P, I, nc = G.P, G.I, G.nc
    l, j, m = 0, 1, 0
    with contextlib.ExitStack() as st:
        sb, ps = _alloc(G, st)
        wo = sb("o_wo", [128, 8, 1024], BF16)
        xt = [sb("o_xt%d" % b, [128, 8, TW], F32) for b in range(2)]
        yt = [sb("o_yt%d" % b, [128, 8, TW], BF16) for b in range(2)]
        po = [ps("o_po%d" % b, [128, TW]) for b in range(2)]
        load_w_cast(G, wo, "o_wo", I["mix_w_out"], 8, 1024)
        def ld(n):
            if n >= 16:
                return
            ti_, b_ = n + 1, n % 2
            P.dma("sync", lambda e, b_=b_, ti_=ti_: e.dma_start(out=xt[b_][:], in_=G.XT[:, :, ti_ * TW:(ti_ + 1) * TW]), reads=["XT%d" % ti_], writes=["o_xt%d" % b_])
            P.dma("sync", lambda e, b_=b_, n=n: e.dma_start(out=yt[b_][:], in_=G.YT[:, :, n * TW:(n + 1) * TW]), reads=["dramYT"], writes=["o_yt%d" % b_])
        ld(0)
        ld(1)
        for n in range(16):
            ti = n + 1
            t0 = ti * TW
            b = n % 2
            for f in range(8):
                q = f % 2
                for k in range(8):
                    P.op("tensor", lambda e, q=q, f=f, k=k, b=b: e.matmul(po[q][:], lhsT=wo[:, k, f * 128:(f + 1) * 128], rhs=yt[b][:, k, :], start=(k == 0), stop=(k == 7)),
                         reads=["o_wo", "o_yt%d" % b], writes=["o_po%d" % q], inc=(k == 7))
                gsc = G.GT[:, l, j, m, f:f + 1]
                P.op("vector", lambda e, q=q, f=f, b=b, gsc=gsc: e.scalar_tensor_tensor(
                    out=xt[b][:, f, :], in0=po[q][:], scalar=gsc, in1=xt[b][:, f, :], op0=ALU.mult, op1=ALU.add),
                    reads=["o_po%d" % q, "modc", "o_xt%d" % b], writes=["o_xt%d" % b])
            P.dma("sync", lambda e, b=b, t0=t0: e.dma_start(out=G.XT[:, :, t0:t0 + TW], in_=xt[b][:]), reads=["o_xt%d" % b], writes=["XT%d" % ti])
            ld(n + 2)
        P.barrier()
        P.emit()


def stage_sg(G):
    P, I, nc = G.P, G.I, G.nc
    l, j, m = 1, 1, 0
    with contextlib.ExitStack() as st:
        sb, ps = _alloc(G, st)
        wu = sb("g_wu", [128, 8, 2048], BF16)
        wv = sb("g_wv", [128, 8, 2048], BF16)
        wo = sb("g_wo", [128, 16, 1024], BF16)
        wsT = sb("g_wsT", [128, 8, 128], BF16)
        bs = sb("g_bs", [1, 1024], BF16)
        ones1 = sb("g_ones1", [1, 256], BF16)
        lng = sb("g_lng", [128, 2048], F32)
        lnb = sb("g_lnb", [128, 2048], F32)
        xt = [sb("g_xt%d" % b, [128, 8, TW], F32) for b in range(2)]
        hh = [sb("g_h%d" % b, [128, 8, TW], BF16) for b in range(2)]
        NS = NormScratch(G, sb, ps, "g_")
        NS.sq8 = [sb("g_sq8_%d" % k, [128, TW], BF16) for k in range(8)]
        uT = sb("g_uT", [128, 16, TW], BF16)
        vraw = [sb("g_vraw%d" % b, [128, 2048], F32) for b in range(2)]
        vn = [sb("g_vn%d" % b, [128, 2048], BF16) for b in range(2)]
        gated = sb("g_gated", [128, 16, TW], BF16)
        stats = [sb("g_stats%d" % b, [128, 4, 6], F32) for b in range(2)]
        mv = [sb("g_mv%d" % b, [128, 4], F32) for b in range(2)]
        pu = [ps("g_pu%d" % b, [128, TW]) for b in range(2)]
        pm = [ps("g_pm%d" % b, [128, 4, 128]) for b in range(2)]
        po1 = ps("g_po", [128, TW])
        po = [po1, po1]
        pv = [ps("g_pv%d" % b, [128, 512]) for b in range(2)]
        load_w_cast(G, wu, "g_wu", I["sg_w_in"][:, 0:2048], 8, 2048)
        load_w_cast(G, wv, "g_wv", I["sg_w_in"][:, 2048:4096], 8, 2048)
        load_w_cast(G, wo, "g_wo", I["sg_w_out"], 16, 1024)
        P.dma("gpsimd", lambda e: e.dma_start(out=wsT[:], in_=I["sg_w_sT"][:, :, :]), writes=["g_wsT"])
        P.dma("gpsimd", lambda e: e.dma_start(out=bs[:], in_=I["sg_b_s"][:, :]), writes=["g_bs"])
        P.op("vector", lambda e: e.memset(ones1[:], 1.0), writes=["g_ones1"])
        P.dma("sync", lambda e: e.dma_start(out=lng[:], in_=I["sg_lng_bc"][:, :]), writes=["g_lng"])
        P.dma("sync", lambda e: e.dma_start(out=lnb[:], in_=I["sg_lnb_bc"][:, :]), writes=["g_lnb"])
        tiles = list(range(1, 17))

        def ld(n):
            ti_ = tiles[n]
            xb = xt[n % 2]
            P.dma("sync", lambda e, xb=xb, ti_=ti_: e.dma_start(out=xb[:], in_=G.XT[:, :, ti_ * TW:(ti_ + 1) * TW]),
                  reads=["XT%d" % ti_], writes=["g_xt%d" % (n % 2)])
        def do_norm(n, phase=None):
            bb = n % 2
            norm_mod(G, NS, xt[bb], "g_xt%d" % bb, G.GM[:, l, j, m, :], G.SH[:, l, j, m, :], hh[bb], "g_h%d" % bb, phase=phase)
        ld(0)
        ld(1)
        do_norm(0)
        vcnt = 0
        for n, ti in enumerate(tiles):
            t0 = ti * TW
            b = n % 2
            x, xn = xt[b], "g_xt%d" % b
            h, hn = hh[b], "g_h%d" % b
            NSB = TW // 128
            for s_ in range(NSB):
                VR = vraw[s_]
                for blk in range(4):
                    q = blk % 2
                    for k in range(8):
                        P.op("tensor", lambda e, q=q, blk=blk, k=k, h=h, s_=s_: e.matmul(
                            pv[q][:], lhsT=h[:, k, s_ * 128:(s_ + 1) * 128], rhs=wv[:, k, blk * 512:(blk + 1) * 512], start=(k == 0), stop=(k == 7)),
                            reads=["g_wv", hn], writes=["g_pv%d" % q], inc=(k == 7))
                    P.op("scalar", lambda e, q=q, blk=blk, VR=VR: e.activation(out=VR[:, blk * 512:(blk + 1) * 512], in_=pv[q][:], func=AF.Gelu_apprx_tanh),
                         reads=["g_pv%d" % q], writes=["g_vraw%d" % s_])
                    P.op("vector", lambda e, blk=blk, VR=VR, s_=s_: e.bn_stats(stats[s_][:, blk, :], VR[:, blk * 512:(blk + 1) * 512]),
                         reads=["g_vraw%d" % s_], writes=["g_stats%d" % s_])
                P.op("vector", lambda e, s_=s_: e.bn_aggr(mv[s_][:, 0:2], stats[s_][:].rearrange("p a b -> p (a b)")), reads=["g_stats%d" % s_], writes=["g_mv%d" % s_])
                P.op("scalar", lambda e, s_=s_: e.activation(out=mv[s_][:, 2:3], in_=mv[s_][:, 1:2], func=AF.Sqrt, bias=G.c_eps[:], scale=1.0),
                     reads=["g_mv%d" % s_, "c_eps"], writes=["g_mv%d" % s_])
                P.op("vector", lambda e, s_=s_: e.reciprocal(mv[s_][:, 2:3], mv[s_][:, 2:3]), reads=["g_mv%d" % s_], writes=["g_mv%d" % s_])
                P.op("vector", lambda e, s_=s_: e.tensor_scalar(mv[s_][:, 3:4], mv[s_][:, 0:1], mv[s_][:, 2:3], -1.0, ALU.mult, ALU.mult),
                     reads=["g_mv%d" % s_], writes=["g_mv%d" % s_])
            for fc in range(16):
                q = fc % 2
                if fc == 6:
                    vns = []
                    for s_ in range(NSB):
                        VR = vraw[s_]
                        VN, vnn = vn[vcnt % 2], "g_vn%d" % (vcnt % 2)
                        vcnt += 1
                        vns.append((VN, vnn))
                        P.op("scalar", lambda e, VR=VR, s_=s_: e.activation(out=VR[:], in_=VR[:], func=AF.Identity, bias=mv[s_][:, 3:4], scale=mv[s_][:, 2:3]),
                             reads=["g_vraw%d" % s_, "g_mv%d" % s_], writes=["g_vraw%d" % s_])
                        P.op("vector", lambda e, VR=VR: e.tensor_tensor(out=VR[:], in0=VR[:], in1=lng[:], op=ALU.mult),
                             reads=["g_vraw%d" % s_, "g_lng"], writes=["g_vraw%d" % s_])
                        P.op("vector", lambda e, VR=VR, VN=VN: e.tensor_tensor(out=VN[:], in0=VR[:], in1=lnb[:], op=ALU.add),
                             reads=["g_vraw%d" % s_, "g_lnb"], writes=[vnn])
                for k in range(8):
                    P.op("tensor", lambda e, q=q, fc=fc, k=k, h=h: e.matmul(pu[q][:], lhsT=wu[:, k, fc * 128:(fc + 1) * 128], rhs=h[:, k, :], start=(k == 0), stop=(k == 7)),
                         reads=["g_wu", hn], writes=["g_pu%d" % q], inc=(k == 7))
                P.op("scalar", lambda e, q=q, fc=fc: e.activation(out=uT[:, fc, :], in_=pu[q][:], func=AF.Gelu_apprx_tanh), reads=["g_pu%d" % q], writes=["g_uT%d" % fc])
            for s_ in range(NSB):
                VN, vnn = vns[s_]
                for g4 in range(4):
                    q = g4 % 2
                    for f4 in range(4):
                        fc = g4 * 4 + f4
                        g_ = fc // 2
                        P.op("tensor", lambda e, q=q, fc=fc, f4=f4, g_=g_, VN=VN: e.matmul(
                            pm[q][:, f4, :], lhsT=VN[:, fc * 128:(fc + 1) * 128], rhs=wsT[:, g_, :], start=True, stop=False),
                            reads=[vnn, "g_wsT"], writes=["g_pm%d" % q], inc=False)
                        P.op("tensor", lambda e, q=q, f4=f4, g_=g_: e.matmul(
                            pm[q][:, f4, :], lhsT=ones1[0:1, 0:128], rhs=bs[0:1, g_ * 128:(g_ + 1) * 128], start=False, stop=True),
                            reads=["g_ones1", "g_bs"], writes=["g_pm%d" % q], inc=(f4 == 3))
                    P.op("vector", lambda e, q=q, g4=g4, s_=s_: e.tensor_tensor(
                        out=gated[:, g4 * 4:(g4 + 1) * 4, s_ * 128:(s_ + 1) * 128], in0=uT[:, g4 * 4:(g4 + 1) * 4, s_ * 128:(s_ + 1) * 128], in1=pm[q][:], op=ALU.mult),
                        reads=["g_uT%d" % fc_ for fc_ in range(g4 * 4, g4 * 4 + 4)] + ["g_pm%d" % q],
                        writes=["g_gated%d" % fc_ for fc_ in range(g4 * 4, g4 * 4 + 4)])
            if n + 1 < len(tiles):
                do_norm(n + 1, "A")
            for f in range(8):
                q = f % 2
                if f == 3 and n + 1 < len(tiles):
                    do_norm(n + 1, "B")
                for fc in range(16):
                    P.op("tensor", lambda e, q=q, f=f, fc=fc: e.matmul(po[q][:], lhsT=wo[:, fc, f * 128:(f + 1) * 128], rhs=gated[:, fc, :], start=(fc == 0), stop=(fc == 15)),
                         reads=["g_wo", "g_gated%d" % fc], writes=["g_po"], inc=(fc == 15))
                gsc = G.GT[:, l, j, m, f:f + 1]
                P.op("vector", lambda e, q=q, f=f, x=x, gsc=gsc: e.scalar_tensor_tensor(
                    out=x[:, f, :], in0=po[q][:], scalar=gsc, in1=x[:, f, :], op0=ALU.mult, op1=ALU.add),
                    reads=["g_po", "modc", xn], writes=[xn])
            P.dma("sync", lambda e, x=x, t0=t0: e.dma_start(out=G.XT[:, :, t0:t0 + TW], in_=x[:]), reads=[xn], writes=["XT%d" % ti])
            if n + 2 < len(tiles):
                ld(n + 2)
        P.barrier()
        P.emit()


_NC_CACHE = {}


def kernel(**inputs):
    maps = _host_prep(inputs)
    if "nc" not in _NC_CACHE:
        _NC_CACHE["nc"] = build()
    nc = _NC_CACHE["nc"]
    res = run_bass_kernel_spmd(nc, maps, core_ids=list(range(8)))
    out = np.zeros((2, 16384, 1024), np.float32)
    for core in range(8):
        b, s = core // 4, core % 4
        out[b, s * 4096:(s + 1) * 4096, :] = res.results[core]["out"]
    return out
```

```python
import contextlib
import numpy as np
import concourse.bass as bass
import concourse.mybir as mybir
from concourse.bass_utils import run_bass_kernel_spmd

F32 = mybir.dt.float32
BF16 = mybir.dt.bfloat16
AF = mybir.ActivationFunctionType
ALU = mybir.AluOpType
AX = mybir.AxisListType

D = 1024
DFF = 2816
NE = 4608
NU = 4864
OWN0 = 256
NOWN = 4096
CTX0 = 4608
TW = 256
EPS = 1e-6
NEG = -30000.0

ENGS = ("tensor", "vector", "scalar", "gpsimd", "sync")
NDMASEM = 32
NHW = 24


class Buf:
    __slots__ = ("name", "writers", "readers")

    def __init__(self, name):
        self.name = name
        self.writers = []
        self.readers = []


class Prog:
    def __init__(self, nc, st):
        self.nc = nc
        self.q = {e: [] for e in ENGS}
        self.cnt = {e: 0 for e in ENGS}
        self.seen = {e: {} for e in ENGS}
        self.dcnt = [0] * NDMASEM
        self.dnext = 0
        self.dnext_sw = 0
        self.bufs = {}
        self.esem = {e: st.enter_context(nc.semaphore("s_" + e)) for e in ENGS}
        self.dsem = [st.enter_context(nc.semaphore("d%d" % i)) for i in range(NDMASEM)]
        self.lastinc = {e: True for e in ENGS}

    def buf(self, name):
        b = self.bufs.get(name)
        if b is None:
            b = self.bufs[name] = Buf(name)
        return b

    def _bl(self, lst):
        return [self.buf(b) if isinstance(b, str) else b for b in lst]

    def _deps(self, reads, writes):
        deps = {}
        for b in reads:
            for k, v in b.writers:
                if deps.get(k, 0) < v:
                    deps[k] = v
        for b in writes:
            for k, v in b.writers:
                if deps.get(k, 0) < v:
                    deps[k] = v
            for k, v in b.readers:
                if deps.get(k, 0) < v:
                    deps[k] = v
        return deps

    def _waits(self, eng, deps):
        seen = self.seen[eng]
        waits = []
        for k, v in deps.items():
            if k == "tensor" and eng == "tensor":
                continue
            if seen.get(k, 0) >= v:
                continue
            seen[k] = v
            waits.append((k, v))
        return waits

    def _record(self, ev, reads, writes):
        k = ev[0]
        for b in writes:
            b.writers = [ev]
            b.readers = []
        for b in reads:
            if b in writes:
                continue
            b.readers = [e for e in b.readers if e[0] != k] + [ev]

    def op(self, eng, fn, reads=(), writes=(), inc=True):
        reads = self._bl(reads)
        writes = self._bl(writes)
        waits = self._waits(eng, self._deps(reads, writes))
        if inc:
            self.cnt[eng] += 1
            ev = (eng, self.cnt[eng])
        else:
            ev = (eng, self.cnt[eng] + 1)
        self.lastinc[eng] = inc
        self.q[eng].append(("op", fn, waits, inc))
        self._record(ev, reads, writes)

    def dma(self, eng, fn, reads=(), writes=()):
        reads = self._bl(reads)
        writes = self._bl(writes)
        deps = self._deps(reads, writes)
        if eng == "gpsimd":
            i = NHW + self.dnext_sw
            self.dnext_sw = (self.dnext_sw + 1) % (NDMASEM - NHW)
        else:
            i = self.dnext
            self.dnext = (self.dnext + 1) % NHW
        key = ("d", i)
        if self.dcnt[i] > 0 and deps.get(key, 0) < self.dcnt[i]:
            deps[key] = self.dcnt[i]
        waits = self._waits(eng, deps)
        self.dcnt[i] += 16
        ev = (key, self.dcnt[i])
        self.q[eng].append(("dma", fn, waits, i))
        self._record(ev, reads, writes)

    def cc(self, fn, reads=(), writes=()):
        self.op("gpsimd", fn, reads=reads, writes=writes, inc=True)

    def barrier(self):
        for e in ENGS:
            assert self.lastinc[e], e
        deps = {e: self.cnt[e] for e in ENGS if self.cnt[e] > 0}
        for i in range(NDMASEM):
            if self.dcnt[i] > 0:
                deps[("d", i)] = self.dcnt[i]
        for e in ENGS:
            d = {k: v for k, v in deps.items() if k != e}
            waits = self._waits(e, d)
            self.q[e].append(("wait", None, waits, None))

    def emit(self):
        nc = self.nc
        esem, dsem = self.esem, self.dsem

        def semof(k):
            return esem[k] if isinstance(k, str) else dsem[k[1]]

        def run(engname):
            items = self.q[engname]

            def body(e):
                for kind, fn, waits, x in items:
                    for k, v in waits:
                        e.wait_ge(semof(k), v)
                    if kind == "op":
                        ins = fn(e)
                        if x:
                            ins.then_inc(esem[engname], 1)
                    elif kind == "dma":
                        fn(e).then_inc(dsem[x], 16)
            return body

        with nc.Block() as block:
            block.tensor(run("tensor"))
            block.vector(run("vector"))
            block.scalar(run("scalar"))
            block.gpsimd(run("gpsimd"))
            block.sync(run("sync"))
        self.q = {e: [] for e in ENGS}


def _rowmap(s):
    rm = np.zeros(72, np.int64)
    rm[4:68] = s * 64 + np.arange(64)
    rm[0:4] = (s * 64 - 4 + np.arange(4)) if s > 0 else np.array([5, 6, 7, 8])
    rm[68:72] = (s * 64 + 64 + np.arange(4)) if s < 3 else np.array([248, 249, 250, 251])
    return rm


def _na_bias_tables(rpb, s):
    rm = _rowmap(s)
    out = np.full((5, 8, 128, 576), NEG, np.float32)
    reps = [0, 1, 2, 30, 31]
    cols = np.arange(64)
    c0 = np.clip(cols - 8, 0, 64 - 16)
    for ci, p in enumerate(reps):
        for r in range(2):
            i = s * 64 + 2 * p + r
            r0 = min(max(i - 4, 0), 256 - 8)
            seen_rows = set()
            for j in range(9):
                krow = int(rm[2 * p + j])
                if krow < r0 or krow >= r0 + 8 or krow in seen_rows:
                    continue
                seen_rows.add(krow)
                rr = krow - i + 7
                for qc in range(64):
                    kc = np.arange(c0[qc], c0[qc] + 16)
                    out[ci][:, r * 64 + qc, j * 64 + kc] = rpb[:, rr, kc - qc + 15]
    return out


def _rope_tables(s):
    rm = _rowmap(s)
    row = np.repeat(rm, 64).astype(np.float32)
    col = np.tile(np.arange(64), 72).astype(np.float32)
    inv = (10000.0 ** (-np.arange(32, dtype=np.float32) / 32)).astype(np.float32)
    ang = np.concatenate([row[:, None] * inv, col[:, None] * inv], axis=-1).astype(np.float32)
    cos = np.cos(ang).astype(np.float32)
    sin = np.sin(ang).astype(np.float32)
    cos2 = np.repeat(cos, 2, axis=1)
    sin2 = np.repeat(sin, 2, axis=1)
    sin2[:, 0::2] *= -1.0
    cos_u = np.ones((NU, 128), np.float32)
    sin_u = np.zeros((NU, 128), np.float32)
    cos_u[:NE] = cos2
    sin_u[:NE] = sin2
    return np.tile(cos_u, (1, 4)), np.tile(sin_u, (1, 4))


def _host_prep(inp):
    x = np.asarray(inp["x"], np.float32)
    shared = {}
    shared["w_mod"] = np.ascontiguousarray(inp["w_mod"], np.float32)
    shared["b_modT"] = np.ascontiguousarray(np.asarray(inp["b_mod"], np.float32).reshape(2, 72, 128).transpose(2, 0, 1))
    shared["norm_gT"] = np.ascontiguousarray(np.asarray(inp["norm_g"], np.float32).reshape(2, 3, 8, 128).transpose(3, 0, 1, 2))
    shared["ffn_w_in"] = np.ascontiguousarray(inp["ffn_w_in"], np.float32)
    shared["ffn_w_out"] = np.ascontiguousarray(inp["ffn_w_out"], np.float32)
    mw = np.asarray(inp["mix_w_in"], np.float32)[0]
    shared["mix_w_in"] = np.ascontiguousarray(mw[:, :3584])
    wg = np.zeros((1024, 128), np.float32)
    gb = np.zeros((128, 1), np.float32)
    gate_b = np.asarray(inp["ml_gate_b"], np.float32)[0]
    for h in range(4):
        for d in range(2):
            for q in range(2):
                wg[:, q * 64 + d * 32 + h] = mw[:, 3584 + h * 4 + d * 2 + q]
                gb[q * 64 + d * 32 + h, 0] = gate_b[h, d, q]
    shared["w_gate"] = wg
    shared["gate_b"] = gb
    shared["headg_bc"] = np.ascontiguousarray(np.broadcast_to(np.asarray(inp["ml_head_g"], np.float32)[0].reshape(1, 512), (128, 512)))
    shared["mix_w_out"] = np.ascontiguousarray(np.asarray(inp["mix_w_out"], np.float32)[0])
    shared["sg_w_in"] = np.ascontiguousarray(np.asarray(inp["sg_w_in"], np.float32)[0])
    shared["sg_w_out"] = np.ascontiguousarray(np.asarray(inp["sg_w_out"], np.float32)[0])
    shared["sg_lng_bc"] = np.ascontiguousarray(np.broadcast_to(np.asarray(inp["sg_ln_g"], np.float32)[0].reshape(1, 2048), (128, 2048)))
    shared["sg_lnb_bc"] = np.ascontiguousarray(np.broadcast_to(np.asarray(inp["sg_ln_b"], np.float32)[0].reshape(1, 2048), (128, 2048)))
    shared["sg_w_sT"] = np.ascontiguousarray(np.asarray(inp["sg_w_s"], np.float32)[0].transpose(2, 0, 1))
    shared["sg_b_s"] = np.ascontiguousarray(np.asarray(inp["sg_b_s"], np.float32)[0].reshape(1, 1024))
    shared["final_g_bc"] = np.ascontiguousarray(np.broadcast_to(np.asarray(inp["final_g"], np.float32).reshape(1, 1024), (128, 1024)))
    shared["ident"] = np.eye(128, dtype=np.float32)
    tri = np.zeros((128, 2, 128), np.float32)
    ss, tt = np.meshgrid(np.arange(128), np.arange(128), indexing="ij")
    tri[:, 0, :] = (tt >= ss)
    tri[:, 1, :] = (tt <= ss)
    shared["tri"] = tri
    sel = np.zeros((64, 8, 128), np.float32)
    for d in range(2):
        for h in range(4):
            sel[d * 32 + h, d * 4 + h, :] = 1.0
    shared["selm"] = sel
    rpb = np.asarray(inp["na_rpb"], np.float32)[0]
    c = np.asarray(inp["c"], np.float32)
    cctx = np.asarray(inp["c_ctx"], np.float32)
    ctx = np.asarray(inp["ctx"], np.float32)
    maps = []
    for core in range(8):
        b, s = core // 4, core % 4
        rm = _rowmap(s)
        tok = (rm[:, None] * 64 + np.arange(64)[None, :]).reshape(-1)
        xin = np.concatenate([x[b][tok], ctx[b]], axis=0)
        cT = np.stack([c[b].reshape(8, 128).T, cctx.reshape(8, 128).T], axis=-1)
        cos4, sin4 = _rope_tables(s)
        selv = np.zeros((128, 16), np.float32)
        for j in range(4):
            selv[:, j] = 1.0 if j < s else 0.0
            selv[:, 4 + j] = 1.0 - selv[:, j]
            selv[:, 8 + j] = 1.0 if j > s else 0.0
            selv[:, 12 + j] = 1.0 - selv[:, 8 + j]
        m = dict(shared)
        m["xin"] = np.ascontiguousarray(xin)
        m["cT"] = np.ascontiguousarray(cT.astype(np.float32))
        m["ropecos"] = cos4
        m["ropesin"] = sin4
        nb = _na_bias_tables(rpb, s)
        nbp = np.full((5, 8, 128, 640), NEG, np.float32)
        nbp[..., :576] = nb
        m["nabiasT"] = np.ascontiguousarray(nbp.reshape(5, 8, 128, 5, 128).transpose(0, 1, 4, 3, 2))
        m["selv"] = selv
        maps.append(m)
    return maps


INPUT_SHAPES = {
    "xin": [NU, 1024], "cT": [128, 8, 2], "w_mod": [2, 1024, 9216], "b_modT": [128, 2, 72], "norm_gT": [128, 2, 3, 8],
    "ffn_w_in": [2, 2, 1024, 5632], "ffn_w_out": [2, 2, 2816, 1024], "mix_w_in": [1024, 3584], "w_gate": [1024, 128],
    "gate_b": [128, 1], "headg_bc": [128, 512], "mix_w_out": [1024, 1024], "sg_w_in": [1024, 4096], "sg_w_out": [2048, 1024],
    "sg_lng_bc": [128, 2048], "sg_lnb_bc": [128, 2048], "sg_w_sT": [128, 8, 128], "sg_b_s": [1, 1024], "final_g_bc": [128, 1024],
    "ident": [128, 128], "tri": [128, 2, 128], "selm": [64, 8, 128], "ropecos": [NU, 512], "ropesin": [NU, 512],
    "nabiasT": [5, 8, 128, 5, 128], "selv": [128, 16],
}


class Ctx:
    pass


_UC = [0]


def _u():
    _UC[0] += 1
    return "u%d" % _UC[0]


def build(stages="all", dbg=()):
    nc = bass.Bass("TRN2", target_bir_lowering=False)
    I = {k: nc.dram_tensor(k, shp, F32, kind="ExternalInput").ap() for k, shp in INPUT_SHAPES.items()}
    OUT = nc.dram_tensor("out", [NOWN, 1024], F32, kind="ExternalOutput").ap()

    def scratch(name, shape, dt):
        if name in dbg:
            return nc.dram_tensor(name, shape, dt, kind="ExternalOutput").ap()
        return nc.dram_tensor(name, shape, dt).ap()

    G = Ctx()
    G.nc, G.I, G.OUT = nc, I, OUT
    G.dbg = dbg
    if dbg:
        G.DBG = {k: nc.dram_tensor(k, shp, dt, kind="ExternalOutput").ap() for k, (shp, dt) in {
            "D_MOD": ([128, 3, 2, 3, 2, 8], F32), "D_X0": ([128, 8, TW], F32), "D_X1": ([128, 8, TW], F32),
            "D_H": ([128, 8, TW], BF16), "D_HID": ([128, 22, TW], BF16), "D_RSTD": ([128, TW], F32)}.items()}
    G.XT = scratch("XT", [128, 8, NU], F32)
    G.NAQT = scratch("NAQT", [128, 4, NU], BF16)
    G.NAKT = scratch("NAKT", [128, 4, NU], BF16)
    G.NAV = scratch("NAV", [NU, 520], BF16)
    G.MLQT = scratch("MLQT", [128, 4, NU], BF16)
    G.MLKT = scratch("MLKT", [128, 4, NU], BF16)
    G.MLK = scratch("MLK", [NU, 512], BF16)
    G.MLV = scratch("MLV", [NU, 516], BF16)
    G.MLG2 = scratch("MLG2", [NU, 512], F32)
    G.GIF = scratch("GIF", [128, NU], F32)
    G.YT = scratch("YT", [128, 8, NOWN], BF16)
    G.HF = scratch("HF", [NOWN, 512], F32)
    G.HB = scratch("HB", [NOWN, 512], F32)
    G.CCI = scratch("CCI", [128, 1040], F32)
    G.CCO = scratch("CCO", [512, 1040], F32)

    with contextlib.ExitStack() as gst:
        P = Prog(nc, gst)
        G.P = P

        def gsb(name, shape, dt):
            return gst.enter_context(nc.sbuf_tensor(name, shape, dt))
        G.ident_f = gsb("ident_f", [128, 128], F32)
        G.ident_b = gsb("ident_b", [128, 128], BF16)
        G.ident8 = gsb("ident8", [128, 128], BF16)
        G.ones_b = gsb("ones_b", [128, 128], BF16)
        G.ones_f = gsb("ones_f", [128, 128], F32)
        G.c_eps = gsb("c_eps", [128, 1], F32)
        G.c_one = gsb("c_one", [128, 1], F32)
        G.SH = gsb("SH", [128, 2, 3, 2, 8], F32)
        G.GM = gsb("GM", [128, 2, 3, 2, 8], F32)
        G.GT = gsb("GT", [128, 2, 3, 2, 8], F32)

        P.dma("sync", lambda e: e.dma_start(out=G.ident_f[:], in_=I["ident"][:, :]), writes=["ident_f"])
        P.op("vector", lambda e: e.tensor_copy(G.ident_b[:], G.ident_f[:]), reads=["ident_f"], writes=["ident_b"])
        P.op("vector", lambda e: e.tensor_scalar(G.ident8[:], G.ident_f[:], 8.0, None, ALU.mult), reads=["ident_f"], writes=["ident8"])
        P.op("vector", lambda e: e.memset(G.ones_b[:], 1.0), writes=["ones_b"])
        P.op("vector", lambda e: e.memset(G.ones_f[:], 1.0), writes=["ones_f"])
        P.op("vector", lambda e: e.memset(G.c_eps[:], EPS), writes=["c_eps"])
        P.op("vector", lambda e: e.memset(G.c_one[:], 1.0), writes=["c_one"])

        ext_tiles = [(ti, 0) for ti in range(18)] + [(18, 1)]
        own_tiles = [(ti, 0) for ti in range(1, 17)]
        S = stages
        stage_mod(G)
        stage_t0(G)
        if S in ("ffn_only",):
            stage_ffn(G, 0, 0, ext_tiles)
            stage_final(G)
        elif S == "ffn_dbg":
            stage_ffn(G, 0, 0, [(1, 0)])
        else:
            stage_ffn(G, 0, 0, ext_tiles)
            stage_inproj(G)
            stage_na(G)
            stage_ml(G)
            stage_outproj(G)
            stage_ffn(G, 0, 1, own_tiles)
            stage_ffn(G, 1, 0, own_tiles)
            stage_sg(G)
            stage_ffn(G, 1, 1, own_tiles)
            stage_final(G)
    return nc


_AC = [0]


def _alloc(G, st):
    nc = G.nc
    _AC[0] += 1
    sfx = "_%d" % _AC[0]

    def sb(name, shape, dt):
        return st.enter_context(nc.sbuf_tensor(name + sfx, shape, dt))

    def ps(name, shape, dt=F32):
        return st.enter_context(nc.psum_tensor(name + sfx, shape, dt))
    return sb, ps


def stage_mod(G):
    P, I, nc = G.P, G.I, G.nc
    with contextlib.ExitStack() as st:
        sb, ps = _alloc(G, st)
        cT = sb("m_cT", [128, 8, 2], F32)
        sil = sb("m_sil", [128, 8, 2], BF16)
        wm = [sb("m_wm%d" % i, [128, 8, 1024], BF16) for i in range(3)]
        modv = sb("m_modv", [128, 2, 72, 2], F32)
        bmod = sb("m_bmod", [128, 2, 72], F32)
        ng = sb("m_ng", [128, 2, 3, 8], F32)
        pp = [ps("m_ps%d" % i, [128, 8, 2]) for i in range(2)]
        xs = [sb("t_xs%d" % i, [128, 1024], F32) for i in range(3)]
        xo = [sb("t_xo%d" % i, [128, 8, 128], F32) for i in range(2)]
        pt = [ps("t_pt%d" % i, [128, 8, 128]) for i in range(2)]
        P.dma("sync", lambda e: e.dma_start(out=cT[:], in_=I["cT"][:, :, :]), writes=["m_cT"])
        P.dma("sync", lambda e: e.dma_start(out=bmod[:], in_=I["b_modT"][:, :, :]), writes=["m_bmod"])
        P.dma("sync", lambda e: e.dma_start(out=ng[:], in_=I["norm_gT"][:, :, :, :]), writes=["m_ng"])
        P.op("scalar", lambda e: e.activation(out=sil[:], in_=cT[:], func=AF.Silu), reads=["m_cT"], writes=["m_sil"])
        NSUB = NU // 128

        def t0_load(i):
            if i < NSUB:
                b3 = i % 3
                P.dma("sync", lambda e, b3=b3, i=i: e.dma_start(out=xs[b3][:], in_=I["xin"][i * 128:(i + 1) * 128, :]), writes=["t_xs%d" % b3])

        def t0_sub(i):
            t0 = i * 128
            b, b3 = i % 2, i % 3
            for k in range(8):
                P.op("tensor", lambda e, b=b, b3=b3, k=k: e.transpose(pt[b][:, k, :], xs[b3][:, k * 128:(k + 1) * 128], G.ident_f[:]),
                     reads=["t_xs%d" % b3, "ident_f"], writes=["t_pt%d" % b], inc=(k == 7))
            if b == 0:
                P.op("vector", lambda e, b=b: e.tensor_copy(xo[b][:], pt[b][:]), reads=["t_pt%d" % b], writes=["t_xo%d" % b])
            else:
                P.op("scalar", lambda e, b=b: e.activation(out=xo[b][:], in_=pt[b][:], func=AF.Identity), reads=["t_pt%d" % b], writes=["t_xo%d" % b])
            t0_load(i + 3)
            P.dma("sync", lambda e, b=b, t0=t0: e.dma_start(out=G.XT[:, :, t0:t0 + 128], in_=xo[b][:]),
                  reads=["t_xo%d" % b], writes=["XT%d" % (t0 // TW)])

        for i in range(3):
            t0_load(i)
        n = 0
        ti = 0
        for l in range(2):
            wsrc = I["w_mod"][l].rearrange("(k p) n -> p k n", p=128)
            for blk in range(9):
                w = wm[n % 3]
                wn = "m_wm%d" % (n % 3)
                pt_ = pp[n % 2]
                pn = "m_ps%d" % (n % 2)
                P.dma("gpsimd", lambda e, w=w, blk=blk, wsrc=wsrc: e.dma_start(out=w[:], in_=wsrc[:, :, blk * 1024:(blk + 1) * 1024]),
                      writes=[wn])
                for jj in range(8):
                    for k in range(8):
                        P.op("tensor", lambda e, w=w, pt_=pt_, jj=jj, k=k: e.matmul(
                            pt_[:, jj, :], lhsT=w[:, k, jj * 128:(jj + 1) * 128], rhs=sil[:, k, :], start=(k == 0), stop=(k == 7)),
                            reads=[wn, "m_sil"], writes=[pn], inc=(jj == 7 and k == 7))
                for m in range(2):
                    P.op("vector", lambda e, pt_=pt_, l=l, blk=blk, m=m: e.tensor_tensor(
                        out=modv[:, l, blk * 8:(blk + 1) * 8, m], in0=pt_[:, :, m], in1=bmod[:, l, blk * 8:(blk + 1) * 8], op=ALU.add),
                        reads=[pn, "m_bmod"], writes=["m_modv"])
                n += 1
                for _ in range(2):
                    if ti < NSUB:
                        t0_sub(ti)
                        ti += 1
        while ti < NSUB:
            t0_sub(ti)
            ti += 1
        for l in range(2):
            for j in range(3):
                for m in range(2):
                    sh = modv[:, l, (3 * j) * 8:(3 * j) * 8 + 8, m]
                    sc = modv[:, l, (3 * j + 1) * 8:(3 * j + 1) * 8 + 8, m]
                    gt = modv[:, l, (3 * j + 2) * 8:(3 * j + 2) * 8 + 8, m]
                    P.op("vector", lambda e, sh=sh, l=l, j=j, m=m: e.tensor_copy(G.SH[:, l, j, m, :], sh), reads=["m_modv"], writes=["modc"])
                    P.op("vector", lambda e, sc=sc, l=l, j=j, m=m: e.scalar_tensor_tensor(
                        out=G.GM[:, l, j, m, :], in0=sc, scalar=1.0, in1=ng[:, l, j, :], op0=ALU.add, op1=ALU.mult),
                        reads=["m_modv", "m_ng"], writes=["modc"])
                    P.op("vector", lambda e, gt=gt, l=l, j=j, m=m: e.tensor_scalar(
                        G.GT[:, l, j, m, :], gt, (1.0 if j == 1 else 0.5), None, ALU.mult), reads=["m_modv"], writes=["modc"])
        P.barrier()
        P.emit()


def stage_t0(G):
    return


class NormScratch:
    def __init__(self, G, sb, ps, pfx, W=TW):
        self.sq = [sb(pfx + "sq%d" % i, [128, W], BF16) for i in range(2)]
        self.rs = sb(pfx + "rs", [128, W], F32)
        self.rstd = sb(pfx + "rstd", [128, W], F32)
        self.tmp = [sb(pfx + "tmp%d" % i, [128, W], F32) for i in range(2)]
        self.pn = ps(pfx + "pn", [128, W])
        self.pfx = pfx


def norm_mod(G, NS, x, xname, gm, sh, h, hname, W=TW, phase=None):
    P = G.P
    pfx = NS.pfx
    for k in range(8):
        if phase is None:
            sq, sqn = NS.sq[k % 2], pfx + "sq%d" % (k % 2)
        else:
            sq, sqn = NS.sq8[k], pfx + "sq8_%d" % k
        if phase in (None, "A"):
            P.op("scalar", lambda e, sq=sq, k=k: e.activation(out=sq[:, :W], in_=x[:, k, :], func=AF.Square), reads=[xname], writes=[sqn])
        if phase in (None, "B"):
            P.op("tensor", lambda e, sq=sq, k=k: e.matmul(NS.pn[:, :W], lhsT=G.ones_b[:], rhs=sq[:, :W], start=(k == 0), stop=(k == 7)),
                 reads=[sqn, "ones_b"], writes=[pfx + "pn"], inc=True)
    if phase == "A":
        return
    P.op("scalar", lambda e: e.activation(out=NS.rs[:, :W], in_=NS.pn[:, :W], func=AF.Sqrt, bias=G.c_eps[:], scale=1.0 / 1024.0),
         reads=[pfx + "pn", "c_eps"], writes=[pfx + "rs"])
    P.op("vector", lambda e: e.reciprocal(NS.rstd[:, :W], NS.rs[:, :W]), reads=[pfx + "rs"], writes=[pfx + "rstd"])
    for k in range(8):
        tmp = NS.tmp[k % 2]
        tn = pfx + "tmp%d" % (k % 2)
        P.op("vector", lambda e, tmp=tmp, k=k: e.tensor_tensor(out=tmp[:, :W], in0=x[:, k, :], in1=NS.rstd[:, :W], op=ALU.mult),
             reads=[xname, pfx + "rstd"], writes=[tn])
        P.op("scalar", lambda e, tmp=tmp, k=k: e.activation(out=h[:, k, :], in_=tmp[:, :W], func=AF.Identity, bias=sh[:, k:k + 1], scale=gm[:, k:k + 1]),
             reads=[tn, "modc"], writes=[hname])


def load_w_cast(G, dst, dname, src, nk, ncols, step=1024):
    P = G.P
    v = src.rearrange("(k p) n -> p k n", p=128)
    for c0 in range(0, ncols, step):
        c1 = min(ncols, c0 + step)
        P.dma("gpsimd", lambda e, c0=c0, c1=c1: e.dma_start(out=dst[:, :, c0:c1], in_=v[:, :, c0:c1]), writes=[dname])


def stage_ffn(G, l, i, tiles):
    P, I, nc = G.P, G.I, G.nc
    j = 0 if i == 0 else 2
    with contextlib.ExitStack() as st:
        sb, ps = _alloc(G, st)
        wi = sb("f_wi", [128, 8, 2 * DFF], BF16)
        wo = sb("f_wo", [128, 22, 1024], BF16)
        xt = [sb("f_xt%d" % b, [128, 8, TW], F32) for b in range(2)]
        hh = [sb("f_h%d" % b, [128, 8, TW], BF16) for b in range(2)]
        hid = sb("f_hid", [128, 22, TW], BF16)
        sa = [sb("f_sa%d" % b, [128, TW], F32) for b in range(2)]
        NS = NormScratch(G, sb, ps, "f_")
        NS.sq8 = [sb("f_sq8_%d" % k, [128, TW], BF16) for k in range(8)]
        pa = [ps("f_pa%d" % b, [128, TW]) for b in range(2)]
        pb = [ps("f_pb%d" % b, [128, TW]) for b in range(2)]
        po = [ps("f_po%d" % b, [128, TW]) for b in range(2)]
        wv_ = I["ffn_w_in"][l, i].rearrange("(k p) n -> p k n", p=128)
        for pc in (0, 2, 3, 1, 4, 5):
            c0, c1 = pc * 1024, min(2 * DFF, (pc + 1) * 1024)
            P.dma("gpsimd", lambda e, c0=c0, c1=c1: e.dma_start(out=wi[:, :, c0:c1], in_=wv_[:, :, c0:c1]), writes=["f_wi%d" % pc])
        load_w_cast(G, wo, "f_wo", I["ffn_w_out"][l, i], 22, 1024)
        def ld(n):
            ti_ = tiles[n][0]
            xb = xt[n % 2]
            P.dma("sync", lambda e, xb=xb, ti_=ti_: e.dma_start(out=xb[:], in_=G.XT[:, :, ti_ * TW:(ti_ + 1) * TW]),
                  reads=["XT%d" % ti_], writes=["f_xt%d" % (n % 2)])
        def do_norm(n, phase=None):
            ti_, m_ = tiles[n]
            bb = n % 2
            norm_mod(G, NS, xt[bb], "f_xt%d" % bb, G.GM[:, l, j, m_, :], G.SH[:, l, j, m_, :], hh[bb], "f_h%d" % bb, phase=phase)
        ld(0)
        if len(tiles) > 1:
            ld(1)
        do_norm(0)
        for n, (ti, m) in enumerate(tiles):
            t0 = ti * TW
            b = n % 2
            x, xn = xt[b], "f_xt%d" % b
            h, hn = hh[b], "f_h%d" % b
            for jj in range(22):
                q = jj % 2
                for half, pp, pn in ((0, pa[q], "f_pa%d" % q), (1, pb[q], "f_pb%d" % q)):
                    c0 = half * DFF + jj * 128
                    for k in range(8):
                        P.op("tensor", lambda e, pp=pp, c0=c0, k=k, h=h: e.matmul(
                            pp[:], lhsT=wi[:, k, c0:c0 + 128], rhs=h[:, k, :], start=(k == 0), stop=(k == 7)),
                            reads=["f_wi%d" % (c0 // 1024), "f_wi%d" % ((c0 + 127) // 1024), hn], writes=[pn], inc=(k == 7))
                P.op("scalar", lambda e, q=q: e.activation(out=sa[q][:], in_=pa[q][:], func=AF.Silu), reads=["f_pa%d" % q], writes=["f_sa%d" % q])
                P.op("vector", lambda e, q=q, jj=jj: e.tensor_tensor(out=hid[:, jj, :], in0=sa[q][:], in1=pb[q][:], op=ALU.mult),
                     reads=["f_sa%d" % q, "f_pb%d" % q], writes=["f_hid%d" % jj])
            if n + 1 < len(tiles):
                do_norm(n + 1, "A")
            for f in range(8):
                q = f % 2
                if f == 3 and n + 1 < len(tiles):
                    do_norm(n + 1, "B")
                for jj in range(22):
                    P.op("tensor", lambda e, q=q, f=f, jj=jj: e.matmul(
                        po[q][:], lhsT=wo[:, jj, f * 128:(f + 1) * 128], rhs=hid[:, jj, :], start=(jj == 0), stop=(jj == 21)),
                        reads=["f_wo", "f_hid%d" % jj], writes=["f_po%d" % q], inc=(jj == 21))
                gsc = G.GT[:, l, j, m, f:f + 1]
                P.op("vector", lambda e, q=q, f=f, x=x, gsc=gsc: e.scalar_tensor_tensor(
                    out=x[:, f, :], in0=po[q][:], scalar=gsc, in1=x[:, f, :], op0=ALU.mult, op1=ALU.add),
                    reads=["f_po%d" % q, "modc", xn], writes=[xn])
            if G.dbg and ti == 1:
                P.dma("sync", lambda e: e.dma_start(out=G.DBG["D_HID"][:, :, :], in_=hid[:]), reads=["f_hid%d" % q for q in range(22)], writes=["dbg3"])
                P.dma("sync", lambda e, x=x: e.dma_start(out=G.DBG["D_X1"][:, :, :], in_=x[:]), reads=[xn], writes=["dbg4"])
            P.dma("sync", lambda e, x=x, t0=t0: e.dma_start(out=G.XT[:, :, t0:t0 + TW], in_=x[:]), reads=[xn], writes=["XT%d" % ti])
            if n + 2 < len(tiles):
                ld(n + 2)
        P.barrier()
        P.emit()


def stage_final(G):
    P, I, nc = G.P, G.I, G.nc
    with contextlib.ExitStack() as st:
        sb, ps = _alloc(G, st)
        fg = sb("k_fg", [128, 1024], F32)
        xs = [sb("k_xs%d" % b, [128, 8, 128], F32) for b in range(2)]
        junk = sb("k_junk", [128, 1024], F32)
        yo = [sb("k_yo%d" % b, [128, 1024], F32) for b in range(2)]
        ssq = sb("k_ssq", [128, 2], F32)
        rs = sb("k_rs", [128, 2], F32)
        pt = [ps("k_pt%d" % b, [128, 8, 128]) for b in range(2)]
        P.dma("sync", lambda e: e.dma_start(out=fg[:], in_=I["final_g_bc"][:, :]), writes=["k_fg"])
        for i in range(NOWN // 128):
            t0 = OWN0 + i * 128
            b = i % 2
            P.dma("gpsimd", lambda e, b=b, t0=t0: e.dma_start(out=xs[b][:], in_=G.XT[:, :, t0:t0 + 128]),
                  reads=["XT%d" % (t0 // TW)], writes=["k_xs%d" % b])
            for k in range(8):
                P.op("tensor", lambda e, b=b, k=k: e.transpose(pt[b][:, k, :], xs[b][:, k, :], G.ident_f[:]),
                     reads=["k_xs%d" % b, "ident_f"], writes=["k_pt%d" % b], inc=(k == 7))
            ptf = pt[b][:].rearrange("p k n -> p (k n)")
            P.op("scalar", lambda e, b=b, ptf=ptf: e.activation(out=junk[:], in_=ptf, func=AF.Square, accum_out=ssq[:, b:b + 1]),
                 reads=["k_pt%d" % b], writes=["k_junk", "k_ssq%d" % b])
            P.op("scalar", lambda e, b=b: e.activation(out=rs[:, b:b + 1], in_=ssq[:, b:b + 1], func=AF.Sqrt, bias=G.c_eps[:], scale=1.0 / 1024.0),
                 reads=["k_ssq%d" % b, "c_eps"], writes=["k_rs%d" % b])
            P.op("vector", lambda e, b=b: e.reciprocal(rs[:, b:b + 1], rs[:, b:b + 1]), reads=["k_rs%d" % b], writes=["k_rs%d" % b])
            P.op("vector", lambda e, b=b, ptf=ptf: e.scalar_tensor_tensor(
                out=yo[b][:], in0=ptf, scalar=rs[:, b:b + 1], in1=fg[:], op0=ALU.mult, op1=ALU.mult),
                reads=["k_pt%d" % b, "k_rs%d" % b, "k_fg"], writes=["k_yo%d" % b])
            P.dma("sync", lambda e, b=b, i=i: e.dma_start(out=G.OUT[i * 128:(i + 1) * 128, :], in_=yo[b][:]),
                  reads=["k_yo%d" % b], writes=["OUT"])
        P.barrier()
        P.emit()


def stage_inproj(G):
    P, I, nc = G.P, G.I, G.nc
    l, j = 0, 1
    tiles = [(ti, 0) for ti in range(18)] + [(18, 1)]
    with contextlib.ExitStack() as st:
        sb, ps = _alloc(G, st)
        wfm = sb("i_wfm", [128, 8, 1024], BF16)
        wg = sb("i_wg", [128, 8, 128], BF16)
        wtm = sb("i_wtm", [128, 8, 2560], BF16)
        xt = [sb("i_xt%d" % b, [128, 8, TW], F32) for b in range(2)]
        hh = [sb("i_h%d" % b, [128, 8, TW], BF16) for b in range(2)]
        NS = NormScratch(G, sb, ps, "i_")
        NS.sq8 = [sb("i_sq8_%d" % k, [128, TW], BF16) for k in range(8)]
        fm = [sb("i_fm%d" % b, [128, 8, TW], BF16) for b in range(2)]
        gts = [sb("i_gt%d" % b, [128, TW], F32) for b in range(2)]
        cosT = [sb("i_cos%d" % b, [128, 512], F32) for b in range(2)]
        sinT = [sb("i_sin%d" % b, [128, 512], F32) for b in range(2)]
        hg = sb("i_hg", [128, 512], F32)
        vt = [sb("i_vt%d" % b, [128, 8, 65], BF16) for b in range(2)]
        mv = [sb("i_mv%d" % b, [128, 4, 129], BF16) for b in range(2)]
        xs = [sb("i_xs%d" % b, [128, 512], F32) for b in range(2)]
        r1 = [sb("i_r1%d" % b, [128, 512], F32) for b in range(2)]
        r2 = [sb("i_r2%d" % b, [128, 512], F32) for b in range(2)]
        qr = [sb("i_qr%d" % b, [128, 512], BF16) for b in range(2)]
        sg = sb("i_sg", [128, 512], F32)
        g2 = [sb("i_g2%d" % b, [128, 512], F32) for b in range(2)]
        tq = [sb("i_tq%d" % b, [128, 4, 128], BF16) for b in range(2)]
        pfm = [ps("i_pfm%d" % b, [128, TW]) for b in range(2)]
        ptm = [ps("i_ptm%d" % b, [128, 512]) for b in range(2)]
        ptr = [ps("i_ptr%d" % b, [128, 4, 128], BF16) for b in range(2)]
        load_w_cast(G, wfm, "i_wfm", I["mix_w_in"][:, 0:1024], 8, 1024)
        load_w_cast(G, wg, "i_wg", I["w_gate"], 8, 128)
        load_w_cast(G, wtm, "i_wtm", I["mix_w_in"][:, 1024:3584], 8, 2560)
        P.dma("sync", lambda e: e.dma_start(out=hg[:], in_=I["headg_bc"][:, :]), writes=["i_hg"])
        for b in range(2):
            P.op("vector", lambda e, b=b: e.memset(vt[b][:], 1.0), writes=["i_vt%d" % b])
            P.op("vector", lambda e, b=b: e.memset(mv[b][:], 1.0), writes=["i_mv%d" % b])

        def ld(n):
            ti_ = tiles[n][0]
            xb = xt[n % 2]
            P.dma("gpsimd", lambda e, xb=xb, ti_=ti_: e.dma_start(out=xb[:], in_=G.XT[:, :, ti_ * TW:(ti_ + 1) * TW]),
                  reads=["XT%d" % ti_], writes=["i_xt%d" % (n % 2)])
        ld(0)
        cnt = {"s": 0, "r": 0, "t": 0}
        deferred = []

        def rope_and_T(pt_, ptn, scale, dstT, tmaj_dst, ts0):
            a = cnt["r"] % 2
            cnt["r"] += 1
            sb_ = cnt["s"] % 2
            X, R1, R2, QR, TQ, PT = xs[a], r1[a], r2[a], qr[a], tq[a], ptr[a]
            xn_, r1n, r2n, qrn, tqn, ptn2 = "i_xs%d" % a, "i_r1%d" % a, "i_r2%d" % a, "i_qr%d" % a, "i_tq%d" % a, "i_ptr%d" % a
            P.op("scalar", lambda e: e.activation(out=X[:], in_=pt_[:], func=AF.Copy, scale=scale), reads=[ptn], writes=[xn_])
            P.op("vector", lambda e: e.tensor_tensor(out=R1[:], in0=X[:], in1=cosT[sb_][:], op=ALU.mult), reads=[xn_, "i_cos%d" % sb_], writes=[r1n])
            Xv = X[:].rearrange("p (i t) -> p i t", t=2)
            Sv = sinT[sb_][:].rearrange("p (i t) -> p i t", t=2)
            Rv = R2[:].rearrange("p (i t) -> p i t", t=2)
            P.op("vector", lambda e: e.tensor_tensor(out=Rv[:, :, 0], in0=Xv[:, :, 1], in1=Sv[:, :, 0], op=ALU.mult),
                 reads=[xn_, "i_sin%d" % sb_], writes=[r2n])
            P.op("vector", lambda e: e.tensor_tensor(out=Rv[:, :, 1], in0=Xv[:, :, 0], in1=Sv[:, :, 1], op=ALU.mult),
                 reads=[xn_, "i_sin%d" % sb_], writes=[r2n])
            P.op("vector", lambda e: e.tensor_tensor(out=QR[:], in0=R1[:], in1=R2[:], op=ALU.add), reads=[r1n, r2n], writes=[qrn])
            if tmaj_dst is not None:
                P.dma("sync", lambda e: e.dma_start(out=tmaj_dst[ts0:ts0 + 128, :], in_=QR[:]), reads=[qrn], writes=[_u()])
            def later():
                for hd in range(4):
                    P.op("tensor", lambda e, hd=hd: e.transpose(PT[:, hd, :], QR[:, hd * 128:(hd + 1) * 128], G.ident_b[:]),
                         reads=[qrn, "ident_b"], writes=[ptn2], inc=(hd == 3))
                P.op("scalar", lambda e: e.activation(out=TQ[:], in_=PT[:], func=AF.Copy), reads=[ptn2], writes=[tqn])
                P.dma("sync", lambda e: e.dma_start(out=dstT[:, :, ts0:ts0 + 128], in_=TQ[:]), reads=[tqn], writes=[_u()])
            deferred.append(later)

        for n, (ti, m) in enumerate(tiles):
            t0 = ti * TW
            b = n % 2
            x, xn = xt[b], "i_xt%d" % b
            h, hn = hh[b], "i_h%d" % b
            if n + 1 < len(tiles):
                ld(n + 1)
            if n == 0:
                norm_mod(G, NS, x, xn, G.GM[:, l, j, m, :], G.SH[:, l, j, m, :], h, hn)
            FM, fmn = fm[b], "i_fm%d" % b
            no_q = ti in (0, 17, 18)
            blks = [0] if ti in (0, 17) else ([0, 2, 3] if ti == 18 else [0, 1, 2, 3, 4])
            for fc in (range(4, 8) if no_q else range(8)):
                q = fc % 2
                for k in range(8):
                    P.op("tensor", lambda e, q=q, fc=fc, k=k, h=h: e.matmul(
                        pfm[q][:], lhsT=wfm[:, k, fc * 128:(fc + 1) * 128], rhs=h[:, k, :], start=(k == 0), stop=(k == 7)),
                        reads=["i_wfm", hn], writes=["i_pfm%d" % q], inc=(k == 7))
                P.op("scalar", lambda e, q=q, fc=fc, FM=FM: e.activation(out=FM[:, fc, :], in_=pfm[q][:], func=AF.Copy),
                     reads=["i_pfm%d" % q], writes=[fmn])
            if not no_q:
                P.dma("sync", lambda e, FM=FM, t0=t0: e.dma_start(out=G.NAQT[:, :, t0:t0 + TW], in_=FM[:, 0:4, :]), reads=[fmn], writes=[_u()])
            P.dma("sync", lambda e, FM=FM, t0=t0: e.dma_start(out=G.NAKT[:, :, t0:t0 + TW], in_=FM[:, 4:8, :]), reads=[fmn], writes=[_u()])
            GTS, gtn = gts[b], "i_gt%d" % b
            for k in range(8):
                P.op("tensor", lambda e, k=k, h=h: e.matmul(pfm[0][:], lhsT=wg[:, k, :], rhs=h[:, k, :], start=(k == 0), stop=(k == 7)),
                     reads=["i_wg", hn], writes=["i_pfm0"], inc=(k == 7))
            P.op("vector", lambda e, GTS=GTS: e.tensor_copy(GTS[:], pfm[0][:]), reads=["i_pfm0"], writes=[gtn])
            P.dma("sync", lambda e, GTS=GTS, t0=t0: e.dma_start(out=G.GIF[:, t0:t0 + TW], in_=GTS[:]), reads=[gtn], writes=[_u()])
            if n + 1 < len(tiles):
                ti2, m2 = tiles[n + 1]
                b2 = (n + 1) % 2
                norm_mod(G, NS, xt[b2], "i_xt%d" % b2, G.GM[:, l, j, m2, :], G.SH[:, l, j, m2, :], hh[b2], "i_h%d" % b2, phase="A")
            for s_ in range(TW // 128):
                if s_ == 1 and n + 1 < len(tiles):
                    norm_mod(G, NS, xt[b2], "i_xt%d" % b2, G.GM[:, l, j, m2, :], G.SH[:, l, j, m2, :], hh[b2], "i_h%d" % b2, phase="B")
                ts0 = t0 + s_ * 128
                sbi = cnt["s"] % 2
                P.dma("gpsimd", lambda e, sbi=sbi, ts0=ts0: e.dma_start(out=cosT[sbi][:], in_=I["ropecos"][ts0:ts0 + 128, :]), writes=["i_cos%d" % sbi])
                P.dma("gpsimd", lambda e, sbi=sbi, ts0=ts0: e.dma_start(out=sinT[sbi][:], in_=I["ropesin"][ts0:ts0 + 128, :]), writes=["i_sin%d" % sbi])
                for blk in blks:
                    a = cnt["t"] % 2
                    cnt["t"] += 1
                    PT_, ptn = ptm[a], "i_ptm%d" % a
                    for k in range(8):
                        P.op("tensor", lambda e, PT_=PT_, k=k, h=h, s_=s_, blk=blk: e.matmul(
                            PT_[:], lhsT=h[:, k, s_ * 128:(s_ + 1) * 128], rhs=wtm[:, k, blk * 512:(blk + 1) * 512], start=(k == 0), stop=(k == 7)),
                            reads=["i_wtm", hn], writes=[ptn], inc=(k == 7))
                    while len(deferred) > (1 if blk == 3 else 0):
                        deferred.pop(0)()
                    if blk == 0:
                        VT = vt[sbi]
                        P.op("scalar", lambda e, VT=VT, PT_=PT_: e.activation(out=VT[:, :, 0:64], in_=PT_[:].rearrange("p (h d) -> p h d", d=64), func=AF.Copy),
                             reads=[ptn], writes=["i_vt%d" % sbi])
                        P.dma("sync", lambda e, VT=VT, ts0=ts0: e.dma_start(out=G.NAV[ts0:ts0 + 128, :], in_=VT[:].rearrange("p h d -> p (h d)")),
                              reads=["i_vt%d" % sbi], writes=[_u()])
                    elif blk == 1:
                        rope_and_T(PT_, ptn, 1.0, G.MLQT, None, ts0)
                    elif blk == 2:
                        rope_and_T(PT_, ptn, 128.0 ** -0.5, G.MLKT, G.MLK, ts0)
                    elif blk == 3:
                        MV = mv[sbi]
                        P.op("scalar", lambda e, MV=MV, PT_=PT_: e.activation(out=MV[:, :, 0:128], in_=PT_[:].rearrange("p (h d) -> p h d", d=128), func=AF.Copy),
                             reads=[ptn], writes=["i_mv%d" % sbi])
                        P.dma("sync", lambda e, MV=MV, ts0=ts0: e.dma_start(out=G.MLV[ts0:ts0 + 128, :], in_=MV[:].rearrange("p h d -> p (h d)")),
                              reads=["i_mv%d" % sbi], writes=[_u()])
                    else:
                        G2 = g2[sbi]
                        P.op("scalar", lambda e, PT_=PT_: e.activation(out=sg[:], in_=PT_[:], func=AF.Sigmoid), reads=[ptn], writes=["i_sg"])
                        P.op("vector", lambda e, G2=G2: e.tensor_tensor(out=G2[:], in0=sg[:], in1=hg[:], op=ALU.mult), reads=["i_sg", "i_hg"], writes=["i_g2%d" % sbi])
                        P.dma("sync", lambda e, G2=G2, ts0=ts0: e.dma_start(out=G.MLG2[ts0:ts0 + 128, :], in_=G2[:]), reads=["i_g2%d" % sbi], writes=[_u()])
                while deferred:
                    deferred.pop(0)()
                cnt["s"] += 1
        P.barrier()
        P.emit()


def stage_na(G):
    P, I, nc = G.P, G.I, G.nc
    with contextlib.ExitStack() as st:
        sb, ps = _alloc(G, st)
        KT = sb("n_KT", [128, 4, NU], BF16)
        V = sb("n_V", [128, 38, 520], BF16)
        QT = sb("n_QT", [128, 4, NOWN], BF16)
        BI = sb("n_BI", [128, 5, 8, 5, 128], BF16)
        sAb = [sb("n_sAb%d" % b, [128, 4, 128], F32) for b in range(2)]
        sBb = [sb("n_sBb%d" % b, [128, 128], F32) for b in range(2)]
        pt = [sb("n_pt%d" % b, [128, 7, 128], BF16) for b in range(2)]
        ya = [sb("n_ya%d" % b, [128, 512], BF16) for b in range(2)]
        rec = [sb("n_rec%d" % b, [128, 8], F32) for b in range(2)]
        yt = [sb("n_yt%d" % b, [128, 4, 128], BF16) for b in range(2)]
        sA = [ps("n_sA%d" % b, [128, 4, 128]) for b in range(2)]
        sB = [ps("n_sB%d" % b, [128, 4, 128]) for b in range(2)]
        O = ps("n_O", [128, 8, 128])
        ptr = ps("n_ptr", [128, 4, 128], BF16)
        P.dma("sync", lambda e: e.dma_start(out=KT[:], in_=G.NAKT[:, :, :]), reads=["dramNA"], writes=["n_KT"])
        P.dma("sync", lambda e: e.dma_start(out=V[:], in_=G.NAV.rearrange("(c p) n -> p c n", p=128)), reads=["dramNA"], writes=["n_V"])
        P.dma("sync", lambda e: e.dma_start(out=QT[:], in_=G.NAQT[:, :, OWN0:OWN0 + NOWN]), reads=["dramNA"], writes=["n_QT"])
        for c5 in range(5):
            P.dma("gpsimd", lambda e, c5=c5: e.dma_start(out=BI[:, c5], in_=I["nabiasT"][c5].rearrange("h k j q -> k h j q")), writes=["n_BI"])
        def scores(p, h, a):
            cls = 0 if p == 0 else 1 if p == 1 else 3 if p == 30 else 4 if p == 31 else 2
            hc, b0 = h // 2, (h % 2) * 64
            q_ap = QT[b0:b0 + 64, hc, p * 128:(p + 1) * 128]
            SA, SB, PT = sA[a], sB[a], pt[a]
            san, sbn, ptn = "n_sA%d" % a, "n_sB%d" % a, "n_pt%d" % a
            for jj in range(4):
                k0 = (p + jj) * 128
                P.op("tensor", lambda e, jj=jj, k0=k0: e.matmul(
                    SA[:, jj, :], lhsT=KT[b0:b0 + 64, hc, k0:k0 + 128], rhs=q_ap, start=True, stop=True),
                    reads=["n_KT", "n_QT"], writes=[san], inc=(jj == 3))
            k0h = (p + 4) * 128
            P.op("tensor", lambda e: e.matmul(
                SB[0:64, 0, :], lhsT=KT[b0:b0 + 64, hc, k0h:k0h + 64], rhs=q_ap, start=True, stop=True),
                reads=["n_KT", "n_QT"], writes=[sbn], inc=False)
            for c in range(2):
                k0 = CTX0 + c * 128
                P.op("tensor", lambda e, c=c, k0=k0: e.matmul(
                    SB[:, 1 + c, :], lhsT=KT[b0:b0 + 64, hc, k0:k0 + 128], rhs=q_ap, start=True, stop=True),
                    reads=["n_KT", "n_QT"], writes=[sbn], inc=(c == 1))
            AB, BB = sAb[a], sBb[a]
            abn, bbn = "n_sAb%d" % a, "n_sBb%d" % a
            P.op("vector", lambda e: e.scalar_tensor_tensor(
                out=AB[:], in0=SA[:], scalar=0.125, in1=BI[:, cls, h, 0:4, :], op0=ALU.mult, op1=ALU.add),
                reads=[san, "n_BI"], writes=[abn])
            P.op("vector", lambda e: e.scalar_tensor_tensor(
                out=BB[0:64, :], in0=SB[0:64, 0, :], scalar=0.125, in1=BI[0:64, cls, h, 4, :], op0=ALU.mult, op1=ALU.add),
                reads=[sbn, "n_BI"], writes=[bbn])
            P.op("scalar", lambda e: e.activation(out=PT[:, 0:4, :], in_=AB[:], func=AF.Exp), reads=[abn], writes=[ptn])
            P.op("scalar", lambda e: e.activation(out=PT[:, 5:7, :], in_=SB[:, 1:3, :], func=AF.Exp, scale=0.125), reads=[sbn, bbn], writes=[ptn])
            P.op("scalar", lambda e: e.activation(out=PT[0:64, 4, :], in_=BB[0:64, :], func=AF.Exp), reads=[bbn], writes=[ptn])

        def pv(p, h, a):
            pb2 = p % 2
            PT, ptn = pt[a], "n_pt%d" % a
            specs = [(jj, 128, p + jj) for jj in range(4)] + [(4, 64, p + 4), (5, 128, 36), (6, 128, 37)]
            for si, (slot, nk, vc) in enumerate(specs):
                P.op("tensor", lambda e, slot=slot, nk=nk, vc=vc, si=si: e.matmul(
                    O[:, h, 0:65], lhsT=PT[0:nk, slot, :], rhs=V[0:nk, vc, h * 65:(h + 1) * 65], start=(si == 0), stop=(si == 6)),
                    reads=[ptn, "n_V"], writes=["n_O%d" % (h // 4)], inc=(si == 6))
            if h in (3, 7):
                hf_ = h // 4
                R, YA = rec[pb2], ya[pb2]
                rn, yan = "n_rec%d_%d" % (pb2, hf_), "n_ya%d" % pb2
                P.op("vector", lambda e: e.reciprocal(R[:, hf_ * 4:hf_ * 4 + 4], O[:, hf_ * 4:hf_ * 4 + 4, 64]), reads=["n_O%d" % hf_], writes=[rn])
                for h2 in range(hf_ * 4, hf_ * 4 + 4):
                    P.op("scalar", lambda e, h2=h2: e.activation(out=YA[:, h2 * 64:(h2 + 1) * 64], in_=O[:, h2, 0:64], func=AF.Copy, scale=R[:, h2:h2 + 1]),
                         reads=["n_O%d" % hf_, rn], writes=[yan])
            if h == 7:
                YA, YT_ = ya[pb2], yt[pb2]
                yan, ytn = "n_ya%d" % pb2, "n_yt%d" % pb2
                for c in range(4):
                    P.op("tensor", lambda e, c=c: e.transpose(ptr[:, c, :], YA[:, c * 128:(c + 1) * 128], G.ident_b[:]),
                         reads=[yan, "ident_b"], writes=["n_ptr"], inc=(c == 3))
                P.op("vector", lambda e: e.tensor_copy(YT_[:], ptr[:]), reads=["n_ptr"], writes=[ytn])
                P.dma("sync", lambda e: e.dma_start(out=G.YT[:, 0:4, p * 128:(p + 1) * 128], in_=YT_[:]), reads=[ytn], writes=[_u()])

        items = [(p, h) for p in range(32) for h in range(8)]
        scores(items[0][0], items[0][1], 0)
        for i_, (p, h) in enumerate(items):
            if i_ + 1 < len(items):
                scores(items[i_ + 1][0], items[i_ + 1][1], (i_ + 1) % 2)
            pv(p, h, i_ % 2)
        P.barrier()
        P.emit()


def stage_ml(G):
    P, I, nc = G.P, G.I, G.nc
    NCH = 32
    with contextlib.ExitStack() as st:
        sb, ps = _alloc(G, st)
        TOK = sb("l_TOK", [128, 34, 5, 8], F32)
        EBEND = sb("l_EBEND", [128, 8, 32], F32)
        ATOT = sb("l_ATOT", [128, 8], F32)
        with contextlib.ExitStack() as st1:
            sb1, ps1 = _alloc(G, st1)
            LI = sb1("l_LI", [64, NU], F32)
            SP = sb1("l_SP", [64, NU], F32)
            CL = sb1("l_CL", [64, NU], F32)
            CG = sb1("l_CG", [64, NU], F32)
            TM = sb1("l_TM", [64, NU], F32)
            OQ = [sb1("l_OQ%d" % b, [64, NU], F32) for b in range(2)]
            gbI = sb1("l_gbI", [64, 1], F32)
            gbF = sb1("l_gbF", [64, 1], F32)
            CE = sb1("l_CE", [64, 32], F32)
            ntot = sb1("l_ntot", [64, 2], F32)
            tot = sb1("l_tot", [64, 2], F32)
            SELM = sb1("l_SELM", [64, 8, 128], F32)
            ptr = [ps1("l_ptr%d" % b, [128, 8, 64]) for b in range(2)]
            pe = ps1("l_pe", [128, 8, 32])
            pa = ps1("l_pa", [128, 8, 2])
            own = slice(OWN0, OWN0 + NOWN)
            cxs = slice(CTX0, CTX0 + 256)
            P.dma("sync", lambda e: e.dma_start(out=LI[:], in_=G.GIF[0:64, :]), reads=["dramML"], writes=["l_LI"])
            P.dma("sync", lambda e: e.dma_start(out=SP[:], in_=G.GIF[64:128, :]), reads=["dramML"], writes=["l_SP"])
            P.dma("sync", lambda e: e.dma_start(out=gbI[:], in_=I["gate_b"][0:64, :]), writes=["l_gbI"])
            P.dma("sync", lambda e: e.dma_start(out=gbF[:], in_=I["gate_b"][64:128, :]), writes=["l_gbF"])
            P.dma("sync", lambda e: e.dma_start(out=SELM[:], in_=I["selm"][:, :, :]), writes=["l_SELM"])
            P.op("vector", lambda e: e.tensor_scalar(gbF[:], gbF[:], -1.0, None, ALU.mult), reads=["l_gbF"], writes=["l_gbF"])
            P.op("scalar", lambda e: e.activation(out=LI[:], in_=LI[:], func=AF.Identity, bias=gbI[:]), reads=["l_LI", "l_gbI"], writes=["l_LI"])
            P.op("scalar", lambda e: e.activation(out=SP[:], in_=SP[:], func=AF.Exp, bias=gbF[:], scale=-1.0), reads=["l_SP", "l_gbF"], writes=["l_SP"])
            P.op("scalar", lambda e: e.activation(out=SP[:], in_=SP[:], func=AF.Ln, bias=G.c_one[0:64, :]), reads=["l_SP", "c_one"], writes=["l_SP"])
            P.op("vector", lambda e: e.memset(TM[:], 1.0), writes=["l_TM"])
            P.op("vector", lambda e: e.tensor_tensor_scan(out=CG[:, own], data0=TM[:, own], data1=SP[:, own], initial=0.0, op0=ALU.mult, op1=ALU.add),
                 reads=["l_TM", "l_SP"], writes=["l_CG"])
            P.op("vector", lambda e: e.tensor_tensor_scan(out=CG[:, cxs], data0=TM[:, cxs], data1=SP[:, cxs], initial=0.0, op0=ALU.mult, op1=ALU.add),
                 reads=["l_TM", "l_SP"], writes=["l_CG"])
            TMo = TM[:, own].rearrange("p (c t) -> p c t", t=128)
            P.op("vector", lambda e: e.memset(TMo[:, :, 0:1], 0.0), reads=["l_CG"], writes=["l_TM"])
            P.op("vector", lambda e: e.tensor_tensor_scan(out=CL[:, own], data0=TM[:, own], data1=SP[:, own], initial=0.0, op0=ALU.mult, op1=ALU.add),
                 reads=["l_TM", "l_SP"], writes=["l_CL"])
            CLo = CL[:, own].rearrange("p (c t) -> p c t", t=128)
            SPo = SP[:, own].rearrange("p (c t) -> p c t", t=128)
            P.op("vector", lambda e: e.tensor_copy(CE[:], CLo[:, :, 127]), reads=["l_CL"], writes=["l_CE"])
            P.op("vector", lambda e: e.tensor_copy(tot[:, 0:1], CG[:, OWN0 + NOWN - 1:OWN0 + NOWN]), reads=["l_CG"], writes=["l_tot"])
            P.op("vector", lambda e: e.tensor_copy(tot[:, 1:2], CG[:, CTX0 + 255:CTX0 + 256]), reads=["l_CG"], writes=["l_tot"])
            P.op("vector", lambda e: e.tensor_scalar(ntot[:], tot[:], -1.0, None, ALU.mult), reads=["l_tot"], writes=["l_ntot"])
            for c in range(NCH):
                P.op("vector", lambda e, c=c: e.tensor_scalar(CLo[32:64, c, :], CLo[32:64, c, :], CE[32:64, c:c + 1], -1.0, ALU.subtract, ALU.mult),
                     reads=["l_CL", "l_CE"], writes=["l_CL"])
            P.op("vector", lambda e: e.tensor_tensor(out=CL[32:64, own], in0=CL[32:64, own], in1=SP[32:64, own], op=ALU.add),
                 reads=["l_CL", "l_SP"], writes=["l_CL"])
            for r in range(8):
                P.op("tensor", lambda e, r=r: e.matmul(pe[:, r, :], lhsT=SELM[:, r, :], rhs=CE[:], start=True, stop=True),
                     reads=["l_SELM", "l_CE"], writes=["l_pe"], inc=(r == 7))
            P.op("scalar", lambda e: e.activation(out=EBEND[:], in_=pe[:], func=AF.Exp, scale=-1.0), reads=["l_pe"], writes=["l_EBEND"])
            for r in range(8):
                P.op("tensor", lambda e, r=r: e.matmul(pa[:, r, :], lhsT=SELM[:, r, :], rhs=tot[:], start=True, stop=True),
                     reads=["l_SELM", "l_tot"], writes=["l_pa"], inc=(r == 7))
            P.op("scalar", lambda e: e.activation(out=ATOT[:], in_=pa[:, :, 0], func=AF.Exp, scale=-1.0), reads=["l_pa"], writes=["l_ATOT"])

            tcnt = {"n": 0}

            def transpose_out(Q, qn, qty, chunks):
                for g0 in range(0, len(chunks), 8):
                    grp = chunks[g0:g0 + 8]
                    a = tcnt["n"] % 2
                    tcnt["n"] += 1
                    for gi, (ci, col0) in enumerate(grp):
                        P.op("tensor", lambda e, a=a, gi=gi, col0=col0: e.transpose(ptr[a][:, gi, :], Q[0:64, col0:col0 + 128], G.ident_f[0:64, 0:64]),
                             reads=[qn, "ident_f"], writes=["l_ptr%d" % a], inc=(gi == len(grp) - 1))
                    c_first = grp[0][0]
                    ng_ = len(grp)
                    src = ptr[a][:, 0:ng_, :].rearrange("p g (d x) -> p g d x", d=2)[:, :, :, 0:4]
                    dst = TOK[:, c_first:c_first + ng_, qty, :].rearrange("p g (d x) -> p g d x", d=2)
                    P.op("vector", lambda e, src=src, dst=dst: e.tensor_copy(dst, src), reads=["l_ptr%d" % a], writes=["l_TOK"])

            own_chunks = [(c, OWN0 + c * 128) for c in range(NCH)]
            ctx_chunks = [(32 + c, CTX0 + c * 128) for c in range(2)]
            P.op("scalar", lambda e: e.activation(out=OQ[0][:, own], in_=CL[:, own], func=AF.Exp, scale=-1.0), reads=["l_CL"], writes=["l_OQ0"])
            transpose_out(OQ[0], "l_OQ0", 0, own_chunks)
            P.op("scalar", lambda e: e.activation(out=OQ[1][:, own], in_=CL[:, own], func=AF.Exp), reads=["l_CL"], writes=["l_OQ1"])
            transpose_out(OQ[1], "l_OQ1", 4, own_chunks)
            P.op("vector", lambda e: e.tensor_tensor(out=TM[:, own], in0=LI[:, own], in1=CL[:, own], op=ALU.add), reads=["l_LI", "l_CL"], writes=["l_TM"])
            P.op("scalar", lambda e: e.activation(out=OQ[1][:, own], in_=TM[:, own], func=AF.Exp), reads=["l_TM"], writes=["l_OQ1"])
            transpose_out(OQ[1], "l_OQ1", 1, own_chunks)
            TMo2 = TM[:, own].rearrange("p (c t) -> p c t", t=128)
            for c in range(NCH):
                P.op("vector", lambda e, c=c: e.tensor_scalar(TMo2[:, c, :], TMo2[:, c, :], CE[:, c:c + 1], None, ALU.subtract),
                     reads=["l_TM", "l_CE", "l_OQ1"], writes=["l_TM"])
            P.op("scalar", lambda e: e.activation(out=OQ[0][:, own], in_=TM[:, own], func=AF.Exp), reads=["l_TM"], writes=["l_OQ0"])
            transpose_out(OQ[0], "l_OQ0", 2, own_chunks)
            for (sl, ti_) in ((own, 0), (cxs, 1)):
                P.op("vector", lambda e, sl=sl: e.tensor_tensor(out=TM[0:32, sl], in0=LI[0:32, sl], in1=CG[0:32, sl], op=ALU.add),
                     reads=["l_LI", "l_CG"], writes=["l_TM"])
                P.op("vector", lambda e, sl=sl: e.tensor_tensor(out=TM[32:64, sl], in0=LI[32:64, sl], in1=CG[32:64, sl], op=ALU.subtract),
                     reads=["l_LI", "l_CG"], writes=["l_TM"])
                P.op("vector", lambda e, sl=sl: e.tensor_tensor(out=TM[32:64, sl], in0=TM[32:64, sl], in1=SP[32:64, sl], op=ALU.add),
                     reads=["l_TM", "l_SP"], writes=["l_TM"])
                P.op("scalar", lambda e, sl=sl, ti_=ti_: e.activation(out=OQ[1][0:32, sl], in_=TM[0:32, sl], func=AF.Exp, bias=ntot[0:32, ti_:ti_ + 1]),
                     reads=["l_TM", "l_ntot"], writes=["l_OQ1"])
                P.op("scalar", lambda e, sl=sl: e.activation(out=OQ[1][32:64, sl], in_=TM[32:64, sl], func=AF.Exp), reads=["l_TM"], writes=["l_OQ1"])
            transpose_out(OQ[1], "l_OQ1", 3, own_chunks + ctx_chunks)
            P.barrier()
            P.emit()

        KTOK = sb("l_KTOK", [128, 34, 512], BF16)
        VTOK = sb("l_VTOK", [128, 34, 516], BF16)
        TRI = sb("l_TRI", [128, 2, 128], F32)
        SELV = sb("l_SELV", [128, 16], F32)
        PAY = sb("l_PAY", [128, 8, 130], F32)
        GATH = sb("l_GATH", [128, 4, 1040], F32)
        STATE = sb("l_STATE", [128, 8, 129], F32)
        STB = sb("l_STB", [128, 8, 129], BF16)
        KA = [sb("l_KA%d" % b, [128, 128], BF16) for b in range(3)]
        alpha = sb("l_alpha", [128, 1], F32)
        tmpL = sb("l_tmpL", [128, 129], F32)
        QTc = [[sb("l_QTc%d%d" % (d_, b), [128, 4, 128], BF16) for b in range(2)] for d_ in range(2)]
        KTc = [[sb("l_KTc%d%d" % (d_, b), [128, 4, 128], BF16) for b in range(2)] for d_ in range(2)]
        HS = [[sb("l_HS%d%d" % (d_, b), [128, 512], F32) for b in range(2)] for d_ in range(2)]
        PTt = [sb("l_PT%d" % b, [128, 128], BF16) for b in range(2)]
        pS = [ps("l_pS%d" % b, [128, 128]) for b in range(2)]
        pU = [ps("l_pU%d" % b, [128, 132]) for b in range(4)]
        pN = [ps("l_pN%d" % b, [128, 132]) for b in range(2)]
        pL = [pU[0], pU[1]]
        den = [sb("l_den%d" % b, [128, 8], F32) for b in range(2)]
        P.dma("sync", lambda e: e.dma_start(out=KTOK[:, 0:32, :], in_=G.MLK[OWN0:OWN0 + NOWN, :].rearrange("(c p) n -> p c n", p=128)), reads=["dramML"], writes=["l_KTOK"])
        P.dma("sync", lambda e: e.dma_start(out=KTOK[:, 32:34, :], in_=G.MLK[CTX0:CTX0 + 256, :].rearrange("(c p) n -> p c n", p=128)), reads=["dramML"], writes=["l_KTOK"])
        P.dma("sync", lambda e: e.dma_start(out=VTOK[:, 0:32, :], in_=G.MLV[OWN0:OWN0 + NOWN, :].rearrange("(c p) n -> p c n", p=128)), reads=["dramML"], writes=["l_VTOK"])
        P.dma("sync", lambda e: e.dma_start(out=VTOK[:, 32:34, :], in_=G.MLV[CTX0:CTX0 + 256, :].rearrange("(c p) n -> p c n", p=128)), reads=["dramML"], writes=["l_VTOK"])
        P.dma("sync", lambda e: e.dma_start(out=TRI[:], in_=I["tri"][:, :, :]), writes=["l_TRI"])
        P.dma("sync", lambda e: e.dma_start(out=SELV[:], in_=I["selv"][:, :]), writes=["l_SELV"])
        kacnt = {"n": 0}

        def scaled_k(c, h, qty, r):
            a = kacnt["n"] % 3
            kacnt["n"] += 1
            eng = ("vector", "scalar", "scalar")[a]
            src = KTOK[:, c, h * 128:(h + 1) * 128]
            sc = TOK[:, c, qty, r:r + 1]
            if eng == "scalar":
                P.op("scalar", lambda e: e.activation(out=KA[a][:], in_=src, func=AF.Copy, scale=sc), reads=["l_KTOK", "l_TOK"], writes=["l_KA%d" % a])
            else:
                P.op(eng, lambda e: e.tensor_scalar(KA[a][:], src, sc, None, ALU.mult), reads=["l_KTOK", "l_TOK"], writes=["l_KA%d" % a])
            return KA[a], "l_KA%d" % a

        n2 = 0
        for r in range(8):
            h = r % 4
            for (chs, dstname) in ((list(range(32)), "own"), ([32, 33], "ctx")):
                pp, ppn = pL[n2 % 2], "l_pU%d" % (n2 % 2)
                n2 += 1
                for i_, c in enumerate(chs):
                    ka, kan = scaled_k(c, h, 3, r)
                    P.op("tensor", lambda e, pp=pp, ka=ka, c=c, h=h, i_=i_, L=len(chs): e.matmul(
                        pp[:, 0:129], lhsT=ka[:], rhs=VTOK[:, c, h * 129:(h + 1) * 129], start=(i_ == 0), stop=(i_ == L - 1)),
                        reads=[kan, "l_VTOK"], writes=[ppn], inc=True)
                if dstname == "own":
                    P.op("vector", lambda e, pp=pp, r=r: e.tensor_copy(PAY[:, r, 0:129], pp[:, 0:129]), reads=[ppn], writes=["l_PAY"])
                else:
                    P.op("vector", lambda e, pp=pp, r=r: e.tensor_copy(STATE[:, r, :], pp[:, 0:129]), reads=[ppn], writes=["l_STATE"])
        P.op("vector", lambda e: e.tensor_copy(PAY[:, :, 129], ATOT[:]), reads=["l_ATOT", "l_PAY"], writes=["l_PAY"])
        P.dma("sync", lambda e: e.dma_start(out=G.CCI[:, :], in_=PAY[:].rearrange("p r n -> p (r n)")), reads=["l_PAY"], writes=["CCI"])
        for _rep in range(3):
            P.cc(lambda e: e.collective_compute("AllGather", ALU.bypass, replica_groups=[[0, 1, 2, 3], [4, 5, 6, 7]],
                                                ins=[G.CCI.opt()], outs=[G.CCO.opt()]), reads=["CCI"], writes=["CCO"])
        P.dma("sync", lambda e: e.dma_start(out=GATH[:], in_=G.CCO.rearrange("(j p) n -> p j n", p=128)), reads=["CCO"], writes=["l_GATH"])
        for r in range(8):
            d = r // 4
            order = range(4) if d == 0 else range(3, -1, -1)
            for jseg in order:
                so = 0 if d == 0 else 8
                A_j = GATH[:, jseg, r * 130 + 129:r * 130 + 130]
                L_j = GATH[:, jseg, r * 130:r * 130 + 129]
                P.op("vector", lambda e, A_j=A_j, so=so, jseg=jseg: e.tensor_scalar(
                    alpha[:], A_j, SELV[:, so + jseg:so + jseg + 1], SELV[:, so + 4 + jseg:so + 5 + jseg], ALU.mult, ALU.add),
                    reads=["l_GATH", "l_SELV"], writes=["l_alpha"])
                P.op("vector", lambda e, L_j=L_j, so=so, jseg=jseg: e.tensor_scalar(tmpL[:], L_j, SELV[:, so + jseg:so + jseg + 1], None, ALU.mult),
                     reads=["l_GATH", "l_SELV"], writes=["l_tmpL"])
                P.op("vector", lambda e, r=r: e.scalar_tensor_tensor(out=STATE[:, r, :], in0=STATE[:, r, :], scalar=alpha[:], in1=tmpL[:], op0=ALU.mult, op1=ALU.add),
                     reads=["l_STATE", "l_alpha", "l_tmpL"], writes=["l_STATE"])
        P.op("scalar", lambda e: e.activation(out=STB[:], in_=STATE[:], func=AF.Copy), reads=["l_STATE"], writes=["l_STB"])

        def ld4(i):
            if i >= NCH:
                return
            bb = i % 2
            for d_ in range(2):
                c_ = i if d_ == 0 else NCH - 1 - i
                tk0 = OWN0 + c_ * 128
                P.dma("sync", lambda e, bb=bb, d_=d_, tk0=tk0: e.dma_start(out=QTc[d_][bb][:], in_=G.MLQT[:, :, tk0:tk0 + 128]), writes=["l_QTc%d%d" % (d_, bb)])
                P.dma("sync", lambda e, bb=bb, d_=d_, tk0=tk0: e.dma_start(out=KTc[d_][bb][:], in_=G.MLKT[:, :, tk0:tk0 + 128]), writes=["l_KTc%d%d" % (d_, bb)])
        ld4(0)
        for i in range(NCH):
            b = i % 2
            ld4(i + 1)
            cs = (i, NCH - 1 - i)
            for hh_ in range(2):
                items = [(h, d) for h in (2 * hh_, 2 * hh_ + 1) for d in range(2)]
                def front(i2, h, d):
                    b2 = i2 % 2
                    c = (i2, NCH - 1 - i2)[d]
                    r = d * 4 + h
                    qn, kn = "l_QTc%d%d" % (d, b2), "l_KTc%d%d" % (d, b2)
                    P.op("tensor", lambda e, h=h, d=d, b2=b2: e.matmul(pS[d][:], lhsT=KTc[d][b2][:, h, :], rhs=QTc[d][b2][:, h, :], start=True, stop=True),
                         reads=[kn, qn], writes=["l_pS%d" % d], inc=True)
                    P.op("vector", lambda e, c=c, r=r, d=d: e.scalar_tensor_tensor(
                        out=PTt[d][:], in0=pS[d][:], scalar=TOK[:, c, 1, r:r + 1], in1=TRI[:, d, :], op0=ALU.mult, op1=ALU.mult),
                        reads=["l_pS%d" % d, "l_TOK", "l_TRI"], writes=["l_PT%d" % d])

                def back(h, d):
                    c = cs[d]
                    r = d * 4 + h
                    u = (h % 2) * 2 + d
                    qn = "l_QTc%d%d" % (d, b)
                    P.op("tensor", lambda e, c=c, h=h, d=d, u=u: e.matmul(pU[u][:, 0:129], lhsT=PTt[d][:], rhs=VTOK[:, c, h * 129:(h + 1) * 129], start=True, stop=False),
                         reads=["l_PT%d" % d, "l_VTOK"], writes=["l_pU%d" % u], inc=False)
                    P.op("tensor", lambda e, h=h, d=d, r=r, u=u, b=b: e.matmul(pU[u][:, 0:129], lhsT=QTc[d][b][:, h, :], rhs=STB[:, r, :], start=False, stop=True),
                         reads=[qn, "l_STB%d" % r, "l_STB"], writes=["l_pU%d" % u], inc=True)
                if i == 0 and hh_ == 0:
                    front(i, *items[0])
                for k_ in range(4):
                    if k_ + 1 < 4:
                        front(i, *items[k_ + 1])
                    elif hh_ == 0:
                        front(i, 2, 0)
                    elif i + 1 < NCH:
                        front(i + 1, 0, 0)
                    back(*items[k_])
                for (h, d) in items:
                    c = cs[d]
                    r = d * 4 + h
                    a = r % 2
                    ka, kan = scaled_k(c, h, 2, r)
                    P.op("tensor", lambda e, a=a, ka=ka, c=c, h=h: e.matmul(pN[a][:, 0:129], lhsT=ka[:], rhs=VTOK[:, c, h * 129:(h + 1) * 129], start=True, stop=True),
                         reads=[kan, "l_VTOK"], writes=["l_pN%d" % a], inc=True)
                    P.op("vector", lambda e, a=a, r=r, c=c: e.scalar_tensor_tensor(
                        out=STATE[:, r, :], in0=STATE[:, r, :], scalar=EBEND[:, r, c:c + 1], in1=pN[a][:, 0:129], op0=ALU.mult, op1=ALU.add),
                        reads=["l_STATE%d" % r, "l_EBEND", "l_pN%d" % a, "l_STATE"], writes=["l_STATE%d" % r])
                    P.op("scalar", lambda e, r=r: e.activation(out=STB[:, r, :], in_=STATE[:, r, :], func=AF.Copy),
                         reads=["l_STATE%d" % r], writes=["l_STB%d" % r])
                for (h, d) in items:
                    u = (h % 2) * 2 + d
                    dn, dnn = den[d], "l_den%d" % d
                    P.op("scalar", lambda e, dn=dn, u=u, h=h: e.activation(out=dn[:, h:h + 1], in_=pU[u][:, 128:129], func=AF.Abs),
                         reads=["l_pU%d" % u], writes=[dnn])
                for d in range(2):
                    c = cs[d]
                    dn, dnn = den[d], "l_den%d" % d
                    h0 = 2 * hh_
                    REB2 = TOK[:, c, 4, d * 4 + h0:d * 4 + h0 + 2]
                    P.op("vector", lambda e, dn=dn, REB2=REB2, h0=h0: e.tensor_tensor(out=dn[:, h0:h0 + 2], in0=dn[:, h0:h0 + 2], in1=REB2, op=ALU.max),
                         reads=[dnn, "l_TOK"], writes=[dnn])
                    P.op("vector", lambda e, dn=dn, h0=h0: e.reciprocal(dn[:, h0:h0 + 2], dn[:, h0:h0 + 2]), reads=[dnn], writes=[dnn])
                for (h, d) in items:
                    u = (h % 2) * 2 + d
                    dn, dnn = den[d], "l_den%d" % d
                    hsn = "l_HS%d%d" % (d, b)
                    if d == 0:
                        P.op("scalar", lambda e, b=b, h=h, dn=dn, u=u: e.activation(out=HS[0][b][:, h * 128:(h + 1) * 128], in_=pU[u][:, 0:128], func=AF.Copy, scale=dn[:, h:h + 1]),
                             reads=["l_pU%d" % u, dnn], writes=[hsn])
                    else:
                        P.op("vector", lambda e, b=b, h=h, dn=dn, u=u: e.tensor_scalar(HS[1][b][:, h * 128:(h + 1) * 128], pU[u][:, 0:128], dn[:, h:h + 1], None, ALU.mult),
                             reads=["l_pU%d" % u, dnn], writes=[hsn])
            P.dma("sync", lambda e, b=b, c=cs[0]: e.dma_start(out=G.HF[c * 128:(c + 1) * 128, :], in_=HS[0][b][:]), reads=["l_HS0%d" % b], writes=[_u()])
            P.dma("sync", lambda e, b=b, c=cs[1]: e.dma_start(out=G.HB[c * 128:(c + 1) * 128, :], in_=HS[1][b][:]), reads=["l_HS1%d" % b], writes=[_u()])
        P.barrier()
        P.emit()

    with contextlib.ExitStack() as st:
        sb, ps = _alloc(G, st)
        NB = 3
        hf = [sb("r_hf%d" % b, [128, 512], F32) for b in range(NB)]
        hb = [sb("r_hb%d" % b, [128, 512], F32) for b in range(NB)]
        g2 = [sb("r_g2%d" % b, [128, 512], F32) for b in range(NB)]
        junk = sb("r_junk", [128, 128], F32)
        ssq = [sb("r_ssq%d" % b, [128, 4], F32) for b in range(2)]
        Yb = [sb("r_Y%d" % b, [128, 512], BF16) for b in range(2)]
        ytb = [sb("r_yt%d" % b, [128, 4, 128], BF16) for b in range(2)]
        ptr2 = [ps("r_ptr%d" % b, [128, 4, 128], BF16) for b in range(2)]

        def ld5(c):
            if c >= NCH:
                return
            b3 = c % NB
            tk0 = OWN0 + c * 128
            P.dma("sync", lambda e, b3=b3, c=c: e.dma_start(out=hf[b3][:], in_=G.HF[c * 128:(c + 1) * 128, :]), writes=["r_hf%d" % b3])
            P.dma("sync", lambda e, b3=b3, c=c: e.dma_start(out=hb[b3][:], in_=G.HB[c * 128:(c + 1) * 128, :]), writes=["r_hb%d" % b3])
            P.dma("sync", lambda e, b3=b3, tk0=tk0: e.dma_start(out=g2[b3][:], in_=G.MLG2[tk0:tk0 + 128, :]), writes=["r_g2%d" % b3])
        ld5(0)
        ld5(1)
        for c in range(NCH):
            ld5(c + 2)
            b3, b = c % NB, c % 2
            P.op("vector", lambda e, b3=b3: e.tensor_tensor(out=hf[b3][:], in0=hf[b3][:], in1=hb[b3][:], op=ALU.add),
                 reads=["r_hf%d" % b3, "r_hb%d" % b3], writes=["r_hf%d" % b3])
            for h in range(4):
                P.op("scalar", lambda e, b3=b3, b=b, h=h: e.activation(out=junk[:], in_=hf[b3][:, h * 128:(h + 1) * 128], func=AF.Square, accum_out=ssq[b][:, h:h + 1]),
                     reads=["r_hf%d" % b3], writes=["r_junk", "r_ssq%d" % b])
            P.op("scalar", lambda e, b=b: e.activation(out=ssq[b][:], in_=ssq[b][:], func=AF.Sqrt, bias=G.c_eps[:], scale=1.0 / 128.0),
                 reads=["r_ssq%d" % b, "c_eps"], writes=["r_ssq%d" % b])
            P.op("vector", lambda e, b=b: e.reciprocal(ssq[b][:], ssq[b][:]), reads=["r_ssq%d" % b], writes=["r_ssq%d" % b])
            for h in range(4):
                P.op("vector", lambda e, b3=b3, b=b, h=h: e.scalar_tensor_tensor(
                    out=Yb[b][:, h * 128:(h + 1) * 128], in0=hf[b3][:, h * 128:(h + 1) * 128], scalar=ssq[b][:, h:h + 1],
                    in1=g2[b3][:, h * 128:(h + 1) * 128], op0=ALU.mult, op1=ALU.mult),
                    reads=["r_hf%d" % b3, "r_ssq%d" % b, "r_g2%d" % b3], writes=["r_Y%d" % b])
            for h in range(4):
                P.op("tensor", lambda e, b=b, h=h: e.transpose(ptr2[b][:, h, :], Yb[b][:, h * 128:(h + 1) * 128], G.ident_b[:]),
                     reads=["r_Y%d" % b, "ident_b"], writes=["r_ptr%d" % b], inc=(h == 3))
            P.op("scalar", lambda e, b=b: e.activation(out=ytb[b][:], in_=ptr2[b][:], func=AF.Copy), reads=["r_ptr%d" % b], writes=["r_yt%d" % b])
            P.dma("sync", lambda e, b=b, c=c: e.dma_start(out=G.YT[:, 4:8, c * 128:(c + 1) * 128], in_=ytb[b][:]), reads=["r_yt%d" % b], writes=[_u()])
        P.barrier()
        P.emit()


def stage_outproj(G):
    P, I, nc = G.P, G.I, G.nc
    l, j, m = 0, 1, 0
    with contextlib.ExitStack() as st:
        sb, ps = _alloc(G, st)
        wo = sb("o_wo", [128, 8, 1024], BF16)
        xt = [sb("o_xt%d" % b, [128, 8, TW], F32) for b in range(2)]
        yt = [sb("o_yt%d" % b, [128, 8, TW], BF16) for b in range(2)]
        po = [ps("o_po%d" % b, [128, TW]) for b in range(2)]
        load_w_cast(G, wo, "o_wo", I["mix_w_out"], 8, 1024)
        def ld(n):
            if n >= 16:
                return
            ti_, b_ = n + 1, n % 2
            P.dma("sync", lambda e, b_=b_, ti_=ti_: e.dma_start(out=xt[b_][:], in_=G.XT[:, :, ti_ * TW:(ti_ + 1) * TW]), reads=["XT%d" % ti_], writes=["o_xt%d" % b_])
            P.dma("sync", lambda e, b_=b_, n=n: e.dma_start(out=yt[b_][:], in_=G.YT[:, :, n * TW:(n + 1) * TW]), reads=["dramYT"], writes=["o_yt%d" % b_])
        ld(0)
        ld(1)
        for n in range(16):
            ti = n + 1
            t0 = ti * TW
            b = n % 2
            for f in range(8):
                q = f % 2
                for k in range(8):
                    P.op("tensor", lambda e, q=q, f=f, k=k, b=b: e.matmul(po[q][:], lhsT=wo[:, k, f * 128:(f + 1) * 128], rhs=yt[b][:, k, :], start=(k == 0), stop=(k == 7)),
                         reads=["o_wo", "o_yt%d" % b], writes=["o_po%d" % q], inc=(k == 7))
                gsc = G.GT[:, l, j, m, f:f + 1]
                P.op("vector", lambda e, q=q, f=f, b=b, gsc=gsc: e.scalar_tensor_tensor(
                    out=xt[b][:, f, :], in0=po[q][:], scalar=gsc, in1=xt[b][:, f, :], op0=ALU.mult, op1=ALU.add),
                    reads=["o_po%d" % q, "modc", "o_xt%d" % b], writes=["o_xt%d" % b])
            P.dma("sync", lambda e, b=b, t0=t0: e.dma_start(out=G.XT[:, :, t0:t0 + TW], in_=xt[b][:]), reads=["o_xt%d" % b], writes=["XT%d" % ti])
            ld(n + 2)
        P.barrier()
        P.emit()


def stage_sg(G):
    P, I, nc = G.P, G.I, G.nc
    l, j, m = 1, 1, 0
    with contextlib.ExitStack() as st:
        sb, ps = _alloc(G, st)
        wu = sb("g_wu", [128, 8, 2048], BF16)
        wv = sb("g_wv", [128, 8, 2048], BF16)
        wo = sb("g_wo", [128, 16, 1024], BF16)
        wsT = sb("g_wsT", [128, 8, 128], BF16)
        bs = sb("g_bs", [1, 1024], BF16)
        ones1 = sb("g_ones1", [1, 256], BF16)
        lng = sb("g_lng", [128, 2048], F32)
        lnb = sb("g_lnb", [128, 2048], F32)
        xt = [sb("g_xt%d" % b, [128, 8, TW], F32) for b in range(2)]
        hh = [sb("g_h%d" % b, [128, 8, TW], BF16) for b in range(2)]
        NS = NormScratch(G, sb, ps, "g_")
        NS.sq8 = [sb("g_sq8_%d" % k, [128, TW], BF16) for k in range(8)]
        uT = sb("g_uT", [128, 16, TW], BF16)
        vraw = [sb("g_vraw%d" % b, [128, 2048], F32) for b in range(2)]
        vn = [sb("g_vn%d" % b, [128, 2048], BF16) for b in range(2)]
        gated = sb("g_gated", [128, 16, TW], BF16)
        stats = [sb("g_stats%d" % b, [128, 4, 6], F32) for b in range(2)]
        mv = [sb("g_mv%d" % b, [128, 4], F32) for b in range(2)]
        pu = [ps("g_pu%d" % b, [128, TW]) for b in range(2)]
        pm = [ps("g_pm%d" % b, [128, 4, 128]) for b in range(2)]
        po1 = ps("g_po", [128, TW])
        po = [po1, po1]
        pv = [ps("g_pv%d" % b, [128, 512]) for b in range(2)]
        load_w_cast(G, wu, "g_wu", I["sg_w_in"][:, 0:2048], 8, 2048)
        load_w_cast(G, wv, "g_wv", I["sg_w_in"][:, 2048:4096], 8, 2048)
        load_w_cast(G, wo, "g_wo", I["sg_w_out"], 16, 1024)
        P.dma("gpsimd", lambda e: e.dma_start(out=wsT[:], in_=I["sg_w_sT"][:, :, :]), writes=["g_wsT"])
        P.dma("gpsimd", lambda e: e.dma_start(out=bs[:], in_=I["sg_b_s"][:, :]), writes=["g_bs"])
        P.op("vector", lambda e: e.memset(ones1[:], 1.0), writes=["g_ones1"])
        P.dma("sync", lambda e: e.dma_start(out=lng[:], in_=I["sg_lng_bc"][:, :]), writes=["g_lng"])
        P.dma("sync", lambda e: e.dma_start(out=lnb[:], in_=I["sg_lnb_bc"][:, :]), writes=["g_lnb"])
        tiles = list(range(1, 17))

        def ld(n):
            ti_ = tiles[n]
            xb = xt[n % 2]
            P.dma("sync", lambda e, xb=xb, ti_=ti_: e.dma_start(out=xb[:], in_=G.XT[:, :, ti_ * TW:(ti_ + 1) * TW]),
                  reads=["XT%d" % ti_], writes=["g_xt%d" % (n % 2)])
        def do_norm(n, phase=None):
            bb = n % 2
            norm_mod(G, NS, xt[bb], "g_xt%d" % bb, G.GM[:, l, j, m, :], G.SH[:, l, j, m, :], hh[bb], "g_h%d" % bb, phase=phase)
        ld(0)
        ld(1)
        do_norm(0)
        vcnt = 0
        for n, ti in enumerate(tiles):
            t0 = ti * TW
            b = n % 2
            x, xn = xt[b], "g_xt%d" % b
            h, hn = hh[b], "g_h%d" % b
            NSB = TW // 128
            for s_ in range(NSB):
                VR = vraw[s_]
                for blk in range(4):
                    q = blk % 2
                    for k in range(8):
                        P.op("tensor", lambda e, q=q, blk=blk, k=k, h=h, s_=s_: e.matmul(
                            pv[q][:], lhsT=h[:, k, s_ * 128:(s_ + 1) * 128], rhs=wv[:, k, blk * 512:(blk + 1) * 512], start=(k == 0), stop=(k == 7)),
                            reads=["g_wv", hn], writes=["g_pv%d" % q], inc=(k == 7))
                    P.op("scalar", lambda e, q=q, blk=blk, VR=VR: e.activation(out=VR[:, blk * 512:(blk + 1) * 512], in_=pv[q][:], func=AF.Gelu_apprx_tanh),
                         reads=["g_pv%d" % q], writes=["g_vraw%d" % s_])
                    P.op("vector", lambda e, blk=blk, VR=VR, s_=s_: e.bn_stats(stats[s_][:, blk, :], VR[:, blk * 512:(blk + 1) * 512]),
                         reads=["g_vraw%d" % s_], writes=["g_stats%d" % s_])
                P.op("vector", lambda e, s_=s_: e.bn_aggr(mv[s_][:, 0:2], stats[s_][:].rearrange("p a b -> p (a b)")), reads=["g_stats%d" % s_], writes=["g_mv%d" % s_])
                P.op("scalar", lambda e, s_=s_: e.activation(out=mv[s_][:, 2:3], in_=mv[s_][:, 1:2], func=AF.Sqrt, bias=G.c_eps[:], scale=1.0),
                     reads=["g_mv%d" % s_, "c_eps"], writes=["g_mv%d" % s_])
                P.op("vector", lambda e, s_=s_: e.reciprocal(mv[s_][:, 2:3], mv[s_][:, 2:3]), reads=["g_mv%d" % s_], writes=["g_mv%d" % s_])
                P.op("vector", lambda e, s_=s_: e.tensor_scalar(mv[s_][:, 3:4], mv[s_][:, 0:1], mv[s_][:, 2:3], -1.0, ALU.mult, ALU.mult),
                     reads=["g_mv%d" % s_], writes=["g_mv%d" % s_])
            for fc in range(16):
                q = fc % 2
                if fc == 6:
                    vns = []
                    for s_ in range(NSB):
                        VR = vraw[s_]
                        VN, vnn = vn[vcnt % 2], "g_vn%d" % (vcnt % 2)
                        vcnt += 1
                        vns.append((VN, vnn))
                        P.op("scalar", lambda e, VR=VR, s_=s_: e.activation(out=VR[:], in_=VR[:], func=AF.Identity, bias=mv[s_][:, 3:4], scale=mv[s_][:, 2:3]),
                             reads=["g_vraw%d" % s_, "g_mv%d" % s_], writes=["g_vraw%d" % s_])
                        P.op("vector", lambda e, VR=VR: e.tensor_tensor(out=VR[:], in0=VR[:], in1=lng[:], op=ALU.mult),
                             reads=["g_vraw%d" % s_, "g_lng"], writes=["g_vraw%d" % s_])
                        P.op("vector", lambda e, VR=VR, VN=VN: e.tensor_tensor(out=VN[:], in0=VR[:], in1=lnb[:], op=ALU.add),
                             reads=["g_vraw%d" % s_, "g_lnb"], writes=[vnn])
                for k in range(8):
                    P.op("tensor", lambda e, q=q, fc=fc, k=k, h=h: e.matmul(pu[q][:], lhsT=wu[:, k, fc * 128:(fc + 1) * 128], rhs=h[:, k, :], start=(k == 0), stop=(k == 7)),
                         reads=["g_wu", hn], writes=["g_pu%d" % q], inc=(k == 7))
                P.op("scalar", lambda e, q=q, fc=fc: e.activation(out=uT[:, fc, :], in_=pu[q][:], func=AF.Gelu_apprx_tanh), reads=["g_pu%d" % q], writes=["g_uT%d" % fc])
            for s_ in range(NSB):
                VN, vnn = vns[s_]
                for g4 in range(4):
                    q = g4 % 2
                    for f4 in range(4):
                        fc = g4 * 4 + f4
                        g_ = fc // 2
                        P.op("tensor", lambda e, q=q, fc=fc, f4=f4, g_=g_, VN=VN: e.matmul(
                            pm[q][:, f4, :], lhsT=VN[:, fc * 128:(fc + 1) * 128], rhs=wsT[:, g_, :], start=True, stop=False),
                            reads=[vnn, "g_wsT"], writes=["g_pm%d" % q], inc=False)
                        P.op("tensor", lambda e, q=q, f4=f4, g_=g_: e.matmul(
                            pm[q][:, f4, :], lhsT=ones1[0:1, 0:128], rhs=bs[0:1, g_ * 128:(g_ + 1) * 128], start=False, stop=True),
                            reads=["g_ones1", "g_bs"], writes=["g_pm%d" % q], inc=(f4 == 3))
                    P.op("vector", lambda e, q=q, g4=g4, s_=s_: e.tensor_tensor(
                        out=gated[:, g4 * 4:(g4 + 1) * 4, s_ * 128:(s_ + 1) * 128], in0=uT[:, g4 * 4:(g4 + 1) * 4, s_ * 128:(s_ + 1) * 128], in1=pm[q][:], op=ALU.mult),
                        reads=["g_uT%d" % fc_ for fc_ in range(g4 * 4, g4 * 4 + 4)] + ["g_pm%d" % q],
                        writes=["g_gated%d" % fc_ for fc_ in range(g4 * 4, g4 * 4 + 4)])
            if n + 1 < len(tiles):
                do_norm(n + 1, "A")
            for f in range(8):
                q = f % 2
                if f == 3 and n + 1 < len(tiles):
                    do_norm(n + 1, "B")
                for fc in range(16):
                    P.op("tensor", lambda e, q=q, f=f, fc=fc: e.matmul(po[q][:], lhsT=wo[:, fc, f * 128:(f + 1) * 128], rhs=gated[:, fc, :], start=(fc == 0), stop=(fc == 15)),
                         reads=["g_wo", "g_gated%d" % fc], writes=["g_po"], inc=(fc == 15))
                gsc = G.GT[:, l, j, m, f:f + 1]
                P.op("vector", lambda e, q=q, f=f, x=x, gsc=gsc: e.scalar_tensor_tensor(
                    out=x[:, f, :], in0=po[q][:], scalar=gsc, in1=x[:, f, :], op0=ALU.mult, op1=ALU.add),
                    reads=["g_po", "modc", xn], writes=[xn])
            P.dma("sync", lambda e, x=x, t0=t0: e.dma_start(out=G.XT[:, :, t0:t0 + TW], in_=x[:]), reads=[xn], writes=["XT%d" % ti])
            if n + 2 < len(tiles):
                ld(n + 2)
        P.barrier()
        P.emit()


_NC_CACHE = {}


def kernel(**inputs):
    maps = _host_prep(inputs)
    if "nc" not in _NC_CACHE:
        _NC_CACHE["nc"] = build()
    nc = _NC_CACHE["nc"]
    res = run_bass_kernel_spmd(nc, maps, core_ids=list(range(8)))
    out = np.zeros((2, 16384, 1024), np.float32)
    for core in range(8):
        b, s = core // 4, core % 4
        out[b, s * 4096:(s + 1) * 4096, :] = res.results[core]["out"]
    return out
```

```python
import contextlib
import numpy as np
import concourse.bass as bass
import concourse.mybir as mybir
from concourse.bass_utils import run_bass_kernel_spmd

F32 = mybir.dt.float32
BF16 = mybir.dt.bfloat16
AF = mybir.ActivationFunctionType
ALU = mybir.AluOpType
AX = mybir.AxisListType

D = 1024
DFF = 2816
NE = 4608
NU = 4864
OWN0 = 256
NOWN = 4096
CTX0 = 4608
TW = 256
EPS = 1e-6
NEG = -30000.0

ENGS = ("tensor", "vector", "scalar", "gpsimd", "sync")
NDMASEM = 32
NHW = 24


class Buf:
    __slots__ = ("name", "writers", "readers")

    def __init__(self, name):
        self.name = name
        self.writers = []
        self.readers = []


class Prog:
    def __init__(self, nc, st):
        self.nc = nc
        self.q = {e: [] for e in ENGS}
        self.cnt = {e: 0 for e in ENGS}
        self.seen = {e: {} for e in ENGS}
        self.dcnt = [0] * NDMASEM
        self.dnext = 0
        self.dnext_sw = 0
        self.bufs = {}
        self.esem = {e: st.enter_context(nc.semaphore("s_" + e)) for e in ENGS}
        self.dsem = [st.enter_context(nc.semaphore("d%d" % i)) for i in range(NDMASEM)]
        self.lastinc = {e: True for e in ENGS}

    def buf(self, name):
        b = self.bufs.get(name)
        if b is None:
            b = self.bufs[name] = Buf(name)
        return b

    def _bl(self, lst):
        return [self.buf(b) if isinstance(b, str) else b for b in lst]

    def _deps(self, reads, writes):
        deps = {}
        for b in reads:
            for k, v in b.writers:
                if deps.get(k, 0) < v:
                    deps[k] = v
        for b in writes:
            for k, v in b.writers:
                if deps.get(k, 0) < v:
                    deps[k] = v
            for k, v in b.readers:
                if deps.get(k, 0) < v:
                    deps[k] = v
        return deps

    def _waits(self, eng, deps):
        seen = self.seen[eng]
        waits = []
        for k, v in deps.items():
            if k == "tensor" and eng == "tensor":
                continue
            if seen.get(k, 0) >= v:
                continue
            seen[k] = v
            waits.append((k, v))
        return waits

    def _record(self, ev, reads, writes):
        k = ev[0]
        for b in writes:
            b.writers = [ev]
            b.readers = []
        for b in reads:
            if b in writes:
                continue
            b.readers = [e for e in b.readers if e[0] != k] + [ev]

    def op(self, eng, fn, reads=(), writes=(), inc=True):
        reads = self._bl(reads)
        writes = self._bl(writes)
        waits = self._waits(eng, self._deps(reads, writes))
        if inc:
            self.cnt[eng] += 1
            ev = (eng, self.cnt[eng])
        else:
            ev = (eng, self.cnt[eng] + 1)
        self.lastinc[eng] = inc
        self.q[eng].append(("op", fn, waits, inc))
        self._record(ev, reads, writes)

    def dma(self, eng, fn, reads=(), writes=()):
        reads = self._bl(reads)
        writes = self._bl(writes)
        deps = self._deps(reads, writes)
        if eng == "gpsimd":
            i = NHW + self.dnext_sw
            self.dnext_sw = (self.dnext_sw + 1) % (NDMASEM - NHW)
        else:
            i = self.dnext
            self.dnext = (self.dnext + 1) % NHW
        key = ("d", i)
        if self.dcnt[i] > 0 and deps.get(key, 0) < self.dcnt[i]:
            deps[key] = self.dcnt[i]
        waits = self._waits(eng, deps)
        self.dcnt[i] += 16
        ev = (key, self.dcnt[i])
        self.q[eng].append(("dma", fn, waits, i))
        self._record(ev, reads, writes)

    def cc(self, fn, reads=(), writes=()):
        self.op("gpsimd", fn, reads=reads, writes=writes, inc=True)

    def barrier(self):
        for e in ENGS:
            assert self.lastinc[e], e
        deps = {e: self.cnt[e] for e in ENGS if self.cnt[e] > 0}
        for i in range(NDMASEM):
            if self.dcnt[i] > 0:
                deps[("d", i)] = self.dcnt[i]
        for e in ENGS:
            d = {k: v for k, v in deps.items() if k != e}
            waits = self._waits(e, d)
            self.q[e].append(("wait", None, waits, None))

    def emit(self):
        nc = self.nc
        esem, dsem = self.esem, self.dsem

        def semof(k):
            return esem[k] if isinstance(k, str) else dsem[k[1]]

        def run(engname):
            items = self.q[engname]

            def body(e):
                for kind, fn, waits, x in items:
                    for k, v in waits:
                        e.wait_ge(semof(k), v)
                    if kind == "op":
                        ins = fn(e)
                        if x:
                            ins.then_inc(esem[engname], 1)
                    elif kind == "dma":
                        fn(e).then_inc(dsem[x], 16)
            return body

        with nc.Block() as block:
            block.tensor(run("tensor"))
            block.vector(run("vector"))
            block.scalar(run("scalar"))
            block.gpsimd(run("gpsimd"))
            block.sync(run("sync"))
        self.q = {e: [] for e in ENGS}


def _rowmap(s):
    rm = np.zeros(72, np.int64)
    rm[4:68] = s * 64 + np.arange(64)
    rm[0:4] = (s * 64 - 4 + np.arange(4)) if s > 0 else np.array([5, 6, 7, 8])
    rm[68:72] = (s * 64 + 64 + np.arange(4)) if s < 3 else np.array([248, 249, 250, 251])
    return rm


def _na_bias_tables(rpb, s):
    rm = _rowmap(s)
    out = np.full((5, 8, 128, 576), NEG, np.float32)
    reps = [0, 1, 2, 30, 31]
    cols = np.arange(64)
    c0 = np.clip(cols - 8, 0, 64 - 16)
    for ci, p in enumerate(reps):
        for r in range(2):
            i = s * 64 + 2 * p + r
            r0 = min(max(i - 4, 0), 256 - 8)
            seen_rows = set()
            for j in range(9):
                krow = int(rm[2 * p + j])
                if krow < r0 or krow >= r0 + 8 or krow in seen_rows:
                    continue
                seen_rows.add(krow)
                rr = krow - i + 7
                for qc in range(64):
                    kc = np.arange(c0[qc], c0[qc] + 16)
                    out[ci][:, r * 64 + qc, j * 64 + kc] = rpb[:, rr, kc - qc + 15]
    return out


def _rope_tables(s):
    rm = _rowmap(s)
    row = np.repeat(rm, 64).astype(np.float32)
    col = np.tile(np.arange(64), 72).astype(np.float32)
    inv = (10000.0 ** (-np.arange(32, dtype=np.float32) / 32)).astype(np.float32)
    ang = np.concatenate([row[:, None] * inv, col[:, None] * inv], axis=-1).astype(np.float32)
    cos = np.cos(ang).astype(np.float32)
    sin = np.sin(ang).astype(np.float32)
    cos2 = np.repeat(cos, 2, axis=1)
    sin2 = np.repeat(sin, 2, axis=1)
    sin2[:, 0::2] *= -1.0
    cos_u = np.ones((NU, 128), np.float32)
    sin_u = np.zeros((NU, 128), np.float32)
    cos_u[:NE] = cos2
    sin_u[:NE] = sin2
    return np.tile(cos_u, (1, 4)), np.tile(sin_u, (1, 4))


def _host_prep(inp):
    x = np.asarray(inp["x"], np.float32)
    shared = {}
    shared["w_mod"] = np.ascontiguousarray(inp["w_mod"], np.float32)
    shared["b_modT"] = np.ascontiguousarray(np.asarray(inp["b_mod"], np.float32).reshape(2, 72, 128).transpose(2, 0, 1))
    shared["norm_gT"] = np.ascontiguousarray(np.asarray(inp["norm_g"], np.float32).reshape(2, 3, 8, 128).transpose(3, 0, 1, 2))
    shared["ffn_w_in"] = np.ascontiguousarray(inp["ffn_w_in"], np.float32)
    shared["ffn_w_out"] = np.ascontiguousarray(inp["ffn_w_out"], np.float32)
    mw = np.asarray(inp["mix_w_in"], np.float32)[0]
    shared["mix_w_in"] = np.ascontiguousarray(mw[:, :3584])
    wg = np.zeros((1024, 128), np.float32)
    gb = np.zeros((128, 1), np.float32)
    gate_b = np.asarray(inp["ml_gate_b"], np.float32)[0]
    for h in range(4):
        for d in range(2):
            for q in range(2):
                wg[:, q * 64 + d * 32 + h] = mw[:, 3584 + h * 4 + d * 2 + q]
                gb[q * 64 + d * 32 + h, 0] = gate_b[h, d, q]
    shared["w_gate"] = wg
    shared["gate_b"] = gb
    shared["headg_bc"] = np.ascontiguousarray(np.broadcast_to(np.asarray(inp["ml_head_g"], np.float32)[0].reshape(1, 512), (128, 512)))
    shared["mix_w_out"] = np.ascontiguousarray(np.asarray(inp["mix_w_out"], np.float32)[0])
    shared["sg_w_in"] = np.ascontiguousarray(np.asarray(inp["sg_w_in"], np.float32)[0])
    shared["sg_w_out"] = np.ascontiguousarray(np.asarray(inp["sg_w_out"], np.float32)[0])
    shared["sg_lng_bc"] = np.ascontiguousarray(np.broadcast_to(np.asarray(inp["sg_ln_g"], np.float32)[0].reshape(1, 2048), (128, 2048)))
    shared["sg_lnb_bc"] = np.ascontiguousarray(np.broadcast_to(np.asarray(inp["sg_ln_b"], np.float32)[0].reshape(1, 2048), (128, 2048)))
    shared["sg_w_sT"] = np.ascontiguousarray(np.asarray(inp["sg_w_s"], np.float32)[0].transpose(2, 0, 1))
    shared["sg_b_s"] = np.ascontiguousarray(np.asarray(inp["sg_b_s"], np.float32)[0].reshape(1, 1024))
    shared["final_g_bc"] = np.ascontiguousarray(np.broadcast_to(np.asarray(inp["final_g"], np.float32).reshape(1, 1024), (128, 1024)))
    shared["ident"] = np.eye(128, dtype=np.float32)
    tri = np.zeros((128, 2, 128), np.float32)
    ss, tt = np.meshgrid(np.arange(128), np.arange(128), indexing="ij")
    tri[:, 0, :] = (tt >= ss)
    tri[:, 1, :] = (tt <= ss)
    shared["tri"] = tri
    sel = np.zeros((64, 8, 128), np.float32)
    for d in range(2):
        for h in range(4):
            sel[d * 32 + h, d * 4 + h, :] = 1.0
    shared["selm"] = sel
    rpb = np.asarray(inp["na_rpb"], np.float32)[0]
    c = np.asarray(inp["c"], np.float32)
    cctx = np.asarray(inp["c_ctx"], np.float32)
    ctx = np.asarray(inp["ctx"], np.float32)
    maps = []
    for core in range(8):
        b, s = core // 4, core % 4
        rm = _rowmap(s)
        tok = (rm[:, None] * 64 + np.arange(64)[None, :]).reshape(-1)
        xin = np.concatenate([x[b][tok], ctx[b]], axis=0)
        cT = np.stack([c[b].reshape(8, 128).T, cctx.reshape(8, 128).T], axis=-1)
        cos4, sin4 = _rope_tables(s)
        selv = np.zeros((128, 16), np.float32)
        for j in range(4):
            selv[:, j] = 1.0 if j < s else 0.0
            selv[:, 4 + j] = 1.0 - selv[:, j]
            selv[:, 8 + j] = 1.0 if j > s else 0.0
            selv[:, 12 + j] = 1.0 - selv[:, 8 + j]
        m = dict(shared)
        m["xin"] = np.ascontiguousarray(xin)
        m["cT"] = np.ascontiguousarray(cT.astype(np.float32))
        m["ropecos"] = cos4
        m["ropesin"] = sin4
        nb = _na_bias_tables(rpb, s)
        nbp = np.full((5, 8, 128, 640), NEG, np.float32)
        nbp[..., :576] = nb
        m["nabiasT"] = np.ascontiguousarray(nbp.reshape(5, 8, 128, 5, 128).transpose(0, 1, 4, 3, 2))
        m["selv"] = selv
        maps.append(m)
    return maps


INPUT_SHAPES = {
    "xin": [NU, 1024], "cT": [128, 8, 2], "w_mod": [2, 1024, 9216], "b_modT": [128, 2, 72], "norm_gT": [128, 2, 3, 8],
    "ffn_w_in": [2, 2, 1024, 5632], "ffn_w_out": [2, 2, 2816, 1024], "mix_w_in": [1024, 3584], "w_gate": [1024, 128],
    "gate_b": [128, 1], "headg_bc": [128, 512], "mix_w_out": [1024, 1024], "sg_w_in": [1024, 4096], "sg_w_out": [2048, 1024],
    "sg_lng_bc": [128, 2048], "sg_lnb_bc": [128, 2048], "sg_w_sT": [128, 8, 128], "sg_b_s": [1, 1024], "final_g_bc": [128, 1024],
    "ident": [128, 128], "tri": [128, 2, 128], "selm": [64, 8, 128], "ropecos": [NU, 512], "ropesin": [NU, 512],
    "nabiasT": [5, 8, 128, 5, 128], "selv": [128, 16],
}


class Ctx:
    pass


_UC = [0]


def _u():
    _UC[0] += 1
    return "u%d" % _UC[0]


def build(stages="all", dbg=()):
    nc = bass.Bass("TRN2", target_bir_lowering=False)
    I = {k: nc.dram_tensor(k, shp, F32, kind="ExternalInput").ap() for k, shp in INPUT_SHAPES.items()}
    OUT = nc.dram_tensor("out", [NOWN, 1024], F32, kind="ExternalOutput").ap()

    def scratch(name, shape, dt):
        if name in dbg:
            return nc.dram_tensor(name, shape, dt, kind="ExternalOutput").ap()
        return nc.dram_tensor(name, shape, dt).ap()

    G = Ctx()
    G.nc, G.I, G.OUT = nc, I, OUT
    G.dbg = dbg
    if dbg:
        G.DBG = {k: nc.dram_tensor(k, shp, dt, kind="ExternalOutput").ap() for k, (shp, dt) in {
            "D_MOD": ([128, 3, 2, 3, 2, 8], F32), "D_X0": ([128, 8, TW], F32), "D_X1": ([128, 8, TW], F32),
            "D_H": ([128, 8, TW], BF16), "D_HID": ([128, 22, TW], BF16), "D_RSTD": ([128, TW], F32)}.items()}
    G.XT = scratch("XT", [128, 8, NU], F32)
    G.NAQT = scratch("NAQT", [128, 4, NU], BF16)
    G.NAKT = scratch("NAKT", [128, 4, NU], BF16)
    G.NAV = scratch("NAV", [NU, 520], BF16)
    G.MLQT = scratch("MLQT", [128, 4, NU], BF16)
    G.MLKT = scratch("MLKT", [128, 4, NU], BF16)
    G.MLK = scratch("MLK", [NU, 512], BF16)
    G.MLV = scratch("MLV", [NU, 516], BF16)
    G.MLG2 = scratch("MLG2", [NU, 512], F32)
    G.GIF = scratch("GIF", [128, NU], F32)
    G.YT = scratch("YT", [128, 8, NOWN], BF16)
    G.HF = scratch("HF", [NOWN, 512], F32)
    G.HB = scratch("HB", [NOWN, 512], F32)
    G.CCI = scratch("CCI", [128, 1040], F32)
    G.CCO = scratch("CCO", [512, 1040], F32)

    with contextlib.ExitStack() as gst:
        P = Prog(nc, gst)
        G.P = P

        def gsb(name, shape, dt):
            return gst.enter_context(nc.sbuf_tensor(name, shape, dt))
        G.ident_f = gsb("ident_f", [128, 128], F32)
        G.ident_b = gsb("ident_b", [128, 128], BF16)
        G.ident8 = gsb("ident8", [128, 128], BF16)
        G.ones_b = gsb("ones_b", [128, 128], BF16)
        G.ones_f = gsb("ones_f", [128, 128], F32)
        G.c_eps = gsb("c_eps", [128, 1], F32)
        G.c_one = gsb("c_one", [128, 1], F32)
        G.SH = gsb("SH", [128, 2, 3, 2, 8], F32)
        G.GM = gsb("GM", [128, 2, 3, 2, 8], F32)
        G.GT = gsb("GT", [128, 2, 3, 2, 8], F32)

        P.dma("sync", lambda e: e.dma_start(out=G.ident_f[:], in_=I["ident"][:, :]), writes=["ident_f"])
        P.op("vector", lambda e: e.tensor_copy(G.ident_b[:], G.ident_f[:]), reads=["ident_f"], writes=["ident_b"])
        P.op("vector", lambda e: e.tensor_scalar(G.ident8[:], G.ident_f[:], 8.0, None, ALU.mult), reads=["ident_f"], writes=["ident8"])
        P.op("vector", lambda e: e.memset(G.ones_b[:], 1.0), writes=["ones_b"])
        P.op("vector", lambda e: e.memset(G.ones_f[:], 1.0), writes=["ones_f"])
        P.op("vector", lambda e: e.memset(G.c_eps[:], EPS), writes=["c_eps"])
        P.op("vector", lambda e: e.memset(G.c_one[:], 1.0), writes=["c_one"])

        ext_tiles = [(ti, 0) for ti in range(18)] + [(18, 1)]
        own_tiles = [(ti, 0) for ti in range(1, 17)]
        S = stages
        stage_mod(G)
        stage_t0(G)
        if S in ("ffn_only",):
            stage_ffn(G, 0, 0, ext_tiles)
            stage_final(G)
        elif S == "ffn_dbg":
            stage_ffn(G, 0, 0, [(1, 0)])
        else:
            stage_ffn(G, 0, 0, ext_tiles)
            stage_inproj(G)
            stage_na(G)
            stage_ml(G)
            stage_outproj(G)
            stage_ffn(G, 0, 1, own_tiles)
            stage_ffn(G, 1, 0, own_tiles)
            stage_sg(G)
            stage_ffn(G, 1, 1, own_tiles)
            stage_final(G)
    return nc


_AC = [0]


def _alloc(G, st):
    nc = G.nc
    _AC[0] += 1
    sfx = "_%d" % _AC[0]

    def sb(name, shape, dt):
        return st.enter_context(nc.sbuf_tensor(name + sfx, shape, dt))

    def ps(name, shape, dt=F32):
        return st.enter_context(nc.psum_tensor(name + sfx, shape, dt))
    return sb, ps


def stage_mod(G):
    P, I, nc = G.P, G.I, G.nc
    with contextlib.ExitStack() as st:
        sb, ps = _alloc(G, st)
        cT = sb("m_cT", [128, 8, 2], F32)
        sil = sb("m_sil", [128, 8, 2], BF16)
        wm = [sb("m_wm%d" % i, [128, 8, 1024], BF16) for i in range(3)]
        modv = sb("m_modv", [128, 2, 72, 2], F32)
        bmod = sb("m_bmod", [128, 2, 72], F32)
        ng = sb("m_ng", [128, 2, 3, 8], F32)
        pp = [ps("m_ps%d" % i, [128, 8, 2]) for i in range(2)]
        xs = [sb("t_xs%d" % i, [128, 1024], F32) for i in range(3)]
        xo = [sb("t_xo%d" % i, [128, 8, 128], F32) for i in range(2)]
        pt = [ps("t_pt%d" % i, [128, 8, 128]) for i in range(2)]
        P.dma("sync", lambda e: e.dma_start(out=cT[:], in_=I["cT"][:, :, :]), writes=["m_cT"])
        P.dma("sync", lambda e: e.dma_start(out=bmod[:], in_=I["b_modT"][:, :, :]), writes=["m_bmod"])
        P.dma("sync", lambda e: e.dma_start(out=ng[:], in_=I["norm_gT"][:, :, :, :]), writes=["m_ng"])
        P.op("scalar", lambda e: e.activation(out=sil[:], in_=cT[:], func=AF.Silu), reads=["m_cT"], writes=["m_sil"])
        NSUB = NU // 128

        def t0_load(i):
            if i < NSUB:
                b3 = i % 3
                P.dma("sync", lambda e, b3=b3, i=i: e.dma_start(out=xs[b3][:], in_=I["xin"][i * 128:(i + 1) * 128, :]), writes=["t_xs%d" % b3])

        def t0_sub(i):
            t0 = i * 128
            b, b3 = i % 2, i % 3
            for k in range(8):
                P.op("tensor", lambda e, b=b, b3=b3, k=k: e.transpose(pt[b][:, k, :], xs[b3][:, k * 128:(k + 1) * 128], G.ident_f[:]),
                     reads=["t_xs%d" % b3, "ident_f"], writes=["t_pt%d" % b], inc=(k == 7))
            if b == 0:
                P.op("vector", lambda e, b=b: e.tensor_copy(xo[b][:], pt[b][:]), reads=["t_pt%d" % b], writes=["t_xo%d" % b])
            else:
                P.op("scalar", lambda e, b=b: e.activation(out=xo[b][:], in_=pt[b][:], func=AF.Identity), reads=["t_pt%d" % b], writes=["t_xo%d" % b])
            t0_load(i + 3)
            P.dma("sync", lambda e, b=b, t0=t0: e.dma_start(out=G.XT[:, :, t0:t0 + 128], in_=xo[b][:]),
                  reads=["t_xo%d" % b], writes=["XT%d" % (t0 // TW)])

        for i in range(3):
            t0_load(i)
        n = 0
        ti = 0
        for l in range(2):
            wsrc = I["w_mod"][l].rearrange("(k p) n -> p k n", p=128)
            for blk in range(9):
                w = wm[n % 3]
                wn = "m_wm%d" % (n % 3)
                pt_ = pp[n % 2]
                pn = "m_ps%d" % (n % 2)
                P.dma("gpsimd", lambda e, w=w, blk=blk, wsrc=wsrc: e.dma_start(out=w[:], in_=wsrc[:, :, blk * 1024:(blk + 1) * 1024]),
                      writes=[wn])
                for jj in range(8):
                    for k in range(8):
                        P.op("tensor", lambda e, w=w, pt_=pt_, jj=jj, k=k: e.matmul(
                            pt_[:, jj, :], lhsT=w[:, k, jj * 128:(jj + 1) * 128], rhs=sil[:, k, :], start=(k == 0), stop=(k == 7)),
                            reads=[wn, "m_sil"], writes=[pn], inc=(jj == 7 and k == 7))
                for m in range(2):
                    P.op("vector", lambda e, pt_=pt_, l=l, blk=blk, m=m: e.tensor_tensor(
                        out=modv[:, l, blk * 8:(blk + 1) * 8, m], in0=pt_[:, :, m], in1=bmod[:, l, blk * 8:(blk + 1) * 8], op=ALU.add),
                        reads=[pn, "m_bmod"], writes=["m_modv"])
                n += 1
                for _ in range(2):
                    if ti < NSUB:
                        t0_sub(ti)
                        ti += 1
        while ti < NSUB:
            t0_sub(ti)
            ti += 1
        for l in range(2):
            for j in range(3):
                for m in range(2):
                    sh = modv[:, l, (3 * j) * 8:(3 * j) * 8 + 8, m]
                    sc = modv[:, l, (3 * j + 1) * 8:(3 * j + 1) * 8 + 8, m]
                    gt = modv[:, l, (3 * j + 2) * 8:(3 * j + 2) * 8 + 8, m]
                    P.op("vector", lambda e, sh=sh, l=l, j=j, m=m: e.tensor_copy(G.SH[:, l, j, m, :], sh), reads=["m_modv"], writes=["modc"])
                    P.op("vector", lambda e, sc=sc, l=l, j=j, m=m: e.scalar_tensor_tensor(
                        out=G.GM[:, l, j, m, :], in0=sc, scalar=1.0, in1=ng[:, l, j, :], op0=ALU.add, op1=ALU.mult),
                        reads=["m_modv", "m_ng"], writes=["modc"])
                    P.op("vector", lambda e, gt=gt, l=l, j=j, m=m: e.tensor_scalar(
                        G.GT[:, l, j, m, :], gt, (1.0 if j == 1 else 0.5), None, ALU.mult), reads=["m_modv"], writes=["modc"])
        P.barrier()
        P.emit()


def stage_t0(G):
    return


class NormScratch:
    def __init__(self, G, sb, ps, pfx, W=TW):
        self.sq = [sb(pfx + "sq%d" % i, [128, W], BF16) for i in range(2)]
        self.rs = sb(pfx + "rs", [128, W], F32)
        self.rstd = sb(pfx + "rstd", [128, W], F32)
        self.tmp = [sb(pfx + "tmp%d" % i, [128, W], F32) for i in range(2)]
        self.pn = ps(pfx + "pn", [128, W])
        self.pfx = pfx


def norm_mod(G, NS, x, xname, gm, sh, h, hname, W=TW, phase=None):
    P = G.P
    pfx = NS.pfx
    for k in range(8):
        if phase is None:
            sq, sqn = NS.sq[k % 2], pfx + "sq%d" % (k % 2)
        else:
            sq, sqn = NS.sq8[k], pfx + "sq8_%d" % k
        if phase in (None, "A"):
            P.op("scalar", lambda e, sq=sq, k=k: e.activation(out=sq[:, :W], in_=x[:, k, :], func=AF.Square), reads=[xname], writes=[sqn])
        if phase in (None, "B"):
            P.op("tensor", lambda e, sq=sq, k=k: e.matmul(NS.pn[:, :W], lhsT=G.ones_b[:], rhs=sq[:, :W], start=(k == 0), stop=(k == 7)),
                 reads=[sqn, "ones_b"], writes=[pfx + "pn"], inc=True)
    if phase == "A":
        return
    P.op("scalar", lambda e: e.activation(out=NS.rs[:, :W], in_=NS.pn[:, :W], func=AF.Sqrt, bias=G.c_eps[:], scale=1.0 / 1024.0),
         reads=[pfx + "pn", "c_eps"], writes=[pfx + "rs"])
    P.op("vector", lambda e: e.reciprocal(NS.rstd[:, :W], NS.rs[:, :W]), reads=[pfx + "rs"], writes=[pfx + "rstd"])
    for k in range(8):
        tmp = NS.tmp[k % 2]
        tn = pfx + "tmp%d" % (k % 2)
        P.op("vector", lambda e, tmp=tmp, k=k: e.tensor_tensor(out=tmp[:, :W], in0=x[:, k, :], in1=NS.rstd[:, :W], op=ALU.mult),
             reads=[xname, pfx + "rstd"], writes=[tn])
        P.op("scalar", lambda e, tmp=tmp, k=k: e.activation(out=h[:, k, :], in_=tmp[:, :W], func=AF.Identity, bias=sh[:, k:k + 1], scale=gm[:, k:k + 1]),
             reads=[tn, "modc"], writes=[hname])


def load_w_cast(G, dst, dname, src, nk, ncols, step=1024):
    P = G.P
    v = src.rearrange("(k p) n -> p k n", p=128)
    for c0 in range(0, ncols, step):
        c1 = min(ncols, c0 + step)
        P.dma("gpsimd", lambda e, c0=c0, c1=c1: e.dma_start(out=dst[:, :, c0:c1], in_=v[:, :, c0:c1]), writes=[dname])


def stage_ffn(G, l, i, tiles):
    P, I, nc = G.P, G.I, G.nc
    j = 0 if i == 0 else 2
    with contextlib.ExitStack() as st:
        sb, ps = _alloc(G, st)
        wi = sb("f_wi", [128, 8, 2 * DFF], BF16)
        wo = sb("f_wo", [128, 22, 1024], BF16)
        xt = [sb("f_xt%d" % b, [128, 8, TW], F32) for b in range(2)]
        hh = [sb("f_h%d" % b, [128, 8, TW], BF16) for b in range(2)]
        hid = sb("f_hid", [128, 22, TW], BF16)
        sa = [sb("f_sa%d" % b, [128, TW], F32) for b in range(2)]
        NS = NormScratch(G, sb, ps, "f_")
        NS.sq8 = [sb("f_sq8_%d" % k, [128, TW], BF16) for k in range(8)]
        pa = [ps("f_pa%d" % b, [128, TW]) for b in range(2)]
        pb = [ps("f_pb%d" % b, [128, TW]) for b in range(2)]
        po = [ps("f_po%d" % b, [128, TW]) for b in range(2)]
        wv_ = I["ffn_w_in"][l, i].rearrange("(k p) n -> p k n", p=128)
        for pc in (0, 2, 3, 1, 4, 5):
            c0, c1 = pc * 1024, min(2 * DFF, (pc + 1) * 1024)
            P.dma("gpsimd", lambda e, c0=c0, c1=c1: e.dma_start(out=wi[:, :, c0:c1], in_=wv_[:, :, c0:c1]), writes=["f_wi%d" % pc])
        load_w_cast(G, wo, "f_wo", I["ffn_w_out"][l, i], 22, 1024)
        def ld(n):
            ti_ = tiles[n][0]
            xb = xt[n % 2]
            P.dma("sync", lambda e, xb=xb, ti_=ti_: e.dma_start(out=xb[:], in_=G.XT[:, :, ti_ * TW:(ti_ + 1) * TW]),
                  reads=["XT%d" % ti_], writes=["f_xt%d" % (n % 2)])
        def do_norm(n, phase=None):
            ti_, m_ = tiles[n]
            bb = n % 2
            norm_mod(G, NS, xt[bb], "f_xt%d" % bb, G.GM[:, l, j, m_, :], G.SH[:, l, j, m_, :], hh[bb], "f_h%d" % bb, phase=phase)
        ld(0)
        if len(tiles) > 1:
            ld(1)
        do_norm(0)
        for n, (ti, m) in enumerate(tiles):
            t0 = ti * TW
            b = n % 2
            x, xn = xt[b], "f_xt%d" % b
            h, hn = hh[b], "f_h%d" % b
            for jj in range(22):
                q = jj % 2
                for half, pp, pn in ((0, pa[q], "f_pa%d" % q), (1, pb[q], "f_pb%d" % q)):
                    c0 = half * DFF + jj * 128
                    for k in range(8):
                        P.op("tensor", lambda e, pp=pp, c0=c0, k=k, h=h: e.matmul(
                            pp[:], lhsT=wi[:, k, c0:c0 + 128], rhs=h[:, k, :], start=(k == 0), stop=(k == 7)),
                            reads=["f_wi%d" % (c0 // 1024), "f_wi%d" % ((c0 + 127) // 1024), hn], writes=[pn], inc=(k == 7))
                P.op("scalar", lambda e, q=q: e.activation(out=sa[q][:], in_=pa[q][:], func=AF.Silu), reads=["f_pa%d" % q], writes=["f_sa%d" % q])
                P.op("vector", lambda e, q=q, jj=jj: e.tensor_tensor(out=hid[:, jj, :], in0=sa[q][:], in1=pb[q][:], op=ALU.mult),
                     reads=["f_sa%d" % q, "f_pb%d" % q], writes=["f_hid%d" % jj])
            if n + 1 < len(tiles):
                do_norm(n + 1, "A")
            for f in range(8):
                q = f % 2
                if f == 3 and n + 1 < len(tiles):
                    do_norm(n + 1, "B")
                for jj in range(22):
                    P.op("tensor", lambda e, q=q, f=f, jj=jj: e.matmul(
                        po[q][:], lhsT=wo[:, jj, f * 128:(f + 1) * 128], rhs=hid[:, jj, :], start=(jj == 0), stop=(jj == 21)),
                        reads=["f_wo", "f_hid%d" % jj], writes=["f_po%d" % q], inc=(jj == 21))
                gsc = G.GT[:, l, j, m, f:f + 1]
                P.op("vector", lambda e, q=q, f=f, x=x, gsc=gsc: e.scalar_tensor_tensor(
                    out=x[:, f, :], in0=po[q][:], scalar=gsc, in1=x[:, f, :], op0=ALU.mult, op1=ALU.add),
                    reads=["f_po%d" % q, "modc", xn], writes=[xn])
            if G.dbg and ti == 1:
                P.dma("sync", lambda e: e.dma_start(out=G.DBG["D_HID"][:, :, :], in_=hid[:]), reads=["f_hid%d" % q for q in range(22)], writes=["dbg3"])
                P.dma("sync", lambda e, x=x: e.dma_start(out=G.DBG["D_X1"][:, :, :], in_=x[:]), reads=[xn], writes=["dbg4"])
            P.dma("sync", lambda e, x=x, t0=t0: e.dma_start(out=G.XT[:, :, t0:t0 + TW], in_=x[:]), reads=[xn], writes=["XT%d" % ti])
            if n + 2 < len(tiles):
                ld(n + 2)
        P.barrier()
        P.emit()


def stage_final(G):
    P, I, nc = G.P, G.I, G.nc
    with contextlib.ExitStack() as st:
        sb, ps = _alloc(G, st)
        fg = sb("k_fg", [128, 1024], F32)
        xs = [sb("k_xs%d" % b, [128, 8, 128], F32) for b in range(2)]
        junk = sb("k_junk", [128, 1024], F32)
        yo = [sb("k_yo%d" % b, [128, 1024], F32) for b in range(2)]
        ssq = sb("k_ssq", [128, 2], F32)
        rs = sb("k_rs", [128, 2], F32)
        pt = [ps("k_pt%d" % b, [128, 8, 128]) for b in range(2)]
        P.dma("sync", lambda e: e.dma_start(out=fg[:], in_=I["final_g_bc"][:, :]), writes=["k_fg"])
        for i in range(NOWN // 128):
            t0 = OWN0 + i * 128
            b = i % 2
            P.dma("gpsimd", lambda e, b=b, t0=t0: e.dma_start(out=xs[b][:], in_=G.XT[:, :, t0:t0 + 128]),
                  reads=["XT%d" % (t0 // TW)], writes=["k_xs%d" % b])
            for k in range(8):
                P.op("tensor", lambda e, b=b, k=k: e.transpose(pt[b][:, k, :], xs[b][:, k, :], G.ident_f[:]),
                     reads=["k_xs%d" % b, "ident_f"], writes=["k_pt%d" % b], inc=(k == 7))
            ptf = pt[b][:].rearrange("p k n -> p (k n)")
            P.op("scalar", lambda e, b=b, ptf=ptf: e.activation(out=junk[:], in_=ptf, func=AF.Square, accum_out=ssq[:, b:b + 1]),
                 reads=["k_pt%d" % b], writes=["k_junk", "k_ssq%d" % b])
            P.op("scalar", lambda e, b=b: e.activation(out=rs[:, b:b + 1], in_=ssq[:, b:b + 1], func=AF.Sqrt, bias=G.c_eps[:], scale=1.0 / 1024.0),
                 reads=["k_ssq%d" % b, "c_eps"], writes=["k_rs%d" % b])
            P.op("vector", lambda e, b=b: e.reciprocal(rs[:, b:b + 1], rs[:, b:b + 1]), reads=["k_rs%d" % b], writes=["k_rs%d" % b])
            P.op("vector", lambda e, b=b, ptf=ptf: e.scalar_tensor_tensor(
                out=yo[b][:], in0=ptf, scalar=rs[:, b:b + 1], in1=fg[:], op0=ALU.mult, op1=ALU.mult),
                reads=["k_pt%d" % b, "k_rs%d" % b, "k_fg"], writes=["k_yo%d" % b])
            P.dma("sync", lambda e, b=b, i=i: e.dma_start(out=G.OUT[i * 128:(i + 1) * 128, :], in_=yo[b][:]),
                  reads=["k_yo%d" % b], writes=["OUT"])
        P.barrier()
        P.emit()


def stage_inproj(G):
    P, I, nc = G.P, G.I, G.nc
    l, j = 0, 1
    tiles = [(ti, 0) for ti in range(18)] + [(18, 1)]
    with contextlib.ExitStack() as st:
        sb, ps = _alloc(G, st)
        wfm = sb("i_wfm", [128, 8, 1024], BF16)
        wg = sb("i_wg", [128, 8, 128], BF16)
        wtm = sb("i_wtm", [128, 8, 2560], BF16)
        xt = [sb("i_xt%d" % b, [128, 8, TW], F32) for b in range(2)]
        hh = [sb("i_h%d" % b, [128, 8, TW], BF16) for b in range(2)]
        NS = NormScratch(G, sb, ps, "i_")
        NS.sq8 = [sb("i_sq8_%d" % k, [128, TW], BF16) for k in range(8)]
        fm = [sb("i_fm%d" % b, [128, 8, TW], BF16) for b in range(2)]
        gts = [sb("i_gt%d" % b, [128, TW], F32) for b in range(2)]
        cosT = [sb("i_cos%d" % b, [128, 512], F32) for b in range(2)]
        sinT = [sb("i_sin%d" % b, [128, 512], F32) for b in range(2)]
        hg = sb("i_hg", [128, 512], F32)
        vt = [sb("i_vt%d" % b, [128, 8, 65], BF16) for b in range(2)]
        mv = [sb("i_mv%d" % b, [128, 4, 129], BF16) for b in range(2)]
        xs = [sb("i_xs%d" % b, [128, 512], F32) for b in range(2)]
        r1 = [sb("i_r1%d" % b, [128, 512], F32) for b in range(2)]
        r2 = [sb("i_r2%d" % b, [128, 512], F32) for b in range(2)]
        qr = [sb("i_qr%d" % b, [128, 512], BF16) for b in range(2)]
        sg = sb("i_sg", [128, 512], F32)
        g2 = [sb("i_g2%d" % b, [128, 512], F32) for b in range(2)]
        tq = [sb("i_tq%d" % b, [128, 4, 128], BF16) for b in range(2)]
        pfm = [ps("i_pfm%d" % b, [128, TW]) for b in range(2)]
        ptm = [ps("i_ptm%d" % b, [128, 512]) for b in range(2)]
        ptr = [ps("i_ptr%d" % b, [128, 4, 128], BF16) for b in range(2)]
        load_w_cast(G, wfm, "i_wfm", I["mix_w_in"][:, 0:1024], 8, 1024)
        load_w_cast(G, wg, "i_wg", I["w_gate"], 8, 128)
        load_w_cast(G, wtm, "i_wtm", I["mix_w_in"][:, 1024:3584], 8, 2560)
        P.dma("sync", lambda e: e.dma_start(out=hg[:], in_=I["headg_bc"][:, :]), writes=["i_hg"])
        for b in range(2):
            P.op("vector", lambda e, b=b: e.memset(vt[b][:], 1.0), writes=["i_vt%d" % b])
            P.op("vector", lambda e, b=b: e.memset(mv[b][:], 1.0), writes=["i_mv%d" % b])

        def ld(n):
            ti_ = tiles[n][0]
            xb = xt[n % 2]
            P.dma("gpsimd", lambda e, xb=xb, ti_=ti_: e.dma_start(out=xb[:], in_=G.XT[:, :, ti_ * TW:(ti_ + 1) * TW]),
                  reads=["XT%d" % ti_], writes=["i_xt%d" % (n % 2)])
        ld(0)
        cnt = {"s": 0, "r": 0, "t": 0}
        deferred = []

        def rope_and_T(pt_, ptn, scale, dstT, tmaj_dst, ts0):
            a = cnt["r"] % 2
            cnt["r"] += 1
            sb_ = cnt["s"] % 2
            X, R1, R2, QR, TQ, PT = xs[a], r1[a], r2[a], qr[a], tq[a], ptr[a]
            xn_, r1n, r2n, qrn, tqn, ptn2 = "i_xs%d" % a, "i_r1%d" % a, "i_r2%d" % a, "i_qr%d" % a, "i_tq%d" % a, "i_ptr%d" % a
            P.op("scalar", lambda e: e.activation(out=X[:], in_=pt_[:], func=AF.Copy, scale=scale), reads=[ptn], writes=[xn_])
            P.op("vector", lambda e: e.tensor_tensor(out=R1[:], in0=X[:], in1=cosT[sb_][:], op=ALU.mult), reads=[xn_, "i_cos%d" % sb_], writes=[r1n])
            Xv = X[:].rearrange("p (i t) -> p i t", t=2)
            Sv = sinT[sb_][:].rearrange("p (i t) -> p i t", t=2)
            Rv = R2[:].rearrange("p (i t) -> p i t", t=2)
            P.op("vector", lambda e: e.tensor_tensor(out=Rv[:, :, 0], in0=Xv[:, :, 1], in1=Sv[:, :, 0], op=ALU.mult),
                 reads=[xn_, "i_sin%d" % sb_], writes=[r2n])
            P.op("vector", lambda e: e.tensor_tensor(out=Rv[:, :, 1], in0=Xv[:, :, 0], in1=Sv[:, :, 1], op=ALU.mult),
                 reads=[xn_, "i_sin%d" % sb_], writes=[r2n])
            P.op("vector", lambda e: e.tensor_tensor(out=QR[:], in0=R1[:], in1=R2[:], op=ALU.add), reads=[r1n, r2n], writes=[qrn])
            if tmaj_dst is not None:
                P.dma("sync", lambda e: e.dma_start(out=tmaj_dst[ts0:ts0 + 128, :], in_=QR[:]), reads=[qrn], writes=[_u()])
            def later():
                for hd in range(4):
                    P.op("tensor", lambda e, hd=hd: e.transpose(PT[:, hd, :], QR[:, hd * 128:(hd + 1) * 128], G.ident_b[:]),
                         reads=[qrn, "ident_b"], writes=[ptn2], inc=(hd == 3))
                P.op("scalar", lambda e: e.activation(out=TQ[:], in_=PT[:], func=AF.Copy), reads=[ptn2], writes=[tqn])
                P.dma("sync", lambda e: e.dma_start(out=dstT[:, :, ts0:ts0 + 128], in_=TQ[:]), reads=[tqn], writes=[_u()])
            deferred.append(later)

        for n, (ti, m) in enumerate(tiles):
            t0 = ti * TW
            b = n % 2
            x, xn = xt[b], "i_xt%d" % b
            h, hn = hh[b], "i_h%d" % b
            if n + 1 < len(tiles):
                ld(n + 1)
            if n == 0:
                norm_mod(G, NS, x, xn, G.GM[:, l, j, m, :], G.SH[:, l, j, m, :], h, hn)
            FM, fmn = fm[b], "i_fm%d" % b
            no_q = ti in (0, 17, 18)
            blks = [0] if ti in (0, 17) else ([0, 2, 3] if ti == 18 else [0, 1, 2, 3, 4])
            for fc in (range(4, 8) if no_q else range(8)):
                q = fc % 2
                for k in range(8):
                    P.op("tensor", lambda e, q=q, fc=fc, k=k, h=h: e.matmul(
                        pfm[q][:], lhsT=wfm[:, k, fc * 128:(fc + 1) * 128], rhs=h[:, k, :], start=(k == 0), stop=(k == 7)),
                        reads=["i_wfm", hn], writes=["i_pfm%d" % q], inc=(k == 7))
                P.op("scalar", lambda e, q=q, fc=fc, FM=FM: e.activation(out=FM[:, fc, :], in_=pfm[q][:], func=AF.Copy),
                     reads=["i_pfm%d" % q], writes=[fmn])
            if not no_q:
                P.dma("sync", lambda e, FM=FM, t0=t0: e.dma_start(out=G.NAQT[:, :, t0:t0 + TW], in_=FM[:, 0:4, :]), reads=[fmn], writes=[_u()])
            P.dma("sync", lambda e, FM=FM, t0=t0: e.dma_start(out=G.NAKT[:, :, t0:t0 + TW], in_=FM[:, 4:8, :]), reads=[fmn], writes=[_u()])
            GTS, gtn = gts[b], "i_gt%d" % b
            for k in range(8):
                P.op("tensor", lambda e, k=k, h=h: e.matmul(pfm[0][:], lhsT=wg[:, k, :], rhs=h[:, k, :], start=(k == 0), stop=(k == 7)),
                     reads=["i_wg", hn], writes=["i_pfm0"], inc=(k == 7))
            P.op("vector", lambda e, GTS=GTS: e.tensor_copy(GTS[:], pfm[0][:]), reads=["i_pfm0"], writes=[gtn])
            P.dma("sync", lambda e, GTS=GTS, t0=t0: e.dma_start(out=G.GIF[:, t0:t0 + TW], in_=GTS[:]), reads=[gtn], writes=[_u()])
            if n + 1 < len(tiles):
                ti2, m2 = tiles[n + 1]
                b2 = (n + 1) % 2
                norm_mod(G, NS, xt[b2], "i_xt%d" % b2, G.GM[:, l, j, m2, :], G.SH[:, l, j, m2, :], hh[b2], "i_h%d" % b2, phase="A")
            for s_ in range(TW // 128):
                if s_ == 1 and n + 1 < len(tiles):
                    norm_mod(G, NS, xt[b2], "i_xt%d" % b2, G.GM[:, l, j, m2, :], G.SH[:, l, j, m2, :], hh[b2], "i_h%d" % b2, phase="B")
                ts0 = t0 + s_ * 128
                sbi = cnt["s"] % 2
                P.dma("gpsimd", lambda e, sbi=sbi, ts0=ts0: e.dma_start(out=cosT[sbi][:], in_=I["ropecos"][ts0:ts0 + 128, :]), writes=["i_cos%d" % sbi])
                P.dma("gpsimd", lambda e, sbi=sbi, ts0=ts0: e.dma_start(out=sinT[sbi][:], in_=I["ropesin"][ts0:ts0 + 128, :]), writes=["i_sin%d" % sbi])
                for blk in blks:
                    a = cnt["t"] % 2
                    cnt["t"] += 1
                    PT_, ptn = ptm[a], "i_ptm%d" % a
                    for k in range(8):
                        P.op("tensor", lambda e, PT_=PT_, k=k, h=h, s_=s_, blk=blk: e.matmul(
                            PT_[:], lhsT=h[:, k, s_ * 128:(s_ + 1) * 128], rhs=wtm[:, k, blk * 512:(blk + 1) * 512], start=(k == 0), stop=(k == 7)),
                            reads=["i_wtm", hn], writes=[ptn], inc=(k == 7))
                    while len(deferred) > (1 if blk == 3 else 0):
                        deferred.pop(0)()
                    if blk == 0:
                        VT = vt[sbi]
                        P.op("scalar", lambda e, VT=VT, PT_=PT_: e.activation(out=VT[:, :, 0:64], in_=PT_[:].rearrange("p (h d) -> p h d", d=64), func=AF.Copy),
                             reads=[ptn], writes=["i_vt%d" % sbi])
                        P.dma("sync", lambda e, VT=VT, ts0=ts0: e.dma_start(out=G.NAV[ts0:ts0 + 128, :], in_=VT[:].rearrange("p h d -> p (h d)")),
                              reads=["i_vt%d" % sbi], writes=[_u()])
                    elif blk == 1:
                        rope_and_T(PT_, ptn, 1.0, G.MLQT, None, ts0)
                    elif blk == 2:
                        rope_and_T(PT_, ptn, 128.0 ** -0.5, G.MLKT, G.MLK, ts0)
                    elif blk == 3:
                        MV = mv[sbi]
                        P.op("scalar", lambda e, MV=MV, PT_=PT_: e.activation(out=MV[:, :, 0:128], in_=PT_[:].rearrange("p (h d) -> p h d", d=128), func=AF.Copy),
                             reads=[ptn], writes=["i_mv%d" % sbi])
                        P.dma("sync", lambda e, MV=MV, ts0=ts0: e.dma_start(out=G.MLV[ts0:ts0 + 128, :], in_=MV[:].rearrange("p h d -> p (h d)")),
                              reads=["i_mv%d" % sbi], writes=[_u()])
                    else:
                        G2 = g2[sbi]
                        P.op("scalar", lambda e, PT_=PT_: e.activation(out=sg[:], in_=PT_[:], func=AF.Sigmoid), reads=[ptn], writes=["i_sg"])
                        P.op("vector", lambda e, G2=G2: e.tensor_tensor(out=G2[:], in0=sg[:], in1=hg[:], op=ALU.mult), reads=["i_sg", "i_hg"], writes=["i_g2%d" % sbi])
                        P.dma("sync", lambda e, G2=G2, ts0=ts0: e.dma_start(out=G.MLG2[ts0:ts0 + 128, :], in_=G2[:]), reads=["i_g2%d" % sbi], writes=[_u()])
                while deferred:
                    deferred.pop(0)()
                cnt["s"] += 1
        P.barrier()
        P.emit()


def stage_na(G):
    P, I, nc = G.P, G.I, G.nc
    with contextlib.ExitStack() as st:
        sb, ps = _alloc(G, st)
        KT = sb("n_KT", [128, 4, NU], BF16)
        V = sb("n_V", [128, 38, 520], BF16)
        QT = sb("n_QT", [128, 4, NOWN], BF16)
        BI = sb("n_BI", [128, 5, 8, 5, 128], BF16)
        sAb = [sb("n_sAb%d" % b, [128, 4, 128], F32) for b in range(2)]
        sBb = [sb("n_sBb%d" % b, [128, 128], F32) for b in range(2)]
        pt = [sb("n_pt%d" % b, [128, 7, 128], BF16) for b in range(2)]
        ya = [sb("n_ya%d" % b, [128, 512], BF16) for b in range(2)]
        rec = [sb("n_rec%d" % b, [128, 8], F32) for b in range(2)]
        yt = [sb("n_yt%d" % b, [128, 4, 128], BF16) for b in range(2)]
        sA = [ps("n_sA%d" % b, [128, 4, 128]) for b in range(2)]
        sB = [ps("n_sB%d" % b, [128, 4, 128]) for b in range(2)]
        O = ps("n_O", [128, 8, 128])
        ptr = ps("n_ptr", [128, 4, 128], BF16)
        P.dma("sync", lambda e: e.dma_start(out=KT[:], in_=G.NAKT[:, :, :]), reads=["dramNA"], writes=["n_KT"])
        P.dma("sync", lambda e: e.dma_start(out=QT[:], in_=G.NAQT[:, :, OWN0:OWN0 + NOWN]), reads=["dramNA"], writes=["n_QT"])
        P.dma("sync", lambda e: e.dma_start(out=V[:], in_=G.NAV.rearrange("(c p) n -> p c n", p=128)), reads=["dramNA"], writes=["n_V"])
        for c5 in range(5):
            P.dma("gpsimd", lambda e, c5=c5: e.dma_start(out=BI[:, c5], in_=I["nabiasT"][c5].rearrange("h k j q -> k h j q")), writes=["n_BI"])
        def scores(p, h, a):
            cls = 0 if p == 0 else 1 if p == 1 else 3 if p == 30 else 4 if p == 31 else 2
            hc, b0 = h // 2, (h % 2) * 64
            q_ap = QT[b0:b0 + 64, hc, p * 128:(p + 1) * 128]
            SA, SB, PT = sA[a], sB[a], pt[a]
            san, sbn, ptn = "n_sA%d" % a, "n_sB%d" % a, "n_pt%d" % a
            for jj in range(4):
                k0 = (p + jj) * 128
                P.op("tensor", lambda e, jj=jj, k0=k0: e.matmul(
                    SA[:, jj, :], lhsT=KT[b0:b0 + 64, hc, k0:k0 + 128], rhs=q_ap, start=True, stop=True),
                    reads=["n_KT", "n_QT"], writes=[san], inc=(jj == 3))
            k0h = (p + 4) * 128
            P.op("tensor", lambda e: e.matmul(
                SB[0:64, 0, :], lhsT=KT[b0:b0 + 64, hc, k0h:k0h + 64], rhs=q_ap, start=True, stop=True),
                reads=["n_KT", "n_QT"], writes=[sbn], inc=False)
            for c in range(2):
                k0 = CTX0 + c * 128
                P.op("tensor", lambda e, c=c, k0=k0: e.matmul(
                    SB[:, 1 + c, :], lhsT=KT[b0:b0 + 64, hc, k0:k0 + 128], rhs=q_ap, start=True, stop=True),
                    reads=["n_KT", "n_QT"], writes=[sbn], inc=(c == 1))
            AB, BB = sAb[a], sBb[a]
            abn, bbn = "n_sAb%d" % a, "n_sBb%d" % a
            P.op("vector", lambda e: e.scalar_tensor_tensor(
                out=AB[:], in0=SA[:], scalar=0.125, in1=BI[:, cls, h, 0:4, :], op0=ALU.mult, op1=ALU.add),
                reads=[san, "n_BI"], writes=[abn])
            P.op("vector", lambda e: e.scalar_tensor_tensor(
                out=BB[0:64, :], in0=SB[0:64, 0, :], scalar=0.125, in1=BI[0:64, cls, h, 4, :], op0=ALU.mult, op1=ALU.add),
                reads=[sbn, "n_BI"], writes=[bbn])
            P.op("scalar", lambda e: e.activation(out=PT[:, 0:4, :], in_=AB[:], func=AF.Exp), reads=[abn], writes=[ptn])
            P.op("scalar", lambda e: e.activation(out=PT[:, 5:7, :], in_=SB[:, 1:3, :], func=AF.Exp, scale=0.125), reads=[sbn, bbn], writes=[ptn])
            P.op("scalar", lambda e: e.activation(out=PT[0:64, 4, :], in_=BB[0:64, :], func=AF.Exp), reads=[bbn], writes=[ptn])

        def pv(p, h, a):
            pb2 = p % 2
            PT, ptn = pt[a], "n_pt%d" % a
            specs = [(jj, 128, p + jj) for jj in range(4)] + [(4, 64, p + 4), (5, 128, 36), (6, 128, 37)]
            for si, (slot, nk, vc) in enumerate(specs):
                P.op("tensor", lambda e, slot=slot, nk=nk, vc=vc, si=si: e.matmul(
                    O[:, h, 0:65], lhsT=PT[0:nk, slot, :], rhs=V[0:nk, vc, h * 65:(h + 1) * 65], start=(si == 0), stop=(si == 6)),
                    reads=[ptn, "n_V"], writes=["n_O%d" % (h // 4)], inc=(si == 6))
            if h in (3, 7):
                hf_ = h // 4
                R, YA = rec[pb2], ya[pb2]
                rn, yan = "n_rec%d_%d" % (pb2, hf_), "n_ya%d" % pb2
                P.op("vector", lambda e: e.reciprocal(R[:, hf_ * 4:hf_ * 4 + 4], O[:, hf_ * 4:hf_ * 4 + 4, 64]), reads=["n_O%d" % hf_], writes=[rn])
                for h2 in range(hf_ * 4, hf_ * 4 + 4):
                    P.op("scalar", lambda e, h2=h2: e.activation(out=YA[:, h2 * 64:(h2 + 1) * 64], in_=O[:, h2, 0:64], func=AF.Copy, scale=R[:, h2:h2 + 1]),
                         reads=["n_O%d" % hf_, rn], writes=[yan])
            if h == 7:
                YA, YT_ = ya[pb2], yt[pb2]
                yan, ytn = "n_ya%d" % pb2, "n_yt%d" % pb2
                for c in range(4):
                    P.op("tensor", lambda e, c=c: e.transpose(ptr[:, c, :], YA[:, c * 128:(c + 1) * 128], G.ident_b[:]),
                         reads=[yan, "ident_b"], writes=["n_ptr"], inc=(c == 3))
                P.op("vector", lambda e: e.tensor_copy(YT_[:], ptr[:]), reads=["n_ptr"], writes=[ytn])
                P.dma("sync", lambda e: e.dma_start(out=G.YT[:, 0:4, p * 128:(p + 1) * 128], in_=YT_[:]), reads=[ytn], writes=[_u()])

        items = [(p, h) for p in range(32) for h in range(8)]
        scores(items[0][0], items[0][1], 0)
        for i_, (p, h) in enumerate(items):
            if i_ + 1 < len(items):
                scores(items[i_ + 1][0], items[i_ + 1][1], (i_ + 1) % 2)
            pv(p, h, i_ % 2)
        P.barrier()
        P.emit()


def stage_ml(G):
    P, I, nc = G.P, G.I, G.nc
    NCH = 32
    with contextlib.ExitStack() as st:
        sb, ps = _alloc(G, st)
        TOK = sb("l_TOK", [128, 34, 5, 8], F32)
        EBEND = sb("l_EBEND", [128, 8, 32], F32)
        ATOT = sb("l_ATOT", [128, 8], F32)
        with contextlib.ExitStack() as st1:
            sb1, ps1 = _alloc(G, st1)
            LI = sb1("l_LI", [64, NU], F32)
            SP = sb1("l_SP", [64, NU], F32)
            CL = sb1("l_CL", [64, NU], F32)
            CG = sb1("l_CG", [64, NU], F32)
            TM = sb1("l_TM", [64, NU], F32)
            OQ = [sb1("l_OQ%d" % b, [64, NU], F32) for b in range(2)]
            gbI = sb1("l_gbI", [64, 1], F32)
            gbF = sb1("l_gbF", [64, 1], F32)
            CE = sb1("l_CE", [64, 32], F32)
            ntot = sb1("l_ntot", [64, 2], F32)
            tot = sb1("l_tot", [64, 2], F32)
            SELM = sb1("l_SELM", [64, 8, 128], F32)
            ptr = [ps1("l_ptr%d" % b, [128, 8, 64]) for b in range(2)]
            pe = ps1("l_pe", [128, 8, 32])
            pa = ps1("l_pa", [128, 8, 2])
            own = slice(OWN0, OWN0 + NOWN)
            cxs = slice(CTX0, CTX0 + 256)
            P.dma("sync", lambda e: e.dma_start(out=LI[:], in_=G.GIF[0:64, :]), reads=["dramML"], writes=["l_LI"])
            P.dma("sync", lambda e: e.dma_start(out=SP[:], in_=G.GIF[64:128, :]), reads=["dramML"], writes=["l_SP"])
            P.dma("sync", lambda e: e.dma_start(out=gbI[:], in_=I["gate_b"][0:64, :]), writes=["l_gbI"])
            P.dma("sync", lambda e: e.dma_start(out=gbF[:], in_=I["gate_b"][64:128, :]), writes=["l_gbF"])
            P.dma("sync", lambda e: e.dma_start(out=SELM[:], in_=I["selm"][:, :, :]), writes=["l_SELM"])
            P.op("vector", lambda e: e.tensor_scalar(gbF[:], gbF[:], -1.0, None, ALU.mult), reads=["l_gbF"], writes=["l_gbF"])
            P.op("scalar", lambda e: e.activation(out=LI[:], in_=LI[:], func=AF.Identity, bias=gbI[:]), reads=["l_LI", "l_gbI"], writes=["l_LI"])
            P.op("scalar", lambda e: e.activation(out=SP[:], in_=SP[:], func=AF.Exp, bias=gbF[:], scale=-1.0), reads=["l_SP", "l_gbF"], writes=["l_SP"])
            P.op("scalar", lambda e: e.activation(out=SP[:], in_=SP[:], func=AF.Ln, bias=G.c_one[0:64, :]), reads=["l_SP", "c_one"], writes=["l_SP"])
            P.op("vector", lambda e: e.memset(TM[:], 1.0), writes=["l_TM"])
            P.op("vector", lambda e: e.tensor_tensor_scan(out=CG[:, own], data0=TM[:, own], data1=SP[:, own], initial=0.0, op0=ALU.mult, op1=ALU.add),
                 reads=["l_TM", "l_SP"], writes=["l_CG"])
            P.op("vector", lambda e: e.tensor_tensor_scan(out=CG[:, cxs], data0=TM[:, cxs], data1=SP[:, cxs], initial=0.0, op0=ALU.mult, op1=ALU.add),
                 reads=["l_TM", "l_SP"], writes=["l_CG"])
            TMo = TM[:, own].rearrange("p (c t) -> p c t", t=128)
            P.op("vector", lambda e: e.memset(TMo[:, :, 0:1], 0.0), reads=["l_CG"], writes=["l_TM"])
            P.op("vector", lambda e: e.tensor_tensor_scan(out=CL[:, own], data0=TM[:, own], data1=SP[:, own], initial=0.0, op0=ALU.mult, op1=ALU.add),
                 reads=["l_TM", "l_SP"], writes=["l_CL"])
            CLo = CL[:, own].rearrange("p (c t) -> p c t", t=128)
            SPo = SP[:, own].rearrange("p (c t) -> p c t", t=128)
            P.op("vector", lambda e: e.tensor_copy(CE[:], CLo[:, :, 127]), reads=["l_CL"], writes=["l_CE"])
            P.op("vector", lambda e: e.tensor_copy(tot[:, 0:1], CG[:, OWN0 + NOWN - 1:OWN0 + NOWN]), reads=["l_CG"], writes=["l_tot"])
            P.op("vector", lambda e: e.tensor_copy(tot[:, 1:2], CG[:, CTX0 + 255:CTX0 + 256]), reads=["l_CG"], writes=["l_tot"])
            P.op("vector", lambda e: e.tensor_scalar(ntot[:], tot[:], -1.0, None, ALU.mult), reads=["l_tot"], writes=["l_ntot"])
            for c in range(NCH):
                P.op("vector", lambda e, c=c: e.tensor_scalar(CLo[32:64, c, :], CLo[32:64, c, :], CE[32:64, c:c + 1], -1.0, ALU.subtract, ALU.mult),
                     reads=["l_CL", "l_CE"], writes=["l_CL"])
            P.op("vector", lambda e: e.tensor_tensor(out=CL[32:64, own], in0=CL[32:64, own], in1=SP[32:64, own], op=ALU.add),
                 reads=["l_CL", "l_SP"], writes=["l_CL"])
            for r in range(8):
                P.op("tensor", lambda e, r=r: e.matmul(pe[:, r, :], lhsT=SELM[:, r, :], rhs=CE[:], start=True, stop=True),
                     reads=["l_SELM", "l_CE"], writes=["l_pe"], inc=(r == 7))
            P.op("scalar", lambda e: e.activation(out=EBEND[:], in_=pe[:], func=AF.Exp, scale=-1.0), reads=["l_pe"], writes=["l_EBEND"])
            for r in range(8):
                P.op("tensor", lambda e, r=r: e.matmul(pa[:, r, :], lhsT=SELM[:, r, :], rhs=tot[:], start=True, stop=True),
                     reads=["l_SELM", "l_tot"], writes=["l_pa"], inc=(r == 7))
            P.op("scalar", lambda e: e.activation(out=ATOT[:], in_=pa[:, :, 0], func=AF.Exp, scale=-1.0), reads=["l_pa"], writes=["l_ATOT"])

            tcnt = {"n": 0}

            def transpose_out(Q, qn, qty, chunks):
                for g0 in range(0, len(chunks), 8):
                    grp = chunks[g0:g0 + 8]
                    a = tcnt["n"] % 2
                    tcnt["n"] += 1
                    for gi, (ci, col0) in enumerate(grp):
                        P.op("tensor", lambda e, a=a, gi=gi, col0=col0: e.transpose(ptr[a][:, gi, :], Q[0:64, col0:col0 + 128], G.ident_f[0:64, 0:64]),
                             reads=[qn, "ident_f"], writes=["l_ptr%d" % a], inc=(gi == len(grp) - 1))
                    c_first = grp[0][0]
                    ng_ = len(grp)
                    src = ptr[a][:, 0:ng_, :].rearrange("p g (d x) -> p g d x", d=2)[:, :, :, 0:4]
                    dst = TOK[:, c_first:c_first + ng_, qty, :].rearrange("p g (d x) -> p g d x", d=2)
                    P.op("vector", lambda e, src=src, dst=dst: e.tensor_copy(dst, src), reads=["l_ptr%d" % a], writes=["l_TOK"])

            own_chunks = [(c, OWN0 + c * 128) for c in range(NCH)]
            ctx_chunks = [(32 + c, CTX0 + c * 128) for c in range(2)]
            P.op("scalar", lambda e: e.activation(out=OQ[0][:, own], in_=CL[:, own], func=AF.Exp, scale=-1.0), reads=["l_CL"], writes=["l_OQ0"])
            transpose_out(OQ[0], "l_OQ0", 0, own_chunks)
            P.op("scalar", lambda e: e.activation(out=OQ[1][:, own], in_=CL[:, own], func=AF.Exp), reads=["l_CL"], writes=["l_OQ1"])
            transpose_out(OQ[1], "l_OQ1", 4, own_chunks)
            P.op("vector", lambda e: e.tensor_tensor(out=TM[:, own], in0=LI[:, own], in1=CL[:, own], op=ALU.add), reads=["l_LI", "l_CL"], writes=["l_TM"])
            P.op("scalar", lambda e: e.activation(out=OQ[1][:, own], in_=TM[:, own], func=AF.Exp), reads=["l_TM"], writes=["l_OQ1"])
            transpose_out(OQ[1], "l_OQ1", 1, own_chunks)
            TMo2 = TM[:, own].rearrange("p (c t) -> p c t", t=128)
            for c in range(NCH):
                P.op("vector", lambda e, c=c: e.tensor_scalar(TMo2[:, c, :], TMo2[:, c, :], CE[:, c:c + 1], None, ALU.subtract),
                     reads=["l_TM", "l_CE", "l_OQ1"], writes=["l_TM"])
            P.op("scalar", lambda e: e.activation(out=OQ[0][:, own], in_=TM[:, own], func=AF.Exp), reads=["l_TM"], writes=["l_OQ0"])
            transpose_out(OQ[0], "l_OQ0", 2, own_chunks)
            for (sl, ti_) in ((own, 0), (cxs, 1)):
                P.op("vector", lambda e, sl=sl: e.tensor_tensor(out=TM[0:32, sl], in0=LI[0:32, sl], in1=CG[0:32, sl], op=ALU.add),
                     reads=["l_LI", "l_CG"], writes=["l_TM"])
                P.op("vector", lambda e, sl=sl: e.tensor_tensor(out=TM[32:64, sl], in0=LI[32:64, sl], in1=CG[32:64, sl], op=ALU.subtract),
                     reads=["l_LI", "l_CG"], writes=["l_TM"])
                P.op("vector", lambda e, sl=sl: e.tensor_tensor(out=TM[32:64, sl], in0=TM[32:64, sl], in1=SP[32:64, sl], op=ALU.add),
                     reads=["l_TM", "l_SP"], writes=["l_TM"])
                P.op("scalar", lambda e, sl=sl, ti_=ti_: e.activation(out=OQ[1][0:32, sl], in_=TM[0:32, sl], func=AF.Exp, bias=ntot[0:32, ti_:ti_ + 1]),
                     reads=["l_TM", "l_ntot"], writes=["l_OQ1"])
                P.op("scalar", lambda e, sl=sl: e.activation(out=OQ[1][32:64, sl], in_=TM[32:64, sl], func=AF.Exp), reads=["l_TM"], writes=["l_OQ1"])
            transpose_out(OQ[1], "l_OQ1", 3, own_chunks + ctx_chunks)
            P.barrier()
            P.emit()

        KTOK = sb("l_KTOK", [128, 34, 512], BF16)
        VTOK = sb("l_VTOK", [128, 34, 516], BF16)
        TRI = sb("l_TRI", [128, 2, 128], F32)
        SELV = sb("l_SELV", [128, 16], F32)
        PAY = sb("l_PAY", [128, 8, 130], F32)
        GATH = sb("l_GATH", [128, 4, 1040], F32)
        STATE = sb("l_STATE", [128, 8, 129], F32)
        STB = sb("l_STB", [128, 8, 129], BF16)
        KA = [sb("l_KA%d" % b, [128, 128], BF16) for b in range(3)]
        alpha = sb("l_alpha", [128, 1], F32)
        tmpL = sb("l_tmpL", [128, 129], F32)
        QTc = [[sb("l_QTc%d%d" % (d_, b), [128, 4, 128], BF16) for b in range(2)] for d_ in range(2)]
        KTc = [[sb("l_KTc%d%d" % (d_, b), [128, 4, 128], BF16) for b in range(2)] for d_ in range(2)]
        HS = [[sb("l_HS%d%d" % (d_, b), [128, 512], F32) for b in range(2)] for d_ in range(2)]
        PTt = [sb("l_PT%d" % b, [128, 128], BF16) for b in range(2)]
        pS = [ps("l_pS%d" % b, [128, 128]) for b in range(2)]
        pU = [ps("l_pU%d" % b, [128, 132]) for b in range(4)]
        pN = [ps("l_pN%d" % b, [128, 132]) for b in range(2)]
        pL = [pU[0], pU[1]]
        den = [sb("l_den%d" % b, [128, 8], F32) for b in range(2)]
        P.dma("sync", lambda e: e.dma_start(out=KTOK[:, 0:32, :], in_=G.MLK[OWN0:OWN0 + NOWN, :].rearrange("(c p) n -> p c n", p=128)), reads=["dramML"], writes=["l_KTOK"])
        P.dma("sync", lambda e: e.dma_start(out=KTOK[:, 32:34, :], in_=G.MLK[CTX0:CTX0 + 256, :].rearrange("(c p) n -> p c n", p=128)), reads=["dramML"], writes=["l_KTOK"])
        P.dma("sync", lambda e: e.dma_start(out=VTOK[:, 0:32, :], in_=G.MLV[OWN0:OWN0 + NOWN, :].rearrange("(c p) n -> p c n", p=128)), reads=["dramML"], writes=["l_VTOK"])
        P.dma("sync", lambda e: e.dma_start(out=VTOK[:, 32:34, :], in_=G.MLV[CTX0:CTX0 + 256, :].rearrange("(c p) n -> p c n", p=128)), reads=["dramML"], writes=["l_VTOK"])
        P.dma("sync", lambda e: e.dma_start(out=TRI[:], in_=I["tri"][:, :, :]), writes=["l_TRI"])
        P.dma("sync", lambda e: e.dma_start(out=SELV[:], in_=I["selv"][:, :]), writes=["l_SELV"])
        kacnt = {"n": 0}

        def scaled_k(c, h, qty, r):
            a = kacnt["n"] % 3
            kacnt["n"] += 1
            eng = ("vector", "scalar", "scalar")[a]
            src = KTOK[:, c, h * 128:(h + 1) * 128]
            sc = TOK[:, c, qty, r:r + 1]
            if eng == "scalar":
                P.op("scalar", lambda e: e.activation(out=KA[a][:], in_=src, func=AF.Copy, scale=sc), reads=["l_KTOK", "l_TOK"], writes=["l_KA%d" % a])
            else:
                P.op(eng, lambda e: e.tensor_scalar(KA[a][:], src, sc, None, ALU.mult), reads=["l_KTOK", "l_TOK"], writes=["l_KA%d" % a])
            return KA[a], "l_KA%d" % a

        n2 = 0
        for r in range(8):
            h = r % 4
            for (chs, dstname) in ((list(range(32)), "own"), ([32, 33], "ctx")):
                pp, ppn = pL[n2 % 2], "l_pU%d" % (n2 % 2)
                n2 += 1
                for i_, c in enumerate(chs):
                    ka, kan = scaled_k(c, h, 3, r)
                    P.op("tensor", lambda e, pp=pp, ka=ka, c=c, h=h, i_=i_, L=len(chs): e.matmul(
                        pp[:, 0:129], lhsT=ka[:], rhs=VTOK[:, c, h * 129:(h + 1) * 129], start=(i_ == 0), stop=(i_ == L - 1)),
                        reads=[kan, "l_VTOK"], writes=[ppn], inc=True)
                if dstname == "own":
                    P.op("vector", lambda e, pp=pp, r=r: e.tensor_copy(PAY[:, r, 0:129], pp[:, 0:129]), reads=[ppn], writes=["l_PAY"])
                else:
                    P.op("vector", lambda e, pp=pp, r=r: e.tensor_copy(STATE[:, r, :], pp[:, 0:129]), reads=[ppn], writes=["l_STATE"])
        P.op("vector", lambda e: e.tensor_copy(PAY[:, :, 129], ATOT[:]), reads=["l_ATOT", "l_PAY"], writes=["l_PAY"])
        P.dma("sync", lambda e: e.dma_start(out=G.CCI[:, :], in_=PAY[:].rearrange("p r n -> p (r n)")), reads=["l_PAY"], writes=["CCI"])
        for _rep in range(3):
            P.cc(lambda e: e.collective_compute("AllGather", ALU.bypass, replica_groups=[[0, 1, 2, 3], [4, 5, 6, 7]],
                                                ins=[G.CCI.opt()], outs=[G.CCO.opt()]), reads=["CCI"], writes=["CCO"])
        P.dma("sync", lambda e: e.dma_start(out=GATH[:], in_=G.CCO.rearrange("(j p) n -> p j n", p=128)), reads=["CCO"], writes=["l_GATH"])
        for r in range(8):
            d = r // 4
            order = range(4) if d == 0 else range(3, -1, -1)
            for jseg in order:
                so = 0 if d == 0 else 8
                A_j = GATH[:, jseg, r * 130 + 129:r * 130 + 130]
                L_j = GATH[:, jseg, r * 130:r * 130 + 129]
                P.op("vector", lambda e, A_j=A_j, so=so, jseg=jseg: e.tensor_scalar(
                    alpha[:], A_j, SELV[:, so + jseg:so + jseg + 1], SELV[:, so + 4 + jseg:so + 5 + jseg], ALU.mult, ALU.add),
                    reads=["l_GATH", "l_SELV"], writes=["l_alpha"])
                P.op("vector", lambda e, L_j=L_j, so=so, jseg=jseg: e.tensor_scalar(tmpL[:], L_j, SELV[:, so + jseg:so + jseg + 1], None, ALU.mult),
                     reads=["l_GATH", "l_SELV"], writes=["l_tmpL"])
                P.op("vector", lambda e, r=r: e.scalar_tensor_tensor(out=STATE[:, r, :], in0=STATE[:, r, :], scalar=alpha[:], in1=tmpL[:], op0=ALU.mult, op1=ALU.add),
                     reads=["l_STATE", "l_alpha", "l_tmpL"], writes=["l_STATE"])
        P.op("scalar", lambda e: e.activation(out=STB[:], in_=STATE[:], func=AF.Copy), reads=["l_STATE"], writes=["l_STB"])

        def ld4(i):
            if i >= NCH:
                return
            bb = i % 2
            for d_ in range(2):
                c_ = i if d_ == 0 else NCH - 1 - i
                tk0 = OWN0 + c_ * 128
                P.dma("sync", lambda e, bb=bb, d_=d_, tk0=tk0: e.dma_start(out=QTc[d_][bb][:], in_=G.MLQT[:, :, tk0:tk0 + 128]), writes=["l_QTc%d%d" % (d_, bb)])
                P.dma("sync", lambda e, bb=bb, d_=d_, tk0=tk0: e.dma_start(out=KTc[d_][bb][:], in_=G.MLKT[:, :, tk0:tk0 + 128]), writes=["l_KTc%d%d" % (d_, bb)])
        ld4(0)
        for i in range(NCH):
            b = i % 2
            ld4(i + 1)
            cs = (i, NCH - 1 - i)
            for hh_ in range(2):
                items = [(h, d) for h in (2 * hh_, 2 * hh_ + 1) for d in range(2)]
                def front(i2, h, d):
                    b2 = i2 % 2
                    c = (i2, NCH - 1 - i2)[d]
                    r = d * 4 + h
                    qn, kn = "l_QTc%d%d" % (d, b2), "l_KTc%d%d" % (d, b2)
                    P.op("tensor", lambda e, h=h, d=d, b2=b2: e.matmul(pS[d][:], lhsT=KTc[d][b2][:, h, :], rhs=QTc[d][b2][:, h, :], start=True, stop=True),
                         reads=[kn, qn], writes=["l_pS%d" % d], inc=True)
                    P.op("vector", lambda e, c=c, r=r, d=d: e.scalar_tensor_tensor(
                        out=PTt[d][:], in0=pS[d][:], scalar=TOK[:, c, 1, r:r + 1], in1=TRI[:, d, :], op0=ALU.mult, op1=ALU.mult),
                        reads=["l_pS%d" % d, "l_TOK", "l_TRI"], writes=["l_PT%d" % d])

                def back(h, d):
                    c = cs[d]
                    r = d * 4 + h
                    u = (h % 2) * 2 + d
                    qn = "l_QTc%d%d" % (d, b)
                    P.op("tensor", lambda e, c=c, h=h, d=d, u=u: e.matmul(pU[u][:, 0:129], lhsT=PTt[d][:], rhs=VTOK[:, c, h * 129:(h + 1) * 129], start=True, stop=False),
                         reads=["l_PT%d" % d, "l_VTOK"], writes=["l_pU%d" % u], inc=False)
                    P.op("tensor", lambda e, h=h, d=d, r=r, u=u, b=b: e.matmul(pU[u][:, 0:129], lhsT=QTc[d][b][:, h, :], rhs=STB[:, r, :], start=False, stop=True),
                         reads=[qn, "l_STB%d" % r, "l_STB"], writes=["l_pU%d" % u], inc=True)
                if i == 0 and hh_ == 0:
                    front(i, *items[0])
                for k_ in range(4):
                    if k_ + 1 < 4:
                        front(i, *items[k_ + 1])
                    elif hh_ == 0:
                        front(i, 2, 0)
                    elif i + 1 < NCH:
                        front(i + 1, 0, 0)
                    back(*items[k_])
                for (h, d) in items:
                    c = cs[d]
                    r = d * 4 + h
                    a = r % 2
                    ka, kan = scaled_k(c, h, 2, r)
                    P.op("tensor", lambda e, a=a, ka=ka, c=c, h=h: e.matmul(pN[a][:, 0:129], lhsT=ka[:], rhs=VTOK[:, c, h * 129:(h + 1) * 129], start=True, stop=True),
                         reads=[kan, "l_VTOK"], writes=["l_pN%d" % a], inc=True)
                    P.op("vector", lambda e, a=a, r=r, c=c: e.scalar_tensor_tensor(
                        out=STATE[:, r, :], in0=STATE[:, r, :], scalar=EBEND[:, r, c:c + 1], in1=pN[a][:, 0:129], op0=ALU.mult, op1=ALU.add),
                        reads=["l_STATE%d" % r, "l_EBEND", "l_pN%d" % a, "l_STATE"], writes=["l_STATE%d" % r])
                    P.op("scalar", lambda e, r=r: e.activation(out=STB[:, r, :], in_=STATE[:, r, :], func=AF.Copy),
                         reads=["l_STATE%d" % r], writes=["l_STB%d" % r])
                for (h, d) in items:
                    u = (h % 2) * 2 + d
                    dn, dnn = den[d], "l_den%d" % d
                    P.op("scalar", lambda e, dn=dn, u=u, h=h: e.activation(out=dn[:, h:h + 1], in_=pU[u][:, 128:129], func=AF.Abs),
                         reads=["l_pU%d" % u], writes=[dnn])
                for d in range(2):
                    c = cs[d]
                    dn, dnn = den[d], "l_den%d" % d
                    h0 = 2 * hh_
                    REB2 = TOK[:, c, 4, d * 4 + h0:d * 4 + h0 + 2]
                    P.op("vector", lambda e, dn=dn, REB2=REB2, h0=h0: e.tensor_tensor(out=dn[:, h0:h0 + 2], in0=dn[:, h0:h0 + 2], in1=REB2, op=ALU.max),
                         reads=[dnn, "l_TOK"], writes=[dnn])
                    P.op("vector", lambda e, dn=dn, h0=h0: e.reciprocal(dn[:, h0:h0 + 2], dn[:, h0:h0 + 2]), reads=[dnn], writes=[dnn])
                for (h, d) in items:
                    u = (h % 2) * 2 + d
                    dn, dnn = den[d], "l_den%d" % d
                    hsn = "l_HS%d%d" % (d, b)
                    if d == 0:
                        P.op("scalar", lambda e, b=b, h=h, dn=dn, u=u: e.activation(out=HS[0][b][:, h * 128:(h + 1) * 128], in_=pU[u][:, 0:128], func=AF.Copy, scale=dn[:, h:h + 1]),
                             reads=["l_pU%d" % u, dnn], writes=[hsn])
                    else:
                        P.op("vector", lambda e, b=b, h=h, dn=dn, u=u: e.tensor_scalar(HS[1][b][:, h * 128:(h + 1) * 128], pU[u][:, 0:128], dn[:, h:h + 1], None, ALU.mult),
                             reads=["l_pU%d" % u, dnn], writes=[hsn])
            P.dma("sync", lambda e, b=b, c=cs[0]: e.dma_start(out=G.HF[c * 128:(c + 1) * 128, :], in_=HS[0][b][:]), reads=["l_HS0%d" % b], writes=[_u()])
            P.dma("sync", lambda e, b=b, c=cs[1]: e.dma_start(out=G.HB[c * 128:(c + 1) * 128, :], in_=HS[1][b][:]), reads=["l_HS1%d" % b], writes=[_u()])
        P.barrier()
        P.emit()

    with contextlib.ExitStack() as st:
        sb, ps = _alloc(G, st)
        NB = 3
        hf = [sb("r_hf%d" % b, [128, 512], F32) for b in range(NB)]
        hb = [sb("r_hb%d" % b, [128, 512], F32) for b in range(NB)]
        g2 = [sb("r_g2%d" % b, [128, 512], F32) for b in range(NB)]
        junk = sb("r_junk", [128, 128], F32)
        ssq = [sb("r_ssq%d" % b, [128, 4], F32) for b in range(2)]
        Yb = [sb("r_Y%d" % b, [128, 512], BF16) for b in range(2)]
        ytb = [sb("r_yt%d" % b, [128, 4, 128], BF16) for b in range(2)]
        ptr2 = [ps("r_ptr%d" % b, [128, 4, 128], BF16) for b in range(2)]

        def ld5(c):
            if c >= NCH:
                return
            b3 = c % NB
            tk0 = OWN0 + c * 128
            P.dma("sync", lambda e, b3=b3, c=c: e.dma_start(out=hf[b3][:], in_=G.HF[c * 128:(c + 1) * 128, :]), writes=["r_hf%d" % b3])
            P.dma("sync", lambda e, b3=b3, c=c: e.dma_start(out=hb[b3][:], in_=G.HB[c * 128:(c + 1) * 128, :]), writes=["r_hb%d" % b3])
            P.dma("sync", lambda e, b3=b3, tk0=tk0: e.dma_start(out=g2[b3][:], in_=G.MLG2[tk0:tk0 + 128, :]), writes=["r_g2%d" % b3])
        ld5(0)
        ld5(1)
        for c in range(NCH):
            ld5(c + 2)
            b3, b = c % NB, c % 2
            P.op("vector", lambda e, b3=b3: e.tensor_tensor(out=hf[b3][:], in0=hf[b3][:], in1=hb[b3][:], op=ALU.add),
                 reads=["r_hf%d" % b3, "r_hb%d" % b3], writes=["r_hf%d" % b3])
            for h in range(4):
                P.op("scalar", lambda e, b3=b3, b=b, h=h: e.activation(out=junk[:], in_=hf[b3][:, h * 128:(h + 1) * 128], func=AF.Square, accum_out=ssq[b][:, h:h + 1]),
                     reads=["r_hf%d" % b3], writes=["r_junk", "r_ssq%d" % b])
            P.op("scalar", lambda e, b=b: e.activation(out=ssq[b][:], in_=ssq[b][:], func=AF.Sqrt, bias=G.c_eps[:], scale=1.0 / 128.0),
                 reads=["r_ssq%d" % b, "c_eps"], writes=["r_ssq%d" % b])
            P.op("vector", lambda e, b=b: e.reciprocal(ssq[b][:], ssq[b][:]), reads=["r_ssq%d" % b], writes=["r_ssq%d" % b])
            for h in range(4):
                P.op("vector", lambda e, b3=b3, b=b, h=h: e.scalar_tensor_tensor(
                    out=Yb[b][:, h * 128:(h + 1) * 128], in0=hf[b3][:, h * 128:(h + 1) * 128], scalar=ssq[b][:, h:h + 1],
                    in1=g2[b3][:, h * 128:(h + 1) * 128], op0=ALU.mult, op1=ALU.mult),
                    reads=["r_hf%d" % b3, "r_ssq%d" % b, "r_g2%d" % b3], writes=["r_Y%d" % b])
            for h in range(4):
                P.op("tensor", lambda e, b=b, h=h: e.transpose(ptr2[b][:, h, :], Yb[b][:, h * 128:(h + 1) * 128], G.ident_b[:]),
                     reads=["r_Y%d" % b, "ident_b"], writes=["r_ptr%d" % b], inc=(h == 3))
            P.op("scalar", lambda e, b=b: e.activation(out=ytb[b][:], in_=ptr2[b][:], func=AF.Copy), reads=["r_ptr%d" % b], writes=["r_yt%d" % b])
            P.dma("sync", lambda e, b=b, c=c: e.dma_start(out=G.YT[:, 4:8, c * 128:(c + 1) * 128], in_=ytb[b][:]), reads=["r_yt%d" % b], writes=[_u()])
        P.barrier()
        P.emit()


def stage_outproj(G):
    P, I, nc = G.P, G.I, G.nc
    l, j, m = 0, 1, 0
    with contextlib.ExitStack() as st:
        sb, ps = _alloc(G, st)
        wo = sb("o_wo", [128, 8, 1024], BF16)
        xt = [sb("o_xt%d" % b, [128, 8, TW], F32) for b in range(2)]
        yt = [sb("o_yt%d" % b, [128, 8, TW], BF16) for b in range(2)]
        po = [ps("o_po%d" % b, [128, TW]) for b in range(2)]
        load_w_cast(G, wo, "o_wo", I["mix_w_out"], 8, 1024)
        def ld(n):
            if n >= 16:
                return
            ti_, b_ = n + 1, n % 2
            P.dma("sync", lambda e, b_=b_, ti_=ti_: e.dma_start(out=xt[b_][:], in_=G.XT[:, :, ti_ * TW:(ti_ + 1) * TW]), reads=["XT%d" % ti_], writes=["o_xt%d" % b_])
            P.dma("sync", lambda e, b_=b_, n=n: e.dma_start(out=yt[b_][:], in_=G.YT[:, :, n * TW:(n + 1) * TW]), reads=["dramYT"], writes=["o_yt%d" % b_])
        ld(0)
        ld(1)
        for n in range(16):
            ti = n + 1
            t0 = ti * TW
            b = n % 2
            for f in range(8):
                q = f % 2
                for k in range(8):
                    P.op("tensor", lambda e, q=q, f=f, k=k, b=b: e.matmul(po[q][:], lhsT=wo[:, k, f * 128:(f + 1) * 128], rhs=yt[b][:, k, :], start=(k == 0), stop=(k == 7)),
                         reads=["o_wo", "o_yt%d" % b], writes=["o_po%d" % q], inc=(k == 7))
                gsc = G.GT[:, l, j, m, f:f + 1]
                P.op("vector", lambda e, q=q, f=f, b=b, gsc=gsc: e.scalar_tensor_tensor(
                    out=xt[b][:, f, :], in0=po[q][:], scalar=gsc, in1=xt[b][:, f, :], op0=ALU.mult, op1=ALU.add),
                    reads=["o_po%d" % q, "modc", "o_xt%d" % b], writes=["o_xt%d" % b])
            P.dma("sync", lambda e, b=b, t0=t0: e.dma_start(out=G.XT[:, :, t0:t0 + TW], in_=xt[b][:]), reads=["o_xt%d" % b], writes=["XT%d" % ti])
            ld(n + 2)
        P.barrier()
        P.emit()


def stage_sg(G):
    P, I, nc = G.P, G.I, G.nc
    l, j, m = 1, 1, 0
    with contextlib.ExitStack() as st:
        sb, ps = _alloc(G, st)
        wu = sb("g_wu", [128, 8, 2048], BF16)
        wv = sb("g_wv", [128, 8, 2048], BF16)
        wo = sb("g_wo", [128, 16, 1024], BF16)
        wsT = sb("g_wsT", [128, 8, 128], BF16)
        bs = sb("g_bs", [1, 1024], BF16)
        ones1 = sb("g_ones1", [1, 256], BF16)
        lng = sb("g_lng", [128, 2048], F32)
        lnb = sb("g_lnb", [128, 2048], F32)
        xt = [sb("g_xt%d" % b, [128, 8, TW], F32) for b in range(2)]
        hh = [sb("g_h%d" % b, [128, 8, TW], BF16) for b in range(2)]
        NS = NormScratch(G, sb, ps, "g_")
        NS.sq8 = [sb("g_sq8_%d" % k, [128, TW], BF16) for k in range(8)]
        uT = sb("g_uT", [128, 16, TW], BF16)
        vraw = [sb("g_vraw%d" % b, [128, 2048], F32) for b in range(2)]
        vn = [sb("g_vn%d" % b, [128, 2048], BF16) for b in range(2)]
        gated = sb("g_gated", [128, 16, TW], BF16)
        stats = [sb("g_stats%d" % b, [128, 4, 6], F32) for b in range(2)]
        mv = [sb("g_mv%d" % b, [128, 4], F32) for b in range(2)]
        pu = [ps("g_pu%d" % b, [128, TW]) for b in range(2)]
        pm = [ps("g_pm%d" % b, [128, 4, 128]) for b in range(2)]
        po1 = ps("g_po", [128, TW])
        po = [po1, po1]
        pv = [ps("g_pv%d" % b, [128, 512]) for b in range(2)]
        load_w_cast(G, wu, "g_wu", I["sg_w_in"][:, 0:2048], 8, 2048)
        load_w_cast(G, wv, "g_wv", I["sg_w_in"][:, 2048:4096], 8, 2048)
        load_w_cast(G, wo, "g_wo", I["sg_w_out"], 16, 1024)
        P.dma("gpsimd", lambda e: e.dma_start(out=wsT[:], in_=I["sg_w_sT"][:, :, :]), writes=["g_wsT"])
        P.dma("gpsimd", lambda e: e.dma_start(out=bs[:], in_=I["sg_b_s"][:, :]), writes=["g_bs"])
        P.op("vector", lambda e: e.memset(ones1[:], 1.0), writes=["g_ones1"])
        P.dma("sync", lambda e: e.dma_start(out=lng[:], in_=I["sg_lng_bc"][:, :]), writes=["g_lng"])
        P.dma("sync", lambda e: e.dma_start(out=lnb[:], in_=I["sg_lnb_bc"][:, :]), writes=["g_lnb"])
        tiles = list(range(1, 17))

        def ld(n):
            ti_ = tiles[n]
            xb = xt[n % 2]
            P.dma("sync", lambda e, xb=xb, ti_=ti_: e.dma_start(out=xb[:], in_=G.XT[:, :, ti_ * TW:(ti_ + 1) * TW]),
                  reads=["XT%d" % ti_], writes=["g_xt%d" % (n % 2)])
        def do_norm(n, phase=None):
            bb = n % 2
            norm_mod(G, NS, xt[bb], "g_xt%d" % bb, G.GM[:, l, j, m, :], G.SH[:, l, j, m, :], hh[bb], "g_h%d" % bb, phase=phase)
        ld(0)
        ld(1)
        do_norm(0)
        vcnt = 0
        for n, ti in enumerate(tiles):
            t0 = ti * TW
            b = n % 2
            x, xn = xt[b], "g_xt%d" % b
            h, hn = hh[b], "g_h%d" % b
            NSB = TW // 128
            for s_ in range(NSB):
                VR = vraw[s_]
                for blk in range(4):
                    q = blk % 2
                    for k in range(8):
                        P.op("tensor", lambda e, q=q, blk=blk, k=k, h=h, s_=s_: e.matmul(
                            pv[q][:], lhsT=h[:, k, s_ * 128:(s_ + 1) * 128], rhs=wv[:, k, blk * 512:(blk + 1) * 512], start=(k == 0), stop=(k == 7)),
                            reads=["g_wv", hn], writes=["g_pv%d" % q], inc=(k == 7))
                    P.op("scalar", lambda e, q=q, blk=blk, VR=VR: e.activation(out=VR[:, blk * 512:(blk + 1) * 512], in_=pv[q][:], func=AF.Gelu_apprx_tanh),
                         reads=["g_pv%d" % q], writes=["g_vraw%d" % s_])
                    P.op("vector", lambda e, blk=blk, VR=VR, s_=s_: e.bn_stats(stats[s_][:, blk, :], VR[:, blk * 512:(blk + 1) * 512]),
                         reads=["g_vraw%d" % s_], writes=["g_stats%d" % s_])
                P.op("vector", lambda e, s_=s_: e.bn_aggr(mv[s_][:, 0:2], stats[s_][:].rearrange("p a b -> p (a b)")), reads=["g_stats%d" % s_], writes=["g_mv%d" % s_])
                P.op("scalar", lambda e, s_=s_: e.activation(out=mv[s_][:, 2:3], in_=mv[s_][:, 1:2], func=AF.Sqrt, bias=G.c_eps[:], scale=1.0),
                     reads=["g_mv%d" % s_, "c_eps"], writes=["g_mv%d" % s_])
                P.op("vector", lambda e, s_=s_: e.reciprocal(mv[s_][:, 2:3], mv[s_][:, 2:3]), reads=["g_mv%d" % s_], writes=["g_mv%d" % s_])
                P.op("vector", lambda e, s_=s_: e.tensor_scalar(mv[s_][:, 3:4], mv[s_][:, 0:1], mv[s_][:, 2:3], -1.0, ALU.mult, ALU.mult),
                     reads=["g_mv%d" % s_], writes=["g_mv%d" % s_])
            for fc in range(16):
                q = fc % 2
                if fc == 6:
                    vns = []
                    for s_ in range(NSB):
                        VR = vraw[s_]
                        VN, vnn = vn[vcnt % 2], "g_vn%d" % (vcnt % 2)
                        vcnt += 1
                        vns.append((VN, vnn))
                        P.op("scalar", lambda e, VR=VR, s_=s_: e.activation(out=VR[:], in_=VR[:], func=AF.Identity, bias=mv[s_][:, 3:4], scale=mv[s_][:, 2:3]),
                             reads=["g_vraw%d" % s_, "g_mv%d" % s_], writes=["g_vraw%d" % s_])
                        P.op("vector", lambda e, VR=VR: e.tensor_tensor(out=VR[:], in0=VR[:], in1=lng[:], op=ALU.mult),
                             reads=["g_vraw%d" % s_, "g_lng"], writes=["g_vraw%d" % s_])
                        P.op("vector", lambda e, VR=VR, VN=VN: e.tensor_tensor(out=VN[:], in0=VR[:], in1=lnb[:], op=ALU.add),
                             reads=["g_vraw%d" % s_, "g_lnb"], writes=[vnn])
                for k in range(8):
                    P.op("tensor", lambda e, q=q, fc=fc, k=k, h=h: e.matmul(pu[q][:], lhsT=wu[:, k, fc * 128:(fc + 1) * 128], rhs=h[:, k, :], start=(k == 0), stop=(k == 7)),
                         reads=["g_wu", hn], writes=["g_pu%d" % q], inc=(k == 7))
                P.op("scalar", lambda e, q=q, fc=fc: e.activation(out=uT[:, fc, :], in_=pu[q][:], func=AF.Gelu_apprx_tanh), reads=["g_pu%d" % q], writes=["g_uT%d" % fc])
            for s_ in range(NSB):
                VN, vnn = vns[s_]
                for g4 in range(4):
                    q = g4 % 2
                    for f4 in range(4):
                        fc = g4 * 4 + f4
                        g_ = fc // 2
                        P.op("tensor", lambda e, q=q, fc=fc, f4=f4, g_=g_, VN=VN: e.matmul(
                            pm[q][:, f4, :], lhsT=VN[:, fc * 128:(fc + 1) * 128], rhs=wsT[:, g_, :], start=True, stop=False),
                            reads=[vnn, "g_wsT"], writes=["g_pm%d" % q], inc=False)
                        P.op("tensor", lambda e, q=q, f4=f4, g_=g_: e.matmul(
                            pm[q][:, f4, :], lhsT=ones1[0:1, 0:128], rhs=bs[0:1, g_ * 128:(g_ + 1) * 128], start=False, stop=True),
                            reads=["g_ones1", "g_bs"], writes=["g_pm%d" % q], inc=(f4 == 3))
                    P.op("vector", lambda e, q=q, g4=g4, s_=s_: e.tensor_tensor(
                        out=gated[:, g4 * 4:(g4 + 1) * 4, s_ * 128:(s_ + 1) * 128], in0=uT[:, g4 * 4:(g4 + 1) * 4, s_ * 128:(s_ + 1) * 128], in1=pm[q][:], op=ALU.mult),
                        reads=["g_uT%d" % fc_ for fc_ in range(g4 * 4, g4 * 4 + 4)] + ["g_pm%d" % q],
                        writes=["g_gated%d" % fc_ for fc_ in range(g4 * 4, g4 * 4 + 4)])
            if n + 1 < len(tiles):
                do_norm(n + 1, "A")
            for f in range(8):
                q = f % 2
                if f == 3 and n + 1 < len(tiles):
                    do_norm(n + 1, "B")
                for fc in range(16):
                    P.op("tensor", lambda e, q=q, f=f, fc=fc: e.matmul(po[q][:], lhsT=wo[:, fc, f * 128:(f + 1) * 128], rhs=gated[:, fc, :], start=(fc == 0), stop=(fc == 15)),
                         reads=["g_wo", "g_gated%d" % fc], writes=["g_po"], inc=(fc == 15))
                gsc = G.GT[:, l, j, m, f:f + 1]
                P.op("vector", lambda e, q=q, f=f, x=x, gsc=gsc: e.scalar_tensor_tensor(
                    out=x[:, f, :], in0=po[q][:], scalar=gsc, in1=x[:, f, :], op0=ALU.mult, op1=ALU.add),
                    reads=["g_po", "modc", xn], writes=[xn])
            P.dma("sync", lambda e, x=x, t0=t0: e.dma_start(out=G.XT[:, :, t0:t0 + TW], in_=x[:]), reads=[xn], writes=["XT%d" % ti])
            if n + 2 < len(tiles):
                ld(n + 2)
        P.barrier()
        P.emit()


_NC_CACHE = {}


def kernel(**inputs):
    maps = _host_prep(inputs)
    if "nc" not in _NC_CACHE:
        _NC_CACHE["nc"] = build()
    nc = _NC_CACHE["nc"]
    res = run_bass_kernel_spmd(nc, maps, core_ids=list(range(8)))
    out = np.zeros((2, 16384, 1024), np.float32)
    for core in range(8):
        b, s = core // 4, core % 4
        out[b, s * 4096:(s + 1) * 4096, :] = res.results[core]["out"]
    return out
```

```python
import contextlib
import numpy as np
import concourse.bass as bass
import concourse.mybir as mybir
from concourse.bass_utils import run_bass_kernel_spmd

F32 = mybir.dt.float32
BF16 = mybir.dt.bfloat16
AF = mybir.ActivationFunctionType
ALU = mybir.AluOpType
AX = mybir.AxisListType

D = 1024
DFF = 2816
NE = 4608
NU = 4864
OWN0 = 256
NOWN = 4096
CTX0 = 4608
TW = 256
EPS = 1e-6
NEG = -30000.0

ENGS = ("tensor", "vector", "scalar", "gpsimd", "sync")
NDMASEM = 32
NHW = 24


class Buf:
    __slots__ = ("name", "writers", "readers")

    def __init__(self, name):
        self.name = name
        self.writers = []
        self.readers = []


class Prog:
    def __init__(self, nc, st):
        self.nc = nc
        self.q = {e: [] for e in ENGS}
        self.cnt = {e: 0 for e in ENGS}
        self.seen = {e: {} for e in ENGS}
        self.dcnt = [0] * NDMASEM
        self.dnext = 0
        self.dnext_sw = 0
        self.bufs = {}
        self.esem = {e: st.enter_context(nc.semaphore("s_" + e)) for e in ENGS}
        self.dsem = [st.enter_context(nc.semaphore("d%d" % i)) for i in range(NDMASEM)]
        self.lastinc = {e: True for e in ENGS}

    def buf(self, name):
        b = self.bufs.get(name)
        if b is None:
            b = self.bufs[name] = Buf(name)
        return b

    def _bl(self, lst):
        return [self.buf(b) if isinstance(b, str) else b for b in lst]

    def _deps(self, reads, writes):
        deps = {}
        for b in reads:
            for k, v in b.writers:
                if deps.get(k, 0) < v:
                    deps[k] = v
        for b in writes:
            for k, v in b.writers:
                if deps.get(k, 0) < v:
                    deps[k] = v
            for k, v in b.readers:
                if deps.get(k, 0) < v:
                    deps[k] = v
        return deps

    def _waits(self, eng, deps):
        seen = self.seen[eng]
        waits = []
        for k, v in deps.items():
            if k == "tensor" and eng == "tensor":
                continue
            if seen.get(k, 0) >= v:
                continue
            seen[k] = v
            waits.append((k, v))
        return waits

    def _record(self, ev, reads, writes):
        k = ev[0]
        for b in writes:
            b.writers = [ev]
            b.readers = []
        for b in reads:
            if b in writes:
                continue
            b.readers = [e for e in b.readers if e[0] != k] + [ev]

    def op(self, eng, fn, reads=(), writes=(), inc=True):
        reads = self._bl(reads)
        writes = self._bl(writes)
        waits = self._waits(eng, self._deps(reads, writes))
        if inc:
            self.cnt[eng] += 1
            ev = (eng, self.cnt[eng])
        else:
            ev = (eng, self.cnt[eng] + 1)
        self.lastinc[eng] = inc
        self.q[eng].append(("op", fn, waits, inc))
        self._record(ev, reads, writes)

    def dma(self, eng, fn, reads=(), writes=()):
        reads = self._bl(reads)
        writes = self._bl(writes)
        deps = self._deps(reads, writes)
        if eng == "gpsimd":
            i = NHW + self.dnext_sw
            self.dnext_sw = (self.dnext_sw + 1) % (NDMASEM - NHW)
        else:
            i = self.dnext
            self.dnext = (self.dnext + 1) % NHW
        key = ("d", i)
        if self.dcnt[i] > 0 and deps.get(key, 0) < self.dcnt[i]:
            deps[key] = self.dcnt[i]
        waits = self._waits(eng, deps)
        self.dcnt[i] += 16
        ev = (key, self.dcnt[i])
        self.q[eng].append(("dma", fn, waits, i))
        self._record(ev, reads, writes)

    def cc(self, fn, reads=(), writes=()):
        self.op("gpsimd", fn, reads=reads, writes=writes, inc=True)

    def barrier(self):
        for e in ENGS:
            assert self.lastinc[e], e
        deps = {e: self.cnt[e] for e in ENGS if self.cnt[e] > 0}
        for i in range(NDMASEM):
            if self.dcnt[i] > 0:
                deps[("d", i)] = self.dcnt[i]
        for e in ENGS:
            d = {k: v for k, v in deps.items() if k != e}
            waits = self._waits(e, d)
            self.q[e].append(("wait", None, waits, None))

    def emit(self):
        nc = self.nc
        esem, dsem = self.esem, self.dsem

        def semof(k):
            return esem[k] if isinstance(k, str) else dsem[k[1]]

        def run(engname):
            items = self.q[engname]

            def body(e):
                for kind, fn, waits, x in items:
                    for k, v in waits:
                        e.wait_ge(semof(k), v)
                    if kind == "op":
                        ins = fn(e)
                        if x:
                            ins.then_inc(esem[engname], 1)
                    elif kind == "dma":
                        fn(e).then_inc(dsem[x], 16)
            return body

        with nc.Block() as block:
            block.tensor(run("tensor"))
            block.vector(run("vector"))
            block.scalar(run("scalar"))
            block.gpsimd(run("gpsimd"))
            block.sync(run("sync"))
        self.q = {e: [] for e in ENGS}


def _rowmap(s):
    rm = np.zeros(72, np.int64)
    rm[4:68] = s * 64 + np.arange(64)
    rm[0:4] = (s * 64 - 4 + np.arange(4)) if s > 0 else np.array([5, 6, 7, 8])
    rm[68:72] = (s * 64 + 64 + np.arange(4)) if s < 3 else np.array([248, 249, 250, 251])
    return rm


def _na_bias_tables(rpb, s):
    rm = _rowmap(s)
    out = np.full((5, 8, 128, 576), NEG, np.float32)
    reps = [0, 1, 2, 30, 31]
    cols = np.arange(64)
    c0 = np.clip(cols - 8, 0, 64 - 16)
    for ci, p in enumerate(reps):
        for r in range(2):
            i = s * 64 + 2 * p + r
            r0 = min(max(i - 4, 0), 256 - 8)
            seen_rows = set()
            for j in range(9):
                krow = int(rm[2 * p + j])
                if krow < r0 or krow >= r0 + 8 or krow in seen_rows:
                    continue
                seen_rows.add(krow)
                rr = krow - i + 7
                for qc in range(64):
                    kc = np.arange(c0[qc], c0[qc] + 16)
                    out[ci][:, r * 64 + qc, j * 64 + kc] = rpb[:, rr, kc - qc + 15]
    return out


def _rope_tables(s):
    rm = _rowmap(s)
    row = np.repeat(rm, 64).astype(np.float32)
    col = np.tile(np.arange(64), 72).astype(np.float32)
    inv = (10000.0 ** (-np.arange(32, dtype=np.float32) / 32)).astype(np.float32)
    ang = np.concatenate([row[:, None] * inv, col[:, None] * inv], axis=-1).astype(np.float32)
    cos = np.cos(ang).astype(np.float32)
    sin = np.sin(ang).astype(np.float32)
    cos2 = np.repeat(cos, 2, axis=1)
    sin2 = np.repeat(sin, 2, axis=1)
    sin2[:, 0::2] *= -1.0
    cos_u = np.ones((NU, 128), np.float32)
    sin_u = np.zeros((NU, 128), np.float32)
    cos_u[:NE] = cos2
    sin_u[:NE] = sin2
    return np.tile(cos_u, (1, 4)), np.tile(sin_u, (1, 4))


def _host_prep(inp):
    x = np.asarray(inp["x"], np.float32)
    shared = {}
    shared["w_mod"] = np.ascontiguousarray(inp["w_mod"], np.float32)
    shared["b_modT"] = np.ascontiguousarray(np.asarray(inp["b_mod"], np.float32).reshape(2, 72, 128).transpose(2, 0, 1))
    shared["norm_gT"] = np.ascontiguousarray(np.asarray(inp["norm_g"], np.float32).reshape(2, 3, 8, 128).transpose(3, 0, 1, 2))
    shared["ffn_w_in"] = np.ascontiguousarray(inp["ffn_w_in"], np.float32)
    shared["ffn_w_out"] = np.ascontiguousarray(inp["ffn_w_out"], np.float32)
    mw = np.asarray(inp["mix_w_in"], np.float32)[0]
    shared["mix_w_in"] = np.ascontiguousarray(mw[:, :3584])
    wg = np.zeros((1024, 128), np.float32)
    gb = np.zeros((128, 1), np.float32)
    gate_b = np.asarray(inp["ml_gate_b"], np.float32)[0]
    for h in range(4):
        for d in range(2):
            for q in range(2):
                wg[:, q * 64 + d * 32 + h] = mw[:, 3584 + h * 4 + d * 2 + q]
                gb[q * 64 + d * 32 + h, 0] = gate_b[h, d, q]
    shared["w_gate"] = wg
    shared["gate_b"] = gb
    shared["headg_bc"] = np.ascontiguousarray(np.broadcast_to(np.asarray(inp["ml_head_g"], np.float32)[0].reshape(1, 512), (128, 512)))
    shared["mix_w_out"] = np.ascontiguousarray(np.asarray(inp["mix_w_out"], np.float32)[0])
    shared["sg_w_in"] = np.ascontiguousarray(np.asarray(inp["sg_w_in"], np.float32)[0])
    shared["sg_w_out"] = np.ascontiguousarray(np.asarray(inp["sg_w_out"], np.float32)[0])
    shared["sg_lng_bc"] = np.ascontiguousarray(np.broadcast_to(np.asarray(inp["sg_ln_g"], np.float32)[0].reshape(1, 2048), (128, 2048)))
    shared["sg_lnb_bc"] = np.ascontiguousarray(np.broadcast_to(np.asarray(inp["sg_ln_b"], np.float32)[0].reshape(1, 2048), (128, 2048)))
    shared["sg_w_sT"] = np.ascontiguousarray(np.asarray(inp["sg_w_s"], np.float32)[0].transpose(2, 0, 1))
    shared["sg_b_s"] = np.ascontiguousarray(np.asarray(inp["sg_b_s"], np.float32)[0].reshape(1, 1024))
    shared["final_g_bc"] = np.ascontiguousarray(np.broadcast_to(np.asarray(inp["final_g"], np.float32).reshape(1, 1024), (128, 1024)))
    shared["ident"] = np.eye(128, dtype=np.float32)
    tri = np.zeros((128, 2, 128), np.float32)
    ss, tt = np.meshgrid(np.arange(128), np.arange(128), indexing="ij")
    tri[:, 0, :] = (tt >= ss)
    tri[:, 1, :] = (tt <= ss)
    shared["tri"] = tri
    sel = np.zeros((64, 8, 128), np.float32)
    for d in range(2):
        for h in range(4):
            sel[d * 32 + h, d * 4 + h, :] = 1.0
    shared["selm"] = sel
    rpb = np.asarray(inp["na_rpb"], np.float32)[0]
    c = np.asarray(inp["c"], np.float32)
    cctx = np.asarray(inp["c_ctx"], np.float32)
    ctx = np.asarray(inp["ctx"], np.float32)
    maps = []
    for core in range(8):
        b, s = core // 4, core % 4
        rm = _rowmap(s)
        tok = (rm[:, None] * 64 + np.arange(64)[None, :]).reshape(-1)
        xin = np.concatenate([x[b][tok], ctx[b]], axis=0)
        cT = np.stack([c[b].reshape(8, 128).T, cctx.reshape(8, 128).T], axis=-1)
        cos4, sin4 = _rope_tables(s)
        selv = np.zeros((128, 16), np.float32)
        for j in range(4):
            selv[:, j] = 1.0 if j < s else 0.0
            selv[:, 4 + j] = 1.0 - selv[:, j]
            selv[:, 8 + j] = 1.0 if j > s else 0.0
            selv[:, 12 + j] = 1.0 - selv[:, 8 + j]
        m = dict(shared)
        m["xin"] = np.ascontiguousarray(xin)
        m["cT"] = np.ascontiguousarray(cT.astype(np.float32))
        m["ropecos"] = cos4
        m["ropesin"] = sin4
        nb = _na_bias_tables(rpb, s)
        nbp = np.full((5, 8, 128, 640), NEG, np.float32)
        nbp[..., :576] = nb
        m["nabiasT"] = np.ascontiguousarray(nbp.reshape(5, 8, 128, 5, 128).transpose(0, 1, 4, 3, 2))
        m["selv"] = selv
        maps.append(m)
    return maps


INPUT_SHAPES = {
    "xin": [NU, 1024], "cT": [128, 8, 2], "w_mod": [2, 1024, 9216], "b_modT": [128, 2, 72], "norm_gT": [128, 2, 3, 8],
    "ffn_w_in": [2, 2, 1024, 5632], "ffn_w_out": [2, 2, 2816, 1024], "mix_w_in": [1024, 3584], "w_gate": [1024, 128],
    "gate_b": [128, 1], "headg_bc": [128, 512], "mix_w_out": [1024, 1024], "sg_w_in": [1024, 4096], "sg_w_out": [2048, 1024],
    "sg_lng_bc": [128, 2048], "sg_lnb_bc": [128, 2048], "sg_w_sT": [128, 8, 128], "sg_b_s": [1, 1024], "final_g_bc": [128, 1024],
    "ident": [128, 128], "tri": [128, 2, 128], "selm": [64, 8, 128], "ropecos": [NU, 512], "ropesin": [NU, 512],
    "nabiasT": [5, 8, 128, 5, 128], "selv": [128, 16],
}


class Ctx:
    pass


_UC = [0]


def _u():
    _UC[0] += 1
    return "u%d" % _UC[0]


def build(stages="all", dbg=()):
    nc = bass.Bass("TRN2", target_bir_lowering=False)
    I = {k: nc.dram_tensor(k, shp, F32, kind="ExternalInput").ap() for k, shp in INPUT_SHAPES.items()}
    OUT = nc.dram_tensor("out", [NOWN, 1024], F32, kind="ExternalOutput").ap()

    def scratch(name, shape, dt):
        if name in dbg:
            return nc.dram_tensor(name, shape, dt, kind="ExternalOutput").ap()
        return nc.dram_tensor(name, shape, dt).ap()

    G = Ctx()
    G.nc, G.I, G.OUT = nc, I, OUT
    G.dbg = dbg
    if dbg:
        G.DBG = {k: nc.dram_tensor(k, shp, dt, kind="ExternalOutput").ap() for k, (shp, dt) in {
            "D_MOD": ([128, 3, 2, 3, 2, 8], F32), "D_X0": ([128, 8, TW], F32), "D_X1": ([128, 8, TW], F32),
            "D_H": ([128, 8, TW], BF16), "D_HID": ([128, 22, TW], BF16), "D_RSTD": ([128, TW], F32)}.items()}
    G.XT = scratch("XT", [128, 8, NU], F32)
    G.NAQT = scratch("NAQT", [128, 4, NU], BF16)
    G.NAKT = scratch("NAKT", [128, 4, NU], BF16)
    G.NAV = scratch("NAV", [NU, 520], BF16)
    G.MLQT = scratch("MLQT", [128, 4, NU], BF16)
    G.MLKT = scratch("MLKT", [128, 4, NU], BF16)
    G.MLK = scratch("MLK", [NU, 512], BF16)
    G.MLV = scratch("MLV", [NU, 516], BF16)
    G.MLG2 = scratch("MLG2", [NU, 512], F32)
    G.GIF = scratch("GIF", [128, NU], F32)
    G.YT = scratch("YT", [128, 8, NOWN], BF16)
    G.HF = scratch("HF", [NOWN, 512], F32)
    G.HB = scratch("HB", [NOWN, 512], F32)
    G.CCI = scratch("CCI", [128, 1040], F32)
    G.CCO = scratch("CCO", [512, 1040], F32)

    with contextlib.ExitStack() as gst:
        P = Prog(nc, gst)
        G.P = P

        def gsb(name, shape, dt):
            return gst.enter_context(nc.sbuf_tensor(name, shape, dt))
        G.ident_f = gsb("ident_f", [128, 128], F32)
        G.ident_b = gsb("ident_b", [128, 128], BF16)
        G.ident8 = gsb("ident8", [128, 128], BF16)
        G.ones_b = gsb("ones_b", [128, 128], BF16)
        G.ones_f = gsb("ones_f", [128, 128], F32)
        G.c_eps = gsb("c_eps", [128, 1], F32)
        G.c_one = gsb("c_one", [128, 1], F32)
        G.SH = gsb("SH", [128, 2, 3, 2, 8], F32)
        G.GM = gsb("GM", [128, 2, 3, 2, 8], F32)
        G.GT = gsb("GT", [128, 2, 3, 2, 8], F32)

        P.dma("sync", lambda e: e.dma_start(out=G.ident_f[:], in_=I["ident"][:, :]), writes=["ident_f"])
        P.op("vector", lambda e: e.tensor_copy(G.ident_b[:], G.ident_f[:]), reads=["ident_f"], writes=["ident_b"])
        P.op("vector", lambda e: e.tensor_scalar(G.ident8[:], G.ident_f[:], 8.0, None, ALU.mult), reads=["ident_f"], writes=["ident8"])
        P.op("vector", lambda e: e.memset(G.ones_b[:], 1.0), writes=["ones_b"])
        P.op("vector", lambda e: e.memset(G.ones_f[:], 1.0), writes=["ones_f"])
        P.op("vector", lambda e: e.memset(G.c_eps[:], EPS), writes=["c_eps"])
        P.op("vector", lambda e: e.memset(G.c_one[:], 1.0), writes=["c_one"])

        ext_tiles = [(ti, 0) for ti in range(18)] + [(18, 1)]
        own_tiles = [(ti, 0) for ti in range(1, 17)]
        S = stages
        stage_mod(G)
        stage_t0(G)
        if S in ("ffn_only",):
            stage_ffn(G, 0, 0, ext_tiles)
            stage_final(G)
        elif S == "ffn_dbg":
            stage_ffn(G, 0, 0, [(1, 0)])
        else:
            stage_ffn(G, 0, 0, ext_tiles)
            stage_inproj(G)
            stage_na(G)
            stage_ml(G)
            stage_outproj(G)
            stage_ffn(G, 0, 1, own_tiles)
            stage_ffn(G, 1, 0, own_tiles)
            stage_sg(G)
            stage_ffn(G, 1, 1, own_tiles)
            stage_final(G)
    return nc


_AC = [0]


def _alloc(G, st):
    nc = G.nc
    _AC[0] += 1
    sfx = "_%d" % _AC[0]

    def sb(name, shape, dt):
        return st.enter_context(nc.sbuf_tensor(name + sfx, shape, dt))

    def ps(name, shape, dt=F32):
        return st.enter_context(nc.psum_tensor(name + sfx, shape, dt))
    return sb, ps


def stage_mod(G):
    P, I, nc = G.P, G.I, G.nc
    with contextlib.ExitStack() as st:
        sb, ps = _alloc(G, st)
        cT = sb("m_cT", [128, 8, 2], F32)
        sil = sb("m_sil", [128, 8, 2], BF16)
        wm = [sb("m_wm%d" % i, [128, 8, 1024], BF16) for i in range(3)]
        modv = sb("m_modv", [128, 2, 72, 2], F32)
        bmod = sb("m_bmod", [128, 2, 72], F32)
        ng = sb("m_ng", [128, 2, 3, 8], F32)
        pp = [ps("m_ps%d" % i, [128, 8, 2]) for i in range(2)]
        xs = [sb("t_xs%d" % i, [128, 1024], F32) for i in range(3)]
        xo = [sb("t_xo%d" % i, [128, 8, 128], F32) for i in range(2)]
        pt = [ps("t_pt%d" % i, [128, 8, 128]) for i in range(2)]
        P.dma("sync", lambda e: e.dma_start(out=cT[:], in_=I["cT"][:, :, :]), writes=["m_cT"])
        P.dma("sync", lambda e: e.dma_start(out=bmod[:], in_=I["b_modT"][:, :, :]), writes=["m_bmod"])
        P.dma("sync", lambda e: e.dma_start(out=ng[:], in_=I["norm_gT"][:, :, :, :]), writes=["m_ng"])
        P.op("scalar", lambda e: e.activation(out=sil[:], in_=cT[:], func=AF.Silu), reads=["m_cT"], writes=["m_sil"])
        NSUB = NU // 128

        def t0_load(i):
            if i < NSUB:
                b3 = i % 3
                P.dma("sync", lambda e, b3=b3, i=i: e.dma_start(out=xs[b3][:], in_=I["xin"][i * 128:(i + 1) * 128, :]), writes=["t_xs%d" % b3])

        def t0_sub(i):
            t0 = i * 128
            b, b3 = i % 2, i % 3
            for k in range(8):
                P.op("tensor", lambda e, b=b, b3=b3, k=k: e.transpose(pt[b][:, k, :], xs[b3][:, k * 128:(k + 1) * 128], G.ident_f[:]),
                     reads=["t_xs%d" % b3, "ident_f"], writes=["t_pt%d" % b], inc=(k == 7))
            if b == 0:
                P.op("vector", lambda e, b=b: e.tensor_copy(xo[b][:], pt[b][:]), reads=["t_pt%d" % b], writes=["t_xo%d" % b])
            else:
                P.op("scalar", lambda e, b=b: e.activation(out=xo[b][:], in_=pt[b][:], func=AF.Identity), reads=["t_pt%d" % b], writes=["t_xo%d" % b])
            t0_load(i + 3)
            P.dma("sync", lambda e, b=b, t0=t0: e.dma_start(out=G.XT[:, :, t0:t0 + 128], in_=xo[b][:]),
                  reads=["t_xo%d" % b], writes=["XT%d" % (t0 // TW)])

        for i in range(3):
            t0_load(i)
        n = 0
        ti = 0
        for l in range(2):
            wsrc = I["w_mod"][l].rearrange("(k p) n -> p k n", p=128)
            for blk in range(9):
                w = wm[n % 3]
                wn = "m_wm%d" % (n % 3)
                pt_ = pp[n % 2]
                pn = "m_ps%d" % (n % 2)
                P.dma("gpsimd", lambda e, w=w, blk=blk, wsrc=wsrc: e.dma_start(out=w[:], in_=wsrc[:, :, blk * 1024:(blk + 1) * 1024]),
                      writes=[wn])
                for jj in range(8):
                    for k in range(8):
                        P.op("tensor", lambda e, w=w, pt_=pt_, jj=jj, k=k: e.matmul(
                            pt_[:, jj, :], lhsT=w[:, k, jj * 128:(jj + 1) * 128], rhs=sil[:, k, :], start=(k == 0), stop=(k == 7)),
                            reads=[wn, "m_sil"], writes=[pn], inc=(jj == 7 and k == 7))
                for m in range(2):
                    P.op("vector", lambda e, pt_=pt_, l=l, blk=blk, m=m: e.tensor_tensor(
                        out=modv[:, l, blk * 8:(blk + 1) * 8, m], in0=pt_[:, :, m], in1=bmod[:, l, blk * 8:(blk + 1) * 8], op=ALU.add),
                        reads=[pn, "m_bmod"], writes=["m_modv"])
                n += 1
                for _ in range(2):
                    if ti < NSUB:
                        t0_sub(ti)
                        ti += 1
        while ti < NSUB:
            t0_sub(ti)
            ti += 1
        for l in range(2):
            for j in range(3):
                for m in range(2):
                    sh = modv[:, l, (3 * j) * 8:(3 * j) * 8 + 8, m]
                    sc = modv[:, l, (3 * j + 1) * 8:(3 * j + 1) * 8 + 8, m]
                    gt = modv[:, l, (3 * j + 2) * 8:(3 * j + 2) * 8 + 8, m]
                    P.op("vector", lambda e, sh=sh, l=l, j=j, m=m: e.tensor_copy(G.SH[:, l, j, m, :], sh), reads=["m_modv"], writes=["modc"])
                    P.op("vector", lambda e, sc=sc, l=l, j=j, m=m: e.scalar_tensor_tensor(
                        out=G.GM[:, l, j, m, :], in0=sc, scalar=1.0, in1=ng[:, l, j, :], op0=ALU.add, op1=ALU.mult),
                        reads=["m_modv", "m_ng"], writes=["modc"])
                    P.op("vector", lambda e, gt=gt, l=l, j=j, m=m: e.tensor_scalar(
                        G.GT[:, l, j, m, :], gt, (1.0 if j == 1 else 0.5), None, ALU.mult), reads=["m_modv"], writes=["modc"])
        P.barrier()
        P.emit()


def stage_t0(G):
    return


class NormScratch:
    def __init__(self, G, sb, ps, pfx, W=TW):
        self.sq = [sb(pfx + "sq%d" % i, [128, W], BF16) for i in range(2)]
        self.rs = sb(pfx + "rs", [128, W], F32)
        self.rstd = sb(pfx + "rstd", [128, W], F32)
        self.tmp = [sb(pfx + "tmp%d" % i, [128, W], F32) for i in range(2)]
        self.pn = ps(pfx + "pn", [128, W])
        self.pfx = pfx


def norm_mod(G, NS, x, xname, gm, sh, h, hname, W=TW, phase=None):
    P = G.P
    pfx = NS.pfx
    for k in range(8):
        if phase is None:
            sq, sqn = NS.sq[k % 2], pfx + "sq%d" % (k % 2)
        else:
            sq, sqn = NS.sq8[k], pfx + "sq8_%d" % k
        if phase in (None, "A"):
            P.op("scalar", lambda e, sq=sq, k=k: e.activation(out=sq[:, :W], in_=x[:, k, :], func=AF.Square), reads=[xname], writes=[sqn])
        if phase in (None, "B"):
            P.op("tensor", lambda e, sq=sq, k=k: e.matmul(NS.pn[:, :W], lhsT=G.ones_b[:], rhs=sq[:, :W], start=(k == 0), stop=(k == 7)),
                 reads=[sqn, "ones_b"], writes=[pfx + "pn"], inc=True)
    if phase == "A":
        return
    P.op("scalar", lambda e: e.activation(out=NS.rs[:, :W], in_=NS.pn[:, :W], func=AF.Sqrt, bias=G.c_eps[:], scale=1.0 / 1024.0),
         reads=[pfx + "pn", "c_eps"], writes=[pfx + "rs"])
    P.op("vector", lambda e: e.reciprocal(NS.rstd[:, :W], NS.rs[:, :W]), reads=[pfx + "rs"], writes=[pfx + "rstd"])
    for k in range(8):
        tmp = NS.tmp[k % 2]
        tn = pfx + "tmp%d" % (k % 2)
        P.op("vector", lambda e, tmp=tmp, k=k: e.tensor_tensor(out=tmp[:, :W], in0=x[:, k, :], in1=NS.rstd[:, :W], op=ALU.mult),
             reads=[xname, pfx + "rstd"], writes=[tn])
        P.op("scalar", lambda e, tmp=tmp, k=k: e.activation(out=h[:, k, :], in_=tmp[:, :W], func=AF.Identity, bias=sh[:, k:k + 1], scale=gm[:, k:k + 1]),
             reads=[tn, "modc"], writes=[hname])


def load_w_cast(G, dst, dname, src, nk, ncols, step=1024):
    P = G.P
    v = src.rearrange("(k p) n -> p k n", p=128)
    for c0 in range(0, ncols, step):
        c1 = min(ncols, c0 + step)
        P.dma("gpsimd", lambda e, c0=c0, c1=c1: e.dma_start(out=dst[:, :, c0:c1], in_=v[:, :, c0:c1]), writes=[dname])


def stage_ffn(G, l, i, tiles):
    P, I, nc = G.P, G.I, G.nc
    j = 0 if i == 0 else 2
    with contextlib.ExitStack() as st:
        sb, ps = _alloc(G, st)
        wi = sb("f_wi", [128, 8, 2 * DFF], BF16)
        wo = sb("f_wo", [128, 22, 1024], BF16)
        xt = [sb("f_xt%d" % b, [128, 8, TW], F32) for b in range(2)]
        hh = [sb("f_h%d" % b, [128, 8, TW], BF16) for b in range(2)]
        hid = sb("f_hid", [128, 22, TW], BF16)
        sa = [sb("f_sa%d" % b, [128, TW], F32) for b in range(2)]
        NS = NormScratch(G, sb, ps, "f_")
        NS.sq8 = [sb("f_sq8_%d" % k, [128, TW], BF16) for k in range(8)]
        pa = [ps("f_pa%d" % b, [128, TW]) for b in range(2)]
        pb = [ps("f_pb%d" % b, [128, TW]) for b in range(2)]
        po = [ps("f_po%d" % b, [128, TW]) for b in range(2)]
        wv_ = I["ffn_w_in"][l, i].rearrange("(k p) n -> p k n", p=128)
        for pc in (0, 2, 3, 1, 4, 5):
            c0, c1 = pc * 1024, min(2 * DFF, (pc + 1) * 1024)
            P.dma("gpsimd", lambda e, c0=c0, c1=c1: e.dma_start(out=wi[:, :, c0:c1], in_=wv_[:, :, c0:c1]), writes=["f_wi%d" % pc])
        load_w_cast(G, wo, "f_wo", I["ffn_w_out"][l, i], 22, 1024)
        def ld(n):
            ti_ = tiles[n][0]
            xb = xt[n % 2]
            P.dma("sync", lambda e, xb=xb, ti_=ti_: e.dma_start(out=xb[:], in_=G.XT[:, :, ti_ * TW:(ti_ + 1) * TW]),
                  reads=["XT%d" % ti_], writes=["f_xt%d" % (n % 2)])
        def do_norm(n, phase=None):
            ti_, m_ = tiles[n]
            bb = n % 2
            norm_mod(G, NS, xt[bb], "f_xt%d" % bb, G.GM[:, l, j, m_, :], G.SH[:, l, j, m_, :], hh[bb], "f_h%d" % bb, phase=phase)
        ld(0)
        if len(tiles) > 1:
            ld(1)
        do_norm(0)
        for n, (ti, m) in enumerate(tiles):
            t0 = ti * TW
            b = n % 2
            x, xn = xt[b], "f_xt%d" % b
            h, hn = hh[b], "f_h%d" % b
            for jj in range(22):
                q = jj % 2
                for half, pp, pn in ((0, pa[q], "f_pa%d" % q), (1, pb[q], "f_pb%d" % q)):
                    c0 = half * DFF + jj * 128
                    for k in range(8):
                        P.op("tensor", lambda e, pp=pp, c0=c0, k=k, h=h: e.matmul(
                            pp[:], lhsT=wi[:, k, c0:c0 + 128], rhs=h[:, k, :], start=(k == 0), stop=(k == 7)),
                            reads=["f_wi%d" % (c0 // 1024), "f_wi%d" % ((c0 + 127) // 1024), hn], writes=[pn], inc=(k == 7))
                P.op("scalar", lambda e, q=q: e.activation(out=sa[q][:], in_=pa[q][:], func=AF.Silu), reads=["f_pa%d" % q], writes=["f_sa%d" % q])
                P.op("vector", lambda e, q=q, jj=jj: e.tensor_tensor(out=hid[:, jj, :], in0=sa[q][:], in1=pb[q][:], op=ALU.mult),
                     reads=["f_sa%d" % q, "f_pb%d" % q], writes=["f_hid%d" % jj])
            if n + 1 < len(tiles):
                do_norm(n + 1, "A")
            for f in range(8):
                q = f % 2
                if f == 3 and n + 1 < len(tiles):
                    do_norm(n + 1, "B")
                for jj in range(22):
                    P.op("tensor", lambda e, q=q, f=f, jj=jj: e.matmul(
                        po[q][:], lhsT=wo[:, jj, f * 128:(f + 1) * 128], rhs=hid[:, jj, :], start=(jj == 0), stop=(jj == 21)),
                        reads=["f_wo", "f_hid%d" % jj], writes=["f_po%d" % q], inc=(jj == 21))
                gsc = G.GT[:, l, j, m, f:f + 1]
                P.op("vector", lambda e, q=q, f=f, x=x, gsc=gsc: e.scalar_tensor_tensor(
                    out=x[:, f, :], in0=po[q][:], scalar=gsc, in1=x[:, f, :], op0=ALU.mult, op1=ALU.add),
                    reads=["f_po%d" % q, "modc", xn], writes=[xn])
            if G.dbg and ti == 1:
                P.dma("sync", lambda e: e.dma_start(out=G.DBG["D_HID"][:, :, :], in_=hid[:]), reads=["f_hid%d" % q for q in range(22)], writes=["dbg3"])
                P.dma("sync", lambda e, x=x: e.dma_start(out=G.DBG["D_X1"][:, :, :], in_=x[:]), reads=[xn], writes=["dbg4"])
            P.dma("sync", lambda e, x=x, t0=t0: e.dma_start(out=G.XT[:, :, t0:t0 + TW], in_=x[:]), reads=[xn], writes=["XT%d" % ti])
            if n + 2 < len(tiles):
                ld(n + 2)
        P.barrier()
        P.emit()


def stage_final(G):
    P, I, nc = G.P, G.I, G.nc
    with contextlib.ExitStack() as st:
        sb, ps = _alloc(G, st)
        fg = sb("k_fg", [128, 1024], F32)
        xs = [sb("k_xs%d" % b, [128, 8, 128], F32) for b in range(2)]
        junk = sb("k_junk", [128, 1024], F32)
        yo = [sb("k_yo%d" % b, [128, 1024], F32) for b in range(2)]
        ssq = sb("k_ssq", [128, 2], F32)
        rs = sb("k_rs", [128, 2], F32)
        pt = [ps("k_pt%d" % b, [128, 8, 128]) for b in range(2)]
        P.dma("sync", lambda e: e.dma_start(out=fg[:], in_=I["final_g_bc"][:, :]), writes=["k_fg"])
        for i in range(NOWN // 128):
            t0 = OWN0 + i * 128
            b = i % 2
            P.dma("gpsimd", lambda e, b=b, t0=t0: e.dma_start(out=xs[b][:], in_=G.XT[:, :, t0:t0 + 128]),
                  reads=["XT%d" % (t0 // TW)], writes=["k_xs%d" % b])
            for k in range(8):
                P.op("tensor", lambda e, b=b, k=k: e.transpose(pt[b][:, k, :], xs[b][:, k, :], G.ident_f[:]),
                     reads=["k_xs%d" % b, "ident_f"], writes=["k_pt%d" % b], inc=(k == 7))
            ptf = pt[b][:].rearrange("p k n -> p (k n)")
            P.op("scalar", lambda e, b=b, ptf=ptf: e.activation(out=junk[:], in_=ptf, func=AF.Square, accum_out=ssq[:, b:b + 1]),
                 reads=["k_pt%d" % b], writes=["k_junk", "k_ssq%d" % b])
            P.op("scalar", lambda e, b=b: e.activation(out=rs[:, b:b + 1], in_=ssq[:, b:b + 1], func=AF.Sqrt, bias=G.c_eps[:], scale=1.0 / 1024.0),
                 reads=["k_ssq%d" % b, "c_eps"], writes=["k_rs%d" % b])
            P.op("vector", lambda e, b=b: e.reciprocal(rs[:, b:b + 1], rs[:, b:b + 1]), reads=["k_rs%d" % b], writes=["k_rs%d" % b])
            P.op("vector", lambda e, b=b, ptf=ptf: e.scalar_tensor_tensor(
                out=yo[b][:], in0=ptf, scalar=rs[:, b:b + 1], in1=fg[:], op0=ALU.mult, op1=ALU.mult),
                reads=["k_pt%d" % b, "k_rs%d" % b, "k_fg"], writes=["k_yo%d" % b])
            P.dma("sync", lambda e, b=b, i=i: e.dma_start(out=G.OUT[i * 128:(i + 1) * 128, :], in_=yo[b][:]),
                  reads=["k_yo%d" % b], writes=["OUT"])
        P.barrier()
        P.emit()


def stage_inproj(G):
    P, I, nc = G.P, G.I, G.nc
    l, j = 0, 1
    tiles = [(ti, 0) for ti in range(18)] + [(18, 1)]
    with contextlib.ExitStack() as st:
        sb, ps = _alloc(G, st)
        wfm = sb("i_wfm", [128, 8, 1024], BF16)
        wg = sb("i_wg", [128, 8, 128], BF16)
        wtm = sb("i_wtm", [128, 8, 2560], BF16)
        xt = [sb("i_xt%d" % b, [128, 8, TW], F32) for b in range(2)]
        hh = [sb("i_h%d" % b, [128, 8, TW], BF16) for b in range(2)]
        NS = NormScratch(G, sb, ps, "i_")
        NS.sq8 = [sb("i_sq8_%d" % k, [128, TW], BF16) for k in range(8)]
        fm = [sb("i_fm%d" % b, [128, 8, TW], BF16) for b in range(2)]
        gts = [sb("i_gt%d" % b, [128, TW], F32) for b in range(2)]
        cosT = [sb("i_cos%d" % b, [128, 512], F32) for b in range(2)]
        sinT = [sb("i_sin%d" % b, [128, 512], F32) for b in range(2)]
        hg = sb("i_hg", [128, 512], F32)
        vt = [sb("i_vt%d" % b, [128, 8, 65], BF16) for b in range(2)]
        mv = [sb("i_mv%d" % b, [128, 4, 129], BF16) for b in range(2)]
        xs = [sb("i_xs%d" % b, [128, 512], F32) for b in range(2)]
        r1 = [sb("i_r1%d" % b, [128, 512], F32) for b in range(2)]
        r2 = [sb("i_r2%d" % b, [128, 512], F32) for b in range(2)]
        qr = [sb("i_qr%d" % b, [128, 512], BF16) for b in range(2)]
        sg = sb("i_sg", [128, 512], F32)
        g2 = [sb("i_g2%d" % b, [128, 512], F32) for b in range(2)]
        tq = [sb("i_tq%d" % b, [128, 4, 128], BF16) for b in range(2)]
        pfm = [ps("i_pfm%d" % b, [128, TW]) for b in range(2)]
        ptm = [ps("i_ptm%d" % b, [128, 512]) for b in range(2)]
        ptr = [ps("i_ptr%d" % b, [128, 4, 128], BF16) for b in range(2)]
        load_w_cast(G, wfm, "i_wfm", I["mix_w_in"][:, 0:1024], 8, 1024)
        load_w_cast(G, wg, "i_wg", I["w_gate"], 8, 128)
        load_w_cast(G, wtm, "i_wtm", I["mix_w_in"][:, 1024:3584], 8, 2560)
        P.dma("sync", lambda e: e.dma_start(out=hg[:], in_=I["headg_bc"][:, :]), writes=["i_hg"])
        for b in range(2):
            P.op("vector", lambda e, b=b: e.memset(vt[b][:], 1.0), writes=["i_vt%d" % b])
            P.op("vector", lambda e, b=b: e.memset(mv[b][:], 1.0), writes=["i_mv%d" % b])

        def ld(n):
            ti_ = tiles[n][0]
            xb = xt[n % 2]
            P.dma("gpsimd", lambda e, xb=xb, ti_=ti_: e.dma_start(out=xb[:], in_=G.XT[:, :, ti_ * TW:(ti_ + 1) * TW]),
                  reads=["XT%d" % ti_], writes=["i_xt%d" % (n % 2)])
        ld(0)
        cnt = {"s": 0, "r": 0, "t": 0}
        deferred = []

        def rope_and_T(pt_, ptn, scale, dstT, tmaj_dst, ts0):
            a = cnt["r"] % 2
            cnt["r"] += 1
            sb_ = cnt["s"] % 2
            X, R1, R2, QR, TQ, PT = xs[a], r1[a], r2[a], qr[a], tq[a], ptr[a]
            xn_, r1n, r2n, qrn, tqn, ptn2 = "i_xs%d" % a, "i_r1%d" % a, "i_r2%d" % a, "i_qr%d" % a, "i_tq%d" % a, "i_ptr%d" % a
            P.op("scalar", lambda e: e.activation(out=X[:], in_=pt_[:], func=AF.Copy, scale=scale), reads=[ptn], writes=[xn_])
            P.op("vector", lambda e: e.tensor_tensor(out=R1[:], in0=X[:], in1=cosT[sb_][:], op=ALU.mult), reads=[xn_, "i_cos%d" % sb_], writes=[r1n])
            Xv = X[:].rearrange("p (i t) -> p i t", t=2)
            Sv = sinT[sb_][:].rearrange("p (i t) -> p i t", t=2)
            Rv = R2[:].rearrange("p (i t) -> p i t", t=2)
            P.op("vector", lambda e: e.tensor_tensor(out=Rv[:, :, 0], in0=Xv[:, :, 1], in1=Sv[:, :, 0], op=ALU.mult),
                 reads=[xn_, "i_sin%d" % sb_], writes=[r2n])
            P.op("vector", lambda e: e.tensor_tensor(out=Rv[:, :, 1], in0=Xv[:, :, 0], in1=Sv[:, :, 1], op=ALU.mult),
                 reads=[xn_, "i_sin%d" % sb_], writes=[r2n])
            P.op("vector", lambda e: e.tensor_tensor(out=QR[:], in0=R1[:], in1=R2[:], op=ALU.add), reads=[r1n, r2n], writes=[qrn])
            if tmaj_dst is not None:
                P.dma("sync", lambda e: e.dma_start(out=tmaj_dst[ts0:ts0 + 128, :], in_=QR[:]), reads=[qrn], writes=[_u()])
            def later():
                for hd in range(4):
                    P.op("tensor", lambda e, hd=hd: e.transpose(PT[:, hd, :], QR[:, hd * 128:(hd + 1) * 128], G.ident_b[:]),
                         reads=[qrn, "ident_b"], writes=[ptn2], inc=(hd == 3))
                P.op("scalar", lambda e: e.activation(out=TQ[:], in_=PT[:], func=AF.Copy), reads=[ptn2], writes=[tqn])
                P.dma("sync", lambda e: e.dma_start(out=dstT[:, :, ts0:ts0 + 128], in_=TQ[:]), reads=[tqn], writes=[_u()])
            deferred.append(later)

        for n, (ti, m) in enumerate(tiles):
            t0 = ti * TW
            b = n % 2
            x, xn = xt[b], "i_xt%d" % b
            h, hn = hh[b], "i_h%d" % b
            if n + 1 < len(tiles):
                ld(n + 1)
            if n == 0:
                norm_mod(G, NS, x, xn, G.GM[:, l, j, m, :], G.SH[:, l, j, m, :], h, hn)
            FM, fmn = fm[b], "i_fm%d" % b
            no_q = ti in (0, 17, 18)
            blks = [0] if ti in (0, 17) else ([0, 2, 3] if ti == 18 else [0, 1, 2, 3, 4])
            for fc in (range(4, 8) if no_q else range(8)):
                q = fc % 2
                for k in range(8):
                    P.op("tensor", lambda e, q=q, fc=fc, k=k, h=h: e.matmul(
                        pfm[q][:], lhsT=wfm[:, k, fc * 128:(fc + 1) * 128], rhs=h[:, k, :], start=(k == 0), stop=(k == 7)),
                        reads=["i_wfm", hn], writes=["i_pfm%d" % q], inc=(k == 7))
                P.op("scalar", lambda e, q=q, fc=fc, FM=FM: e.activation(out=FM[:, fc, :], in_=pfm[q][:], func=AF.Copy),
                     reads=["i_pfm%d" % q], writes=[fmn])
            if not no_q:
                P.dma("sync", lambda e, FM=FM, t0=t0: e.dma_start(out=G.NAQT[:, :, t0:t0 + TW], in_=FM[:, 0:4, :]), reads=[fmn], writes=[_u()])
            P.dma("sync", lambda e, FM=FM, t0=t0: e.dma_start(out=G.NAKT[:, :, t0:t0 + TW], in_=FM[:, 4:8, :]), reads=[fmn], writes=[_u()])
            GTS, gtn = gts[b], "i_gt%d" % b
            for k in range(8):
                P.op("tensor", lambda e, k=k, h=h: e.matmul(pfm[0][:], lhsT=wg[:, k, :], rhs=h[:, k, :], start=(k == 0), stop=(k == 7)),
                     reads=["i_wg", hn], writes=["i_pfm0"], inc=(k == 7))
            P.op("vector", lambda e, GTS=GTS: e.tensor_copy(GTS[:], pfm[0][:]), reads=["i_pfm0"], writes=[gtn])
            P.dma("sync", lambda e, GTS=GTS, t0=t0: e.dma_start(out=G.GIF[:, t0:t0 + TW], in_=GTS[:]), reads=[gtn], writes=[_u()])
            if n + 1 < len(tiles):
                ti2, m2 = tiles[n + 1]
                b2 = (n + 1) % 2
                norm_mod(G, NS, xt[b2], "i_xt%d" % b2, G.GM[:, l, j, m2, :], G.SH[:, l, j, m2, :], hh[b2], "i_h%d" % b2, phase="A")
            for s_ in range(TW // 128):
                if s_ == 1 and n + 1 < len(tiles):
                    norm_mod(G, NS, xt[b2], "i_xt%d" % b2, G.GM[:, l, j, m2, :], G.SH[:, l, j, m2, :], hh[b2], "i_h%d" % b2, phase="B")
                ts0 = t0 + s_ * 128
                sbi = cnt["s"] % 2
                P.dma("gpsimd", lambda e, sbi=sbi, ts0=ts0: e.dma_start(out=cosT[sbi][:], in_=I["ropecos"][ts0:ts0 + 128, :]), writes=["i_cos%d" % sbi])
                P.dma("gpsimd", lambda e, sbi=sbi, ts0=ts0: e.dma_start(out=sinT[sbi][:], in_=I["ropesin"][ts0:ts0 + 128, :]), writes=["i_sin%d" % sbi])
                for blk in blks:
                    a = cnt["t"] % 2
                    cnt["t"] += 1
                    PT_, ptn = ptm[a], "i_ptm%d" % a
                    for k in range(8):
                        P.op("tensor", lambda e, PT_=PT_, k=k, h=h, s_=s_, blk=blk: e.matmul(
                            PT_[:], lhsT=h[:, k, s_ * 128:(s_ + 1) * 128], rhs=wtm[:, k, blk * 512:(blk + 1) * 512], start=(k == 0), stop=(k == 7)),
                            reads=["i_wtm", hn], writes=[ptn], inc=(k == 7))
                    while len(deferred) > (1 if blk == 3 else 0):
                        deferred.pop(0)()
                    if blk == 0:
                        VT = vt[sbi]
                        P.op("scalar", lambda e, VT=VT, PT_=PT_: e.activation(out=VT[:, :, 0:64], in_=PT_[:].rearrange("p (h d) -> p h d", d=64), func=AF.Copy),
                             reads=[ptn], writes=["i_vt%d" % sbi])
                        P.dma("sync", lambda e, VT=VT, ts0=ts0: e.dma_start(out=G.NAV[ts0:ts0 + 128, :], in_=VT[:].rearrange("p h d -> p (h d)")),
                              reads=["i_vt%d" % sbi], writes=[_u()])
                    elif blk == 1:
                        rope_and_T(PT_, ptn, 1.0, G.MLQT, None, ts0)
                    elif blk == 2:
                        rope_and_T(PT_, ptn, 128.0 ** -0.5, G.MLKT, G.MLK, ts0)
                    elif blk == 3:
                        MV = mv[sbi]
                        P.op("scalar", lambda e, MV=MV, PT_=PT_: e.activation(out=MV[:, :, 0:128], in_=PT_[:].rearrange("p (h d) -> p h d", d=128), func=AF.Copy),
                             reads=[ptn], writes=["i_mv%d" % sbi])
                        P.dma("sync", lambda e, MV=MV, ts0=ts0: e.dma_start(out=G.MLV[ts0:ts0 + 128, :], in_=MV[:].rearrange("p h d -> p (h d)")),
                              reads=["i_mv%d" % sbi], writes=[_u()])
                    else:
                        G2 = g2[sbi]
                        P.op("scalar", lambda e, PT_=PT_: e.activation(out=sg[:], in_=PT_[:], func=AF.Sigmoid), reads=[ptn], writes=["i_sg"])
                        P.op("vector", lambda e, G2=G2: e.tensor_tensor(out=G2[:], in0=sg[:], in1=hg[:], op=ALU.mult), reads=["i_sg", "i_hg"], writes=["i_g2%d" % sbi])
                        P.dma("sync", lambda e, G2=G2, ts0=ts0: e.dma_start(out=G.MLG2[ts0:ts0 + 128, :], in_=G2[:]), reads=["i_g2%d" % sbi], writes=[_u()])
                while deferred:
                    deferred.pop(0)()
                cnt["s"] += 1
        P.barrier()
        P.emit()


def stage_na(G):
    P, I, nc = G.P, G.I, G.nc
    with contextlib.ExitStack() as st:
        sb, ps = _alloc(G, st)
        KT = sb("n_KT", [128, 4, NU], BF16)
        V = sb("n_V", [128, 38, 520], BF16)
        QT = sb("n_QT", [128, 4, NOWN], BF16)
        BI = sb("n_BI", [128, 5, 8, 5, 128], BF16)
        sAb = [sb("n_sAb%d" % b, [128, 4, 128], F32) for b in range(2)]
        sBb = [sb("n_sBb%d" % b, [128, 128], F32) for b in range(2)]
        pt = [sb("n_pt%d" % b, [128, 7, 128], BF16) for b in range(2)]
        ya = [sb("n_ya%d" % b, [128, 512], BF16) for b in range(2)]
        rec = [sb("n_rec%d" % b, [128, 8], F32) for b in range(2)]
        yt = [sb("n_yt%d" % b, [128, 4, 128], BF16) for b in range(2)]
        sA = [ps("n_sA%d" % b, [128, 4, 128]) for b in range(2)]
        sB = [ps("n_sB%d" % b, [128, 4, 128]) for b in range(2)]
        O = ps("n_O", [128, 8, 128])
        ptr = ps("n_ptr", [128, 4, 128], BF16)
        P.dma("sync", lambda e: e.dma_start(out=KT[:], in_=G.NAKT[:, :, :]), reads=["dramNA"], writes=["n_KT"])
        P.dma("sync", lambda e: e.dma_start(out=V[:], in_=G.NAV.rearrange("(c p) n -> p c n", p=128)), reads=["dramNA"], writes=["n_V"])
        P.dma("sync", lambda e: e.dma_start(out=QT[:], in_=G.NAQT[:, :, OWN0:OWN0 + NOWN]), reads=["dramNA"], writes=["n_QT"])
        for c5 in range(5):
            P.dma("gpsimd", lambda e, c5=c5: e.dma_start(out=BI[:, c5], in_=I["nabiasT"][c5].rearrange("h k j q -> k h j q")), writes=["n_BI"])
        def scores(p, h, a):
            cls = 0 if p == 0 else 1 if p == 1 else 3 if p == 30 else 4 if p == 31 else 2
            hc, b0 = h // 2, (h % 2) * 64
            q_ap = QT[b0:b0 + 64, hc, p * 128:(p + 1) * 128]
            SA, SB, PT = sA[a], sB[a], pt[a]
            san, sbn, ptn = "n_sA%d" % a, "n_sB%d" % a, "n_pt%d" % a
            for jj in range(4):
                k0 = (p + jj) * 128
                P.op("tensor", lambda e, jj=jj, k0=k0: e.matmul(
                    SA[:, jj, :], lhsT=KT[b0:b0 + 64, hc, k0:k0 + 128], rhs=q_ap, start=True, stop=True),
                    reads=["n_KT", "n_QT"], writes=[san], inc=(jj == 3))
            k0h = (p + 4) * 128
            P.op("tensor", lambda e: e.matmul(
                SB[0:64, 0, :], lhsT=KT[b0:b0 + 64, hc, k0h:k0h + 64], rhs=q_ap, start=True, stop=True),
                reads=["n_KT", "n_QT"], writes=[sbn], inc=False)
            for c in range(2):
                k0 = CTX0 + c * 128
                P.op("tensor", lambda e, c=c, k0=k0: e.matmul(
                    SB[:, 1 + c, :], lhsT=KT[b0:b0 + 64, hc, k0:k0 + 128], rhs=q_ap, start=True, stop=True),
                    reads=["n_KT", "n_QT"], writes=[sbn], inc=(c == 1))
            AB, BB = sAb[a], sBb[a]
            abn, bbn = "n_sAb%d" % a, "n_sBb%d" % a
            P.op("vector", lambda e: e.scalar_tensor_tensor(
                out=AB[:], in0=SA[:], scalar=0.125, in1=BI[:, cls, h, 0:4, :], op0=ALU.mult, op1=ALU.add),
                reads=[san, "n_BI"], writes=[abn])
            P.op("vector", lambda e: e.scalar_tensor_tensor(
                out=BB[0:64, :], in0=SB[0:64, 0, :], scalar=0.125, in1=BI[0:64, cls, h, 4, :], op0=ALU.mult, op1=ALU.add),
                reads=[sbn, "n_BI"], writes=[bbn])
            P.op("scalar", lambda e: e.activation(out=PT[:, 0:4, :], in_=AB[:], func=AF.Exp), reads=[abn], writes=[ptn])
            P.op("scalar", lambda e: e.activation(out=PT[:, 5:7, :], in_=SB[:, 1:3, :], func=AF.Exp, scale=0.125), reads=[sbn, bbn], writes=[ptn])
            P.op("scalar", lambda e: e.activation(out=PT[0:64, 4, :], in_=BB[0:64, :], func=AF.Exp), reads=[bbn], writes=[ptn])

        def pv(p, h, a):
            pb2 = p % 2
            PT, ptn = pt[a], "n_pt%d" % a
            specs = [(jj, 128, p + jj) for jj in range(4)] + [(4, 64, p + 4), (5, 128, 36), (6, 128, 37)]
            for si, (slot, nk, vc) in enumerate(specs):
                P.op("tensor", lambda e, slot=slot, nk=nk, vc=vc, si=si: e.matmul(
                    O[:, h, 0:65], lhsT=PT[0:nk, slot, :], rhs=V[0:nk, vc, h * 65:(h + 1) * 65], start=(si == 0), stop=(si == 6)),
                    reads=[ptn, "n_V"], writes=["n_O%d" % (h // 4)], inc=(si == 6))
            if h in (3, 7):
                hf_ = h // 4
                R, YA = rec[pb2], ya[pb2]
                rn, yan = "n_rec%d_%d" % (pb2, hf_), "n_ya%d" % pb2
                P.op("vector", lambda e: e.reciprocal(R[:, hf_ * 4:hf_ * 4 + 4], O[:, hf_ * 4:hf_ * 4 + 4, 64]), reads=["n_O%d" % hf_], writes=[rn])
                for h2 in range(hf_ * 4, hf_ * 4 + 4):
                    P.op("scalar", lambda e, h2=h2: e.activation(out=YA[:, h2 * 64:(h2 + 1) * 64], in_=O[:, h2, 0:64], func=AF.Copy, scale=R[:, h2:h2 + 1]),
                         reads=["n_O%d" % hf_, rn], writes=[yan])
            if h == 7:
                YA, YT_ = ya[pb2], yt[pb2]
                yan, ytn = "n_ya%d" % pb2, "n_yt%d" % pb2
                for c in range(4):
                    P.op("tensor", lambda e, c=c: e.transpose(ptr[:, c, :], YA[:, c * 128:(c + 1) * 128], G.ident_b[:]),
                         reads=[yan, "ident_b"], writes=["n_ptr"], inc=(c == 3))
                P.op("vector", lambda e: e.tensor_copy(YT_[:], ptr[:]), reads=["n_ptr"], writes=[ytn])
                P.dma("sync", lambda e: e.dma_start(out=G.YT[:, 0:4, p * 128:(p + 1) * 128], in_=YT_[:]), reads=[ytn], writes=[_u()])

        items = [(p, h) for p in range(32) for h in range(8)]
        scores(items[0][0], items[0][1], 0)
        for i_, (p, h) in enumerate(items):
            if i_ + 1 < len(items):
                scores(items[i_ + 1][0], items[i_ + 1][1], (i_ + 1) % 2)
            pv(p, h, i_ % 2)
        P.barrier()
        P.emit()


def stage_ml(G):
    P, I, nc = G.P, G.I, G.nc
    NCH = 32
    with contextlib.ExitStack() as st:
        sb, ps = _alloc(G, st)
        TOK = sb("l_TOK", [128, 34, 5, 8], F32)
        EBEND = sb("l_EBEND", [128, 8, 32], F32)
        ATOT = sb("l_ATOT", [128, 8], F32)
        with contextlib.ExitStack() as st1:
            sb1, ps1 = _alloc(G, st1)
            LI = sb1("l_LI", [64, NU], F32)
            SP = sb1("l_SP", [64, NU], F32)
            CL = sb1("l_CL", [64, NU], F32)
            CG = sb1("l_CG", [64, NU], F32)
            TM = sb1("l_TM", [64, NU], F32)
            OQ = [sb1("l_OQ%d" % b, [64, NU], F32) for b in range(2)]
            gbI = sb1("l_gbI", [64, 1], F32)
            gbF = sb1("l_gbF", [64, 1], F32)
            CE = sb1("l_CE", [64, 32], F32)
            ntot = sb1("l_ntot", [64, 2], F32)
            tot = sb1("l_tot", [64, 2], F32)
            SELM = sb1("l_SELM", [64, 8, 128], F32)
            ptr = [ps1("l_ptr%d" % b, [128, 8, 64]) for b in range(2)]
            pe = ps1("l_pe", [128, 8, 32])
            pa = ps1("l_pa", [128, 8, 2])
            own = slice(OWN0, OWN0 + NOWN)
            cxs = slice(CTX0, CTX0 + 256)
            P.dma("sync", lambda e: e.dma_start(out=LI[:], in_=G.GIF[0:64, :]), reads=["dramML"], writes=["l_LI"])
            P.dma("sync", lambda e: e.dma_start(out=SP[:], in_=G.GIF[64:128, :]), reads=["dramML"], writes=["l_SP"])
            P.dma("sync", lambda e: e.dma_start(out=gbI[:], in_=I["gate_b"][0:64, :]), writes=["l_gbI"])
            P.dma("sync", lambda e: e.dma_start(out=gbF[:], in_=I["gate_b"][64:128, :]), writes=["l_gbF"])
            P.dma("sync", lambda e: e.dma_start(out=SELM[:], in_=I["selm"][:, :, :]), writes=["l_SELM"])
            P.op("vector", lambda e: e.tensor_scalar(gbF[:], gbF[:], -1.0, None, ALU.mult), reads=["l_gbF"], writes=["l_gbF"])
            P.op("scalar", lambda e: e.activation(out=LI[:], in_=LI[:], func=AF.Identity, bias=gbI[:]), reads=["l_LI", "l_gbI"], writes=["l_LI"])
            P.op("scalar", lambda e: e.activation(out=SP[:], in_=SP[:], func=AF.Exp, bias=gbF[:], scale=-1.0), reads=["l_SP", "l_gbF"], writes=["l_SP"])
            P.op("scalar", lambda e: e.activation(out=SP[:], in_=SP[:], func=AF.Ln, bias=G.c_one[0:64, :]), reads=["l_SP", "c_one"], writes=["l_SP"])
            P.op("vector", lambda e: e.memset(TM[:], 1.0), writes=["l_TM"])
            P.op("vector", lambda e: e.tensor_tensor_scan(out=CG[:, own], data0=TM[:, own], data1=SP[:, own], initial=0.0, op0=ALU.mult, op1=ALU.add),
                 reads=["l_TM", "l_SP"], writes=["l_CG"])
            P.op("vector", lambda e: e.tensor_tensor_scan(out=CG[:, cxs], data0=TM[:, cxs], data1=SP[:, cxs], initial=0.0, op0=ALU.mult, op1=ALU.add),
                 reads=["l_TM", "l_SP"], writes=["l_CG"])
            TMo = TM[:, own].rearrange("p (c t) -> p c t", t=128)
            P.op("vector", lambda e: e.memset(TMo[:, :, 0:1], 0.0), reads=["l_CG"], writes=["l_TM"])
            P.op("vector", lambda e: e.tensor_tensor_scan(out=CL[:, own], data0=TM[:, own], data1=SP[:, own], initial=0.0, op0=ALU.mult, op1=ALU.add),
                 reads=["l_TM", "l_SP"], writes=["l_CL"])
            CLo = CL[:, own].rearrange("p (c t) -> p c t", t=128)
            SPo = SP[:, own].rearrange("p (c t) -> p c t", t=128)
            P.op("vector", lambda e: e.tensor_copy(CE[:], CLo[:, :, 127]), reads=["l_CL"], writes=["l_CE"])
            P.op("vector", lambda e: e.tensor_copy(tot[:, 0:1], CG[:, OWN0 + NOWN - 1:OWN0 + NOWN]), reads=["l_CG"], writes=["l_tot"])
            P.op("vector", lambda e: e.tensor_copy(tot[:, 1:2], CG[:, CTX0 + 255:CTX0 + 256]), reads=["l_CG"], writes=["l_tot"])
            P.op("vector", lambda e: e.tensor_scalar(ntot[:], tot[:], -1.0, None, ALU.mult), reads=["l_tot"], writes=["l_ntot"])
            for c in range(NCH):
                P.op("vector", lambda e, c=c: e.tensor_scalar(CLo[32:64, c, :], CLo[32:64, c, :], CE[32:64, c:c + 1], -1.0, ALU.subtract, ALU.mult),
                     reads=["l_CL", "l_CE"], writes=["l_CL"])
            P.op("vector", lambda e: e.tensor_tensor(out=CL[32:64, own], in0=CL[32:64, own], in1=SP[32:64, own], op=ALU.add),
                 reads=["l_CL", "l_SP"], writes=["l_CL"])
            for r in range(8):
                P.op("tensor", lambda e, r=r: e.matmul(pe[:, r, :], lhsT=SELM[:, r, :], rhs=CE[:], start=True, stop=True),
                     reads=["l_SELM", "l_CE"], writes=["l_pe"], inc=(r == 7))
            P.op("scalar", lambda e: e.activation(out=EBEND[:], in_=pe[:], func=AF.Exp, scale=-1.0), reads=["l_pe"], writes=["l_EBEND"])
            for r in range(8):
                P.op("tensor", lambda e, r=r: e.matmul(pa[:, r, :], lhsT=SELM[:, r, :], rhs=tot[:], start=True, stop=True),
                     reads=["l_SELM", "l_tot"], writes=["l_pa"], inc=(r == 7))
            P.op("scalar", lambda e: e.activation(out=ATOT[:], in_=pa[:, :, 0], func=AF.Exp, scale=-1.0), reads=["l_pa"], writes=["l_ATOT"])

            tcnt = {"n": 0}

            def transpose_out(Q, qn, qty, chunks):
                for g0 in range(0, len(chunks), 8):
                    grp = chunks[g0:g0 + 8]
                    a = tcnt["n"] % 2
                    tcnt["n"] += 1
                    for gi, (ci, col0) in enumerate(grp):
                        P.op("tensor", lambda e, a=a, gi=gi, col0=col0: e.transpose(ptr[a][:, gi, :], Q[0:64, col0:col0 + 128], G.ident_f[0:64, 0:64]),
                             reads=[qn, "ident_f"], writes=["l_ptr%d" % a], inc=(gi == len(grp) - 1))
                    c_first = grp[0][0]
                    ng_ = len(grp)
                    src = ptr[a][:, 0:ng_, :].rearrange("p g (d x) -> p g d x", d=2)[:, :, :, 0:4]
                    dst = TOK[:, c_first:c_first + ng_, qty, :].rearrange("p g (d x) -> p g d x", d=2)
                    P.op("vector", lambda e, src=src, dst=dst: e.tensor_copy(dst, src), reads=["l_ptr%d" % a], writes=["l_TOK"])

            own_chunks = [(c, OWN0 + c * 128) for c in range(NCH)]
            ctx_chunks = [(32 + c, CTX0 + c * 128) for c in range(2)]
            P.op("scalar", lambda e: e.activation(out=OQ[0][:, own], in_=CL[:, own], func=AF.Exp, scale=-1.0), reads=["l_CL"], writes=["l_OQ0"])
            transpose_out(OQ[0], "l_OQ0", 0, own_chunks)
            P.op("scalar", lambda e: e.activation(out=OQ[1][:, own], in_=CL[:, own], func=AF.Exp), reads=["l_CL"], writes=["l_OQ1"])
            transpose_out(OQ[1], "l_OQ1", 4, own_chunks)
            P.op("vector", lambda e: e.tensor_tensor(out=TM[:, own], in0=LI[:, own], in1=CL[:, own], op=ALU.add), reads=["l_LI", "l_CL"], writes=["l_TM"])
            P.op("scalar", lambda e: e.activation(out=OQ[1][:, own], in_=TM[:, own], func=AF.Exp), reads=["l_TM"], writes=["l_OQ1"])
            transpose_out(OQ[1], "l_OQ1", 1, own_chunks)
            TMo2 = TM[:, own].rearrange("p (c t) -> p c t", t=128)
            for c in range(NCH):
                P.op("vector", lambda e, c=c: e.tensor_scalar(TMo2[:, c, :], TMo2[:, c, :], CE[:, c:c + 1], None, ALU.subtract),
                     reads=["l_TM", "l_CE", "l_OQ1"], writes=["l_TM"])
            P.op("scalar", lambda e: e.activation(out=OQ[0][:, own], in_=TM[:, own], func=AF.Exp), reads=["l_TM"], writes=["l_OQ0"])
            transpose_out(OQ[0], "l_OQ0", 2, own_chunks)
            for (sl, ti_) in ((own, 0), (cxs, 1)):
                P.op("vector", lambda e, sl=sl: e.tensor_tensor(out=TM[0:32, sl], in0=LI[0:32, sl], in1=CG[0:32, sl], op=ALU.add),
                     reads=["l_LI", "l_CG"], writes=["l_TM"])
                P.op("vector", lambda e, sl=sl: e.tensor_tensor(out=TM[32:64, sl], in0=LI[32:64, sl], in1=CG[32:64, sl], op=ALU.subtract),
                     reads=["l_LI", "l_CG"], writes=["l_TM"])
                P.op("vector", lambda e, sl=sl: e.tensor_tensor(out=TM[32:64, sl], in0=TM[32:64, sl], in1=SP[32:64, sl], op=ALU.add),
                     reads=["l_TM", "l_SP"], writes=["l_TM"])
                P.op("scalar", lambda e, sl=sl, ti_=ti_: e.activation(out=OQ[1][0:32, sl], in_=TM[0:32, sl], func=AF.Exp, bias=ntot[0:32, ti_:ti_ + 1]),
                     reads=["l_TM", "l_ntot"], writes=["l_OQ1"])
                P.op("scalar", lambda e, sl=sl: e.activation(out=OQ[1][32:64, sl], in_=TM[32:64, sl], func=AF.Exp), reads=["l_TM"], writes=["l_OQ1"])
            transpose_out(OQ[1], "l_OQ1", 3, own_chunks + ctx_chunks)
            P.barrier()
            P.emit()

        KTOK = sb("l_KTOK", [128, 34, 512], BF16)
        VTOK = sb("l_VTOK", [128, 34, 516], BF16)
        TRI = sb("l_TRI", [128, 2, 128], F32)
        SELV = sb("l_SELV", [128, 16], F32)
        PAY = sb("l_PAY", [128, 8, 130], F32)
        GATH = sb("l_GATH", [128, 4, 1040], F32)
        STATE = sb("l_STATE", [128, 8, 129], F32)
        STB = sb("l_STB", [128, 8, 129], BF16)
        KA = [sb("l_KA%d" % b, [128, 128], BF16) for b in range(3)]
        alpha = sb("l_alpha", [128, 1], F32)
        tmpL = sb("l_tmpL", [128, 129], F32)
        QTc = [[sb("l_QTc%d%d" % (d_, b), [128, 4, 128], BF16) for b in range(2)] for d_ in range(2)]
        KTc = [[sb("l_KTc%d%d" % (d_, b), [128, 4, 128], BF16) for b in range(2)] for d_ in range(2)]
        HS = [[sb("l_HS%d%d" % (d_, b), [128, 512], F32) for b in range(2)] for d_ in range(2)]
        PTt = [sb("l_PT%d" % b, [128, 128], BF16) for b in range(2)]
        pS = [ps("l_pS%d" % b, [128, 128]) for b in range(2)]
        pU = [ps("l_pU%d" % b, [128, 132]) for b in range(4)]
        pN = [ps("l_pN%d" % b, [128, 132]) for b in range(2)]
        pL = [pU[0], pU[1]]
        den = [sb("l_den%d" % b, [128, 8], F32) for b in range(2)]
        P.dma("sync", lambda e: e.dma_start(out=KTOK[:, 0:32, :], in_=G.MLK[OWN0:OWN0 + NOWN, :].rearrange("(c p) n -> p c n", p=128)), reads=["dramML"], writes=["l_KTOK"])
        P.dma("sync", lambda e: e.dma_start(out=KTOK[:, 32:34, :], in_=G.MLK[CTX0:CTX0 + 256, :].rearrange("(c p) n -> p c n", p=128)), reads=["dramML"], writes=["l_KTOK"])
        P.dma("sync", lambda e: e.dma_start(out=VTOK[:, 0:32, :], in_=G.MLV[OWN0:OWN0 + NOWN, :].rearrange("(c p) n -> p c n", p=128)), reads=["dramML"], writes=["l_VTOK"])
        P.dma("sync", lambda e: e.dma_start(out=VTOK[:, 32:34, :], in_=G.MLV[CTX0:CTX0 + 256, :].rearrange("(c p) n -> p c n", p=128)), reads=["dramML"], writes=["l_VTOK"])
        P.dma("sync", lambda e: e.dma_start(out=TRI[:], in_=I["tri"][:, :, :]), writes=["l_TRI"])
        P.dma("sync", lambda e: e.dma_start(out=SELV[:], in_=I["selv"][:, :]), writes=["l_SELV"])
        kacnt = {"n": 0}

        def scaled_k(c, h, qty, r):
            a = kacnt["n"] % 3
            kacnt["n"] += 1
            eng = ("vector", "scalar", "scalar")[a]
            src = KTOK[:, c, h * 128:(h + 1) * 128]
            sc = TOK[:, c, qty, r:r + 1]
            if eng == "scalar":
                P.op("scalar", lambda e: e.activation(out=KA[a][:], in_=src, func=AF.Copy, scale=sc), reads=["l_KTOK", "l_TOK"], writes=["l_KA%d" % a])
            else:
                P.op(eng, lambda e: e.tensor_scalar(KA[a][:], src, sc, None, ALU.mult), reads=["l_KTOK", "l_TOK"], writes=["l_KA%d" % a])
            return KA[a], "l_KA%d" % a

        n2 = 0
        for r in range(8):
            h = r % 4
            for (chs, dstname) in ((list(range(32)), "own"), ([32, 33], "ctx")):
                pp, ppn = pL[n2 % 2], "l_pU%d" % (n2 % 2)
                n2 += 1
                for i_, c in enumerate(chs):
                    ka, kan = scaled_k(c, h, 3, r)
                    P.op("tensor", lambda e, pp=pp, ka=ka, c=c, h=h, i_=i_, L=len(chs): e.matmul(
                        pp[:, 0:129], lhsT=ka[:], rhs=VTOK[:, c, h * 129:(h + 1) * 129], start=(i_ == 0), stop=(i_ == L - 1)),
                        reads=[kan, "l_VTOK"], writes=[ppn], inc=True)
                if dstname == "own":
                    P.op("vector", lambda e, pp=pp, r=r: e.tensor_copy(PAY[:, r, 0:129], pp[:, 0:129]), reads=[ppn], writes=["l_PAY"])
                else:
                    P.op("vector", lambda e, pp=pp, r=r: e.tensor_copy(STATE[:, r, :], pp[:, 0:129]), reads=[ppn], writes=["l_STATE"])
        P.op("vector", lambda e: e.tensor_copy(PAY[:, :, 129], ATOT[:]), reads=["l_ATOT", "l_PAY"], writes=["l_PAY"])
        P.dma("sync", lambda e: e.dma_start(out=G.CCI[:, :], in_=PAY[:].rearrange("p r n -> p (r n)")), reads=["l_PAY"], writes=["CCI"])
        for _rep in range(3):
            P.cc(lambda e: e.collective_compute("AllGather", ALU.bypass, replica_groups=[[0, 1, 2, 3], [4, 5, 6, 7]],
                                                ins=[G.CCI.opt()], outs=[G.CCO.opt()]), reads=["CCI"], writes=["CCO"])
        P.dma("sync", lambda e: e.dma_start(out=GATH[:], in_=G.CCO.rearrange("(j p) n -> p j n", p=128)), reads=["CCO"], writes=["l_GATH"])
        for r in range(8):
            d = r // 4
            order = range(4) if d == 0 else range(3, -1, -1)
            for jseg in order:
                so = 0 if d == 0 else 8
                A_j = GATH[:, jseg, r * 130 + 129:r * 130 + 130]
                L_j = GATH[:, jseg, r * 130:r * 130 + 129]
                P.op("vector", lambda e, A_j=A_j, so=so, jseg=jseg: e.tensor_scalar(
                    alpha[:], A_j, SELV[:, so + jseg:so + jseg + 1], SELV[:, so + 4 + jseg:so + 5 + jseg], ALU.mult, ALU.add),
                    reads=["l_GATH", "l_SELV"], writes=["l_alpha"])
                P.op("vector", lambda e, L_j=L_j, so=so, jseg=jseg: e.tensor_scalar(tmpL[:], L_j, SELV[:, so + jseg:so + jseg + 1], None, ALU.mult),
                     reads=["l_GATH", "l_SELV"], writes=["l_tmpL"])
                P.op("vector", lambda e, r=r: e.scalar_tensor_tensor(out=STATE[:, r, :], in0=STATE[:, r, :], scalar=alpha[:], in1=tmpL[:], op0=ALU.mult, op1=ALU.add),
                     reads=["l_STATE", "l_alpha", "l_tmpL"], writes=["l_STATE"])
        P.op("scalar", lambda e: e.activation(out=STB[:], in_=STATE[:], func=AF.Copy), reads=["l_STATE"], writes=["l_STB"])

        def ld4(i):
            if i >= NCH:
                return
            bb = i % 2
            for d_ in range(2):
                c_ = i if d_ == 0 else NCH - 1 - i
                tk0 = OWN0 + c_ * 128
                P.dma("sync", lambda e, bb=bb, d_=d_, tk0=tk0: e.dma_start(out=QTc[d_][bb][:], in_=G.MLQT[:, :, tk0:tk0 + 128]), writes=["l_QTc%d%d" % (d_, bb)])
                P.dma("sync", lambda e, bb=bb, d_=d_, tk0=tk0: e.dma_start(out=KTc[d_][bb][:], in_=G.MLKT[:, :, tk0:tk0 + 128]), writes=["l_KTc%d%d" % (d_, bb)])
        ld4(0)
        for i in range(NCH):
            b = i % 2
            ld4(i + 1)
            cs = (i, NCH - 1 - i)
            for hh_ in range(2):
                items = [(h, d) for h in (2 * hh_, 2 * hh_ + 1) for d in range(2)]
                def front(i2, h, d):
                    b2 = i2 % 2
                    c = (i2, NCH - 1 - i2)[d]
                    r = d * 4 + h
                    qn, kn = "l_QTc%d%d" % (d, b2), "l_KTc%d%d" % (d, b2)
                    P.op("tensor", lambda e, h=h, d=d, b2=b2: e.matmul(pS[d][:], lhsT=KTc[d][b2][:, h, :], rhs=QTc[d][b2][:, h, :], start=True, stop=True),
                         reads=[kn, qn], writes=["l_pS%d" % d], inc=True)
                    P.op("vector", lambda e, c=c, r=r, d=d: e.scalar_tensor_tensor(
                        out=PTt[d][:], in0=pS[d][:], scalar=TOK[:, c, 1, r:r + 1], in1=TRI[:, d, :], op0=ALU.mult, op1=ALU.mult),
                        reads=["l_pS%d" % d, "l_TOK", "l_TRI"], writes=["l_PT%d" % d])

                def back(h, d):
                    c = cs[d]
                    r = d * 4 + h
                    u = (h % 2) * 2 + d
                    qn = "l_QTc%d%d" % (d, b)
                    P.op("tensor", lambda e, c=c, h=h, d=d, u=u: e.matmul(pU[u][:, 0:129], lhsT=PTt[d][:], rhs=VTOK[:, c, h * 129:(h + 1) * 129], start=True, stop=False),
                         reads=["l_PT%d" % d, "l_VTOK"], writes=["l_pU%d" % u], inc=False)
                    P.op("tensor", lambda e, h=h, d=d, r=r, u=u, b=b: e.matmul(pU[u][:, 0:129], lhsT=QTc[d][b][:, h, :], rhs=STB[:, r, :], start=False, stop=True),
                         reads=[qn, "l_STB%d" % r, "l_STB"], writes=["l_pU%d" % u], inc=True)
                if i == 0 and hh_ == 0:
                    front(i, *items[0])
                for k_ in range(4):
                    if k_ + 1 < 4:
                        front(i, *items[k_ + 1])
                    elif hh_ == 0:
                        front(i, 2, 0)
                    elif i + 1 < NCH:
                        front(i + 1, 0, 0)
                    back(*items[k_])
                for (h, d) in items:
                    c = cs[d]
                    r = d * 4 + h
                    a = r % 2
                    ka, kan = scaled_k(c, h, 2, r)
                    P.op("tensor", lambda e, a=a, ka=ka, c=c, h=h: e.matmul(pN[a][:, 0:129], lhsT=ka[:], rhs=VTOK[:, c, h * 129:(h + 1) * 129], start=True, stop=True),
                         reads=[kan, "l_VTOK"], writes=["l_pN%d" % a], inc=True)
                    P.op("vector", lambda e, a=a, r=r, c=c: e.scalar_tensor_tensor(
                        out=STATE[:, r, :], in0=STATE[:, r, :], scalar=EBEND[:, r, c:c + 1], in1=pN[a][:, 0:129], op0=ALU.mult, op1=ALU.add),
                        reads=["l_STATE%d" % r, "l_EBEND", "l_pN%d" % a, "l_STATE"], writes=["l_STATE%d" % r])
                    P.op("scalar", lambda e, r=r: e.activation(out=STB[:, r, :], in_=STATE[:, r, :], func=AF.Copy),
                         reads=["l_STATE%d" % r], writes=["l_STB%d" % r])
                for (h, d) in items:
                    u = (h % 2) * 2 + d
                    dn, dnn = den[d], "l_den%d" % d
                    P.op("scalar", lambda e, dn=dn, u=u, h=h: e.activation(out=dn[:, h:h + 1], in_=pU[u][:, 128:129], func=AF.Abs),
                         reads=["l_pU%d" % u], writes=[dnn])
                for d in range(2):
                    c = cs[d]
                    dn, dnn = den[d], "l_den%d" % d
                    h0 = 2 * hh_
                    REB2 = TOK[:, c, 4, d * 4 + h0:d * 4 + h0 + 2]
                    P.op("vector", lambda e, dn=dn, REB2=REB2, h0=h0: e.tensor_tensor(out=dn[:, h0:h0 + 2], in0=dn[:, h0:h0 + 2], in1=REB2, op=ALU.max),
                         reads=[dnn, "l_TOK"], writes=[dnn])
                    P.op("vector", lambda e, dn=dn, h0=h0: e.reciprocal(dn[:, h0:h0 + 2], dn[:, h0:h0 + 2]), reads=[dnn], writes=[dnn])
                for (h, d) in items:
                    u = (h % 2) * 2 + d
                    dn, dnn = den[d], "l_den%d" % d
                    hsn = "l_HS%d%d" % (d, b)
                    if d == 0:
                        P.op("scalar", lambda e, b=b, h=h, dn=dn, u=u: e.activation(out=HS[0][b][:, h * 128:(h + 1) * 128], in_=pU[u][:, 0:128], func=AF.Copy, scale=dn[:, h:h + 1]),
                             reads=["l_pU%d" % u, dnn], writes=[hsn])
                    else:
                        P.op("vector", lambda e, b=b, h=h, dn=dn, u=u: e.tensor_scalar(HS[1][b][:, h * 128:(h + 1) * 128], pU[u][:, 0:128], dn[:, h:h + 1], None, ALU.mult),
                             reads=["l_pU%d" % u, dnn], writes=[hsn])
            P.dma("sync", lambda e, b=b, c=cs[0]: e.dma_start(out=G.HF[c * 128:(c + 1) * 128, :], in_=HS[0][b][:]), reads=["l_HS0%d" % b], writes=[_u()])
            P.dma("sync", lambda e, b=b, c=cs[1]: e.dma_start(out=G.HB[c * 128:(c + 1) * 128, :], in_=HS[1][b][:]), reads=["l_HS1%d" % b], writes=[_u()])
        P.barrier()
        P.emit()

    with contextlib.ExitStack() as st:
        sb, ps = _alloc(G, st)
        NB = 3
        hf = [sb("r_hf%d" % b, [128, 512], F32) for b in range(NB)]
        hb = [sb("r_hb%d" % b, [128, 512], F32) for b in range(NB)]
        g2 = [sb("r_g2%d" % b, [128, 512], F32) for b in range(NB)]
        junk = sb("r_junk", [128, 128], F32)
        ssq = [sb("r_ssq%d" % b, [128, 4], F32) for b in range(2)]
        Yb = [sb("r_Y%d" % b, [128, 512], BF16) for b in range(2)]
        ytb = [sb("r_yt%d" % b, [128, 4, 128], BF16) for b in range(2)]
        ptr2 = [ps("r_ptr%d" % b, [128, 4, 128], BF16) for b in range(2)]

        def ld5(c):
            if c >= NCH:
                return
            b3 = c % NB
            tk0 = OWN0 + c * 128
            P.dma("sync", lambda e, b3=b3, c=c: e.dma_start(out=hf[b3][:], in_=G.HF[c * 128:(c + 1) * 128, :]), writes=["r_hf%d" % b3])
            P.dma("sync", lambda e, b3=b3, c=c: e.dma_start(out=hb[b3][:], in_=G.HB[c * 128:(c + 1) * 128, :]), writes=["r_hb%d" % b3])
            P.dma("sync", lambda e, b3=b3, tk0=tk0: e.dma_start(out=g2[b3][:], in_=G.MLG2[tk0:tk0 + 128, :]), writes=["r_g2%d" % b3])
        ld5(0)
        ld5(1)
        for c in range(NCH):
            ld5(c + 2)
            b3, b = c % NB, c % 2
            P.op("vector", lambda e, b3=b3: e.tensor_tensor(out=hf[b3][:], in0=hf[b3][:], in1=hb[b3][:], op=ALU.add),
                 reads=["r_hf%d" % b3, "r_hb%d" % b3], writes=["r_hf%d" % b3])
            for h in range(4):
                P.op("scalar", lambda e, b3=b3, b=b, h=h: e.activation(out=junk[:], in_=hf[b3][:, h * 128:(h + 1) * 128], func=AF.Square, accum_out=ssq[b][:, h:h + 1]),
                     reads=["r_hf%d" % b3], writes=["r_junk", "r_ssq%d" % b])
            P.op("scalar", lambda e, b=b: e.activation(out=ssq[b][:], in_=ssq[b][:], func=AF.Sqrt, bias=G.c_eps[:], scale=1.0 / 128.0),
                 reads=["r_ssq%d" % b, "c_eps"], writes=["r_ssq%d" % b])
            P.op("vector", lambda e, b=b: e.reciprocal(ssq[b][:], ssq[b][:]), reads=["r_ssq%d" % b], writes=["r_ssq%d" % b])
            for h in range(4):
                P.op("vector", lambda e, b3=b3, b=b, h=h: e.scalar_tensor_tensor(
                    out=Yb[b][:, h * 128:(h + 1) * 128], in0=hf[b3][:, h * 128:(h + 1) * 128], scalar=ssq[b][:, h:h + 1],
                    in1=g2[b3][:, h * 128:(h + 1) * 128], op0=ALU.mult, op1=ALU.mult),
                    reads=["r_hf%d" % b3, "r_ssq%d" % b, "r_g2%d" % b3], writes=["r_Y%d" % b])
            for h in range(4):
                P.op("tensor", lambda e, b=b, h=h: e.transpose(ptr2[b][:, h, :], Yb[b][:, h * 128:(h + 1) * 128], G.ident_b[:]),
                     reads=["r_Y%d" % b, "ident_b"], writes=["r_ptr%d" % b], inc=(h == 3))
            P.op("scalar", lambda e, b=b: e.activation(out=ytb[b][:], in_=ptr2[b][:], func=AF.Copy), reads=["r_ptr%d" % b], writes=["r_yt%d" % b])
            P.dma("sync", lambda e, b=b, c=c: e.dma_start(out=G.YT[:, 4:8, c * 128:(c + 1) * 128], in_=ytb[b][:]), reads=["r_yt%d" % b], writes=[_u()])
        P.barrier()
        P.emit()


def stage_outproj(G):
    P, I, nc = G.P, G.I, G.nc
    l, j, m = 0, 1, 0
    with contextlib.ExitStack() as st:
        sb, ps = _alloc(G, st)
        wo = sb("o_wo", [128, 8, 1024], BF16)
        xt = [sb("o_xt%d" % b, [128, 8, TW], F32) for b in range(2)]
        yt = [sb("o_yt%d" % b, [128, 8, TW], BF16) for b in range(2)]
        po = [ps("o_po%d" % b, [128, TW]) for b in range(2)]
        load_w_cast(G, wo, "o_wo", I["mix_w_out"], 8, 1024)
        def ld(n):
            if n >= 16:
                return
            ti_, b_ = n + 1, n % 2
            P.dma("sync", lambda e, b_=b_, ti_=ti_: e.dma_start(out=xt[b_][:], in_=G.XT[:, :, ti_ * TW:(ti_ + 1) * TW]), reads=["XT%d" % ti_], writes=["o_xt%d" % b_])
            P.dma("sync", lambda e, b_=b_, n=n: e.dma_start(out=yt[b_][:], in_=G.YT[:, :, n * TW:(n + 1) * TW]), reads=["dramYT"], writes=["o_yt%d" % b_])
        ld(0)
        ld(1)
        for n in range(16):
            ti = n + 1
            t0 = ti * TW
            b = n % 2
            for f in range(8):
                q = f % 2
                for k in range(8):
                    P.op("tensor", lambda e, q=q, f=f, k=k, b=b: e.matmul(po[q][:], lhsT=wo[:, k, f * 128:(f + 1) * 128], rhs=yt[b][:, k, :], start=(k == 0), stop=(k == 7)),
                         reads=["o_wo", "o_yt%d" % b], writes=["o_po%d" % q], inc=(k == 7))
                gsc = G.GT[:, l, j, m, f:f + 1]
                P.op("vector", lambda e, q=q, f=f, b=b, gsc=gsc: e.scalar_tensor_tensor(
                    out=xt[b][:, f, :], in0=po[q][:], scalar=gsc, in1=xt[b][:, f, :], op0=ALU.mult, op1=ALU.add),
                    reads=["o_po%d" % q, "modc", "o_xt%d" % b], writes=["o_xt%d" % b])
            P.dma("sync", lambda e, b=b, t0=t0: e.dma_start(out=G.XT[:, :, t0:t0 + TW], in_=xt[b][:]), reads=["o_xt%d" % b], writes=["XT%d" % ti])
            ld(n + 2)
        P.barrier()
        P.emit()


def stage_sg(G):
    P, I, nc = G.P, G.I, G.nc
    l, j, m = 1, 1, 0
    with contextlib.ExitStack() as st:
        sb, ps = _alloc(G, st)
        wu = sb("g_wu", [128, 8, 2048], BF16)
        wv = sb("g_wv", [128, 8, 2048], BF16)
        wo = sb("g_wo", [128, 16, 1024], BF16)
        wsT = sb("g_wsT", [128, 8, 128], BF16)
        bs = sb("g_bs", [1, 1024], BF16)
        ones1 = sb("g_ones1", [1, 256], BF16)
        lng = sb("g_lng", [128, 2048], F32)
        lnb = sb("g_lnb", [128, 2048], F32)
        xt = [sb("g_xt%d" % b, [128, 8, TW], F32) for b in range(2)]
        hh = [sb("g_h%d" % b, [128, 8, TW], BF16) for b in range(2)]
        NS = NormScratch(G, sb, ps, "g_")
        NS.sq8 = [sb("g_sq8_%d" % k, [128, TW], BF16) for k in range(8)]
        uT = sb("g_uT", [128, 16, TW], BF16)
        vraw = [sb("g_vraw%d" % b, [128, 2048], F32) for b in range(2)]
        vn = [sb("g_vn%d" % b, [128, 2048], BF16) for b in range(2)]
        gated = sb("g_gated", [128, 16, TW], BF16)
        stats = [sb("g_stats%d" % b, [128, 4, 6], F32) for b in range(2)]
        mv = [sb("g_mv%d" % b, [128, 4], F32) for b in range(2)]
        pu = [ps("g_pu%d" % b, [128, TW]) for b in range(2)]
        pm = [ps("g_pm%d" % b, [128, 4, 128]) for b in range(2)]
        po1 = ps("g_po", [128, TW])
        po = [po1, po1]
        pv = [ps("g_pv%d" % b, [128, 512]) for b in range(2)]
        load_w_cast(G, wv, "g_wv", I["sg_w_in"][:, 2048:4096], 8, 2048)
        load_w_cast(G, wu, "g_wu", I["sg_w_in"][:, 0:2048], 8, 2048)
        load_w_cast(G, wo, "g_wo", I["sg_w_out"], 16, 1024)
        P.dma("gpsimd", lambda e: e.dma_start(out=wsT[:], in_=I["sg_w_sT"][:, :, :]), writes=["g_wsT"])
        P.dma("gpsimd", lambda e: e.dma_start(out=bs[:], in_=I["sg_b_s"][:, :]), writes=["g_bs"])
        P.op("vector", lambda e: e.memset(ones1[:], 1.0), writes=["g_ones1"])
        P.dma("sync", lambda e: e.dma_start(out=lng[:], in_=I["sg_lng_bc"][:, :]), writes=["g_lng"])
        P.dma("sync", lambda e: e.dma_start(out=lnb[:], in_=I["sg_lnb_bc"][:, :]), writes=["g_lnb"])
        tiles = list(range(1, 17))

        def ld(n):
            ti_ = tiles[n]
            xb = xt[n % 2]
            P.dma("sync", lambda e, xb=xb, ti_=ti_: e.dma_start(out=xb[:], in_=G.XT[:, :, ti_ * TW:(ti_ + 1) * TW]),
                  reads=["XT%d" % ti_], writes=["g_xt%d" % (n % 2)])
        def do_norm(n, phase=None):
            bb = n % 2
            norm_mod(G, NS, xt[bb], "g_xt%d" % bb, G.GM[:, l, j, m, :], G.SH[:, l, j, m, :], hh[bb], "g_h%d" % bb, phase=phase)
        ld(0)
        ld(1)
        do_norm(0)
        vcnt = 0
        for n, ti in enumerate(tiles):
            t0 = ti * TW
            b = n % 2
            x, xn = xt[b], "g_xt%d" % b
            h, hn = hh[b], "g_h%d" % b
            NSB = TW // 128
            for s_ in range(NSB):
                VR = vraw[s_]
                for blk in range(4):
                    q = blk % 2
                    for k in range(8):
                        P.op("tensor", lambda e, q=q, blk=blk, k=k, h=h, s_=s_: e.matmul(
                            pv[q][:], lhsT=h[:, k, s_ * 128:(s_ + 1) * 128], rhs=wv[:, k, blk * 512:(blk + 1) * 512], start=(k == 0), stop=(k == 7)),
                            reads=["g_wv", hn], writes=["g_pv%d" % q], inc=(k == 7))
                    P.op("scalar", lambda e, q=q, blk=blk, VR=VR: e.activation(out=VR[:, blk * 512:(blk + 1) * 512], in_=pv[q][:], func=AF.Gelu_apprx_tanh),
                         reads=["g_pv%d" % q], writes=["g_vraw%d" % s_])
                    P.op("vector", lambda e, blk=blk, VR=VR, s_=s_: e.bn_stats(stats[s_][:, blk, :], VR[:, blk * 512:(blk + 1) * 512]),
                         reads=["g_vraw%d" % s_], writes=["g_stats%d" % s_])
                P.op("vector", lambda e, s_=s_: e.bn_aggr(mv[s_][:, 0:2], stats[s_][:].rearrange("p a b -> p (a b)")), reads=["g_stats%d" % s_], writes=["g_mv%d" % s_])
                P.op("scalar", lambda e, s_=s_: e.activation(out=mv[s_][:, 2:3], in_=mv[s_][:, 1:2], func=AF.Sqrt, bias=G.c_eps[:], scale=1.0),
                     reads=["g_mv%d" % s_, "c_eps"], writes=["g_mv%d" % s_])
                P.op("vector", lambda e, s_=s_: e.reciprocal(mv[s_][:, 2:3], mv[s_][:, 2:3]), reads=["g_mv%d" % s_], writes=["g_mv%d" % s_])
                P.op("vector", lambda e, s_=s_: e.tensor_scalar(mv[s_][:, 3:4], mv[s_][:, 0:1], mv[s_][:, 2:3], -1.0, ALU.mult, ALU.mult),
                     reads=["g_mv%d" % s_], writes=["g_mv%d" % s_])
            for fc in range(16):
                q = fc % 2
                if fc == 6:
                    vns = []
                    for s_ in range(NSB):
                        VR = vraw[s_]
                        VN, vnn = vn[vcnt % 2], "g_vn%d" % (vcnt % 2)
                        vcnt += 1
                        vns.append((VN, vnn))
                        P.op("scalar", lambda e, VR=VR, s_=s_: e.activation(out=VR[:], in_=VR[:], func=AF.Identity, bias=mv[s_][:, 3:4], scale=mv[s_][:, 2:3]),
                             reads=["g_vraw%d" % s_, "g_mv%d" % s_], writes=["g_vraw%d" % s_])
                        P.op("vector", lambda e, VR=VR: e.tensor_tensor(out=VR[:], in0=VR[:], in1=lng[:], op=ALU.mult),
                             reads=["g_vraw%d" % s_, "g_lng"], writes=["g_vraw%d" % s_])
                        P.op("vector", lambda e, VR=VR, VN=VN: e.tensor_tensor(out=VN[:], in0=VR[:], in1=lnb[:], op=ALU.add),
                             reads=["g_vraw%d" % s_, "g_lnb"], writes=[vnn])
                for k in range(8):
                    P.op("tensor", lambda e, q=q, fc=fc, k=k, h=h: e.matmul(pu[q][:], lhsT=wu[:, k, fc * 128:(fc + 1) * 128], rhs=h[:, k, :], start=(k == 0), stop=(k == 7)),
                         reads=["g_wu", hn], writes=["g_pu%d" % q], inc=(k == 7))
                P.op("scalar", lambda e, q=q, fc=fc: e.activation(out=uT[:, fc, :], in_=pu[q][:], func=AF.Gelu_apprx_tanh), reads=["g_pu%d" % q], writes=["g_uT%d" % fc])
            for s_ in range(NSB):
                VN, vnn = vns[s_]
                for g4 in range(4):
                    q = g4 % 2
                    for f4 in range(4):
                        fc = g4 * 4 + f4
                        g_ = fc // 2
                        P.op("tensor", lambda e, q=q, fc=fc, f4=f4, g_=g_, VN=VN: e.matmul(
                            pm[q][:, f4, :], lhsT=VN[:, fc * 128:(fc + 1) * 128], rhs=wsT[:, g_, :], start=True, stop=False),
                            reads=[vnn, "g_wsT"], writes=["g_pm%d" % q], inc=False)
                        P.op("tensor", lambda e, q=q, f4=f4, g_=g_: e.matmul(
                            pm[q][:, f4, :], lhsT=ones1[0:1, 0:128], rhs=bs[0:1, g_ * 128:(g_ + 1) * 128], start=False, stop=True),
                            reads=["g_ones1", "g_bs"], writes=["g_pm%d" % q], inc=(f4 == 3))
                    P.op("vector", lambda e, q=q, g4=g4, s_=s_: e.tensor_tensor(
                        out=gated[:, g4 * 4:(g4 + 1) * 4, s_ * 128:(s_ + 1) * 128], in0=uT[:, g4 * 4:(g4 + 1) * 4, s_ * 128:(s_ + 1) * 128], in1=pm[q][:], op=ALU.mult),
                        reads=["g_uT%d" % fc_ for fc_ in range(g4 * 4, g4 * 4 + 4)] + ["g_pm%d" % q],
                        writes=["g_gated%d" % fc_ for fc_ in range(g4 * 4, g4 * 4 + 4)])
            if n + 1 < len(tiles):
                do_norm(n + 1, "A")
            for f in range(8):
                q = f % 2
                if f == 3 and n + 1 < len(tiles):
                    do_norm(n + 1, "B")
                for fc in range(16):
                    P.op("tensor", lambda e, q=q, f=f, fc=fc: e.matmul(po[q][:], lhsT=wo[:, fc, f * 128:(f + 1) * 128], rhs=gated[:, fc, :], start=(fc == 0), stop=(fc == 15)),
                         reads=["g_wo", "g_gated%d" % fc], writes=["g_po"], inc=(fc == 15))
                gsc = G.GT[:, l, j, m, f:f + 1]
                P.op("vector", lambda e, q=q, f=f, x=x, gsc=gsc: e.scalar_tensor_tensor(
                    out=x[:, f, :], in0=po[q][:], scalar=gsc, in1=x[:, f, :], op0=ALU.mult, op1=ALU.add),
                    reads=["g_po", "modc", xn], writes=[xn])
            P.dma("sync", lambda e, x=x, t0=t0: e.dma_start(out=G.XT[:, :, t0:t0 + TW], in_=x[:]), reads=[xn], writes=["XT%d" % ti])
            if n + 2 < len(tiles):
                ld(n + 2)
        P.barrier()
        P.emit()


_NC_CACHE = {}


def kernel(**inputs):
    maps = _host_prep(inputs)
    if "nc" not in _NC_CACHE:
        _NC_CACHE["nc"] = build()
    nc = _NC_CACHE["nc"]
    res = run_bass_kernel_spmd(nc, maps, core_ids=list(range(8)))
    out = np.zeros((2, 16384, 1024), np.float32)
    for core in range(8):
        b, s = core // 4, core % 4
        out[b, s * 4096:(s + 1) * 4096, :] = res.results[core]["out"]
    return out
```
